# Optimizing a Trainium2 kernel written in Bass

```python
import math
import jax
import jax.numpy as jnp
from jax import lax
import numpy as np

D_MODEL = 1024
BATCH = 8
SEQ = 4096
DEPTH = 2

CHUNK = 64
QBLK = 128

POOL_WINDOWS = (2, 4, 8, 16)
POOL_GROUP = D_MODEL // 8
POOL_WIDTH = POOL_GROUP * len(POOL_WINDOWS)

DSA_HEADS = 8
DSA_HEAD_DIM = D_MODEL // 16
DSA_WIDTH = DSA_HEADS * DSA_HEAD_DIM
KV_RANK = D_MODEL // 8
IDX_HEADS = 4
IDX_DIM = D_MODEL // 16
TOPK_MAX = 256
MIX_WIDTH = POOL_WIDTH + DSA_WIDTH
EVEN_SPLITS = (POOL_WIDTH,
               POOL_WIDTH + DSA_WIDTH,
               POOL_WIDTH + DSA_WIDTH + KV_RANK,
               POOL_WIDTH + DSA_WIDTH + KV_RANK + IDX_HEADS * IDX_DIM,
               POOL_WIDTH + DSA_WIDTH + KV_RANK + IDX_HEADS * IDX_DIM + IDX_DIM)
EVEN_IN = EVEN_SPLITS[-1] + IDX_HEADS

GDN_HEADS = 8
GDN_HEAD_DIM = D_MODEL // GDN_HEADS
GDN_WIDTH = GDN_HEADS * GDN_HEAD_DIM
SHORT_CONV = 4
ODD_SPLITS = (3 * GDN_WIDTH, 4 * GDN_WIDTH, 4 * GDN_WIDTH + GDN_HEADS)
ODD_IN = 4 * GDN_WIDTH + 2 * GDN_HEADS

D_FF = (8 * D_MODEL // 3) // 128 * 128
FFN_CONV = 3

DEEPNORM_ALPHA = (2 * DEPTH) ** 0.25
DEEPNORM_BETA = (8 * DEPTH) ** -0.25
LN_EPS = 1e-5
RMS_EPS = 1e-6
N_EVEN = (DEPTH + 1) // 2
N_ODD = DEPTH // 2

kernel_name = 'hybrid_pool_dsa_gdn_convffn_trunk'


def layer_norm(x, g, b):
    xf = x.astype(jnp.float32)
    mu = xf.mean(-1, keepdims=True)
    var = jnp.square(xf - mu).mean(-1, keepdims=True)
    return ((xf - mu) * lax.rsqrt(var + LN_EPS) * g + b).astype(x.dtype)


def rms_norm(x, g):
    xf = x.astype(jnp.float32)
    return (xf * lax.rsqrt(jnp.mean(xf * xf, -1, keepdims=True) + RMS_EPS) * g).astype(x.dtype)


def l2_norm(x):
    xf = x.astype(jnp.float32)
    return xf * lax.rsqrt(jnp.sum(xf * xf, -1, keepdims=True) + RMS_EPS)


def causal_dwconv(x, w):
    k = w.shape[0]
    return lax.conv_general_dilated(
        x, w[:, None, :], window_strides=(1,), padding=[(k - 1, 0)],
        dimension_numbers=('NWC', 'WIO', 'NWC'), feature_group_count=x.shape[-1])


def adaln_post_norm(x, c, mod_w, mod_b, ln_g, ln_b, sublayer):
    mod = jax.nn.silu(c) @ mod_w + mod_b
    shift, scale, gate = jnp.split(mod[:, None, :], 3, axis=-1)
    y = sublayer(x * (1.0 + scale) + shift)
    return layer_norm(DEEPNORM_ALPHA * x + gate * y, ln_g, ln_b)


def pool_mixer(u, w_pool, scale):
    b, s, _ = u.shape
    ug = u.reshape(b, s, len(POOL_WINDOWS), POOL_GROUP)
    cs = jnp.pad(jnp.cumsum(ug.astype(jnp.float32), axis=1), ((0, 0), (1, 0), (0, 0), (0, 0)))
    t = jnp.arange(1, s + 1)
    means = []
    for gi, win in enumerate(POOL_WINDOWS):
        lo = jnp.maximum(t - win, 0)
        cnt = (t - lo).astype(jnp.float32)
        means.append((cs[:, 1:, gi] - cs[:, lo, gi]) / cnt[None, :, None])
    pooled = jnp.stack(means, axis=2).astype(u.dtype) - ug
    y = jnp.einsum('bsgc,gcd->bsgd', pooled, w_pool)
    return y.reshape(b, s, POOL_WIDTH) * scale


def dsa_attention(q, ckv, q_idx, k_idx, w_idx, w_uk, w_uv):
    b, s, h, dh = q.shape
    topk = min(TOPK_MAX, s // 4)
    nblk = s // QBLK
    q_lat = jnp.einsum('bshd,hrd->bshr', q, w_uk)
    w_idx = w_idx * (IDX_HEADS ** -0.5 * IDX_DIM ** -0.5)
    key_chunk = jnp.arange(s) // CHUNK

    def blocks(t):
        return jnp.moveaxis(t.reshape((b, nblk, QBLK) + t.shape[2:]), 1, 0)

    def attend(args):
        ql, qi, wi, pos = args
        qchunk = pos // CHUNK
        rel = jax.nn.relu(jnp.einsum('bthd,bsd->bths', qi, k_idx).astype(jnp.float32))
        score = jnp.einsum('bths,bth->bts', rel, wi.astype(jnp.float32))
        adm = key_chunk[None, :] <= qchunk[:, None]
        score = jnp.where(adm[None], score, -jnp.inf)
        _, sel = lax.top_k(score, topk)
        kv_sel = jax.vmap(lambda cb, ib: cb[ib])(ckv, sel)
        valid = (sel // CHUNK) <= qchunk[None, :, None]
        logits = jnp.einsum('bthr,btkr->bthk', ql, kv_sel).astype(jnp.float32) * dh ** -0.5
        logits = jnp.where(valid[:, :, None, :], logits, -jnp.inf)
        p = jax.nn.softmax(logits, axis=-1).astype(kv_sel.dtype)
        return jnp.einsum('bthk,btkr->bthr', p, kv_sel)

    qpos = jnp.arange(s).reshape(nblk, QBLK)
    o = lax.map(attend, (blocks(q_lat), blocks(q_idx), blocks(w_idx), qpos))
    o = jnp.moveaxis(o, 0, 1).reshape(b, s, h, KV_RANK)
    return jnp.einsum('bshr,hrd->bshd', o, w_uv)


def pool_dsa_mixer(h, w_in, pool_w, pool_scale, kv_norm, w_uk, w_uv, w_out):
    b, s, _ = h.shape
    proj = h @ w_in
    u, q, ckv, qi, ki, wi = jnp.split(proj, EVEN_SPLITS, axis=-1)
    y_pool = pool_mixer(u, pool_w, pool_scale)
    y_dsa = dsa_attention(q.reshape(b, s, DSA_HEADS, DSA_HEAD_DIM), rms_norm(ckv, kv_norm),
                          qi.reshape(b, s, IDX_HEADS, IDX_DIM), ki, wi, w_uk, w_uv)
    y = jnp.concatenate([y_pool, y_dsa.reshape(b, s, DSA_WIDTH)], axis=-1)
    return y @ w_out


def gated_delta_rule(q, k, v, beta, g):
    b, s, h, dk = q.shape
    dv = v.shape[-1]
    n = s // CHUNK

    def to_chunks(t):
        t = t.astype(jnp.float32).reshape((b, n, CHUNK, h) + t.shape[3:])
        return jnp.moveaxis(t, 3, 1)

    q, k, v, beta, g = [to_chunks(t) for t in (q, k, v, beta, g)]
    q = q * dk ** -0.5
    gc = jnp.cumsum(g, axis=-1)
    ar = jnp.arange(CHUNK)
    strict = ar[:, None] > ar[None, :]
    incl = ar[:, None] >= ar[None, :]
    diff = gc[..., :, None] - gc[..., None, :]
    dec_strict = jnp.where(strict, jnp.exp(jnp.where(strict, diff, 0.0)), 0.0)
    dec_incl = jnp.where(incl, jnp.exp(jnp.where(incl, diff, 0.0)), 0.0)
    kb = k * beta[..., None]
    a = jnp.einsum('bhncd,bhnsd->bhncs', kb, k) * dec_strict
    eye = jnp.eye(CHUNK, dtype=jnp.float32)
    t_mat = lax.linalg.triangular_solve(a + eye, jnp.broadcast_to(eye, a.shape),
                                        left_side=True, lower=True, unit_diagonal=True)
    u = t_mat @ (v * beta[..., None])
    w = t_mat @ (kb * jnp.exp(gc)[..., None])
    attn = jnp.einsum('bhncd,bhnsd->bhncs', q, k) * dec_incl
    q_dec = q * jnp.exp(gc)[..., None]
    k_dec = k * jnp.exp(gc[..., -1:] - gc)[..., None]
    chunk_decay = jnp.exp(gc[..., -1])
    xs = tuple(jnp.moveaxis(t, 2, 0) for t in (q_dec, k_dec, u, w, attn, chunk_decay))

    def step(state, inp):
        qd, kd, uc, wc, at, cd = inp
        v_new = uc - wc @ state
        o = qd @ state + at @ v_new
        state = state * cd[..., None, None] + jnp.swapaxes(kd, -1, -2) @ v_new
        return state, o

    s0 = jnp.zeros((b, h, dk, dv), jnp.float32)
    _, o = lax.scan(step, s0, xs)
    return jnp.transpose(o, (1, 0, 3, 2, 4)).reshape(b, s, h, dv)


def gdn_mixer(h, w_in, conv_w, a_log, dt_bias, out_norm, w_out):
    b, s, _ = h.shape
    proj = h @ w_in
    qkv, gate, beta_raw, a_raw = jnp.split(proj, ODD_SPLITS, axis=-1)
    qkv = jax.nn.silu(causal_dwconv(qkv, conv_w))
    q, k, v = [t.reshape(b, s, GDN_HEADS, GDN_HEAD_DIM) for t in jnp.split(qkv, 3, axis=-1)]
    beta = jax.nn.sigmoid(beta_raw.astype(jnp.float32))
    g = -jnp.exp(a_log.astype(jnp.float32)) * jax.nn.softplus(a_raw.astype(jnp.float32) + dt_bias)
    o = gated_delta_rule(l2_norm(q), l2_norm(k), v, beta, g)
    o = rms_norm(o, out_norm) * jax.nn.silu(gate.reshape(b, s, GDN_HEADS, GDN_HEAD_DIM).astype(jnp.float32))
    return o.reshape(b, s, GDN_WIDTH).astype(h.dtype) @ w_out


def conv_ffn(h, w_up, conv_w, conv_b, w_down):
    up = causal_dwconv(h @ w_up, conv_w) + conv_b
    a, v = jnp.split(up, 2, axis=-1)
    return (jax.nn.silu(a) * v) @ w_down


def setup_inputs(seed: int = 0) -> dict:
    key = jax.random.key(seed)
    keys = iter(jax.random.split(key, 64))

    def nrm(shape, std):
        return std * jax.random.normal(next(keys), shape, jnp.float32)

    d = D_MODEL
    ne, no, nl = N_EVEN, N_ODD, DEPTH
    mod_std = 0.5 * d ** -0.5
    x = nrm((BATCH, SEQ, d), 1.0)
    c = nrm((BATCH, d), 1.0)
    e_mod_w = nrm((ne, d, 3 * d), mod_std)
    e_mod_b = nrm((ne, 3 * d), 0.01)
    e_w_in = nrm((ne, d, EVEN_IN), d ** -0.5)
    e_pool_w = nrm((ne, len(POOL_WINDOWS), POOL_GROUP, POOL_GROUP), POOL_GROUP ** -0.5)
    e_pool_scale = 1.0 + nrm((ne, POOL_WIDTH), 0.1)
    e_kv_norm = 1.0 + nrm((ne, KV_RANK), 0.05)
    e_w_uk = nrm((ne, DSA_HEADS, KV_RANK, DSA_HEAD_DIM), KV_RANK ** -0.5)
    e_w_uv = nrm((ne, DSA_HEADS, KV_RANK, DSA_HEAD_DIM), KV_RANK ** -0.5)
    e_w_out = nrm((ne, MIX_WIDTH, d), DEEPNORM_BETA * MIX_WIDTH ** -0.5)
    e_ln_g = 1.0 + nrm((ne, d), 0.05)
    e_ln_b = nrm((ne, d), 0.01)
    o_mod_w = nrm((no, d, 3 * d), mod_std)
    o_mod_b = nrm((no, 3 * d), 0.01)
    o_w_in = nrm((no, d, ODD_IN), d ** -0.5)
    o_conv_w = nrm((no, SHORT_CONV, 3 * GDN_WIDTH), SHORT_CONV ** -0.5)
    o_a_log = jnp.log(1.0 + 15.0 * jax.random.uniform(next(keys), (no, GDN_HEADS), jnp.float32))
    dt = jnp.exp(jax.random.uniform(next(keys), (no, GDN_HEADS), jnp.float32,
                                    math.log(1e-3), math.log(1e-1)))
    o_dt_bias = dt + jnp.log(-jnp.expm1(-dt))
    o_out_norm = 1.0 + nrm((no, GDN_HEAD_DIM), 0.05)
    o_w_out = nrm((no, GDN_WIDTH, d), DEEPNORM_BETA * GDN_WIDTH ** -0.5)
    o_ln_g = 1.0 + nrm((no, d), 0.05)
    o_ln_b = nrm((no, d), 0.01)
    f_mod_w = nrm((nl, d, 3 * d), mod_std)
    f_mod_b = nrm((nl, 3 * d), 0.01)
    f_w_up = nrm((nl, d, 2 * D_FF), d ** -0.5)
    f_conv_w = nrm((nl, FFN_CONV, 2 * D_FF), FFN_CONV ** -0.5)
    f_conv_b = nrm((nl, 2 * D_FF), 0.01)
    f_w_down = nrm((nl, D_FF, d), DEEPNORM_BETA * D_FF ** -0.5)
    f_ln_g = 1.0 + nrm((nl, d), 0.05)
    f_ln_b = nrm((nl, d), 0.01)
    return {'x': x, 'c': c,
            'e_mod_w': e_mod_w, 'e_mod_b': e_mod_b, 'e_w_in': e_w_in, 'e_pool_w': e_pool_w,
            'e_pool_scale': e_pool_scale, 'e_kv_norm': e_kv_norm, 'e_w_uk': e_w_uk, 'e_w_uv': e_w_uv,
            'e_w_out': e_w_out, 'e_ln_g': e_ln_g, 'e_ln_b': e_ln_b,
            'o_mod_w': o_mod_w, 'o_mod_b': o_mod_b, 'o_w_in': o_w_in, 'o_conv_w': o_conv_w,
            'o_a_log': o_a_log, 'o_dt_bias': o_dt_bias, 'o_out_norm': o_out_norm, 'o_w_out': o_w_out,
            'o_ln_g': o_ln_g, 'o_ln_b': o_ln_b,
            'f_mod_w': f_mod_w, 'f_mod_b': f_mod_b, 'f_w_up': f_w_up, 'f_conv_w': f_conv_w,
            'f_conv_b': f_conv_b, 'f_w_down': f_w_down, 'f_ln_g': f_ln_g, 'f_ln_b': f_ln_b}


def reference(x, c, e_mod_w, e_mod_b, e_w_in, e_pool_w, e_pool_scale, e_kv_norm, e_w_uk, e_w_uv,
              e_w_out, e_ln_g, e_ln_b, o_mod_w, o_mod_b, o_w_in, o_conv_w, o_a_log, o_dt_bias,
              o_out_norm, o_w_out, o_ln_g, o_ln_b, f_mod_w, f_mod_b, f_w_up, f_conv_w, f_conv_b,
              f_w_down, f_ln_g, f_ln_b):
    for layer in range(DEPTH):
        i = layer // 2
        if layer % 2 == 0:
            mixer = lambda h: pool_dsa_mixer(h, e_w_in[i], e_pool_w[i], e_pool_scale[i], e_kv_norm[i],
                                             e_w_uk[i], e_w_uv[i], e_w_out[i])
            x = adaln_post_norm(x, c, e_mod_w[i], e_mod_b[i], e_ln_g[i], e_ln_b[i], mixer)
        else:
            mixer = lambda h: gdn_mixer(h, o_w_in[i], o_conv_w[i], o_a_log[i], o_dt_bias[i],
                                        o_out_norm[i], o_w_out[i])
            x = adaln_post_norm(x, c, o_mod_w[i], o_mod_b[i], o_ln_g[i], o_ln_b[i], mixer)
        ffn = lambda h: conv_ffn(h, f_w_up[layer], f_conv_w[layer], f_conv_b[layer], f_w_down[layer])
        x = adaln_post_norm(x, c, f_mod_w[layer], f_mod_b[layer], f_ln_g[layer], f_ln_b[layer], ffn)
    return x
```

```python
import os
import numpy as np
from contextlib import ExitStack
import concourse.bass as bass
import concourse.mybir as mybir
from concourse.bass_utils import run_bass_kernel_spmd

F32 = mybir.dt.float32
BF16 = mybir.dt.bfloat16
AF = mybir.ActivationFunctionType
ALU = mybir.AluOpType
AX = mybir.AxisListType

D = 1024
S = 4096
NB = S // 128
DFF = 2688
NFC = DFF // 128
ALPHA = float(4 ** 0.25)
LN_EPS = 1e-5
RMS_EPS = 1e-6
NDS = 12


class Sch:
    def __init__(self, nc, es):
        self.nc = nc
        self.E = {'pe': nc.tensor, 'act': nc.scalar, 'dve': nc.vector,
                  'pool': nc.gpsimd, 'sp': nc.sync}
        self.sem = {}
        for e in self.E:
            self.sem[e] = es.enter_context(nc.semaphore('s_' + e))
        self.cnt = {e: 0 for e in self.E}
        self.waited = {e: {} for e in self.E}
        self.lastw = {}
        self.readers = {}
        self.dq = ('sp', 'pool', 'act')
        self.dcnt = {}
        self.drr = {q: 0 for q in self.dq}
        for q in self.dq:
            for i in range(NDS):
                k = (q, i)
                self.sem[k] = es.enter_context(nc.semaphore('d_%s%d' % (q, i)))
                self.dcnt[k] = 0
        self.nwaits = 0

    def _wait(self, e, tok):
        key, val = tok
        if key == e and e == 'pe':
            return
        if self.waited[e].get(key, 0) >= val:
            return
        self.E[e].wait_ge(self.sem[key], val)
        self.waited[e][key] = val
        self.nwaits += 1

    def _collect(self, r, w, e=None):
        deps = {}

        def add(t):
            if t is None:
                return
            if deps.get(t[0], 0) < t[1]:
                deps[t[0]] = t[1]
        for x in r:
            for k, v in self.lastw.get(x, {}).items():
                add((k, v))
            if x.startswith('ps') and e is not None:
                for k, v in self.readers.get(x, {}).items():
                    if k != e:
                        add((k, v))
        for x in w:
            for k, v in self.lastw.get(x, {}).items():
                add((k, v))
            for k, v in self.readers.get(x, {}).items():
                add((k, v))
        return list(deps.items())

    def _record(self, tok, r, w):
        for x in w:
            self.lastw.setdefault(x, {})[tok[0]] = tok[1]
            self.readers[x] = {}
        for x in r:
            d = self.readers.setdefault(x, {})
            if d.get(tok[0], 0) < tok[1]:
                d[tok[0]] = tok[1]

    def op(self, e, fn, r=(), w=(), inc=True):
        for t in self._collect(r, w, e):
            self._wait(e, t)
        ins = fn(self.E[e])
        if inc:
            self.cnt[e] += 1
            ins.then_inc(self.sem[e], 1)
            tok = (e, self.cnt[e])
        else:
            tok = (e, self.cnt[e] + 1)
        self._record(tok, r, w)
        return ins

    def dma(self, q, out, in_, r=(), w=()):
        i = self.drr[q]
        self.drr[q] = (i + 1) % NDS
        k = (q, i)
        if self.dcnt[k] > 0:
            self._wait(q, (k, self.dcnt[k]))
        for t in self._collect(r, w):
            self._wait(q, t)
        ins = self.E[q].dma_start(out=out, in_=in_)
        self.dcnt[k] += 16
        ins.then_inc(self.sem[k], 16)
        tok = (k, self.dcnt[k])
        self._record(tok, r, w)
        return ins

    def barrier(self):
        toks = [(e, self.cnt[e]) for e in self.E if self.cnt[e] > 0]
        toks += [(k, v) for k, v in self.dcnt.items() if v > 0]
        for e in self.E:
            for t in toks:
                self._wait(e, t)
        self.lastw = {}
        self.readers = {}

    def finish(self):
        for k, v in self.dcnt.items():
            if v > 0:
                self._wait('sp', (k, v))


class Ctx:
    pass


_UID = [0]


def _sb(nc, es, name, shape, dt):
    _UID[0] += 1
    return es.enter_context(nc.sbuf_tensor("sb%d_%s" % (_UID[0], name), list(shape), dt))


def emit_consts(g):
    nc, s, es = g.nc, g.s, g.es
    g.ident = _sb(nc, es, "ident", [128, 128], F32)
    g.identb = _sb(nc, es, "identb", [128, 128], BF16)
    g.ones = _sb(nc, es, "ones", [128, 128], F32)
    g.onesb = _sb(nc, es, "onesb", [128, 128], BF16)
    s.dma('sp', g.ident[:], g.d_ident[:, :], w=['ident'])
    s.dma('pool', g.identb[:], g.d_ident[:, :], w=['identb'])
    s.op('dve', lambda e: e.memset(g.ones[:], 1.0), w=['ones'])
    s.op('dve', lambda e: e.memset(g.onesb[:], 1.0), w=['onesb'])
    g.ccol = _sb(nc, es, "ccol", [128, 8], F32)
    g.sc = _sb(nc, es, "sc", [128, 8], F32)
    g.scb = _sb(nc, es, "scb", [128, 8, 128], F32)
    s.dma('sp', g.ccol[:], g.d_ccol[:, :], w=['ccol'])
    s.op('act', lambda e: e.activation(out=g.sc[:], in_=g.ccol[:], func=AF.Silu), r=['ccol'], w=['sc'])
    for kc in range(8):
        s.op('dve', lambda e, kc=kc: e.tensor_scalar(out=g.scb[:, kc, :], in0=g.ones[:], scalar1=g.sc[:, kc:kc + 1],
                                                     scalar2=None, op0=ALU.mult), r=['ones', 'sc'], w=['scb'])


def emit_mods(g, L, modw, modb_row, lng, lnb):
    nc, s, es = g.nc, g.s, g.es
    L.shift = _sb(nc, L.es, "shift", [128, 8], F32)
    L.scale1 = _sb(nc, L.es, "scale1", [128, 8], F32)
    L.gate_bc = _sb(nc, L.es, "gate_bc", [128, D], F32)
    L.lng_bc = _sb(nc, L.es, "lng_bc", [128, D], F32)
    L.lnb_bc = _sb(nc, L.es, "lnb_bc", [128, D], F32)
    s.dma('sp', L.lng_bc[:], lng.partition_broadcast(128), w=['lng_bc'])
    s.dma('sp', L.lnb_bc[:], lnb.partition_broadcast(128), w=['lnb_bc'])
    with ExitStack() as es2:
        mw = [_sb(nc, es2, "mw%d" % i, [128, 3 * D], F32) for i in range(2)]
        brow = _sb(nc, es2, "brow", [1, 3 * D], F32)
        bc = _sb(nc, es2, "modbc", [128, 2 * D], F32)
        one11 = _sb(nc, es2, "one11", [1, 1], F32)
        s.op('dve', lambda e: e.memset(one11[:], 1.0), w=['one11'])
        s.dma('sp', brow[:], modb_row[:, :], w=['brow'])
        P = g.ps
        for kc in range(8):
            t = mw[kc % 2]
            nm = 'mw%d' % (kc % 2)
            s.dma('sp' if kc % 2 == 0 else 'pool', t[:], modw[kc * 128:(kc + 1) * 128, :], w=[nm])
            for j in range(6):
                s.op('pe', lambda e, j=j, kc=kc, t=t: e.matmul(P[:, j * 512:(j + 1) * 512], lhsT=g.scb[:, kc, :],
                                                            rhs=t[:, j * 512:(j + 1) * 512], start=(kc == 0), stop=False),
                     r=[nm, 'scb'], w=['ps%d' % j], inc=(j == 5))
        for j in range(6):
            s.op('pe', lambda e, j=j: e.matmul(P[:, j * 512:(j + 1) * 512], lhsT=g.ones[0:1, :],
                                               rhs=brow[0:1, j * 512:(j + 1) * 512], start=False, stop=True),
                 r=['brow', 'ones'], w=['ps%d' % j])
        for j in range(4):
            s.op('act' if j % 2 else 'dve',
                 (lambda e, j=j: e.activation(out=bc[:, j * 512:(j + 1) * 512], in_=P[:, j * 512:(j + 1) * 512], func=AF.Copy))
                 if j % 2 else
                 (lambda e, j=j: e.tensor_copy(out=bc[:, j * 512:(j + 1) * 512], in_=P[:, j * 512:(j + 1) * 512])),
                 r=['ps%d' % j], w=['modbc%d' % j])
        for j in range(2):
            s.op('dve', lambda e, j=j: e.tensor_copy(out=L.gate_bc[:, j * 512:(j + 1) * 512], in_=P[:, (4 + j) * 512:(5 + j) * 512]),
                 r=['ps%d' % (4 + j)], w=['gate_bc'])
        for j in range(16):
            s.op('pe', lambda e, j=j: e.matmul(P[:, 6 * 512 + j:6 * 512 + j + 1], lhsT=bc[0:1, j * 128:(j + 1) * 128],
                                               rhs=one11[0:1, 0:1], start=True, stop=True),
                 r=['modbc%d' % (j // 4), 'one11'], w=['ps6'])
        s.op('dve', lambda e: e.tensor_copy(out=L.shift[:], in_=P[:, 6 * 512:6 * 512 + 8]), r=['ps6'], w=['shift'])
        s.op('dve', lambda e: e.tensor_scalar(out=L.scale1[:], in0=P[:, 6 * 512 + 8:6 * 512 + 16], scalar1=1.0, scalar2=None,
                                              op0=ALU.add), r=['ps6'], w=['scale1'])
        s.barrier()


def emit_hT(g, L, xin, xin_nm, hT_ap_fn, hT_nm, pbanks):
    s = g.s
    P = g.ps
    for kc in range(8):
        b = pbanks[kc // 4]
        off = b * 512 + (kc % 4) * 128
        s.op('pe', lambda e, kc=kc, off=off: e.transpose(P[:, off:off + 128], xin[:, kc * 128:(kc + 1) * 128], g.ident[:]),
             r=[xin_nm, 'ident'], w=['ps%d' % b])
    for kc in range(8):
        b = pbanks[kc // 4]
        off = b * 512 + (kc % 4) * 128
        s.op('act', lambda e, kc=kc, off=off: e.activation(out=hT_ap_fn(kc), in_=P[:, off:off + 128], func=AF.Identity,
                                                           scale=L.scale1[:, kc:kc + 1], bias=L.shift[:, kc:kc + 1]),
             r=['ps%d' % b, 'scale1', 'shift'], w=[hT_nm])


def emit_epilogue(g, L, ybanks, xres, xres_nm, zb, xo, xo_nm, dst_rows):
    s = g.s
    P = g.ps
    for j in range(2):
        b = ybanks[j]
        s.op('dve', lambda e, j=j, b=b: e.tensor_tensor(out=zb[:, j * 512:(j + 1) * 512], in0=P[:, b * 512:(b + 1) * 512],
                                                        in1=L.gate_bc[:, j * 512:(j + 1) * 512], op=ALU.mult),
             r=['ps%d' % b, 'gate_bc'], w=['zb'])
    s.op('dve', lambda e: e.scalar_tensor_tensor(out=zb[:], in0=xres[:], scalar=ALPHA, in1=zb[:], op0=ALU.mult, op1=ALU.add),
         r=[xres_nm, 'zb'], w=['zb'])
    st = g.lnst
    for j in range(2):
        s.op('dve', lambda e, j=j: e.bn_stats(out=st[:, j * 6:(j + 1) * 6], in_=zb[:, j * 512:(j + 1) * 512]), r=['zb'], w=['lnst'])
    s.op('dve', lambda e: e.bn_aggr(out=g.lnmv[:, 0:2], in_=st[:, 0:12]), r=['lnst'], w=['lnmv'])
    s.op('dve', lambda e: e.tensor_scalar(out=g.lnmv[:, 2:3], in0=g.lnmv[:, 1:2], scalar1=LN_EPS, scalar2=None, op0=ALU.add),
         r=['lnmv'], w=['lnmv2'])
    s.op('act', lambda e: e.activation(out=g.lnmv[:, 3:4], in_=g.lnmv[:, 2:3], func=AF.Sqrt), r=['lnmv2'], w=['lnmv3'])
    s.op('dve', lambda e: e.reciprocal(out=g.lnmv[:, 4:5], in_=g.lnmv[:, 3:4]), r=['lnmv3'], w=['lnmv4'])
    s.op('dve', lambda e: e.tensor_scalar(out=zb[:], in0=zb[:], scalar1=g.lnmv[:, 0:1], scalar2=g.lnmv[:, 4:5],
                                          op0=ALU.subtract, op1=ALU.mult), r=['zb', 'lnmv', 'lnmv4'], w=['zb'])
    s.op('pool', lambda e: e.tensor_tensor(out=zb[:], in0=zb[:], in1=L.lng_bc[:], op=ALU.mult), r=['zb', 'lng_bc'], w=['zb'])
    s.op('pool', lambda e: e.tensor_tensor(out=xo[:], in0=zb[:], in1=L.lnb_bc[:], op=ALU.add), r=['zb', 'lnb_bc'], w=[xo_nm])
    s.dma('sp', dst_rows, xo[:], r=[xo_nm], w=[])


def emit_ffn(g, li, src, dst):
    nc, s = g.nc, g.s
    TT = 256
    NT = S // TT
    NBT = TT // 128
    with ExitStack() as les:
        L = Ctx()
        L.es = les
        emit_mods(g, L, g.d_f_mod_w[li], g.d_f_mod_b[li], g.d_f_ln_g[li], g.d_f_ln_b[li])
        wup = _sb(nc, les, "wup", [128, 8, 2 * DFF], BF16)
        wdn = _sb(nc, les, "wdn", [128, NFC, D], BF16)
        cw = _sb(nc, les, "cw", [128, 2 * NFC, 4], F32)
        halo = _sb(nc, les, "halo", [128, 2 * NFC, 2], F32)
        hT = [_sb(nc, les, "hT%d" % i, [128, 8, TT], BF16) for i in range(2)]
        gT = _sb(nc, les, "gT", [128, NFC, TT], BF16)
        upre = [_sb(nc, les, "upre%d" % i, [128, TT + 2], F32) for i in range(2)]
        c0 = [_sb(nc, les, "c0%d" % i, [128, TT], F32) for i in range(2)]
        asil = _sb(nc, les, "asil", [128, TT], F32)
        xin = [_sb(nc, les, "xin%d" % i, [128, D], F32) for i in range(2)]
        xrs = [_sb(nc, les, "xrs%d" % i, [128, D], F32) for i in range(2)]
        zb = _sb(nc, les, "zb", [128, D], F32)
        xo = [_sb(nc, les, "xo%d" % i, [128, D], F32) for i in range(2)]
        P = g.ps
        wupd = g.d_f_w_up[li].rearrange("(kc p) n -> p kc n", p=128)
        for kc in range(8):
            for hf in range(2):
                s.dma('pool', wup[:, kc, hf * DFF:(hf + 1) * DFF], wupd[:, kc, hf * DFF:(hf + 1) * DFF], w=['wup'])
        wdnd = g.d_f_w_down[li].rearrange("(fc p) n -> p fc n", p=128)
        for fc in range(NFC):
            s.dma('pool', wdn[:, fc, :], wdnd[:, fc, :], w=['wdn'])
        s.dma('sp', cw[:], g.d_f_cw[li], w=['cw'])
        s.op('dve', lambda e: e.memset(halo[:], 0.0), w=['halo'])
        blk = 0
        for t in range(NT):
            t0 = t * TT
            hs = t % 2
            hnm = 'hT%d' % hs
            for bi in range(NBT):
                xs_ = (t * NBT + bi) % 2
                s.dma('sp', xin[xs_][:], src[t0 + bi * 128:t0 + (bi + 1) * 128, :], w=['xin%d' % xs_])
                emit_hT(g, L, xin[xs_], 'xin%d' % xs_, lambda kc, bi=bi, hs=hs: hT[hs][:, kc, bi * 128:(bi + 1) * 128], hnm, (0, 1))
            for j in range(NFC):
                for half in range(2):
                    fc = j + half * NFC
                    b = 2 + ((2 * j + half) % 4)
                    pb = 'ps%d' % b
                    up = upre[half]
                    unm = 'upre%d' % half
                    for kc in range(8):
                        s.op('pe', lambda e, kc=kc, fc=fc, b=b: e.matmul(P[:, b * 512:b * 512 + TT], lhsT=wup[:, kc, fc * 128:(fc + 1) * 128],
                                                                      rhs=hT[hs][:, kc, :], start=(kc == 0), stop=(kc == 7)),
                             r=['wup', hnm], w=[pb], inc=(kc == 7))
                    s.op('pool', lambda e, fc=fc, up=up: e.tensor_copy(out=up[:, 0:2], in_=halo[:, fc, :]), r=['halo'], w=[unm])
                    s.op('act', lambda e, b=b, up=up: e.activation(out=up[:, 2:TT + 2], in_=P[:, b * 512:b * 512 + TT], func=AF.Copy),
                         r=[pb], w=[unm])
                    s.op('act', lambda e, b=b, fc=fc, half=half: e.activation(out=c0[half][:], in_=P[:, b * 512:b * 512 + TT], func=AF.Identity,
                                                                             scale=cw[:, fc, 2:3], bias=cw[:, fc, 3:4]),
                         r=[pb, 'cw'], w=['c0%d' % half])
                    s.op('pool', lambda e, fc=fc, up=up: e.tensor_copy(out=halo[:, fc, :], in_=up[:, TT:TT + 2]), r=[unm], w=['halo'])
                    s.op('dve', lambda e, fc=fc, up=up, half=half: e.scalar_tensor_tensor(out=c0[half][:], in0=up[:, 1:TT + 1], scalar=cw[:, fc, 1:2],
                                                                                        in1=c0[half][:], op0=ALU.mult, op1=ALU.add),
                         r=[unm, 'cw', 'c0%d' % half], w=['c0%d' % half])
                    s.op('dve', lambda e, fc=fc, up=up, half=half: e.scalar_tensor_tensor(out=c0[half][:], in0=up[:, 0:TT], scalar=cw[:, fc, 0:1],
                                                                                        in1=c0[half][:], op0=ALU.mult, op1=ALU.add),
                         r=[unm, 'cw', 'c0%d' % half], w=['c0%d' % half])
                    if half == 0:
                        s.op('act', lambda e: e.activation(out=asil[:], in_=c0[0][:], func=AF.Silu), r=['c00'], w=['asil'])
                    else:
                        s.op('dve', lambda e, j=j: e.tensor_tensor(out=gT[:, j, :], in0=asil[:], in1=c0[1][:], op=ALU.mult),
                             r=['asil', 'c01'], w=['gT'])
            for bi in range(NBT):
                r0 = t0 + bi * 128
                xs_ = blk % 2
                blk += 1
                s.dma('sp', xrs[xs_][:], src[r0:r0 + 128, :], w=['xrs%d' % xs_])
                for half in range(2):
                    b = 6 + half
                    for j in range(NFC):
                        s.op('pe', lambda e, j=j, half=half, b=b, bi=bi: e.matmul(P[:, b * 512:(b + 1) * 512], lhsT=gT[:, j, bi * 128:(bi + 1) * 128],
                                                                                  rhs=wdn[:, j, half * 512:(half + 1) * 512],
                                                                                  start=(j == 0), stop=(j == NFC - 1)),
                             r=['gT', 'wdn'], w=['ps%d' % b], inc=(j == NFC - 1))
                emit_epilogue(g, L, (6, 7), xrs[xs_], 'xrs%d' % xs_, zb, xo[xs_], 'xo%d' % xs_, dst[r0:r0 + 128, :])
        s.barrier()


NEG = -1.0e30
DSTOP = int(os.environ.get('DSA_STOP', '99'))
DSUB = int(os.environ.get('DSA_SUB', '99'))
GSTOP = int(os.environ.get('GDN_STOP', '99'))
DSC = int(os.environ.get('DSA_SC', '3'))
DSKIP = os.environ.get('DSA_SKIP', '').split(',')
REP = -3.0e38


def emit_dsa(g, src, dst):
    nc, s = g.nc, g.s
    P = g.ps
    with ExitStack() as les:
        L = Ctx()
        L.es = les
        emit_mods(g, L, g.d_e_mod_w[0], g.d_e_mod_b[0], g.d_e_ln_g[0], g.d_e_ln_b[0])
        A = lambda name, shape, dt: _sb(nc, les, name, shape, dt)
        win = A("win", [128, 8, 1536], BF16)
        wif = A("wif", [128, 8, 4], F32)
        wiw = A("wiw", [128, 8, 4], BF16)
        poolw = A("poolw", [128, 4, 128], BF16)
        pscale = A("pscale", [128, 4], F32)
        kvn_bc = A("kvn_bc", [128, 128], F32)
        ukT = A("ukT", [128, 4, 128], BF16)
        uvpad = A("uvpad", [128, 8, 128], BF16)
        wout = A("wout", [128, 8, D], BF16)
        negI4 = A("negI4", [128, 512], BF16)
        corr = A("corr", [128, 4, 15], F32)
        ckvn_all = A("ckvn_all", [128, NB, 128], BF16)
        ckvnT_all = A("ckvnT_all", [128, S], BF16)
        kiT_all = A("kiT_all", [128, S], BF16)
        xin = [A("xin%d" % i, [128, D], F32) for i in range(2)]
        hT = A("hT", [128, 8, 128], BF16)
        ut = A("ut", [128, 4, 143], F32)
        ta = A("ta", [128, 143], F32)
        tb = A("tb", [128, 143], F32)
        dT = A("dT", [128, 4, 128], BF16)
        qT = A("qT", [128, 512], BF16)
        qiT = [A("qiT%d" % i, [128, 256], BF16) for i in range(2)]
        wis = [A("wis%d" % i, [128, 4], F32) for i in range(2)]
        qlT = [A("qlT%d" % i, [128, 1024], BF16) for i in range(2)]
        W = [A("W%d" % i, [128, S], F32) for i in range(2)]
        rbuf = [A("rbuf%d" % i, [128, 512], F32) for i in range(2)]
        rb2 = A("rb2", [128, 512], F32)
        notm = [A("notm%d" % i, [128, S], BF16) for i in range(2)]
        m8 = A("m8", [128, 8], F32)
        pT = [A("pT%d" % i, [128, 512], BF16) for i in range(2)]
        rden = A("rden", [128, 512], F32)
        oTn = A("oTn", [128, 1024], BF16)
        yinT = [A("yinT%d" % i, [128, 1024], BF16) for i in range(2)]
        zb = A("zb", [128, D], F32)
        xo = [A("xo%d" % i, [128, D], F32) for i in range(2)]
        sq = A("sq", [128, 128], F32)
        ckf = A("ckf", [128, 128], F32)
        rs = A("rs", [128, 4], F32)
        Pb3 = P[:, 3 * 512:4 * 512].bitcast(BF16)

        wind = g.d_e_w_in[0].rearrange("(kc p) n -> p kc n", p=128)
        for kc in range(8):
            if 'win' not in DSKIP:
                s.dma('pool', win[:, kc, 0:1472], wind[:, kc, 0:1472], w=['win'])
            if 'win2' not in DSKIP:
                s.dma('pool', win[:, kc, 1472:1536], wind[:, kc, 1408:1472], w=['win'])
        s.dma('sp', wif[:], wind[:, :, 1472:1476], w=['wif'])
        s.op('dve', lambda e: e.tensor_copy(out=wiw[:], in_=wif[:]), r=['wif'], w=['wiw'])
        if 'poolw' not in DSKIP:
            s.dma('pool', poolw[:], g.d_e_pool_w[0].rearrange("g c d -> c g d"), w=['poolw'])
        if 'pscale' not in DSKIP:
            s.dma('sp', pscale[:], g.d_pscale[:, :], w=['pscale'])
        if 'kvn' not in DSKIP:
            s.dma('sp', kvn_bc[:], g.d_e_kv_norm[0].partition_broadcast(128), w=['kvn_bc'])
        if 'ukT' not in DSKIP:
            s.dma('pool', ukT[:], g.d_ukT[:, :, :], w=['ukT'])
        if 'uvpad' not in DSKIP:
            s.dma('pool', uvpad[:], g.d_uvpad[:, :, :], w=['uvpad'])
        woutd = g.d_e_w_out[0].rearrange("(kc p) n -> p kc n", p=128)
        if 'wout' not in DSKIP:
            for kc in range(8):
                s.dma('pool', wout[:, kc, :], woutd[:, kc, :], w=['wout'])
        if 'negI4' not in DSKIP:
            s.dma('pool', negI4[:], g.d_negI4[:, :], w=['negI4'])
        if 'corr' not in DSKIP:
            s.dma('sp', corr[:], g.d_corr[:, :, :], w=['corr'])
        s.op('dve', lambda e: e.memset(ut[:], 0.0), w=['ut'])

        def front(qb):
            sl = qb % 2
            t0 = qb * 128
            nk = t0 + 128
            xn = 'xin%d' % sl
            s.dma('sp', xin[sl][:], src[t0:t0 + 128, :], w=[xn])
            emit_hT(g, L, xin[sl], xn, lambda kc: hT[:, kc, :], 'hT', (0, 1))
            if DSTOP <= 1:
                return
            def grp(out_ap, cols, bank, last=True):
                for kc in range(8):
                    s.op('pe', lambda e, kc=kc: e.matmul(out_ap, lhsT=win[:, kc, cols[0]:cols[1]], rhs=hT[:, kc, :],
                                                        start=(kc == 0), stop=(kc == 7)),
                         r=['win', 'hT'], w=['ps%d' % bank], inc=(kc == 7))
            for gi in range(4):
                grp(P[:, gi * 128:(gi + 1) * 128], (gi * 128, (gi + 1) * 128), 0)
            for j in range(4):
                grp(P[:, 512 + j * 128:512 + (j + 1) * 128], (512 + j * 128, 512 + (j + 1) * 128), 1)
            for j in range(2):
                grp(P[:, 1024 + j * 128:1024 + (j + 1) * 128], (1152 + j * 128, 1152 + (j + 1) * 128), 2)
            grp(P[:, 1024 + 256:1024 + 384], (1408, 1536), 2)
            for kc in range(8):
                s.op('pe', lambda e, kc=kc: e.matmul(P[:, 1536:1536 + 128], lhsT=hT[:, kc, :], rhs=win[:, kc, 1024:1152],
                                                    start=(kc == 0), stop=(kc == 7)), r=['win', 'hT'], w=['ps3'], inc=(kc == 7))
            for kc in range(8):
                s.op('pe', lambda e, kc=kc: e.matmul(P[:, 1536 + 128:1536 + 132], lhsT=hT[:, kc, :], rhs=wiw[:, kc, :],
                                                    start=(kc == 0), stop=(kc == 7)), r=['win', 'hT'], w=['ps3'], inc=(kc == 7))
            if DSTOP <= 2:
                return
            s.op('act', lambda e: e.activation(out=ut[:, :, 15:143], in_=P[:, 0:512].rearrange("p (g t) -> p g t", g=4), func=AF.Copy),
                 r=['ps0'], w=['ut'])
            s.op('act', lambda e: e.activation(out=qT[:], in_=P[:, 512:1024], func=AF.Copy), r=['ps1'], w=['qT'])
            s.op('dve', lambda e: e.tensor_copy(out=qiT[sl][:], in_=P[:, 1024:1024 + 256]), r=['ps2'], w=['qiT%d' % sl])
            s.op('dve', lambda e: e.tensor_copy(out=kiT_all[:, t0:t0 + 128], in_=P[:, 1024 + 256:1024 + 384]), r=['ps2'], w=['kiT_all'])
            s.op('dve', lambda e: e.tensor_copy(out=wis[sl][:], in_=P[:, 1536 + 128:1536 + 132]), r=['ps3'], w=['wis%d' % sl])
            if DSTOP <= 3:
                return
            if DSUB < 1:
                return
            s.op('act', lambda e: e.activation(out=ckf[:], in_=P[:, 1536:1536 + 128], func=AF.Copy), r=['ps3'], w=['ckf'])
            if DSUB < 2:
                return
            s.op('dve', lambda e: e.tensor_tensor(out=sq[:], in0=ckf[:], in1=ckf[:], op=ALU.mult), r=['ckf'], w=['sq'])
            if DSUB < 2:
                return
            s.op('dve', lambda e: e.reduce_sum(out=rs[:, 0:1], in_=sq[:], axis=AX.X), r=['sq'], w=['rs0'])
            if DSUB < 3:
                return
            s.op('dve', lambda e: e.tensor_scalar(out=rs[:, 1:2], in0=rs[:, 0:1], scalar1=1.0 / 128, scalar2=RMS_EPS, op0=ALU.mult, op1=ALU.add),
                 r=['rs0'], w=['rs1'])
            if DSUB < 4:
                return
            s.op('act', lambda e: e.activation(out=rs[:, 2:3], in_=rs[:, 1:2], func=AF.Sqrt), r=['rs1'], w=['rs2'])
            if DSUB < 5:
                return
            s.op('dve', lambda e: e.reciprocal(out=rs[:, 3:4], in_=rs[:, 2:3]), r=['rs2'], w=['rs3'])
            if DSUB < 6:
                return
            s.op('dve', lambda e: e.scalar_tensor_tensor(out=ckf[:], in0=ckf[:], scalar=rs[:, 3:4], in1=kvn_bc[:],
                                                         op0=ALU.mult, op1=ALU.mult), r=['ckf', 'rs3', 'kvn_bc'], w=['ckf'])
            if DSUB < 7:
                return
            s.op('act', lambda e: e.activation(out=ckvn_all[:, qb, :], in_=ckf[:], func=AF.Copy), r=['ckf'], w=['ckvn_all'])
            if DSUB < 8:
                return
            s.op('pe', lambda e: e.transpose(P[:, 1536 + 256:1536 + 384], ckf[:], g.ident[:]), r=['ckf', 'ident'], w=['ps3'])
            if DSUB < 9:
                return
            s.op('act', lambda e: e.activation(out=ckvnT_all[:, t0:t0 + 128], in_=P[:, 1536 + 256:1536 + 384], func=AF.Copy), r=['ps3'], w=['ckvnT_all'])
            if DSTOP <= 4:
                return
            for gi in range(4):
                win_ = 2 << gi
                U = ut[:, gi, :]
                s.op('dve', lambda e, U=U: e.tensor_tensor(out=ta[:, 1:143], in0=U[:, 1:143], in1=U[:, 0:142], op=ALU.add), r=['ut'], w=['ta'])
                sw = ta
                swn = 'ta'
                if gi >= 1:
                    s.op('dve', lambda e: e.tensor_tensor(out=tb[:, 3:143], in0=ta[:, 3:143], in1=ta[:, 1:141], op=ALU.add), r=['ta'], w=['tb'])
                    sw, swn = tb, 'tb'
                if gi >= 2:
                    s.op('dve', lambda e: e.tensor_tensor(out=ta[:, 7:143], in0=tb[:, 7:143], in1=tb[:, 3:139], op=ALU.add), r=['tb'], w=['ta'])
                    sw, swn = ta, 'ta'
                if gi >= 3:
                    s.op('dve', lambda e: e.tensor_tensor(out=tb[:, 15:143], in0=ta[:, 15:143], in1=ta[:, 7:135], op=ALU.add), r=['ta'], w=['tb'])
                    sw, swn = tb, 'tb'
                if qb == 0:
                    s.op('dve', lambda e, sw=sw, gi=gi, win_=win_: e.tensor_tensor(out=sw[:, 15:15 + win_ - 1], in0=sw[:, 15:15 + win_ - 1],
                                                                                 in1=corr[:, gi, 0:win_ - 1], op=ALU.mult),
                         r=[swn, 'corr'], w=[swn])
                s.op('dve', lambda e, sw=sw, gi=gi, win_=win_, U=U: e.scalar_tensor_tensor(out=dT[:, gi, :], in0=sw[:, 15:143], scalar=1.0 / win_,
                                                                                          in1=U[:, 15:143], op0=ALU.mult, op1=ALU.subtract),
                     r=[swn, 'ut'], w=['dT'])
            s.op('pool', lambda e: e.tensor_copy(out=ut[:, :, 0:15], in_=ut[:, :, 128:143]), r=['ut'], w=['ut'])
            for gi in range(4):
                s.op('pe', lambda e, gi=gi: e.matmul(P[:, gi * 128:(gi + 1) * 128], lhsT=poolw[:, gi, :], rhs=dT[:, gi, :], start=True, stop=True),
                     r=['poolw', 'dT'], w=['ps0'])
            for gi in range(4):
                s.op('act', lambda e, gi=gi: e.activation(out=yinT[sl][:, gi * 128:(gi + 1) * 128], in_=P[:, gi * 128:(gi + 1) * 128],
                                                          func=AF.Identity, scale=pscale[:, gi:gi + 1]),
                     r=['ps0', 'pscale'], w=['yinT%d' % sl])
            if DSTOP <= 5:
                return
            for h in range(8):
                po = (h % 2) * 64
                bank = 1 + h % 2
                off = bank * 512 + (h // 2) * 128
                s.op('pe', lambda e, h=h, po=po, off=off: e.matmul(P[:, off:off + 128], lhsT=ukT[po:po + 64, h // 2, :],
                                                                  rhs=qT[po:po + 64, (h // 2) * 128:(h // 2 + 1) * 128], start=True, stop=True),
                     r=['ukT', 'qT'], w=['ps%d' % bank])
            for j in range(2):
                s.op('act', lambda e, j=j: e.activation(out=qlT[sl][:, j * 512:(j + 1) * 512], in_=P[:, (1 + j) * 512:(2 + j) * 512], func=AF.Copy),
                     r=['ps%d' % (1 + j)], w=['qlT%d' % sl])
            if DSTOP <= 6:
                return
            Wn = 'W%d' % sl
            cnt = 0
            chunks = []
            k0_ = 0
            while k0_ < nk:
                rem = nk - k0_
                w0 = 512 if rem >= 512 else (256 if rem >= 256 else 128)
                chunks.append((k0_, w0))
                k0_ += w0
            for (k0, w_) in chunks:
                for h in range(4):
                    po = (h % 2) * 64
                    bank = 2 + cnt % 2
                    rb = rbuf[cnt % 2]
                    rbn = 'rbuf%d' % (cnt % 2)
                    cnt += 1
                    s.op('pe', lambda e, h=h, po=po, bank=bank, k0=k0, w_=w_: e.matmul(P[:, bank * 512:bank * 512 + w_],
                                                                                     lhsT=qiT[sl][po:po + 64, (h // 2) * 128:(h // 2 + 1) * 128],
                                                                                     rhs=kiT_all[po:po + 64, k0:k0 + w_], start=True, stop=True),
                         r=['qiT%d' % sl, 'kiT_all'], w=['ps%d' % bank])
                    if DSC < 2:
                        continue
                    s.op('act', lambda e, bank=bank, rb=rb, w_=w_: e.activation(out=rb[:, 0:w_], in_=P[:, bank * 512:bank * 512 + w_], func=AF.Relu),
                         r=['ps%d' % bank], w=[rbn])
                    if DSC < 3:
                        continue
                    if h == 0:
                        s.op('dve', lambda e, rb=rb, k0=k0, w_=w_: e.tensor_scalar(out=W[sl][:, k0:k0 + w_], in0=rb[:, 0:w_], scalar1=wis[sl][:, 0:1],
                                                                                 scalar2=None, op0=ALU.mult), r=[rbn, 'wis%d' % sl], w=[Wn])
                    else:
                        s.op('dve', lambda e, rb=rb, k0=k0, w_=w_, h=h: e.scalar_tensor_tensor(out=W[sl][:, k0:k0 + w_], in0=rb[:, 0:w_], scalar=wis[sl][:, h:h + 1],
                                                                                            in1=W[sl][:, k0:k0 + w_], op0=ALU.mult, op1=ALU.add),
                             r=[rbn, 'wis%d' % sl, Wn], w=[Wn])
            if DSTOP <= 7:
                return
            nmn = 'notm%d' % sl
            if nk <= 256:
                s.op('dve', lambda e: e.memset(notm[sl][:, 0:nk], 0.0), w=[nmn])
            else:
                s.op('dve', lambda e: e.memset(W[sl][0:64, nk - 64:nk], NEG), r=[Wn], w=[Wn])
                for it in range(32):
                    s.op('dve', lambda e: e.max(out=m8[:], in_=W[sl][:, 0:nk]), r=[Wn], w=['m8'])
                    s.op('dve', lambda e: e.match_replace(out=W[sl][:, 0:nk], in_to_replace=m8[:], in_values=W[sl][:, 0:nk], imm_value=REP),
                         r=[Wn, 'm8'], w=[Wn])
                s.op('dve', lambda e: e.tensor_scalar(out=notm[sl][:, 0:nk], in0=W[sl][:, 0:nk], scalar1=0.5 * REP, scalar2=None, op0=ALU.is_gt),
                     r=[Wn], w=[nmn])
            s.op('dve', lambda e: e.memset(notm[sl][0:64, nk - 64:nk], 1.0), r=[nmn], w=[nmn])

        def back(qb):
            if DSTOP <= 8:
                return
            sl = qb % 2
            t0 = qb * 128
            cnt = 0
            for hg in range(2):
                for kb in range(qb + 1):
                    j = cnt % 2
                    cnt += 1
                    bank = 4 + j
                    s.op('pe', lambda e, bank=bank, kb=kb, hg=hg: e.matmul(P[:, bank * 512:(bank + 1) * 512], lhsT=ckvnT_all[:, kb * 128:(kb + 1) * 128],
                                                                          rhs=qlT[sl][:, hg * 512:(hg + 1) * 512], start=True, stop=False),
                         r=['ckvnT_all', 'qlT%d' % sl], w=['ps%d' % bank], inc=False)
                    s.op('pe', lambda e, bank=bank, kb=kb: e.matmul(P[:, bank * 512:(bank + 1) * 512], lhsT=notm[sl][:, kb * 128:(kb + 1) * 128],
                                                                   rhs=negI4[:], start=False, stop=True),
                         r=['notm%d' % sl, 'negI4'], w=['ps%d' % bank])
                    s.op('act', lambda e, bank=bank, j=j: e.activation(out=pT[j][:], in_=P[:, bank * 512:(bank + 1) * 512], func=AF.Exp, scale=0.125),
                         r=['ps%d' % bank], w=['pT%d' % j])
                    s.op('pe', lambda e, kb=kb, j=j: e.matmul(P[:, 6 * 512:7 * 512], lhsT=ckvn_all[:, kb, :], rhs=pT[j][:], start=(kb == 0), stop=(kb == qb)),
                         r=['ckvn_all', 'pT%d' % j], w=['ps6'], inc=False)
                    s.op('pe', lambda e, kb=kb, j=j: e.matmul(P[:, 7 * 512:8 * 512], lhsT=g.onesb[:], rhs=pT[j][:], start=(kb == 0), stop=(kb == qb)),
                         r=['onesb', 'pT%d' % j], w=['ps7'])
                s.op('dve', lambda e: e.reciprocal(out=rden[:], in_=P[:, 7 * 512:8 * 512]), r=['ps7'], w=['rden'])
                s.op('dve', lambda e, hg=hg: e.tensor_tensor(out=oTn[:, hg * 512:(hg + 1) * 512], in0=P[:, 6 * 512:7 * 512], in1=rden[:], op=ALU.mult),
                     r=['ps6', 'rden'], w=['oTn'])
            if DSTOP <= 9:
                return
            for hp in range(4):
                for h2 in range(2):
                    h = 2 * hp + h2
                    s.op('pe', lambda e, hp=hp, h=h, h2=h2: e.matmul(P[:, 7 * 512 + hp * 128:7 * 512 + (hp + 1) * 128], lhsT=uvpad[:, h, :],
                                                                    rhs=oTn[:, (h // 2 + 4 * (h % 2)) * 128:(h // 2 + 4 * (h % 2) + 1) * 128], start=(h2 == 0), stop=(h2 == 1)),
                         r=['uvpad', 'oTn'], w=['ps7'], inc=(h2 == 1))
            s.op('act', lambda e: e.activation(out=yinT[sl][:, 512:1024], in_=P[:, 7 * 512:8 * 512], func=AF.Copy), r=['ps7'], w=['yinT%d' % sl])
            for half in range(2):
                b = 4 + half
                for kc in range(8):
                    s.op('pe', lambda e, kc=kc, half=half, b=b: e.matmul(P[:, b * 512:(b + 1) * 512], lhsT=yinT[sl][:, kc * 128:(kc + 1) * 128],
                                                                        rhs=wout[:, kc, half * 512:(half + 1) * 512], start=(kc == 0), stop=(kc == 7)),
                         r=['yinT%d' % sl, 'wout'], w=['ps%d' % b], inc=(kc == 7))
            emit_epilogue(g, L, (4, 5), xin[sl], 'xin%d' % sl, zb, xo[sl], 'xo%d' % sl, dst[t0:t0 + 128, :])

        nblk = g.nblk_dsa
        for i in range(nblk + 1):
            if i < nblk:
                front(i)
            if i >= 1:
                back(i - 1)
        s.barrier()


def emit_gdn(g, src, dst):
    nc, s = g.nc, g.s
    P = g.ps
    with ExitStack() as les:
        L = Ctx()
        L.es = les
        emit_mods(g, L, g.d_o_mod_w[0], g.d_o_mod_b[0], g.d_o_ln_g[0], g.d_o_ln_b[0])
        A = lambda name, shape, dt: _sb(nc, les, name, shape, dt)
        win = A("gwin", [128, 8, 4096], BF16)
        wbf = A("gwbf", [128, 8, 16], F32)
        wbb = A("gwbb", [128, 8, 16], BF16)
        wout = A("gwout", [128, 8, D], BF16)
        cw = A("gcw", [128, 24, 4], F32)
        msl = A("msl", [128, 128], F32)
        mil = A("mil", [128, 128], F32)
        triu = A("triu", [128, 128], F32)
        alog = A("alog", [128, 8], F32)
        dtb = A("dtb", [128, 8], F32)
        onb = A("onb", [128, D], F32)
        xin = A("gxin", [128, D], F32)
        hT = A("ghT", [128, 8, 128], BF16)
        xpre = A("xpre", [128, 24, 131], F32)
        cbuf = A("cbuf", [128, 128], F32)
        act = A("gact", [128, 24 * 128], F32)
        sq = A("gsq", [128, 2048], F32)
        rstd = A("grstd", [128, 2048], F32)
        qkb = A("qkb", [128, 2048], BF16)
        gsil = A("gsil", [128, D], F32)
        ba = A("ba", [128, 16], F32)
        sm = A("gsm", [128, 64], F32)
        dg = A("dg", [128, 128], F32)
        E = A("gE", [128, 128], F32)
        EB = A("gEB", [128, 128], F32)
        t1 = A("gt1", [128, 128], F32)
        Nf = A("gNf", [128, 128], F32)
        atf = A("gatf", [128, 128], F32)
        Pk = [A("gPk%d" % i, [128, 128], F32) for i in range(2)]
        Pt = [A("gPt%d" % i, [128, 128], F32) for i in range(2)]
        Rp = A("gRp", [128, 128], F32)
        Rpb = A("gRpb", [128, 128], BF16)
        attT = A("gattT", [128, 128], BF16)
        qdT = A("gqdT", [128, 128], BF16)
        kd = A("gkd", [128, 128], BF16)
        Kbg = A("gKbg", [128, 128], BF16)
        Vb = A("gVb", [128, 128], BF16)
        nwT = A("gnwT", [128, 128], BF16)
        vnew = A("gvnew", [128, 128], BF16)
        Sf = A("gSf", [128, 8, 128], F32)
        Sb = A("gSb", [128, 8, 128], BF16)
        osb = A("gosb", [128, D], F32)
        yinT = A("gyinT", [128, 1024], BF16)
        zb = A("gzb", [128, D], F32)
        xo = [A("gxo%d" % i, [128, D], F32) for i in range(2)]
        wind = g.d_o_w_in[0].rearrange("(kc p) n -> p kc n", p=128)
        for kc in range(8):
            for q4 in range(2):
                s.dma('pool', win[:, kc, q4 * 2048:(q4 + 1) * 2048], wind[:, kc, q4 * 2048:(q4 + 1) * 2048], w=['gwin'])
        s.dma('sp', wbf[:], wind[:, :, 4096:4112], w=['gwbf'])
        s.op('dve', lambda e: e.tensor_copy(out=wbb[:], in_=wbf[:]), r=['gwbf'], w=['gwbb'])
        woutd = g.d_o_w_out[0].rearrange("(kc p) n -> p kc n", p=128)
        for kc in range(8):
            s.dma('pool', wout[:, kc, :], woutd[:, kc, :], w=['gwout'])
        s.dma('sp', cw[:], g.d_o_cw[:, :, :], w=['gcw'])
        s.dma('sp', msl[:], g.d_msl[:, :], w=['msl'])
        s.dma('sp', mil[:], g.d_mil[:, :], w=['mil'])
        s.dma('sp', triu[:], g.d_triu[:, :], w=['triu'])
        s.dma('sp', alog[:], g.d_o_a_log[0].partition_broadcast(128), w=['alog'])
        s.dma('sp', dtb[:], g.d_o_dt_bias[0].partition_broadcast(128), w=['dtb'])
        s.dma('sp', onb[:], g.d_onb[0].partition_broadcast(128), w=['onb'])
        s.op('act', lambda e: e.activation(out=alog[:], in_=alog[:], func=AF.Exp), r=['alog'], w=['alog'])
        s.op('dve', lambda e: e.tensor_scalar(out=alog[:], in0=alog[:], scalar1=-1.0, scalar2=None, op0=ALU.mult), r=['alog'], w=['alog'])
        s.op('dve', lambda e: e.memset(xpre[:], 0.0), w=['xpre'])
        s.op('dve', lambda e: e.memset(Sf[:], 0.0), w=['gSf'])
        s.op('dve', lambda e: e.memset(Sb[:], 0.0), w=['gSb'])
        DK = float(128 ** -0.5)

        def block(qb):
            t0 = qb * 128
            s.dma('sp', xin[:], src[t0:t0 + 128, :], w=['gxin'])
            emit_hT(g, L, xin, 'gxin', lambda kc: hT[:, kc, :], 'ghT', (0, 1))
            if GSTOP <= 1:
                return
            for ch in range(24):
                b = ch % 2
                for kc in range(8):
                    s.op('pe', lambda e, kc=kc, ch=ch, b=b: e.matmul(P[:, b * 512:b * 512 + 128], lhsT=win[:, kc, ch * 128:(ch + 1) * 128], rhs=hT[:, kc, :],
                                                                    start=(kc == 0), stop=(kc == 7)), r=['gwin', 'ghT'], w=['ps%d' % b], inc=(kc == 7))
                s.op('act', lambda e, ch=ch, b=b: e.activation(out=xpre[:, ch, 3:131], in_=P[:, b * 512:b * 512 + 128], func=AF.Copy), r=['ps%d' % b], w=['xpre'])
                s.op('act', lambda e, ch=ch, b=b: e.activation(out=cbuf[:], in_=P[:, b * 512:b * 512 + 128], func=AF.Identity, scale=cw[:, ch, 3:4]),
                     r=['ps%d' % b, 'gcw'], w=['cbuf'])
                for j in range(3):
                    s.op('dve', lambda e, ch=ch, j=j: e.scalar_tensor_tensor(out=cbuf[:], in0=xpre[:, ch, j:j + 128], scalar=cw[:, ch, j:j + 1], in1=cbuf[:],
                                                                            op0=ALU.mult, op1=ALU.add), r=['xpre', 'gcw', 'cbuf'], w=['cbuf'])
                s.op('act', lambda e, ch=ch: e.activation(out=act[:, ch * 128:(ch + 1) * 128], in_=cbuf[:], func=AF.Silu), r=['cbuf'], w=['gact'])
            s.op('pool', lambda e: e.tensor_copy(out=xpre[:, :, 0:3], in_=xpre[:, :, 128:131]), r=['xpre'], w=['xpre'])
            if GSTOP <= 2:
                return
            s.op('dve', lambda e: e.tensor_tensor(out=sq[:], in0=act[:, 0:2048], in1=act[:, 0:2048], op=ALU.mult), r=['gact'], w=['gsq'])
            for j in range(4):
                s.op('pe', lambda e, j=j: e.matmul(P[:, (2 + j) * 512:(3 + j) * 512], lhsT=g.ones[:], rhs=sq[:, j * 512:(j + 1) * 512], start=True, stop=True),
                     r=['ones', 'gsq'], w=['ps%d' % (2 + j)])
            for j in range(4):
                s.op('dve', lambda e, j=j: e.tensor_scalar(out=rstd[:, j * 512:(j + 1) * 512], in0=P[:, (2 + j) * 512:(3 + j) * 512], scalar1=RMS_EPS, scalar2=None, op0=ALU.add),
                     r=['ps%d' % (2 + j)], w=['grstd'])
            s.op('act', lambda e: e.activation(out=rstd[:], in_=rstd[:], func=AF.Sqrt), r=['grstd'], w=['grstd'])
            s.op('dve', lambda e: e.reciprocal(out=rstd[:], in_=rstd[:]), r=['grstd'], w=['grstd'])
            s.op('dve', lambda e: e.tensor_tensor(out=act[:, 0:2048], in0=act[:, 0:2048], in1=rstd[:], op=ALU.mult), r=['gact', 'grstd'], w=['gact'])
            s.op('act', lambda e: e.activation(out=qkb[:], in_=act[:, 0:2048], func=AF.Copy), r=['gact'], w=['qkb'])
            if GSTOP <= 3:
                return
            for half in range(2):
                b = 6 + half
                for kc in range(8):
                    s.op('pe', lambda e, kc=kc, half=half, b=b: e.matmul(P[:, b * 512:(b + 1) * 512], lhsT=hT[:, kc, :],
                                                                        rhs=win[:, kc, 3072 + half * 512:3072 + (half + 1) * 512],
                                                                        start=(kc == 0), stop=(kc == 7)), r=['gwin', 'ghT'], w=['ps%d' % b], inc=(kc == 7))
                s.op('act', lambda e, half=half, b=b: e.activation(out=gsil[:, half * 512:(half + 1) * 512], in_=P[:, b * 512:(b + 1) * 512], func=AF.Silu),
                     r=['ps%d' % b], w=['gsil'])
            for kc in range(8):
                s.op('pe', lambda e, kc=kc: e.matmul(P[:, 0:16], lhsT=hT[:, kc, :], rhs=wbb[:, kc, :], start=(kc == 0), stop=(kc == 7)),
                     r=['gwbb', 'ghT'], w=['ps0'], inc=(kc == 7))
            s.op('dve', lambda e: e.tensor_copy(out=ba[:], in_=P[:, 0:16]), r=['ps0'], w=['ba'])
            if GSTOP <= 4:
                return
            s.op('act', lambda e: e.activation(out=sm[:, 0:8], in_=ba[:, 0:8], func=AF.Exp, scale=-1.0), r=['ba'], w=['sm'])
            s.op('dve', lambda e: e.tensor_scalar(out=sm[:, 0:8], in0=sm[:, 0:8], scalar1=1.0, scalar2=None, op0=ALU.add), r=['sm'], w=['sm'])
            s.op('dve', lambda e: e.reciprocal(out=sm[:, 0:8], in_=sm[:, 0:8]), r=['sm'], w=['sm'])
            s.op('dve', lambda e: e.tensor_scalar(out=sm[:, 32:40], in0=sm[:, 0:8], scalar1=-1.0, scalar2=None, op0=ALU.mult), r=['sm'], w=['sm'])
            s.op('dve', lambda e: e.tensor_tensor(out=sm[:, 56:64], in0=ba[:, 8:16], in1=dtb[:], op=ALU.add), r=['ba', 'dtb', 'sm'], w=['sm'])
            s.op('act', lambda e: e.activation(out=sm[:, 56:64], in_=sm[:, 56:64], func=AF.Exp), r=['sm'], w=['sm'])
            s.op('act', lambda e: e.activation(out=sm[:, 56:64], in_=sm[:, 56:64], func=AF.Ln, bias=1.0), r=['sm'], w=['sm'])
            s.op('dve', lambda e: e.tensor_tensor(out=sm[:, 8:16], in0=sm[:, 56:64], in1=alog[:], op=ALU.mult), r=['sm', 'alog'], w=['sm'])
            s.op('pe', lambda e: e.matmul(P[:, 16:24], lhsT=triu[:], rhs=sm[:, 8:16], start=True, stop=True), r=['triu', 'sm'], w=['ps0'])
            s.op('dve', lambda e: e.tensor_copy(out=sm[:, 16:24], in_=P[:, 16:24]), r=['ps0'], w=['sm'])
            s.op('act', lambda e: e.activation(out=sm[:, 24:32], in_=sm[:, 16:24], func=AF.Exp), r=['sm'], w=['sm'])
            s.op('dve', lambda e: e.tensor_tensor(out=sm[:, 40:48], in0=sm[:, 24:32], in1=sm[:, 0:8], op=ALU.mult), r=['sm'], w=['sm'])
            if GSTOP <= 5:
                return
            for h in range(8):
                qn = act[:, h * 128:(h + 1) * 128]
                kn = act[:, (8 + h) * 128:(9 + h) * 128]
                vv = act[:, (16 + h) * 128:(17 + h) * 128]
                qnb = qkb[:, h * 128:(h + 1) * 128]
                knb = qkb[:, (8 + h) * 128:(9 + h) * 128]
                s.op('dve', lambda e, h=h: e.tensor_scalar(out=dg[:], in0=g.ident[:], scalar1=sm[:, 16 + h:17 + h], scalar2=None, op0=ALU.mult),
                     r=['ident', 'sm'], w=['dg'])
                s.op('pe', lambda e: e.matmul(P[:, 512:640], lhsT=g.ones[:], rhs=dg[:], start=True, stop=True), r=['ones', 'dg'], w=['ps1'])
                s.op('dve', lambda e, h=h: e.tensor_scalar(out=E[:], in0=P[:, 512:640], scalar1=sm[:, 16 + h:17 + h], scalar2=0.0, op0=ALU.subtract, op1=ALU.max),
                     r=['ps1', 'sm'], w=['gE'])
                s.op('act', lambda e: e.activation(out=E[:], in_=E[:], func=AF.Exp, scale=-1.0), r=['gE'], w=['gE'])
                s.op('act', lambda e: e.activation(out=EB[:], in_=P[:, 512:640], func=AF.Exp), r=['ps1'], w=['gEB'])
                s.op('act', lambda e: e.activation(out=sm[:, 56:57], in_=P[:, 512 + 127:512 + 128], func=AF.Exp), r=['ps1', 'sm'], w=['sm'])
                s.op('dve', lambda e, h=h: e.tensor_scalar(out=sm[:, 57:58], in0=P[:, 512 + 127:512 + 128], scalar1=sm[:, 16 + h:17 + h], scalar2=None, op0=ALU.subtract),
                     r=['ps1', 'sm'], w=['sm'])
                s.op('act', lambda e: e.activation(out=sm[:, 57:58], in_=sm[:, 57:58], func=AF.Exp), r=['sm'], w=['sm'])
                if GSTOP <= 6:
                    return
                s.op('pe', lambda e, knb=knb: e.matmul(P[:, 640:768], lhsT=knb, rhs=knb, start=True, stop=True), r=['qkb'], w=['ps1'])
                s.op('pe', lambda e, knb=knb, qnb=qnb: e.matmul(P[:, 768:896], lhsT=qnb, rhs=knb, start=True, stop=True), r=['qkb'], w=['ps1'])
                s.op('dve', lambda e: e.tensor_tensor(out=t1[:], in0=E[:], in1=msl[:], op=ALU.mult), r=['gE', 'msl'], w=['gt1'])
                s.op('dve', lambda e, h=h: e.scalar_tensor_tensor(out=Nf[:], in0=P[:, 640:768], scalar=sm[:, 32 + h:33 + h], in1=t1[:], op0=ALU.mult, op1=ALU.mult),
                     r=['ps1', 'sm', 'gt1'], w=['gNf'])
                s.op('dve', lambda e: e.tensor_tensor(out=t1[:], in0=E[:], in1=mil[:], op=ALU.mult), r=['gE', 'mil', 'gNf'], w=['gt1'])
                s.op('dve', lambda e: e.scalar_tensor_tensor(out=atf[:], in0=P[:, 768:896], scalar=DK, in1=t1[:], op0=ALU.mult, op1=ALU.mult),
                     r=['ps1', 'gt1'], w=['gatf'])
                if GSTOP <= 7:
                    return
                s.op('pe', lambda e: e.transpose(P[:, 1024:1152], Nf[:], g.ident[:]), r=['gNf', 'ident'], w=['ps2'])
                s.op('pe', lambda e: e.transpose(P[:, 1152:1280], atf[:], g.ident[:]), r=['gatf', 'ident'], w=['ps2'])
                s.op('pe', lambda e, kn=kn: e.transpose(P[:, 1280:1408], kn, g.ident[:]), r=['gact', 'ident'], w=['ps2'])
                s.op('pe', lambda e, vv=vv: e.transpose(P[:, 1408:1536], vv, g.ident[:]), r=['gact', 'ident'], w=['ps2'])
                s.op('act', lambda e: e.activation(out=Pk[0][:], in_=Nf[:], func=AF.Copy), r=['gNf'], w=['gPk0'])
                s.op('act', lambda e: e.activation(out=Pt[0][:], in_=P[:, 1024:1152], func=AF.Copy), r=['ps2'], w=['gPt0'])
                s.op('dve', lambda e: e.tensor_tensor(out=Rp[:], in0=P[:, 1024:1152], in1=g.ident[:], op=ALU.add), r=['ps2', 'ident'], w=['gRp'])
                s.op('act', lambda e: e.activation(out=attT[:], in_=P[:, 1152:1280], func=AF.Copy), r=['ps2'], w=['gattT'])
                s.op('dve', lambda e: e.tensor_scalar(out=kd[:], in0=P[:, 1280:1408], scalar1=sm[:, 57:58], scalar2=None, op0=ALU.mult), r=['ps2', 'sm'], w=['gkd'])
                s.op('dve', lambda e, h=h: e.tensor_scalar(out=Kbg[:], in0=P[:, 1280:1408], scalar1=sm[:, 40 + h:41 + h], scalar2=None, op0=ALU.mult),
                     r=['ps2', 'sm'], w=['gKbg'])
                s.op('dve', lambda e, h=h: e.tensor_scalar(out=Vb[:], in0=P[:, 1408:1536], scalar1=sm[:, h:h + 1], scalar2=None, op0=ALU.mult), r=['ps2', 'sm'], w=['gVb'])
                s.op('dve', lambda e, qn=qn: e.scalar_tensor_tensor(out=qdT[:], in0=qn, scalar=DK, in1=EB[:], op0=ALU.mult, op1=ALU.mult),
                     r=['gact', 'gEB'], w=['gqdT'])
                if GSTOP <= 8:
                    return
                cur = 0
                for it in range(6):
                    nx = 1 - cur
                    s.op('pe', lambda e, cur=cur: e.matmul(P[:, 1536:1664], lhsT=Pt[cur][:], rhs=Pk[cur][:], start=True, stop=True),
                         r=['gPt%d' % cur, 'gPk%d' % cur], w=['ps3'])
                    s.op('act', lambda e, nx=nx: e.activation(out=Pk[nx][:], in_=P[:, 1536:1664], func=AF.Copy), r=['ps3'], w=['gPk%d' % nx])
                    if it < 5:
                        s.op('pe', lambda e, cur=cur: e.matmul(P[:, 1664:1792], lhsT=Pk[cur][:], rhs=Pt[cur][:], start=True, stop=True),
                             r=['gPt%d' % cur, 'gPk%d' % cur], w=['ps3'])
                        s.op('act', lambda e, nx=nx: e.activation(out=Pt[nx][:], in_=P[:, 1664:1792], func=AF.Copy), r=['ps3'], w=['gPt%d' % nx])
                    s.op('pe', lambda e, nx=nx: e.matmul(P[:, 1792:1920], lhsT=Pk[nx][:], rhs=Rp[:], start=True, stop=True), r=['gPk%d' % nx, 'gRp'], w=['ps3'])
                    s.op('dve', lambda e: e.tensor_tensor(out=Rp[:], in0=P[:, 1792:1920], in1=Rp[:], op=ALU.add), r=['ps3', 'gRp'], w=['gRp'])
                    cur = nx
                if GSTOP <= 9:
                    return
                s.op('act', lambda e: e.activation(out=Rpb[:], in_=Rp[:], func=AF.Copy), r=['gRp'], w=['gRpb'])
                s.op('pe', lambda e: e.matmul(P[:, 2048:2176], lhsT=Kbg[:], rhs=Rpb[:], start=True, stop=True), r=['gKbg', 'gRpb'], w=['ps4'])
                s.op('act', lambda e: e.activation(out=nwT[:], in_=P[:, 2048:2176], func=AF.Copy, scale=-1.0), r=['ps4'], w=['gnwT'])
                s.op('pe', lambda e: e.matmul(P[:, 2176:2304], lhsT=Rpb[:], rhs=Vb[:], start=True, stop=False), r=['gRpb', 'gVb'], w=['ps4'], inc=False)
                s.op('pe', lambda e, h=h: e.matmul(P[:, 2176:2304], lhsT=nwT[:], rhs=Sb[:, h, :], start=False, stop=True), r=['gnwT', 'gSb'], w=['ps4'])
                s.op('act', lambda e: e.activation(out=vnew[:], in_=P[:, 2176:2304], func=AF.Copy), r=['ps4'], w=['gvnew'])
                s.op('pe', lambda e, h=h: e.matmul(P[:, 2560 + (h % 4) * 128:2560 + (h % 4 + 1) * 128], lhsT=qdT[:], rhs=Sb[:, h, :], start=True, stop=False),
                     r=['gqdT', 'gSb'], w=['ps5'], inc=False)
                s.op('pe', lambda e, h=h: e.matmul(P[:, 2560 + (h % 4) * 128:2560 + (h % 4 + 1) * 128], lhsT=attT[:], rhs=vnew[:], start=False, stop=True),
                     r=['gattT', 'gvnew'], w=['ps5'])
                s.op('act', lambda e, h=h: e.activation(out=osb[:, h * 128:(h + 1) * 128], in_=P[:, 2560 + (h % 4) * 128:2560 + (h % 4 + 1) * 128], func=AF.Copy),
                     r=['ps5'], w=['gosb'])
                s.op('pe', lambda e: e.matmul(P[:, 2304:2432], lhsT=kd[:], rhs=vnew[:], start=True, stop=True), r=['gkd', 'gvnew'], w=['ps4'])
                s.op('dve', lambda e, h=h: e.scalar_tensor_tensor(out=Sf[:, h, :], in0=Sf[:, h, :], scalar=sm[:, 56:57], in1=P[:, 2304:2432], op0=ALU.mult, op1=ALU.add),
                     r=['gSf', 'sm', 'ps4'], w=['gSf'])
                s.op('act', lambda e, h=h: e.activation(out=Sb[:, h, :], in_=Sf[:, h, :], func=AF.Copy), r=['gSf'], w=['gSb'])
            if GSTOP <= 10:
                return
            s.op('dve', lambda e: e.tensor_tensor(out=zb[:], in0=osb[:], in1=osb[:], op=ALU.mult), r=['gosb'], w=['gzb'])
            for h in range(8):
                s.op('dve', lambda e, h=h: e.reduce_sum(out=sm[:, 48 + h:49 + h], in_=zb[:, h * 128:(h + 1) * 128], axis=AX.X), r=['gzb', 'sm'], w=['sm'])
            s.op('dve', lambda e: e.tensor_scalar(out=sm[:, 48:56], in0=sm[:, 48:56], scalar1=1.0 / 128, scalar2=RMS_EPS, op0=ALU.mult, op1=ALU.add), r=['sm'], w=['sm'])
            s.op('act', lambda e: e.activation(out=sm[:, 48:56], in_=sm[:, 48:56], func=AF.Sqrt), r=['sm'], w=['sm'])
            s.op('dve', lambda e: e.reciprocal(out=sm[:, 48:56], in_=sm[:, 48:56]), r=['sm'], w=['sm'])
            for h in range(8):
                s.op('dve', lambda e, h=h: e.tensor_scalar(out=osb[:, h * 128:(h + 1) * 128], in0=osb[:, h * 128:(h + 1) * 128], scalar1=sm[:, 48 + h:49 + h],
                                                           scalar2=None, op0=ALU.mult), r=['gosb', 'sm'], w=['gosb'])
            s.op('dve', lambda e: e.tensor_tensor(out=osb[:], in0=osb[:], in1=onb[:], op=ALU.mult), r=['gosb', 'onb'], w=['gosb'])
            s.op('dve', lambda e: e.tensor_tensor(out=osb[:], in0=osb[:], in1=gsil[:], op=ALU.mult), r=['gosb', 'gsil'], w=['gosb'])
            for kc in range(8):
                b = 6 + kc // 4
                off = b * 512 + (kc % 4) * 128
                s.op('pe', lambda e, kc=kc, off=off: e.transpose(P[:, off:off + 128], osb[:, kc * 128:(kc + 1) * 128], g.ident[:]), r=['gosb', 'ident'], w=['ps%d' % b])
            for j in range(2):
                s.op('act', lambda e, j=j: e.activation(out=yinT[:, j * 512:(j + 1) * 512], in_=P[:, (6 + j) * 512:(7 + j) * 512], func=AF.Copy),
                     r=['ps%d' % (6 + j)], w=['gyinT'])
            for half in range(2):
                b = 6 + half
                for kc in range(8):
                    s.op('pe', lambda e, kc=kc, half=half, b=b: e.matmul(P[:, b * 512:(b + 1) * 512], lhsT=yinT[:, kc * 128:(kc + 1) * 128],
                                                                        rhs=wout[:, kc, half * 512:(half + 1) * 512], start=(kc == 0), stop=(kc == 7)),
                         r=['gyinT', 'gwout'], w=['ps%d' % b], inc=(kc == 7))
            emit_epilogue(g, L, (6, 7), xin, 'gxin', zb, xo[qb % 2], 'gxo%d' % (qb % 2), dst[t0:t0 + 128, :])

        for qb in range(g.nblk_gdn):
            block(qb)
        s.barrier()

W_SPECS = [
    ("e_mod_w", [1, D, 3 * D]), ("e_mod_b", [1, 1, 3 * D]), ("e_ln_g", [1, D]), ("e_ln_b", [1, D]),
    ("o_mod_w", [1, D, 3 * D]), ("o_mod_b", [1, 1, 3 * D]), ("o_ln_g", [1, D]), ("o_ln_b", [1, D]),
    ("f_mod_w", [2, D, 3 * D]), ("f_mod_b", [2, 1, 3 * D]), ("f_ln_g", [2, D]), ("f_ln_b", [2, D]),
    ("f_w_up", [2, D, 2 * DFF]), ("f_w_down", [2, DFF, D]), ("f_cw", [2, 128, 2 * NFC, 4]),
    ("ident", [128, 128]),
    ("e_w_in", [1, D, 1476]), ("e_pool_w", [1, 4, 128, 128]), ("pscale", [128, 4]), ("e_kv_norm", [1, 128]),
    ("o_w_in", [1, D, 4112]), ("o_w_out", [1, D, D]), ("o_cw", [128, 24, 4]), ("msl", [128, 128]), ("mil", [128, 128]), ("triu", [128, 128]),
    ("o_a_log", [1, 8]), ("o_dt_bias", [1, 8]), ("onb", [1, D]),
    ("ukT", [128, 4, 128]), ("uvpad", [128, 8, 128]), ("e_w_out", [1, D, D]), ("negI4", [128, 512]), ("corr", [128, 4, 15]),
]


def build(stages):
    nc = bass.Bass("TRN2", target_bir_lowering=False)
    g = Ctx()
    g.nc = nc
    g.d_x = nc.dram_tensor("x", [S, D], F32, kind="ExternalInput").ap()
    g.d_ccol = nc.dram_tensor("ccol", [128, 8], F32, kind="ExternalInput").ap()
    for nm, shp in W_SPECS:
        setattr(g, "d_" + nm, nc.dram_tensor(nm, shp, F32, kind="ExternalInput").ap())
    g.d_out = nc.dram_tensor("out", [S, D], F32, kind="ExternalOutput").ap()
    scr = [nc.dram_tensor("xscr%d" % i, [S, D], F32, kind="Internal").ap() for i in range(3)]
    with ExitStack() as es:
        g.es = es
        g.s = Sch(nc, es)
        g.ps = es.enter_context(nc.psum_tensor("ps", [128, 8 * 512], F32))
        g.lnst = _sb(nc, es, "lnst", [128, 12], F32)
        g.lnmv = _sb(nc, es, "lnmv", [128, 8], F32)
        emit_consts(g)
        bufs = [g.d_x] + scr
        n = len(stages)
        for i, st in enumerate(stages):
            src = g.d_x if i == 0 else scr[(i - 1) % 3]
            dst = g.d_out if i == n - 1 else scr[i % 3]
            if st[0] == 'ffn':
                emit_ffn(g, st[1], src, dst)
            elif st[0] == 'gdn':
                g.nblk_gdn = st[1] if len(st) > 1 else NB
                emit_gdn(g, src, dst)
            elif st[0] == 'dsa':
                g.nblk_dsa = st[1] if len(st) > 1 else NB
                emit_dsa(g, src, dst)
            else:
                raise ValueError(st)
        g.s.finish()
    return nc


def prep_weights(inp):
    f = lambda a: np.ascontiguousarray(np.asarray(a, dtype=np.float32))
    w = {}
    for k in ("e_mod_w", "e_ln_g", "e_ln_b", "o_mod_w", "o_ln_g", "o_ln_b", "f_mod_w", "f_ln_g", "f_ln_b", "f_w_up", "f_w_down"):
        w[k] = f(inp[k])
    for k in ("e_mod_b", "o_mod_b", "f_mod_b"):
        a = f(inp[k])
        w[k] = np.ascontiguousarray(a.reshape(a.shape[0], 1, 3 * D))
    cwt = f(inp["f_conv_w"])
    cb = f(inp["f_conv_b"])
    a = np.concatenate([cwt, cb[:, None, :]], axis=1)
    a = a.reshape(2, 4, 2 * NFC, 128).transpose(0, 3, 2, 1)
    w["f_cw"] = np.ascontiguousarray(a)
    w["ident"] = np.eye(128, dtype=np.float32)
    for k in ("e_w_in", "e_pool_w", "e_kv_norm", "e_w_out"):
        w[k] = f(inp[k])
    w["pscale"] = np.ascontiguousarray(f(inp["e_pool_scale"])[0].reshape(4, 128).T)
    uk = f(inp["e_w_uk"])[0]
    w["ukT"] = np.ascontiguousarray(uk.reshape(4, 2, 128, 64).transpose(1, 3, 0, 2).reshape(128, 4, 128))
    uv = f(inp["e_w_uv"])[0]
    uvp = np.zeros((128, 8, 128), np.float32)
    for h in range(8):
        uvp[:, h, (h % 2) * 64:(h % 2) * 64 + 64] = uv[h]
    w["uvpad"] = uvp
    w["negI4"] = np.ascontiguousarray(np.tile(-30000.0 * np.eye(128, dtype=np.float32), (1, 4)))
    corr = np.ones((128, 4, 15), np.float32)
    for gi in range(4):
        win_ = 2 << gi
        for t in range(win_ - 1):
            corr[:, gi, t] = win_ / (t + 1.0)
    w["corr"] = corr
    for k in ("o_w_in", "o_w_out", "o_a_log", "o_dt_bias"):
        w[k] = f(inp[k])
    ocw = f(inp["o_conv_w"])[0]
    w["o_cw"] = np.ascontiguousarray(ocw.reshape(4, 24, 128).transpose(2, 1, 0))
    ar = np.arange(128)
    w["msl"] = (ar[:, None] > ar[None, :]).astype(np.float32)
    w["mil"] = (ar[:, None] >= ar[None, :]).astype(np.float32)
    w["triu"] = (ar[:, None] <= ar[None, :]).astype(np.float32)
    w["onb"] = np.ascontiguousarray(np.tile(f(inp["o_out_norm"])[0], 8)[None, :])
    return w


STAGES = [('dsa',), ('ffn', 0), ('gdn',), ('ffn', 1)]


def kernel(**inp):
    x = np.asarray(inp["x"], dtype=np.float32)
    c = np.asarray(inp["c"], dtype=np.float32)
    w = prep_weights(inp)
    nc = build(STAGES)
    in_maps = []
    for b in range(8):
        m = dict(w)
        m["x"] = np.ascontiguousarray(x[b])
        m["ccol"] = np.ascontiguousarray(c[b].reshape(8, 128).T)
        in_maps.append(m)
    res = run_bass_kernel_spmd(nc, in_maps, core_ids=list(range(8)))
    return np.stack([np.asarray(r["out"], dtype=np.float32) for r in res.results], axis=0)
```

```python
import os
import numpy as np
from contextlib import ExitStack
import concourse.bass as bass
import concourse.mybir as mybir
from concourse.bass_utils import run_bass_kernel_spmd

F32 = mybir.dt.float32
BF16 = mybir.dt.bfloat16
AF = mybir.ActivationFunctionType
ALU = mybir.AluOpType
AX = mybir.AxisListType

D = 1024
S = 4096
NB = S // 128
DFF = 2688
NFC = DFF // 128
ALPHA = float(4 ** 0.25)
LN_EPS = 1e-5
RMS_EPS = 1e-6
NDS = 12


class Sch:
    def __init__(self, nc, es):
        self.nc = nc
        self.E = {'pe': nc.tensor, 'act': nc.scalar, 'dve': nc.vector,
                  'pool': nc.gpsimd, 'sp': nc.sync}
        self.sem = {}
        for e in self.E:
            self.sem[e] = es.enter_context(nc.semaphore('s_' + e))
        self.cnt = {e: 0 for e in self.E}
        self.waited = {e: {} for e in self.E}
        self.lastw = {}
        self.readers = {}
        self.dq = ('sp', 'pool', 'act')
        self.dcnt = {}
        self.drr = {q: 0 for q in self.dq}
        for q in self.dq:
            for i in range(NDS):
                k = (q, i)
                self.sem[k] = es.enter_context(nc.semaphore('d_%s%d' % (q, i)))
                self.dcnt[k] = 0
        self.nwaits = 0

    def _wait(self, e, tok):
        key, val = tok
        if key == e and e == 'pe':
            return
        if self.waited[e].get(key, 0) >= val:
            return
        self.E[e].wait_ge(self.sem[key], val)
        self.waited[e][key] = val
        self.nwaits += 1

    def _collect(self, r, w, e=None):
        deps = {}

        def add(t):
            if t is None:
                return
            if deps.get(t[0], 0) < t[1]:
                deps[t[0]] = t[1]
        for x in r:
            for k, v in self.lastw.get(x, {}).items():
                add((k, v))
            if x.startswith('ps') and e is not None:
                for k, v in self.readers.get(x, {}).items():
                    if k != e:
                        add((k, v))
        for x in w:
            for k, v in self.lastw.get(x, {}).items():
                add((k, v))
            for k, v in self.readers.get(x, {}).items():
                add((k, v))
        return list(deps.items())

    def _record(self, tok, r, w):
        for x in w:
            self.lastw.setdefault(x, {})[tok[0]] = tok[1]
            self.readers[x] = {}
        for x in r:
            d = self.readers.setdefault(x, {})
            if d.get(tok[0], 0) < tok[1]:
                d[tok[0]] = tok[1]

    def op(self, e, fn, r=(), w=(), inc=True):
        for t in self._collect(r, w, e):
            self._wait(e, t)
        ins = fn(self.E[e])
        if inc:
            self.cnt[e] += 1
            ins.then_inc(self.sem[e], 1)
            tok = (e, self.cnt[e])
        else:
            tok = (e, self.cnt[e] + 1)
        self._record(tok, r, w)
        return ins

    def dma(self, q, out, in_, r=(), w=()):
        i = self.drr[q]
        self.drr[q] = (i + 1) % NDS
        k = (q, i)
        if self.dcnt[k] > 0:
            self._wait(q, (k, self.dcnt[k]))
        for t in self._collect(r, w):
            self._wait(q, t)
        ins = self.E[q].dma_start(out=out, in_=in_)
        self.dcnt[k] += 16
        ins.then_inc(self.sem[k], 16)
        tok = (k, self.dcnt[k])
        self._record(tok, r, w)
        return ins

    def barrier(self):
        toks = [(e, self.cnt[e]) for e in self.E if self.cnt[e] > 0]
        toks += [(k, v) for k, v in self.dcnt.items() if v > 0]
        for e in self.E:
            for t in toks:
                self._wait(e, t)
        self.lastw = {}
        self.readers = {}

    def finish(self):
        for k, v in self.dcnt.items():
            if v > 0:
                self._wait('sp', (k, v))


class Ctx:
    pass


def _interleave(gens):
    live = list(gens)
    while live:
        nxt = []
        for item in live:
            gen, n = item
            done = False
            for _ in range(n):
                try:
                    next(gen)
                except StopIteration:
                    done = True
                    break
            if not done:
                nxt.append(item)
        live = nxt


_UID = [0]


def _sb(nc, es, name, shape, dt):
    _UID[0] += 1
    return es.enter_context(nc.sbuf_tensor("sb%d_%s" % (_UID[0], name), list(shape), dt))


def emit_consts(g):
    nc, s, es = g.nc, g.s, g.es
    g.ident = _sb(nc, es, "ident", [128, 128], F32)
    g.identb = _sb(nc, es, "identb", [128, 128], BF16)
    g.ones = _sb(nc, es, "ones", [128, 128], F32)
    g.onesb = _sb(nc, es, "onesb", [128, 128], BF16)
    s.dma('sp', g.ident[:], g.d_ident[:, :], w=['ident'])
    s.dma('pool', g.identb[:], g.d_ident[:, :], w=['identb'])
    s.op('dve', lambda e: e.memset(g.ones[:], 1.0), w=['ones'])
    s.op('dve', lambda e: e.memset(g.onesb[:], 1.0), w=['onesb'])
    g.ccol = _sb(nc, es, "ccol", [128, 8], F32)
    g.sc = _sb(nc, es, "sc", [128, 8], F32)
    g.scb = _sb(nc, es, "scb", [128, 8, 128], F32)
    s.dma('sp', g.ccol[:], g.d_ccol[:, :], w=['ccol'])
    s.op('act', lambda e: e.activation(out=g.sc[:], in_=g.ccol[:], func=AF.Silu), r=['ccol'], w=['sc'])
    for kc in range(8):
        s.op('dve', lambda e, kc=kc: e.tensor_scalar(out=g.scb[:, kc, :], in0=g.ones[:], scalar1=g.sc[:, kc:kc + 1],
                                                     scalar2=None, op0=ALU.mult), r=['ones', 'sc'], w=['scb'])


def emit_mods(g, L, modw, modb_row, lng, lnb):
    nc, s, es = g.nc, g.s, g.es
    L.shift = _sb(nc, L.es, "shift", [128, 8], F32)
    L.scale1 = _sb(nc, L.es, "scale1", [128, 8], F32)
    L.gate_bc = _sb(nc, L.es, "gate_bc", [128, D], F32)
    L.lng_bc = _sb(nc, L.es, "lng_bc", [128, D], F32)
    L.lnb_bc = _sb(nc, L.es, "lnb_bc", [128, D], F32)
    s.dma('sp', L.lng_bc[:], lng.partition_broadcast(128), w=['lng_bc'])
    s.dma('sp', L.lnb_bc[:], lnb.partition_broadcast(128), w=['lnb_bc'])
    with ExitStack() as es2:
        mw = [_sb(nc, es2, "mw%d" % i, [128, 3 * D], F32) for i in range(2)]
        brow = _sb(nc, es2, "brow", [1, 3 * D], F32)
        bc = _sb(nc, es2, "modbc", [128, 2 * D], F32)
        one11 = _sb(nc, es2, "one11", [1, 1], F32)
        s.op('dve', lambda e: e.memset(one11[:], 1.0), w=['one11'])
        s.dma('sp', brow[:], modb_row[:, :], w=['brow'])
        P = g.ps
        for kc in range(8):
            t = mw[kc % 2]
            nm = 'mw%d' % (kc % 2)
            s.dma('sp' if kc % 2 == 0 else 'pool', t[:], modw[kc * 128:(kc + 1) * 128, :], w=[nm])
            for j in range(6):
                s.op('pe', lambda e, j=j, kc=kc, t=t: e.matmul(P[:, j * 512:(j + 1) * 512], lhsT=g.scb[:, kc, :],
                                                            rhs=t[:, j * 512:(j + 1) * 512], start=(kc == 0), stop=False),
                     r=[nm, 'scb'], w=['ps%d' % j], inc=(j == 5))
        for j in range(6):
            s.op('pe', lambda e, j=j: e.matmul(P[:, j * 512:(j + 1) * 512], lhsT=g.ones[0:1, :],
                                               rhs=brow[0:1, j * 512:(j + 1) * 512], start=False, stop=True),
                 r=['brow', 'ones'], w=['ps%d' % j])
        for j in range(4):
            s.op('act' if j % 2 else 'dve',
                 (lambda e, j=j: e.activation(out=bc[:, j * 512:(j + 1) * 512], in_=P[:, j * 512:(j + 1) * 512], func=AF.Copy))
                 if j % 2 else
                 (lambda e, j=j: e.tensor_copy(out=bc[:, j * 512:(j + 1) * 512], in_=P[:, j * 512:(j + 1) * 512])),
                 r=['ps%d' % j], w=['modbc%d' % j])
        for j in range(2):
            s.op('dve', lambda e, j=j: e.tensor_copy(out=L.gate_bc[:, j * 512:(j + 1) * 512], in_=P[:, (4 + j) * 512:(5 + j) * 512]),
                 r=['ps%d' % (4 + j)], w=['gate_bc'])
        for j in range(16):
            s.op('pe', lambda e, j=j: e.matmul(P[:, 6 * 512 + j:6 * 512 + j + 1], lhsT=bc[0:1, j * 128:(j + 1) * 128],
                                               rhs=one11[0:1, 0:1], start=True, stop=True),
                 r=['modbc%d' % (j // 4), 'one11'], w=['ps6'])
        s.op('dve', lambda e: e.tensor_copy(out=L.shift[:], in_=P[:, 6 * 512:6 * 512 + 8]), r=['ps6'], w=['shift'])
        s.op('dve', lambda e: e.tensor_scalar(out=L.scale1[:], in0=P[:, 6 * 512 + 8:6 * 512 + 16], scalar1=1.0, scalar2=None,
                                              op0=ALU.add), r=['ps6'], w=['scale1'])
        s.barrier()


def emit_hT(g, L, xin, xin_nm, hT_ap_fn, hT_nm, pbanks):
    s = g.s
    P = g.ps
    for kc in range(8):
        b = pbanks[kc // 4]
        off = b * 512 + (kc % 4) * 128
        s.op('pe', lambda e, kc=kc, off=off: e.transpose(P[:, off:off + 128], xin[:, kc * 128:(kc + 1) * 128], g.ident[:]),
             r=[xin_nm, 'ident'], w=['ps%d' % b])
    for kc in range(8):
        b = pbanks[kc // 4]
        off = b * 512 + (kc % 4) * 128
        s.op('act', lambda e, kc=kc, off=off: e.activation(out=hT_ap_fn(kc), in_=P[:, off:off + 128], func=AF.Identity,
                                                           scale=L.scale1[:, kc:kc + 1], bias=L.shift[:, kc:kc + 1]),
             r=['ps%d' % b, 'scale1', 'shift'], w=[hT_nm])


def emit_epilogue(g, L, ybanks, xres, xres_nm, zb, xo, xo_nm, dst_rows):
    s = g.s
    P = g.ps
    for j in range(2):
        b = ybanks[j]
        s.op('dve', lambda e, j=j, b=b: e.tensor_tensor(out=zb[:, j * 512:(j + 1) * 512], in0=P[:, b * 512:(b + 1) * 512],
                                                        in1=L.gate_bc[:, j * 512:(j + 1) * 512], op=ALU.mult),
             r=['ps%d' % b, 'gate_bc'], w=['zb'])
    s.op('dve', lambda e: e.scalar_tensor_tensor(out=zb[:], in0=xres[:], scalar=ALPHA, in1=zb[:], op0=ALU.mult, op1=ALU.add),
         r=[xres_nm, 'zb'], w=['zb'])
    st = g.lnst
    for j in range(2):
        s.op('dve', lambda e, j=j: e.bn_stats(out=st[:, j * 6:(j + 1) * 6], in_=zb[:, j * 512:(j + 1) * 512]), r=['zb'], w=['lnst'])
    s.op('dve', lambda e: e.bn_aggr(out=g.lnmv[:, 0:2], in_=st[:, 0:12]), r=['lnst'], w=['lnmv'])
    s.op('dve', lambda e: e.tensor_scalar(out=g.lnmv[:, 2:3], in0=g.lnmv[:, 1:2], scalar1=LN_EPS, scalar2=None, op0=ALU.add),
         r=['lnmv'], w=['lnmv2'])
    s.op('act', lambda e: e.activation(out=g.lnmv[:, 3:4], in_=g.lnmv[:, 2:3], func=AF.Sqrt), r=['lnmv2'], w=['lnmv3'])
    s.op('dve', lambda e: e.reciprocal(out=g.lnmv[:, 4:5], in_=g.lnmv[:, 3:4]), r=['lnmv3'], w=['lnmv4'])
    s.op('dve', lambda e: e.tensor_scalar(out=zb[:], in0=zb[:], scalar1=g.lnmv[:, 0:1], scalar2=g.lnmv[:, 4:5],
                                          op0=ALU.subtract, op1=ALU.mult), r=['zb', 'lnmv', 'lnmv4'], w=['zb'])
    s.op('pool', lambda e: e.tensor_tensor(out=zb[:], in0=zb[:], in1=L.lng_bc[:], op=ALU.mult), r=['zb', 'lng_bc'], w=['zb'])
    s.op('pool', lambda e: e.tensor_tensor(out=xo[:], in0=zb[:], in1=L.lnb_bc[:], op=ALU.add), r=['zb', 'lnb_bc'], w=[xo_nm])
    s.dma('sp', dst_rows, xo[:], r=[xo_nm], w=[])


def emit_ffn(g, li, src, dst):
    nc, s = g.nc, g.s
    TT = 256
    NT = S // TT
    NBT = TT // 128
    with ExitStack() as les:
        L = Ctx()
        L.es = les
        emit_mods(g, L, g.d_f_mod_w[li], g.d_f_mod_b[li], g.d_f_ln_g[li], g.d_f_ln_b[li])
        wup = _sb(nc, les, "wup", [128, 8, 2 * DFF], BF16)
        wdn = _sb(nc, les, "wdn", [128, NFC, D], BF16)
        cw = _sb(nc, les, "cw", [128, 2 * NFC, 4], F32)
        halo = _sb(nc, les, "halo", [128, 2 * NFC, 2], F32)
        hT = [_sb(nc, les, "hT%d" % i, [128, 8, TT], BF16) for i in range(2)]
        gT = _sb(nc, les, "gT", [128, NFC, TT], BF16)
        upre = [_sb(nc, les, "upre%d" % i, [128, TT + 2], F32) for i in range(2)]
        c0 = [_sb(nc, les, "c0%d" % i, [128, TT], F32) for i in range(2)]
        asil = _sb(nc, les, "asil", [128, TT], F32)
        xin = [_sb(nc, les, "xin%d" % i, [128, D], F32) for i in range(2)]
        xrs = [_sb(nc, les, "xrs%d" % i, [128, D], F32) for i in range(2)]
        zb = _sb(nc, les, "zb", [128, D], F32)
        xo = [_sb(nc, les, "xo%d" % i, [128, D], F32) for i in range(2)]
        P = g.ps
        wupd = g.d_f_w_up[li].rearrange("(kc p) n -> p kc n", p=128)
        for kc in range(8):
            for hf in range(2):
                s.dma('pool', wup[:, kc, hf * DFF:(hf + 1) * DFF], wupd[:, kc, hf * DFF:(hf + 1) * DFF], w=['wup'])
        wdnd = g.d_f_w_down[li].rearrange("(fc p) n -> p fc n", p=128)
        for fc in range(NFC):
            s.dma('pool', wdn[:, fc, :], wdnd[:, fc, :], w=['wdn'])
        s.dma('sp', cw[:], g.d_f_cw[li], w=['cw'])
        s.op('dve', lambda e: e.memset(halo[:], 0.0), w=['halo'])
        blk = 0
        for t in range(NT):
            t0 = t * TT
            hs = t % 2
            hnm = 'hT%d' % hs
            for bi in range(NBT):
                xs_ = (t * NBT + bi) % 2
                s.dma('sp', xin[xs_][:], src[t0 + bi * 128:t0 + (bi + 1) * 128, :], w=['xin%d' % xs_])
                emit_hT(g, L, xin[xs_], 'xin%d' % xs_, lambda kc, bi=bi, hs=hs: hT[hs][:, kc, bi * 128:(bi + 1) * 128], hnm, (0, 1))
            for j in range(NFC):
                for half in range(2):
                    fc = j + half * NFC
                    b = 2 + ((2 * j + half) % 4)
                    pb = 'ps%d' % b
                    up = upre[half]
                    unm = 'upre%d' % half
                    for kc in range(8):
                        s.op('pe', lambda e, kc=kc, fc=fc, b=b: e.matmul(P[:, b * 512:b * 512 + TT], lhsT=wup[:, kc, fc * 128:(fc + 1) * 128],
                                                                      rhs=hT[hs][:, kc, :], start=(kc == 0), stop=(kc == 7)),
                             r=['wup', hnm], w=[pb], inc=(kc == 7))
                    s.op('pool', lambda e, fc=fc, up=up: e.tensor_copy(out=up[:, 0:2], in_=halo[:, fc, :]), r=['halo'], w=[unm])
                    s.op('act', lambda e, b=b, up=up: e.activation(out=up[:, 2:TT + 2], in_=P[:, b * 512:b * 512 + TT], func=AF.Copy),
                         r=[pb], w=[unm])
                    s.op('act', lambda e, b=b, fc=fc, half=half: e.activation(out=c0[half][:], in_=P[:, b * 512:b * 512 + TT], func=AF.Identity,
                                                                             scale=cw[:, fc, 2:3], bias=cw[:, fc, 3:4]),
                         r=[pb, 'cw'], w=['c0%d' % half])
                    s.op('pool', lambda e, fc=fc, up=up: e.tensor_copy(out=halo[:, fc, :], in_=up[:, TT:TT + 2]), r=[unm], w=['halo'])
                    s.op('dve', lambda e, fc=fc, up=up, half=half: e.scalar_tensor_tensor(out=c0[half][:], in0=up[:, 1:TT + 1], scalar=cw[:, fc, 1:2],
                                                                                        in1=c0[half][:], op0=ALU.mult, op1=ALU.add),
                         r=[unm, 'cw', 'c0%d' % half], w=['c0%d' % half])
                    s.op('dve', lambda e, fc=fc, up=up, half=half: e.scalar_tensor_tensor(out=c0[half][:], in0=up[:, 0:TT], scalar=cw[:, fc, 0:1],
                                                                                        in1=c0[half][:], op0=ALU.mult, op1=ALU.add),
                         r=[unm, 'cw', 'c0%d' % half], w=['c0%d' % half])
                    if half == 0:
                        s.op('act', lambda e: e.activation(out=asil[:], in_=c0[0][:], func=AF.Silu), r=['c00'], w=['asil'])
                    else:
                        s.op('dve', lambda e, j=j: e.tensor_tensor(out=gT[:, j, :], in0=asil[:], in1=c0[1][:], op=ALU.mult),
                             r=['asil', 'c01'], w=['gT'])
            for bi in range(NBT):
                r0 = t0 + bi * 128
                xs_ = blk % 2
                blk += 1
                s.dma('sp', xrs[xs_][:], src[r0:r0 + 128, :], w=['xrs%d' % xs_])
                for half in range(2):
                    b = 6 + half
                    for j in range(NFC):
                        s.op('pe', lambda e, j=j, half=half, b=b, bi=bi: e.matmul(P[:, b * 512:(b + 1) * 512], lhsT=gT[:, j, bi * 128:(bi + 1) * 128],
                                                                                  rhs=wdn[:, j, half * 512:(half + 1) * 512],
                                                                                  start=(j == 0), stop=(j == NFC - 1)),
                             r=['gT', 'wdn'], w=['ps%d' % b], inc=(j == NFC - 1))
                emit_epilogue(g, L, (6, 7), xrs[xs_], 'xrs%d' % xs_, zb, xo[xs_], 'xo%d' % xs_, dst[r0:r0 + 128, :])
        s.barrier()


NEG = -1.0e30
DSTOP = int(os.environ.get('DSA_STOP', '99'))
DSUB = int(os.environ.get('DSA_SUB', '99'))
GSTOP = int(os.environ.get('GDN_STOP', '99'))
DSC = int(os.environ.get('DSA_SC', '3'))
DSKIP = os.environ.get('DSA_SKIP', '').split(',')
REP = -3.0e38


def emit_dsa(g, src, dst):
    nc, s = g.nc, g.s
    P = g.ps
    with ExitStack() as les:
        L = Ctx()
        L.es = les
        emit_mods(g, L, g.d_e_mod_w[0], g.d_e_mod_b[0], g.d_e_ln_g[0], g.d_e_ln_b[0])
        A = lambda name, shape, dt: _sb(nc, les, name, shape, dt)
        win = A("win", [128, 8, 1536], BF16)
        wif = A("wif", [128, 8, 4], F32)
        wiw = A("wiw", [128, 8, 4], BF16)
        poolw = A("poolw", [128, 4, 128], BF16)
        pscale = A("pscale", [128, 4], F32)
        kvn_bc = A("kvn_bc", [128, 128], F32)
        ukT = A("ukT", [128, 4, 128], BF16)
        uvpad = A("uvpad", [128, 8, 128], BF16)
        wout = A("wout", [128, 8, D], BF16)
        negI4 = A("negI4", [128, 512], BF16)
        corr = A("corr", [128, 4, 15], F32)
        ckvn_all = A("ckvn_all", [128, NB, 128], BF16)
        ckvnT_all = A("ckvnT_all", [128, S], BF16)
        kiT_all = A("kiT_all", [128, S], BF16)
        xin = [A("xin%d" % i, [128, D], F32) for i in range(3)]
        hT = A("hT", [128, 8, 128], BF16)
        ut = A("ut", [128, 4, 143], F32)
        ta = A("ta", [128, 143], F32)
        tb = A("tb", [128, 143], F32)
        dT = A("dT", [128, 4, 128], BF16)
        qT = A("qT", [128, 512], BF16)
        qiT = [A("qiT%d" % i, [128, 256], BF16) for i in range(2)]
        wis = [A("wis%d" % i, [128, 4], F32) for i in range(2)]
        qlT = [A("qlT%d" % i, [128, 1024], BF16) for i in range(3)]
        W = [A("W%d" % i, [128, S], F32) for i in range(2)]
        rbuf = [A("rbuf%d" % i, [128, 512], F32) for i in range(2)]
        rb2 = A("rb2", [128, 512], F32)
        notm = [A("notm%d" % i, [128, S], BF16) for i in range(2)]
        m8 = A("m8", [128, 8], F32)
        pT = [A("pT%d" % i, [128, 512], BF16) for i in range(2)]
        rden = A("rden", [128, 512], F32)
        oTn = A("oTn", [128, 1024], BF16)
        yinT = [A("yinT%d" % i, [128, 1024], BF16) for i in range(3)]
        zb = A("zb", [128, D], F32)
        xo = [A("xo%d" % i, [128, D], F32) for i in range(2)]
        sq = A("sq", [128, 128], F32)
        ckf = A("ckf", [128, 128], F32)
        rs = A("rs", [128, 4], F32)
        Pb3 = P[:, 3 * 512:4 * 512].bitcast(BF16)

        wind = g.d_e_w_in[0].rearrange("(kc p) n -> p kc n", p=128)
        for kc in range(8):
            s.dma('pool', win[:, kc, 0:1472], wind[:, kc, 0:1472], w=['win'])
            s.dma('pool', win[:, kc, 1472:1536], wind[:, kc, 1408:1472], w=['win'])
        s.dma('sp', wif[:], wind[:, :, 1472:1476], w=['wif'])
        s.op('dve', lambda e: e.tensor_copy(out=wiw[:], in_=wif[:]), r=['wif'], w=['wiw'])
        s.dma('pool', poolw[:], g.d_e_pool_w[0].rearrange("g c d -> c g d"), w=['poolw'])
        s.dma('sp', pscale[:], g.d_pscale[:, :], w=['pscale'])
        s.dma('sp', kvn_bc[:], g.d_e_kv_norm[0].partition_broadcast(128), w=['kvn_bc'])
        s.dma('pool', ukT[:], g.d_ukT[:, :, :], w=['ukT'])
        s.dma('pool', uvpad[:], g.d_uvpad[:, :, :], w=['uvpad'])
        woutd = g.d_e_w_out[0].rearrange("(kc p) n -> p kc n", p=128)
        for kc in range(8):
            s.dma('pool', wout[:, kc, :], woutd[:, kc, :], w=['wout'])
        s.dma('pool', negI4[:], g.d_negI4[:, :], w=['negI4'])
        s.dma('sp', corr[:], g.d_corr[:, :, :], w=['corr'])
        s.op('dve', lambda e: e.memset(ut[:], 0.0), w=['ut'])

        def front_a(qb):
            sl = qb % 2
            s3 = qb % 3
            t0 = qb * 128
            nk = t0 + 128
            xn = 'xin%d' % s3
            s.dma('sp', xin[s3][:], src[t0:t0 + 128, :], w=[xn])
            emit_hT(g, L, xin[s3], xn, lambda kc: hT[:, kc, :], 'hT', (0, 1))
            yield
            def grp(out_ap, cols, bank, last=True):
                for kc in range(8):
                    s.op('pe', lambda e, kc=kc: e.matmul(out_ap, lhsT=win[:, kc, cols[0]:cols[1]], rhs=hT[:, kc, :],
                                                        start=(kc == 0), stop=(kc == 7)),
                         r=['win', 'hT'], w=['ps%d' % bank], inc=(kc == 7))
            for gi in range(4):
                grp(P[:, gi * 128:(gi + 1) * 128], (gi * 128, (gi + 1) * 128), 0)
                yield
            for j in range(4):
                grp(P[:, 512 + j * 128:512 + (j + 1) * 128], (512 + j * 128, 512 + (j + 1) * 128), 1)
                yield
            for j in range(2):
                grp(P[:, 1024 + j * 128:1024 + (j + 1) * 128], (1152 + j * 128, 1152 + (j + 1) * 128), 2)
                yield
            grp(P[:, 1024 + 256:1024 + 384], (1408, 1536), 2)
            yield
            for kc in range(8):
                s.op('pe', lambda e, kc=kc: e.matmul(P[:, 1536:1536 + 128], lhsT=hT[:, kc, :], rhs=win[:, kc, 1024:1152],
                                                    start=(kc == 0), stop=(kc == 7)), r=['win', 'hT'], w=['ps3'], inc=(kc == 7))
            for kc in range(8):
                s.op('pe', lambda e, kc=kc: e.matmul(P[:, 1536 + 128:1536 + 132], lhsT=hT[:, kc, :], rhs=wiw[:, kc, :],
                                                    start=(kc == 0), stop=(kc == 7)), r=['wiw', 'hT'], w=['ps3'], inc=(kc == 7))
            yield
            s.op('act', lambda e: e.activation(out=ut[:, :, 15:143], in_=P[:, 0:512].rearrange("p (g t) -> p g t", g=4), func=AF.Copy),
                 r=['ps0'], w=['ut'])
            s.op('act', lambda e: e.activation(out=qT[:], in_=P[:, 512:1024], func=AF.Copy), r=['ps1'], w=['qT'])
            s.op('dve', lambda e: e.tensor_copy(out=qiT[sl][:], in_=P[:, 1024:1024 + 256]), r=['ps2'], w=['qiT%d' % sl])
            s.op('dve', lambda e: e.tensor_copy(out=kiT_all[:, t0:t0 + 128], in_=P[:, 1024 + 256:1024 + 384]), r=['ps2'], w=['kiT_all'])
            s.op('dve', lambda e: e.tensor_copy(out=wis[sl][:], in_=P[:, 1536 + 128:1536 + 132]), r=['ps3'], w=['wis%d' % sl])
            yield
            s.op('act', lambda e: e.activation(out=ckf[:], in_=P[:, 1536:1536 + 128], func=AF.Copy), r=['ps3'], w=['ckf'])
            s.op('dve', lambda e: e.tensor_tensor(out=sq[:], in0=ckf[:], in1=ckf[:], op=ALU.mult), r=['ckf'], w=['sq'])
            s.op('dve', lambda e: e.reduce_sum(out=rs[:, 0:1], in_=sq[:], axis=AX.X), r=['sq'], w=['rs0'])
            s.op('dve', lambda e: e.tensor_scalar(out=rs[:, 1:2], in0=rs[:, 0:1], scalar1=1.0 / 128, scalar2=RMS_EPS, op0=ALU.mult, op1=ALU.add),
                 r=['rs0'], w=['rs1'])
            s.op('act', lambda e: e.activation(out=rs[:, 2:3], in_=rs[:, 1:2], func=AF.Sqrt), r=['rs1'], w=['rs2'])
            s.op('dve', lambda e: e.reciprocal(out=rs[:, 3:4], in_=rs[:, 2:3]), r=['rs2'], w=['rs3'])
            s.op('dve', lambda e: e.scalar_tensor_tensor(out=ckf[:], in0=ckf[:], scalar=rs[:, 3:4], in1=kvn_bc[:],
                                                         op0=ALU.mult, op1=ALU.mult), r=['ckf', 'rs3', 'kvn_bc'], w=['ckf'])
            s.op('act', lambda e: e.activation(out=ckvn_all[:, qb, :], in_=ckf[:], func=AF.Copy), r=['ckf'], w=['ckvn_all'])
            s.op('pe', lambda e: e.transpose(P[:, 1536 + 256:1536 + 384], ckf[:], g.ident[:]), r=['ckf', 'ident'], w=['ps3'])
            s.op('act', lambda e: e.activation(out=ckvnT_all[:, t0:t0 + 128], in_=P[:, 1536 + 256:1536 + 384], func=AF.Copy), r=['ps3'], w=['ckvnT_all'])
            yield
            for gi in range(4):
                win_ = 2 << gi
                U = ut[:, gi, :]
                s.op('dve', lambda e, U=U: e.tensor_tensor(out=ta[:, 1:143], in0=U[:, 1:143], in1=U[:, 0:142], op=ALU.add), r=['ut'], w=['ta'])
                sw = ta
                swn = 'ta'
                if gi >= 1:
                    s.op('dve', lambda e: e.tensor_tensor(out=tb[:, 3:143], in0=ta[:, 3:143], in1=ta[:, 1:141], op=ALU.add), r=['ta'], w=['tb'])
                    sw, swn = tb, 'tb'
                if gi >= 2:
                    s.op('dve', lambda e: e.tensor_tensor(out=ta[:, 7:143], in0=tb[:, 7:143], in1=tb[:, 3:139], op=ALU.add), r=['tb'], w=['ta'])
                    sw, swn = ta, 'ta'
                if gi >= 3:
                    s.op('dve', lambda e: e.tensor_tensor(out=tb[:, 15:143], in0=ta[:, 15:143], in1=ta[:, 7:135], op=ALU.add), r=['ta'], w=['tb'])
                    sw, swn = tb, 'tb'
                if qb == 0:
                    s.op('dve', lambda e, sw=sw, gi=gi, win_=win_: e.tensor_tensor(out=sw[:, 15:15 + win_ - 1], in0=sw[:, 15:15 + win_ - 1],
                                                                                 in1=corr[:, gi, 0:win_ - 1], op=ALU.mult),
                         r=[swn, 'corr'], w=[swn])
                s.op('dve', lambda e, sw=sw, gi=gi, win_=win_, U=U: e.scalar_tensor_tensor(out=dT[:, gi, :], in0=sw[:, 15:143], scalar=1.0 / win_,
                                                                                          in1=U[:, 15:143], op0=ALU.mult, op1=ALU.subtract),
                     r=[swn, 'ut'], w=['dT'])
                yield
            s.op('pool', lambda e: e.tensor_copy(out=ut[:, :, 0:15], in_=ut[:, :, 128:143]), r=['ut'], w=['ut'])
            for gi in range(4):
                s.op('pe', lambda e, gi=gi: e.matmul(P[:, gi * 128:(gi + 1) * 128], lhsT=poolw[:, gi, :], rhs=dT[:, gi, :], start=True, stop=True),
                     r=['poolw', 'dT'], w=['ps0'])
            for gi in range(4):
                s.op('act', lambda e, gi=gi: e.activation(out=yinT[s3][:, gi * 128:(gi + 1) * 128], in_=P[:, gi * 128:(gi + 1) * 128],
                                                          func=AF.Identity, scale=pscale[:, gi:gi + 1]),
                     r=['ps0', 'pscale'], w=['yinT%d' % s3])
            yield
            for h in range(8):
                po = (h % 2) * 64
                bank = 1 + h % 2
                off = bank * 512 + (h // 2) * 128
                s.op('pe', lambda e, h=h, po=po, off=off: e.matmul(P[:, off:off + 128], lhsT=ukT[po:po + 64, h // 2, :],
                                                                  rhs=qT[po:po + 64, (h // 2) * 128:(h // 2 + 1) * 128], start=True, stop=True),
                     r=['ukT', 'qT'], w=['ps%d' % bank])
            for j in range(2):
                s.op('act', lambda e, j=j: e.activation(out=qlT[s3][:, j * 512:(j + 1) * 512], in_=P[:, (1 + j) * 512:(2 + j) * 512], func=AF.Copy),
                     r=['ps%d' % (1 + j)], w=['qlT%d' % s3])
            yield
            Wn = 'W%d' % sl
            cnt = 0
            chunks = []
            k0_ = 0
            while k0_ < nk:
                rem = nk - k0_
                w0 = 512 if rem >= 512 else (256 if rem >= 256 else 128)
                chunks.append((k0_, w0))
                k0_ += w0
            for (k0, w_) in chunks:
                for h in range(4):
                    po = (h % 2) * 64
                    bank = 2 + cnt % 2
                    rb = rbuf[cnt % 2]
                    rbn = 'rbuf%d' % (cnt % 2)
                    cnt += 1
                    s.op('pe', lambda e, h=h, po=po, bank=bank, k0=k0, w_=w_: e.matmul(P[:, bank * 512:bank * 512 + w_],
                                                                                     lhsT=qiT[sl][po:po + 64, (h // 2) * 128:(h // 2 + 1) * 128],
                                                                                     rhs=kiT_all[po:po + 64, k0:k0 + w_], start=True, stop=True),
                         r=['qiT%d' % sl, 'kiT_all'], w=['ps%d' % bank])
                    s.op('act', lambda e, bank=bank, rb=rb, w_=w_: e.activation(out=rb[:, 0:w_], in_=P[:, bank * 512:bank * 512 + w_], func=AF.Relu),
                         r=['ps%d' % bank], w=[rbn])
                    if h == 0:
                        s.op('dve', lambda e, rb=rb, k0=k0, w_=w_: e.tensor_scalar(out=W[sl][:, k0:k0 + w_], in0=rb[:, 0:w_], scalar1=wis[sl][:, 0:1],
                                                                                 scalar2=None, op0=ALU.mult), r=[rbn, 'wis%d' % sl], w=[Wn])
                    else:
                        s.op('dve', lambda e, rb=rb, k0=k0, w_=w_, h=h: e.scalar_tensor_tensor(out=W[sl][:, k0:k0 + w_], in0=rb[:, 0:w_], scalar=wis[sl][:, h:h + 1],
                                                                                            in1=W[sl][:, k0:k0 + w_], op0=ALU.mult, op1=ALU.add),
                             r=[rbn, 'wis%d' % sl, Wn], w=[Wn])
                    yield

        def topk(qb):
            sl = qb % 2
            nk = qb * 128 + 128
            Wn = 'W%d' % sl
            nmn = 'notm%d' % sl
            if nk <= 256:
                s.op('dve', lambda e: e.memset(notm[sl][:, 0:nk], 0.0), w=[nmn])
            else:
                s.op('dve', lambda e: e.memset(W[sl][0:64, nk - 64:nk], NEG), r=[Wn], w=[Wn])
                for it in range(32):
                    s.op('dve', lambda e: e.max(out=m8[:], in_=W[sl][:, 0:nk]), r=[Wn], w=['m8'])
                    s.op('dve', lambda e: e.match_replace(out=W[sl][:, 0:nk], in_to_replace=m8[:], in_values=W[sl][:, 0:nk], imm_value=REP),
                         r=[Wn, 'm8'], w=[Wn])
                    yield
                s.op('dve', lambda e: e.tensor_scalar(out=notm[sl][:, 0:nk], in0=W[sl][:, 0:nk], scalar1=0.5 * REP, scalar2=None, op0=ALU.is_gt),
                     r=[Wn], w=[nmn])
            s.op('dve', lambda e: e.memset(notm[sl][0:64, nk - 64:nk], 1.0), r=[nmn], w=[nmn])
            yield

        def back(qb):
            sl = qb % 2
            s3 = qb % 3
            t0 = qb * 128
            cnt = 0
            for hg in range(2):
                for kb in range(qb + 1):
                    j = cnt % 2
                    cnt += 1
                    bank = 4 + j
                    s.op('pe', lambda e, bank=bank, kb=kb, hg=hg: e.matmul(P[:, bank * 512:(bank + 1) * 512], lhsT=ckvnT_all[:, kb * 128:(kb + 1) * 128],
                                                                          rhs=qlT[s3][:, hg * 512:(hg + 1) * 512], start=True, stop=False),
                         r=['ckvnT_all', 'qlT%d' % s3], w=['ps%d' % bank], inc=False)
                    s.op('pe', lambda e, bank=bank, kb=kb: e.matmul(P[:, bank * 512:(bank + 1) * 512], lhsT=notm[sl][:, kb * 128:(kb + 1) * 128],
                                                                   rhs=negI4[:], start=False, stop=True),
                         r=['notm%d' % sl, 'negI4'], w=['ps%d' % bank])
                    s.op('act', lambda e, bank=bank, j=j: e.activation(out=pT[j][:], in_=P[:, bank * 512:(bank + 1) * 512], func=AF.Exp, scale=0.125),
                         r=['ps%d' % bank], w=['pT%d' % j])
                    s.op('pe', lambda e, kb=kb, j=j: e.matmul(P[:, 6 * 512:7 * 512], lhsT=ckvn_all[:, kb, :], rhs=pT[j][:], start=(kb == 0), stop=(kb == qb)),
                         r=['ckvn_all', 'pT%d' % j], w=['ps6'], inc=False)
                    s.op('pe', lambda e, kb=kb, j=j: e.matmul(P[:, 7 * 512:8 * 512], lhsT=g.onesb[:], rhs=pT[j][:], start=(kb == 0), stop=(kb == qb)),
                         r=['onesb', 'pT%d' % j], w=['ps7'])
                    yield
                s.op('dve', lambda e: e.reciprocal(out=rden[:], in_=P[:, 7 * 512:8 * 512]), r=['ps7'], w=['rden'])
                s.op('dve', lambda e, hg=hg: e.tensor_tensor(out=oTn[:, hg * 512:(hg + 1) * 512], in0=P[:, 6 * 512:7 * 512], in1=rden[:], op=ALU.mult),
                     r=['ps6', 'rden'], w=['oTn'])
                yield
            for hp in range(4):
                for h2 in range(2):
                    h = 2 * hp + h2
                    s.op('pe', lambda e, hp=hp, h=h, h2=h2: e.matmul(P[:, 7 * 512 + hp * 128:7 * 512 + (hp + 1) * 128], lhsT=uvpad[:, h, :],
                                                                    rhs=oTn[:, (h // 2 + 4 * (h % 2)) * 128:(h // 2 + 4 * (h % 2) + 1) * 128], start=(h2 == 0), stop=(h2 == 1)),
                         r=['uvpad', 'oTn'], w=['ps7'], inc=(h2 == 1))
            s.op('act', lambda e: e.activation(out=yinT[s3][:, 512:1024], in_=P[:, 7 * 512:8 * 512], func=AF.Copy), r=['ps7'], w=['yinT%d' % s3])
            yield
            for half in range(2):
                b = 4 + half
                for kc in range(8):
                    s.op('pe', lambda e, kc=kc, half=half, b=b: e.matmul(P[:, b * 512:(b + 1) * 512], lhsT=yinT[s3][:, kc * 128:(kc + 1) * 128],
                                                                        rhs=wout[:, kc, half * 512:(half + 1) * 512], start=(kc == 0), stop=(kc == 7)),
                         r=['yinT%d' % s3, 'wout'], w=['ps%d' % b], inc=(kc == 7))
            yield
            emit_epilogue(g, L, (4, 5), xin[s3], 'xin%d' % s3, zb, xo[sl], 'xo%d' % sl, dst[t0:t0 + 128, :])
            yield

        nblk = g.nblk_dsa
        for it in range(nblk + 2):
            gens = []
            if it < nblk:
                gens.append([front_a(it), 1])
            if 1 <= it <= nblk:
                gens.append([topk(it - 1), 1])
            if it >= 2:
                gens.append([back(it - 2), 1])
            _interleave(gens)
        s.barrier()


def emit_gdn(g, src, dst):
    nc, s = g.nc, g.s
    P = g.ps
    with ExitStack() as les:
        L = Ctx()
        L.es = les
        emit_mods(g, L, g.d_o_mod_w[0], g.d_o_mod_b[0], g.d_o_ln_g[0], g.d_o_ln_b[0])
        A = lambda name, shape, dt: _sb(nc, les, name, shape, dt)
        win = A("gwin", [128, 8, 4096], BF16)
        wbf = A("gwbf", [128, 8, 16], F32)
        wbb = A("gwbb", [128, 8, 16], BF16)
        wout = A("gwout", [128, 8, D], BF16)
        cw = A("gcw", [128, 24, 4], F32)
        msl = A("msl", [128, 128], F32)
        mil = A("mil", [128, 128], F32)
        triu = A("triu", [128, 128], F32)
        alog = A("alog", [128, 8], F32)
        dtb = A("dtb", [128, 8], F32)
        onb = A("onb", [128, D], F32)
        xin = [A("gxin%d" % i, [128, D], F32) for i in range(1)]
        xrs = A("gxrs", [128, D], F32)
        hT = A("ghT", [128, 8, 128], BF16)
        xw = [A("xw%d" % i, [128, 131], F32) for i in range(2)]
        halo = A("ghalo", [128, 24, 3], F32)
        cbuf = [A("cbuf%d" % i, [128, 128], F32) for i in range(2)]
        act = [A("gact%d" % i, [128, 24 * 128], F32) for i in range(2)]
        sq = A("gsq", [128, 1024], F32)
        rstd = sq
        qkb = [A("qkb%d" % i, [128, 2048], BF16) for i in range(2)]
        gsil = [A("gsil%d" % i, [128, D], F32) for i in range(3)]
        ba = A("ba", [128, 16], F32)
        tmpa = A("tmpa", [128, 8], F32)
        smb = [A("smb%d" % i, [128, 48], F32) for i in range(2)]
        zsm = A("zsm", [128, 8], F32)
        HP = []
        for p in range(2):
            h_ = Ctx()
            h_.sm = A("hsm%d" % p, [128, 2], F32)
            for nm in ("dg", "E", "EB", "t1", "Nf", "atf", "Pk0", "Pk1", "Pt0", "Pt1", "Rp"):
                setattr(h_, nm, A("h%s%d" % (nm, p), [128, 128], F32))
            for nm in ("Rpb", "attT", "qdT", "kd", "Kbg", "Vb", "nwT", "vnew"):
                setattr(h_, nm, A("h%s%d" % (nm, p), [128, 128], BF16))
            HP.append(h_)
        Sf = A("gSf", [128, 8, 128], F32)
        Sb = A("gSb", [128, 8, 128], BF16)
        osb = [A("gosb%d" % i, [128, D], F32) for i in range(2)]
        yinT = A("gyinT", [128, 1024], BF16)
        zb = A("gzb", [128, D], F32)
        xo = [A("gxo%d" % i, [128, D], F32) for i in range(1)]
        wind = g.d_o_w_in[0].rearrange("(kc p) n -> p kc n", p=128)
        for kc in range(8):
            for q4 in range(2):
                s.dma('pool', win[:, kc, q4 * 2048:(q4 + 1) * 2048], wind[:, kc, q4 * 2048:(q4 + 1) * 2048], w=['gwin'])
        s.dma('sp', wbf[:], wind[:, :, 4096:4112], w=['gwbf'])
        s.op('dve', lambda e: e.tensor_copy(out=wbb[:], in_=wbf[:]), r=['gwbf'], w=['gwbb'])
        woutd = g.d_o_w_out[0].rearrange("(kc p) n -> p kc n", p=128)
        for kc in range(8):
            s.dma('pool', wout[:, kc, :], woutd[:, kc, :], w=['gwout'])
        s.dma('sp', cw[:], g.d_o_cw[:, :, :], w=['gcw'])
        s.dma('sp', msl[:], g.d_msl[:, :], w=['msl'])
        s.dma('sp', mil[:], g.d_mil[:, :], w=['mil'])
        s.dma('sp', triu[:], g.d_triu[:, :], w=['triu'])
        s.dma('sp', alog[:], g.d_o_a_log[0].partition_broadcast(128), w=['alog'])
        s.dma('sp', dtb[:], g.d_o_dt_bias[0].partition_broadcast(128), w=['dtb'])
        s.dma('sp', onb[:], g.d_onb[0].partition_broadcast(128), w=['onb'])
        s.op('act', lambda e: e.activation(out=alog[:], in_=alog[:], func=AF.Exp), r=['alog'], w=['alog'])
        s.op('dve', lambda e: e.tensor_scalar(out=alog[:], in0=alog[:], scalar1=-1.0, scalar2=None, op0=ALU.mult), r=['alog'], w=['alog'])
        s.op('dve', lambda e: e.memset(halo[:], 0.0), w=['ghalo'])
        s.op('dve', lambda e: e.memset(Sf[:], 0.0), w=['gSf0', 'gSf1'])
        s.op('dve', lambda e: e.memset(Sb[:], 0.0), w=['gSb0', 'gSb1'])
        DK = float(128 ** -0.5)

        def phaseA(qb):
            t0 = qb * 128
            bp = qb % 2
            x3 = qb % 3
            xn = 'gxin0'
            an = 'gact%d' % bp
            sn = 'smb%d' % bp
            sm = smb[bp]
            ac = act[bp]
            s.dma('sp', xin[0][:], src[t0:t0 + 128, :], w=[xn])
            emit_hT(g, L, xin[0], xn, lambda kc: hT[:, kc, :], 'ghT', (0, 1))
            yield
            for ch in range(24):
                b = ch % 2
                cb = cbuf[b]
                cn = 'cbuf%d' % b
                for kc in range(8):
                    s.op('pe', lambda e, kc=kc, ch=ch, b=b: e.matmul(P[:, b * 512:b * 512 + 128], lhsT=win[:, kc, ch * 128:(ch + 1) * 128], rhs=hT[:, kc, :],
                                                                    start=(kc == 0), stop=(kc == 7)), r=['gwin', 'ghT'], w=['ps%d' % b], inc=(kc == 7))
                xb = xw[b]
                xbn = 'xw%d' % b
                s.op('pool', lambda e, ch=ch, xb=xb: e.tensor_copy(out=xb[:, 0:3], in_=halo[:, ch, :]), r=['ghalo'], w=[xbn])
                s.op('act', lambda e, b=b, xb=xb: e.activation(out=xb[:, 3:131], in_=P[:, b * 512:b * 512 + 128], func=AF.Copy), r=['ps%d' % b], w=[xbn])
                s.op('act', lambda e, ch=ch, b=b, cb=cb: e.activation(out=cb[:], in_=P[:, b * 512:b * 512 + 128], func=AF.Identity, scale=cw[:, ch, 3:4]),
                     r=['ps%d' % b, 'gcw'], w=[cn])
                s.op('pool', lambda e, ch=ch, xb=xb: e.tensor_copy(out=halo[:, ch, :], in_=xb[:, 128:131]), r=[xbn], w=['ghalo'])
                for j in range(3):
                    s.op('dve', lambda e, ch=ch, j=j, cb=cb, xb=xb: e.scalar_tensor_tensor(out=cb[:], in0=xb[:, j:j + 128], scalar=cw[:, ch, j:j + 1], in1=cb[:],
                                                                                          op0=ALU.mult, op1=ALU.add), r=[xbn, 'gcw', cn], w=[cn])
                s.op('act', lambda e, ch=ch, cb=cb: e.activation(out=ac[:, ch * 128:(ch + 1) * 128], in_=cb[:], func=AF.Silu), r=[cn], w=[an])
                yield
            for hf in range(2):
                seg = ac[:, hf * 1024:(hf + 1) * 1024]
                s.op('dve', lambda e, seg=seg: e.tensor_tensor(out=sq[:], in0=seg, in1=seg, op=ALU.mult), r=[an], w=['gsq'])
                yield
                for j in range(2):
                    b = j % 2
                    s.op('pe', lambda e, j=j, b=b: e.matmul(P[:, b * 512:(b + 1) * 512], lhsT=g.ones[:], rhs=sq[:, j * 512:(j + 1) * 512], start=True, stop=True),
                         r=['ones', 'gsq'], w=['ps%d' % b])
                for j in range(2):
                    b = j % 2
                    s.op('dve', lambda e, j=j, b=b: e.tensor_scalar(out=sq[:, j * 512:(j + 1) * 512], in0=P[:, b * 512:(b + 1) * 512], scalar1=RMS_EPS, scalar2=None, op0=ALU.add),
                         r=['ps%d' % b], w=['gsq'])
                yield
                s.op('act', lambda e: e.activation(out=sq[:], in_=sq[:], func=AF.Sqrt), r=['gsq'], w=['gsq'])
                s.op('dve', lambda e: e.reciprocal(out=sq[:], in_=sq[:]), r=['gsq'], w=['gsq'])
                yield
                s.op('dve', lambda e, seg=seg: e.tensor_tensor(out=seg, in0=seg, in1=sq[:], op=ALU.mult), r=[an, 'gsq'], w=[an])
                s.op('act', lambda e, seg=seg, hf=hf: e.activation(out=qkb[bp][:, hf * 1024:(hf + 1) * 1024], in_=seg, func=AF.Copy), r=[an], w=['qkb%d' % bp])
                yield
            for half in range(2):
                b = half
                for kc in range(8):
                    s.op('pe', lambda e, kc=kc, half=half, b=b: e.matmul(P[:, b * 512:(b + 1) * 512], lhsT=hT[:, kc, :],
                                                                        rhs=win[:, kc, 3072 + half * 512:3072 + (half + 1) * 512],
                                                                        start=(kc == 0), stop=(kc == 7)), r=['gwin', 'ghT'], w=['ps%d' % b], inc=(kc == 7))
                s.op('act', lambda e, half=half, b=b: e.activation(out=gsil[x3][:, half * 512:(half + 1) * 512], in_=P[:, b * 512:(b + 1) * 512], func=AF.Silu),
                     r=['ps%d' % b], w=['gsil%d' % x3])
                yield
            for kc in range(8):
                s.op('pe', lambda e, kc=kc: e.matmul(P[:, 0:16], lhsT=hT[:, kc, :], rhs=wbb[:, kc, :], start=(kc == 0), stop=(kc == 7)),
                     r=['gwbb', 'ghT'], w=['ps0'], inc=(kc == 7))
            s.op('dve', lambda e: e.tensor_copy(out=ba[:], in_=P[:, 0:16]), r=['ps0'], w=['ba'])
            yield
            s.op('act', lambda e: e.activation(out=sm[:, 0:8], in_=ba[:, 0:8], func=AF.Exp, scale=-1.0), r=['ba'], w=[sn])
            s.op('dve', lambda e: e.tensor_scalar(out=sm[:, 0:8], in0=sm[:, 0:8], scalar1=1.0, scalar2=None, op0=ALU.add), r=[sn], w=[sn])
            s.op('dve', lambda e: e.reciprocal(out=sm[:, 0:8], in_=sm[:, 0:8]), r=[sn], w=[sn])
            s.op('dve', lambda e: e.tensor_scalar(out=sm[:, 32:40], in0=sm[:, 0:8], scalar1=-1.0, scalar2=None, op0=ALU.mult), r=[sn], w=[sn])
            yield
            s.op('dve', lambda e: e.tensor_tensor(out=tmpa[:], in0=ba[:, 8:16], in1=dtb[:], op=ALU.add), r=['ba', 'dtb'], w=['tmpa'])
            s.op('act', lambda e: e.activation(out=tmpa[:], in_=tmpa[:], func=AF.Exp), r=['tmpa'], w=['tmpa'])
            s.op('act', lambda e: e.activation(out=tmpa[:], in_=tmpa[:], func=AF.Ln, bias=1.0), r=['tmpa'], w=['tmpa'])
            s.op('dve', lambda e: e.tensor_tensor(out=sm[:, 8:16], in0=tmpa[:], in1=alog[:], op=ALU.mult), r=['tmpa', 'alog', sn], w=[sn])
            yield
            s.op('pe', lambda e: e.matmul(P[:, 16:24], lhsT=triu[:], rhs=sm[:, 8:16], start=True, stop=True), r=['triu', sn], w=['ps0'])
            s.op('dve', lambda e: e.tensor_copy(out=sm[:, 16:24], in_=P[:, 16:24]), r=['ps0', sn], w=[sn])
            s.op('act', lambda e: e.activation(out=sm[:, 24:32], in_=sm[:, 16:24], func=AF.Exp), r=[sn], w=[sn])
            s.op('dve', lambda e: e.tensor_tensor(out=sm[:, 40:48], in0=sm[:, 24:32], in1=sm[:, 0:8], op=ALU.mult), r=[sn], w=[sn])
            yield

        def heads(qb, p):
            bp = qb % 2
            sm = smb[bp]
            sn = 'smb%d' % bp
            an = 'gact%d' % bp
            ac = act[bp]
            H = HP[p]
            X0 = (2 + 3 * p) * 512
            Y0 = X0 + 512
            Z0 = X0 + 1024
            xn_, yn_, zn_ = 'ps%d' % (2 + 3 * p), 'ps%d' % (3 + 3 * p), 'ps%d' % (4 + 3 * p)
            n = lambda nm: 'h%s%d' % (nm, p)
            Pk = [H.Pk0, H.Pk1]
            Pt = [H.Pt0, H.Pt1]
            for h in range(p, 8, 2):
                qn = ac[:, h * 128:(h + 1) * 128]
                kn = ac[:, (8 + h) * 128:(9 + h) * 128]
                vv = ac[:, (16 + h) * 128:(17 + h) * 128]
                qnb = qkb[bp][:, h * 128:(h + 1) * 128]
                knb = qkb[bp][:, (8 + h) * 128:(9 + h) * 128]
                qbn = 'qkb%d' % bp
                s.op('dve', lambda e, h=h: e.tensor_scalar(out=H.dg[:], in0=g.ident[:], scalar1=sm[:, 16 + h:17 + h], scalar2=None, op0=ALU.mult),
                     r=['ident', sn], w=[n('dg')])
                s.op('pe', lambda e: e.matmul(P[:, X0:X0 + 128], lhsT=g.ones[:], rhs=H.dg[:], start=True, stop=True), r=['ones', n('dg')], w=[xn_])
                s.op('dve', lambda e, h=h: e.tensor_scalar(out=H.E[:], in0=P[:, X0:X0 + 128], scalar1=sm[:, 16 + h:17 + h], scalar2=0.0, op0=ALU.subtract, op1=ALU.max),
                     r=[xn_, sn], w=[n('E')])
                s.op('act', lambda e: e.activation(out=H.E[:], in_=H.E[:], func=AF.Exp, scale=-1.0), r=[n('E')], w=[n('E')])
                s.op('act', lambda e: e.activation(out=H.EB[:], in_=P[:, X0:X0 + 128], func=AF.Exp), r=[xn_], w=[n('EB')])
                s.op('act', lambda e: e.activation(out=H.sm[:, 0:1], in_=P[:, X0 + 127:X0 + 128], func=AF.Exp), r=[xn_], w=[n('sm0')])
                s.op('dve', lambda e, h=h: e.tensor_scalar(out=H.sm[:, 1:2], in0=P[:, X0 + 127:X0 + 128], scalar1=sm[:, 16 + h:17 + h], scalar2=None, op0=ALU.subtract),
                     r=[xn_, sn], w=[n('sm1')])
                s.op('act', lambda e: e.activation(out=H.sm[:, 1:2], in_=H.sm[:, 1:2], func=AF.Exp), r=[n('sm1')], w=[n('sm1')])
                yield
                s.op('pe', lambda e, knb=knb: e.matmul(P[:, X0 + 128:X0 + 256], lhsT=knb, rhs=knb, start=True, stop=True), r=[qbn], w=[xn_])
                s.op('pe', lambda e, knb=knb, qnb=qnb: e.matmul(P[:, X0 + 256:X0 + 384], lhsT=qnb, rhs=knb, start=True, stop=True), r=[qbn], w=[xn_])
                s.op('dve', lambda e: e.tensor_tensor(out=H.t1[:], in0=H.E[:], in1=msl[:], op=ALU.mult), r=[n('E'), 'msl'], w=[n('t1')])
                s.op('dve', lambda e, h=h: e.scalar_tensor_tensor(out=H.Nf[:], in0=P[:, X0 + 128:X0 + 256], scalar=sm[:, 32 + h:33 + h], in1=H.t1[:], op0=ALU.mult, op1=ALU.mult),
                     r=[xn_, sn, n('t1')], w=[n('Nf')])
                s.op('dve', lambda e: e.tensor_tensor(out=H.t1[:], in0=H.E[:], in1=mil[:], op=ALU.mult), r=[n('E'), 'mil', n('Nf')], w=[n('t1')])
                s.op('dve', lambda e: e.scalar_tensor_tensor(out=H.atf[:], in0=P[:, X0 + 256:X0 + 384], scalar=DK, in1=H.t1[:], op0=ALU.mult, op1=ALU.mult),
                     r=[xn_, n('t1')], w=[n('atf')])
                yield
                s.op('pe', lambda e: e.transpose(P[:, Y0:Y0 + 128], H.Nf[:], g.ident[:]), r=[n('Nf'), 'ident'], w=[yn_])
                s.op('pe', lambda e: e.transpose(P[:, Y0 + 128:Y0 + 256], H.atf[:], g.ident[:]), r=[n('atf'), 'ident'], w=[yn_])
                s.op('pe', lambda e, kn=kn: e.transpose(P[:, Y0 + 256:Y0 + 384], kn, g.ident[:]), r=[an, 'ident'], w=[yn_])
                s.op('pe', lambda e, vv=vv: e.transpose(P[:, Y0 + 384:Y0 + 512], vv, g.ident[:]), r=[an, 'ident'], w=[yn_])
                s.op('act', lambda e: e.activation(out=Pk[0][:], in_=H.Nf[:], func=AF.Copy), r=[n('Nf')], w=[n('Pk0')])
                s.op('act', lambda e: e.activation(out=Pt[0][:], in_=P[:, Y0:Y0 + 128], func=AF.Copy), r=[yn_], w=[n('Pt0')])
                s.op('act', lambda e: e.activation(out=H.attT[:], in_=P[:, Y0 + 128:Y0 + 256], func=AF.Copy), r=[yn_], w=[n('attT')])
                s.op('dve', lambda e: e.tensor_tensor(out=H.Rp[:], in0=P[:, Y0:Y0 + 128], in1=g.ident[:], op=ALU.add), r=[yn_, 'ident'], w=[n('Rp')])
                s.op('dve', lambda e: e.tensor_scalar(out=H.kd[:], in0=P[:, Y0 + 256:Y0 + 384], scalar1=H.sm[:, 1:2], scalar2=None, op0=ALU.mult), r=[yn_, n('sm1')], w=[n('kd')])
                s.op('dve', lambda e, h=h: e.tensor_scalar(out=H.Kbg[:], in0=P[:, Y0 + 256:Y0 + 384], scalar1=sm[:, 40 + h:41 + h], scalar2=None, op0=ALU.mult),
                     r=[yn_, sn], w=[n('Kbg')])
                s.op('dve', lambda e, h=h: e.tensor_scalar(out=H.Vb[:], in0=P[:, Y0 + 384:Y0 + 512], scalar1=sm[:, h:h + 1], scalar2=None, op0=ALU.mult), r=[yn_, sn], w=[n('Vb')])
                s.op('dve', lambda e, qn=qn: e.scalar_tensor_tensor(out=H.qdT[:], in0=qn, scalar=DK, in1=H.EB[:], op0=ALU.mult, op1=ALU.mult),
                     r=[an, n('EB')], w=[n('qdT')])
                yield
                cur = 0
                for it in range(6):
                    nx = 1 - cur
                    s.op('pe', lambda e, cur=cur: e.matmul(P[:, Z0:Z0 + 128], lhsT=Pt[cur][:], rhs=Pk[cur][:], start=True, stop=True),
                         r=[n('Pt%d' % cur), n('Pk%d' % cur)], w=[zn_])
                    s.op('act', lambda e, nx=nx: e.activation(out=Pk[nx][:], in_=P[:, Z0:Z0 + 128], func=AF.Copy), r=[zn_], w=[n('Pk%d' % nx)])
                    if it < 5:
                        s.op('pe', lambda e, cur=cur: e.matmul(P[:, Z0 + 128:Z0 + 256], lhsT=Pk[cur][:], rhs=Pt[cur][:], start=True, stop=True),
                             r=[n('Pt%d' % cur), n('Pk%d' % cur)], w=[zn_])
                        s.op('act', lambda e, nx=nx: e.activation(out=Pt[nx][:], in_=P[:, Z0 + 128:Z0 + 256], func=AF.Copy), r=[zn_], w=[n('Pt%d' % nx)])
                    s.op('pe', lambda e, nx=nx: e.matmul(P[:, Z0 + 256:Z0 + 384], lhsT=Pk[nx][:], rhs=H.Rp[:], start=True, stop=True), r=[n('Pk%d' % nx), n('Rp')], w=[zn_])
                    s.op('dve', lambda e: e.tensor_tensor(out=H.Rp[:], in0=P[:, Z0 + 256:Z0 + 384], in1=H.Rp[:], op=ALU.add), r=[zn_, n('Rp')], w=[n('Rp')])
                    cur = nx
                    yield
                s.op('act', lambda e: e.activation(out=H.Rpb[:], in_=H.Rp[:], func=AF.Copy), r=[n('Rp')], w=[n('Rpb')])
                s.op('pe', lambda e: e.matmul(P[:, Z0 + 384:Z0 + 512], lhsT=H.Kbg[:], rhs=H.Rpb[:], start=True, stop=True), r=[n('Kbg'), n('Rpb')], w=[zn_])
                s.op('act', lambda e: e.activation(out=H.nwT[:], in_=P[:, Z0 + 384:Z0 + 512], func=AF.Copy, scale=-1.0), r=[zn_], w=[n('nwT')])
                yield
                s.op('pe', lambda e: e.matmul(P[:, Y0:Y0 + 128], lhsT=H.Rpb[:], rhs=H.Vb[:], start=True, stop=False), r=[n('Rpb'), n('Vb')], w=[yn_], inc=False)
                s.op('pe', lambda e, h=h: e.matmul(P[:, Y0:Y0 + 128], lhsT=H.nwT[:], rhs=Sb[:, h, :], start=False, stop=True), r=[n('nwT'), 'gSb%d' % p], w=[yn_])
                s.op('act', lambda e: e.activation(out=H.vnew[:], in_=P[:, Y0:Y0 + 128], func=AF.Copy), r=[yn_], w=[n('vnew')])
                yield
                s.op('pe', lambda e, h=h: e.matmul(P[:, X0 + 384:X0 + 512], lhsT=H.qdT[:], rhs=Sb[:, h, :], start=True, stop=False),
                     r=[n('qdT'), 'gSb%d' % p], w=[xn_], inc=False)
                s.op('pe', lambda e: e.matmul(P[:, X0 + 384:X0 + 512], lhsT=H.attT[:], rhs=H.vnew[:], start=False, stop=True),
                     r=[n('attT'), n('vnew')], w=[xn_])
                s.op('act', lambda e, h=h: e.activation(out=osb[bp][:, h * 128:(h + 1) * 128], in_=P[:, X0 + 384:X0 + 512], func=AF.Copy),
                     r=[xn_], w=['gosb%d_%d' % (bp, p)])
                s.op('pe', lambda e: e.matmul(P[:, Y0 + 128:Y0 + 256], lhsT=H.kd[:], rhs=H.vnew[:], start=True, stop=True), r=[n('kd'), n('vnew')], w=[yn_])
                s.op('dve', lambda e, h=h: e.scalar_tensor_tensor(out=Sf[:, h, :], in0=Sf[:, h, :], scalar=H.sm[:, 0:1], in1=P[:, Y0 + 128:Y0 + 256], op0=ALU.mult, op1=ALU.add),
                     r=['gSf%d' % p, n('sm0'), yn_], w=['gSf%d' % p])
                s.op('act', lambda e, h=h: e.activation(out=Sb[:, h, :], in_=Sf[:, h, :], func=AF.Copy), r=['gSf%d' % p], w=['gSb%d' % p])
                yield

        def phaseZ(qb):
            t0 = qb * 128
            bp = qb % 2
            x3 = qb % 3
            ob = osb[bp]
            on = ['gosb%d_0' % bp, 'gosb%d_1' % bp]
            s.op('dve', lambda e: e.tensor_tensor(out=zb[:], in0=ob[:], in1=ob[:], op=ALU.mult), r=on, w=['zb'])
            for h in range(8):
                s.op('dve', lambda e, h=h: e.reduce_sum(out=zsm[:, h:h + 1], in_=zb[:, h * 128:(h + 1) * 128], axis=AX.X), r=['zb'], w=['zsm'])
            yield
            s.op('dve', lambda e: e.tensor_scalar(out=zsm[:], in0=zsm[:], scalar1=1.0 / 128, scalar2=RMS_EPS, op0=ALU.mult, op1=ALU.add), r=['zsm'], w=['zsm'])
            s.op('act', lambda e: e.activation(out=zsm[:], in_=zsm[:], func=AF.Sqrt), r=['zsm'], w=['zsm'])
            s.op('dve', lambda e: e.reciprocal(out=zsm[:], in_=zsm[:]), r=['zsm'], w=['zsm'])
            yield
            for h in range(8):
                s.op('dve', lambda e, h=h: e.tensor_scalar(out=ob[:, h * 128:(h + 1) * 128], in0=ob[:, h * 128:(h + 1) * 128], scalar1=zsm[:, h:h + 1],
                                                           scalar2=None, op0=ALU.mult), r=on + ['zsm'], w=on)
            yield
            s.op('dve', lambda e: e.tensor_tensor(out=ob[:], in0=ob[:], in1=onb[:], op=ALU.mult), r=on + ['onb'], w=on)
            s.op('dve', lambda e: e.tensor_tensor(out=ob[:], in0=ob[:], in1=gsil[x3][:], op=ALU.mult), r=on + ['gsil%d' % x3], w=on)
            yield
            for kc in range(8):
                b = kc // 4
                off = b * 512 + (kc % 4) * 128
                s.op('pe', lambda e, kc=kc, off=off: e.transpose(P[:, off:off + 128], ob[:, kc * 128:(kc + 1) * 128], g.ident[:]), r=on + ['ident'], w=['ps%d' % b])
            for j in range(2):
                s.op('act', lambda e, j=j: e.activation(out=yinT[:, j * 512:(j + 1) * 512], in_=P[:, j * 512:(j + 1) * 512], func=AF.Copy),
                     r=['ps%d' % j], w=['gyinT'])
            yield
            for half in range(2):
                b = half
                for kc in range(8):
                    s.op('pe', lambda e, kc=kc, half=half, b=b: e.matmul(P[:, b * 512:(b + 1) * 512], lhsT=yinT[:, kc * 128:(kc + 1) * 128],
                                                                        rhs=wout[:, kc, half * 512:(half + 1) * 512], start=(kc == 0), stop=(kc == 7)),
                         r=['gyinT', 'gwout'], w=['ps%d' % b], inc=(kc == 7))
            s.dma('sp', xrs[:], src[t0:t0 + 128, :], w=['gxrs'])
            emit_epilogue(g, L, (0, 1), xrs, 'gxrs', zb, xo[0], 'gxo0', dst[t0:t0 + 128, :])
            yield

        nblk = g.nblk_gdn
        for it in range(nblk + 2):
            gens = []
            if 1 <= it <= nblk:
                gens.append([heads(it - 1, 0), 1])
                gens.append([heads(it - 1, 1), 1])
            if it < nblk:
                gens.append([phaseA(it), 1])
            if it >= 2:
                gens.append([phaseZ(it - 2), 1])
            _interleave(gens)
        s.barrier()


W_SPECS = [
    ("e_mod_w", [1, D, 3 * D]), ("e_mod_b", [1, 1, 3 * D]), ("e_ln_g", [1, D]), ("e_ln_b", [1, D]),
    ("o_mod_w", [1, D, 3 * D]), ("o_mod_b", [1, 1, 3 * D]), ("o_ln_g", [1, D]), ("o_ln_b", [1, D]),
    ("f_mod_w", [2, D, 3 * D]), ("f_mod_b", [2, 1, 3 * D]), ("f_ln_g", [2, D]), ("f_ln_b", [2, D]),
    ("f_w_up", [2, D, 2 * DFF]), ("f_w_down", [2, DFF, D]), ("f_cw", [2, 128, 2 * NFC, 4]),
    ("ident", [128, 128]),
    ("e_w_in", [1, D, 1476]), ("e_pool_w", [1, 4, 128, 128]), ("pscale", [128, 4]), ("e_kv_norm", [1, 128]),
    ("o_w_in", [1, D, 4112]), ("o_w_out", [1, D, D]), ("o_cw", [128, 24, 4]), ("msl", [128, 128]), ("mil", [128, 128]), ("triu", [128, 128]),
    ("o_a_log", [1, 8]), ("o_dt_bias", [1, 8]), ("onb", [1, D]),
    ("ukT", [128, 4, 128]), ("uvpad", [128, 8, 128]), ("e_w_out", [1, D, D]), ("negI4", [128, 512]), ("corr", [128, 4, 15]),
]


def build(stages):
    nc = bass.Bass("TRN2", target_bir_lowering=False)
    g = Ctx()
    g.nc = nc
    g.d_x = nc.dram_tensor("x", [S, D], F32, kind="ExternalInput").ap()
    g.d_ccol = nc.dram_tensor("ccol", [128, 8], F32, kind="ExternalInput").ap()
    for nm, shp in W_SPECS:
        setattr(g, "d_" + nm, nc.dram_tensor(nm, shp, F32, kind="ExternalInput").ap())
    g.d_out = nc.dram_tensor("out", [S, D], F32, kind="ExternalOutput").ap()
    scr = [nc.dram_tensor("xscr%d" % i, [S, D], F32, kind="Internal").ap() for i in range(3)]
    with ExitStack() as es:
        g.es = es
        g.s = Sch(nc, es)
        g.ps = es.enter_context(nc.psum_tensor("ps", [128, 8 * 512], F32))
        g.lnst = _sb(nc, es, "lnst", [128, 12], F32)
        g.lnmv = _sb(nc, es, "lnmv", [128, 8], F32)
        emit_consts(g)
        bufs = [g.d_x] + scr
        n = len(stages)
        for i, st in enumerate(stages):
            src = g.d_x if i == 0 else scr[(i - 1) % 3]
            dst = g.d_out if i == n - 1 else scr[i % 3]
            if st[0] == 'ffn':
                emit_ffn(g, st[1], src, dst)
            elif st[0] == 'gdn':
                g.nblk_gdn = st[1] if len(st) > 1 else NB
                emit_gdn(g, src, dst)
            elif st[0] == 'dsa':
                g.nblk_dsa = st[1] if len(st) > 1 else NB
                emit_dsa(g, src, dst)
            else:
                raise ValueError(st)
        g.s.finish()
    return nc


def prep_weights(inp):
    f = lambda a: np.ascontiguousarray(np.asarray(a, dtype=np.float32))
    w = {}
    for k in ("e_mod_w", "e_ln_g", "e_ln_b", "o_mod_w", "o_ln_g", "o_ln_b", "f_mod_w", "f_ln_g", "f_ln_b", "f_w_up", "f_w_down"):
        w[k] = f(inp[k])
    for k in ("e_mod_b", "o_mod_b", "f_mod_b"):
        a = f(inp[k])
        w[k] = np.ascontiguousarray(a.reshape(a.shape[0], 1, 3 * D))
    cwt = f(inp["f_conv_w"])
    cb = f(inp["f_conv_b"])
    a = np.concatenate([cwt, cb[:, None, :]], axis=1)
    a = a.reshape(2, 4, 2 * NFC, 128).transpose(0, 3, 2, 1)
    w["f_cw"] = np.ascontiguousarray(a)
    w["ident"] = np.eye(128, dtype=np.float32)
    for k in ("e_w_in", "e_pool_w", "e_kv_norm", "e_w_out"):
        w[k] = f(inp[k])
    w["pscale"] = np.ascontiguousarray(f(inp["e_pool_scale"])[0].reshape(4, 128).T)
    uk = f(inp["e_w_uk"])[0]
    w["ukT"] = np.ascontiguousarray(uk.reshape(4, 2, 128, 64).transpose(1, 3, 0, 2).reshape(128, 4, 128))
    uv = f(inp["e_w_uv"])[0]
    uvp = np.zeros((128, 8, 128), np.float32)
    for h in range(8):
        uvp[:, h, (h % 2) * 64:(h % 2) * 64 + 64] = uv[h]
    w["uvpad"] = uvp
    w["negI4"] = np.ascontiguousarray(np.tile(-30000.0 * np.eye(128, dtype=np.float32), (1, 4)))
    corr = np.ones((128, 4, 15), np.float32)
    for gi in range(4):
        win_ = 2 << gi
        for t in range(win_ - 1):
            corr[:, gi, t] = win_ / (t + 1.0)
    w["corr"] = corr
    for k in ("o_w_in", "o_w_out", "o_a_log", "o_dt_bias"):
        w[k] = f(inp[k])
    ocw = f(inp["o_conv_w"])[0]
    w["o_cw"] = np.ascontiguousarray(ocw.reshape(4, 24, 128).transpose(2, 1, 0))
    ar = np.arange(128)
    w["msl"] = (ar[:, None] > ar[None, :]).astype(np.float32)
    w["mil"] = (ar[:, None] >= ar[None, :]).astype(np.float32)
    w["triu"] = (ar[:, None] <= ar[None, :]).astype(np.float32)
    w["onb"] = np.ascontiguousarray(np.tile(f(inp["o_out_norm"])[0], 8)[None, :])
    return w


STAGES = [('dsa',), ('ffn', 0), ('gdn',), ('ffn', 1)]


def kernel(**inp):
    x = np.asarray(inp["x"], dtype=np.float32)
    c = np.asarray(inp["c"], dtype=np.float32)
    w = prep_weights(inp)
    nc = build(STAGES)
    in_maps = []
    for b in range(8):
        m = dict(w)
        m["x"] = np.ascontiguousarray(x[b])
        m["ccol"] = np.ascontiguousarray(c[b].reshape(8, 128).T)
        in_maps.append(m)
    res = run_bass_kernel_spmd(nc, in_maps, core_ids=list(range(8)))
    return np.stack([np.asarray(r["out"], dtype=np.float32) for r in res.results], axis=0)
```

```python
import os
import numpy as np
from contextlib import ExitStack
import concourse.bass as bass
import concourse.mybir as mybir
from concourse.bass_utils import run_bass_kernel_spmd

F32 = mybir.dt.float32
BF16 = mybir.dt.bfloat16
AF = mybir.ActivationFunctionType
ALU = mybir.AluOpType
AX = mybir.AxisListType

D = 1024
S = 4096
NB = S // 128
DFF = 2688
NFC = DFF // 128
ALPHA = float(4 ** 0.25)
LN_EPS = 1e-5
RMS_EPS = 1e-6
NDS = 12


class Sch:
    def __init__(self, nc, es):
        self.nc = nc
        self.E = {'pe': nc.tensor, 'act': nc.scalar, 'dve': nc.vector,
                  'pool': nc.gpsimd, 'sp': nc.sync}
        self.sem = {}
        for e in self.E:
            self.sem[e] = es.enter_context(nc.semaphore('s_' + e))
        self.cnt = {e: 0 for e in self.E}
        self.waited = {e: {} for e in self.E}
        self.lastw = {}
        self.readers = {}
        self.dq = ('sp', 'pool', 'act')
        self.dcnt = {}
        self.drr = {q: 0 for q in self.dq}
        for q in self.dq:
            for i in range(NDS):
                k = (q, i)
                self.sem[k] = es.enter_context(nc.semaphore('d_%s%d' % (q, i)))
                self.dcnt[k] = 0
        self.nwaits = 0

    def _wait(self, e, tok):
        key, val = tok
        if key == e and e == 'pe':
            return
        if self.waited[e].get(key, 0) >= val:
            return
        self.E[e].wait_ge(self.sem[key], val)
        self.waited[e][key] = val
        self.nwaits += 1

    def _collect(self, r, w, e=None):
        deps = {}

        def add(t):
            if t is None:
                return
            if deps.get(t[0], 0) < t[1]:
                deps[t[0]] = t[1]
        for x in r:
            for k, v in self.lastw.get(x, {}).items():
                add((k, v))
            if x.startswith('ps') and e is not None:
                for k, v in self.readers.get(x, {}).items():
                    if k != e:
                        add((k, v))
        for x in w:
            for k, v in self.lastw.get(x, {}).items():
                add((k, v))
            for k, v in self.readers.get(x, {}).items():
                add((k, v))
        return list(deps.items())

    def _record(self, tok, r, w):
        for x in w:
            self.lastw.setdefault(x, {})[tok[0]] = tok[1]
            self.readers[x] = {}
        for x in r:
            d = self.readers.setdefault(x, {})
            if d.get(tok[0], 0) < tok[1]:
                d[tok[0]] = tok[1]

    def op(self, e, fn, r=(), w=(), inc=True):
        for t in self._collect(r, w, e):
            self._wait(e, t)
        ins = fn(self.E[e])
        if inc:
            self.cnt[e] += 1
            ins.then_inc(self.sem[e], 1)
            tok = (e, self.cnt[e])
        else:
            tok = (e, self.cnt[e] + 1)
        self._record(tok, r, w)
        return ins

    def dma(self, q, out, in_, r=(), w=()):
        i = self.drr[q]
        self.drr[q] = (i + 1) % NDS
        k = (q, i)
        if self.dcnt[k] > 0:
            self._wait(q, (k, self.dcnt[k]))
        for t in self._collect(r, w):
            self._wait(q, t)
        ins = self.E[q].dma_start(out=out, in_=in_)
        self.dcnt[k] += 16
        ins.then_inc(self.sem[k], 16)
        tok = (k, self.dcnt[k])
        self._record(tok, r, w)
        return ins

    def barrier(self):
        toks = [(e, self.cnt[e]) for e in self.E if self.cnt[e] > 0]
        toks += [(k, v) for k, v in self.dcnt.items() if v > 0]
        for e in self.E:
            for t in toks:
                self._wait(e, t)
        self.lastw = {}
        self.readers = {}

    def finish(self):
        for k, v in self.dcnt.items():
            if v > 0:
                self._wait('sp', (k, v))


class Ctx:
    pass


def _interleave(gens):
    live = list(gens)
    while live:
        nxt = []
        for item in live:
            gen, n = item
            done = False
            for _ in range(n):
                try:
                    next(gen)
                except StopIteration:
                    done = True
                    break
            if not done:
                nxt.append(item)
        live = nxt


_UID = [0]


def _sb(nc, es, name, shape, dt):
    _UID[0] += 1
    return es.enter_context(nc.sbuf_tensor("sb%d_%s" % (_UID[0], name), list(shape), dt))


def emit_consts(g):
    nc, s, es = g.nc, g.s, g.es
    g.ident = _sb(nc, es, "ident", [128, 128], F32)
    g.identb = _sb(nc, es, "identb", [128, 128], BF16)
    g.ones = _sb(nc, es, "ones", [128, 128], F32)
    g.onesb = _sb(nc, es, "onesb", [128, 128], BF16)
    s.dma('sp', g.ident[:], g.d_ident[:, :], w=['ident'])
    s.dma('pool', g.identb[:], g.d_ident[:, :], w=['identb'])
    s.op('dve', lambda e: e.memset(g.ones[:], 1.0), w=['ones'])
    s.op('dve', lambda e: e.memset(g.onesb[:], 1.0), w=['onesb'])
    g.ccol = _sb(nc, es, "ccol", [128, 8], F32)
    g.sc = _sb(nc, es, "sc", [128, 8], F32)
    g.scb = _sb(nc, es, "scb", [128, 8, 128], F32)
    s.dma('sp', g.ccol[:], g.d_ccol[:, :], w=['ccol'])
    s.op('act', lambda e: e.activation(out=g.sc[:], in_=g.ccol[:], func=AF.Silu), r=['ccol'], w=['sc'])
    for kc in range(8):
        s.op('dve', lambda e, kc=kc: e.tensor_scalar(out=g.scb[:, kc, :], in0=g.ones[:], scalar1=g.sc[:, kc:kc + 1],
                                                     scalar2=None, op0=ALU.mult), r=['ones', 'sc'], w=['scb'])


def emit_mods(g, L, modw, modb_row, lng, lnb):
    nc, s, es = g.nc, g.s, g.es
    L.shift = _sb(nc, L.es, "shift", [128, 8], F32)
    L.scale1 = _sb(nc, L.es, "scale1", [128, 8], F32)
    L.gate_bc = _sb(nc, L.es, "gate_bc", [128, D], F32)
    L.lng_bc = _sb(nc, L.es, "lng_bc", [128, D], F32)
    L.lnb_bc = _sb(nc, L.es, "lnb_bc", [128, D], F32)
    s.dma('sp', L.lng_bc[:], lng.partition_broadcast(128), w=['lng_bc'])
    s.dma('sp', L.lnb_bc[:], lnb.partition_broadcast(128), w=['lnb_bc'])
    with ExitStack() as es2:
        mw = [_sb(nc, es2, "mw%d" % i, [128, 3 * D], F32) for i in range(2)]
        brow = _sb(nc, es2, "brow", [1, 3 * D], F32)
        bc = _sb(nc, es2, "modbc", [128, 2 * D], F32)
        one11 = _sb(nc, es2, "one11", [1, 1], F32)
        s.op('dve', lambda e: e.memset(one11[:], 1.0), w=['one11'])
        s.dma('sp', brow[:], modb_row[:, :], w=['brow'])
        P = g.ps
        for kc in range(8):
            t = mw[kc % 2]
            nm = 'mw%d' % (kc % 2)
            s.dma('sp' if kc % 2 == 0 else 'pool', t[:], modw[kc * 128:(kc + 1) * 128, :], w=[nm])
            for j in range(6):
                s.op('pe', lambda e, j=j, kc=kc, t=t: e.matmul(P[:, j * 512:(j + 1) * 512], lhsT=g.scb[:, kc, :],
                                                            rhs=t[:, j * 512:(j + 1) * 512], start=(kc == 0), stop=False),
                     r=[nm, 'scb'], w=['ps%d' % j], inc=(j == 5))
        for j in range(6):
            s.op('pe', lambda e, j=j: e.matmul(P[:, j * 512:(j + 1) * 512], lhsT=g.ones[0:1, :],
                                               rhs=brow[0:1, j * 512:(j + 1) * 512], start=False, stop=True),
                 r=['brow', 'ones'], w=['ps%d' % j])
        for j in range(4):
            s.op('act' if j % 2 else 'dve',
                 (lambda e, j=j: e.activation(out=bc[:, j * 512:(j + 1) * 512], in_=P[:, j * 512:(j + 1) * 512], func=AF.Copy))
                 if j % 2 else
                 (lambda e, j=j: e.tensor_copy(out=bc[:, j * 512:(j + 1) * 512], in_=P[:, j * 512:(j + 1) * 512])),
                 r=['ps%d' % j], w=['modbc%d' % j])
        for j in range(2):
            s.op('dve', lambda e, j=j: e.tensor_copy(out=L.gate_bc[:, j * 512:(j + 1) * 512], in_=P[:, (4 + j) * 512:(5 + j) * 512]),
                 r=['ps%d' % (4 + j)], w=['gate_bc'])
        for j in range(16):
            s.op('pe', lambda e, j=j: e.matmul(P[:, 6 * 512 + j:6 * 512 + j + 1], lhsT=bc[0:1, j * 128:(j + 1) * 128],
                                               rhs=one11[0:1, 0:1], start=True, stop=True),
                 r=['modbc%d' % (j // 4), 'one11'], w=['ps6'])
        s.op('dve', lambda e: e.tensor_copy(out=L.shift[:], in_=P[:, 6 * 512:6 * 512 + 8]), r=['ps6'], w=['shift'])
        s.op('dve', lambda e: e.tensor_scalar(out=L.scale1[:], in0=P[:, 6 * 512 + 8:6 * 512 + 16], scalar1=1.0, scalar2=None,
                                              op0=ALU.add), r=['ps6'], w=['scale1'])
        s.barrier()


def emit_hT(g, L, xin, xin_nm, hT_ap_fn, hT_nm, pbanks):
    s = g.s
    P = g.ps
    for kc in range(8):
        b = pbanks[kc // 4]
        off = b * 512 + (kc % 4) * 128
        s.op('pe', lambda e, kc=kc, off=off: e.transpose(P[:, off:off + 128], xin[:, kc * 128:(kc + 1) * 128], g.ident[:]),
             r=[xin_nm, 'ident'], w=['ps%d' % b])
    for kc in range(8):
        b = pbanks[kc // 4]
        off = b * 512 + (kc % 4) * 128
        s.op('act', lambda e, kc=kc, off=off: e.activation(out=hT_ap_fn(kc), in_=P[:, off:off + 128], func=AF.Identity,
                                                           scale=L.scale1[:, kc:kc + 1], bias=L.shift[:, kc:kc + 1]),
             r=['ps%d' % b, 'scale1', 'shift'], w=[hT_nm])


def emit_epilogue(g, L, ybanks, xres, xres_nm, zb, xo, xo_nm, dst_rows):
    s = g.s
    P = g.ps
    for j in range(2):
        b = ybanks[j]
        s.op('dve', lambda e, j=j, b=b: e.tensor_tensor(out=zb[:, j * 512:(j + 1) * 512], in0=P[:, b * 512:(b + 1) * 512],
                                                        in1=L.gate_bc[:, j * 512:(j + 1) * 512], op=ALU.mult),
             r=['ps%d' % b, 'gate_bc'], w=['zb'])
    s.op('dve', lambda e: e.scalar_tensor_tensor(out=zb[:], in0=xres[:], scalar=ALPHA, in1=zb[:], op0=ALU.mult, op1=ALU.add),
         r=[xres_nm, 'zb'], w=['zb'])
    st = g.lnst
    for j in range(2):
        s.op('dve', lambda e, j=j: e.bn_stats(out=st[:, j * 6:(j + 1) * 6], in_=zb[:, j * 512:(j + 1) * 512]), r=['zb'], w=['lnst'])
    s.op('dve', lambda e: e.bn_aggr(out=g.lnmv[:, 0:2], in_=st[:, 0:12]), r=['lnst'], w=['lnmv'])
    s.op('dve', lambda e: e.tensor_scalar(out=g.lnmv[:, 2:3], in0=g.lnmv[:, 1:2], scalar1=LN_EPS, scalar2=None, op0=ALU.add),
         r=['lnmv'], w=['lnmv2'])
    s.op('act', lambda e: e.activation(out=g.lnmv[:, 3:4], in_=g.lnmv[:, 2:3], func=AF.Sqrt), r=['lnmv2'], w=['lnmv3'])
    s.op('dve', lambda e: e.reciprocal(out=g.lnmv[:, 4:5], in_=g.lnmv[:, 3:4]), r=['lnmv3'], w=['lnmv4'])
    s.op('dve', lambda e: e.tensor_scalar(out=zb[:], in0=zb[:], scalar1=g.lnmv[:, 0:1], scalar2=g.lnmv[:, 4:5],
                                          op0=ALU.subtract, op1=ALU.mult), r=['zb', 'lnmv', 'lnmv4'], w=['zb'])
    s.op('pool', lambda e: e.tensor_tensor(out=zb[:], in0=zb[:], in1=L.lng_bc[:], op=ALU.mult), r=['zb', 'lng_bc'], w=['zb'])
    s.op('pool', lambda e: e.tensor_tensor(out=xo[:], in0=zb[:], in1=L.lnb_bc[:], op=ALU.add), r=['zb', 'lnb_bc'], w=[xo_nm])
    s.dma('sp', dst_rows, xo[:], r=[xo_nm], w=[])


def emit_ffn(g, li, src, dst):
    nc, s = g.nc, g.s
    TT = 256
    NT = S // TT
    NBT = TT // 128
    with ExitStack() as les:
        L = Ctx()
        L.es = les
        emit_mods(g, L, g.d_f_mod_w[li], g.d_f_mod_b[li], g.d_f_ln_g[li], g.d_f_ln_b[li])
        wup = _sb(nc, les, "wup", [128, 8, 2 * DFF], BF16)
        wdn = _sb(nc, les, "wdn", [128, NFC, D], BF16)
        cw = _sb(nc, les, "cw", [128, 2 * NFC, 4], F32)
        halo = _sb(nc, les, "halo", [128, 2 * NFC, 2], F32)
        hT = [_sb(nc, les, "hT%d" % i, [128, 8, TT], BF16) for i in range(2)]
        gT = _sb(nc, les, "gT", [128, NFC, TT], BF16)
        upre = [_sb(nc, les, "upre%d" % i, [128, TT + 2], F32) for i in range(2)]
        c0 = [_sb(nc, les, "c0%d" % i, [128, TT], F32) for i in range(2)]
        asil = _sb(nc, les, "asil", [128, TT], F32)
        xin = [_sb(nc, les, "xin%d" % i, [128, D], F32) for i in range(2)]
        xrs = [_sb(nc, les, "xrs%d" % i, [128, D], F32) for i in range(2)]
        zb = _sb(nc, les, "zb", [128, D], F32)
        xo = [_sb(nc, les, "xo%d" % i, [128, D], F32) for i in range(2)]
        P = g.ps
        wupd = g.d_f_w_up[li].rearrange("(kc p) n -> p kc n", p=128)
        for kc in range(8):
            for hf in range(2):
                s.dma('pool', wup[:, kc, hf * DFF:(hf + 1) * DFF], wupd[:, kc, hf * DFF:(hf + 1) * DFF], w=['wup'])
        wdnd = g.d_f_w_down[li].rearrange("(fc p) n -> p fc n", p=128)
        for fc in range(NFC):
            s.dma('pool', wdn[:, fc, :], wdnd[:, fc, :], w=['wdn'])
        s.dma('sp', cw[:], g.d_f_cw[li], w=['cw'])
        s.op('dve', lambda e: e.memset(halo[:], 0.0), w=['halo'])
        blk = 0
        for t in range(NT):
            t0 = t * TT
            hs = t % 2
            hnm = 'hT%d' % hs
            for bi in range(NBT):
                xs_ = (t * NBT + bi) % 2
                s.dma('sp', xin[xs_][:], src[t0 + bi * 128:t0 + (bi + 1) * 128, :], w=['xin%d' % xs_])
                emit_hT(g, L, xin[xs_], 'xin%d' % xs_, lambda kc, bi=bi, hs=hs: hT[hs][:, kc, bi * 128:(bi + 1) * 128], hnm, (0, 1))
            for j in range(NFC):
                for half in range(2):
                    fc = j + half * NFC
                    b = 2 + ((2 * j + half) % 4)
                    pb = 'ps%d' % b
                    up = upre[half]
                    unm = 'upre%d' % half
                    for kc in range(8):
                        s.op('pe', lambda e, kc=kc, fc=fc, b=b: e.matmul(P[:, b * 512:b * 512 + TT], lhsT=wup[:, kc, fc * 128:(fc + 1) * 128],
                                                                      rhs=hT[hs][:, kc, :], start=(kc == 0), stop=(kc == 7)),
                             r=['wup', hnm], w=[pb], inc=(kc == 7))
                    s.op('pool', lambda e, fc=fc, up=up: e.tensor_copy(out=up[:, 0:2], in_=halo[:, fc, :]), r=['halo'], w=[unm])
                    s.op('act', lambda e, b=b, up=up: e.activation(out=up[:, 2:TT + 2], in_=P[:, b * 512:b * 512 + TT], func=AF.Copy),
                         r=[pb], w=[unm])
                    s.op('act', lambda e, b=b, fc=fc, half=half: e.activation(out=c0[half][:], in_=P[:, b * 512:b * 512 + TT], func=AF.Identity,
                                                                             scale=cw[:, fc, 2:3], bias=cw[:, fc, 3:4]),
                         r=[pb, 'cw'], w=['c0%d' % half])
                    s.op('pool', lambda e, fc=fc, up=up: e.tensor_copy(out=halo[:, fc, :], in_=up[:, TT:TT + 2]), r=[unm], w=['halo'])
                    s.op('dve', lambda e, fc=fc, up=up, half=half: e.scalar_tensor_tensor(out=c0[half][:], in0=up[:, 1:TT + 1], scalar=cw[:, fc, 1:2],
                                                                                        in1=c0[half][:], op0=ALU.mult, op1=ALU.add),
                         r=[unm, 'cw', 'c0%d' % half], w=['c0%d' % half])
                    s.op('dve', lambda e, fc=fc, up=up, half=half: e.scalar_tensor_tensor(out=c0[half][:], in0=up[:, 0:TT], scalar=cw[:, fc, 0:1],
                                                                                        in1=c0[half][:], op0=ALU.mult, op1=ALU.add),
                         r=[unm, 'cw', 'c0%d' % half], w=['c0%d' % half])
                    if half == 0:
                        s.op('act', lambda e: e.activation(out=asil[:], in_=c0[0][:], func=AF.Silu), r=['c00'], w=['asil'])
                    else:
                        s.op('dve', lambda e, j=j: e.tensor_tensor(out=gT[:, j, :], in0=asil[:], in1=c0[1][:], op=ALU.mult),
                             r=['asil', 'c01'], w=['gT'])
            for bi in range(NBT):
                r0 = t0 + bi * 128
                xs_ = blk % 2
                blk += 1
                s.dma('sp', xrs[xs_][:], src[r0:r0 + 128, :], w=['xrs%d' % xs_])
                for half in range(2):
                    b = 6 + half
                    for j in range(NFC):
                        s.op('pe', lambda e, j=j, half=half, b=b, bi=bi: e.matmul(P[:, b * 512:(b + 1) * 512], lhsT=gT[:, j, bi * 128:(bi + 1) * 128],
                                                                                  rhs=wdn[:, j, half * 512:(half + 1) * 512],
                                                                                  start=(j == 0), stop=(j == NFC - 1)),
                             r=['gT', 'wdn'], w=['ps%d' % b], inc=(j == NFC - 1))
                emit_epilogue(g, L, (6, 7), xrs[xs_], 'xrs%d' % xs_, zb, xo[xs_], 'xo%d' % xs_, dst[r0:r0 + 128, :])
        s.barrier()


NEG = -1.0e30
DSTOP = int(os.environ.get('DSA_STOP', '99'))
DSUB = int(os.environ.get('DSA_SUB', '99'))
GSTOP = int(os.environ.get('GDN_STOP', '99'))
DSC = int(os.environ.get('DSA_SC', '3'))
DSKIP = os.environ.get('DSA_SKIP', '').split(',')
REP = -3.0e38


def emit_dsa(g, src, dst):
    nc, s = g.nc, g.s
    P = g.ps
    with ExitStack() as les:
        L = Ctx()
        L.es = les
        emit_mods(g, L, g.d_e_mod_w[0], g.d_e_mod_b[0], g.d_e_ln_g[0], g.d_e_ln_b[0])
        A = lambda name, shape, dt: _sb(nc, les, name, shape, dt)
        win = A("win", [128, 8, 1536], BF16)
        wif = A("wif", [128, 8, 4], F32)
        wiw = A("wiw", [128, 8, 4], BF16)
        poolw = A("poolw", [128, 4, 128], BF16)
        pscale = A("pscale", [128, 4], F32)
        kvn_bc = A("kvn_bc", [128, 128], F32)
        ukT = A("ukT", [128, 4, 128], BF16)
        uvpad = A("uvpad", [128, 8, 128], BF16)
        wout = A("wout", [128, 8, D], BF16)
        negI4 = A("negI4", [128, 512], BF16)
        corr = A("corr", [128, 4, 15], F32)
        ckvn_all = A("ckvn_all", [128, NB, 128], BF16)
        ckvnT_all = A("ckvnT_all", [128, S], BF16)
        kiT_all = A("kiT_all", [128, S], BF16)
        xin = [A("xin%d" % i, [128, D], F32) for i in range(3)]
        hT = A("hT", [128, 8, 128], BF16)
        ut = A("ut", [128, 4, 143], F32)
        ta = A("ta", [128, 143], F32)
        tb = A("tb", [128, 143], F32)
        dT = A("dT", [128, 4, 128], BF16)
        qT = A("qT", [128, 512], BF16)
        qiT = [A("qiT%d" % i, [128, 256], BF16) for i in range(2)]
        wis = [A("wis%d" % i, [128, 4], F32) for i in range(2)]
        qlT = [A("qlT%d" % i, [128, 1024], BF16) for i in range(3)]
        W = [A("W%d" % i, [128, S], F32) for i in range(2)]
        rbuf = [A("rbuf%d" % i, [128, 512], F32) for i in range(2)]
        rb2 = A("rb2", [128, 512], F32)
        notm = [A("notm%d" % i, [128, S], BF16) for i in range(2)]
        m8 = A("m8", [128, 8], F32)
        pT = [A("pT%d" % i, [128, 512], BF16) for i in range(2)]
        rden = A("rden", [128, 512], F32)
        oTn = A("oTn", [128, 1024], BF16)
        yinT = [A("yinT%d" % i, [128, 1024], BF16) for i in range(3)]
        zb = A("zb", [128, D], F32)
        xo = [A("xo%d" % i, [128, D], F32) for i in range(2)]
        sq = A("sq", [128, 128], F32)
        ckf = A("ckf", [128, 128], F32)
        rs = A("rs", [128, 4], F32)
        Pb3 = P[:, 3 * 512:4 * 512].bitcast(BF16)

        wind = g.d_e_w_in[0].rearrange("(kc p) n -> p kc n", p=128)
        for kc in range(8):
            s.dma('pool', win[:, kc, 0:1472], wind[:, kc, 0:1472], w=['win'])
            s.dma('pool', win[:, kc, 1472:1536], wind[:, kc, 1408:1472], w=['win'])
        s.dma('sp', wif[:], wind[:, :, 1472:1476], w=['wif'])
        s.op('dve', lambda e: e.tensor_copy(out=wiw[:], in_=wif[:]), r=['wif'], w=['wiw'])
        s.dma('pool', poolw[:], g.d_e_pool_w[0].rearrange("g c d -> c g d"), w=['poolw'])
        s.dma('sp', pscale[:], g.d_pscale[:, :], w=['pscale'])
        s.dma('sp', kvn_bc[:], g.d_e_kv_norm[0].partition_broadcast(128), w=['kvn_bc'])
        s.dma('pool', ukT[:], g.d_ukT[:, :, :], w=['ukT'])
        s.dma('pool', uvpad[:], g.d_uvpad[:, :, :], w=['uvpad'])
        woutd = g.d_e_w_out[0].rearrange("(kc p) n -> p kc n", p=128)
        for kc in range(8):
            s.dma('pool', wout[:, kc, :], woutd[:, kc, :], w=['wout'])
        s.dma('pool', negI4[:], g.d_negI4[:, :], w=['negI4'])
        s.dma('sp', corr[:], g.d_corr[:, :, :], w=['corr'])
        s.op('dve', lambda e: e.memset(ut[:], 0.0), w=['ut'])

        def front_a(qb):
            sl = qb % 2
            s3 = qb % 3
            t0 = qb * 128
            nk = t0 + 128
            xn = 'xin%d' % s3
            s.dma('sp', xin[s3][:], src[t0:t0 + 128, :], w=[xn])
            emit_hT(g, L, xin[s3], xn, lambda kc: hT[:, kc, :], 'hT', (0, 1))
            yield
            def grp(out_ap, cols, bank, last=True):
                for kc in range(8):
                    s.op('pe', lambda e, kc=kc: e.matmul(out_ap, lhsT=win[:, kc, cols[0]:cols[1]], rhs=hT[:, kc, :],
                                                        start=(kc == 0), stop=(kc == 7)),
                         r=['win', 'hT'], w=['ps%d' % bank], inc=(kc == 7))
            for gi in range(4):
                grp(P[:, gi * 128:(gi + 1) * 128], (gi * 128, (gi + 1) * 128), 0)
                yield
            for j in range(4):
                grp(P[:, 512 + j * 128:512 + (j + 1) * 128], (512 + j * 128, 512 + (j + 1) * 128), 1)
                yield
            for j in range(2):
                grp(P[:, 1024 + j * 128:1024 + (j + 1) * 128], (1152 + j * 128, 1152 + (j + 1) * 128), 2)
                yield
            grp(P[:, 1024 + 256:1024 + 384], (1408, 1536), 2)
            yield
            for kc in range(8):
                s.op('pe', lambda e, kc=kc: e.matmul(P[:, 1536:1536 + 128], lhsT=hT[:, kc, :], rhs=win[:, kc, 1024:1152],
                                                    start=(kc == 0), stop=(kc == 7)), r=['win', 'hT'], w=['ps3'], inc=(kc == 7))
            for kc in range(8):
                s.op('pe', lambda e, kc=kc: e.matmul(P[:, 1536 + 128:1536 + 132], lhsT=hT[:, kc, :], rhs=wiw[:, kc, :],
                                                    start=(kc == 0), stop=(kc == 7)), r=['wiw', 'hT'], w=['ps3'], inc=(kc == 7))
            yield
            s.op('act', lambda e: e.activation(out=ut[:, :, 15:143], in_=P[:, 0:512].rearrange("p (g t) -> p g t", g=4), func=AF.Copy),
                 r=['ps0'], w=['ut'])
            s.op('act', lambda e: e.activation(out=qT[:], in_=P[:, 512:1024], func=AF.Copy), r=['ps1'], w=['qT'])
            s.op('dve', lambda e: e.tensor_copy(out=qiT[sl][:], in_=P[:, 1024:1024 + 256]), r=['ps2'], w=['qiT%d' % sl])
            s.op('dve', lambda e: e.tensor_copy(out=kiT_all[:, t0:t0 + 128], in_=P[:, 1024 + 256:1024 + 384]), r=['ps2'], w=['kiT_all'])
            s.op('dve', lambda e: e.tensor_copy(out=wis[sl][:], in_=P[:, 1536 + 128:1536 + 132]), r=['ps3'], w=['wis%d' % sl])
            yield
            s.op('act', lambda e: e.activation(out=ckf[:], in_=P[:, 1536:1536 + 128], func=AF.Copy), r=['ps3'], w=['ckf'])
            s.op('dve', lambda e: e.tensor_tensor(out=sq[:], in0=ckf[:], in1=ckf[:], op=ALU.mult), r=['ckf'], w=['sq'])
            s.op('dve', lambda e: e.reduce_sum(out=rs[:, 0:1], in_=sq[:], axis=AX.X), r=['sq'], w=['rs0'])
            s.op('dve', lambda e: e.tensor_scalar(out=rs[:, 1:2], in0=rs[:, 0:1], scalar1=1.0 / 128, scalar2=RMS_EPS, op0=ALU.mult, op1=ALU.add),
                 r=['rs0'], w=['rs1'])
            s.op('act', lambda e: e.activation(out=rs[:, 2:3], in_=rs[:, 1:2], func=AF.Sqrt), r=['rs1'], w=['rs2'])
            s.op('dve', lambda e: e.reciprocal(out=rs[:, 3:4], in_=rs[:, 2:3]), r=['rs2'], w=['rs3'])
            s.op('dve', lambda e: e.scalar_tensor_tensor(out=ckf[:], in0=ckf[:], scalar=rs[:, 3:4], in1=kvn_bc[:],
                                                         op0=ALU.mult, op1=ALU.mult), r=['ckf', 'rs3', 'kvn_bc'], w=['ckf'])
            s.op('act', lambda e: e.activation(out=ckvn_all[:, qb, :], in_=ckf[:], func=AF.Copy), r=['ckf'], w=['ckvn_all'])
            s.op('pe', lambda e: e.transpose(P[:, 1536 + 256:1536 + 384], ckf[:], g.ident[:]), r=['ckf', 'ident'], w=['ps3'])
            s.op('act', lambda e: e.activation(out=ckvnT_all[:, t0:t0 + 128], in_=P[:, 1536 + 256:1536 + 384], func=AF.Copy), r=['ps3'], w=['ckvnT_all'])
            yield
            for gi in range(4):
                win_ = 2 << gi
                U = ut[:, gi, :]
                s.op('dve', lambda e, U=U: e.tensor_tensor(out=ta[:, 1:143], in0=U[:, 1:143], in1=U[:, 0:142], op=ALU.add), r=['ut'], w=['ta'])
                sw = ta
                swn = 'ta'
                if gi >= 1:
                    s.op('dve', lambda e: e.tensor_tensor(out=tb[:, 3:143], in0=ta[:, 3:143], in1=ta[:, 1:141], op=ALU.add), r=['ta'], w=['tb'])
                    sw, swn = tb, 'tb'
                if gi >= 2:
                    s.op('dve', lambda e: e.tensor_tensor(out=ta[:, 7:143], in0=tb[:, 7:143], in1=tb[:, 3:139], op=ALU.add), r=['tb'], w=['ta'])
                    sw, swn = ta, 'ta'
                if gi >= 3:
                    s.op('dve', lambda e: e.tensor_tensor(out=tb[:, 15:143], in0=ta[:, 15:143], in1=ta[:, 7:135], op=ALU.add), r=['ta'], w=['tb'])
                    sw, swn = tb, 'tb'
                if qb == 0:
                    s.op('dve', lambda e, sw=sw, gi=gi, win_=win_: e.tensor_tensor(out=sw[:, 15:15 + win_ - 1], in0=sw[:, 15:15 + win_ - 1],
                                                                                 in1=corr[:, gi, 0:win_ - 1], op=ALU.mult),
                         r=[swn, 'corr'], w=[swn])
                s.op('dve', lambda e, sw=sw, gi=gi, win_=win_, U=U: e.scalar_tensor_tensor(out=dT[:, gi, :], in0=sw[:, 15:143], scalar=1.0 / win_,
                                                                                          in1=U[:, 15:143], op0=ALU.mult, op1=ALU.subtract),
                     r=[swn, 'ut'], w=['dT'])
                yield
            s.op('pool', lambda e: e.tensor_copy(out=ut[:, :, 0:15], in_=ut[:, :, 128:143]), r=['ut'], w=['ut'])
            for gi in range(4):
                s.op('pe', lambda e, gi=gi: e.matmul(P[:, gi * 128:(gi + 1) * 128], lhsT=poolw[:, gi, :], rhs=dT[:, gi, :], start=True, stop=True),
                     r=['poolw', 'dT'], w=['ps0'])
            for gi in range(4):
                s.op('act', lambda e, gi=gi: e.activation(out=yinT[s3][:, gi * 128:(gi + 1) * 128], in_=P[:, gi * 128:(gi + 1) * 128],
                                                          func=AF.Identity, scale=pscale[:, gi:gi + 1]),
                     r=['ps0', 'pscale'], w=['yinT%d' % s3])
            yield
            for h in range(8):
                po = (h % 2) * 64
                bank = 1 + h % 2
                off = bank * 512 + (h // 2) * 128
                s.op('pe', lambda e, h=h, po=po, off=off: e.matmul(P[:, off:off + 128], lhsT=ukT[po:po + 64, h // 2, :],
                                                                  rhs=qT[po:po + 64, (h // 2) * 128:(h // 2 + 1) * 128], start=True, stop=True),
                     r=['ukT', 'qT'], w=['ps%d' % bank])
            for j in range(2):
                s.op('act', lambda e, j=j: e.activation(out=qlT[s3][:, j * 512:(j + 1) * 512], in_=P[:, (1 + j) * 512:(2 + j) * 512], func=AF.Copy),
                     r=['ps%d' % (1 + j)], w=['qlT%d' % s3])
            yield
            Wn = 'W%d' % sl
            cnt = 0
            chunks = []
            k0_ = 0
            while k0_ < nk:
                rem = nk - k0_
                w0 = 512 if rem >= 512 else (256 if rem >= 256 else 128)
                chunks.append((k0_, w0))
                k0_ += w0
            for (k0, w_) in chunks:
                for h in range(4):
                    po = (h % 2) * 64
                    bank = 2 + cnt % 2
                    rb = rbuf[cnt % 2]
                    rbn = 'rbuf%d' % (cnt % 2)
                    cnt += 1
                    s.op('pe', lambda e, h=h, po=po, bank=bank, k0=k0, w_=w_: e.matmul(P[:, bank * 512:bank * 512 + w_],
                                                                                     lhsT=qiT[sl][po:po + 64, (h // 2) * 128:(h // 2 + 1) * 128],
                                                                                     rhs=kiT_all[po:po + 64, k0:k0 + w_], start=True, stop=True),
                         r=['qiT%d' % sl, 'kiT_all'], w=['ps%d' % bank])
                    s.op('act', lambda e, bank=bank, rb=rb, w_=w_: e.activation(out=rb[:, 0:w_], in_=P[:, bank * 512:bank * 512 + w_], func=AF.Relu),
                         r=['ps%d' % bank], w=[rbn])
                    if h == 0:
                        s.op('dve', lambda e, rb=rb, k0=k0, w_=w_: e.tensor_scalar(out=W[sl][:, k0:k0 + w_], in0=rb[:, 0:w_], scalar1=wis[sl][:, 0:1],
                                                                                 scalar2=None, op0=ALU.mult), r=[rbn, 'wis%d' % sl], w=[Wn])
                    else:
                        s.op('dve', lambda e, rb=rb, k0=k0, w_=w_, h=h: e.scalar_tensor_tensor(out=W[sl][:, k0:k0 + w_], in0=rb[:, 0:w_], scalar=wis[sl][:, h:h + 1],
                                                                                            in1=W[sl][:, k0:k0 + w_], op0=ALU.mult, op1=ALU.add),
                             r=[rbn, 'wis%d' % sl, Wn], w=[Wn])
                    yield

        def topk(qb):
            sl = qb % 2
            nk = qb * 128 + 128
            Wn = 'W%d' % sl
            nmn = 'notm%d' % sl
            if nk <= 256:
                s.op('dve', lambda e: e.memset(notm[sl][:, 0:nk], 0.0), w=[nmn])
            else:
                s.op('dve', lambda e: e.memset(W[sl][0:64, nk - 64:nk], NEG), r=[Wn], w=[Wn])
                for it in range(32):
                    s.op('dve', lambda e: e.max(out=m8[:], in_=W[sl][:, 0:nk]), r=[Wn], w=['m8'])
                    s.op('dve', lambda e: e.match_replace(out=W[sl][:, 0:nk], in_to_replace=m8[:], in_values=W[sl][:, 0:nk], imm_value=REP),
                         r=[Wn, 'm8'], w=[Wn])
                    yield
                s.op('dve', lambda e: e.tensor_scalar(out=notm[sl][:, 0:nk], in0=W[sl][:, 0:nk], scalar1=0.5 * REP, scalar2=None, op0=ALU.is_gt),
                     r=[Wn], w=[nmn])
            s.op('dve', lambda e: e.memset(notm[sl][0:64, nk - 64:nk], 1.0), r=[nmn], w=[nmn])
            yield

        def back(qb):
            sl = qb % 2
            s3 = qb % 3
            t0 = qb * 128
            cnt = 0
            for hg in range(2):
                for kb in range(qb + 1):
                    j = cnt % 2
                    cnt += 1
                    bank = 4 + j
                    s.op('pe', lambda e, bank=bank, kb=kb, hg=hg: e.matmul(P[:, bank * 512:(bank + 1) * 512], lhsT=ckvnT_all[:, kb * 128:(kb + 1) * 128],
                                                                          rhs=qlT[s3][:, hg * 512:(hg + 1) * 512], start=True, stop=False),
                         r=['ckvnT_all', 'qlT%d' % s3], w=['ps%d' % bank], inc=False)
                    s.op('pe', lambda e, bank=bank, kb=kb: e.matmul(P[:, bank * 512:(bank + 1) * 512], lhsT=notm[sl][:, kb * 128:(kb + 1) * 128],
                                                                   rhs=negI4[:], start=False, stop=True),
                         r=['notm%d' % sl, 'negI4'], w=['ps%d' % bank])
                    s.op('act', lambda e, bank=bank, j=j: e.activation(out=pT[j][:], in_=P[:, bank * 512:(bank + 1) * 512], func=AF.Exp, scale=0.125),
                         r=['ps%d' % bank], w=['pT%d' % j])
                    s.op('pe', lambda e, kb=kb, j=j: e.matmul(P[:, 6 * 512:7 * 512], lhsT=ckvn_all[:, kb, :], rhs=pT[j][:], start=(kb == 0), stop=(kb == qb)),
                         r=['ckvn_all', 'pT%d' % j], w=['ps6'], inc=False)
                    s.op('pe', lambda e, kb=kb, j=j: e.matmul(P[:, 7 * 512:8 * 512], lhsT=g.onesb[:], rhs=pT[j][:], start=(kb == 0), stop=(kb == qb)),
                         r=['onesb', 'pT%d' % j], w=['ps7'])
                    yield
                s.op('dve', lambda e: e.reciprocal(out=rden[:], in_=P[:, 7 * 512:8 * 512]), r=['ps7'], w=['rden'])
                s.op('dve', lambda e, hg=hg: e.tensor_tensor(out=oTn[:, hg * 512:(hg + 1) * 512], in0=P[:, 6 * 512:7 * 512], in1=rden[:], op=ALU.mult),
                     r=['ps6', 'rden'], w=['oTn'])
                yield
            for hp in range(4):
                for h2 in range(2):
                    h = 2 * hp + h2
                    s.op('pe', lambda e, hp=hp, h=h, h2=h2: e.matmul(P[:, 7 * 512 + hp * 128:7 * 512 + (hp + 1) * 128], lhsT=uvpad[:, h, :],
                                                                    rhs=oTn[:, (h // 2 + 4 * (h % 2)) * 128:(h // 2 + 4 * (h % 2) + 1) * 128], start=(h2 == 0), stop=(h2 == 1)),
                         r=['uvpad', 'oTn'], w=['ps7'], inc=(h2 == 1))
            s.op('act', lambda e: e.activation(out=yinT[s3][:, 512:1024], in_=P[:, 7 * 512:8 * 512], func=AF.Copy), r=['ps7'], w=['yinT%d' % s3])
            yield
            for half in range(2):
                b = 4 + half
                for kc in range(8):
                    s.op('pe', lambda e, kc=kc, half=half, b=b: e.matmul(P[:, b * 512:(b + 1) * 512], lhsT=yinT[s3][:, kc * 128:(kc + 1) * 128],
                                                                        rhs=wout[:, kc, half * 512:(half + 1) * 512], start=(kc == 0), stop=(kc == 7)),
                         r=['yinT%d' % s3, 'wout'], w=['ps%d' % b], inc=(kc == 7))
            yield
            emit_epilogue(g, L, (4, 5), xin[s3], 'xin%d' % s3, zb, xo[sl], 'xo%d' % sl, dst[t0:t0 + 128, :])
            yield

        nblk = g.nblk_dsa
        for it in range(nblk + 2):
            gens = []
            if it < nblk:
                gens.append([front_a(it), 1])
            if 1 <= it <= nblk:
                gens.append([topk(it - 1), 1])
            if it >= 2:
                gens.append([back(it - 2), 1])
            _interleave(gens)
        s.barrier()


def emit_gdn(g, src, dst):
    nc, s = g.nc, g.s
    P = g.ps
    with ExitStack() as les:
        L = Ctx()
        L.es = les
        emit_mods(g, L, g.d_o_mod_w[0], g.d_o_mod_b[0], g.d_o_ln_g[0], g.d_o_ln_b[0])
        A = lambda name, shape, dt: _sb(nc, les, name, shape, dt)
        win = A("gwin", [128, 8, 4096], BF16)
        wbf = A("gwbf", [128, 8, 16], F32)
        wbb = A("gwbb", [128, 8, 16], BF16)
        wout = A("gwout", [128, 8, D], BF16)
        cw = A("gcw", [128, 24, 4], F32)
        msl = A("msl", [128, 128], F32)
        mil = A("mil", [128, 128], F32)
        triu = A("triu", [128, 128], F32)
        mdm = A("mdm", [128, 5, 128], BF16)
        mdmT = A("mdmT", [128, 5, 128], BF16)
        alog = A("alog", [128, 8], F32)
        dtb = A("dtb", [128, 8], F32)
        onb = A("onb", [128, D], F32)
        xin = [A("gxin%d" % i, [128, D], F32) for i in range(1)]
        xrs = A("gxrs", [128, D], F32)
        hT = A("ghT", [128, 8, 128], BF16)
        xw = [A("xw%d" % i, [128, 131], F32) for i in range(2)]
        halo = A("ghalo", [128, 24, 3], F32)
        cbuf = [A("cbuf%d" % i, [128, 128], F32) for i in range(2)]
        act = [A("gact%d" % i, [128, 24 * 128], F32) for i in range(2)]
        sq = A("gsq", [128, 1024], F32)
        rstd = sq
        qkb = [A("qkb%d" % i, [128, 2048], BF16) for i in range(2)]
        gsil = [A("gsil%d" % i, [128, D], BF16) for i in range(3)]
        ba = A("ba", [128, 16], F32)
        tmpa = A("tmpa", [128, 8], F32)
        smb = [A("smb%d" % i, [128, 48], F32) for i in range(2)]
        zsm = A("zsm", [128, 8], F32)
        HP = []
        for p in range(2):
            h_ = Ctx()
            h_.sm = A("hsm%d" % p, [128, 2], F32)
            for nm in ("E", "EB", "t1", "Nf", "atf"):
                setattr(h_, nm, A("h%s%d" % (nm, p), [128, 128], F32))
            for nm in ("X", "Xt", "Q2", "Q2t", "Q4", "Q4t", "Yv", "Yt", "attT", "qdT", "kd", "Kbg", "Vb", "nwT", "vnew"):
                setattr(h_, nm, A("h%s%d" % (nm, p), [128, 128], BF16))
            h_.NM = A("hNM%d" % p, [128, 5, 128], BF16)
            h_.NMt = A("hNMt%d" % p, [128, 5, 128], BF16)
            HP.append(h_)
        Sf = A("gSf", [128, 8, 128], F32)
        Sb = A("gSb", [128, 8, 128], BF16)
        osb = [A("gosb%d" % i, [128, D], F32) for i in range(2)]
        yinT = A("gyinT", [128, 1024], BF16)
        zb = A("gzb", [128, D], F32)
        xo = [A("gxo%d" % i, [128, D], F32) for i in range(1)]
        wind = g.d_o_w_in[0].rearrange("(kc p) n -> p kc n", p=128)
        for kc in range(8):
            for q4 in range(2):
                s.dma('pool', win[:, kc, q4 * 2048:(q4 + 1) * 2048], wind[:, kc, q4 * 2048:(q4 + 1) * 2048], w=['gwin'])
        s.dma('sp', wbf[:], wind[:, :, 4096:4112], w=['gwbf'])
        s.op('dve', lambda e: e.tensor_copy(out=wbb[:], in_=wbf[:]), r=['gwbf'], w=['gwbb'])
        woutd = g.d_o_w_out[0].rearrange("(kc p) n -> p kc n", p=128)
        for kc in range(8):
            s.dma('pool', wout[:, kc, :], woutd[:, kc, :], w=['gwout'])
        s.dma('sp', cw[:], g.d_o_cw[:, :, :], w=['gcw'])
        s.dma('sp', msl[:], g.d_msl[:, :], w=['msl'])
        s.dma('sp', mil[:], g.d_mil[:, :], w=['mil'])
        s.dma('sp', triu[:], g.d_triu[:, :], w=['triu'])
        s.dma('pool', mdm[:], g.d_mdm[:, :, :], w=['mdm'])
        s.dma('pool', mdmT[:], g.d_mdmT[:, :, :], w=['mdmT'])
        s.dma('sp', alog[:], g.d_o_a_log[0].partition_broadcast(128), w=['alog'])
        s.dma('sp', dtb[:], g.d_o_dt_bias[0].partition_broadcast(128), w=['dtb'])
        s.dma('sp', onb[:], g.d_onb[0].partition_broadcast(128), w=['onb'])
        s.op('act', lambda e: e.activation(out=alog[:], in_=alog[:], func=AF.Exp), r=['alog'], w=['alog'])
        s.op('dve', lambda e: e.tensor_scalar(out=alog[:], in0=alog[:], scalar1=-1.0, scalar2=None, op0=ALU.mult), r=['alog'], w=['alog'])
        s.op('dve', lambda e: e.memset(halo[:], 0.0), w=['ghalo'])
        s.op('dve', lambda e: e.memset(Sf[:], 0.0), w=['gSf0', 'gSf1'])
        s.op('dve', lambda e: e.memset(Sb[:], 0.0), w=['gSb0', 'gSb1'])
        DK = float(128 ** -0.5)

        def phaseA(qb):
            t0 = qb * 128
            bp = qb % 2
            x3 = qb % 3
            xn = 'gxin0'
            an = 'gact%d' % bp
            sn = 'smb%d' % bp
            sm = smb[bp]
            ac = act[bp]
            s.dma('sp', xin[0][:], src[t0:t0 + 128, :], w=[xn])
            emit_hT(g, L, xin[0], xn, lambda kc: hT[:, kc, :], 'ghT', (0, 1))
            yield
            for ch in range(24):
                b = ch % 2
                cb = cbuf[b]
                cn = 'cbuf%d' % b
                for kc in range(8):
                    s.op('pe', lambda e, kc=kc, ch=ch, b=b: e.matmul(P[:, b * 512:b * 512 + 128], lhsT=win[:, kc, ch * 128:(ch + 1) * 128], rhs=hT[:, kc, :],
                                                                    start=(kc == 0), stop=(kc == 7)), r=['gwin', 'ghT'], w=['ps%d' % b], inc=(kc == 7))
                xb = xw[b]
                xbn = 'xw%d' % b
                s.op('pool', lambda e, ch=ch, xb=xb: e.tensor_copy(out=xb[:, 0:3], in_=halo[:, ch, :]), r=['ghalo'], w=[xbn])
                s.op('act', lambda e, b=b, xb=xb: e.activation(out=xb[:, 3:131], in_=P[:, b * 512:b * 512 + 128], func=AF.Copy), r=['ps%d' % b], w=[xbn])
                s.op('act', lambda e, ch=ch, b=b, cb=cb: e.activation(out=cb[:], in_=P[:, b * 512:b * 512 + 128], func=AF.Identity, scale=cw[:, ch, 3:4]),
                     r=['ps%d' % b, 'gcw'], w=[cn])
                s.op('pool', lambda e, ch=ch, xb=xb: e.tensor_copy(out=halo[:, ch, :], in_=xb[:, 128:131]), r=[xbn], w=['ghalo'])
                for j in range(3):
                    s.op('dve', lambda e, ch=ch, j=j, cb=cb, xb=xb: e.scalar_tensor_tensor(out=cb[:], in0=xb[:, j:j + 128], scalar=cw[:, ch, j:j + 1], in1=cb[:],
                                                                                          op0=ALU.mult, op1=ALU.add), r=[xbn, 'gcw', cn], w=[cn])
                s.op('act', lambda e, ch=ch, cb=cb: e.activation(out=ac[:, ch * 128:(ch + 1) * 128], in_=cb[:], func=AF.Silu), r=[cn], w=[an])
                yield
            for hf in range(2):
                seg = ac[:, hf * 1024:(hf + 1) * 1024]
                s.op('dve', lambda e, seg=seg: e.tensor_tensor(out=sq[:], in0=seg, in1=seg, op=ALU.mult), r=[an], w=['gsq'])
                yield
                for j in range(2):
                    b = j % 2
                    s.op('pe', lambda e, j=j, b=b: e.matmul(P[:, b * 512:(b + 1) * 512], lhsT=g.ones[:], rhs=sq[:, j * 512:(j + 1) * 512], start=True, stop=True),
                         r=['ones', 'gsq'], w=['ps%d' % b])
                for j in range(2):
                    b = j % 2
                    s.op('dve', lambda e, j=j, b=b: e.tensor_scalar(out=sq[:, j * 512:(j + 1) * 512], in0=P[:, b * 512:(b + 1) * 512], scalar1=RMS_EPS, scalar2=None, op0=ALU.add),
                         r=['ps%d' % b], w=['gsq'])
                yield
                s.op('act', lambda e: e.activation(out=sq[:], in_=sq[:], func=AF.Sqrt), r=['gsq'], w=['gsq'])
                s.op('dve', lambda e: e.reciprocal(out=sq[:], in_=sq[:]), r=['gsq'], w=['gsq'])
                yield
                s.op('dve', lambda e, seg=seg: e.tensor_tensor(out=seg, in0=seg, in1=sq[:], op=ALU.mult), r=[an, 'gsq'], w=[an])
                s.op('act', lambda e, seg=seg, hf=hf: e.activation(out=qkb[bp][:, hf * 1024:(hf + 1) * 1024], in_=seg, func=AF.Copy), r=[an], w=['qkb%d' % bp])
                yield
            for half in range(2):
                b = half
                for kc in range(8):
                    s.op('pe', lambda e, kc=kc, half=half, b=b: e.matmul(P[:, b * 512:(b + 1) * 512], lhsT=hT[:, kc, :],
                                                                        rhs=win[:, kc, 3072 + half * 512:3072 + (half + 1) * 512],
                                                                        start=(kc == 0), stop=(kc == 7)), r=['gwin', 'ghT'], w=['ps%d' % b], inc=(kc == 7))
                s.op('act', lambda e, half=half, b=b: e.activation(out=gsil[x3][:, half * 512:(half + 1) * 512], in_=P[:, b * 512:(b + 1) * 512], func=AF.Silu),
                     r=['ps%d' % b], w=['gsil%d' % x3])
                yield
            for kc in range(8):
                s.op('pe', lambda e, kc=kc: e.matmul(P[:, 0:16], lhsT=hT[:, kc, :], rhs=wbb[:, kc, :], start=(kc == 0), stop=(kc == 7)),
                     r=['gwbb', 'ghT'], w=['ps0'], inc=(kc == 7))
            s.op('dve', lambda e: e.tensor_copy(out=ba[:], in_=P[:, 0:16]), r=['ps0'], w=['ba'])
            yield
            s.op('act', lambda e: e.activation(out=sm[:, 0:8], in_=ba[:, 0:8], func=AF.Exp, scale=-1.0), r=['ba'], w=[sn])
            s.op('dve', lambda e: e.tensor_scalar(out=sm[:, 0:8], in0=sm[:, 0:8], scalar1=1.0, scalar2=None, op0=ALU.add), r=[sn], w=[sn])
            s.op('dve', lambda e: e.reciprocal(out=sm[:, 0:8], in_=sm[:, 0:8]), r=[sn], w=[sn])
            s.op('dve', lambda e: e.tensor_scalar(out=sm[:, 32:40], in0=sm[:, 0:8], scalar1=-1.0, scalar2=None, op0=ALU.mult), r=[sn], w=[sn])
            yield
            s.op('dve', lambda e: e.tensor_tensor(out=tmpa[:], in0=ba[:, 8:16], in1=dtb[:], op=ALU.add), r=['ba', 'dtb'], w=['tmpa'])
            s.op('act', lambda e: e.activation(out=tmpa[:], in_=tmpa[:], func=AF.Exp), r=['tmpa'], w=['tmpa'])
            s.op('act', lambda e: e.activation(out=tmpa[:], in_=tmpa[:], func=AF.Ln, bias=1.0), r=['tmpa'], w=['tmpa'])
            s.op('dve', lambda e: e.tensor_tensor(out=sm[:, 8:16], in0=tmpa[:], in1=alog[:], op=ALU.mult), r=['tmpa', 'alog', sn], w=[sn])
            yield
            s.op('pe', lambda e: e.matmul(P[:, 16:24], lhsT=triu[:], rhs=sm[:, 8:16], start=True, stop=True), r=['triu', sn], w=['ps0'])
            s.op('dve', lambda e: e.tensor_copy(out=sm[:, 16:24], in_=P[:, 16:24]), r=['ps0', sn], w=[sn])
            s.op('act', lambda e: e.activation(out=sm[:, 24:32], in_=sm[:, 16:24], func=AF.Exp), r=[sn], w=[sn])
            s.op('dve', lambda e: e.tensor_tensor(out=sm[:, 40:48], in0=sm[:, 24:32], in1=sm[:, 0:8], op=ALU.mult), r=[sn], w=[sn])
            yield

        def heads(qb, p):
            bp = qb % 2
            sm = smb[bp]
            sn = 'smb%d' % bp
            an = 'gact%d' % bp
            ac = act[bp]
            H = HP[p]
            X0 = (2 + 3 * p) * 512
            Y0 = X0 + 512
            Z0 = X0 + 1024
            xn_, yn_, zn_ = 'ps%d' % (2 + 3 * p), 'ps%d' % (3 + 3 * p), 'ps%d' % (4 + 3 * p)
            n = lambda nm: 'h%s%d' % (nm, p)
            for h in range(p, 8, 2):
                qn = ac[:, h * 128:(h + 1) * 128]
                kn = ac[:, (8 + h) * 128:(9 + h) * 128]
                vv = ac[:, (16 + h) * 128:(17 + h) * 128]
                qnb = qkb[bp][:, h * 128:(h + 1) * 128]
                knb = qkb[bp][:, (8 + h) * 128:(9 + h) * 128]
                qbn = 'qkb%d' % bp
                s.op('dve', lambda e, h=h: e.tensor_scalar(out=H.t1[:], in0=g.ident[:], scalar1=sm[:, 16 + h:17 + h], scalar2=None, op0=ALU.mult),
                     r=['ident', sn], w=[n('t1')])
                s.op('pe', lambda e: e.matmul(P[:, X0:X0 + 128], lhsT=g.ones[:], rhs=H.t1[:], start=True, stop=True), r=['ones', n('t1')], w=[xn_])
                s.op('dve', lambda e, h=h: e.tensor_scalar(out=H.E[:], in0=P[:, X0:X0 + 128], scalar1=sm[:, 16 + h:17 + h], scalar2=0.0, op0=ALU.subtract, op1=ALU.max),
                     r=[xn_, sn], w=[n('E')])
                s.op('act', lambda e: e.activation(out=H.E[:], in_=H.E[:], func=AF.Exp, scale=-1.0), r=[n('E')], w=[n('E')])
                s.op('act', lambda e: e.activation(out=H.EB[:], in_=P[:, X0:X0 + 128], func=AF.Exp), r=[xn_], w=[n('EB')])
                s.op('act', lambda e: e.activation(out=H.sm[:, 0:1], in_=P[:, X0 + 127:X0 + 128], func=AF.Exp), r=[xn_], w=[n('sm0')])
                s.op('dve', lambda e, h=h: e.tensor_scalar(out=H.sm[:, 1:2], in0=P[:, X0 + 127:X0 + 128], scalar1=sm[:, 16 + h:17 + h], scalar2=None, op0=ALU.subtract),
                     r=[xn_, sn], w=[n('sm1')])
                s.op('act', lambda e: e.activation(out=H.sm[:, 1:2], in_=H.sm[:, 1:2], func=AF.Exp), r=[n('sm1')], w=[n('sm1')])
                yield
                s.op('pe', lambda e, knb=knb: e.matmul(P[:, X0 + 128:X0 + 256], lhsT=knb, rhs=knb, start=True, stop=True), r=[qbn], w=[xn_])
                s.op('pe', lambda e, knb=knb, qnb=qnb: e.matmul(P[:, X0 + 256:X0 + 384], lhsT=qnb, rhs=knb, start=True, stop=True), r=[qbn], w=[xn_])
                s.op('dve', lambda e: e.tensor_tensor(out=H.t1[:], in0=H.E[:], in1=msl[:], op=ALU.mult), r=[n('E'), 'msl'], w=[n('t1')])
                s.op('dve', lambda e, h=h: e.scalar_tensor_tensor(out=H.Nf[:], in0=P[:, X0 + 128:X0 + 256], scalar=sm[:, 32 + h:33 + h], in1=H.t1[:], op0=ALU.mult, op1=ALU.mult),
                     r=[xn_, sn, n('t1')], w=[n('Nf')])
                s.op('dve', lambda e: e.tensor_tensor(out=H.t1[:], in0=H.E[:], in1=mil[:], op=ALU.mult), r=[n('E'), 'mil', n('Nf')], w=[n('t1')])
                s.op('dve', lambda e: e.scalar_tensor_tensor(out=H.atf[:], in0=P[:, X0 + 256:X0 + 384], scalar=DK, in1=H.t1[:], op0=ALU.mult, op1=ALU.mult),
                     r=[xn_, n('t1')], w=[n('atf')])
                yield
                s.op('pe', lambda e: e.transpose(P[:, Y0:Y0 + 128], H.Nf[:], g.ident[:]), r=[n('Nf'), 'ident'], w=[yn_])
                s.op('pe', lambda e: e.transpose(P[:, Y0 + 128:Y0 + 256], H.atf[:], g.ident[:]), r=[n('atf'), 'ident'], w=[yn_])
                s.op('pe', lambda e, kn=kn: e.transpose(P[:, Y0 + 256:Y0 + 384], kn, g.ident[:]), r=[an, 'ident'], w=[yn_])
                s.op('pe', lambda e, vv=vv: e.transpose(P[:, Y0 + 384:Y0 + 512], vv, g.ident[:]), r=[an, 'ident'], w=[yn_])
                s.op('dve', lambda e: e.tensor_tensor(out=H.NM[:], in0=H.Nf[:].unsqueeze(1).to_broadcast([128, 5, 128]), in1=mdm[:], op=ALU.mult),
                     r=[n('Nf'), 'mdm'], w=[n('NM')])
                s.op('dve', lambda e: e.tensor_tensor(out=H.NMt[:], in0=P[:, Y0:Y0 + 128].unsqueeze(1).to_broadcast([128, 5, 128]), in1=mdmT[:], op=ALU.mult),
                     r=[yn_, 'mdmT'], w=[n('NMt')])
                s.op('act', lambda e: e.activation(out=H.attT[:], in_=P[:, Y0 + 128:Y0 + 256], func=AF.Copy), r=[yn_], w=[n('attT')])
                s.op('dve', lambda e: e.tensor_scalar(out=H.kd[:], in0=P[:, Y0 + 256:Y0 + 384], scalar1=H.sm[:, 1:2], scalar2=None, op0=ALU.mult), r=[yn_, n('sm1')], w=[n('kd')])
                s.op('dve', lambda e, h=h: e.tensor_scalar(out=H.Kbg[:], in0=P[:, Y0 + 256:Y0 + 384], scalar1=sm[:, 40 + h:41 + h], scalar2=None, op0=ALU.mult),
                     r=[yn_, sn], w=[n('Kbg')])
                s.op('dve', lambda e, h=h: e.tensor_scalar(out=H.Vb[:], in0=P[:, Y0 + 384:Y0 + 512], scalar1=sm[:, h:h + 1], scalar2=None, op0=ALU.mult), r=[yn_, sn], w=[n('Vb')])
                s.op('dve', lambda e, qn=qn: e.scalar_tensor_tensor(out=H.qdT[:], in0=qn, scalar=DK, in1=H.EB[:], op0=ALU.mult, op1=ALU.mult),
                     r=[an, n('EB')], w=[n('qdT')])
                yield
                zc = [0]

                def mm(lhsT, rhs, rn):
                    c0 = Z0 + (zc[0] % 4) * 128
                    zc[0] += 1
                    s.op('pe', lambda e: e.matmul(P[:, c0:c0 + 128], lhsT=lhsT, rhs=rhs, start=True, stop=True), r=rn, w=[zn_])
                    return P[:, c0:c0 + 128]

                def cp(dst, dn, src_ps):
                    s.op('act', lambda e: e.activation(out=dst, in_=src_ps, func=AF.Copy), r=[zn_], w=[dn])

                def acc(dst, dn, src_ps):
                    s.op('dve', lambda e: e.tensor_tensor(out=dst, in0=src_ps, in1=dst, op=ALU.add), r=[zn_, dn], w=[dn])

                M0, M0t = H.NM[:, 0, :], H.NMt[:, 0, :]
                s.op('dve', lambda e: e.tensor_tensor(out=H.X[:], in0=M0, in1=g.identb[:], op=ALU.add), r=[n('NM'), 'identb'], w=[n('X')])
                s.op('dve', lambda e: e.tensor_tensor(out=H.Xt[:], in0=M0t, in1=g.identb[:], op=ALU.add), r=[n('NMt'), 'identb'], w=[n('Xt')])
                cp(H.Q2[:], n('Q2'), mm(M0t, M0, [n('NM'), n('NMt')]))
                cp(H.Q2t[:], n('Q2t'), mm(M0, M0t, [n('NM'), n('NMt')]))
                yield
                acc(H.X[:], n('X'), mm(H.Q2t[:], H.X[:], [n('Q2t'), n('X')]))
                acc(H.Xt[:], n('Xt'), mm(H.Q2[:], H.Xt[:], [n('Q2'), n('Xt')]))
                cp(H.Q4[:], n('Q4'), mm(H.Q2t[:], H.Q2[:], [n('Q2'), n('Q2t')]))
                cp(H.Q4t[:], n('Q4t'), mm(H.Q2[:], H.Q2t[:], [n('Q2'), n('Q2t')]))
                yield
                acc(H.X[:], n('X'), mm(H.Q4t[:], H.X[:], [n('Q4t'), n('X')]))
                acc(H.Xt[:], n('Xt'), mm(H.Q4[:], H.Xt[:], [n('Q4'), n('Xt')]))
                yield
                for lv in range(1, 5):
                    Nb, Nbt = H.NM[:, lv, :], H.NMt[:, lv, :]
                    if lv < 4:
                        cp(H.Yv[:], n('Yv'), mm(Nbt, H.X[:], [n('NMt'), n('X')]))
                    cp(H.Yt[:], n('Yt'), mm(Nb, H.Xt[:], [n('NM'), n('Xt')]))
                    yield
                    pa = mm(H.Xt[:], H.Yv[:], [n('Xt'), n('Yv')]) if lv < 4 else None
                    pb = mm(H.X[:], H.Yt[:], [n('X'), n('Yt')])
                    if lv < 4:
                        acc(H.X[:], n('X'), pa)
                    acc(H.Xt[:], n('Xt'), pb)
                    yield
                pw = mm(H.Kbg[:], H.Xt[:], [n('Kbg'), n('Xt')])
                s.op('act', lambda e: e.activation(out=H.nwT[:], in_=pw, func=AF.Copy, scale=-1.0), r=[zn_], w=[n('nwT')])
                yield
                s.op('pe', lambda e: e.matmul(P[:, Y0:Y0 + 128], lhsT=H.Xt[:], rhs=H.Vb[:], start=True, stop=False), r=[n('Xt'), n('Vb')], w=[yn_], inc=False)
                s.op('pe', lambda e, h=h: e.matmul(P[:, Y0:Y0 + 128], lhsT=H.nwT[:], rhs=Sb[:, h, :], start=False, stop=True), r=[n('nwT'), 'gSb%d' % p], w=[yn_])
                s.op('act', lambda e: e.activation(out=H.vnew[:], in_=P[:, Y0:Y0 + 128], func=AF.Copy), r=[yn_], w=[n('vnew')])
                yield
                s.op('pe', lambda e, h=h: e.matmul(P[:, X0 + 384:X0 + 512], lhsT=H.qdT[:], rhs=Sb[:, h, :], start=True, stop=False),
                     r=[n('qdT'), 'gSb%d' % p], w=[xn_], inc=False)
                s.op('pe', lambda e: e.matmul(P[:, X0 + 384:X0 + 512], lhsT=H.attT[:], rhs=H.vnew[:], start=False, stop=True),
                     r=[n('attT'), n('vnew')], w=[xn_])
                s.op('act', lambda e, h=h: e.activation(out=osb[bp][:, h * 128:(h + 1) * 128], in_=P[:, X0 + 384:X0 + 512], func=AF.Copy),
                     r=[xn_], w=['gosb%d_%d' % (bp, p)])
                s.op('pe', lambda e: e.matmul(P[:, Y0 + 128:Y0 + 256], lhsT=H.kd[:], rhs=H.vnew[:], start=True, stop=True), r=[n('kd'), n('vnew')], w=[yn_])
                s.op('dve', lambda e, h=h: e.scalar_tensor_tensor(out=Sf[:, h, :], in0=Sf[:, h, :], scalar=H.sm[:, 0:1], in1=P[:, Y0 + 128:Y0 + 256], op0=ALU.mult, op1=ALU.add),
                     r=['gSf%d' % p, n('sm0'), yn_], w=['gSf%d' % p])
                s.op('act', lambda e, h=h: e.activation(out=Sb[:, h, :], in_=Sf[:, h, :], func=AF.Copy), r=['gSf%d' % p], w=['gSb%d' % p])
                yield

        def phaseZ(qb):
            t0 = qb * 128
            bp = qb % 2
            x3 = qb % 3
            ob = osb[bp]
            on = ['gosb%d_0' % bp, 'gosb%d_1' % bp]
            s.op('dve', lambda e: e.tensor_tensor(out=zb[:], in0=ob[:], in1=ob[:], op=ALU.mult), r=on, w=['zb'])
            for h in range(8):
                s.op('dve', lambda e, h=h: e.reduce_sum(out=zsm[:, h:h + 1], in_=zb[:, h * 128:(h + 1) * 128], axis=AX.X), r=['zb'], w=['zsm'])
            yield
            s.op('dve', lambda e: e.tensor_scalar(out=zsm[:], in0=zsm[:], scalar1=1.0 / 128, scalar2=RMS_EPS, op0=ALU.mult, op1=ALU.add), r=['zsm'], w=['zsm'])
            s.op('act', lambda e: e.activation(out=zsm[:], in_=zsm[:], func=AF.Sqrt), r=['zsm'], w=['zsm'])
            s.op('dve', lambda e: e.reciprocal(out=zsm[:], in_=zsm[:]), r=['zsm'], w=['zsm'])
            yield
            for h in range(8):
                s.op('dve', lambda e, h=h: e.tensor_scalar(out=ob[:, h * 128:(h + 1) * 128], in0=ob[:, h * 128:(h + 1) * 128], scalar1=zsm[:, h:h + 1],
                                                           scalar2=None, op0=ALU.mult), r=on + ['zsm'], w=on)
            yield
            s.op('dve', lambda e: e.tensor_tensor(out=ob[:], in0=ob[:], in1=onb[:], op=ALU.mult), r=on + ['onb'], w=on)
            s.op('dve', lambda e: e.tensor_tensor(out=ob[:], in0=ob[:], in1=gsil[x3][:], op=ALU.mult), r=on + ['gsil%d' % x3], w=on)
            yield
            for kc in range(8):
                b = kc // 4
                off = b * 512 + (kc % 4) * 128
                s.op('pe', lambda e, kc=kc, off=off: e.transpose(P[:, off:off + 128], ob[:, kc * 128:(kc + 1) * 128], g.ident[:]), r=on + ['ident'], w=['ps%d' % b])
            for j in range(2):
                s.op('act', lambda e, j=j: e.activation(out=yinT[:, j * 512:(j + 1) * 512], in_=P[:, j * 512:(j + 1) * 512], func=AF.Copy),
                     r=['ps%d' % j], w=['gyinT'])
            yield
            for half in range(2):
                b = half
                for kc in range(8):
                    s.op('pe', lambda e, kc=kc, half=half, b=b: e.matmul(P[:, b * 512:(b + 1) * 512], lhsT=yinT[:, kc * 128:(kc + 1) * 128],
                                                                        rhs=wout[:, kc, half * 512:(half + 1) * 512], start=(kc == 0), stop=(kc == 7)),
                         r=['gyinT', 'gwout'], w=['ps%d' % b], inc=(kc == 7))
            s.dma('sp', xrs[:], src[t0:t0 + 128, :], w=['gxrs'])
            emit_epilogue(g, L, (0, 1), xrs, 'gxrs', zb, xo[0], 'gxo0', dst[t0:t0 + 128, :])
            yield

        nblk = g.nblk_gdn
        for it in range(nblk + 2):
            gens = []
            if 1 <= it <= nblk:
                gens.append([heads(it - 1, 0), 1])
                gens.append([heads(it - 1, 1), 1])
            if it < nblk:
                gens.append([phaseA(it), 1])
            if it >= 2:
                gens.append([phaseZ(it - 2), 1])
            _interleave(gens)
        s.barrier()


W_SPECS = [
    ("e_mod_w", [1, D, 3 * D]), ("e_mod_b", [1, 1, 3 * D]), ("e_ln_g", [1, D]), ("e_ln_b", [1, D]),
    ("o_mod_w", [1, D, 3 * D]), ("o_mod_b", [1, 1, 3 * D]), ("o_ln_g", [1, D]), ("o_ln_b", [1, D]),
    ("f_mod_w", [2, D, 3 * D]), ("f_mod_b", [2, 1, 3 * D]), ("f_ln_g", [2, D]), ("f_ln_b", [2, D]),
    ("f_w_up", [2, D, 2 * DFF]), ("f_w_down", [2, DFF, D]), ("f_cw", [2, 128, 2 * NFC, 4]),
    ("ident", [128, 128]),
    ("e_w_in", [1, D, 1476]), ("e_pool_w", [1, 4, 128, 128]), ("pscale", [128, 4]), ("e_kv_norm", [1, 128]),
    ("o_w_in", [1, D, 4112]), ("o_w_out", [1, D, D]), ("o_cw", [128, 24, 4]), ("msl", [128, 128]), ("mil", [128, 128]), ("triu", [128, 128]), ("mdm", [128, 5, 128]), ("mdmT", [128, 5, 128]),
    ("o_a_log", [1, 8]), ("o_dt_bias", [1, 8]), ("onb", [1, D]),
    ("ukT", [128, 4, 128]), ("uvpad", [128, 8, 128]), ("e_w_out", [1, D, D]), ("negI4", [128, 512]), ("corr", [128, 4, 15]),
]


def build(stages):
    nc = bass.Bass("TRN2", target_bir_lowering=False)
    g = Ctx()
    g.nc = nc
    g.d_x = nc.dram_tensor("x", [S, D], F32, kind="ExternalInput").ap()
    g.d_ccol = nc.dram_tensor("ccol", [128, 8], F32, kind="ExternalInput").ap()
    for nm, shp in W_SPECS:
        setattr(g, "d_" + nm, nc.dram_tensor(nm, shp, F32, kind="ExternalInput").ap())
    g.d_out = nc.dram_tensor("out", [S, D], F32, kind="ExternalOutput").ap()
    scr = [nc.dram_tensor("xscr%d" % i, [S, D], F32, kind="Internal").ap() for i in range(3)]
    with ExitStack() as es:
        g.es = es
        g.s = Sch(nc, es)
        g.ps = es.enter_context(nc.psum_tensor("ps", [128, 8 * 512], F32))
        g.lnst = _sb(nc, es, "lnst", [128, 12], F32)
        g.lnmv = _sb(nc, es, "lnmv", [128, 8], F32)
        emit_consts(g)
        bufs = [g.d_x] + scr
        n = len(stages)
        for i, st in enumerate(stages):
            src = g.d_x if i == 0 else scr[(i - 1) % 3]
            dst = g.d_out if i == n - 1 else scr[i % 3]
            if st[0] == 'ffn':
                emit_ffn(g, st[1], src, dst)
            elif st[0] == 'gdn':
                g.nblk_gdn = st[1] if len(st) > 1 else NB
                emit_gdn(g, src, dst)
            elif st[0] == 'dsa':
                g.nblk_dsa = st[1] if len(st) > 1 else NB
                emit_dsa(g, src, dst)
            else:
                raise ValueError(st)
        g.s.finish()
    return nc


def prep_weights(inp):
    f = lambda a: np.ascontiguousarray(np.asarray(a, dtype=np.float32))
    w = {}
    for k in ("e_mod_w", "e_ln_g", "e_ln_b", "o_mod_w", "o_ln_g", "o_ln_b", "f_mod_w", "f_ln_g", "f_ln_b", "f_w_up", "f_w_down"):
        w[k] = f(inp[k])
    for k in ("e_mod_b", "o_mod_b", "f_mod_b"):
        a = f(inp[k])
        w[k] = np.ascontiguousarray(a.reshape(a.shape[0], 1, 3 * D))
    cwt = f(inp["f_conv_w"])
    cb = f(inp["f_conv_b"])
    a = np.concatenate([cwt, cb[:, None, :]], axis=1)
    a = a.reshape(2, 4, 2 * NFC, 128).transpose(0, 3, 2, 1)
    w["f_cw"] = np.ascontiguousarray(a)
    w["ident"] = np.eye(128, dtype=np.float32)
    for k in ("e_w_in", "e_pool_w", "e_kv_norm", "e_w_out"):
        w[k] = f(inp[k])
    w["pscale"] = np.ascontiguousarray(f(inp["e_pool_scale"])[0].reshape(4, 128).T)
    uk = f(inp["e_w_uk"])[0]
    w["ukT"] = np.ascontiguousarray(uk.reshape(4, 2, 128, 64).transpose(1, 3, 0, 2).reshape(128, 4, 128))
    uv = f(inp["e_w_uv"])[0]
    uvp = np.zeros((128, 8, 128), np.float32)
    for h in range(8):
        uvp[:, h, (h % 2) * 64:(h % 2) * 64 + 64] = uv[h]
    w["uvpad"] = uvp
    w["negI4"] = np.ascontiguousarray(np.tile(-30000.0 * np.eye(128, dtype=np.float32), (1, 4)))
    corr = np.ones((128, 4, 15), np.float32)
    for gi in range(4):
        win_ = 2 << gi
        for t in range(win_ - 1):
            corr[:, gi, t] = win_ / (t + 1.0)
    w["corr"] = corr
    for k in ("o_w_in", "o_w_out", "o_a_log", "o_dt_bias"):
        w[k] = f(inp[k])
    ocw = f(inp["o_conv_w"])[0]
    w["o_cw"] = np.ascontiguousarray(ocw.reshape(4, 24, 128).transpose(2, 1, 0))
    ar = np.arange(128)
    w["msl"] = (ar[:, None] > ar[None, :]).astype(np.float32)
    w["mil"] = (ar[:, None] >= ar[None, :]).astype(np.float32)
    w["triu"] = (ar[:, None] <= ar[None, :]).astype(np.float32)
    w["onb"] = np.ascontiguousarray(np.tile(f(inp["o_out_norm"])[0], 8)[None, :])
    mdm = np.zeros((128, 5, 128), np.float32)
    mdm[:, 0, :] = (ar[:, None] // 8 == ar[None, :] // 8)
    for li, bsz in enumerate((8, 16, 32, 64)):
        bl = ar // bsz
        mdm[:, 1 + li, :] = (bl[:, None] % 2 == 1) & (bl[None, :] == bl[:, None] - 1)
    w["mdm"] = mdm
    w["mdmT"] = np.ascontiguousarray(mdm.transpose(2, 1, 0))
    return w


STAGES = [('dsa',), ('ffn', 0), ('gdn',), ('ffn', 1)]


def kernel(**inp):
    x = np.asarray(inp["x"], dtype=np.float32)
    c = np.asarray(inp["c"], dtype=np.float32)
    w = prep_weights(inp)
    nc = build(STAGES)
    in_maps = []
    for b in range(8):
        m = dict(w)
        m["x"] = np.ascontiguousarray(x[b])
        m["ccol"] = np.ascontiguousarray(c[b].reshape(8, 128).T)
        in_maps.append(m)
    res = run_bass_kernel_spmd(nc, in_maps, core_ids=list(range(8)))
    return np.stack([np.asarray(r["out"], dtype=np.float32) for r in res.results], axis=0)
```

```python
import os
import numpy as np
from contextlib import ExitStack
import concourse.bass as bass
import concourse.mybir as mybir
from concourse.bass_utils import run_bass_kernel_spmd

F32 = mybir.dt.float32
BF16 = mybir.dt.bfloat16
AF = mybir.ActivationFunctionType
ALU = mybir.AluOpType
AX = mybir.AxisListType

D = 1024
S = 4096
NB = S // 128
DFF = 2688
NFC = DFF // 128
ALPHA = float(4 ** 0.25)
LN_EPS = 1e-5
RMS_EPS = 1e-6
NDS = 12


class Sch:
    def __init__(self, nc, es):
        self.nc = nc
        self.E = {'pe': nc.tensor, 'act': nc.scalar, 'dve': nc.vector,
                  'pool': nc.gpsimd, 'sp': nc.sync}
        self.sem = {}
        for e in self.E:
            self.sem[e] = es.enter_context(nc.semaphore('s_' + e))
        self.cnt = {e: 0 for e in self.E}
        self.waited = {e: {} for e in self.E}
        self.lastw = {}
        self.readers = {}
        self.dq = ('sp', 'pool', 'act')
        self.dcnt = {}
        self.drr = {q: 0 for q in self.dq}
        for q in self.dq:
            for i in range(NDS):
                k = (q, i)
                self.sem[k] = es.enter_context(nc.semaphore('d_%s%d' % (q, i)))
                self.dcnt[k] = 0
        self.nwaits = 0

    def _wait(self, e, tok):
        key, val = tok
        if key == e and e == 'pe':
            return
        if self.waited[e].get(key, 0) >= val:
            return
        self.E[e].wait_ge(self.sem[key], val)
        self.waited[e][key] = val
        self.nwaits += 1

    def _collect(self, r, w, e=None):
        deps = {}

        def add(t):
            if t is None:
                return
            if deps.get(t[0], 0) < t[1]:
                deps[t[0]] = t[1]
        for x in r:
            for k, v in self.lastw.get(x, {}).items():
                add((k, v))
            if x.startswith('ps') and e is not None:
                for k, v in self.readers.get(x, {}).items():
                    if k != e:
                        add((k, v))
        for x in w:
            for k, v in self.lastw.get(x, {}).items():
                add((k, v))
            for k, v in self.readers.get(x, {}).items():
                add((k, v))
        return list(deps.items())

    def _record(self, tok, r, w):
        for x in w:
            self.lastw.setdefault(x, {})[tok[0]] = tok[1]
            self.readers[x] = {}
        for x in r:
            d = self.readers.setdefault(x, {})
            if d.get(tok[0], 0) < tok[1]:
                d[tok[0]] = tok[1]

    def op(self, e, fn, r=(), w=(), inc=True):
        for t in self._collect(r, w, e):
            self._wait(e, t)
        ins = fn(self.E[e])
        if inc:
            self.cnt[e] += 1
            ins.then_inc(self.sem[e], 1)
            tok = (e, self.cnt[e])
        else:
            tok = (e, self.cnt[e] + 1)
        self._record(tok, r, w)
        return ins

    def dma(self, q, out, in_, r=(), w=()):
        i = self.drr[q]
        self.drr[q] = (i + 1) % NDS
        k = (q, i)
        if self.dcnt[k] > 0:
            self._wait(q, (k, self.dcnt[k]))
        for t in self._collect(r, w):
            self._wait(q, t)
        ins = self.E[q].dma_start(out=out, in_=in_)
        self.dcnt[k] += 16
        ins.then_inc(self.sem[k], 16)
        tok = (k, self.dcnt[k])
        self._record(tok, r, w)
        return ins

    def barrier(self):
        toks = [(e, self.cnt[e]) for e in self.E if self.cnt[e] > 0]
        toks += [(k, v) for k, v in self.dcnt.items() if v > 0]
        for e in self.E:
            for t in toks:
                self._wait(e, t)
        self.lastw = {}
        self.readers = {}

    def finish(self):
        for k, v in self.dcnt.items():
            if v > 0:
                self._wait('sp', (k, v))


class Ctx:
    pass


def _interleave(gens):
    live = list(gens)
    while live:
        nxt = []
        for item in live:
            gen, n = item
            done = False
            for _ in range(n):
                try:
                    next(gen)
                except StopIteration:
                    done = True
                    break
            if not done:
                nxt.append(item)
        live = nxt


_UID = [0]


def _sb(nc, es, name, shape, dt):
    _UID[0] += 1
    return es.enter_context(nc.sbuf_tensor("sb%d_%s" % (_UID[0], name), list(shape), dt))


def emit_consts(g):
    nc, s, es = g.nc, g.s, g.es
    g.ident = _sb(nc, es, "ident", [128, 128], F32)
    g.identb = _sb(nc, es, "identb", [128, 128], BF16)
    g.ones = _sb(nc, es, "ones", [128, 128], F32)
    g.onesb = _sb(nc, es, "onesb", [128, 128], BF16)
    s.dma('sp', g.ident[:], g.d_ident[:, :], w=['ident'])
    s.dma('pool', g.identb[:], g.d_ident[:, :], w=['identb'])
    s.op('dve', lambda e: e.memset(g.ones[:], 1.0), w=['ones'])
    s.op('dve', lambda e: e.memset(g.onesb[:], 1.0), w=['onesb'])
    g.ccol = _sb(nc, es, "ccol", [128, 8], F32)
    g.sc = _sb(nc, es, "sc", [128, 8], F32)
    g.scb = _sb(nc, es, "scb", [128, 8, 128], F32)
    s.dma('sp', g.ccol[:], g.d_ccol[:, :], w=['ccol'])
    s.op('act', lambda e: e.activation(out=g.sc[:], in_=g.ccol[:], func=AF.Silu), r=['ccol'], w=['sc'])
    for kc in range(8):
        s.op('dve', lambda e, kc=kc: e.tensor_scalar(out=g.scb[:, kc, :], in0=g.ones[:], scalar1=g.sc[:, kc:kc + 1],
                                                     scalar2=None, op0=ALU.mult), r=['ones', 'sc'], w=['scb'])


def emit_mods(g, L, modw, modb_row, lng, lnb):
    nc, s, es = g.nc, g.s, g.es
    L.shift = _sb(nc, L.es, "shift", [128, 8], F32)
    L.scale1 = _sb(nc, L.es, "scale1", [128, 8], F32)
    L.gate_bc = _sb(nc, L.es, "gate_bc", [128, D], F32)
    L.lng_bc = _sb(nc, L.es, "lng_bc", [128, D], F32)
    L.lnb_bc = _sb(nc, L.es, "lnb_bc", [128, D], F32)
    s.dma('sp', L.lng_bc[:], lng.partition_broadcast(128), w=['lng_bc'])
    s.dma('sp', L.lnb_bc[:], lnb.partition_broadcast(128), w=['lnb_bc'])
    with ExitStack() as es2:
        mw = [_sb(nc, es2, "mw%d" % i, [128, 3 * D], F32) for i in range(2)]
        brow = _sb(nc, es2, "brow", [1, 3 * D], F32)
        bc = _sb(nc, es2, "modbc", [128, 2 * D], F32)
        one11 = _sb(nc, es2, "one11", [1, 1], F32)
        s.op('dve', lambda e: e.memset(one11[:], 1.0), w=['one11'])
        s.dma('sp', brow[:], modb_row[:, :], w=['brow'])
        P = g.ps
        for kc in range(8):
            t = mw[kc % 2]
            nm = 'mw%d' % (kc % 2)
            s.dma('sp' if kc % 2 == 0 else 'pool', t[:], modw[kc * 128:(kc + 1) * 128, :], w=[nm])
            for j in range(6):
                s.op('pe', lambda e, j=j, kc=kc, t=t: e.matmul(P[:, j * 512:(j + 1) * 512], lhsT=g.scb[:, kc, :],
                                                            rhs=t[:, j * 512:(j + 1) * 512], start=(kc == 0), stop=False),
                     r=[nm, 'scb'], w=['ps%d' % j], inc=(j == 5))
        for j in range(6):
            s.op('pe', lambda e, j=j: e.matmul(P[:, j * 512:(j + 1) * 512], lhsT=g.ones[0:1, :],
                                               rhs=brow[0:1, j * 512:(j + 1) * 512], start=False, stop=True),
                 r=['brow', 'ones'], w=['ps%d' % j])
        for j in range(4):
            s.op('act' if j % 2 else 'dve',
                 (lambda e, j=j: e.activation(out=bc[:, j * 512:(j + 1) * 512], in_=P[:, j * 512:(j + 1) * 512], func=AF.Copy))
                 if j % 2 else
                 (lambda e, j=j: e.tensor_copy(out=bc[:, j * 512:(j + 1) * 512], in_=P[:, j * 512:(j + 1) * 512])),
                 r=['ps%d' % j], w=['modbc%d' % j])
        for j in range(2):
            s.op('dve', lambda e, j=j: e.tensor_copy(out=L.gate_bc[:, j * 512:(j + 1) * 512], in_=P[:, (4 + j) * 512:(5 + j) * 512]),
                 r=['ps%d' % (4 + j)], w=['gate_bc'])
        for j in range(16):
            s.op('pe', lambda e, j=j: e.matmul(P[:, 6 * 512 + j:6 * 512 + j + 1], lhsT=bc[0:1, j * 128:(j + 1) * 128],
                                               rhs=one11[0:1, 0:1], start=True, stop=True),
                 r=['modbc%d' % (j // 4), 'one11'], w=['ps6'])
        s.op('dve', lambda e: e.tensor_copy(out=L.shift[:], in_=P[:, 6 * 512:6 * 512 + 8]), r=['ps6'], w=['shift'])
        s.op('dve', lambda e: e.tensor_scalar(out=L.scale1[:], in0=P[:, 6 * 512 + 8:6 * 512 + 16], scalar1=1.0, scalar2=None,
                                              op0=ALU.add), r=['ps6'], w=['scale1'])
        s.barrier()


def emit_hT(g, L, xin, xin_nm, hT_ap_fn, hT_nm, pbanks):
    s = g.s
    P = g.ps
    for kc in range(8):
        b = pbanks[kc // 4]
        off = b * 512 + (kc % 4) * 128
        s.op('pe', lambda e, kc=kc, off=off: e.transpose(P[:, off:off + 128], xin[:, kc * 128:(kc + 1) * 128], g.ident[:]),
             r=[xin_nm, 'ident'], w=['ps%d' % b])
    for kc in range(8):
        b = pbanks[kc // 4]
        off = b * 512 + (kc % 4) * 128
        s.op('act', lambda e, kc=kc, off=off: e.activation(out=hT_ap_fn(kc), in_=P[:, off:off + 128], func=AF.Identity,
                                                           scale=L.scale1[:, kc:kc + 1], bias=L.shift[:, kc:kc + 1]),
             r=['ps%d' % b, 'scale1', 'shift'], w=[hT_nm])


def emit_epilogue(g, L, ybanks, xres, xres_nm, zb, xo, xo_nm, dst_rows):
    s = g.s
    P = g.ps
    for j in range(2):
        b = ybanks[j]
        s.op('dve', lambda e, j=j, b=b: e.tensor_tensor(out=zb[:, j * 512:(j + 1) * 512], in0=P[:, b * 512:(b + 1) * 512],
                                                        in1=L.gate_bc[:, j * 512:(j + 1) * 512], op=ALU.mult),
             r=['ps%d' % b, 'gate_bc'], w=['zb'])
    s.op('dve', lambda e: e.scalar_tensor_tensor(out=zb[:], in0=xres[:], scalar=ALPHA, in1=zb[:], op0=ALU.mult, op1=ALU.add),
         r=[xres_nm, 'zb'], w=['zb'])
    st = g.lnst
    for j in range(2):
        s.op('dve', lambda e, j=j: e.bn_stats(out=st[:, j * 6:(j + 1) * 6], in_=zb[:, j * 512:(j + 1) * 512]), r=['zb'], w=['lnst'])
    s.op('dve', lambda e: e.bn_aggr(out=g.lnmv[:, 0:2], in_=st[:, 0:12]), r=['lnst'], w=['lnmv'])
    s.op('dve', lambda e: e.tensor_scalar(out=g.lnmv[:, 2:3], in0=g.lnmv[:, 1:2], scalar1=LN_EPS, scalar2=None, op0=ALU.add),
         r=['lnmv'], w=['lnmv2'])
    s.op('act', lambda e: e.activation(out=g.lnmv[:, 3:4], in_=g.lnmv[:, 2:3], func=AF.Sqrt), r=['lnmv2'], w=['lnmv3'])
    s.op('dve', lambda e: e.reciprocal(out=g.lnmv[:, 4:5], in_=g.lnmv[:, 3:4]), r=['lnmv3'], w=['lnmv4'])
    s.op('dve', lambda e: e.tensor_scalar(out=zb[:], in0=zb[:], scalar1=g.lnmv[:, 0:1], scalar2=g.lnmv[:, 4:5],
                                          op0=ALU.subtract, op1=ALU.mult), r=['zb', 'lnmv', 'lnmv4'], w=['zb'])
    s.op('pool', lambda e: e.tensor_tensor(out=zb[:], in0=zb[:], in1=L.lng_bc[:], op=ALU.mult), r=['zb', 'lng_bc'], w=['zb'])
    s.op('pool', lambda e: e.tensor_tensor(out=xo[:], in0=zb[:], in1=L.lnb_bc[:], op=ALU.add), r=['zb', 'lnb_bc'], w=[xo_nm])
    s.dma('sp', dst_rows, xo[:], r=[xo_nm], w=[])


def emit_ffn(g, li, src, dst):
    nc, s = g.nc, g.s
    TT = 256
    NT = S // TT
    NBT = TT // 128
    with ExitStack() as les:
        L = Ctx()
        L.es = les
        emit_mods(g, L, g.d_f_mod_w[li], g.d_f_mod_b[li], g.d_f_ln_g[li], g.d_f_ln_b[li])
        wup = _sb(nc, les, "wup", [128, 8, 2 * DFF], BF16)
        wdn = _sb(nc, les, "wdn", [128, NFC, D], BF16)
        cw = _sb(nc, les, "cw", [128, 2 * NFC, 4], F32)
        halo = _sb(nc, les, "halo", [128, 2 * NFC, 2], F32)
        hT = [_sb(nc, les, "hT%d" % i, [128, 8, TT], BF16) for i in range(2)]
        gTs = [_sb(nc, les, "gT%d" % i, [128, NFC, TT], BF16) for i in range(2)]
        upre = [_sb(nc, les, "upre%d" % i, [128, TT + 2], F32) for i in range(2)]
        c0 = [_sb(nc, les, "c0%d" % i, [128, TT], F32) for i in range(2)]
        asil = _sb(nc, les, "asil", [128, TT], F32)
        xin = [_sb(nc, les, "xin%d" % i, [128, D], F32) for i in range(2)]
        xrs = [_sb(nc, les, "xrs%d" % i, [128, D], F32) for i in range(2)]
        zb = _sb(nc, les, "zb", [128, D], F32)
        xo = [_sb(nc, les, "xo%d" % i, [128, D], F32) for i in range(2)]
        P = g.ps
        wupd = g.d_f_w_up[li].rearrange("(kc p) n -> p kc n", p=128)
        for kc in range(8):
            for hf in range(2):
                s.dma('pool', wup[:, kc, hf * DFF:(hf + 1) * DFF], wupd[:, kc, hf * DFF:(hf + 1) * DFF], w=['wup'])
        wdnd = g.d_f_w_down[li].rearrange("(fc p) n -> p fc n", p=128)
        for fc in range(NFC):
            s.dma('pool', wdn[:, fc, :], wdnd[:, fc, :], w=['wdn'])
        s.dma('sp', cw[:], g.d_f_cw[li], w=['cw'])
        s.op('dve', lambda e: e.memset(halo[:], 0.0), w=['halo'])
        def phA(t):
            t0 = t * TT
            hs = t % 2
            hnm = 'hT%d' % hs
            for bi in range(NBT):
                xs_ = (t * NBT + bi) % 2
                s.dma('sp', xin[xs_][:], src[t0 + bi * 128:t0 + (bi + 1) * 128, :], w=['xin%d' % xs_])
                emit_hT(g, L, xin[xs_], 'xin%d' % xs_, lambda kc, bi=bi, hs=hs: hT[hs][:, kc, bi * 128:(bi + 1) * 128], hnm, (0, 1))
                yield

        def phU(t):
            hs = t % 2
            hnm = 'hT%d' % hs
            gT = gTs[t % 2]
            gnm = 'gT%d' % (t % 2)
            for j in range(NFC):
                for half in range(2):
                    fc = j + half * NFC
                    b = 2 + ((2 * j + half) % 4)
                    pb = 'ps%d' % b
                    up = upre[half]
                    unm = 'upre%d' % half
                    for kc in range(8):
                        s.op('pe', lambda e, kc=kc, fc=fc, b=b: e.matmul(P[:, b * 512:b * 512 + TT], lhsT=wup[:, kc, fc * 128:(fc + 1) * 128],
                                                                      rhs=hT[hs][:, kc, :], start=(kc == 0), stop=(kc == 7)),
                             r=['wup', hnm], w=[pb], inc=(kc == 7))
                    s.op('pool', lambda e, fc=fc, up=up: e.tensor_copy(out=up[:, 0:2], in_=halo[:, fc, :]), r=['halo'], w=[unm])
                    s.op('act', lambda e, b=b, up=up: e.activation(out=up[:, 2:TT + 2], in_=P[:, b * 512:b * 512 + TT], func=AF.Copy),
                         r=[pb], w=[unm])
                    s.op('act', lambda e, b=b, fc=fc, half=half: e.activation(out=c0[half][:], in_=P[:, b * 512:b * 512 + TT], func=AF.Identity,
                                                                             scale=cw[:, fc, 2:3], bias=cw[:, fc, 3:4]),
                         r=[pb, 'cw'], w=['c0%d' % half])
                    s.op('pool', lambda e, fc=fc, up=up: e.tensor_copy(out=halo[:, fc, :], in_=up[:, TT:TT + 2]), r=[unm], w=['halo'])
                    s.op('dve', lambda e, fc=fc, up=up, half=half: e.scalar_tensor_tensor(out=c0[half][:], in0=up[:, 1:TT + 1], scalar=cw[:, fc, 1:2],
                                                                                        in1=c0[half][:], op0=ALU.mult, op1=ALU.add),
                         r=[unm, 'cw', 'c0%d' % half], w=['c0%d' % half])
                    s.op('dve', lambda e, fc=fc, up=up, half=half: e.scalar_tensor_tensor(out=c0[half][:], in0=up[:, 0:TT], scalar=cw[:, fc, 0:1],
                                                                                        in1=c0[half][:], op0=ALU.mult, op1=ALU.add),
                         r=[unm, 'cw', 'c0%d' % half], w=['c0%d' % half])
                    if half == 0:
                        s.op('act', lambda e: e.activation(out=asil[:], in_=c0[0][:], func=AF.Silu), r=['c00'], w=['asil'])
                    else:
                        s.op('dve', lambda e, j=j, gT=gT: e.tensor_tensor(out=gT[:, j, :], in0=asil[:], in1=c0[1][:], op=ALU.mult),
                             r=['asil', 'c01'], w=[gnm])
                    yield

        def phD(t):
            t0 = t * TT
            gT = gTs[t % 2]
            gnm = 'gT%d' % (t % 2)
            for bi in range(NBT):
                r0 = t0 + bi * 128
                xs_ = (t * NBT + bi) % 2
                s.dma('sp', xrs[xs_][:], src[r0:r0 + 128, :], w=['xrs%d' % xs_])
                for half in range(2):
                    b = 6 + half
                    for j in range(NFC):
                        s.op('pe', lambda e, j=j, half=half, b=b, bi=bi, gT=gT: e.matmul(P[:, b * 512:(b + 1) * 512], lhsT=gT[:, j, bi * 128:(bi + 1) * 128],
                                                                                         rhs=wdn[:, j, half * 512:(half + 1) * 512],
                                                                                         start=(j == 0), stop=(j == NFC - 1)),
                             r=[gnm, 'wdn'], w=['ps%d' % b], inc=(j == NFC - 1))
                    yield
                emit_epilogue(g, L, (6, 7), xrs[xs_], 'xrs%d' % xs_, zb, xo[xs_], 'xo%d' % xs_, dst[r0:r0 + 128, :])
                yield

        for it in range(NT + 2):
            gens = []
            if 1 <= it <= NT:
                gens.append([phU(it - 1), 1])
            if it < NT:
                gens.append([phA(it), 1])
            if it >= 2:
                gens.append([phD(it - 2), 1])
            _interleave(gens)
        s.barrier()


NEG = -1.0e30
DSTOP = int(os.environ.get('DSA_STOP', '99'))
DSUB = int(os.environ.get('DSA_SUB', '99'))
GSTOP = int(os.environ.get('GDN_STOP', '99'))
DSC = int(os.environ.get('DSA_SC', '3'))
DSKIP = os.environ.get('DSA_SKIP', '').split(',')
REP = -3.0e38


def emit_dsa(g, src, dst):
    nc, s = g.nc, g.s
    P = g.ps
    with ExitStack() as les:
        L = Ctx()
        L.es = les
        emit_mods(g, L, g.d_e_mod_w[0], g.d_e_mod_b[0], g.d_e_ln_g[0], g.d_e_ln_b[0])
        A = lambda name, shape, dt: _sb(nc, les, name, shape, dt)
        win = A("win", [128, 8, 1536], BF16)
        wif = A("wif", [128, 8, 4], F32)
        wiw = A("wiw", [128, 8, 4], BF16)
        poolw = A("poolw", [128, 4, 128], BF16)
        pscale = A("pscale", [128, 4], F32)
        kvn_bc = A("kvn_bc", [128, 128], F32)
        ukT = A("ukT", [128, 4, 128], BF16)
        uvpad = A("uvpad", [128, 8, 128], BF16)
        wout = A("wout", [128, 8, D], BF16)
        negI4 = A("negI4", [128, 512], BF16)
        corr = A("corr", [128, 4, 15], F32)
        ckvn_all = A("ckvn_all", [128, NB, 128], BF16)
        ckvnT_all = A("ckvnT_all", [128, S], BF16)
        kiT_all = A("kiT_all", [128, S], BF16)
        xin = [A("xin%d" % i, [128, D], F32) for i in range(3)]
        hT = A("hT", [128, 8, 128], BF16)
        ut = A("ut", [128, 4, 143], F32)
        ta = A("ta", [128, 143], F32)
        tb = A("tb", [128, 143], F32)
        dT = A("dT", [128, 4, 128], BF16)
        qT = A("qT", [128, 512], BF16)
        qiT = [A("qiT%d" % i, [128, 256], BF16) for i in range(2)]
        wis = [A("wis%d" % i, [128, 4], F32) for i in range(2)]
        qlT = [A("qlT%d" % i, [128, 1024], BF16) for i in range(3)]
        W = [A("W%d" % i, [128, S], F32) for i in range(2)]
        rbuf = [A("rbuf%d" % i, [128, 512], F32) for i in range(2)]
        rb2 = A("rb2", [128, 512], F32)
        notm = [A("notm%d" % i, [128, S], BF16) for i in range(2)]
        m8 = A("m8", [128, 8], F32)
        pT = [A("pT%d" % i, [128, 512], BF16) for i in range(2)]
        rden = A("rden", [128, 512], F32)
        oTn = A("oTn", [128, 1024], BF16)
        yinT = [A("yinT%d" % i, [128, 1024], BF16) for i in range(3)]
        zb = A("zb", [128, D], F32)
        xo = [A("xo%d" % i, [128, D], F32) for i in range(2)]
        sq = A("sq", [128, 128], F32)
        ckf = A("ckf", [128, 128], F32)
        rs = A("rs", [128, 4], F32)
        Pb3 = P[:, 3 * 512:4 * 512].bitcast(BF16)

        wind = g.d_e_w_in[0].rearrange("(kc p) n -> p kc n", p=128)
        for kc in range(8):
            s.dma('pool', win[:, kc, 0:1472], wind[:, kc, 0:1472], w=['win'])
            s.dma('pool', win[:, kc, 1472:1536], wind[:, kc, 1408:1472], w=['win'])
        s.dma('sp', wif[:], wind[:, :, 1472:1476], w=['wif'])
        s.op('dve', lambda e: e.tensor_copy(out=wiw[:], in_=wif[:]), r=['wif'], w=['wiw'])
        s.dma('pool', poolw[:], g.d_e_pool_w[0].rearrange("g c d -> c g d"), w=['poolw'])
        s.dma('sp', pscale[:], g.d_pscale[:, :], w=['pscale'])
        s.dma('sp', kvn_bc[:], g.d_e_kv_norm[0].partition_broadcast(128), w=['kvn_bc'])
        s.dma('pool', ukT[:], g.d_ukT[:, :, :], w=['ukT'])
        s.dma('pool', uvpad[:], g.d_uvpad[:, :, :], w=['uvpad'])
        woutd = g.d_e_w_out[0].rearrange("(kc p) n -> p kc n", p=128)
        for kc in range(8):
            s.dma('pool', wout[:, kc, :], woutd[:, kc, :], w=['wout'])
        s.dma('pool', negI4[:], g.d_negI4[:, :], w=['negI4'])
        s.dma('sp', corr[:], g.d_corr[:, :, :], w=['corr'])
        s.op('dve', lambda e: e.memset(ut[:], 0.0), w=['ut'])

        def front_a(qb):
            sl = qb % 2
            s3 = qb % 3
            t0 = qb * 128
            nk = t0 + 128
            xn = 'xin%d' % s3
            s.dma('sp', xin[s3][:], src[t0:t0 + 128, :], w=[xn])
            emit_hT(g, L, xin[s3], xn, lambda kc: hT[:, kc, :], 'hT', (0, 1))
            yield
            def grp(out_ap, cols, bank, last=True):
                for kc in range(8):
                    s.op('pe', lambda e, kc=kc: e.matmul(out_ap, lhsT=win[:, kc, cols[0]:cols[1]], rhs=hT[:, kc, :],
                                                        start=(kc == 0), stop=(kc == 7)),
                         r=['win', 'hT'], w=['ps%d' % bank], inc=(kc == 7))
            for gi in range(4):
                grp(P[:, gi * 128:(gi + 1) * 128], (gi * 128, (gi + 1) * 128), 0)
                yield
            for j in range(4):
                grp(P[:, 512 + j * 128:512 + (j + 1) * 128], (512 + j * 128, 512 + (j + 1) * 128), 1)
                yield
            for j in range(2):
                grp(P[:, 1024 + j * 128:1024 + (j + 1) * 128], (1152 + j * 128, 1152 + (j + 1) * 128), 2)
                yield
            grp(P[:, 1024 + 256:1024 + 384], (1408, 1536), 2)
            yield
            for kc in range(8):
                s.op('pe', lambda e, kc=kc: e.matmul(P[:, 1536:1536 + 128], lhsT=hT[:, kc, :], rhs=win[:, kc, 1024:1152],
                                                    start=(kc == 0), stop=(kc == 7)), r=['win', 'hT'], w=['ps3'], inc=(kc == 7))
            for kc in range(8):
                s.op('pe', lambda e, kc=kc: e.matmul(P[:, 1536 + 128:1536 + 132], lhsT=hT[:, kc, :], rhs=wiw[:, kc, :],
                                                    start=(kc == 0), stop=(kc == 7)), r=['wiw', 'hT'], w=['ps3'], inc=(kc == 7))
            yield
            s.op('act', lambda e: e.activation(out=ut[:, :, 15:143], in_=P[:, 0:512].rearrange("p (g t) -> p g t", g=4), func=AF.Copy),
                 r=['ps0'], w=['ut'])
            s.op('act', lambda e: e.activation(out=qT[:], in_=P[:, 512:1024], func=AF.Copy), r=['ps1'], w=['qT'])
            s.op('dve', lambda e: e.tensor_copy(out=qiT[sl][:], in_=P[:, 1024:1024 + 256]), r=['ps2'], w=['qiT%d' % sl])
            s.op('dve', lambda e: e.tensor_copy(out=kiT_all[:, t0:t0 + 128], in_=P[:, 1024 + 256:1024 + 384]), r=['ps2'], w=['kiT_all'])
            s.op('dve', lambda e: e.tensor_copy(out=wis[sl][:], in_=P[:, 1536 + 128:1536 + 132]), r=['ps3'], w=['wis%d' % sl])
            yield
            s.op('act', lambda e: e.activation(out=ckf[:], in_=P[:, 1536:1536 + 128], func=AF.Copy), r=['ps3'], w=['ckf'])
            s.op('dve', lambda e: e.tensor_tensor(out=sq[:], in0=ckf[:], in1=ckf[:], op=ALU.mult), r=['ckf'], w=['sq'])
            s.op('dve', lambda e: e.reduce_sum(out=rs[:, 0:1], in_=sq[:], axis=AX.X), r=['sq'], w=['rs0'])
            s.op('dve', lambda e: e.tensor_scalar(out=rs[:, 1:2], in0=rs[:, 0:1], scalar1=1.0 / 128, scalar2=RMS_EPS, op0=ALU.mult, op1=ALU.add),
                 r=['rs0'], w=['rs1'])
            s.op('act', lambda e: e.activation(out=rs[:, 2:3], in_=rs[:, 1:2], func=AF.Sqrt), r=['rs1'], w=['rs2'])
            s.op('dve', lambda e: e.reciprocal(out=rs[:, 3:4], in_=rs[:, 2:3]), r=['rs2'], w=['rs3'])
            s.op('dve', lambda e: e.scalar_tensor_tensor(out=ckf[:], in0=ckf[:], scalar=rs[:, 3:4], in1=kvn_bc[:],
                                                         op0=ALU.mult, op1=ALU.mult), r=['ckf', 'rs3', 'kvn_bc'], w=['ckf'])
            s.op('act', lambda e: e.activation(out=ckvn_all[:, qb, :], in_=ckf[:], func=AF.Copy), r=['ckf'], w=['ckvn_all'])
            s.op('pe', lambda e: e.transpose(P[:, 1536 + 256:1536 + 384], ckf[:], g.ident[:]), r=['ckf', 'ident'], w=['ps3'])
            s.op('act', lambda e: e.activation(out=ckvnT_all[:, t0:t0 + 128], in_=P[:, 1536 + 256:1536 + 384], func=AF.Copy), r=['ps3'], w=['ckvnT_all'])
            yield
            for gi in range(4):
                win_ = 2 << gi
                U = ut[:, gi, :]
                s.op('dve', lambda e, U=U: e.tensor_tensor(out=ta[:, 1:143], in0=U[:, 1:143], in1=U[:, 0:142], op=ALU.add), r=['ut'], w=['ta'])
                sw = ta
                swn = 'ta'
                if gi >= 1:
                    s.op('dve', lambda e: e.tensor_tensor(out=tb[:, 3:143], in0=ta[:, 3:143], in1=ta[:, 1:141], op=ALU.add), r=['ta'], w=['tb'])
                    sw, swn = tb, 'tb'
                if gi >= 2:
                    s.op('dve', lambda e: e.tensor_tensor(out=ta[:, 7:143], in0=tb[:, 7:143], in1=tb[:, 3:139], op=ALU.add), r=['tb'], w=['ta'])
                    sw, swn = ta, 'ta'
                if gi >= 3:
                    s.op('dve', lambda e: e.tensor_tensor(out=tb[:, 15:143], in0=ta[:, 15:143], in1=ta[:, 7:135], op=ALU.add), r=['ta'], w=['tb'])
                    sw, swn = tb, 'tb'
                if qb == 0:
                    s.op('dve', lambda e, sw=sw, gi=gi, win_=win_: e.tensor_tensor(out=sw[:, 15:15 + win_ - 1], in0=sw[:, 15:15 + win_ - 1],
                                                                                 in1=corr[:, gi, 0:win_ - 1], op=ALU.mult),
                         r=[swn, 'corr'], w=[swn])
                s.op('dve', lambda e, sw=sw, gi=gi, win_=win_, U=U: e.scalar_tensor_tensor(out=dT[:, gi, :], in0=sw[:, 15:143], scalar=1.0 / win_,
                                                                                          in1=U[:, 15:143], op0=ALU.mult, op1=ALU.subtract),
                     r=[swn, 'ut'], w=['dT'])
                yield
            s.op('pool', lambda e: e.tensor_copy(out=ut[:, :, 0:15], in_=ut[:, :, 128:143]), r=['ut'], w=['ut'])
            for gi in range(4):
                s.op('pe', lambda e, gi=gi: e.matmul(P[:, gi * 128:(gi + 1) * 128], lhsT=poolw[:, gi, :], rhs=dT[:, gi, :], start=True, stop=True),
                     r=['poolw', 'dT'], w=['ps0'])
            for gi in range(4):
                s.op('act', lambda e, gi=gi: e.activation(out=yinT[s3][:, gi * 128:(gi + 1) * 128], in_=P[:, gi * 128:(gi + 1) * 128],
                                                          func=AF.Identity, scale=pscale[:, gi:gi + 1]),
                     r=['ps0', 'pscale'], w=['yinT%d' % s3])
            yield
            for h in range(8):
                po = (h % 2) * 64
                bank = 1 + h % 2
                off = bank * 512 + (h // 2) * 128
                s.op('pe', lambda e, h=h, po=po, off=off: e.matmul(P[:, off:off + 128], lhsT=ukT[po:po + 64, h // 2, :],
                                                                  rhs=qT[po:po + 64, (h // 2) * 128:(h // 2 + 1) * 128], start=True, stop=True),
                     r=['ukT', 'qT'], w=['ps%d' % bank])
            for j in range(2):
                s.op('act', lambda e, j=j: e.activation(out=qlT[s3][:, j * 512:(j + 1) * 512], in_=P[:, (1 + j) * 512:(2 + j) * 512], func=AF.Copy),
                     r=['ps%d' % (1 + j)], w=['qlT%d' % s3])
            yield
            Wn = 'W%d' % sl
            cnt = 0
            chunks = []
            k0_ = 0
            while k0_ < nk:
                rem = nk - k0_
                w0 = 512 if rem >= 512 else (256 if rem >= 256 else 128)
                chunks.append((k0_, w0))
                k0_ += w0
            for (k0, w_) in chunks:
                for h in range(4):
                    po = (h % 2) * 64
                    bank = 2 + cnt % 2
                    rb = rbuf[cnt % 2]
                    rbn = 'rbuf%d' % (cnt % 2)
                    cnt += 1
                    s.op('pe', lambda e, h=h, po=po, bank=bank, k0=k0, w_=w_: e.matmul(P[:, bank * 512:bank * 512 + w_],
                                                                                     lhsT=qiT[sl][po:po + 64, (h // 2) * 128:(h // 2 + 1) * 128],
                                                                                     rhs=kiT_all[po:po + 64, k0:k0 + w_], start=True, stop=True),
                         r=['qiT%d' % sl, 'kiT_all'], w=['ps%d' % bank])
                    s.op('act', lambda e, bank=bank, rb=rb, w_=w_: e.activation(out=rb[:, 0:w_], in_=P[:, bank * 512:bank * 512 + w_], func=AF.Relu),
                         r=['ps%d' % bank], w=[rbn])
                    if h == 0:
                        s.op('dve', lambda e, rb=rb, k0=k0, w_=w_: e.tensor_scalar(out=W[sl][:, k0:k0 + w_], in0=rb[:, 0:w_], scalar1=wis[sl][:, 0:1],
                                                                                 scalar2=None, op0=ALU.mult), r=[rbn, 'wis%d' % sl], w=[Wn])
                    else:
                        s.op('dve', lambda e, rb=rb, k0=k0, w_=w_, h=h: e.scalar_tensor_tensor(out=W[sl][:, k0:k0 + w_], in0=rb[:, 0:w_], scalar=wis[sl][:, h:h + 1],
                                                                                            in1=W[sl][:, k0:k0 + w_], op0=ALU.mult, op1=ALU.add),
                             r=[rbn, 'wis%d' % sl, Wn], w=[Wn])
                    yield

        def topk(qb):
            sl = qb % 2
            nk = qb * 128 + 128
            Wn = 'W%d' % sl
            nmn = 'notm%d' % sl
            if nk <= 256:
                s.op('dve', lambda e: e.memset(notm[sl][:, 0:nk], 0.0), w=[nmn])
            else:
                s.op('dve', lambda e: e.memset(W[sl][0:64, nk - 64:nk], NEG), r=[Wn], w=[Wn])
                for it in range(32):
                    s.op('dve', lambda e: e.max(out=m8[:], in_=W[sl][:, 0:nk]), r=[Wn], w=['m8'])
                    s.op('dve', lambda e: e.match_replace(out=W[sl][:, 0:nk], in_to_replace=m8[:], in_values=W[sl][:, 0:nk], imm_value=REP),
                         r=[Wn, 'm8'], w=[Wn])
                    yield
                s.op('dve', lambda e: e.tensor_scalar(out=notm[sl][:, 0:nk], in0=W[sl][:, 0:nk], scalar1=0.5 * REP, scalar2=None, op0=ALU.is_gt),
                     r=[Wn], w=[nmn])
            s.op('dve', lambda e: e.memset(notm[sl][0:64, nk - 64:nk], 1.0), r=[nmn], w=[nmn])
            yield

        def back(qb):
            sl = qb % 2
            s3 = qb % 3
            t0 = qb * 128
            cnt = 0
            for hg in range(2):
                for kb in range(qb + 1):
                    j = cnt % 2
                    cnt += 1
                    bank = 4 + j
                    s.op('pe', lambda e, bank=bank, kb=kb, hg=hg: e.matmul(P[:, bank * 512:(bank + 1) * 512], lhsT=ckvnT_all[:, kb * 128:(kb + 1) * 128],
                                                                          rhs=qlT[s3][:, hg * 512:(hg + 1) * 512], start=True, stop=False),
                         r=['ckvnT_all', 'qlT%d' % s3], w=['ps%d' % bank], inc=False)
                    s.op('pe', lambda e, bank=bank, kb=kb: e.matmul(P[:, bank * 512:(bank + 1) * 512], lhsT=notm[sl][:, kb * 128:(kb + 1) * 128],
                                                                   rhs=negI4[:], start=False, stop=True),
                         r=['notm%d' % sl, 'negI4'], w=['ps%d' % bank])
                    s.op('act', lambda e, bank=bank, j=j: e.activation(out=pT[j][:], in_=P[:, bank * 512:(bank + 1) * 512], func=AF.Exp, scale=0.125),
                         r=['ps%d' % bank], w=['pT%d' % j])
                    s.op('pe', lambda e, kb=kb, j=j: e.matmul(P[:, 6 * 512:7 * 512], lhsT=ckvn_all[:, kb, :], rhs=pT[j][:], start=(kb == 0), stop=(kb == qb)),
                         r=['ckvn_all', 'pT%d' % j], w=['ps6'], inc=False)
                    s.op('pe', lambda e, kb=kb, j=j: e.matmul(P[:, 7 * 512:8 * 512], lhsT=g.onesb[:], rhs=pT[j][:], start=(kb == 0), stop=(kb == qb)),
                         r=['onesb', 'pT%d' % j], w=['ps7'])
                    yield
                s.op('dve', lambda e: e.reciprocal(out=rden[:], in_=P[:, 7 * 512:8 * 512]), r=['ps7'], w=['rden'])
                s.op('dve', lambda e, hg=hg: e.tensor_tensor(out=oTn[:, hg * 512:(hg + 1) * 512], in0=P[:, 6 * 512:7 * 512], in1=rden[:], op=ALU.mult),
                     r=['ps6', 'rden'], w=['oTn'])
                yield
            for hp in range(4):
                for h2 in range(2):
                    h = 2 * hp + h2
                    s.op('pe', lambda e, hp=hp, h=h, h2=h2: e.matmul(P[:, 7 * 512 + hp * 128:7 * 512 + (hp + 1) * 128], lhsT=uvpad[:, h, :],
                                                                    rhs=oTn[:, (h // 2 + 4 * (h % 2)) * 128:(h // 2 + 4 * (h % 2) + 1) * 128], start=(h2 == 0), stop=(h2 == 1)),
                         r=['uvpad', 'oTn'], w=['ps7'], inc=(h2 == 1))
            s.op('act', lambda e: e.activation(out=yinT[s3][:, 512:1024], in_=P[:, 7 * 512:8 * 512], func=AF.Copy), r=['ps7'], w=['yinT%d' % s3])
            yield
            for half in range(2):
                b = 4 + half
                for kc in range(8):
                    s.op('pe', lambda e, kc=kc, half=half, b=b: e.matmul(P[:, b * 512:(b + 1) * 512], lhsT=yinT[s3][:, kc * 128:(kc + 1) * 128],
                                                                        rhs=wout[:, kc, half * 512:(half + 1) * 512], start=(kc == 0), stop=(kc == 7)),
                         r=['yinT%d' % s3, 'wout'], w=['ps%d' % b], inc=(kc == 7))
            yield
            emit_epilogue(g, L, (4, 5), xin[s3], 'xin%d' % s3, zb, xo[sl], 'xo%d' % sl, dst[t0:t0 + 128, :])
            yield

        nblk = g.nblk_dsa
        for it in range(nblk + 2):
            gens = []
            if it < nblk:
                gens.append([front_a(it), 1])
            if 1 <= it <= nblk:
                gens.append([topk(it - 1), 1])
            if it >= 2:
                gens.append([back(it - 2), 1])
            _interleave(gens)
        s.barrier()


def emit_gdn(g, src, dst):
    nc, s = g.nc, g.s
    P = g.ps
    with ExitStack() as les:
        L = Ctx()
        L.es = les
        emit_mods(g, L, g.d_o_mod_w[0], g.d_o_mod_b[0], g.d_o_ln_g[0], g.d_o_ln_b[0])
        A = lambda name, shape, dt: _sb(nc, les, name, shape, dt)
        win = A("gwin", [128, 8, 4096], BF16)
        wbf = A("gwbf", [128, 8, 16], F32)
        wbb = A("gwbb", [128, 8, 16], BF16)
        wout = A("gwout", [128, 8, D], BF16)
        cw = A("gcw", [128, 24, 4], F32)
        msl = A("msl", [128, 128], F32)
        mil = A("mil", [128, 128], F32)
        triu = A("triu", [128, 128], F32)
        mdm = A("mdm", [128, 5, 128], BF16)
        mdmT = A("mdmT", [128, 5, 128], BF16)
        alog = A("alog", [128, 8], F32)
        dtb = A("dtb", [128, 8], F32)
        onb = A("onb", [128, D], F32)
        xin = [A("gxin%d" % i, [128, D], F32) for i in range(1)]
        xrs = A("gxrs", [128, D], F32)
        hT = A("ghT", [128, 8, 128], BF16)
        xw = [A("xw%d" % i, [128, 131], F32) for i in range(2)]
        halo = A("ghalo", [128, 24, 3], F32)
        cbuf = [A("cbuf%d" % i, [128, 128], F32) for i in range(2)]
        act = [A("gact%d" % i, [128, 24 * 128], F32) for i in range(2)]
        sq = A("gsq", [128, 1024], F32)
        rstd = sq
        qkb = [A("qkb%d" % i, [128, 2048], BF16) for i in range(2)]
        gsil = [A("gsil%d" % i, [128, D], BF16) for i in range(3)]
        ba = A("ba", [128, 16], F32)
        tmpa = A("tmpa", [128, 8], F32)
        smb = [A("smb%d" % i, [128, 48], F32) for i in range(2)]
        zsm = A("zsm", [128, 8], F32)
        HP = []
        for p in range(2):
            h_ = Ctx()
            h_.sm = A("hsm%d" % p, [128, 2], F32)
            for nm in ("E", "EB", "t1", "Nf", "atf"):
                setattr(h_, nm, A("h%s%d" % (nm, p), [128, 128], F32))
            for nm in ("X", "Xt", "Q2", "Q2t", "Q4", "Q4t", "Yv", "Yt", "attT", "qdT", "kd", "Kbg", "Vb", "nwT", "vnew"):
                setattr(h_, nm, A("h%s%d" % (nm, p), [128, 128], BF16))
            h_.NM = A("hNM%d" % p, [128, 5, 128], BF16)
            h_.NMt = A("hNMt%d" % p, [128, 5, 128], BF16)
            HP.append(h_)
        Sf = A("gSf", [128, 8, 128], F32)
        Sb = A("gSb", [128, 8, 128], BF16)
        osb = [A("gosb%d" % i, [128, D], F32) for i in range(2)]
        yinT = A("gyinT", [128, 1024], BF16)
        zb = A("gzb", [128, D], F32)
        xo = [A("gxo%d" % i, [128, D], F32) for i in range(1)]
        wind = g.d_o_w_in[0].rearrange("(kc p) n -> p kc n", p=128)
        for kc in range(8):
            for q4 in range(2):
                s.dma('pool', win[:, kc, q4 * 2048:(q4 + 1) * 2048], wind[:, kc, q4 * 2048:(q4 + 1) * 2048], w=['gwin'])
        s.dma('sp', wbf[:], wind[:, :, 4096:4112], w=['gwbf'])
        s.op('dve', lambda e: e.tensor_copy(out=wbb[:], in_=wbf[:]), r=['gwbf'], w=['gwbb'])
        woutd = g.d_o_w_out[0].rearrange("(kc p) n -> p kc n", p=128)
        for kc in range(8):
            s.dma('pool', wout[:, kc, :], woutd[:, kc, :], w=['gwout'])
        s.dma('sp', cw[:], g.d_o_cw[:, :, :], w=['gcw'])
        s.dma('sp', msl[:], g.d_msl[:, :], w=['msl'])
        s.dma('sp', mil[:], g.d_mil[:, :], w=['mil'])
        s.dma('sp', triu[:], g.d_triu[:, :], w=['triu'])
        s.dma('pool', mdm[:], g.d_mdm[:, :, :], w=['mdm'])
        s.dma('pool', mdmT[:], g.d_mdmT[:, :, :], w=['mdmT'])
        s.dma('sp', alog[:], g.d_o_a_log[0].partition_broadcast(128), w=['alog'])
        s.dma('sp', dtb[:], g.d_o_dt_bias[0].partition_broadcast(128), w=['dtb'])
        s.dma('sp', onb[:], g.d_onb[0].partition_broadcast(128), w=['onb'])
        s.op('act', lambda e: e.activation(out=alog[:], in_=alog[:], func=AF.Exp), r=['alog'], w=['alog'])
        s.op('dve', lambda e: e.tensor_scalar(out=alog[:], in0=alog[:], scalar1=-1.0, scalar2=None, op0=ALU.mult), r=['alog'], w=['alog'])
        s.op('dve', lambda e: e.memset(halo[:], 0.0), w=['ghalo'])
        s.op('dve', lambda e: e.memset(Sf[:], 0.0), w=['gSf0', 'gSf1'])
        s.op('dve', lambda e: e.memset(Sb[:], 0.0), w=['gSb0', 'gSb1'])
        DK = float(128 ** -0.5)

        def phaseA(qb):
            t0 = qb * 128
            bp = qb % 2
            x3 = qb % 3
            xn = 'gxin0'
            an = 'gact%d' % bp
            sn = 'smb%d' % bp
            sm = smb[bp]
            ac = act[bp]
            s.dma('sp', xin[0][:], src[t0:t0 + 128, :], w=[xn])
            emit_hT(g, L, xin[0], xn, lambda kc: hT[:, kc, :], 'ghT', (0, 1))
            yield
            for ch in range(24):
                b = ch % 2
                cb = cbuf[b]
                cn = 'cbuf%d' % b
                for kc in range(8):
                    s.op('pe', lambda e, kc=kc, ch=ch, b=b: e.matmul(P[:, b * 512:b * 512 + 128], lhsT=win[:, kc, ch * 128:(ch + 1) * 128], rhs=hT[:, kc, :],
                                                                    start=(kc == 0), stop=(kc == 7)), r=['gwin', 'ghT'], w=['ps%d' % b], inc=(kc == 7))
                xb = xw[b]
                xbn = 'xw%d' % b
                s.op('pool', lambda e, ch=ch, xb=xb: e.tensor_copy(out=xb[:, 0:3], in_=halo[:, ch, :]), r=['ghalo'], w=[xbn])
                s.op('act', lambda e, b=b, xb=xb: e.activation(out=xb[:, 3:131], in_=P[:, b * 512:b * 512 + 128], func=AF.Copy), r=['ps%d' % b], w=[xbn])
                s.op('act', lambda e, ch=ch, b=b, cb=cb: e.activation(out=cb[:], in_=P[:, b * 512:b * 512 + 128], func=AF.Identity, scale=cw[:, ch, 3:4]),
                     r=['ps%d' % b, 'gcw'], w=[cn])
                s.op('pool', lambda e, ch=ch, xb=xb: e.tensor_copy(out=halo[:, ch, :], in_=xb[:, 128:131]), r=[xbn], w=['ghalo'])
                for j in range(3):
                    s.op('dve', lambda e, ch=ch, j=j, cb=cb, xb=xb: e.scalar_tensor_tensor(out=cb[:], in0=xb[:, j:j + 128], scalar=cw[:, ch, j:j + 1], in1=cb[:],
                                                                                          op0=ALU.mult, op1=ALU.add), r=[xbn, 'gcw', cn], w=[cn])
                s.op('act', lambda e, ch=ch, cb=cb: e.activation(out=ac[:, ch * 128:(ch + 1) * 128], in_=cb[:], func=AF.Silu), r=[cn], w=[an])
                yield
            for hf in range(2):
                seg = ac[:, hf * 1024:(hf + 1) * 1024]
                s.op('dve', lambda e, seg=seg: e.tensor_tensor(out=sq[:], in0=seg, in1=seg, op=ALU.mult), r=[an], w=['gsq'])
                yield
                for j in range(2):
                    b = j % 2
                    s.op('pe', lambda e, j=j, b=b: e.matmul(P[:, b * 512:(b + 1) * 512], lhsT=g.ones[:], rhs=sq[:, j * 512:(j + 1) * 512], start=True, stop=True),
                         r=['ones', 'gsq'], w=['ps%d' % b])
                for j in range(2):
                    b = j % 2
                    s.op('dve', lambda e, j=j, b=b: e.tensor_scalar(out=sq[:, j * 512:(j + 1) * 512], in0=P[:, b * 512:(b + 1) * 512], scalar1=RMS_EPS, scalar2=None, op0=ALU.add),
                         r=['ps%d' % b], w=['gsq'])
                yield
                s.op('act', lambda e: e.activation(out=sq[:], in_=sq[:], func=AF.Sqrt), r=['gsq'], w=['gsq'])
                s.op('dve', lambda e: e.reciprocal(out=sq[:], in_=sq[:]), r=['gsq'], w=['gsq'])
                yield
                s.op('dve', lambda e, seg=seg: e.tensor_tensor(out=seg, in0=seg, in1=sq[:], op=ALU.mult), r=[an, 'gsq'], w=[an])
                s.op('act', lambda e, seg=seg, hf=hf: e.activation(out=qkb[bp][:, hf * 1024:(hf + 1) * 1024], in_=seg, func=AF.Copy), r=[an], w=['qkb%d' % bp])
                yield
            for half in range(2):
                b = half
                for kc in range(8):
                    s.op('pe', lambda e, kc=kc, half=half, b=b: e.matmul(P[:, b * 512:(b + 1) * 512], lhsT=hT[:, kc, :],
                                                                        rhs=win[:, kc, 3072 + half * 512:3072 + (half + 1) * 512],
                                                                        start=(kc == 0), stop=(kc == 7)), r=['gwin', 'ghT'], w=['ps%d' % b], inc=(kc == 7))
                s.op('act', lambda e, half=half, b=b: e.activation(out=gsil[x3][:, half * 512:(half + 1) * 512], in_=P[:, b * 512:(b + 1) * 512], func=AF.Silu),
                     r=['ps%d' % b], w=['gsil%d' % x3])
                yield
            for kc in range(8):
                s.op('pe', lambda e, kc=kc: e.matmul(P[:, 0:16], lhsT=hT[:, kc, :], rhs=wbb[:, kc, :], start=(kc == 0), stop=(kc == 7)),
                     r=['gwbb', 'ghT'], w=['ps0'], inc=(kc == 7))
            s.op('dve', lambda e: e.tensor_copy(out=ba[:], in_=P[:, 0:16]), r=['ps0'], w=['ba'])
            yield
            s.op('act', lambda e: e.activation(out=sm[:, 0:8], in_=ba[:, 0:8], func=AF.Exp, scale=-1.0), r=['ba'], w=[sn])
            s.op('dve', lambda e: e.tensor_scalar(out=sm[:, 0:8], in0=sm[:, 0:8], scalar1=1.0, scalar2=None, op0=ALU.add), r=[sn], w=[sn])
            s.op('dve', lambda e: e.reciprocal(out=sm[:, 0:8], in_=sm[:, 0:8]), r=[sn], w=[sn])
            s.op('dve', lambda e: e.tensor_scalar(out=sm[:, 32:40], in0=sm[:, 0:8], scalar1=-1.0, scalar2=None, op0=ALU.mult), r=[sn], w=[sn])
            yield
            s.op('dve', lambda e: e.tensor_tensor(out=tmpa[:], in0=ba[:, 8:16], in1=dtb[:], op=ALU.add), r=['ba', 'dtb'], w=['tmpa'])
            s.op('act', lambda e: e.activation(out=tmpa[:], in_=tmpa[:], func=AF.Exp), r=['tmpa'], w=['tmpa'])
            s.op('act', lambda e: e.activation(out=tmpa[:], in_=tmpa[:], func=AF.Ln, bias=1.0), r=['tmpa'], w=['tmpa'])
            s.op('dve', lambda e: e.tensor_tensor(out=sm[:, 8:16], in0=tmpa[:], in1=alog[:], op=ALU.mult), r=['tmpa', 'alog', sn], w=[sn])
            yield
            s.op('pe', lambda e: e.matmul(P[:, 16:24], lhsT=triu[:], rhs=sm[:, 8:16], start=True, stop=True), r=['triu', sn], w=['ps0'])
            s.op('dve', lambda e: e.tensor_copy(out=sm[:, 16:24], in_=P[:, 16:24]), r=['ps0', sn], w=[sn])
            s.op('act', lambda e: e.activation(out=sm[:, 24:32], in_=sm[:, 16:24], func=AF.Exp), r=[sn], w=[sn])
            s.op('dve', lambda e: e.tensor_tensor(out=sm[:, 40:48], in0=sm[:, 24:32], in1=sm[:, 0:8], op=ALU.mult), r=[sn], w=[sn])
            yield

        def heads(qb, p):
            bp = qb % 2
            sm = smb[bp]
            sn = 'smb%d' % bp
            an = 'gact%d' % bp
            ac = act[bp]
            H = HP[p]
            X0 = (2 + 3 * p) * 512
            Y0 = X0 + 512
            Z0 = X0 + 1024
            xn_, yn_, zn_ = 'ps%d' % (2 + 3 * p), 'ps%d' % (3 + 3 * p), 'ps%d' % (4 + 3 * p)
            n = lambda nm: 'h%s%d' % (nm, p)
            for h in range(p, 8, 2):
                qn = ac[:, h * 128:(h + 1) * 128]
                kn = ac[:, (8 + h) * 128:(9 + h) * 128]
                vv = ac[:, (16 + h) * 128:(17 + h) * 128]
                qnb = qkb[bp][:, h * 128:(h + 1) * 128]
                knb = qkb[bp][:, (8 + h) * 128:(9 + h) * 128]
                qbn = 'qkb%d' % bp
                s.op('dve', lambda e, h=h: e.tensor_scalar(out=H.t1[:], in0=g.ident[:], scalar1=sm[:, 16 + h:17 + h], scalar2=None, op0=ALU.mult),
                     r=['ident', sn], w=[n('t1')])
                s.op('pe', lambda e: e.matmul(P[:, X0:X0 + 128], lhsT=g.ones[:], rhs=H.t1[:], start=True, stop=True), r=['ones', n('t1')], w=[xn_])
                s.op('dve', lambda e, h=h: e.tensor_scalar(out=H.E[:], in0=P[:, X0:X0 + 128], scalar1=sm[:, 16 + h:17 + h], scalar2=0.0, op0=ALU.subtract, op1=ALU.max),
                     r=[xn_, sn], w=[n('E')])
                s.op('act', lambda e: e.activation(out=H.E[:], in_=H.E[:], func=AF.Exp, scale=-1.0), r=[n('E')], w=[n('E')])
                s.op('act', lambda e: e.activation(out=H.EB[:], in_=P[:, X0:X0 + 128], func=AF.Exp), r=[xn_], w=[n('EB')])
                s.op('act', lambda e: e.activation(out=H.sm[:, 0:1], in_=P[:, X0 + 127:X0 + 128], func=AF.Exp), r=[xn_], w=[n('sm0')])
                s.op('dve', lambda e, h=h: e.tensor_scalar(out=H.sm[:, 1:2], in0=P[:, X0 + 127:X0 + 128], scalar1=sm[:, 16 + h:17 + h], scalar2=None, op0=ALU.subtract),
                     r=[xn_, sn], w=[n('sm1')])
                s.op('act', lambda e: e.activation(out=H.sm[:, 1:2], in_=H.sm[:, 1:2], func=AF.Exp), r=[n('sm1')], w=[n('sm1')])
                yield
                s.op('pe', lambda e, knb=knb: e.matmul(P[:, X0 + 128:X0 + 256], lhsT=knb, rhs=knb, start=True, stop=True), r=[qbn], w=[xn_])
                s.op('pe', lambda e, knb=knb, qnb=qnb: e.matmul(P[:, X0 + 256:X0 + 384], lhsT=qnb, rhs=knb, start=True, stop=True), r=[qbn], w=[xn_])
                s.op('dve', lambda e: e.tensor_tensor(out=H.t1[:], in0=H.E[:], in1=msl[:], op=ALU.mult), r=[n('E'), 'msl'], w=[n('t1')])
                s.op('dve', lambda e, h=h: e.scalar_tensor_tensor(out=H.Nf[:], in0=P[:, X0 + 128:X0 + 256], scalar=sm[:, 32 + h:33 + h], in1=H.t1[:], op0=ALU.mult, op1=ALU.mult),
                     r=[xn_, sn, n('t1')], w=[n('Nf')])
                s.op('dve', lambda e: e.tensor_tensor(out=H.t1[:], in0=H.E[:], in1=mil[:], op=ALU.mult), r=[n('E'), 'mil', n('Nf')], w=[n('t1')])
                s.op('dve', lambda e: e.scalar_tensor_tensor(out=H.atf[:], in0=P[:, X0 + 256:X0 + 384], scalar=DK, in1=H.t1[:], op0=ALU.mult, op1=ALU.mult),
                     r=[xn_, n('t1')], w=[n('atf')])
                yield
                s.op('pe', lambda e: e.transpose(P[:, Y0:Y0 + 128], H.Nf[:], g.ident[:]), r=[n('Nf'), 'ident'], w=[yn_])
                s.op('pe', lambda e: e.transpose(P[:, Y0 + 128:Y0 + 256], H.atf[:], g.ident[:]), r=[n('atf'), 'ident'], w=[yn_])
                s.op('pe', lambda e, kn=kn: e.transpose(P[:, Y0 + 256:Y0 + 384], kn, g.ident[:]), r=[an, 'ident'], w=[yn_])
                s.op('pe', lambda e, vv=vv: e.transpose(P[:, Y0 + 384:Y0 + 512], vv, g.ident[:]), r=[an, 'ident'], w=[yn_])
                s.op('dve', lambda e: e.tensor_tensor(out=H.NM[:], in0=H.Nf[:].unsqueeze(1).to_broadcast([128, 5, 128]), in1=mdm[:], op=ALU.mult),
                     r=[n('Nf'), 'mdm'], w=[n('NM')])
                s.op('dve', lambda e: e.tensor_tensor(out=H.NMt[:], in0=P[:, Y0:Y0 + 128].unsqueeze(1).to_broadcast([128, 5, 128]), in1=mdmT[:], op=ALU.mult),
                     r=[yn_, 'mdmT'], w=[n('NMt')])
                s.op('act', lambda e: e.activation(out=H.attT[:], in_=P[:, Y0 + 128:Y0 + 256], func=AF.Copy), r=[yn_], w=[n('attT')])
                s.op('dve', lambda e: e.tensor_scalar(out=H.kd[:], in0=P[:, Y0 + 256:Y0 + 384], scalar1=H.sm[:, 1:2], scalar2=None, op0=ALU.mult), r=[yn_, n('sm1')], w=[n('kd')])
                s.op('dve', lambda e, h=h: e.tensor_scalar(out=H.Kbg[:], in0=P[:, Y0 + 256:Y0 + 384], scalar1=sm[:, 40 + h:41 + h], scalar2=None, op0=ALU.mult),
                     r=[yn_, sn], w=[n('Kbg')])
                s.op('dve', lambda e, h=h: e.tensor_scalar(out=H.Vb[:], in0=P[:, Y0 + 384:Y0 + 512], scalar1=sm[:, h:h + 1], scalar2=None, op0=ALU.mult), r=[yn_, sn], w=[n('Vb')])
                s.op('dve', lambda e, qn=qn: e.scalar_tensor_tensor(out=H.qdT[:], in0=qn, scalar=DK, in1=H.EB[:], op0=ALU.mult, op1=ALU.mult),
                     r=[an, n('EB')], w=[n('qdT')])
                yield
                zc = [0]

                def mm(lhsT, rhs, rn):
                    c0 = Z0 + (zc[0] % 4) * 128
                    zc[0] += 1
                    s.op('pe', lambda e: e.matmul(P[:, c0:c0 + 128], lhsT=lhsT, rhs=rhs, start=True, stop=True), r=rn, w=[zn_])
                    return P[:, c0:c0 + 128]

                def cp(dst, dn, src_ps):
                    s.op('act', lambda e: e.activation(out=dst, in_=src_ps, func=AF.Copy), r=[zn_], w=[dn])

                def acc(dst, dn, src_ps):
                    s.op('dve', lambda e: e.tensor_tensor(out=dst, in0=src_ps, in1=dst, op=ALU.add), r=[zn_, dn], w=[dn])

                M0, M0t = H.NM[:, 0, :], H.NMt[:, 0, :]
                s.op('dve', lambda e: e.tensor_tensor(out=H.X[:], in0=M0, in1=g.identb[:], op=ALU.add), r=[n('NM'), 'identb'], w=[n('X')])
                s.op('dve', lambda e: e.tensor_tensor(out=H.Xt[:], in0=M0t, in1=g.identb[:], op=ALU.add), r=[n('NMt'), 'identb'], w=[n('Xt')])
                cp(H.Q2[:], n('Q2'), mm(M0t, M0, [n('NM'), n('NMt')]))
                cp(H.Q2t[:], n('Q2t'), mm(M0, M0t, [n('NM'), n('NMt')]))
                yield
                acc(H.X[:], n('X'), mm(H.Q2t[:], H.X[:], [n('Q2t'), n('X')]))
                acc(H.Xt[:], n('Xt'), mm(H.Q2[:], H.Xt[:], [n('Q2'), n('Xt')]))
                cp(H.Q4[:], n('Q4'), mm(H.Q2t[:], H.Q2[:], [n('Q2'), n('Q2t')]))
                cp(H.Q4t[:], n('Q4t'), mm(H.Q2[:], H.Q2t[:], [n('Q2'), n('Q2t')]))
                yield
                acc(H.X[:], n('X'), mm(H.Q4t[:], H.X[:], [n('Q4t'), n('X')]))
                acc(H.Xt[:], n('Xt'), mm(H.Q4[:], H.Xt[:], [n('Q4'), n('Xt')]))
                yield
                for lv in range(1, 5):
                    Nb, Nbt = H.NM[:, lv, :], H.NMt[:, lv, :]
                    if lv < 4:
                        cp(H.Yv[:], n('Yv'), mm(Nbt, H.X[:], [n('NMt'), n('X')]))
                    cp(H.Yt[:], n('Yt'), mm(Nb, H.Xt[:], [n('NM'), n('Xt')]))
                    yield
                    pa = mm(H.Xt[:], H.Yv[:], [n('Xt'), n('Yv')]) if lv < 4 else None
                    pb = mm(H.X[:], H.Yt[:], [n('X'), n('Yt')])
                    if lv < 4:
                        acc(H.X[:], n('X'), pa)
                    acc(H.Xt[:], n('Xt'), pb)
                    yield
                pw = mm(H.Kbg[:], H.Xt[:], [n('Kbg'), n('Xt')])
                s.op('act', lambda e: e.activation(out=H.nwT[:], in_=pw, func=AF.Copy, scale=-1.0), r=[zn_], w=[n('nwT')])
                yield
                s.op('pe', lambda e: e.matmul(P[:, Y0:Y0 + 128], lhsT=H.Xt[:], rhs=H.Vb[:], start=True, stop=False), r=[n('Xt'), n('Vb')], w=[yn_], inc=False)
                s.op('pe', lambda e, h=h: e.matmul(P[:, Y0:Y0 + 128], lhsT=H.nwT[:], rhs=Sb[:, h, :], start=False, stop=True), r=[n('nwT'), 'gSb%d' % p], w=[yn_])
                s.op('act', lambda e: e.activation(out=H.vnew[:], in_=P[:, Y0:Y0 + 128], func=AF.Copy), r=[yn_], w=[n('vnew')])
                yield
                s.op('pe', lambda e, h=h: e.matmul(P[:, X0 + 384:X0 + 512], lhsT=H.qdT[:], rhs=Sb[:, h, :], start=True, stop=False),
                     r=[n('qdT'), 'gSb%d' % p], w=[xn_], inc=False)
                s.op('pe', lambda e: e.matmul(P[:, X0 + 384:X0 + 512], lhsT=H.attT[:], rhs=H.vnew[:], start=False, stop=True),
                     r=[n('attT'), n('vnew')], w=[xn_])
                s.op('act', lambda e, h=h: e.activation(out=osb[bp][:, h * 128:(h + 1) * 128], in_=P[:, X0 + 384:X0 + 512], func=AF.Copy),
                     r=[xn_], w=['gosb%d_%d' % (bp, p)])
                s.op('pe', lambda e: e.matmul(P[:, Y0 + 128:Y0 + 256], lhsT=H.kd[:], rhs=H.vnew[:], start=True, stop=True), r=[n('kd'), n('vnew')], w=[yn_])
                s.op('dve', lambda e, h=h: e.scalar_tensor_tensor(out=Sf[:, h, :], in0=Sf[:, h, :], scalar=H.sm[:, 0:1], in1=P[:, Y0 + 128:Y0 + 256], op0=ALU.mult, op1=ALU.add),
                     r=['gSf%d' % p, n('sm0'), yn_], w=['gSf%d' % p])
                s.op('act', lambda e, h=h: e.activation(out=Sb[:, h, :], in_=Sf[:, h, :], func=AF.Copy), r=['gSf%d' % p], w=['gSb%d' % p])
                yield

        def phaseZ(qb):
            t0 = qb * 128
            bp = qb % 2
            x3 = qb % 3
            ob = osb[bp]
            on = ['gosb%d_0' % bp, 'gosb%d_1' % bp]
            s.op('dve', lambda e: e.tensor_tensor(out=zb[:], in0=ob[:], in1=ob[:], op=ALU.mult), r=on, w=['zb'])
            for h in range(8):
                s.op('dve', lambda e, h=h: e.reduce_sum(out=zsm[:, h:h + 1], in_=zb[:, h * 128:(h + 1) * 128], axis=AX.X), r=['zb'], w=['zsm'])
            yield
            s.op('dve', lambda e: e.tensor_scalar(out=zsm[:], in0=zsm[:], scalar1=1.0 / 128, scalar2=RMS_EPS, op0=ALU.mult, op1=ALU.add), r=['zsm'], w=['zsm'])
            s.op('act', lambda e: e.activation(out=zsm[:], in_=zsm[:], func=AF.Sqrt), r=['zsm'], w=['zsm'])
            s.op('dve', lambda e: e.reciprocal(out=zsm[:], in_=zsm[:]), r=['zsm'], w=['zsm'])
            yield
            for h in range(8):
                s.op('dve', lambda e, h=h: e.tensor_scalar(out=ob[:, h * 128:(h + 1) * 128], in0=ob[:, h * 128:(h + 1) * 128], scalar1=zsm[:, h:h + 1],
                                                           scalar2=None, op0=ALU.mult), r=on + ['zsm'], w=on)
            yield
            s.op('dve', lambda e: e.tensor_tensor(out=ob[:], in0=ob[:], in1=onb[:], op=ALU.mult), r=on + ['onb'], w=on)
            s.op('dve', lambda e: e.tensor_tensor(out=ob[:], in0=ob[:], in1=gsil[x3][:], op=ALU.mult), r=on + ['gsil%d' % x3], w=on)
            yield
            for kc in range(8):
                b = kc // 4
                off = b * 512 + (kc % 4) * 128
                s.op('pe', lambda e, kc=kc, off=off: e.transpose(P[:, off:off + 128], ob[:, kc * 128:(kc + 1) * 128], g.ident[:]), r=on + ['ident'], w=['ps%d' % b])
            for j in range(2):
                s.op('act', lambda e, j=j: e.activation(out=yinT[:, j * 512:(j + 1) * 512], in_=P[:, j * 512:(j + 1) * 512], func=AF.Copy),
                     r=['ps%d' % j], w=['gyinT'])
            yield
            for half in range(2):
                b = half
                for kc in range(8):
                    s.op('pe', lambda e, kc=kc, half=half, b=b: e.matmul(P[:, b * 512:(b + 1) * 512], lhsT=yinT[:, kc * 128:(kc + 1) * 128],
                                                                        rhs=wout[:, kc, half * 512:(half + 1) * 512], start=(kc == 0), stop=(kc == 7)),
                         r=['gyinT', 'gwout'], w=['ps%d' % b], inc=(kc == 7))
            s.dma('sp', xrs[:], src[t0:t0 + 128, :], w=['gxrs'])
            emit_epilogue(g, L, (0, 1), xrs, 'gxrs', zb, xo[0], 'gxo0', dst[t0:t0 + 128, :])
            yield

        nblk = g.nblk_gdn
        for it in range(nblk + 2):
            gens = []
            if 1 <= it <= nblk:
                gens.append([heads(it - 1, 0), 1])
                gens.append([heads(it - 1, 1), 1])
            if it < nblk:
                gens.append([phaseA(it), 1])
            if it >= 2:
                gens.append([phaseZ(it - 2), 1])
            _interleave(gens)
        s.barrier()


W_SPECS = [
    ("e_mod_w", [1, D, 3 * D]), ("e_mod_b", [1, 1, 3 * D]), ("e_ln_g", [1, D]), ("e_ln_b", [1, D]),
    ("o_mod_w", [1, D, 3 * D]), ("o_mod_b", [1, 1, 3 * D]), ("o_ln_g", [1, D]), ("o_ln_b", [1, D]),
    ("f_mod_w", [2, D, 3 * D]), ("f_mod_b", [2, 1, 3 * D]), ("f_ln_g", [2, D]), ("f_ln_b", [2, D]),
    ("f_w_up", [2, D, 2 * DFF]), ("f_w_down", [2, DFF, D]), ("f_cw", [2, 128, 2 * NFC, 4]),
    ("ident", [128, 128]),
    ("e_w_in", [1, D, 1476]), ("e_pool_w", [1, 4, 128, 128]), ("pscale", [128, 4]), ("e_kv_norm", [1, 128]),
    ("o_w_in", [1, D, 4112]), ("o_w_out", [1, D, D]), ("o_cw", [128, 24, 4]), ("msl", [128, 128]), ("mil", [128, 128]), ("triu", [128, 128]), ("mdm", [128, 5, 128]), ("mdmT", [128, 5, 128]),
    ("o_a_log", [1, 8]), ("o_dt_bias", [1, 8]), ("onb", [1, D]),
    ("ukT", [128, 4, 128]), ("uvpad", [128, 8, 128]), ("e_w_out", [1, D, D]), ("negI4", [128, 512]), ("corr", [128, 4, 15]),
]


def build(stages):
    nc = bass.Bass("TRN2", target_bir_lowering=False)
    g = Ctx()
    g.nc = nc
    g.d_x = nc.dram_tensor("x", [S, D], F32, kind="ExternalInput").ap()
    g.d_ccol = nc.dram_tensor("ccol", [128, 8], F32, kind="ExternalInput").ap()
    for nm, shp in W_SPECS:
        setattr(g, "d_" + nm, nc.dram_tensor(nm, shp, F32, kind="ExternalInput").ap())
    g.d_out = nc.dram_tensor("out", [S, D], F32, kind="ExternalOutput").ap()
    scr = [nc.dram_tensor("xscr%d" % i, [S, D], F32, kind="Internal").ap() for i in range(3)]
    with ExitStack() as es:
        g.es = es
        g.s = Sch(nc, es)
        g.ps = es.enter_context(nc.psum_tensor("ps", [128, 8 * 512], F32))
        g.lnst = _sb(nc, es, "lnst", [128, 12], F32)
        g.lnmv = _sb(nc, es, "lnmv", [128, 8], F32)
        emit_consts(g)
        bufs = [g.d_x] + scr
        n = len(stages)
        for i, st in enumerate(stages):
            src = g.d_x if i == 0 else scr[(i - 1) % 3]
            dst = g.d_out if i == n - 1 else scr[i % 3]
            if st[0] == 'ffn':
                emit_ffn(g, st[1], src, dst)
            elif st[0] == 'gdn':
                g.nblk_gdn = st[1] if len(st) > 1 else NB
                emit_gdn(g, src, dst)
            elif st[0] == 'dsa':
                g.nblk_dsa = st[1] if len(st) > 1 else NB
                emit_dsa(g, src, dst)
            else:
                raise ValueError(st)
        g.s.finish()
    return nc


def prep_weights(inp):
    f = lambda a: np.ascontiguousarray(np.asarray(a, dtype=np.float32))
    w = {}
    for k in ("e_mod_w", "e_ln_g", "e_ln_b", "o_mod_w", "o_ln_g", "o_ln_b", "f_mod_w", "f_ln_g", "f_ln_b", "f_w_up", "f_w_down"):
        w[k] = f(inp[k])
    for k in ("e_mod_b", "o_mod_b", "f_mod_b"):
        a = f(inp[k])
        w[k] = np.ascontiguousarray(a.reshape(a.shape[0], 1, 3 * D))
    cwt = f(inp["f_conv_w"])
    cb = f(inp["f_conv_b"])
    a = np.concatenate([cwt, cb[:, None, :]], axis=1)
    a = a.reshape(2, 4, 2 * NFC, 128).transpose(0, 3, 2, 1)
    w["f_cw"] = np.ascontiguousarray(a)
    w["ident"] = np.eye(128, dtype=np.float32)
    for k in ("e_w_in", "e_pool_w", "e_kv_norm", "e_w_out"):
        w[k] = f(inp[k])
    w["pscale"] = np.ascontiguousarray(f(inp["e_pool_scale"])[0].reshape(4, 128).T)
    uk = f(inp["e_w_uk"])[0]
    w["ukT"] = np.ascontiguousarray(uk.reshape(4, 2, 128, 64).transpose(1, 3, 0, 2).reshape(128, 4, 128))
    uv = f(inp["e_w_uv"])[0]
    uvp = np.zeros((128, 8, 128), np.float32)
    for h in range(8):
        uvp[:, h, (h % 2) * 64:(h % 2) * 64 + 64] = uv[h]
    w["uvpad"] = uvp
    w["negI4"] = np.ascontiguousarray(np.tile(-30000.0 * np.eye(128, dtype=np.float32), (1, 4)))
    corr = np.ones((128, 4, 15), np.float32)
    for gi in range(4):
        win_ = 2 << gi
        for t in range(win_ - 1):
            corr[:, gi, t] = win_ / (t + 1.0)
    w["corr"] = corr
    for k in ("o_w_in", "o_w_out", "o_a_log", "o_dt_bias"):
        w[k] = f(inp[k])
    ocw = f(inp["o_conv_w"])[0]
    w["o_cw"] = np.ascontiguousarray(ocw.reshape(4, 24, 128).transpose(2, 1, 0))
    ar = np.arange(128)
    w["msl"] = (ar[:, None] > ar[None, :]).astype(np.float32)
    w["mil"] = (ar[:, None] >= ar[None, :]).astype(np.float32)
    w["triu"] = (ar[:, None] <= ar[None, :]).astype(np.float32)
    w["onb"] = np.ascontiguousarray(np.tile(f(inp["o_out_norm"])[0], 8)[None, :])
    mdm = np.zeros((128, 5, 128), np.float32)
    mdm[:, 0, :] = (ar[:, None] // 8 == ar[None, :] // 8)
    for li, bsz in enumerate((8, 16, 32, 64)):
        bl = ar // bsz
        mdm[:, 1 + li, :] = (bl[:, None] % 2 == 1) & (bl[None, :] == bl[:, None] - 1)
    w["mdm"] = mdm
    w["mdmT"] = np.ascontiguousarray(mdm.transpose(2, 1, 0))
    return w


STAGES = [('dsa',), ('ffn', 0), ('gdn',), ('ffn', 1)]


def kernel(**inp):
    x = np.asarray(inp["x"], dtype=np.float32)
    c = np.asarray(inp["c"], dtype=np.float32)
    w = prep_weights(inp)
    nc = build(STAGES)
    in_maps = []
    for b in range(8):
        m = dict(w)
        m["x"] = np.ascontiguousarray(x[b])
        m["ccol"] = np.ascontiguousarray(c[b].reshape(8, 128).T)
        in_maps.append(m)
    res = run_bass_kernel_spmd(nc, in_maps, core_ids=list(range(8)))
    return np.stack([np.asarray(r["out"], dtype=np.float32) for r in res.results], axis=0)
```

```python
import os
import numpy as np
from contextlib import ExitStack
import concourse.bass as bass
import concourse.mybir as mybir
from concourse.bass_utils import run_bass_kernel_spmd

F32 = mybir.dt.float32
BF16 = mybir.dt.bfloat16
AF = mybir.ActivationFunctionType
ALU = mybir.AluOpType
AX = mybir.AxisListType

D = 1024
S = 4096
NB = S // 128
DFF = 2688
NFC = DFF // 128
ALPHA = float(4 ** 0.25)
LN_EPS = 1e-5
RMS_EPS = 1e-6
NDS = 12


class Sch:
    def __init__(self, nc, es):
        self.nc = nc
        self.E = {'pe': nc.tensor, 'act': nc.scalar, 'dve': nc.vector,
                  'pool': nc.gpsimd, 'sp': nc.sync}
        self.sem = {}
        for e in self.E:
            self.sem[e] = es.enter_context(nc.semaphore('s_' + e))
        self.cnt = {e: 0 for e in self.E}
        self.waited = {e: {} for e in self.E}
        self.lastw = {}
        self.readers = {}
        self.dq = ('sp', 'pool', 'act')
        self.dcnt = {}
        self.drr = {q: 0 for q in self.dq}
        for q in self.dq:
            for i in range(NDS):
                k = (q, i)
                self.sem[k] = es.enter_context(nc.semaphore('d_%s%d' % (q, i)))
                self.dcnt[k] = 0
        self.nwaits = 0

    def _wait(self, e, tok):
        key, val = tok
        if key == e and e == 'pe':
            return
        if self.waited[e].get(key, 0) >= val:
            return
        self.E[e].wait_ge(self.sem[key], val)
        self.waited[e][key] = val
        self.nwaits += 1

    def _collect(self, r, w, e=None):
        deps = {}

        def add(t):
            if t is None:
                return
            if deps.get(t[0], 0) < t[1]:
                deps[t[0]] = t[1]
        for x in r:
            for k, v in self.lastw.get(x, {}).items():
                add((k, v))
            if x.startswith('ps') and e is not None:
                for k, v in self.readers.get(x, {}).items():
                    if k != e:
                        add((k, v))
        for x in w:
            for k, v in self.lastw.get(x, {}).items():
                add((k, v))
            for k, v in self.readers.get(x, {}).items():
                add((k, v))
        return list(deps.items())

    def _record(self, tok, r, w):
        for x in w:
            self.lastw.setdefault(x, {})[tok[0]] = tok[1]
            self.readers[x] = {}
        for x in r:
            d = self.readers.setdefault(x, {})
            if d.get(tok[0], 0) < tok[1]:
                d[tok[0]] = tok[1]

    def op(self, e, fn, r=(), w=(), inc=True):
        for t in self._collect(r, w, e):
            self._wait(e, t)
        ins = fn(self.E[e])
        if inc:
            self.cnt[e] += 1
            ins.then_inc(self.sem[e], 1)
            tok = (e, self.cnt[e])
        else:
            tok = (e, self.cnt[e] + 1)
        self._record(tok, r, w)
        return ins

    def dma(self, q, out, in_, r=(), w=()):
        i = self.drr[q]
        self.drr[q] = (i + 1) % NDS
        k = (q, i)
        if self.dcnt[k] > 0:
            self._wait(q, (k, self.dcnt[k]))
        for t in self._collect(r, w):
            self._wait(q, t)
        ins = self.E[q].dma_start(out=out, in_=in_)
        self.dcnt[k] += 16
        ins.then_inc(self.sem[k], 16)
        tok = (k, self.dcnt[k])
        self._record(tok, r, w)
        return ins

    def barrier(self):
        toks = [(e, self.cnt[e]) for e in self.E if self.cnt[e] > 0]
        toks += [(k, v) for k, v in self.dcnt.items() if v > 0]
        for e in self.E:
            for t in toks:
                self._wait(e, t)
        self.lastw = {}
        self.readers = {}

    def finish(self):
        for k, v in self.dcnt.items():
            if v > 0:
                self._wait('sp', (k, v))


class Ctx:
    pass


def _interleave(gens):
    live = list(gens)
    while live:
        nxt = []
        for item in live:
            gen, n = item
            done = False
            for _ in range(n):
                try:
                    next(gen)
                except StopIteration:
                    done = True
                    break
            if not done:
                nxt.append(item)
        live = nxt


_UID = [0]


def _sb(nc, es, name, shape, dt):
    _UID[0] += 1
    return es.enter_context(nc.sbuf_tensor("sb%d_%s" % (_UID[0], name), list(shape), dt))


def emit_consts(g):
    nc, s, es = g.nc, g.s, g.es
    g.ident = _sb(nc, es, "ident", [128, 128], F32)
    g.identb = _sb(nc, es, "identb", [128, 128], BF16)
    g.ones = _sb(nc, es, "ones", [128, 128], F32)
    g.onesb = _sb(nc, es, "onesb", [128, 128], BF16)
    s.dma('sp', g.ident[:], g.d_ident[:, :], w=['ident'])
    s.dma('pool', g.identb[:], g.d_ident[:, :], w=['identb'])
    s.op('dve', lambda e: e.memset(g.ones[:], 1.0), w=['ones'])
    s.op('dve', lambda e: e.memset(g.onesb[:], 1.0), w=['onesb'])
    g.ccol = _sb(nc, es, "ccol", [128, 8], F32)
    g.sc = _sb(nc, es, "sc", [128, 8], F32)
    g.scb = _sb(nc, es, "scb", [128, 8, 128], F32)
    s.dma('sp', g.ccol[:], g.d_ccol[:, :], w=['ccol'])
    s.op('act', lambda e: e.activation(out=g.sc[:], in_=g.ccol[:], func=AF.Silu), r=['ccol'], w=['sc'])
    for kc in range(8):
        s.op('dve', lambda e, kc=kc: e.tensor_scalar(out=g.scb[:, kc, :], in0=g.ones[:], scalar1=g.sc[:, kc:kc + 1],
                                                     scalar2=None, op0=ALU.mult), r=['ones', 'sc'], w=['scb'])


def emit_mods(g, L, modw, modb_row, lng, lnb):
    nc, s, es = g.nc, g.s, g.es
    L.shift = _sb(nc, L.es, "shift", [128, 8], F32)
    L.scale1 = _sb(nc, L.es, "scale1", [128, 8], F32)
    L.gate_bc = _sb(nc, L.es, "gate_bc", [128, D], F32)
    L.lng_bc = _sb(nc, L.es, "lng_bc", [128, D], F32)
    L.lnb_bc = _sb(nc, L.es, "lnb_bc", [128, D], F32)
    s.dma('sp', L.lng_bc[:], lng.partition_broadcast(128), w=['lng_bc'])
    s.dma('sp', L.lnb_bc[:], lnb.partition_broadcast(128), w=['lnb_bc'])
    with ExitStack() as es2:
        mw = [_sb(nc, es2, "mw%d" % i, [128, 3 * D], F32) for i in range(2)]
        brow = _sb(nc, es2, "brow", [1, 3 * D], F32)
        bc = _sb(nc, es2, "modbc", [128, 2 * D], F32)
        one11 = _sb(nc, es2, "one11", [1, 1], F32)
        s.op('dve', lambda e: e.memset(one11[:], 1.0), w=['one11'])
        s.dma('sp', brow[:], modb_row[:, :], w=['brow'])
        P = g.ps
        for kc in range(8):
            t = mw[kc % 2]
            nm = 'mw%d' % (kc % 2)
            s.dma('sp' if kc % 2 == 0 else 'pool', t[:], modw[kc * 128:(kc + 1) * 128, :], w=[nm])
            for j in range(6):
                s.op('pe', lambda e, j=j, kc=kc, t=t: e.matmul(P[:, j * 512:(j + 1) * 512], lhsT=g.scb[:, kc, :],
                                                            rhs=t[:, j * 512:(j + 1) * 512], start=(kc == 0), stop=False),
                     r=[nm, 'scb'], w=['ps%d' % j], inc=(j == 5))
        for j in range(6):
            s.op('pe', lambda e, j=j: e.matmul(P[:, j * 512:(j + 1) * 512], lhsT=g.ones[0:1, :],
                                               rhs=brow[0:1, j * 512:(j + 1) * 512], start=False, stop=True),
                 r=['brow', 'ones'], w=['ps%d' % j])
        for j in range(4):
            s.op('act' if j % 2 else 'dve',
                 (lambda e, j=j: e.activation(out=bc[:, j * 512:(j + 1) * 512], in_=P[:, j * 512:(j + 1) * 512], func=AF.Copy))
                 if j % 2 else
                 (lambda e, j=j: e.tensor_copy(out=bc[:, j * 512:(j + 1) * 512], in_=P[:, j * 512:(j + 1) * 512])),
                 r=['ps%d' % j], w=['modbc%d' % j])
        for j in range(2):
            s.op('dve', lambda e, j=j: e.tensor_copy(out=L.gate_bc[:, j * 512:(j + 1) * 512], in_=P[:, (4 + j) * 512:(5 + j) * 512]),
                 r=['ps%d' % (4 + j)], w=['gate_bc'])
        for j in range(16):
            s.op('pe', lambda e, j=j: e.matmul(P[:, 6 * 512 + j:6 * 512 + j + 1], lhsT=bc[0:1, j * 128:(j + 1) * 128],
                                               rhs=one11[0:1, 0:1], start=True, stop=True),
                 r=['modbc%d' % (j // 4), 'one11'], w=['ps6'])
        s.op('dve', lambda e: e.tensor_copy(out=L.shift[:], in_=P[:, 6 * 512:6 * 512 + 8]), r=['ps6'], w=['shift'])
        s.op('dve', lambda e: e.tensor_scalar(out=L.scale1[:], in0=P[:, 6 * 512 + 8:6 * 512 + 16], scalar1=1.0, scalar2=None,
                                              op0=ALU.add), r=['ps6'], w=['scale1'])
        s.barrier()


def emit_hT(g, L, xin, xin_nm, hT_ap_fn, hT_nm, pbanks):
    s = g.s
    P = g.ps
    for kc in range(8):
        b = pbanks[kc // 4]
        off = b * 512 + (kc % 4) * 128
        s.op('pe', lambda e, kc=kc, off=off: e.transpose(P[:, off:off + 128], xin[:, kc * 128:(kc + 1) * 128], g.ident[:]),
             r=[xin_nm, 'ident'], w=['ps%d' % b])
    for kc in range(8):
        b = pbanks[kc // 4]
        off = b * 512 + (kc % 4) * 128
        s.op('act', lambda e, kc=kc, off=off: e.activation(out=hT_ap_fn(kc), in_=P[:, off:off + 128], func=AF.Identity,
                                                           scale=L.scale1[:, kc:kc + 1], bias=L.shift[:, kc:kc + 1]),
             r=['ps%d' % b, 'scale1', 'shift'], w=[hT_nm])


def emit_epilogue(g, L, ybanks, xres, xres_nm, zb, xo, xo_nm, dst_rows):
    s = g.s
    P = g.ps
    for j in range(2):
        b = ybanks[j]
        s.op('dve', lambda e, j=j, b=b: e.tensor_tensor(out=zb[:, j * 512:(j + 1) * 512], in0=P[:, b * 512:(b + 1) * 512],
                                                        in1=L.gate_bc[:, j * 512:(j + 1) * 512], op=ALU.mult),
             r=['ps%d' % b, 'gate_bc'], w=['zb'])
    s.op('dve', lambda e: e.scalar_tensor_tensor(out=zb[:], in0=xres[:], scalar=ALPHA, in1=zb[:], op0=ALU.mult, op1=ALU.add),
         r=[xres_nm, 'zb'], w=['zb'])
    st = g.lnst
    for j in range(2):
        s.op('dve', lambda e, j=j: e.bn_stats(out=st[:, j * 6:(j + 1) * 6], in_=zb[:, j * 512:(j + 1) * 512]), r=['zb'], w=['lnst'])
    s.op('dve', lambda e: e.bn_aggr(out=g.lnmv[:, 0:2], in_=st[:, 0:12]), r=['lnst'], w=['lnmv'])
    s.op('dve', lambda e: e.tensor_scalar(out=g.lnmv[:, 2:3], in0=g.lnmv[:, 1:2], scalar1=LN_EPS, scalar2=None, op0=ALU.add),
         r=['lnmv'], w=['lnmv2'])
    s.op('act', lambda e: e.activation(out=g.lnmv[:, 3:4], in_=g.lnmv[:, 2:3], func=AF.Sqrt), r=['lnmv2'], w=['lnmv3'])
    s.op('dve', lambda e: e.reciprocal(out=g.lnmv[:, 4:5], in_=g.lnmv[:, 3:4]), r=['lnmv3'], w=['lnmv4'])
    s.op('dve', lambda e: e.tensor_scalar(out=zb[:], in0=zb[:], scalar1=g.lnmv[:, 0:1], scalar2=g.lnmv[:, 4:5],
                                          op0=ALU.subtract, op1=ALU.mult), r=['zb', 'lnmv', 'lnmv4'], w=['zb'])
    s.op('pool', lambda e: e.tensor_tensor(out=zb[:], in0=zb[:], in1=L.lng_bc[:], op=ALU.mult), r=['zb', 'lng_bc'], w=['zb'])
    s.op('pool', lambda e: e.tensor_tensor(out=xo[:], in0=zb[:], in1=L.lnb_bc[:], op=ALU.add), r=['zb', 'lnb_bc'], w=[xo_nm])
    s.dma('sp', dst_rows, xo[:], r=[xo_nm], w=[])


def emit_ffn(g, li, src, dst):
    nc, s = g.nc, g.s
    TT = 256
    NT = S // TT
    NBT = TT // 128
    with ExitStack() as les:
        L = Ctx()
        L.es = les
        emit_mods(g, L, g.d_f_mod_w[li], g.d_f_mod_b[li], g.d_f_ln_g[li], g.d_f_ln_b[li])
        wup = _sb(nc, les, "wup", [128, 8, 2 * DFF], BF16)
        wdn = _sb(nc, les, "wdn", [128, NFC, D], BF16)
        cw = _sb(nc, les, "cw", [128, 2 * NFC, 4], F32)
        halo = _sb(nc, les, "halo", [128, 2 * NFC, 2], F32)
        hT = [_sb(nc, les, "hT%d" % i, [128, 8, TT], BF16) for i in range(2)]
        gTs = [_sb(nc, les, "gT%d" % i, [128, NFC, TT], BF16) for i in range(2)]
        upre = [_sb(nc, les, "upre%d" % i, [128, TT + 2], F32) for i in range(2)]
        c0 = [_sb(nc, les, "c0%d" % i, [128, TT], F32) for i in range(2)]
        asil = _sb(nc, les, "asil", [128, TT], F32)
        xin = [_sb(nc, les, "xin%d" % i, [128, D], F32) for i in range(2)]
        xrs = [_sb(nc, les, "xrs%d" % i, [128, D], F32) for i in range(2)]
        zb = _sb(nc, les, "zb", [128, D], F32)
        xo = [_sb(nc, les, "xo%d" % i, [128, D], F32) for i in range(2)]
        P = g.ps
        wupd = g.d_f_w_up[li].rearrange("(kc p) n -> p kc n", p=128)
        for kc in range(8):
            for hf in range(2):
                s.dma('pool', wup[:, kc, hf * DFF:(hf + 1) * DFF], wupd[:, kc, hf * DFF:(hf + 1) * DFF], w=['wup'])
        wdnd = g.d_f_w_down[li].rearrange("(fc p) n -> p fc n", p=128)
        for fc in range(NFC):
            s.dma('pool', wdn[:, fc, :], wdnd[:, fc, :], w=['wdn'])
        s.dma('sp', cw[:], g.d_f_cw[li], w=['cw'])
        s.op('dve', lambda e: e.memset(halo[:], 0.0), w=['halo'])
        def phA(t):
            t0 = t * TT
            hs = t % 2
            hnm = 'hT%d' % hs
            for bi in range(NBT):
                xs_ = (t * NBT + bi) % 2
                s.dma('sp', xin[xs_][:], src[t0 + bi * 128:t0 + (bi + 1) * 128, :], w=['xin%d' % xs_])
                emit_hT(g, L, xin[xs_], 'xin%d' % xs_, lambda kc, bi=bi, hs=hs: hT[hs][:, kc, bi * 128:(bi + 1) * 128], hnm, (0, 1))
                yield

        def phU(t):
            hs = t % 2
            hnm = 'hT%d' % hs
            gT = gTs[t % 2]
            gnm = 'gT%d' % (t % 2)
            for j in range(NFC):
                for half in range(2):
                    fc = j + half * NFC
                    b = 2 + ((2 * j + half) % 4)
                    pb = 'ps%d' % b
                    up = upre[half]
                    unm = 'upre%d' % half
                    for kc in range(8):
                        s.op('pe', lambda e, kc=kc, fc=fc, b=b: e.matmul(P[:, b * 512:b * 512 + TT], lhsT=wup[:, kc, fc * 128:(fc + 1) * 128],
                                                                      rhs=hT[hs][:, kc, :], start=(kc == 0), stop=(kc == 7)),
                             r=['wup', hnm], w=[pb], inc=(kc == 7))
                    s.op('pool', lambda e, fc=fc, up=up: e.tensor_copy(out=up[:, 0:2], in_=halo[:, fc, :]), r=['halo'], w=[unm])
                    s.op('act', lambda e, b=b, up=up: e.activation(out=up[:, 2:TT + 2], in_=P[:, b * 512:b * 512 + TT], func=AF.Copy),
                         r=[pb], w=[unm])
                    s.op('act', lambda e, b=b, fc=fc, half=half: e.activation(out=c0[half][:], in_=P[:, b * 512:b * 512 + TT], func=AF.Identity,
                                                                             scale=cw[:, fc, 2:3], bias=cw[:, fc, 3:4]),
                         r=[pb, 'cw'], w=['c0%d' % half])
                    s.op('pool', lambda e, fc=fc, up=up: e.tensor_copy(out=halo[:, fc, :], in_=up[:, TT:TT + 2]), r=[unm], w=['halo'])
                    s.op('dve', lambda e, fc=fc, up=up, half=half: e.scalar_tensor_tensor(out=c0[half][:], in0=up[:, 1:TT + 1], scalar=cw[:, fc, 1:2],
                                                                                        in1=c0[half][:], op0=ALU.mult, op1=ALU.add),
                         r=[unm, 'cw', 'c0%d' % half], w=['c0%d' % half])
                    s.op('dve', lambda e, fc=fc, up=up, half=half: e.scalar_tensor_tensor(out=c0[half][:], in0=up[:, 0:TT], scalar=cw[:, fc, 0:1],
                                                                                        in1=c0[half][:], op0=ALU.mult, op1=ALU.add),
                         r=[unm, 'cw', 'c0%d' % half], w=['c0%d' % half])
                    if half == 0:
                        s.op('act', lambda e: e.activation(out=asil[:], in_=c0[0][:], func=AF.Silu), r=['c00'], w=['asil'])
                    else:
                        s.op('dve', lambda e, j=j, gT=gT: e.tensor_tensor(out=gT[:, j, :], in0=asil[:], in1=c0[1][:], op=ALU.mult),
                             r=['asil', 'c01'], w=[gnm])
                    yield

        def phD(t):
            t0 = t * TT
            gT = gTs[t % 2]
            gnm = 'gT%d' % (t % 2)
            for bi in range(NBT):
                r0 = t0 + bi * 128
                xs_ = (t * NBT + bi) % 2
                s.dma('sp', xrs[xs_][:], src[r0:r0 + 128, :], w=['xrs%d' % xs_])
                for half in range(2):
                    b = 6 + half
                    for j in range(NFC):
                        s.op('pe', lambda e, j=j, half=half, b=b, bi=bi, gT=gT: e.matmul(P[:, b * 512:(b + 1) * 512], lhsT=gT[:, j, bi * 128:(bi + 1) * 128],
                                                                                         rhs=wdn[:, j, half * 512:(half + 1) * 512],
                                                                                         start=(j == 0), stop=(j == NFC - 1)),
                             r=[gnm, 'wdn'], w=['ps%d' % b], inc=(j == NFC - 1))
                    yield
                emit_epilogue(g, L, (6, 7), xrs[xs_], 'xrs%d' % xs_, zb, xo[xs_], 'xo%d' % xs_, dst[r0:r0 + 128, :])
                yield

        for it in range(NT + 2):
            gens = []
            if 1 <= it <= NT:
                gens.append([phU(it - 1), 1])
            if it < NT:
                gens.append([phA(it), 1])
            if it >= 2:
                gens.append([phD(it - 2), 1])
            _interleave(gens)
        s.barrier()


NEG = -1.0e30
GUARD = 1.0e38
NBISECT = 24
TOPK_EXACT_NK = 1024
DSTOP = int(os.environ.get('DSA_STOP', '99'))
DSUB = int(os.environ.get('DSA_SUB', '99'))
GSTOP = int(os.environ.get('GDN_STOP', '99'))
DSC = int(os.environ.get('DSA_SC', '3'))
DSKIP = os.environ.get('DSA_SKIP', '').split(',')
REP = -3.0e38


def emit_dsa(g, src, dst):
    nc, s = g.nc, g.s
    P = g.ps
    with ExitStack() as les:
        L = Ctx()
        L.es = les
        emit_mods(g, L, g.d_e_mod_w[0], g.d_e_mod_b[0], g.d_e_ln_g[0], g.d_e_ln_b[0])
        A = lambda name, shape, dt: _sb(nc, les, name, shape, dt)
        win = A("win", [128, 8, 1536], BF16)
        wif = A("wif", [128, 8, 4], F32)
        wiw = A("wiw", [128, 8, 4], BF16)
        poolw = A("poolw", [128, 4, 128], BF16)
        pscale = A("pscale", [128, 4], F32)
        kvn_bc = A("kvn_bc", [128, 128], F32)
        ukT = A("ukT", [128, 4, 128], BF16)
        uvpad = A("uvpad", [128, 8, 128], BF16)
        wout = A("wout", [128, 8, D], BF16)
        negI4 = A("negI4", [128, 512], BF16)
        corr = A("corr", [128, 4, 15], F32)
        ckvn_all = A("ckvn_all", [128, NB, 128], BF16)
        ckvnT_all = A("ckvnT_all", [128, S], BF16)
        kiT_all = A("kiT_all", [128, S], BF16)
        xin = [A("xin%d" % i, [128, D], F32) for i in range(3)]
        hT = A("hT", [128, 8, 128], BF16)
        ut = A("ut", [128, 4, 143], F32)
        ta = A("ta", [128, 143], F32)
        tb = A("tb", [128, 143], F32)
        dT = A("dT", [128, 4, 128], BF16)
        qT = A("qT", [128, 512], BF16)
        qiT = [A("qiT%d" % i, [128, 256], BF16) for i in range(2)]
        wis = [A("wis%d" % i, [128, 4], F32) for i in range(2)]
        qlT = [A("qlT%d" % i, [128, 1024], BF16) for i in range(3)]
        W = [A("W%d" % i, [128, S + 8], F32) for i in range(2)]
        bs = A("bs", [128, 8], F32)
        cb = A("cb", [128, S], BF16)
        rbuf = [A("rbuf%d" % i, [128, 512], F32) for i in range(2)]
        rb2 = A("rb2", [128, 512], F32)
        notm = [A("notm%d" % i, [128, S], BF16) for i in range(2)]
        m8 = A("m8", [128, 8], F32)
        pT = [A("pT%d" % i, [128, 512], BF16) for i in range(2)]
        rden = A("rden", [128, 512], F32)
        oTn = A("oTn", [128, 1024], BF16)
        yinT = [A("yinT%d" % i, [128, 1024], BF16) for i in range(3)]
        zb = A("zb", [128, D], F32)
        xo = [A("xo%d" % i, [128, D], F32) for i in range(2)]
        sq = A("sq", [128, 128], F32)
        ckf = A("ckf", [128, 128], F32)
        rs = A("rs", [128, 4], F32)
        Pb3 = P[:, 3 * 512:4 * 512].bitcast(BF16)

        wind = g.d_e_w_in[0].rearrange("(kc p) n -> p kc n", p=128)
        for kc in range(8):
            s.dma('pool', win[:, kc, 0:1472], wind[:, kc, 0:1472], w=['win'])
            s.dma('pool', win[:, kc, 1472:1536], wind[:, kc, 1408:1472], w=['win'])
        s.dma('sp', wif[:], wind[:, :, 1472:1476], w=['wif'])
        s.op('dve', lambda e: e.tensor_copy(out=wiw[:], in_=wif[:]), r=['wif'], w=['wiw'])
        s.dma('pool', poolw[:], g.d_e_pool_w[0].rearrange("g c d -> c g d"), w=['poolw'])
        s.dma('sp', pscale[:], g.d_pscale[:, :], w=['pscale'])
        s.dma('sp', kvn_bc[:], g.d_e_kv_norm[0].partition_broadcast(128), w=['kvn_bc'])
        s.dma('pool', ukT[:], g.d_ukT[:, :, :], w=['ukT'])
        s.dma('pool', uvpad[:], g.d_uvpad[:, :, :], w=['uvpad'])
        woutd = g.d_e_w_out[0].rearrange("(kc p) n -> p kc n", p=128)
        for kc in range(8):
            s.dma('pool', wout[:, kc, :], woutd[:, kc, :], w=['wout'])
        s.dma('pool', negI4[:], g.d_negI4[:, :], w=['negI4'])
        s.dma('sp', corr[:], g.d_corr[:, :, :], w=['corr'])
        s.op('dve', lambda e: e.memset(ut[:], 0.0), w=['ut'])

        def front_a(qb):
            sl = qb % 2
            s3 = qb % 3
            t0 = qb * 128
            nk = t0 + 128
            xn = 'xin%d' % s3
            s.dma('sp', xin[s3][:], src[t0:t0 + 128, :], w=[xn])
            emit_hT(g, L, xin[s3], xn, lambda kc: hT[:, kc, :], 'hT', (0, 1))
            yield
            def grp(out_ap, cols, bank, last=True):
                for kc in range(8):
                    s.op('pe', lambda e, kc=kc: e.matmul(out_ap, lhsT=win[:, kc, cols[0]:cols[1]], rhs=hT[:, kc, :],
                                                        start=(kc == 0), stop=(kc == 7)),
                         r=['win', 'hT'], w=['ps%d' % bank], inc=(kc == 7))
            for gi in range(4):
                grp(P[:, gi * 128:(gi + 1) * 128], (gi * 128, (gi + 1) * 128), 0)
                yield
            for j in range(4):
                grp(P[:, 512 + j * 128:512 + (j + 1) * 128], (512 + j * 128, 512 + (j + 1) * 128), 1)
                yield
            for j in range(2):
                grp(P[:, 1024 + j * 128:1024 + (j + 1) * 128], (1152 + j * 128, 1152 + (j + 1) * 128), 2)
                yield
            grp(P[:, 1024 + 256:1024 + 384], (1408, 1536), 2)
            yield
            for kc in range(8):
                s.op('pe', lambda e, kc=kc: e.matmul(P[:, 1536:1536 + 128], lhsT=hT[:, kc, :], rhs=win[:, kc, 1024:1152],
                                                    start=(kc == 0), stop=(kc == 7)), r=['win', 'hT'], w=['ps3'], inc=(kc == 7))
            for kc in range(8):
                s.op('pe', lambda e, kc=kc: e.matmul(P[:, 1536 + 128:1536 + 132], lhsT=hT[:, kc, :], rhs=wiw[:, kc, :],
                                                    start=(kc == 0), stop=(kc == 7)), r=['wiw', 'hT'], w=['ps3'], inc=(kc == 7))
            yield
            s.op('act', lambda e: e.activation(out=ut[:, :, 15:143], in_=P[:, 0:512].rearrange("p (g t) -> p g t", g=4), func=AF.Copy),
                 r=['ps0'], w=['ut'])
            s.op('act', lambda e: e.activation(out=qT[:], in_=P[:, 512:1024], func=AF.Copy), r=['ps1'], w=['qT'])
            s.op('dve', lambda e: e.tensor_copy(out=qiT[sl][:], in_=P[:, 1024:1024 + 256]), r=['ps2'], w=['qiT%d' % sl])
            s.op('dve', lambda e: e.tensor_copy(out=kiT_all[:, t0:t0 + 128], in_=P[:, 1024 + 256:1024 + 384]), r=['ps2'], w=['kiT_all'])
            s.op('dve', lambda e: e.tensor_copy(out=wis[sl][:], in_=P[:, 1536 + 128:1536 + 132]), r=['ps3'], w=['wis%d' % sl])
            yield
            s.op('act', lambda e: e.activation(out=ckf[:], in_=P[:, 1536:1536 + 128], func=AF.Copy), r=['ps3'], w=['ckf'])
            s.op('dve', lambda e: e.tensor_tensor(out=sq[:], in0=ckf[:], in1=ckf[:], op=ALU.mult), r=['ckf'], w=['sq'])
            s.op('dve', lambda e: e.reduce_sum(out=rs[:, 0:1], in_=sq[:], axis=AX.X), r=['sq'], w=['rs0'])
            s.op('dve', lambda e: e.tensor_scalar(out=rs[:, 1:2], in0=rs[:, 0:1], scalar1=1.0 / 128, scalar2=RMS_EPS, op0=ALU.mult, op1=ALU.add),
                 r=['rs0'], w=['rs1'])
            s.op('act', lambda e: e.activation(out=rs[:, 2:3], in_=rs[:, 1:2], func=AF.Sqrt), r=['rs1'], w=['rs2'])
            s.op('dve', lambda e: e.reciprocal(out=rs[:, 3:4], in_=rs[:, 2:3]), r=['rs2'], w=['rs3'])
            s.op('dve', lambda e: e.scalar_tensor_tensor(out=ckf[:], in0=ckf[:], scalar=rs[:, 3:4], in1=kvn_bc[:],
                                                         op0=ALU.mult, op1=ALU.mult), r=['ckf', 'rs3', 'kvn_bc'], w=['ckf'])
            s.op('act', lambda e: e.activation(out=ckvn_all[:, qb, :], in_=ckf[:], func=AF.Copy), r=['ckf'], w=['ckvn_all'])
            s.op('pe', lambda e: e.transpose(P[:, 1536 + 256:1536 + 384], ckf[:], g.ident[:]), r=['ckf', 'ident'], w=['ps3'])
            s.op('act', lambda e: e.activation(out=ckvnT_all[:, t0:t0 + 128], in_=P[:, 1536 + 256:1536 + 384], func=AF.Copy), r=['ps3'], w=['ckvnT_all'])
            yield
            for gi in range(4):
                win_ = 2 << gi
                U = ut[:, gi, :]
                s.op('dve', lambda e, U=U: e.tensor_tensor(out=ta[:, 1:143], in0=U[:, 1:143], in1=U[:, 0:142], op=ALU.add), r=['ut'], w=['ta'])
                sw = ta
                swn = 'ta'
                if gi >= 1:
                    s.op('dve', lambda e: e.tensor_tensor(out=tb[:, 3:143], in0=ta[:, 3:143], in1=ta[:, 1:141], op=ALU.add), r=['ta'], w=['tb'])
                    sw, swn = tb, 'tb'
                if gi >= 2:
                    s.op('dve', lambda e: e.tensor_tensor(out=ta[:, 7:143], in0=tb[:, 7:143], in1=tb[:, 3:139], op=ALU.add), r=['tb'], w=['ta'])
                    sw, swn = ta, 'ta'
                if gi >= 3:
                    s.op('dve', lambda e: e.tensor_tensor(out=tb[:, 15:143], in0=ta[:, 15:143], in1=ta[:, 7:135], op=ALU.add), r=['ta'], w=['tb'])
                    sw, swn = tb, 'tb'
                if qb == 0:
                    s.op('dve', lambda e, sw=sw, gi=gi, win_=win_: e.tensor_tensor(out=sw[:, 15:15 + win_ - 1], in0=sw[:, 15:15 + win_ - 1],
                                                                                 in1=corr[:, gi, 0:win_ - 1], op=ALU.mult),
                         r=[swn, 'corr'], w=[swn])
                s.op('dve', lambda e, sw=sw, gi=gi, win_=win_, U=U: e.scalar_tensor_tensor(out=dT[:, gi, :], in0=sw[:, 15:143], scalar=1.0 / win_,
                                                                                          in1=U[:, 15:143], op0=ALU.mult, op1=ALU.subtract),
                     r=[swn, 'ut'], w=['dT'])
                yield
            s.op('pool', lambda e: e.tensor_copy(out=ut[:, :, 0:15], in_=ut[:, :, 128:143]), r=['ut'], w=['ut'])
            for gi in range(4):
                s.op('pe', lambda e, gi=gi: e.matmul(P[:, gi * 128:(gi + 1) * 128], lhsT=poolw[:, gi, :], rhs=dT[:, gi, :], start=True, stop=True),
                     r=['poolw', 'dT'], w=['ps0'])
            for gi in range(4):
                s.op('act', lambda e, gi=gi: e.activation(out=yinT[s3][:, gi * 128:(gi + 1) * 128], in_=P[:, gi * 128:(gi + 1) * 128],
                                                          func=AF.Identity, scale=pscale[:, gi:gi + 1]),
                     r=['ps0', 'pscale'], w=['yinT%d' % s3])
            yield
            for h in range(8):
                po = (h % 2) * 64
                bank = 1 + h % 2
                off = bank * 512 + (h // 2) * 128
                s.op('pe', lambda e, h=h, po=po, off=off: e.matmul(P[:, off:off + 128], lhsT=ukT[po:po + 64, h // 2, :],
                                                                  rhs=qT[po:po + 64, (h // 2) * 128:(h // 2 + 1) * 128], start=True, stop=True),
                     r=['ukT', 'qT'], w=['ps%d' % bank])
            for j in range(2):
                s.op('act', lambda e, j=j: e.activation(out=qlT[s3][:, j * 512:(j + 1) * 512], in_=P[:, (1 + j) * 512:(2 + j) * 512], func=AF.Copy),
                     r=['ps%d' % (1 + j)], w=['qlT%d' % s3])
            yield
            Wn = 'W%d' % sl
            cnt = 0
            chunks = []
            k0_ = 0
            while k0_ < nk:
                rem = nk - k0_
                w0 = 512 if rem >= 512 else (256 if rem >= 256 else 128)
                chunks.append((k0_, w0))
                k0_ += w0
            for (k0, w_) in chunks:
                for h in range(4):
                    po = (h % 2) * 64
                    bank = 2 + cnt % 2
                    rb = rbuf[cnt % 2]
                    rbn = 'rbuf%d' % (cnt % 2)
                    cnt += 1
                    s.op('pe', lambda e, h=h, po=po, bank=bank, k0=k0, w_=w_: e.matmul(P[:, bank * 512:bank * 512 + w_],
                                                                                     lhsT=qiT[sl][po:po + 64, (h // 2) * 128:(h // 2 + 1) * 128],
                                                                                     rhs=kiT_all[po:po + 64, k0:k0 + w_], start=True, stop=True),
                         r=['qiT%d' % sl, 'kiT_all'], w=['ps%d' % bank])
                    s.op('act', lambda e, bank=bank, rb=rb, w_=w_: e.activation(out=rb[:, 0:w_], in_=P[:, bank * 512:bank * 512 + w_], func=AF.Relu),
                         r=['ps%d' % bank], w=[rbn])
                    if h == 0:
                        s.op('dve', lambda e, rb=rb, k0=k0, w_=w_: e.tensor_scalar(out=W[sl][:, k0:k0 + w_], in0=rb[:, 0:w_], scalar1=wis[sl][:, 0:1],
                                                                                 scalar2=None, op0=ALU.mult), r=[rbn, 'wis%d' % sl], w=[Wn])
                    else:
                        s.op('dve', lambda e, rb=rb, k0=k0, w_=w_, h=h: e.scalar_tensor_tensor(out=W[sl][:, k0:k0 + w_], in0=rb[:, 0:w_], scalar=wis[sl][:, h:h + 1],
                                                                                            in1=W[sl][:, k0:k0 + w_], op0=ALU.mult, op1=ALU.add),
                             r=[rbn, 'wis%d' % sl, Wn], w=[Wn])
                    yield

        def topk(qb):
            sl = qb % 2
            nk = qb * 128 + 128
            Wn = 'W%d' % sl
            nmn = 'notm%d' % sl
            Wt = W[sl]
            if nk <= 256:
                s.op('dve', lambda e: e.memset(notm[sl][:, 0:nk], 0.0), w=[nmn])
            elif nk <= TOPK_EXACT_NK:
                s.op('dve', lambda e: e.memset(Wt[0:64, nk - 64:nk], NEG), r=[Wn], w=[Wn])
                for it in range(32):
                    s.op('dve', lambda e: e.max(out=m8[:], in_=Wt[:, 0:nk]), r=[Wn], w=['m8'])
                    s.op('dve', lambda e: e.match_replace(out=Wt[:, 0:nk], in_to_replace=m8[:], in_values=Wt[:, 0:nk], imm_value=REP),
                         r=[Wn, 'm8'], w=[Wn])
                    yield
                s.op('dve', lambda e: e.tensor_scalar(out=notm[sl][:, 0:nk], in0=Wt[:, 0:nk], scalar1=0.5 * REP, scalar2=None, op0=ALU.is_gt),
                     r=[Wn], w=[nmn])
            else:
                s.op('dve', lambda e: e.tensor_reduce(out=bs[:, 7:8], in_=Wt[:, 0:nk], axis=AX.X, op=ALU.max, apply_absolute_value=True),
                     r=[Wn], w=['bs7'])
                s.op('dve', lambda e: e.memset(Wt[0:64, nk - 64:nk], NEG), r=[Wn], w=[Wn])
                s.op('dve', lambda e: e.tensor_scalar(out=bs[:, 0:1], in0=bs[:, 7:8], scalar1=-1.001, scalar2=None, op0=ALU.mult), r=['bs7'], w=['bs0'])
                s.op('dve', lambda e: e.tensor_scalar(out=bs[:, 1:2], in0=bs[:, 7:8], scalar1=1.0005, scalar2=None, op0=ALU.mult), r=['bs7'], w=['bs1'])
                s.op('dve', lambda e: e.tensor_scalar(out=bs[:, 2:3], in0=bs[:, 0:1], scalar1=bs[:, 1:2], scalar2=-1.0, op0=ALU.add, op1=ALU.mult),
                     r=['bs0', 'bs1'], w=['bs2'])
                yield
                for it in range(NBISECT):
                    s.op('act', lambda e: e.activation(out=notm[sl][:, 0:nk], in_=Wt[:, 0:nk], func=AF.Sign, bias=bs[:, 2:3], scale=1.0, accum_out=bs[:, 3:4]),
                         r=[Wn, 'bs2'], w=[nmn, 'bs3'])
                    s.op('dve', lambda e: e.tensor_scalar(out=bs[:, 4:5], in0=bs[:, 3:4], scalar1=float(512 - nk), scalar2=None, op0=ALU.is_ge), r=['bs3'], w=['bs4'])
                    s.op('dve', lambda e: e.scalar_tensor_tensor(out=bs[:, 0:1], in0=bs[:, 4:5], scalar=bs[:, 1:2], in1=bs[:, 0:1], op0=ALU.mult, op1=ALU.add),
                         r=['bs4', 'bs1', 'bs0'], w=['bs0'])
                    s.op('dve', lambda e: e.tensor_scalar(out=bs[:, 1:2], in0=bs[:, 1:2], scalar1=0.5, scalar2=None, op0=ALU.mult), r=['bs1', 'bs0'], w=['bs1'])
                    s.op('dve', lambda e: e.tensor_scalar(out=bs[:, 2:3], in0=bs[:, 0:1], scalar1=bs[:, 1:2], scalar2=-1.0, op0=ALU.add, op1=ALU.mult),
                         r=['bs0', 'bs1'], w=['bs2'])
                    yield
                s.op('dve', lambda e: e.scalar_tensor_tensor(out=bs[:, 7:8], in0=bs[:, 1:2], scalar=2.0, in1=bs[:, 0:1], op0=ALU.mult, op1=ALU.add),
                     r=['bs1', 'bs0'], w=['bs7'])
                s.op('dve', lambda e: e.tensor_scalar(out=bs[:, 6:7], in0=bs[:, 7:8], scalar1=-1.0, scalar2=None, op0=ALU.mult), r=['bs7'], w=['bs6'])
                s.op('act', lambda e: e.activation(out=notm[sl][:, 0:nk], in_=Wt[:, 0:nk], func=AF.Sign, bias=bs[:, 6:7], scale=1.0, accum_out=bs[:, 3:4]),
                     r=[Wn, 'bs6'], w=[nmn, 'bs3'])
                s.op('dve', lambda e: e.tensor_scalar(out=bs[:, 5:6], in0=bs[:, 3:4], scalar1=-0.5, scalar2=float(256 - nk // 2), op0=ALU.mult, op1=ALU.add),
                     r=['bs3'], w=['bs5'])
                yield
                s.op('dve', lambda e: e.tensor_scalar(out=notm[sl][:, 0:nk], in0=Wt[:, 0:nk], scalar1=bs[:, 7:8], scalar2=None, op0=ALU.is_gt),
                     r=[Wn, 'bs7'], w=[nmn])
                s.op('dve', lambda e: e.tensor_scalar(out=cb[:, 0:nk], in0=Wt[:, 0:nk], scalar1=bs[:, 0:1], scalar2=None, op0=ALU.is_gt),
                     r=[Wn, 'bs0'], w=['cb'])
                yield
                s.op('dve', lambda e: e.scalar_tensor_tensor(out=cb[:, 0:nk], in0=Wt[:, 0:nk], scalar=bs[:, 7:8], in1=cb[:, 0:nk], op0=ALU.is_le, op1=ALU.mult),
                     r=[Wn, 'bs7', 'cb'], w=['cb'])
                yield
                s.op('dve', lambda e: e.tensor_tensor_scan(out=Wt[:, 0:nk], data0=g.onesb[:, 0:1].to_broadcast([128, nk]), data1=cb[:, 0:nk], initial=0.0, op0=ALU.mult, op1=ALU.add),
                     r=['onesb', 'cb', Wn], w=[Wn])
                yield
                s.op('dve', lambda e: e.scalar_tensor_tensor(out=cb[:, 0:nk], in0=Wt[:, 0:nk], scalar=bs[:, 5:6], in1=cb[:, 0:nk], op0=ALU.is_le, op1=ALU.mult),
                     r=[Wn, 'bs5', 'cb'], w=['cb'])
                yield
                s.op('dve', lambda e: e.tensor_tensor(out=notm[sl][:, 0:nk], in0=notm[sl][:, 0:nk], in1=cb[:, 0:nk], op=ALU.add), r=[nmn, 'cb'], w=[nmn])
                s.op('dve', lambda e: e.tensor_scalar(out=notm[sl][:, 0:nk], in0=notm[sl][:, 0:nk], scalar1=-1.0, scalar2=1.0, op0=ALU.mult, op1=ALU.add),
                     r=[nmn], w=[nmn])
            s.op('dve', lambda e: e.memset(notm[sl][0:64, nk - 64:nk], 1.0), r=[nmn], w=[nmn])
            yield

        def back(qb):
            sl = qb % 2
            s3 = qb % 3
            t0 = qb * 128
            cnt = 0
            for hg in range(2):
                for kb in range(qb + 1):
                    j = cnt % 2
                    cnt += 1
                    bank = 4 + j
                    s.op('pe', lambda e, bank=bank, kb=kb, hg=hg: e.matmul(P[:, bank * 512:(bank + 1) * 512], lhsT=ckvnT_all[:, kb * 128:(kb + 1) * 128],
                                                                          rhs=qlT[s3][:, hg * 512:(hg + 1) * 512], start=True, stop=False),
                         r=['ckvnT_all', 'qlT%d' % s3], w=['ps%d' % bank], inc=False)
                    s.op('pe', lambda e, bank=bank, kb=kb: e.matmul(P[:, bank * 512:(bank + 1) * 512], lhsT=notm[sl][:, kb * 128:(kb + 1) * 128],
                                                                   rhs=negI4[:], start=False, stop=True),
                         r=['notm%d' % sl, 'negI4'], w=['ps%d' % bank])
                    s.op('act', lambda e, bank=bank, j=j: e.activation(out=pT[j][:], in_=P[:, bank * 512:(bank + 1) * 512], func=AF.Exp, scale=0.125),
                         r=['ps%d' % bank], w=['pT%d' % j])
                    s.op('pe', lambda e, kb=kb, j=j: e.matmul(P[:, 6 * 512:7 * 512], lhsT=ckvn_all[:, kb, :], rhs=pT[j][:], start=(kb == 0), stop=(kb == qb)),
                         r=['ckvn_all', 'pT%d' % j], w=['ps6'], inc=False)
                    s.op('pe', lambda e, kb=kb, j=j: e.matmul(P[:, 7 * 512:8 * 512], lhsT=g.onesb[:], rhs=pT[j][:], start=(kb == 0), stop=(kb == qb)),
                         r=['onesb', 'pT%d' % j], w=['ps7'])
                    yield
                s.op('dve', lambda e: e.reciprocal(out=rden[:], in_=P[:, 7 * 512:8 * 512]), r=['ps7'], w=['rden'])
                s.op('dve', lambda e, hg=hg: e.tensor_tensor(out=oTn[:, hg * 512:(hg + 1) * 512], in0=P[:, 6 * 512:7 * 512], in1=rden[:], op=ALU.mult),
                     r=['ps6', 'rden'], w=['oTn'])
                yield
            for hp in range(4):
                for h2 in range(2):
                    h = 2 * hp + h2
                    s.op('pe', lambda e, hp=hp, h=h, h2=h2: e.matmul(P[:, 7 * 512 + hp * 128:7 * 512 + (hp + 1) * 128], lhsT=uvpad[:, h, :],
                                                                    rhs=oTn[:, (h // 2 + 4 * (h % 2)) * 128:(h // 2 + 4 * (h % 2) + 1) * 128], start=(h2 == 0), stop=(h2 == 1)),
                         r=['uvpad', 'oTn'], w=['ps7'], inc=(h2 == 1))
            s.op('act', lambda e: e.activation(out=yinT[s3][:, 512:1024], in_=P[:, 7 * 512:8 * 512], func=AF.Copy), r=['ps7'], w=['yinT%d' % s3])
            yield
            for half in range(2):
                b = 4 + half
                for kc in range(8):
                    s.op('pe', lambda e, kc=kc, half=half, b=b: e.matmul(P[:, b * 512:(b + 1) * 512], lhsT=yinT[s3][:, kc * 128:(kc + 1) * 128],
                                                                        rhs=wout[:, kc, half * 512:(half + 1) * 512], start=(kc == 0), stop=(kc == 7)),
                         r=['yinT%d' % s3, 'wout'], w=['ps%d' % b], inc=(kc == 7))
            yield
            emit_epilogue(g, L, (4, 5), xin[s3], 'xin%d' % s3, zb, xo[sl], 'xo%d' % sl, dst[t0:t0 + 128, :])
            yield

        nblk = g.nblk_dsa
        for it in range(nblk + 2):
            gens = []
            if it < nblk:
                gens.append([front_a(it), 1])
            if 1 <= it <= nblk:
                gens.append([topk(it - 1), 1])
            if it >= 2:
                gens.append([back(it - 2), 1])
            _interleave(gens)
        s.barrier()


def emit_gdn(g, src, dst):
    nc, s = g.nc, g.s
    P = g.ps
    with ExitStack() as les:
        L = Ctx()
        L.es = les
        emit_mods(g, L, g.d_o_mod_w[0], g.d_o_mod_b[0], g.d_o_ln_g[0], g.d_o_ln_b[0])
        A = lambda name, shape, dt: _sb(nc, les, name, shape, dt)
        win = A("gwin", [128, 8, 4096], BF16)
        wbf = A("gwbf", [128, 8, 16], F32)
        wbb = A("gwbb", [128, 8, 16], BF16)
        wout = A("gwout", [128, 8, D], BF16)
        cw = A("gcw", [128, 24, 4], F32)
        msl = A("msl", [128, 128], F32)
        mil = A("mil", [128, 128], F32)
        triu = A("triu", [128, 128], F32)
        mdm = A("mdm", [128, 5, 128], BF16)
        mdmT = A("mdmT", [128, 5, 128], BF16)
        alog = A("alog", [128, 8], F32)
        dtb = A("dtb", [128, 8], F32)
        onb = A("onb", [128, D], F32)
        xin = [A("gxin%d" % i, [128, D], F32) for i in range(1)]
        xrs = A("gxrs", [128, D], F32)
        hT = A("ghT", [128, 8, 128], BF16)
        xw = [A("xw%d" % i, [128, 131], F32) for i in range(2)]
        halo = A("ghalo", [128, 24, 3], F32)
        cbuf = [A("cbuf%d" % i, [128, 128], F32) for i in range(2)]
        act = [A("gact%d" % i, [128, 24 * 128], F32) for i in range(2)]
        sq = A("gsq", [128, 1024], F32)
        rstd = sq
        qkb = [A("qkb%d" % i, [128, 2048], BF16) for i in range(2)]
        gsil = [A("gsil%d" % i, [128, D], BF16) for i in range(3)]
        ba = A("ba", [128, 16], F32)
        tmpa = A("tmpa", [128, 8], F32)
        smb = [A("smb%d" % i, [128, 48], F32) for i in range(2)]
        zsm = A("zsm", [128, 8], F32)
        HP = []
        for p in range(2):
            h_ = Ctx()
            h_.sm = A("hsm%d" % p, [128, 2], F32)
            for nm in ("E", "EB", "t1", "Nf", "atf"):
                setattr(h_, nm, A("h%s%d" % (nm, p), [128, 128], F32))
            for nm in ("X", "Xt", "Q2", "Q2t", "Q4", "Q4t", "Yv", "Yt", "attT", "qdT", "kd", "Kbg", "Vb", "nwT", "vnew"):
                setattr(h_, nm, A("h%s%d" % (nm, p), [128, 128], BF16))
            h_.NM = A("hNM%d" % p, [128, 5, 128], BF16)
            h_.NMt = A("hNMt%d" % p, [128, 5, 128], BF16)
            HP.append(h_)
        Sf = A("gSf", [128, 8, 128], F32)
        Sb = A("gSb", [128, 8, 128], BF16)
        osb = [A("gosb%d" % i, [128, D], F32) for i in range(2)]
        yinT = A("gyinT", [128, 1024], BF16)
        zb = A("gzb", [128, D], F32)
        xo = [A("gxo%d" % i, [128, D], F32) for i in range(1)]
        wind = g.d_o_w_in[0].rearrange("(kc p) n -> p kc n", p=128)
        for kc in range(8):
            for q4 in range(2):
                s.dma('pool', win[:, kc, q4 * 2048:(q4 + 1) * 2048], wind[:, kc, q4 * 2048:(q4 + 1) * 2048], w=['gwin'])
        s.dma('sp', wbf[:], wind[:, :, 4096:4112], w=['gwbf'])
        s.op('dve', lambda e: e.tensor_copy(out=wbb[:], in_=wbf[:]), r=['gwbf'], w=['gwbb'])
        woutd = g.d_o_w_out[0].rearrange("(kc p) n -> p kc n", p=128)
        for kc in range(8):
            s.dma('pool', wout[:, kc, :], woutd[:, kc, :], w=['gwout'])
        s.dma('sp', cw[:], g.d_o_cw[:, :, :], w=['gcw'])
        s.dma('sp', msl[:], g.d_msl[:, :], w=['msl'])
        s.dma('sp', mil[:], g.d_mil[:, :], w=['mil'])
        s.dma('sp', triu[:], g.d_triu[:, :], w=['triu'])
        s.dma('pool', mdm[:], g.d_mdm[:, :, :], w=['mdm'])
        s.dma('pool', mdmT[:], g.d_mdmT[:, :, :], w=['mdmT'])
        s.dma('sp', alog[:], g.d_o_a_log[0].partition_broadcast(128), w=['alog'])
        s.dma('sp', dtb[:], g.d_o_dt_bias[0].partition_broadcast(128), w=['dtb'])
        s.dma('sp', onb[:], g.d_onb[0].partition_broadcast(128), w=['onb'])
        s.op('act', lambda e: e.activation(out=alog[:], in_=alog[:], func=AF.Exp), r=['alog'], w=['alog'])
        s.op('dve', lambda e: e.tensor_scalar(out=alog[:], in0=alog[:], scalar1=-1.0, scalar2=None, op0=ALU.mult), r=['alog'], w=['alog'])
        s.op('dve', lambda e: e.memset(halo[:], 0.0), w=['ghalo'])
        s.op('dve', lambda e: e.memset(Sf[:], 0.0), w=['gSf0', 'gSf1'])
        s.op('dve', lambda e: e.memset(Sb[:], 0.0), w=['gSb0', 'gSb1'])
        DK = float(128 ** -0.5)

        def phaseA(qb):
            t0 = qb * 128
            bp = qb % 2
            x3 = qb % 3
            xn = 'gxin0'
            an = 'gact%d' % bp
            sn = 'smb%d' % bp
            sm = smb[bp]
            ac = act[bp]
            s.dma('sp', xin[0][:], src[t0:t0 + 128, :], w=[xn])
            emit_hT(g, L, xin[0], xn, lambda kc: hT[:, kc, :], 'ghT', (0, 1))
            yield
            for ch in range(24):
                b = ch % 2
                cb = cbuf[b]
                cn = 'cbuf%d' % b
                for kc in range(8):
                    s.op('pe', lambda e, kc=kc, ch=ch, b=b: e.matmul(P[:, b * 512:b * 512 + 128], lhsT=win[:, kc, ch * 128:(ch + 1) * 128], rhs=hT[:, kc, :],
                                                                    start=(kc == 0), stop=(kc == 7)), r=['gwin', 'ghT'], w=['ps%d' % b], inc=(kc == 7))
                xb = xw[b]
                xbn = 'xw%d' % b
                s.op('pool', lambda e, ch=ch, xb=xb: e.tensor_copy(out=xb[:, 0:3], in_=halo[:, ch, :]), r=['ghalo'], w=[xbn])
                s.op('act', lambda e, b=b, xb=xb: e.activation(out=xb[:, 3:131], in_=P[:, b * 512:b * 512 + 128], func=AF.Copy), r=['ps%d' % b], w=[xbn])
                s.op('act', lambda e, ch=ch, b=b, cb=cb: e.activation(out=cb[:], in_=P[:, b * 512:b * 512 + 128], func=AF.Identity, scale=cw[:, ch, 3:4]),
                     r=['ps%d' % b, 'gcw'], w=[cn])
                s.op('pool', lambda e, ch=ch, xb=xb: e.tensor_copy(out=halo[:, ch, :], in_=xb[:, 128:131]), r=[xbn], w=['ghalo'])
                for j in range(3):
                    s.op('dve', lambda e, ch=ch, j=j, cb=cb, xb=xb: e.scalar_tensor_tensor(out=cb[:], in0=xb[:, j:j + 128], scalar=cw[:, ch, j:j + 1], in1=cb[:],
                                                                                          op0=ALU.mult, op1=ALU.add), r=[xbn, 'gcw', cn], w=[cn])
                s.op('act', lambda e, ch=ch, cb=cb: e.activation(out=ac[:, ch * 128:(ch + 1) * 128], in_=cb[:], func=AF.Silu), r=[cn], w=[an])
                yield
            for hf in range(2):
                seg = ac[:, hf * 1024:(hf + 1) * 1024]
                s.op('dve', lambda e, seg=seg: e.tensor_tensor(out=sq[:], in0=seg, in1=seg, op=ALU.mult), r=[an], w=['gsq'])
                yield
                for j in range(2):
                    b = j % 2
                    s.op('pe', lambda e, j=j, b=b: e.matmul(P[:, b * 512:(b + 1) * 512], lhsT=g.ones[:], rhs=sq[:, j * 512:(j + 1) * 512], start=True, stop=True),
                         r=['ones', 'gsq'], w=['ps%d' % b])
                for j in range(2):
                    b = j % 2
                    s.op('dve', lambda e, j=j, b=b: e.tensor_scalar(out=sq[:, j * 512:(j + 1) * 512], in0=P[:, b * 512:(b + 1) * 512], scalar1=RMS_EPS, scalar2=None, op0=ALU.add),
                         r=['ps%d' % b], w=['gsq'])
                yield
                s.op('act', lambda e: e.activation(out=sq[:], in_=sq[:], func=AF.Sqrt), r=['gsq'], w=['gsq'])
                s.op('dve', lambda e: e.reciprocal(out=sq[:], in_=sq[:]), r=['gsq'], w=['gsq'])
                yield
                s.op('dve', lambda e, seg=seg: e.tensor_tensor(out=seg, in0=seg, in1=sq[:], op=ALU.mult), r=[an, 'gsq'], w=[an])
                s.op('act', lambda e, seg=seg, hf=hf: e.activation(out=qkb[bp][:, hf * 1024:(hf + 1) * 1024], in_=seg, func=AF.Copy), r=[an], w=['qkb%d' % bp])
                yield
            for half in range(2):
                b = half
                for kc in range(8):
                    s.op('pe', lambda e, kc=kc, half=half, b=b: e.matmul(P[:, b * 512:(b + 1) * 512], lhsT=hT[:, kc, :],
                                                                        rhs=win[:, kc, 3072 + half * 512:3072 + (half + 1) * 512],
                                                                        start=(kc == 0), stop=(kc == 7)), r=['gwin', 'ghT'], w=['ps%d' % b], inc=(kc == 7))
                s.op('act', lambda e, half=half, b=b: e.activation(out=gsil[x3][:, half * 512:(half + 1) * 512], in_=P[:, b * 512:(b + 1) * 512], func=AF.Silu),
                     r=['ps%d' % b], w=['gsil%d' % x3])
                yield
            for kc in range(8):
                s.op('pe', lambda e, kc=kc: e.matmul(P[:, 0:16], lhsT=hT[:, kc, :], rhs=wbb[:, kc, :], start=(kc == 0), stop=(kc == 7)),
                     r=['gwbb', 'ghT'], w=['ps0'], inc=(kc == 7))
            s.op('dve', lambda e: e.tensor_copy(out=ba[:], in_=P[:, 0:16]), r=['ps0'], w=['ba'])
            yield
            s.op('act', lambda e: e.activation(out=sm[:, 0:8], in_=ba[:, 0:8], func=AF.Exp, scale=-1.0), r=['ba'], w=[sn])
            s.op('dve', lambda e: e.tensor_scalar(out=sm[:, 0:8], in0=sm[:, 0:8], scalar1=1.0, scalar2=None, op0=ALU.add), r=[sn], w=[sn])
            s.op('dve', lambda e: e.reciprocal(out=sm[:, 0:8], in_=sm[:, 0:8]), r=[sn], w=[sn])
            s.op('dve', lambda e: e.tensor_scalar(out=sm[:, 32:40], in0=sm[:, 0:8], scalar1=-1.0, scalar2=None, op0=ALU.mult), r=[sn], w=[sn])
            yield
            s.op('dve', lambda e: e.tensor_tensor(out=tmpa[:], in0=ba[:, 8:16], in1=dtb[:], op=ALU.add), r=['ba', 'dtb'], w=['tmpa'])
            s.op('act', lambda e: e.activation(out=tmpa[:], in_=tmpa[:], func=AF.Exp), r=['tmpa'], w=['tmpa'])
            s.op('act', lambda e: e.activation(out=tmpa[:], in_=tmpa[:], func=AF.Ln, bias=1.0), r=['tmpa'], w=['tmpa'])
            s.op('dve', lambda e: e.tensor_tensor(out=sm[:, 8:16], in0=tmpa[:], in1=alog[:], op=ALU.mult), r=['tmpa', 'alog', sn], w=[sn])
            yield
            s.op('pe', lambda e: e.matmul(P[:, 16:24], lhsT=triu[:], rhs=sm[:, 8:16], start=True, stop=True), r=['triu', sn], w=['ps0'])
            s.op('dve', lambda e: e.tensor_copy(out=sm[:, 16:24], in_=P[:, 16:24]), r=['ps0', sn], w=[sn])
            s.op('act', lambda e: e.activation(out=sm[:, 24:32], in_=sm[:, 16:24], func=AF.Exp), r=[sn], w=[sn])
            s.op('dve', lambda e: e.tensor_tensor(out=sm[:, 40:48], in0=sm[:, 24:32], in1=sm[:, 0:8], op=ALU.mult), r=[sn], w=[sn])
            yield

        def heads(qb, p):
            bp = qb % 2
            sm = smb[bp]
            sn = 'smb%d' % bp
            an = 'gact%d' % bp
            ac = act[bp]
            H = HP[p]
            X0 = (2 + 3 * p) * 512
            Y0 = X0 + 512
            Z0 = X0 + 1024
            xn_, yn_, zn_ = 'ps%d' % (2 + 3 * p), 'ps%d' % (3 + 3 * p), 'ps%d' % (4 + 3 * p)
            n = lambda nm: 'h%s%d' % (nm, p)
            for h in range(p, 8, 2):
                qn = ac[:, h * 128:(h + 1) * 128]
                kn = ac[:, (8 + h) * 128:(9 + h) * 128]
                vv = ac[:, (16 + h) * 128:(17 + h) * 128]
                qnb = qkb[bp][:, h * 128:(h + 1) * 128]
                knb = qkb[bp][:, (8 + h) * 128:(9 + h) * 128]
                qbn = 'qkb%d' % bp
                s.op('dve', lambda e, h=h: e.tensor_scalar(out=H.t1[:], in0=g.ident[:], scalar1=sm[:, 16 + h:17 + h], scalar2=None, op0=ALU.mult),
                     r=['ident', sn], w=[n('t1')])
                s.op('pe', lambda e: e.matmul(P[:, X0:X0 + 128], lhsT=g.ones[:], rhs=H.t1[:], start=True, stop=True), r=['ones', n('t1')], w=[xn_])
                s.op('dve', lambda e, h=h: e.tensor_scalar(out=H.E[:], in0=P[:, X0:X0 + 128], scalar1=sm[:, 16 + h:17 + h], scalar2=0.0, op0=ALU.subtract, op1=ALU.max),
                     r=[xn_, sn], w=[n('E')])
                s.op('act', lambda e: e.activation(out=H.E[:], in_=H.E[:], func=AF.Exp, scale=-1.0), r=[n('E')], w=[n('E')])
                s.op('act', lambda e: e.activation(out=H.EB[:], in_=P[:, X0:X0 + 128], func=AF.Exp), r=[xn_], w=[n('EB')])
                s.op('act', lambda e: e.activation(out=H.sm[:, 0:1], in_=P[:, X0 + 127:X0 + 128], func=AF.Exp), r=[xn_], w=[n('sm0')])
                s.op('dve', lambda e, h=h: e.tensor_scalar(out=H.sm[:, 1:2], in0=P[:, X0 + 127:X0 + 128], scalar1=sm[:, 16 + h:17 + h], scalar2=None, op0=ALU.subtract),
                     r=[xn_, sn], w=[n('sm1')])
                s.op('act', lambda e: e.activation(out=H.sm[:, 1:2], in_=H.sm[:, 1:2], func=AF.Exp), r=[n('sm1')], w=[n('sm1')])
                yield
                s.op('pe', lambda e, knb=knb: e.matmul(P[:, X0 + 128:X0 + 256], lhsT=knb, rhs=knb, start=True, stop=True), r=[qbn], w=[xn_])
                s.op('pe', lambda e, knb=knb, qnb=qnb: e.matmul(P[:, X0 + 256:X0 + 384], lhsT=qnb, rhs=knb, start=True, stop=True), r=[qbn], w=[xn_])
                s.op('dve', lambda e: e.tensor_tensor(out=H.t1[:], in0=H.E[:], in1=msl[:], op=ALU.mult), r=[n('E'), 'msl'], w=[n('t1')])
                s.op('dve', lambda e, h=h: e.scalar_tensor_tensor(out=H.Nf[:], in0=P[:, X0 + 128:X0 + 256], scalar=sm[:, 32 + h:33 + h], in1=H.t1[:], op0=ALU.mult, op1=ALU.mult),
                     r=[xn_, sn, n('t1')], w=[n('Nf')])
                s.op('dve', lambda e: e.tensor_tensor(out=H.t1[:], in0=H.E[:], in1=mil[:], op=ALU.mult), r=[n('E'), 'mil', n('Nf')], w=[n('t1')])
                s.op('dve', lambda e: e.scalar_tensor_tensor(out=H.atf[:], in0=P[:, X0 + 256:X0 + 384], scalar=DK, in1=H.t1[:], op0=ALU.mult, op1=ALU.mult),
                     r=[xn_, n('t1')], w=[n('atf')])
                yield
                s.op('pe', lambda e: e.transpose(P[:, Y0:Y0 + 128], H.Nf[:], g.ident[:]), r=[n('Nf'), 'ident'], w=[yn_])
                s.op('pe', lambda e: e.transpose(P[:, Y0 + 128:Y0 + 256], H.atf[:], g.ident[:]), r=[n('atf'), 'ident'], w=[yn_])
                s.op('pe', lambda e, kn=kn: e.transpose(P[:, Y0 + 256:Y0 + 384], kn, g.ident[:]), r=[an, 'ident'], w=[yn_])
                s.op('pe', lambda e, vv=vv: e.transpose(P[:, Y0 + 384:Y0 + 512], vv, g.ident[:]), r=[an, 'ident'], w=[yn_])
                s.op('dve', lambda e: e.tensor_tensor(out=H.NM[:], in0=H.Nf[:].unsqueeze(1).to_broadcast([128, 5, 128]), in1=mdm[:], op=ALU.mult),
                     r=[n('Nf'), 'mdm'], w=[n('NM')])
                s.op('dve', lambda e: e.tensor_tensor(out=H.NMt[:], in0=P[:, Y0:Y0 + 128].unsqueeze(1).to_broadcast([128, 5, 128]), in1=mdmT[:], op=ALU.mult),
                     r=[yn_, 'mdmT'], w=[n('NMt')])
                s.op('act', lambda e: e.activation(out=H.attT[:], in_=P[:, Y0 + 128:Y0 + 256], func=AF.Copy), r=[yn_], w=[n('attT')])
                s.op('dve', lambda e: e.tensor_scalar(out=H.kd[:], in0=P[:, Y0 + 256:Y0 + 384], scalar1=H.sm[:, 1:2], scalar2=None, op0=ALU.mult), r=[yn_, n('sm1')], w=[n('kd')])
                s.op('dve', lambda e, h=h: e.tensor_scalar(out=H.Kbg[:], in0=P[:, Y0 + 256:Y0 + 384], scalar1=sm[:, 40 + h:41 + h], scalar2=None, op0=ALU.mult),
                     r=[yn_, sn], w=[n('Kbg')])
                s.op('dve', lambda e, h=h: e.tensor_scalar(out=H.Vb[:], in0=P[:, Y0 + 384:Y0 + 512], scalar1=sm[:, h:h + 1], scalar2=None, op0=ALU.mult), r=[yn_, sn], w=[n('Vb')])
                s.op('dve', lambda e, qn=qn: e.scalar_tensor_tensor(out=H.qdT[:], in0=qn, scalar=DK, in1=H.EB[:], op0=ALU.mult, op1=ALU.mult),
                     r=[an, n('EB')], w=[n('qdT')])
                yield
                zc = [0]

                def mm(lhsT, rhs, rn):
                    c0 = Z0 + (zc[0] % 4) * 128
                    zc[0] += 1
                    s.op('pe', lambda e: e.matmul(P[:, c0:c0 + 128], lhsT=lhsT, rhs=rhs, start=True, stop=True), r=rn, w=[zn_])
                    return P[:, c0:c0 + 128]

                def cp(dst, dn, src_ps):
                    s.op('act', lambda e: e.activation(out=dst, in_=src_ps, func=AF.Copy), r=[zn_], w=[dn])

                def acc(dst, dn, src_ps):
                    s.op('dve', lambda e: e.tensor_tensor(out=dst, in0=src_ps, in1=dst, op=ALU.add), r=[zn_, dn], w=[dn])

                M0, M0t = H.NM[:, 0, :], H.NMt[:, 0, :]
                s.op('dve', lambda e: e.tensor_tensor(out=H.X[:], in0=M0, in1=g.identb[:], op=ALU.add), r=[n('NM'), 'identb'], w=[n('X')])
                s.op('dve', lambda e: e.tensor_tensor(out=H.Xt[:], in0=M0t, in1=g.identb[:], op=ALU.add), r=[n('NMt'), 'identb'], w=[n('Xt')])
                cp(H.Q2[:], n('Q2'), mm(M0t, M0, [n('NM'), n('NMt')]))
                cp(H.Q2t[:], n('Q2t'), mm(M0, M0t, [n('NM'), n('NMt')]))
                yield
                acc(H.X[:], n('X'), mm(H.Q2t[:], H.X[:], [n('Q2t'), n('X')]))
                acc(H.Xt[:], n('Xt'), mm(H.Q2[:], H.Xt[:], [n('Q2'), n('Xt')]))
                cp(H.Q4[:], n('Q4'), mm(H.Q2t[:], H.Q2[:], [n('Q2'), n('Q2t')]))
                cp(H.Q4t[:], n('Q4t'), mm(H.Q2[:], H.Q2t[:], [n('Q2'), n('Q2t')]))
                yield
                acc(H.X[:], n('X'), mm(H.Q4t[:], H.X[:], [n('Q4t'), n('X')]))
                acc(H.Xt[:], n('Xt'), mm(H.Q4[:], H.Xt[:], [n('Q4'), n('Xt')]))
                yield
                for lv in range(1, 5):
                    Nb, Nbt = H.NM[:, lv, :], H.NMt[:, lv, :]
                    if lv < 4:
                        cp(H.Yv[:], n('Yv'), mm(Nbt, H.X[:], [n('NMt'), n('X')]))
                    cp(H.Yt[:], n('Yt'), mm(Nb, H.Xt[:], [n('NM'), n('Xt')]))
                    yield
                    pa = mm(H.Xt[:], H.Yv[:], [n('Xt'), n('Yv')]) if lv < 4 else None
                    pb = mm(H.X[:], H.Yt[:], [n('X'), n('Yt')])
                    if lv < 4:
                        acc(H.X[:], n('X'), pa)
                    acc(H.Xt[:], n('Xt'), pb)
                    yield
                pw = mm(H.Kbg[:], H.Xt[:], [n('Kbg'), n('Xt')])
                s.op('act', lambda e: e.activation(out=H.nwT[:], in_=pw, func=AF.Copy, scale=-1.0), r=[zn_], w=[n('nwT')])
                yield
                s.op('pe', lambda e: e.matmul(P[:, Y0:Y0 + 128], lhsT=H.Xt[:], rhs=H.Vb[:], start=True, stop=False), r=[n('Xt'), n('Vb')], w=[yn_], inc=False)
                s.op('pe', lambda e, h=h: e.matmul(P[:, Y0:Y0 + 128], lhsT=H.nwT[:], rhs=Sb[:, h, :], start=False, stop=True), r=[n('nwT'), 'gSb%d' % p], w=[yn_])
                s.op('act', lambda e: e.activation(out=H.vnew[:], in_=P[:, Y0:Y0 + 128], func=AF.Copy), r=[yn_], w=[n('vnew')])
                yield
                s.op('pe', lambda e, h=h: e.matmul(P[:, X0 + 384:X0 + 512], lhsT=H.qdT[:], rhs=Sb[:, h, :], start=True, stop=False),
                     r=[n('qdT'), 'gSb%d' % p], w=[xn_], inc=False)
                s.op('pe', lambda e: e.matmul(P[:, X0 + 384:X0 + 512], lhsT=H.attT[:], rhs=H.vnew[:], start=False, stop=True),
                     r=[n('attT'), n('vnew')], w=[xn_])
                s.op('act', lambda e, h=h: e.activation(out=osb[bp][:, h * 128:(h + 1) * 128], in_=P[:, X0 + 384:X0 + 512], func=AF.Copy),
                     r=[xn_], w=['gosb%d_%d' % (bp, p)])
                s.op('pe', lambda e: e.matmul(P[:, Y0 + 128:Y0 + 256], lhsT=H.kd[:], rhs=H.vnew[:], start=True, stop=True), r=[n('kd'), n('vnew')], w=[yn_])
                s.op('dve', lambda e, h=h: e.scalar_tensor_tensor(out=Sf[:, h, :], in0=Sf[:, h, :], scalar=H.sm[:, 0:1], in1=P[:, Y0 + 128:Y0 + 256], op0=ALU.mult, op1=ALU.add),
                     r=['gSf%d' % p, n('sm0'), yn_], w=['gSf%d' % p])
                s.op('act', lambda e, h=h: e.activation(out=Sb[:, h, :], in_=Sf[:, h, :], func=AF.Copy), r=['gSf%d' % p], w=['gSb%d' % p])
                yield

        def phaseZ(qb):
            t0 = qb * 128
            bp = qb % 2
            x3 = qb % 3
            ob = osb[bp]
            on = ['gosb%d_0' % bp, 'gosb%d_1' % bp]
            s.op('dve', lambda e: e.tensor_tensor(out=zb[:], in0=ob[:], in1=ob[:], op=ALU.mult), r=on, w=['zb'])
            for h in range(8):
                s.op('dve', lambda e, h=h: e.reduce_sum(out=zsm[:, h:h + 1], in_=zb[:, h * 128:(h + 1) * 128], axis=AX.X), r=['zb'], w=['zsm'])
            yield
            s.op('dve', lambda e: e.tensor_scalar(out=zsm[:], in0=zsm[:], scalar1=1.0 / 128, scalar2=RMS_EPS, op0=ALU.mult, op1=ALU.add), r=['zsm'], w=['zsm'])
            s.op('act', lambda e: e.activation(out=zsm[:], in_=zsm[:], func=AF.Sqrt), r=['zsm'], w=['zsm'])
            s.op('dve', lambda e: e.reciprocal(out=zsm[:], in_=zsm[:]), r=['zsm'], w=['zsm'])
            yield
            for h in range(8):
                s.op('dve', lambda e, h=h: e.tensor_scalar(out=ob[:, h * 128:(h + 1) * 128], in0=ob[:, h * 128:(h + 1) * 128], scalar1=zsm[:, h:h + 1],
                                                           scalar2=None, op0=ALU.mult), r=on + ['zsm'], w=on)
            yield
            s.op('dve', lambda e: e.tensor_tensor(out=ob[:], in0=ob[:], in1=onb[:], op=ALU.mult), r=on + ['onb'], w=on)
            s.op('dve', lambda e: e.tensor_tensor(out=ob[:], in0=ob[:], in1=gsil[x3][:], op=ALU.mult), r=on + ['gsil%d' % x3], w=on)
            yield
            for kc in range(8):
                b = kc // 4
                off = b * 512 + (kc % 4) * 128
                s.op('pe', lambda e, kc=kc, off=off: e.transpose(P[:, off:off + 128], ob[:, kc * 128:(kc + 1) * 128], g.ident[:]), r=on + ['ident'], w=['ps%d' % b])
            for j in range(2):
                s.op('act', lambda e, j=j: e.activation(out=yinT[:, j * 512:(j + 1) * 512], in_=P[:, j * 512:(j + 1) * 512], func=AF.Copy),
                     r=['ps%d' % j], w=['gyinT'])
            yield
            for half in range(2):
                b = half
                for kc in range(8):
                    s.op('pe', lambda e, kc=kc, half=half, b=b: e.matmul(P[:, b * 512:(b + 1) * 512], lhsT=yinT[:, kc * 128:(kc + 1) * 128],
                                                                        rhs=wout[:, kc, half * 512:(half + 1) * 512], start=(kc == 0), stop=(kc == 7)),
                         r=['gyinT', 'gwout'], w=['ps%d' % b], inc=(kc == 7))
            s.dma('sp', xrs[:], src[t0:t0 + 128, :], w=['gxrs'])
            emit_epilogue(g, L, (0, 1), xrs, 'gxrs', zb, xo[0], 'gxo0', dst[t0:t0 + 128, :])
            yield

        nblk = g.nblk_gdn
        for it in range(nblk + 2):
            gens = []
            if 1 <= it <= nblk:
                gens.append([heads(it - 1, 0), 1])
                gens.append([heads(it - 1, 1), 1])
            if it < nblk:
                gens.append([phaseA(it), 1])
            if it >= 2:
                gens.append([phaseZ(it - 2), 1])
            _interleave(gens)
        s.barrier()


W_SPECS = [
    ("e_mod_w", [1, D, 3 * D]), ("e_mod_b", [1, 1, 3 * D]), ("e_ln_g", [1, D]), ("e_ln_b", [1, D]),
    ("o_mod_w", [1, D, 3 * D]), ("o_mod_b", [1, 1, 3 * D]), ("o_ln_g", [1, D]), ("o_ln_b", [1, D]),
    ("f_mod_w", [2, D, 3 * D]), ("f_mod_b", [2, 1, 3 * D]), ("f_ln_g", [2, D]), ("f_ln_b", [2, D]),
    ("f_w_up", [2, D, 2 * DFF]), ("f_w_down", [2, DFF, D]), ("f_cw", [2, 128, 2 * NFC, 4]),
    ("ident", [128, 128]),
    ("e_w_in", [1, D, 1476]), ("e_pool_w", [1, 4, 128, 128]), ("pscale", [128, 4]), ("e_kv_norm", [1, 128]),
    ("o_w_in", [1, D, 4112]), ("o_w_out", [1, D, D]), ("o_cw", [128, 24, 4]), ("msl", [128, 128]), ("mil", [128, 128]), ("triu", [128, 128]), ("mdm", [128, 5, 128]), ("mdmT", [128, 5, 128]),
    ("o_a_log", [1, 8]), ("o_dt_bias", [1, 8]), ("onb", [1, D]),
    ("ukT", [128, 4, 128]), ("uvpad", [128, 8, 128]), ("e_w_out", [1, D, D]), ("negI4", [128, 512]), ("corr", [128, 4, 15]),
]


def build(stages):
    nc = bass.Bass("TRN2", target_bir_lowering=False)
    g = Ctx()
    g.nc = nc
    g.d_x = nc.dram_tensor("x", [S, D], F32, kind="ExternalInput").ap()
    g.d_ccol = nc.dram_tensor("ccol", [128, 8], F32, kind="ExternalInput").ap()
    for nm, shp in W_SPECS:
        setattr(g, "d_" + nm, nc.dram_tensor(nm, shp, F32, kind="ExternalInput").ap())
    g.d_out = nc.dram_tensor("out", [S, D], F32, kind="ExternalOutput").ap()
    scr = [nc.dram_tensor("xscr%d" % i, [S, D], F32, kind="Internal").ap() for i in range(3)]
    with ExitStack() as es:
        g.es = es
        g.s = Sch(nc, es)
        g.ps = es.enter_context(nc.psum_tensor("ps", [128, 8 * 512], F32))
        g.lnst = _sb(nc, es, "lnst", [128, 12], F32)
        g.lnmv = _sb(nc, es, "lnmv", [128, 8], F32)
        emit_consts(g)
        bufs = [g.d_x] + scr
        n = len(stages)
        for i, st in enumerate(stages):
            src = g.d_x if i == 0 else scr[(i - 1) % 3]
            dst = g.d_out if i == n - 1 else scr[i % 3]
            if st[0] == 'ffn':
                emit_ffn(g, st[1], src, dst)
            elif st[0] == 'gdn':
                g.nblk_gdn = st[1] if len(st) > 1 else NB
                emit_gdn(g, src, dst)
            elif st[0] == 'dsa':
                g.nblk_dsa = st[1] if len(st) > 1 else NB
                emit_dsa(g, src, dst)
            else:
                raise ValueError(st)
        g.s.finish()
    return nc


def prep_weights(inp):
    f = lambda a: np.ascontiguousarray(np.asarray(a, dtype=np.float32))
    w = {}
    for k in ("e_mod_w", "e_ln_g", "e_ln_b", "o_mod_w", "o_ln_g", "o_ln_b", "f_mod_w", "f_ln_g", "f_ln_b", "f_w_up", "f_w_down"):
        w[k] = f(inp[k])
    for k in ("e_mod_b", "o_mod_b", "f_mod_b"):
        a = f(inp[k])
        w[k] = np.ascontiguousarray(a.reshape(a.shape[0], 1, 3 * D))
    cwt = f(inp["f_conv_w"])
    cb = f(inp["f_conv_b"])
    a = np.concatenate([cwt, cb[:, None, :]], axis=1)
    a = a.reshape(2, 4, 2 * NFC, 128).transpose(0, 3, 2, 1)
    w["f_cw"] = np.ascontiguousarray(a)
    w["ident"] = np.eye(128, dtype=np.float32)
    for k in ("e_w_in", "e_pool_w", "e_kv_norm", "e_w_out"):
        w[k] = f(inp[k])
    w["pscale"] = np.ascontiguousarray(f(inp["e_pool_scale"])[0].reshape(4, 128).T)
    uk = f(inp["e_w_uk"])[0]
    w["ukT"] = np.ascontiguousarray(uk.reshape(4, 2, 128, 64).transpose(1, 3, 0, 2).reshape(128, 4, 128))
    uv = f(inp["e_w_uv"])[0]
    uvp = np.zeros((128, 8, 128), np.float32)
    for h in range(8):
        uvp[:, h, (h % 2) * 64:(h % 2) * 64 + 64] = uv[h]
    w["uvpad"] = uvp
    w["negI4"] = np.ascontiguousarray(np.tile(-30000.0 * np.eye(128, dtype=np.float32), (1, 4)))
    corr = np.ones((128, 4, 15), np.float32)
    for gi in range(4):
        win_ = 2 << gi
        for t in range(win_ - 1):
            corr[:, gi, t] = win_ / (t + 1.0)
    w["corr"] = corr
    for k in ("o_w_in", "o_w_out", "o_a_log", "o_dt_bias"):
        w[k] = f(inp[k])
    ocw = f(inp["o_conv_w"])[0]
    w["o_cw"] = np.ascontiguousarray(ocw.reshape(4, 24, 128).transpose(2, 1, 0))
    ar = np.arange(128)
    w["msl"] = (ar[:, None] > ar[None, :]).astype(np.float32)
    w["mil"] = (ar[:, None] >= ar[None, :]).astype(np.float32)
    w["triu"] = (ar[:, None] <= ar[None, :]).astype(np.float32)
    w["onb"] = np.ascontiguousarray(np.tile(f(inp["o_out_norm"])[0], 8)[None, :])
    mdm = np.zeros((128, 5, 128), np.float32)
    mdm[:, 0, :] = (ar[:, None] // 8 == ar[None, :] // 8)
    for li, bsz in enumerate((8, 16, 32, 64)):
        bl = ar // bsz
        mdm[:, 1 + li, :] = (bl[:, None] % 2 == 1) & (bl[None, :] == bl[:, None] - 1)
    w["mdm"] = mdm
    w["mdmT"] = np.ascontiguousarray(mdm.transpose(2, 1, 0))
    return w


STAGES = [('dsa',), ('ffn', 0), ('gdn',), ('ffn', 1)]


def kernel(**inp):
    x = np.asarray(inp["x"], dtype=np.float32)
    c = np.asarray(inp["c"], dtype=np.float32)
    w = prep_weights(inp)
    nc = build(STAGES)
    in_maps = []
    for b in range(8):
        m = dict(w)
        m["x"] = np.ascontiguousarray(x[b])
        m["ccol"] = np.ascontiguousarray(c[b].reshape(8, 128).T)
        in_maps.append(m)
    res = run_bass_kernel_spmd(nc, in_maps, core_ids=list(range(8)))
    return np.stack([np.asarray(r["out"], dtype=np.float32) for r in res.results], axis=0)
```

```python
import os
import numpy as np
from contextlib import ExitStack
import concourse.bass as bass
import concourse.mybir as mybir
from concourse.bass_utils import run_bass_kernel_spmd

F32 = mybir.dt.float32
BF16 = mybir.dt.bfloat16
AF = mybir.ActivationFunctionType
ALU = mybir.AluOpType
AX = mybir.AxisListType

D = 1024
S = 4096
NB = S // 128
DFF = 2688
NFC = DFF // 128
ALPHA = float(4 ** 0.25)
LN_EPS = 1e-5
RMS_EPS = 1e-6
NDS = 12


class Sch:
    def __init__(self, nc, es):
        self.nc = nc
        self.E = {'pe': nc.tensor, 'act': nc.scalar, 'dve': nc.vector,
                  'pool': nc.gpsimd, 'sp': nc.sync}
        self.sem = {}
        for e in self.E:
            self.sem[e] = es.enter_context(nc.semaphore('s_' + e))
        self.cnt = {e: 0 for e in self.E}
        self.waited = {e: {} for e in self.E}
        self.lastw = {}
        self.readers = {}
        self.dq = ('sp', 'pool', 'act')
        self.dcnt = {}
        self.drr = {q: 0 for q in self.dq}
        for q in self.dq:
            for i in range(NDS):
                k = (q, i)
                self.sem[k] = es.enter_context(nc.semaphore('d_%s%d' % (q, i)))
                self.dcnt[k] = 0
        self.nwaits = 0

    def _wait(self, e, tok):
        key, val = tok
        if key == e and e == 'pe':
            return
        if self.waited[e].get(key, 0) >= val:
            return
        self.E[e].wait_ge(self.sem[key], val)
        self.waited[e][key] = val
        self.nwaits += 1

    def _collect(self, r, w, e=None):
        deps = {}

        def add(t):
            if t is None:
                return
            if deps.get(t[0], 0) < t[1]:
                deps[t[0]] = t[1]
        for x in r:
            for k, v in self.lastw.get(x, {}).items():
                add((k, v))
            if x.startswith('ps') and e is not None:
                for k, v in self.readers.get(x, {}).items():
                    if k != e:
                        add((k, v))
        for x in w:
            for k, v in self.lastw.get(x, {}).items():
                add((k, v))
            for k, v in self.readers.get(x, {}).items():
                add((k, v))
        return list(deps.items())

    def _record(self, tok, r, w):
        for x in w:
            self.lastw.setdefault(x, {})[tok[0]] = tok[1]
            self.readers[x] = {}
        for x in r:
            d = self.readers.setdefault(x, {})
            if d.get(tok[0], 0) < tok[1]:
                d[tok[0]] = tok[1]

    def op(self, e, fn, r=(), w=(), inc=True):
        for t in self._collect(r, w, e):
            self._wait(e, t)
        ins = fn(self.E[e])
        if inc:
            self.cnt[e] += 1
            ins.then_inc(self.sem[e], 1)
            tok = (e, self.cnt[e])
        else:
            tok = (e, self.cnt[e] + 1)
        self._record(tok, r, w)
        return ins

    def dma(self, q, out, in_, r=(), w=()):
        i = self.drr[q]
        self.drr[q] = (i + 1) % NDS
        k = (q, i)
        if self.dcnt[k] > 0:
            self._wait(q, (k, self.dcnt[k]))
        for t in self._collect(r, w):
            self._wait(q, t)
        ins = self.E[q].dma_start(out=out, in_=in_)
        self.dcnt[k] += 16
        ins.then_inc(self.sem[k], 16)
        tok = (k, self.dcnt[k])
        self._record(tok, r, w)
        return ins

    def barrier(self, keep_pool_dma=False):
        def is_pool(k):
            return isinstance(k, tuple) and k[0] == 'pool'
        toks = [(e, self.cnt[e]) for e in self.E if self.cnt[e] > 0]
        toks += [(k, v) for k, v in self.dcnt.items() if v > 0 and not (keep_pool_dma and is_pool(k))]
        for e in self.E:
            for t in toks:
                self._wait(e, t)
        keep = {}
        if keep_pool_dma:
            for res, d in self.lastw.items():
                d2 = {k: v for k, v in d.items() if is_pool(k)}
                if d2:
                    keep[res] = d2
        self.lastw = keep
        self.readers = {}

    def finish(self):
        for k, v in self.dcnt.items():
            if v > 0:
                self._wait('sp', (k, v))


class Ctx:
    pass


def _interleave(gens):
    live = list(gens)
    while live:
        nxt = []
        for item in live:
            gen, n = item
            done = False
            for _ in range(n):
                try:
                    next(gen)
                except StopIteration:
                    done = True
                    break
            if not done:
                nxt.append(item)
        live = nxt


_UID = [0]


def _sb(nc, es, name, shape, dt):
    _UID[0] += 1
    return es.enter_context(nc.sbuf_tensor("sb%d_%s" % (_UID[0], name), list(shape), dt))


def emit_consts(g):
    nc, s, es = g.nc, g.s, g.es
    g.ident = _sb(nc, es, "ident", [128, 128], F32)
    g.identb = _sb(nc, es, "identb", [128, 128], BF16)
    g.ones = _sb(nc, es, "ones", [128, 128], F32)
    g.onesb = _sb(nc, es, "onesb", [128, 128], BF16)
    s.dma('sp', g.ident[:], g.d_ident[:, :], w=['ident'])
    s.dma('pool', g.identb[:], g.d_ident[:, :], w=['identb'])
    s.op('dve', lambda e: e.memset(g.ones[:], 1.0), w=['ones'])
    s.op('dve', lambda e: e.memset(g.onesb[:], 1.0), w=['onesb'])
    g.ccol = _sb(nc, es, "ccol", [128, 8], F32)
    g.sc = _sb(nc, es, "sc", [128, 8], F32)
    g.scb = _sb(nc, es, "scb", [128, 8, 128], F32)
    s.dma('sp', g.ccol[:], g.d_ccol[:, :], w=['ccol'])
    s.op('act', lambda e: e.activation(out=g.sc[:], in_=g.ccol[:], func=AF.Silu), r=['ccol'], w=['sc'])
    for kc in range(8):
        s.op('dve', lambda e, kc=kc: e.tensor_scalar(out=g.scb[:, kc, :], in0=g.ones[:], scalar1=g.sc[:, kc:kc + 1],
                                                     scalar2=None, op0=ALU.mult), r=['ones', 'sc'], w=['scb'])


def emit_mods(g, L, modw, modb_row, lng, lnb, pre=None):
    nc, s, es = g.nc, g.s, g.es
    L.shift = _sb(nc, L.es, "shift", [128, 8], F32)
    L.scale1 = _sb(nc, L.es, "scale1", [128, 8], F32)
    L.gate_bc = _sb(nc, L.es, "gate_bc", [128, D], F32)
    L.lng_bc = _sb(nc, L.es, "lng_bc", [128, D], F32)
    L.lnb_bc = _sb(nc, L.es, "lnb_bc", [128, D], F32)
    s.dma('sp', L.lng_bc[:], lng.partition_broadcast(128), w=['lng_bc'])
    s.dma('sp', L.lnb_bc[:], lnb.partition_broadcast(128), w=['lnb_bc'])
    if pre is not None:
        pre()
    with ExitStack() as es2:
        mw = [_sb(nc, es2, "mw%d" % i, [128, 3 * D], F32) for i in range(2)]
        brow = _sb(nc, es2, "brow", [1, 3 * D], F32)
        bc = _sb(nc, es2, "modbc", [128, 2 * D], F32)
        one11 = _sb(nc, es2, "one11", [1, 1], F32)
        s.op('dve', lambda e: e.memset(one11[:], 1.0), w=['one11'])
        s.dma('sp', brow[:], modb_row[:, :], w=['brow'])
        P = g.ps
        for kc in range(8):
            t = mw[kc % 2]
            nm = 'mw%d' % (kc % 2)
            s.dma('sp', t[:], modw[kc * 128:(kc + 1) * 128, :], w=[nm])
            for j in range(6):
                s.op('pe', lambda e, j=j, kc=kc, t=t: e.matmul(P[:, j * 512:(j + 1) * 512], lhsT=g.scb[:, kc, :],
                                                            rhs=t[:, j * 512:(j + 1) * 512], start=(kc == 0), stop=False),
                     r=[nm, 'scb'], w=['ps%d' % j], inc=(j == 5))
        for j in range(6):
            s.op('pe', lambda e, j=j: e.matmul(P[:, j * 512:(j + 1) * 512], lhsT=g.ones[0:1, :],
                                               rhs=brow[0:1, j * 512:(j + 1) * 512], start=False, stop=True),
                 r=['brow', 'ones'], w=['ps%d' % j])
        for j in range(4):
            s.op('act' if j % 2 else 'dve',
                 (lambda e, j=j: e.activation(out=bc[:, j * 512:(j + 1) * 512], in_=P[:, j * 512:(j + 1) * 512], func=AF.Copy))
                 if j % 2 else
                 (lambda e, j=j: e.tensor_copy(out=bc[:, j * 512:(j + 1) * 512], in_=P[:, j * 512:(j + 1) * 512])),
                 r=['ps%d' % j], w=['modbc%d' % j])
        for j in range(2):
            s.op('dve', lambda e, j=j: e.tensor_copy(out=L.gate_bc[:, j * 512:(j + 1) * 512], in_=P[:, (4 + j) * 512:(5 + j) * 512]),
                 r=['ps%d' % (4 + j)], w=['gate_bc'])
        for j in range(16):
            s.op('pe', lambda e, j=j: e.matmul(P[:, 6 * 512 + j:6 * 512 + j + 1], lhsT=bc[0:1, j * 128:(j + 1) * 128],
                                               rhs=one11[0:1, 0:1], start=True, stop=True),
                 r=['modbc%d' % (j // 4), 'one11'], w=['ps6'])
        s.op('dve', lambda e: e.tensor_copy(out=L.shift[:], in_=P[:, 6 * 512:6 * 512 + 8]), r=['ps6'], w=['shift'])
        s.op('dve', lambda e: e.tensor_scalar(out=L.scale1[:], in0=P[:, 6 * 512 + 8:6 * 512 + 16], scalar1=1.0, scalar2=None,
                                              op0=ALU.add), r=['ps6'], w=['scale1'])
        s.barrier(keep_pool_dma=(pre is not None))


def emit_hT(g, L, xin, xin_nm, hT_ap_fn, hT_nm, pbanks):
    s = g.s
    P = g.ps
    for kc in range(8):
        b = pbanks[kc // 4]
        off = b * 512 + (kc % 4) * 128
        s.op('pe', lambda e, kc=kc, off=off: e.transpose(P[:, off:off + 128], xin[:, kc * 128:(kc + 1) * 128], g.ident[:]),
             r=[xin_nm, 'ident'], w=['ps%d' % b])
    for kc in range(8):
        b = pbanks[kc // 4]
        off = b * 512 + (kc % 4) * 128
        s.op('act', lambda e, kc=kc, off=off: e.activation(out=hT_ap_fn(kc), in_=P[:, off:off + 128], func=AF.Identity,
                                                           scale=L.scale1[:, kc:kc + 1], bias=L.shift[:, kc:kc + 1]),
             r=['ps%d' % b, 'scale1', 'shift'], w=[hT_nm])


def emit_epilogue(g, L, ybanks, xres, xres_nm, zb, xo, xo_nm, dst_rows):
    s = g.s
    P = g.ps
    for j in range(2):
        b = ybanks[j]
        s.op('dve', lambda e, j=j, b=b: e.tensor_tensor(out=zb[:, j * 512:(j + 1) * 512], in0=P[:, b * 512:(b + 1) * 512],
                                                        in1=L.gate_bc[:, j * 512:(j + 1) * 512], op=ALU.mult),
             r=['ps%d' % b, 'gate_bc'], w=['zb'])
    s.op('dve', lambda e: e.scalar_tensor_tensor(out=zb[:], in0=xres[:], scalar=ALPHA, in1=zb[:], op0=ALU.mult, op1=ALU.add),
         r=[xres_nm, 'zb'], w=['zb'])
    st = g.lnst
    for j in range(2):
        s.op('dve', lambda e, j=j: e.bn_stats(out=st[:, j * 6:(j + 1) * 6], in_=zb[:, j * 512:(j + 1) * 512]), r=['zb'], w=['lnst'])
    s.op('dve', lambda e: e.bn_aggr(out=g.lnmv[:, 0:2], in_=st[:, 0:12]), r=['lnst'], w=['lnmv'])
    s.op('dve', lambda e: e.tensor_scalar(out=g.lnmv[:, 2:3], in0=g.lnmv[:, 1:2], scalar1=LN_EPS, scalar2=None, op0=ALU.add),
         r=['lnmv'], w=['lnmv2'])
    s.op('act', lambda e: e.activation(out=g.lnmv[:, 3:4], in_=g.lnmv[:, 2:3], func=AF.Sqrt), r=['lnmv2'], w=['lnmv3'])
    s.op('dve', lambda e: e.reciprocal(out=g.lnmv[:, 4:5], in_=g.lnmv[:, 3:4]), r=['lnmv3'], w=['lnmv4'])
    s.op('dve', lambda e: e.tensor_scalar(out=zb[:], in0=zb[:], scalar1=g.lnmv[:, 0:1], scalar2=g.lnmv[:, 4:5],
                                          op0=ALU.subtract, op1=ALU.mult), r=['zb', 'lnmv', 'lnmv4'], w=['zb'])
    s.op('pool', lambda e: e.tensor_tensor(out=zb[:], in0=zb[:], in1=L.lng_bc[:], op=ALU.mult), r=['zb', 'lng_bc'], w=['zb'])
    s.op('pool', lambda e: e.tensor_tensor(out=xo[:], in0=zb[:], in1=L.lnb_bc[:], op=ALU.add), r=['zb', 'lnb_bc'], w=[xo_nm])
    s.dma('sp', dst_rows, xo[:], r=[xo_nm], w=[])


def emit_ffn(g, li, src, dst):
    nc, s = g.nc, g.s
    TT = 256
    NT = S // TT
    NBT = TT // 128
    with ExitStack() as les:
        L = Ctx()
        L.es = les
        WT = Ctx()

        def pre():
            WT.wup = _sb(nc, les, "wup", [128, 8, 2 * DFF], BF16)
            WT.wdn = _sb(nc, les, "wdn", [128, NFC, D], BF16)
            wupd = g.d_f_w_up[li].rearrange("(kc p) n -> p kc n", p=128)
            for kc in range(8):
                for hf in range(2):
                    s.dma('pool', WT.wup[:, kc, hf * DFF:(hf + 1) * DFF], wupd[:, kc, hf * DFF:(hf + 1) * DFF], w=['wup'])
            wdnd = g.d_f_w_down[li].rearrange("(fc p) n -> p fc n", p=128)
            for fc in range(NFC):
                s.dma('pool', WT.wdn[:, fc, :], wdnd[:, fc, :], w=['wdn'])
        emit_mods(g, L, g.d_f_mod_w[li], g.d_f_mod_b[li], g.d_f_ln_g[li], g.d_f_ln_b[li], pre=pre)
        wup, wdn = WT.wup, WT.wdn
        cw = _sb(nc, les, "cw", [128, 2 * NFC, 4], F32)
        halo = _sb(nc, les, "halo", [128, 2 * NFC, 2], F32)
        hT = [_sb(nc, les, "hT%d" % i, [128, 8, TT], BF16) for i in range(2)]
        gTs = [_sb(nc, les, "gT%d" % i, [128, NFC, TT], BF16) for i in range(2)]
        upre = [_sb(nc, les, "upre%d" % i, [128, TT + 2], F32) for i in range(2)]
        c0 = [_sb(nc, les, "c0%d" % i, [128, TT], F32) for i in range(2)]
        asil = _sb(nc, les, "asil", [128, TT], F32)
        xin = [_sb(nc, les, "xin%d" % i, [128, D], F32) for i in range(2)]
        xrs = [_sb(nc, les, "xrs%d" % i, [128, D], F32) for i in range(2)]
        zb = _sb(nc, les, "zb", [128, D], F32)
        xo = [_sb(nc, les, "xo%d" % i, [128, D], F32) for i in range(2)]
        P = g.ps
        s.dma('sp', cw[:], g.d_f_cw[li], w=['cw'])
        s.op('dve', lambda e: e.memset(halo[:], 0.0), w=['halo'])
        def phA(t):
            t0 = t * TT
            hs = t % 2
            hnm = 'hT%d' % hs
            for bi in range(NBT):
                xs_ = (t * NBT + bi) % 2
                s.dma('sp', xin[xs_][:], src[t0 + bi * 128:t0 + (bi + 1) * 128, :], w=['xin%d' % xs_])
                emit_hT(g, L, xin[xs_], 'xin%d' % xs_, lambda kc, bi=bi, hs=hs: hT[hs][:, kc, bi * 128:(bi + 1) * 128], hnm, (0, 1))
                yield

        def phU(t):
            hs = t % 2
            hnm = 'hT%d' % hs
            gT = gTs[t % 2]
            gnm = 'gT%d' % (t % 2)
            for j in range(NFC):
                for half in range(2):
                    fc = j + half * NFC
                    b = 2 + ((2 * j + half) % 4)
                    pb = 'ps%d' % b
                    up = upre[half]
                    unm = 'upre%d' % half
                    for kc in range(8):
                        s.op('pe', lambda e, kc=kc, fc=fc, b=b: e.matmul(P[:, b * 512:b * 512 + TT], lhsT=wup[:, kc, fc * 128:(fc + 1) * 128],
                                                                      rhs=hT[hs][:, kc, :], start=(kc == 0), stop=(kc == 7)),
                             r=['wup', hnm], w=[pb], inc=(kc == 7))
                    s.op('pool', lambda e, fc=fc, up=up: e.tensor_copy(out=up[:, 0:2], in_=halo[:, fc, :]), r=['halo'], w=[unm])
                    s.op('act', lambda e, b=b, up=up: e.activation(out=up[:, 2:TT + 2], in_=P[:, b * 512:b * 512 + TT], func=AF.Copy),
                         r=[pb], w=[unm])
                    s.op('act', lambda e, b=b, fc=fc, half=half: e.activation(out=c0[half][:], in_=P[:, b * 512:b * 512 + TT], func=AF.Identity,
                                                                             scale=cw[:, fc, 2:3], bias=cw[:, fc, 3:4]),
                         r=[pb, 'cw'], w=['c0%d' % half])
                    s.op('pool', lambda e, fc=fc, up=up: e.tensor_copy(out=halo[:, fc, :], in_=up[:, TT:TT + 2]), r=[unm], w=['halo'])
                    s.op('dve', lambda e, fc=fc, up=up, half=half: e.scalar_tensor_tensor(out=c0[half][:], in0=up[:, 1:TT + 1], scalar=cw[:, fc, 1:2],
                                                                                        in1=c0[half][:], op0=ALU.mult, op1=ALU.add),
                         r=[unm, 'cw', 'c0%d' % half], w=['c0%d' % half])
                    s.op('dve', lambda e, fc=fc, up=up, half=half: e.scalar_tensor_tensor(out=c0[half][:], in0=up[:, 0:TT], scalar=cw[:, fc, 0:1],
                                                                                        in1=c0[half][:], op0=ALU.mult, op1=ALU.add),
                         r=[unm, 'cw', 'c0%d' % half], w=['c0%d' % half])
                    if half == 0:
                        s.op('act', lambda e: e.activation(out=asil[:], in_=c0[0][:], func=AF.Silu), r=['c00'], w=['asil'])
                    else:
                        s.op('dve', lambda e, j=j, gT=gT: e.tensor_tensor(out=gT[:, j, :], in0=asil[:], in1=c0[1][:], op=ALU.mult),
                             r=['asil', 'c01'], w=[gnm])
                    yield

        def phD(t):
            t0 = t * TT
            gT = gTs[t % 2]
            gnm = 'gT%d' % (t % 2)
            for bi in range(NBT):
                r0 = t0 + bi * 128
                xs_ = (t * NBT + bi) % 2
                s.dma('sp', xrs[xs_][:], src[r0:r0 + 128, :], w=['xrs%d' % xs_])
                for half in range(2):
                    b = 6 + half
                    for j in range(NFC):
                        s.op('pe', lambda e, j=j, half=half, b=b, bi=bi, gT=gT: e.matmul(P[:, b * 512:(b + 1) * 512], lhsT=gT[:, j, bi * 128:(bi + 1) * 128],
                                                                                         rhs=wdn[:, j, half * 512:(half + 1) * 512],
                                                                                         start=(j == 0), stop=(j == NFC - 1)),
                             r=[gnm, 'wdn'], w=['ps%d' % b], inc=(j == NFC - 1))
                    yield
                emit_epilogue(g, L, (6, 7), xrs[xs_], 'xrs%d' % xs_, zb, xo[xs_], 'xo%d' % xs_, dst[r0:r0 + 128, :])
                yield

        for it in range(NT + 2):
            gens = []
            if 1 <= it <= NT:
                gens.append([phU(it - 1), 1])
            if it < NT:
                gens.append([phA(it), 1])
            if it >= 2:
                gens.append([phD(it - 2), 1])
            _interleave(gens)
        s.barrier()


NEG = -1.0e30
GUARD = 1.0e38
NBISECT = 24
TOPK_EXACT_NK = 1024
DSTOP = int(os.environ.get('DSA_STOP', '99'))
DSUB = int(os.environ.get('DSA_SUB', '99'))
GSTOP = int(os.environ.get('GDN_STOP', '99'))
DSC = int(os.environ.get('DSA_SC', '3'))
DSKIP = os.environ.get('DSA_SKIP', '').split(',')
REP = -3.0e38


def emit_dsa(g, src, dst):
    nc, s = g.nc, g.s
    P = g.ps
    with ExitStack() as les:
        L = Ctx()
        L.es = les
        emit_mods(g, L, g.d_e_mod_w[0], g.d_e_mod_b[0], g.d_e_ln_g[0], g.d_e_ln_b[0])
        A = lambda name, shape, dt: _sb(nc, les, name, shape, dt)
        win = A("win", [128, 8, 1536], BF16)
        wif = A("wif", [128, 8, 4], F32)
        wiw = A("wiw", [128, 8, 4], BF16)
        poolw = A("poolw", [128, 4, 128], BF16)
        pscale = A("pscale", [128, 4], F32)
        kvn_bc = A("kvn_bc", [128, 128], F32)
        ukT = A("ukT", [128, 4, 128], BF16)
        uvpad = A("uvpad", [128, 8, 128], BF16)
        wout = A("wout", [128, 8, D], BF16)
        negI4 = A("negI4", [128, 512], BF16)
        corr = A("corr", [128, 4, 15], F32)
        ckvn_all = A("ckvn_all", [128, NB, 128], BF16)
        ckvnT_all = A("ckvnT_all", [128, S], BF16)
        kiT_all = A("kiT_all", [128, S], BF16)
        xin = [A("xin%d" % i, [128, D], F32) for i in range(3)]
        hT = A("hT", [128, 8, 128], BF16)
        ut = A("ut", [128, 4, 143], F32)
        ta = A("ta", [128, 143], F32)
        tb = A("tb", [128, 143], F32)
        dT = A("dT", [128, 4, 128], BF16)
        qT = A("qT", [128, 512], BF16)
        qiT = [A("qiT%d" % i, [128, 256], BF16) for i in range(2)]
        wis = [A("wis%d" % i, [128, 4], F32) for i in range(2)]
        qlT = [A("qlT%d" % i, [128, 1024], BF16) for i in range(3)]
        W = [A("W%d" % i, [128, S + 8], F32) for i in range(2)]
        bs = A("bs", [128, 8], F32)
        cb = A("cb", [128, S], BF16)
        rbuf = [A("rbuf%d" % i, [128, 512], F32) for i in range(2)]
        rb2 = A("rb2", [128, 512], F32)
        notm = [A("notm%d" % i, [128, S], BF16) for i in range(2)]
        m8 = A("m8", [128, 8], F32)
        pT = [A("pT%d" % i, [128, 512], BF16) for i in range(2)]
        rden = A("rden", [128, 512], F32)
        oTn = A("oTn", [128, 1024], BF16)
        yinT = [A("yinT%d" % i, [128, 1024], BF16) for i in range(3)]
        zb = A("zb", [128, D], F32)
        xo = [A("xo%d" % i, [128, D], F32) for i in range(2)]
        sq = A("sq", [128, 128], F32)
        ckf = A("ckf", [128, 128], F32)
        rs = A("rs", [128, 4], F32)
        Pb3 = P[:, 3 * 512:4 * 512].bitcast(BF16)

        wind = g.d_e_w_in[0].rearrange("(kc p) n -> p kc n", p=128)
        for kc in range(8):
            s.dma('pool', win[:, kc, 0:1472], wind[:, kc, 0:1472], w=['win'])
            s.dma('pool', win[:, kc, 1472:1536], wind[:, kc, 1408:1472], w=['win'])
        s.dma('sp', wif[:], wind[:, :, 1472:1476], w=['wif'])
        s.op('dve', lambda e: e.tensor_copy(out=wiw[:], in_=wif[:]), r=['wif'], w=['wiw'])
        s.dma('pool', poolw[:], g.d_e_pool_w[0].rearrange("g c d -> c g d"), w=['poolw'])
        s.dma('sp', pscale[:], g.d_pscale[:, :], w=['pscale'])
        s.dma('sp', kvn_bc[:], g.d_e_kv_norm[0].partition_broadcast(128), w=['kvn_bc'])
        s.dma('pool', ukT[:], g.d_ukT[:, :, :], w=['ukT'])
        s.dma('pool', uvpad[:], g.d_uvpad[:, :, :], w=['uvpad'])
        woutd = g.d_e_w_out[0].rearrange("(kc p) n -> p kc n", p=128)
        for kc in range(8):
            s.dma('pool', wout[:, kc, :], woutd[:, kc, :], w=['wout'])
        s.dma('pool', negI4[:], g.d_negI4[:, :], w=['negI4'])
        s.dma('sp', corr[:], g.d_corr[:, :, :], w=['corr'])
        s.op('dve', lambda e: e.memset(ut[:], 0.0), w=['ut'])

        def front_a(qb):
            sl = qb % 2
            s3 = qb % 3
            t0 = qb * 128
            nk = t0 + 128
            xn = 'xin%d' % s3
            s.dma('sp', xin[s3][:], src[t0:t0 + 128, :], w=[xn])
            emit_hT(g, L, xin[s3], xn, lambda kc: hT[:, kc, :], 'hT', (0, 1))
            yield
            def grp(out_ap, cols, bank, last=True):
                for kc in range(8):
                    s.op('pe', lambda e, kc=kc: e.matmul(out_ap, lhsT=win[:, kc, cols[0]:cols[1]], rhs=hT[:, kc, :],
                                                        start=(kc == 0), stop=(kc == 7)),
                         r=['win', 'hT'], w=['ps%d' % bank], inc=(kc == 7))
            for gi in range(4):
                grp(P[:, gi * 128:(gi + 1) * 128], (gi * 128, (gi + 1) * 128), 0)
                yield
            for j in range(4):
                grp(P[:, 512 + j * 128:512 + (j + 1) * 128], (512 + j * 128, 512 + (j + 1) * 128), 1)
                yield
            for j in range(2):
                grp(P[:, 1024 + j * 128:1024 + (j + 1) * 128], (1152 + j * 128, 1152 + (j + 1) * 128), 2)
                yield
            grp(P[:, 1024 + 256:1024 + 384], (1408, 1536), 2)
            yield
            for kc in range(8):
                s.op('pe', lambda e, kc=kc: e.matmul(P[:, 1536:1536 + 128], lhsT=hT[:, kc, :], rhs=win[:, kc, 1024:1152],
                                                    start=(kc == 0), stop=(kc == 7)), r=['win', 'hT'], w=['ps3'], inc=(kc == 7))
            for kc in range(8):
                s.op('pe', lambda e, kc=kc: e.matmul(P[:, 1536 + 128:1536 + 132], lhsT=hT[:, kc, :], rhs=wiw[:, kc, :],
                                                    start=(kc == 0), stop=(kc == 7)), r=['wiw', 'hT'], w=['ps3'], inc=(kc == 7))
            yield
            s.op('act', lambda e: e.activation(out=ut[:, :, 15:143], in_=P[:, 0:512].rearrange("p (g t) -> p g t", g=4), func=AF.Copy),
                 r=['ps0'], w=['ut'])
            s.op('act', lambda e: e.activation(out=qT[:], in_=P[:, 512:1024], func=AF.Copy), r=['ps1'], w=['qT'])
            s.op('dve', lambda e: e.tensor_copy(out=qiT[sl][:], in_=P[:, 1024:1024 + 256]), r=['ps2'], w=['qiT%d' % sl])
            s.op('dve', lambda e: e.tensor_copy(out=kiT_all[:, t0:t0 + 128], in_=P[:, 1024 + 256:1024 + 384]), r=['ps2'], w=['kiT_all'])
            s.op('dve', lambda e: e.tensor_copy(out=wis[sl][:], in_=P[:, 1536 + 128:1536 + 132]), r=['ps3'], w=['wis%d' % sl])
            yield
            s.op('act', lambda e: e.activation(out=ckf[:], in_=P[:, 1536:1536 + 128], func=AF.Copy), r=['ps3'], w=['ckf'])
            s.op('dve', lambda e: e.tensor_tensor(out=sq[:], in0=ckf[:], in1=ckf[:], op=ALU.mult), r=['ckf'], w=['sq'])
            s.op('dve', lambda e: e.reduce_sum(out=rs[:, 0:1], in_=sq[:], axis=AX.X), r=['sq'], w=['rs0'])
            s.op('dve', lambda e: e.tensor_scalar(out=rs[:, 1:2], in0=rs[:, 0:1], scalar1=1.0 / 128, scalar2=RMS_EPS, op0=ALU.mult, op1=ALU.add),
                 r=['rs0'], w=['rs1'])
            s.op('act', lambda e: e.activation(out=rs[:, 2:3], in_=rs[:, 1:2], func=AF.Sqrt), r=['rs1'], w=['rs2'])
            s.op('dve', lambda e: e.reciprocal(out=rs[:, 3:4], in_=rs[:, 2:3]), r=['rs2'], w=['rs3'])
            s.op('dve', lambda e: e.scalar_tensor_tensor(out=ckf[:], in0=ckf[:], scalar=rs[:, 3:4], in1=kvn_bc[:],
                                                         op0=ALU.mult, op1=ALU.mult), r=['ckf', 'rs3', 'kvn_bc'], w=['ckf'])
            s.op('act', lambda e: e.activation(out=ckvn_all[:, qb, :], in_=ckf[:], func=AF.Copy), r=['ckf'], w=['ckvn_all'])
            s.op('pe', lambda e: e.transpose(P[:, 1536 + 256:1536 + 384], ckf[:], g.ident[:]), r=['ckf', 'ident'], w=['ps3'])
            s.op('act', lambda e: e.activation(out=ckvnT_all[:, t0:t0 + 128], in_=P[:, 1536 + 256:1536 + 384], func=AF.Copy), r=['ps3'], w=['ckvnT_all'])
            yield
            for gi in range(4):
                win_ = 2 << gi
                U = ut[:, gi, :]
                s.op('dve', lambda e, U=U: e.tensor_tensor(out=ta[:, 1:143], in0=U[:, 1:143], in1=U[:, 0:142], op=ALU.add), r=['ut'], w=['ta'])
                sw = ta
                swn = 'ta'
                if gi >= 1:
                    s.op('dve', lambda e: e.tensor_tensor(out=tb[:, 3:143], in0=ta[:, 3:143], in1=ta[:, 1:141], op=ALU.add), r=['ta'], w=['tb'])
                    sw, swn = tb, 'tb'
                if gi >= 2:
                    s.op('dve', lambda e: e.tensor_tensor(out=ta[:, 7:143], in0=tb[:, 7:143], in1=tb[:, 3:139], op=ALU.add), r=['tb'], w=['ta'])
                    sw, swn = ta, 'ta'
                if gi >= 3:
                    s.op('dve', lambda e: e.tensor_tensor(out=tb[:, 15:143], in0=ta[:, 15:143], in1=ta[:, 7:135], op=ALU.add), r=['ta'], w=['tb'])
                    sw, swn = tb, 'tb'
                if qb == 0:
                    s.op('dve', lambda e, sw=sw, gi=gi, win_=win_: e.tensor_tensor(out=sw[:, 15:15 + win_ - 1], in0=sw[:, 15:15 + win_ - 1],
                                                                                 in1=corr[:, gi, 0:win_ - 1], op=ALU.mult),
                         r=[swn, 'corr'], w=[swn])
                s.op('dve', lambda e, sw=sw, gi=gi, win_=win_, U=U: e.scalar_tensor_tensor(out=dT[:, gi, :], in0=sw[:, 15:143], scalar=1.0 / win_,
                                                                                          in1=U[:, 15:143], op0=ALU.mult, op1=ALU.subtract),
                     r=[swn, 'ut'], w=['dT'])
                yield
            s.op('pool', lambda e: e.tensor_copy(out=ut[:, :, 0:15], in_=ut[:, :, 128:143]), r=['ut'], w=['ut'])
            for gi in range(4):
                s.op('pe', lambda e, gi=gi: e.matmul(P[:, gi * 128:(gi + 1) * 128], lhsT=poolw[:, gi, :], rhs=dT[:, gi, :], start=True, stop=True),
                     r=['poolw', 'dT'], w=['ps0'])
            for gi in range(4):
                s.op('act', lambda e, gi=gi: e.activation(out=yinT[s3][:, gi * 128:(gi + 1) * 128], in_=P[:, gi * 128:(gi + 1) * 128],
                                                          func=AF.Identity, scale=pscale[:, gi:gi + 1]),
                     r=['ps0', 'pscale'], w=['yinT%d' % s3])
            yield
            for h in range(8):
                po = (h % 2) * 64
                bank = 1 + h % 2
                off = bank * 512 + (h // 2) * 128
                s.op('pe', lambda e, h=h, po=po, off=off: e.matmul(P[:, off:off + 128], lhsT=ukT[po:po + 64, h // 2, :],
                                                                  rhs=qT[po:po + 64, (h // 2) * 128:(h // 2 + 1) * 128], start=True, stop=True),
                     r=['ukT', 'qT'], w=['ps%d' % bank])
            for j in range(2):
                s.op('act', lambda e, j=j: e.activation(out=qlT[s3][:, j * 512:(j + 1) * 512], in_=P[:, (1 + j) * 512:(2 + j) * 512], func=AF.Copy),
                     r=['ps%d' % (1 + j)], w=['qlT%d' % s3])
            yield
            Wn = 'W%d' % sl
            cnt = 0
            chunks = []
            k0_ = 0
            while k0_ < nk:
                rem = nk - k0_
                w0 = 512 if rem >= 512 else (256 if rem >= 256 else 128)
                chunks.append((k0_, w0))
                k0_ += w0
            for (k0, w_) in chunks:
                for h in range(4):
                    po = (h % 2) * 64
                    bank = 2 + cnt % 2
                    rb = rbuf[cnt % 2]
                    rbn = 'rbuf%d' % (cnt % 2)
                    cnt += 1
                    s.op('pe', lambda e, h=h, po=po, bank=bank, k0=k0, w_=w_: e.matmul(P[:, bank * 512:bank * 512 + w_],
                                                                                     lhsT=qiT[sl][po:po + 64, (h // 2) * 128:(h // 2 + 1) * 128],
                                                                                     rhs=kiT_all[po:po + 64, k0:k0 + w_], start=True, stop=True),
                         r=['qiT%d' % sl, 'kiT_all'], w=['ps%d' % bank])
                    s.op('act', lambda e, bank=bank, rb=rb, w_=w_: e.activation(out=rb[:, 0:w_], in_=P[:, bank * 512:bank * 512 + w_], func=AF.Relu),
                         r=['ps%d' % bank], w=[rbn])
                    if h == 0:
                        s.op('dve', lambda e, rb=rb, k0=k0, w_=w_: e.tensor_scalar(out=W[sl][:, k0:k0 + w_], in0=rb[:, 0:w_], scalar1=wis[sl][:, 0:1],
                                                                                 scalar2=None, op0=ALU.mult), r=[rbn, 'wis%d' % sl], w=[Wn])
                    else:
                        s.op('dve', lambda e, rb=rb, k0=k0, w_=w_, h=h: e.scalar_tensor_tensor(out=W[sl][:, k0:k0 + w_], in0=rb[:, 0:w_], scalar=wis[sl][:, h:h + 1],
                                                                                            in1=W[sl][:, k0:k0 + w_], op0=ALU.mult, op1=ALU.add),
                             r=[rbn, 'wis%d' % sl, Wn], w=[Wn])
                    yield

        def topk(qb):
            sl = qb % 2
            nk = qb * 128 + 128
            Wn = 'W%d' % sl
            nmn = 'notm%d' % sl
            Wt = W[sl]
            if nk <= 256:
                s.op('dve', lambda e: e.memset(notm[sl][:, 0:nk], 0.0), w=[nmn])
            elif nk <= TOPK_EXACT_NK:
                s.op('dve', lambda e: e.memset(Wt[0:64, nk - 64:nk], NEG), r=[Wn], w=[Wn])
                for it in range(32):
                    s.op('dve', lambda e: e.max(out=m8[:], in_=Wt[:, 0:nk]), r=[Wn], w=['m8'])
                    s.op('dve', lambda e: e.match_replace(out=Wt[:, 0:nk], in_to_replace=m8[:], in_values=Wt[:, 0:nk], imm_value=REP),
                         r=[Wn, 'm8'], w=[Wn])
                    yield
                s.op('dve', lambda e: e.tensor_scalar(out=notm[sl][:, 0:nk], in0=Wt[:, 0:nk], scalar1=0.5 * REP, scalar2=None, op0=ALU.is_gt),
                     r=[Wn], w=[nmn])
            else:
                s.op('dve', lambda e: e.tensor_reduce(out=bs[:, 7:8], in_=Wt[:, 0:nk], axis=AX.X, op=ALU.max, apply_absolute_value=True),
                     r=[Wn], w=['bs7'])
                s.op('dve', lambda e: e.memset(Wt[0:64, nk - 64:nk], NEG), r=[Wn], w=[Wn])
                s.op('dve', lambda e: e.tensor_scalar(out=bs[:, 0:1], in0=bs[:, 7:8], scalar1=-1.001, scalar2=None, op0=ALU.mult), r=['bs7'], w=['bs0'])
                s.op('dve', lambda e: e.tensor_scalar(out=bs[:, 1:2], in0=bs[:, 7:8], scalar1=1.0005, scalar2=None, op0=ALU.mult), r=['bs7'], w=['bs1'])
                s.op('dve', lambda e: e.tensor_scalar(out=bs[:, 2:3], in0=bs[:, 0:1], scalar1=bs[:, 1:2], scalar2=-1.0, op0=ALU.add, op1=ALU.mult),
                     r=['bs0', 'bs1'], w=['bs2'])
                yield
                for it in range(NBISECT):
                    s.op('act', lambda e: e.activation(out=notm[sl][:, 0:nk], in_=Wt[:, 0:nk], func=AF.Sign, bias=bs[:, 2:3], scale=1.0, accum_out=bs[:, 3:4]),
                         r=[Wn, 'bs2'], w=[nmn, 'bs3'])
                    s.op('dve', lambda e: e.tensor_scalar(out=bs[:, 4:5], in0=bs[:, 3:4], scalar1=float(512 - nk), scalar2=None, op0=ALU.is_ge), r=['bs3'], w=['bs4'])
                    s.op('dve', lambda e: e.scalar_tensor_tensor(out=bs[:, 0:1], in0=bs[:, 4:5], scalar=bs[:, 1:2], in1=bs[:, 0:1], op0=ALU.mult, op1=ALU.add),
                         r=['bs4', 'bs1', 'bs0'], w=['bs0'])
                    s.op('dve', lambda e: e.tensor_scalar(out=bs[:, 1:2], in0=bs[:, 1:2], scalar1=0.5, scalar2=None, op0=ALU.mult), r=['bs1', 'bs0'], w=['bs1'])
                    s.op('dve', lambda e: e.tensor_scalar(out=bs[:, 2:3], in0=bs[:, 0:1], scalar1=bs[:, 1:2], scalar2=-1.0, op0=ALU.add, op1=ALU.mult),
                         r=['bs0', 'bs1'], w=['bs2'])
                    yield
                s.op('dve', lambda e: e.scalar_tensor_tensor(out=bs[:, 7:8], in0=bs[:, 1:2], scalar=2.0, in1=bs[:, 0:1], op0=ALU.mult, op1=ALU.add),
                     r=['bs1', 'bs0'], w=['bs7'])
                s.op('dve', lambda e: e.tensor_scalar(out=bs[:, 6:7], in0=bs[:, 7:8], scalar1=-1.0, scalar2=None, op0=ALU.mult), r=['bs7'], w=['bs6'])
                s.op('act', lambda e: e.activation(out=notm[sl][:, 0:nk], in_=Wt[:, 0:nk], func=AF.Sign, bias=bs[:, 6:7], scale=1.0, accum_out=bs[:, 3:4]),
                     r=[Wn, 'bs6'], w=[nmn, 'bs3'])
                s.op('dve', lambda e: e.tensor_scalar(out=bs[:, 5:6], in0=bs[:, 3:4], scalar1=-0.5, scalar2=float(256 - nk // 2), op0=ALU.mult, op1=ALU.add),
                     r=['bs3'], w=['bs5'])
                yield
                s.op('dve', lambda e: e.tensor_scalar(out=notm[sl][:, 0:nk], in0=Wt[:, 0:nk], scalar1=bs[:, 7:8], scalar2=None, op0=ALU.is_gt),
                     r=[Wn, 'bs7'], w=[nmn])
                s.op('dve', lambda e: e.tensor_scalar(out=cb[:, 0:nk], in0=Wt[:, 0:nk], scalar1=bs[:, 0:1], scalar2=None, op0=ALU.is_gt),
                     r=[Wn, 'bs0'], w=['cb'])
                yield
                s.op('dve', lambda e: e.scalar_tensor_tensor(out=cb[:, 0:nk], in0=Wt[:, 0:nk], scalar=bs[:, 7:8], in1=cb[:, 0:nk], op0=ALU.is_le, op1=ALU.mult),
                     r=[Wn, 'bs7', 'cb'], w=['cb'])
                yield
                s.op('dve', lambda e: e.tensor_tensor_scan(out=Wt[:, 0:nk], data0=g.onesb[:, 0:1].to_broadcast([128, nk]), data1=cb[:, 0:nk], initial=0.0, op0=ALU.mult, op1=ALU.add),
                     r=['onesb', 'cb', Wn], w=[Wn])
                yield
                s.op('dve', lambda e: e.scalar_tensor_tensor(out=cb[:, 0:nk], in0=Wt[:, 0:nk], scalar=bs[:, 5:6], in1=cb[:, 0:nk], op0=ALU.is_le, op1=ALU.mult),
                     r=[Wn, 'bs5', 'cb'], w=['cb'])
                yield
                s.op('dve', lambda e: e.tensor_tensor(out=notm[sl][:, 0:nk], in0=notm[sl][:, 0:nk], in1=cb[:, 0:nk], op=ALU.add), r=[nmn, 'cb'], w=[nmn])
                s.op('dve', lambda e: e.tensor_scalar(out=notm[sl][:, 0:nk], in0=notm[sl][:, 0:nk], scalar1=-1.0, scalar2=1.0, op0=ALU.mult, op1=ALU.add),
                     r=[nmn], w=[nmn])
            s.op('dve', lambda e: e.memset(notm[sl][0:64, nk - 64:nk], 1.0), r=[nmn], w=[nmn])
            yield

        def back(qb):
            sl = qb % 2
            s3 = qb % 3
            t0 = qb * 128
            cnt = 0
            for hg in range(2):
                for kb in range(qb + 1):
                    j = cnt % 2
                    cnt += 1
                    bank = 4 + j
                    s.op('pe', lambda e, bank=bank, kb=kb, hg=hg: e.matmul(P[:, bank * 512:(bank + 1) * 512], lhsT=ckvnT_all[:, kb * 128:(kb + 1) * 128],
                                                                          rhs=qlT[s3][:, hg * 512:(hg + 1) * 512], start=True, stop=False),
                         r=['ckvnT_all', 'qlT%d' % s3], w=['ps%d' % bank], inc=False)
                    s.op('pe', lambda e, bank=bank, kb=kb: e.matmul(P[:, bank * 512:(bank + 1) * 512], lhsT=notm[sl][:, kb * 128:(kb + 1) * 128],
                                                                   rhs=negI4[:], start=False, stop=True),
                         r=['notm%d' % sl, 'negI4'], w=['ps%d' % bank])
                    s.op('act', lambda e, bank=bank, j=j: e.activation(out=pT[j][:], in_=P[:, bank * 512:(bank + 1) * 512], func=AF.Exp, scale=0.125),
                         r=['ps%d' % bank], w=['pT%d' % j])
                    s.op('pe', lambda e, kb=kb, j=j: e.matmul(P[:, 6 * 512:7 * 512], lhsT=ckvn_all[:, kb, :], rhs=pT[j][:], start=(kb == 0), stop=(kb == qb)),
                         r=['ckvn_all', 'pT%d' % j], w=['ps6'], inc=False)
                    s.op('pe', lambda e, kb=kb, j=j: e.matmul(P[:, 7 * 512:8 * 512], lhsT=g.onesb[:], rhs=pT[j][:], start=(kb == 0), stop=(kb == qb)),
                         r=['onesb', 'pT%d' % j], w=['ps7'])
                    yield
                s.op('dve', lambda e: e.reciprocal(out=rden[:], in_=P[:, 7 * 512:8 * 512]), r=['ps7'], w=['rden'])
                s.op('dve', lambda e, hg=hg: e.tensor_tensor(out=oTn[:, hg * 512:(hg + 1) * 512], in0=P[:, 6 * 512:7 * 512], in1=rden[:], op=ALU.mult),
                     r=['ps6', 'rden'], w=['oTn'])
                yield
            for hp in range(4):
                for h2 in range(2):
                    h = 2 * hp + h2
                    s.op('pe', lambda e, hp=hp, h=h, h2=h2: e.matmul(P[:, 7 * 512 + hp * 128:7 * 512 + (hp + 1) * 128], lhsT=uvpad[:, h, :],
                                                                    rhs=oTn[:, (h // 2 + 4 * (h % 2)) * 128:(h // 2 + 4 * (h % 2) + 1) * 128], start=(h2 == 0), stop=(h2 == 1)),
                         r=['uvpad', 'oTn'], w=['ps7'], inc=(h2 == 1))
            s.op('act', lambda e: e.activation(out=yinT[s3][:, 512:1024], in_=P[:, 7 * 512:8 * 512], func=AF.Copy), r=['ps7'], w=['yinT%d' % s3])
            yield
            for half in range(2):
                b = 4 + half
                for kc in range(8):
                    s.op('pe', lambda e, kc=kc, half=half, b=b: e.matmul(P[:, b * 512:(b + 1) * 512], lhsT=yinT[s3][:, kc * 128:(kc + 1) * 128],
                                                                        rhs=wout[:, kc, half * 512:(half + 1) * 512], start=(kc == 0), stop=(kc == 7)),
                         r=['yinT%d' % s3, 'wout'], w=['ps%d' % b], inc=(kc == 7))
            yield
            emit_epilogue(g, L, (4, 5), xin[s3], 'xin%d' % s3, zb, xo[sl], 'xo%d' % sl, dst[t0:t0 + 128, :])
            yield

        nblk = g.nblk_dsa
        for it in range(nblk + 2):
            gens = []
            if it < nblk:
                gens.append([front_a(it), 1])
            if 1 <= it <= nblk:
                gens.append([topk(it - 1), 1])
            if it >= 2:
                gens.append([back(it - 2), 1])
            _interleave(gens)
        s.barrier()


def emit_gdn(g, src, dst):
    nc, s = g.nc, g.s
    P = g.ps
    with ExitStack() as les:
        L = Ctx()
        L.es = les
        A = lambda name, shape, dt: _sb(nc, les, name, shape, dt)
        WT = Ctx()

        def pre():
            WT.win = A("gwin", [128, 8, 4096], BF16)
            WT.wout = A("gwout", [128, 8, D], BF16)
            wind_ = g.d_o_w_in[0].rearrange("(kc p) n -> p kc n", p=128)
            for kc in range(8):
                for q4 in range(2):
                    s.dma('pool', WT.win[:, kc, q4 * 2048:(q4 + 1) * 2048], wind_[:, kc, q4 * 2048:(q4 + 1) * 2048], w=['gwin'])
            woutd_ = g.d_o_w_out[0].rearrange("(kc p) n -> p kc n", p=128)
            for kc in range(8):
                s.dma('pool', WT.wout[:, kc, :], woutd_[:, kc, :], w=['gwout'])
        emit_mods(g, L, g.d_o_mod_w[0], g.d_o_mod_b[0], g.d_o_ln_g[0], g.d_o_ln_b[0], pre=pre)
        win = WT.win
        wbf = A("gwbf", [128, 8, 16], F32)
        wbb = A("gwbb", [128, 8, 16], BF16)
        wout = WT.wout
        cw = A("gcw", [128, 24, 4], F32)
        msl = A("msl", [128, 128], F32)
        mil = A("mil", [128, 128], F32)
        triu = A("triu", [128, 128], F32)
        mdm = A("mdm", [128, 5, 128], BF16)
        mdmT = A("mdmT", [128, 5, 128], BF16)
        alog = A("alog", [128, 8], F32)
        dtb = A("dtb", [128, 8], F32)
        onb = A("onb", [128, D], F32)
        xin = [A("gxin%d" % i, [128, D], F32) for i in range(1)]
        xrs = A("gxrs", [128, D], F32)
        hT = A("ghT", [128, 8, 128], BF16)
        xw = [A("xw%d" % i, [128, 131], F32) for i in range(2)]
        halo = A("ghalo", [128, 24, 3], F32)
        cbuf = [A("cbuf%d" % i, [128, 128], F32) for i in range(2)]
        act = [A("gact%d" % i, [128, 24 * 128], F32) for i in range(2)]
        sq = A("gsq", [128, 1024], F32)
        rstd = sq
        qkb = [A("qkb%d" % i, [128, 2048], BF16) for i in range(2)]
        gsil = [A("gsil%d" % i, [128, D], BF16) for i in range(3)]
        ba = A("ba", [128, 16], F32)
        tmpa = A("tmpa", [128, 8], F32)
        smb = [A("smb%d" % i, [128, 48], F32) for i in range(2)]
        zsm = A("zsm", [128, 8], F32)
        HP = []
        for p in range(2):
            h_ = Ctx()
            h_.sm = A("hsm%d" % p, [128, 2], F32)
            for nm in ("E", "EB", "t1", "Nf", "atf"):
                setattr(h_, nm, A("h%s%d" % (nm, p), [128, 128], F32))
            for nm in ("X", "Xt", "Q2", "Q2t", "Q4", "Q4t", "Yv", "Yt", "attT", "qdT", "kd", "Kbg", "Vb", "nwT", "vnew"):
                setattr(h_, nm, A("h%s%d" % (nm, p), [128, 128], BF16))
            h_.NM = A("hNM%d" % p, [128, 5, 128], BF16)
            h_.NMt = A("hNMt%d" % p, [128, 5, 128], BF16)
            HP.append(h_)
        Sf = A("gSf", [128, 8, 128], F32)
        Sb = A("gSb", [128, 8, 128], BF16)
        osb = [A("gosb%d" % i, [128, D], F32) for i in range(2)]
        yinT = A("gyinT", [128, 1024], BF16)
        zb = A("gzb", [128, D], F32)
        xo = [A("gxo%d" % i, [128, D], F32) for i in range(1)]
        wind = g.d_o_w_in[0].rearrange("(kc p) n -> p kc n", p=128)
        s.dma('sp', wbf[:], wind[:, :, 4096:4112], w=['gwbf'])
        s.op('dve', lambda e: e.tensor_copy(out=wbb[:], in_=wbf[:]), r=['gwbf'], w=['gwbb'])
        s.dma('sp', cw[:], g.d_o_cw[:, :, :], w=['gcw'])
        s.dma('sp', msl[:], g.d_msl[:, :], w=['msl'])
        s.dma('sp', mil[:], g.d_mil[:, :], w=['mil'])
        s.dma('sp', triu[:], g.d_triu[:, :], w=['triu'])
        s.dma('pool', mdm[:], g.d_mdm[:, :, :], w=['mdm'])
        s.dma('pool', mdmT[:], g.d_mdmT[:, :, :], w=['mdmT'])
        s.dma('sp', alog[:], g.d_o_a_log[0].partition_broadcast(128), w=['alog'])
        s.dma('sp', dtb[:], g.d_o_dt_bias[0].partition_broadcast(128), w=['dtb'])
        s.dma('sp', onb[:], g.d_onb[0].partition_broadcast(128), w=['onb'])
        s.op('act', lambda e: e.activation(out=alog[:], in_=alog[:], func=AF.Exp), r=['alog'], w=['alog'])
        s.op('dve', lambda e: e.tensor_scalar(out=alog[:], in0=alog[:], scalar1=-1.0, scalar2=None, op0=ALU.mult), r=['alog'], w=['alog'])
        s.op('dve', lambda e: e.memset(halo[:], 0.0), w=['ghalo'])
        s.op('dve', lambda e: e.memset(Sf[:], 0.0), w=['gSf0', 'gSf1'])
        s.op('dve', lambda e: e.memset(Sb[:], 0.0), w=['gSb0', 'gSb1'])
        DK = float(128 ** -0.5)

        def phaseA(qb):
            t0 = qb * 128
            bp = qb % 2
            x3 = qb % 3
            xn = 'gxin0'
            an = 'gact%d' % bp
            sn = 'smb%d' % bp
            sm = smb[bp]
            ac = act[bp]
            s.dma('sp', xin[0][:], src[t0:t0 + 128, :], w=[xn])
            emit_hT(g, L, xin[0], xn, lambda kc: hT[:, kc, :], 'ghT', (0, 1))
            yield
            for ch in range(24):
                b = ch % 2
                cb = cbuf[b]
                cn = 'cbuf%d' % b
                for kc in range(8):
                    s.op('pe', lambda e, kc=kc, ch=ch, b=b: e.matmul(P[:, b * 512:b * 512 + 128], lhsT=win[:, kc, ch * 128:(ch + 1) * 128], rhs=hT[:, kc, :],
                                                                    start=(kc == 0), stop=(kc == 7)), r=['gwin', 'ghT'], w=['ps%d' % b], inc=(kc == 7))
                xb = xw[b]
                xbn = 'xw%d' % b
                s.op('pool', lambda e, ch=ch, xb=xb: e.tensor_copy(out=xb[:, 0:3], in_=halo[:, ch, :]), r=['ghalo'], w=[xbn])
                s.op('act', lambda e, b=b, xb=xb: e.activation(out=xb[:, 3:131], in_=P[:, b * 512:b * 512 + 128], func=AF.Copy), r=['ps%d' % b], w=[xbn])
                s.op('act', lambda e, ch=ch, b=b, cb=cb: e.activation(out=cb[:], in_=P[:, b * 512:b * 512 + 128], func=AF.Identity, scale=cw[:, ch, 3:4]),
                     r=['ps%d' % b, 'gcw'], w=[cn])
                s.op('pool', lambda e, ch=ch, xb=xb: e.tensor_copy(out=halo[:, ch, :], in_=xb[:, 128:131]), r=[xbn], w=['ghalo'])
                for j in range(3):
                    s.op('dve', lambda e, ch=ch, j=j, cb=cb, xb=xb: e.scalar_tensor_tensor(out=cb[:], in0=xb[:, j:j + 128], scalar=cw[:, ch, j:j + 1], in1=cb[:],
                                                                                          op0=ALU.mult, op1=ALU.add), r=[xbn, 'gcw', cn], w=[cn])
                s.op('act', lambda e, ch=ch, cb=cb: e.activation(out=ac[:, ch * 128:(ch + 1) * 128], in_=cb[:], func=AF.Silu), r=[cn], w=[an])
                yield
            for hf in range(2):
                seg = ac[:, hf * 1024:(hf + 1) * 1024]
                s.op('dve', lambda e, seg=seg: e.tensor_tensor(out=sq[:], in0=seg, in1=seg, op=ALU.mult), r=[an], w=['gsq'])
                yield
                for j in range(2):
                    b = j % 2
                    s.op('pe', lambda e, j=j, b=b: e.matmul(P[:, b * 512:(b + 1) * 512], lhsT=g.ones[:], rhs=sq[:, j * 512:(j + 1) * 512], start=True, stop=True),
                         r=['ones', 'gsq'], w=['ps%d' % b])
                for j in range(2):
                    b = j % 2
                    s.op('dve', lambda e, j=j, b=b: e.tensor_scalar(out=sq[:, j * 512:(j + 1) * 512], in0=P[:, b * 512:(b + 1) * 512], scalar1=RMS_EPS, scalar2=None, op0=ALU.add),
                         r=['ps%d' % b], w=['gsq'])
                yield
                s.op('act', lambda e: e.activation(out=sq[:], in_=sq[:], func=AF.Sqrt), r=['gsq'], w=['gsq'])
                s.op('dve', lambda e: e.reciprocal(out=sq[:], in_=sq[:]), r=['gsq'], w=['gsq'])
                yield
                s.op('dve', lambda e, seg=seg: e.tensor_tensor(out=seg, in0=seg, in1=sq[:], op=ALU.mult), r=[an, 'gsq'], w=[an])
                s.op('act', lambda e, seg=seg, hf=hf: e.activation(out=qkb[bp][:, hf * 1024:(hf + 1) * 1024], in_=seg, func=AF.Copy), r=[an], w=['qkb%d' % bp])
                yield
            for half in range(2):
                b = half
                for kc in range(8):
                    s.op('pe', lambda e, kc=kc, half=half, b=b: e.matmul(P[:, b * 512:(b + 1) * 512], lhsT=hT[:, kc, :],
                                                                        rhs=win[:, kc, 3072 + half * 512:3072 + (half + 1) * 512],
                                                                        start=(kc == 0), stop=(kc == 7)), r=['gwin', 'ghT'], w=['ps%d' % b], inc=(kc == 7))
                s.op('act', lambda e, half=half, b=b: e.activation(out=gsil[x3][:, half * 512:(half + 1) * 512], in_=P[:, b * 512:(b + 1) * 512], func=AF.Silu),
                     r=['ps%d' % b], w=['gsil%d' % x3])
                yield
            for kc in range(8):
                s.op('pe', lambda e, kc=kc: e.matmul(P[:, 0:16], lhsT=hT[:, kc, :], rhs=wbb[:, kc, :], start=(kc == 0), stop=(kc == 7)),
                     r=['gwbb', 'ghT'], w=['ps0'], inc=(kc == 7))
            s.op('dve', lambda e: e.tensor_copy(out=ba[:], in_=P[:, 0:16]), r=['ps0'], w=['ba'])
            yield
            s.op('act', lambda e: e.activation(out=sm[:, 0:8], in_=ba[:, 0:8], func=AF.Exp, scale=-1.0), r=['ba'], w=[sn])
            s.op('dve', lambda e: e.tensor_scalar(out=sm[:, 0:8], in0=sm[:, 0:8], scalar1=1.0, scalar2=None, op0=ALU.add), r=[sn], w=[sn])
            s.op('dve', lambda e: e.reciprocal(out=sm[:, 0:8], in_=sm[:, 0:8]), r=[sn], w=[sn])
            s.op('dve', lambda e: e.tensor_scalar(out=sm[:, 32:40], in0=sm[:, 0:8], scalar1=-1.0, scalar2=None, op0=ALU.mult), r=[sn], w=[sn])
            yield
            s.op('dve', lambda e: e.tensor_tensor(out=tmpa[:], in0=ba[:, 8:16], in1=dtb[:], op=ALU.add), r=['ba', 'dtb'], w=['tmpa'])
            s.op('act', lambda e: e.activation(out=tmpa[:], in_=tmpa[:], func=AF.Exp), r=['tmpa'], w=['tmpa'])
            s.op('act', lambda e: e.activation(out=tmpa[:], in_=tmpa[:], func=AF.Ln, bias=1.0), r=['tmpa'], w=['tmpa'])
            s.op('dve', lambda e: e.tensor_tensor(out=sm[:, 8:16], in0=tmpa[:], in1=alog[:], op=ALU.mult), r=['tmpa', 'alog', sn], w=[sn])
            yield
            s.op('pe', lambda e: e.matmul(P[:, 16:24], lhsT=triu[:], rhs=sm[:, 8:16], start=True, stop=True), r=['triu', sn], w=['ps0'])
            s.op('dve', lambda e: e.tensor_copy(out=sm[:, 16:24], in_=P[:, 16:24]), r=['ps0', sn], w=[sn])
            s.op('act', lambda e: e.activation(out=sm[:, 24:32], in_=sm[:, 16:24], func=AF.Exp), r=[sn], w=[sn])
            s.op('dve', lambda e: e.tensor_tensor(out=sm[:, 40:48], in0=sm[:, 24:32], in1=sm[:, 0:8], op=ALU.mult), r=[sn], w=[sn])
            yield

        def heads(qb, p):
            bp = qb % 2
            sm = smb[bp]
            sn = 'smb%d' % bp
            an = 'gact%d' % bp
            ac = act[bp]
            H = HP[p]
            X0 = (2 + 3 * p) * 512
            Y0 = X0 + 512
            Z0 = X0 + 1024
            xn_, yn_, zn_ = 'ps%d' % (2 + 3 * p), 'ps%d' % (3 + 3 * p), 'ps%d' % (4 + 3 * p)
            n = lambda nm: 'h%s%d' % (nm, p)
            for h in range(p, 8, 2):
                qn = ac[:, h * 128:(h + 1) * 128]
                kn = ac[:, (8 + h) * 128:(9 + h) * 128]
                vv = ac[:, (16 + h) * 128:(17 + h) * 128]
                qnb = qkb[bp][:, h * 128:(h + 1) * 128]
                knb = qkb[bp][:, (8 + h) * 128:(9 + h) * 128]
                qbn = 'qkb%d' % bp
                s.op('dve', lambda e, h=h: e.tensor_scalar(out=H.t1[:], in0=g.ident[:], scalar1=sm[:, 16 + h:17 + h], scalar2=None, op0=ALU.mult),
                     r=['ident', sn], w=[n('t1')])
                s.op('pe', lambda e: e.matmul(P[:, X0:X0 + 128], lhsT=g.ones[:], rhs=H.t1[:], start=True, stop=True), r=['ones', n('t1')], w=[xn_])
                s.op('dve', lambda e, h=h: e.tensor_scalar(out=H.E[:], in0=P[:, X0:X0 + 128], scalar1=sm[:, 16 + h:17 + h], scalar2=0.0, op0=ALU.subtract, op1=ALU.max),
                     r=[xn_, sn], w=[n('E')])
                s.op('act', lambda e: e.activation(out=H.E[:], in_=H.E[:], func=AF.Exp, scale=-1.0), r=[n('E')], w=[n('E')])
                s.op('act', lambda e: e.activation(out=H.EB[:], in_=P[:, X0:X0 + 128], func=AF.Exp), r=[xn_], w=[n('EB')])
                s.op('act', lambda e: e.activation(out=H.sm[:, 0:1], in_=P[:, X0 + 127:X0 + 128], func=AF.Exp), r=[xn_], w=[n('sm0')])
                s.op('dve', lambda e, h=h: e.tensor_scalar(out=H.sm[:, 1:2], in0=P[:, X0 + 127:X0 + 128], scalar1=sm[:, 16 + h:17 + h], scalar2=None, op0=ALU.subtract),
                     r=[xn_, sn], w=[n('sm1')])
                s.op('act', lambda e: e.activation(out=H.sm[:, 1:2], in_=H.sm[:, 1:2], func=AF.Exp), r=[n('sm1')], w=[n('sm1')])
                yield
                s.op('pe', lambda e, knb=knb: e.matmul(P[:, X0 + 128:X0 + 256], lhsT=knb, rhs=knb, start=True, stop=True), r=[qbn], w=[xn_])
                s.op('pe', lambda e, knb=knb, qnb=qnb: e.matmul(P[:, X0 + 256:X0 + 384], lhsT=qnb, rhs=knb, start=True, stop=True), r=[qbn], w=[xn_])
                s.op('dve', lambda e: e.tensor_tensor(out=H.t1[:], in0=H.E[:], in1=msl[:], op=ALU.mult), r=[n('E'), 'msl'], w=[n('t1')])
                s.op('dve', lambda e, h=h: e.scalar_tensor_tensor(out=H.Nf[:], in0=P[:, X0 + 128:X0 + 256], scalar=sm[:, 32 + h:33 + h], in1=H.t1[:], op0=ALU.mult, op1=ALU.mult),
                     r=[xn_, sn, n('t1')], w=[n('Nf')])
                s.op('dve', lambda e: e.tensor_tensor(out=H.t1[:], in0=H.E[:], in1=mil[:], op=ALU.mult), r=[n('E'), 'mil', n('Nf')], w=[n('t1')])
                s.op('dve', lambda e: e.scalar_tensor_tensor(out=H.atf[:], in0=P[:, X0 + 256:X0 + 384], scalar=DK, in1=H.t1[:], op0=ALU.mult, op1=ALU.mult),
                     r=[xn_, n('t1')], w=[n('atf')])
                yield
                s.op('pe', lambda e: e.transpose(P[:, Y0:Y0 + 128], H.Nf[:], g.ident[:]), r=[n('Nf'), 'ident'], w=[yn_])
                s.op('pe', lambda e: e.transpose(P[:, Y0 + 128:Y0 + 256], H.atf[:], g.ident[:]), r=[n('atf'), 'ident'], w=[yn_])
                s.op('pe', lambda e, kn=kn: e.transpose(P[:, Y0 + 256:Y0 + 384], kn, g.ident[:]), r=[an, 'ident'], w=[yn_])
                s.op('pe', lambda e, vv=vv: e.transpose(P[:, Y0 + 384:Y0 + 512], vv, g.ident[:]), r=[an, 'ident'], w=[yn_])
                s.op('dve', lambda e: e.tensor_tensor(out=H.NM[:], in0=H.Nf[:].unsqueeze(1).to_broadcast([128, 5, 128]), in1=mdm[:], op=ALU.mult),
                     r=[n('Nf'), 'mdm'], w=[n('NM')])
                s.op('dve', lambda e: e.tensor_tensor(out=H.NMt[:], in0=P[:, Y0:Y0 + 128].unsqueeze(1).to_broadcast([128, 5, 128]), in1=mdmT[:], op=ALU.mult),
                     r=[yn_, 'mdmT'], w=[n('NMt')])
                s.op('act', lambda e: e.activation(out=H.attT[:], in_=P[:, Y0 + 128:Y0 + 256], func=AF.Copy), r=[yn_], w=[n('attT')])
                s.op('dve', lambda e: e.tensor_scalar(out=H.kd[:], in0=P[:, Y0 + 256:Y0 + 384], scalar1=H.sm[:, 1:2], scalar2=None, op0=ALU.mult), r=[yn_, n('sm1')], w=[n('kd')])
                s.op('dve', lambda e, h=h: e.tensor_scalar(out=H.Kbg[:], in0=P[:, Y0 + 256:Y0 + 384], scalar1=sm[:, 40 + h:41 + h], scalar2=None, op0=ALU.mult),
                     r=[yn_, sn], w=[n('Kbg')])
                s.op('dve', lambda e, h=h: e.tensor_scalar(out=H.Vb[:], in0=P[:, Y0 + 384:Y0 + 512], scalar1=sm[:, h:h + 1], scalar2=None, op0=ALU.mult), r=[yn_, sn], w=[n('Vb')])
                s.op('dve', lambda e, qn=qn: e.scalar_tensor_tensor(out=H.qdT[:], in0=qn, scalar=DK, in1=H.EB[:], op0=ALU.mult, op1=ALU.mult),
                     r=[an, n('EB')], w=[n('qdT')])
                yield
                zc = [0]

                def mm(lhsT, rhs, rn):
                    c0 = Z0 + (zc[0] % 4) * 128
                    zc[0] += 1
                    s.op('pe', lambda e: e.matmul(P[:, c0:c0 + 128], lhsT=lhsT, rhs=rhs, start=True, stop=True), r=rn, w=[zn_])
                    return P[:, c0:c0 + 128]

                def cp(dst, dn, src_ps):
                    s.op('act', lambda e: e.activation(out=dst, in_=src_ps, func=AF.Copy), r=[zn_], w=[dn])

                def acc(dst, dn, src_ps):
                    s.op('dve', lambda e: e.tensor_tensor(out=dst, in0=src_ps, in1=dst, op=ALU.add), r=[zn_, dn], w=[dn])

                M0, M0t = H.NM[:, 0, :], H.NMt[:, 0, :]
                s.op('dve', lambda e: e.tensor_tensor(out=H.X[:], in0=M0, in1=g.identb[:], op=ALU.add), r=[n('NM'), 'identb'], w=[n('X')])
                s.op('dve', lambda e: e.tensor_tensor(out=H.Xt[:], in0=M0t, in1=g.identb[:], op=ALU.add), r=[n('NMt'), 'identb'], w=[n('Xt')])
                cp(H.Q2[:], n('Q2'), mm(M0t, M0, [n('NM'), n('NMt')]))
                cp(H.Q2t[:], n('Q2t'), mm(M0, M0t, [n('NM'), n('NMt')]))
                yield
                acc(H.X[:], n('X'), mm(H.Q2t[:], H.X[:], [n('Q2t'), n('X')]))
                acc(H.Xt[:], n('Xt'), mm(H.Q2[:], H.Xt[:], [n('Q2'), n('Xt')]))
                cp(H.Q4[:], n('Q4'), mm(H.Q2t[:], H.Q2[:], [n('Q2'), n('Q2t')]))
                cp(H.Q4t[:], n('Q4t'), mm(H.Q2[:], H.Q2t[:], [n('Q2'), n('Q2t')]))
                yield
                acc(H.X[:], n('X'), mm(H.Q4t[:], H.X[:], [n('Q4t'), n('X')]))
                acc(H.Xt[:], n('Xt'), mm(H.Q4[:], H.Xt[:], [n('Q4'), n('Xt')]))
                yield
                for lv in range(1, 5):
                    Nb, Nbt = H.NM[:, lv, :], H.NMt[:, lv, :]
                    if lv < 4:
                        cp(H.Yv[:], n('Yv'), mm(Nbt, H.X[:], [n('NMt'), n('X')]))
                    cp(H.Yt[:], n('Yt'), mm(Nb, H.Xt[:], [n('NM'), n('Xt')]))
                    yield
                    pa = mm(H.Xt[:], H.Yv[:], [n('Xt'), n('Yv')]) if lv < 4 else None
                    pb = mm(H.X[:], H.Yt[:], [n('X'), n('Yt')])
                    if lv < 4:
                        acc(H.X[:], n('X'), pa)
                    acc(H.Xt[:], n('Xt'), pb)
                    yield
                pw = mm(H.Kbg[:], H.Xt[:], [n('Kbg'), n('Xt')])
                s.op('act', lambda e: e.activation(out=H.nwT[:], in_=pw, func=AF.Copy, scale=-1.0), r=[zn_], w=[n('nwT')])
                yield
                s.op('pe', lambda e: e.matmul(P[:, Y0:Y0 + 128], lhsT=H.Xt[:], rhs=H.Vb[:], start=True, stop=False), r=[n('Xt'), n('Vb')], w=[yn_], inc=False)
                s.op('pe', lambda e, h=h: e.matmul(P[:, Y0:Y0 + 128], lhsT=H.nwT[:], rhs=Sb[:, h, :], start=False, stop=True), r=[n('nwT'), 'gSb%d' % p], w=[yn_])
                s.op('act', lambda e: e.activation(out=H.vnew[:], in_=P[:, Y0:Y0 + 128], func=AF.Copy), r=[yn_], w=[n('vnew')])
                yield
                s.op('pe', lambda e, h=h: e.matmul(P[:, X0 + 384:X0 + 512], lhsT=H.qdT[:], rhs=Sb[:, h, :], start=True, stop=False),
                     r=[n('qdT'), 'gSb%d' % p], w=[xn_], inc=False)
                s.op('pe', lambda e: e.matmul(P[:, X0 + 384:X0 + 512], lhsT=H.attT[:], rhs=H.vnew[:], start=False, stop=True),
                     r=[n('attT'), n('vnew')], w=[xn_])
                s.op('act', lambda e, h=h: e.activation(out=osb[bp][:, h * 128:(h + 1) * 128], in_=P[:, X0 + 384:X0 + 512], func=AF.Copy),
                     r=[xn_], w=['gosb%d_%d' % (bp, p)])
                s.op('pe', lambda e: e.matmul(P[:, Y0 + 128:Y0 + 256], lhsT=H.kd[:], rhs=H.vnew[:], start=True, stop=True), r=[n('kd'), n('vnew')], w=[yn_])
                s.op('dve', lambda e, h=h: e.scalar_tensor_tensor(out=Sf[:, h, :], in0=Sf[:, h, :], scalar=H.sm[:, 0:1], in1=P[:, Y0 + 128:Y0 + 256], op0=ALU.mult, op1=ALU.add),
                     r=['gSf%d' % p, n('sm0'), yn_], w=['gSf%d' % p])
                s.op('act', lambda e, h=h: e.activation(out=Sb[:, h, :], in_=Sf[:, h, :], func=AF.Copy), r=['gSf%d' % p], w=['gSb%d' % p])
                yield

        def phaseZ(qb):
            t0 = qb * 128
            bp = qb % 2
            x3 = qb % 3
            ob = osb[bp]
            on = ['gosb%d_0' % bp, 'gosb%d_1' % bp]
            s.op('dve', lambda e: e.tensor_tensor(out=zb[:], in0=ob[:], in1=ob[:], op=ALU.mult), r=on, w=['zb'])
            for h in range(8):
                s.op('dve', lambda e, h=h: e.reduce_sum(out=zsm[:, h:h + 1], in_=zb[:, h * 128:(h + 1) * 128], axis=AX.X), r=['zb'], w=['zsm'])
            yield
            s.op('dve', lambda e: e.tensor_scalar(out=zsm[:], in0=zsm[:], scalar1=1.0 / 128, scalar2=RMS_EPS, op0=ALU.mult, op1=ALU.add), r=['zsm'], w=['zsm'])
            s.op('act', lambda e: e.activation(out=zsm[:], in_=zsm[:], func=AF.Sqrt), r=['zsm'], w=['zsm'])
            s.op('dve', lambda e: e.reciprocal(out=zsm[:], in_=zsm[:]), r=['zsm'], w=['zsm'])
            yield
            for h in range(8):
                s.op('dve', lambda e, h=h: e.tensor_scalar(out=ob[:, h * 128:(h + 1) * 128], in0=ob[:, h * 128:(h + 1) * 128], scalar1=zsm[:, h:h + 1],
                                                           scalar2=None, op0=ALU.mult), r=on + ['zsm'], w=on)
            yield
            s.op('dve', lambda e: e.tensor_tensor(out=ob[:], in0=ob[:], in1=onb[:], op=ALU.mult), r=on + ['onb'], w=on)
            s.op('dve', lambda e: e.tensor_tensor(out=ob[:], in0=ob[:], in1=gsil[x3][:], op=ALU.mult), r=on + ['gsil%d' % x3], w=on)
            yield
            for kc in range(8):
                b = kc // 4
                off = b * 512 + (kc % 4) * 128
                s.op('pe', lambda e, kc=kc, off=off: e.transpose(P[:, off:off + 128], ob[:, kc * 128:(kc + 1) * 128], g.ident[:]), r=on + ['ident'], w=['ps%d' % b])
            for j in range(2):
                s.op('act', lambda e, j=j: e.activation(out=yinT[:, j * 512:(j + 1) * 512], in_=P[:, j * 512:(j + 1) * 512], func=AF.Copy),
                     r=['ps%d' % j], w=['gyinT'])
            yield
            for half in range(2):
                b = half
                for kc in range(8):
                    s.op('pe', lambda e, kc=kc, half=half, b=b: e.matmul(P[:, b * 512:(b + 1) * 512], lhsT=yinT[:, kc * 128:(kc + 1) * 128],
                                                                        rhs=wout[:, kc, half * 512:(half + 1) * 512], start=(kc == 0), stop=(kc == 7)),
                         r=['gyinT', 'gwout'], w=['ps%d' % b], inc=(kc == 7))
            s.dma('sp', xrs[:], src[t0:t0 + 128, :], w=['gxrs'])
            emit_epilogue(g, L, (0, 1), xrs, 'gxrs', zb, xo[0], 'gxo0', dst[t0:t0 + 128, :])
            yield

        nblk = g.nblk_gdn
        for it in range(nblk + 2):
            gens = []
            if 1 <= it <= nblk:
                gens.append([heads(it - 1, 0), 1])
                gens.append([heads(it - 1, 1), 1])
            if it < nblk:
                gens.append([phaseA(it), 1])
            if it >= 2:
                gens.append([phaseZ(it - 2), 1])
            _interleave(gens)
        s.barrier()


W_SPECS = [
    ("e_mod_w", [1, D, 3 * D]), ("e_mod_b", [1, 1, 3 * D]), ("e_ln_g", [1, D]), ("e_ln_b", [1, D]),
    ("o_mod_w", [1, D, 3 * D]), ("o_mod_b", [1, 1, 3 * D]), ("o_ln_g", [1, D]), ("o_ln_b", [1, D]),
    ("f_mod_w", [2, D, 3 * D]), ("f_mod_b", [2, 1, 3 * D]), ("f_ln_g", [2, D]), ("f_ln_b", [2, D]),
    ("f_w_up", [2, D, 2 * DFF]), ("f_w_down", [2, DFF, D]), ("f_cw", [2, 128, 2 * NFC, 4]),
    ("ident", [128, 128]),
    ("e_w_in", [1, D, 1476]), ("e_pool_w", [1, 4, 128, 128]), ("pscale", [128, 4]), ("e_kv_norm", [1, 128]),
    ("o_w_in", [1, D, 4112]), ("o_w_out", [1, D, D]), ("o_cw", [128, 24, 4]), ("msl", [128, 128]), ("mil", [128, 128]), ("triu", [128, 128]), ("mdm", [128, 5, 128]), ("mdmT", [128, 5, 128]),
    ("o_a_log", [1, 8]), ("o_dt_bias", [1, 8]), ("onb", [1, D]),
    ("ukT", [128, 4, 128]), ("uvpad", [128, 8, 128]), ("e_w_out", [1, D, D]), ("negI4", [128, 512]), ("corr", [128, 4, 15]),
]


def build(stages):
    nc = bass.Bass("TRN2", target_bir_lowering=False)
    g = Ctx()
    g.nc = nc
    g.d_x = nc.dram_tensor("x", [S, D], F32, kind="ExternalInput").ap()
    g.d_ccol = nc.dram_tensor("ccol", [128, 8], F32, kind="ExternalInput").ap()
    for nm, shp in W_SPECS:
        setattr(g, "d_" + nm, nc.dram_tensor(nm, shp, F32, kind="ExternalInput").ap())
    g.d_out = nc.dram_tensor("out", [S, D], F32, kind="ExternalOutput").ap()
    scr = [nc.dram_tensor("xscr%d" % i, [S, D], F32, kind="Internal").ap() for i in range(3)]
    with ExitStack() as es:
        g.es = es
        g.s = Sch(nc, es)
        g.ps = es.enter_context(nc.psum_tensor("ps", [128, 8 * 512], F32))
        g.lnst = _sb(nc, es, "lnst", [128, 12], F32)
        g.lnmv = _sb(nc, es, "lnmv", [128, 8], F32)
        emit_consts(g)
        bufs = [g.d_x] + scr
        n = len(stages)
        for i, st in enumerate(stages):
            src = g.d_x if i == 0 else scr[(i - 1) % 3]
            dst = g.d_out if i == n - 1 else scr[i % 3]
            if st[0] == 'ffn':
                emit_ffn(g, st[1], src, dst)
            elif st[0] == 'gdn':
                g.nblk_gdn = st[1] if len(st) > 1 else NB
                emit_gdn(g, src, dst)
            elif st[0] == 'dsa':
                g.nblk_dsa = st[1] if len(st) > 1 else NB
                emit_dsa(g, src, dst)
            else:
                raise ValueError(st)
        g.s.finish()
    return nc


def prep_weights(inp):
    f = lambda a: np.ascontiguousarray(np.asarray(a, dtype=np.float32))
    w = {}
    for k in ("e_mod_w", "e_ln_g", "e_ln_b", "o_mod_w", "o_ln_g", "o_ln_b", "f_mod_w", "f_ln_g", "f_ln_b", "f_w_up", "f_w_down"):
        w[k] = f(inp[k])
    for k in ("e_mod_b", "o_mod_b", "f_mod_b"):
        a = f(inp[k])
        w[k] = np.ascontiguousarray(a.reshape(a.shape[0], 1, 3 * D))
    cwt = f(inp["f_conv_w"])
    cb = f(inp["f_conv_b"])
    a = np.concatenate([cwt, cb[:, None, :]], axis=1)
    a = a.reshape(2, 4, 2 * NFC, 128).transpose(0, 3, 2, 1)
    w["f_cw"] = np.ascontiguousarray(a)
    w["ident"] = np.eye(128, dtype=np.float32)
    for k in ("e_w_in", "e_pool_w", "e_kv_norm", "e_w_out"):
        w[k] = f(inp[k])
    w["pscale"] = np.ascontiguousarray(f(inp["e_pool_scale"])[0].reshape(4, 128).T)
    uk = f(inp["e_w_uk"])[0]
    w["ukT"] = np.ascontiguousarray(uk.reshape(4, 2, 128, 64).transpose(1, 3, 0, 2).reshape(128, 4, 128))
    uv = f(inp["e_w_uv"])[0]
    uvp = np.zeros((128, 8, 128), np.float32)
    for h in range(8):
        uvp[:, h, (h % 2) * 64:(h % 2) * 64 + 64] = uv[h]
    w["uvpad"] = uvp
    w["negI4"] = np.ascontiguousarray(np.tile(-30000.0 * np.eye(128, dtype=np.float32), (1, 4)))
    corr = np.ones((128, 4, 15), np.float32)
    for gi in range(4):
        win_ = 2 << gi
        for t in range(win_ - 1):
            corr[:, gi, t] = win_ / (t + 1.0)
    w["corr"] = corr
    for k in ("o_w_in", "o_w_out", "o_a_log", "o_dt_bias"):
        w[k] = f(inp[k])
    ocw = f(inp["o_conv_w"])[0]
    w["o_cw"] = np.ascontiguousarray(ocw.reshape(4, 24, 128).transpose(2, 1, 0))
    ar = np.arange(128)
    w["msl"] = (ar[:, None] > ar[None, :]).astype(np.float32)
    w["mil"] = (ar[:, None] >= ar[None, :]).astype(np.float32)
    w["triu"] = (ar[:, None] <= ar[None, :]).astype(np.float32)
    w["onb"] = np.ascontiguousarray(np.tile(f(inp["o_out_norm"])[0], 8)[None, :])
    mdm = np.zeros((128, 5, 128), np.float32)
    mdm[:, 0, :] = (ar[:, None] // 8 == ar[None, :] // 8)
    for li, bsz in enumerate((8, 16, 32, 64)):
        bl = ar // bsz
        mdm[:, 1 + li, :] = (bl[:, None] % 2 == 1) & (bl[None, :] == bl[:, None] - 1)
    w["mdm"] = mdm
    w["mdmT"] = np.ascontiguousarray(mdm.transpose(2, 1, 0))
    return w


STAGES = [('dsa',), ('ffn', 0), ('gdn',), ('ffn', 1)]


def kernel(**inp):
    x = np.asarray(inp["x"], dtype=np.float32)
    c = np.asarray(inp["c"], dtype=np.float32)
    w = prep_weights(inp)
    nc = build(STAGES)
    in_maps = []
    for b in range(8):
        m = dict(w)
        m["x"] = np.ascontiguousarray(x[b])
        m["ccol"] = np.ascontiguousarray(c[b].reshape(8, 128).T)
        in_maps.append(m)
    res = run_bass_kernel_spmd(nc, in_maps, core_ids=list(range(8)))
    return np.stack([np.asarray(r["out"], dtype=np.float32) for r in res.results], axis=0)
```

```python
import os
import numpy as np
from contextlib import ExitStack
import concourse.bass as bass
import concourse.mybir as mybir
from concourse.bass_utils import run_bass_kernel_spmd

F32 = mybir.dt.float32
BF16 = mybir.dt.bfloat16
AF = mybir.ActivationFunctionType
ALU = mybir.AluOpType
AX = mybir.AxisListType

D = 1024
S = 4096
NB = S // 128
DFF = 2688
NFC = DFF // 128
ALPHA = float(4 ** 0.25)
LN_EPS = 1e-5
RMS_EPS = 1e-6
NDS = 12


class Sch:
    def __init__(self, nc, es):
        self.nc = nc
        self.E = {'pe': nc.tensor, 'act': nc.scalar, 'dve': nc.vector,
                  'pool': nc.gpsimd, 'sp': nc.sync}
        self.sem = {}
        for e in self.E:
            self.sem[e] = es.enter_context(nc.semaphore('s_' + e))
        self.cnt = {e: 0 for e in self.E}
        self.waited = {e: {} for e in self.E}
        self.lastw = {}
        self.readers = {}
        self.dq = ('sp', 'pool', 'act')
        self.dcnt = {}
        self.drr = {q: 0 for q in self.dq}
        for q in self.dq:
            for i in range(NDS):
                k = (q, i)
                self.sem[k] = es.enter_context(nc.semaphore('d_%s%d' % (q, i)))
                self.dcnt[k] = 0
        self.nwaits = 0

    def _wait(self, e, tok):
        key, val = tok
        if key == e and e == 'pe':
            return
        if self.waited[e].get(key, 0) >= val:
            return
        self.E[e].wait_ge(self.sem[key], val)
        self.waited[e][key] = val
        self.nwaits += 1

    def _collect(self, r, w, e=None):
        deps = {}

        def add(t):
            if t is None:
                return
            if deps.get(t[0], 0) < t[1]:
                deps[t[0]] = t[1]
        for x in r:
            for k, v in self.lastw.get(x, {}).items():
                add((k, v))
            if x.startswith('ps') and e is not None:
                for k, v in self.readers.get(x, {}).items():
                    if k != e:
                        add((k, v))
        for x in w:
            for k, v in self.lastw.get(x, {}).items():
                add((k, v))
            for k, v in self.readers.get(x, {}).items():
                add((k, v))
        return list(deps.items())

    def _record(self, tok, r, w):
        for x in w:
            self.lastw.setdefault(x, {})[tok[0]] = tok[1]
            self.readers[x] = {}
        for x in r:
            d = self.readers.setdefault(x, {})
            if d.get(tok[0], 0) < tok[1]:
                d[tok[0]] = tok[1]

    def op(self, e, fn, r=(), w=(), inc=True):
        for t in self._collect(r, w, e):
            self._wait(e, t)
        ins = fn(self.E[e])
        if inc:
            self.cnt[e] += 1
            ins.then_inc(self.sem[e], 1)
            tok = (e, self.cnt[e])
        else:
            tok = (e, self.cnt[e] + 1)
        self._record(tok, r, w)
        return ins

    def dma(self, q, out, in_, r=(), w=()):
        i = self.drr[q]
        self.drr[q] = (i + 1) % NDS
        k = (q, i)
        if self.dcnt[k] > 0:
            self._wait(q, (k, self.dcnt[k]))
        for t in self._collect(r, w):
            self._wait(q, t)
        ins = self.E[q].dma_start(out=out, in_=in_)
        self.dcnt[k] += 16
        ins.then_inc(self.sem[k], 16)
        tok = (k, self.dcnt[k])
        self._record(tok, r, w)
        return ins

    def barrier(self, keep_pool_dma=False):
        def is_pool(k):
            return isinstance(k, tuple) and k[0] == 'pool'
        toks = [(e, self.cnt[e]) for e in self.E if self.cnt[e] > 0]
        toks += [(k, v) for k, v in self.dcnt.items() if v > 0 and not (keep_pool_dma and is_pool(k))]
        for e in self.E:
            for t in toks:
                self._wait(e, t)
        keep = {}
        if keep_pool_dma:
            for res, d in self.lastw.items():
                d2 = {k: v for k, v in d.items() if is_pool(k)}
                if d2:
                    keep[res] = d2
        self.lastw = keep
        self.readers = {}

    def finish(self):
        for k, v in self.dcnt.items():
            if v > 0:
                self._wait('sp', (k, v))


class Ctx:
    pass


def _interleave(gens):
    live = list(gens)
    while live:
        nxt = []
        for item in live:
            gen, n = item
            done = False
            for _ in range(n):
                try:
                    next(gen)
                except StopIteration:
                    done = True
                    break
            if not done:
                nxt.append(item)
        live = nxt


_UID = [0]


def _sb(nc, es, name, shape, dt):
    _UID[0] += 1
    return es.enter_context(nc.sbuf_tensor("sb%d_%s" % (_UID[0], name), list(shape), dt))


def emit_consts(g):
    nc, s, es = g.nc, g.s, g.es
    g.ident = _sb(nc, es, "ident", [128, 128], F32)
    g.identb = _sb(nc, es, "identb", [128, 128], BF16)
    g.ones = _sb(nc, es, "ones", [128, 128], F32)
    g.onesb = _sb(nc, es, "onesb", [128, 128], BF16)
    s.dma('sp', g.ident[:], g.d_ident[:, :], w=['ident'])
    s.dma('pool', g.identb[:], g.d_ident[:, :], w=['identb'])
    s.op('dve', lambda e: e.memset(g.ones[:], 1.0), w=['ones'])
    s.op('dve', lambda e: e.memset(g.onesb[:], 1.0), w=['onesb'])
    g.ccol = _sb(nc, es, "ccol", [128, 8], F32)
    g.sc = _sb(nc, es, "sc", [128, 8], F32)
    g.scb = _sb(nc, es, "scb", [128, 8, 128], F32)
    s.dma('sp', g.ccol[:], g.d_ccol[:, :], w=['ccol'])
    s.op('act', lambda e: e.activation(out=g.sc[:], in_=g.ccol[:], func=AF.Silu), r=['ccol'], w=['sc'])
    for kc in range(8):
        s.op('dve', lambda e, kc=kc: e.tensor_scalar(out=g.scb[:, kc, :], in0=g.ones[:], scalar1=g.sc[:, kc:kc + 1],
                                                     scalar2=None, op0=ALU.mult), r=['ones', 'sc'], w=['scb'])


def emit_mods(g, L, modw, modb_row, lng, lnb, pre=None):
    nc, s, es = g.nc, g.s, g.es
    L.shift = _sb(nc, L.es, "shift", [128, 8], F32)
    L.scale1 = _sb(nc, L.es, "scale1", [128, 8], F32)
    L.gate_bc = _sb(nc, L.es, "gate_bc", [128, D], F32)
    L.lng_bc = _sb(nc, L.es, "lng_bc", [128, D], F32)
    L.lnb_bc = _sb(nc, L.es, "lnb_bc", [128, D], F32)
    s.dma('sp', L.lng_bc[:], lng.partition_broadcast(128), w=['lng_bc'])
    s.dma('sp', L.lnb_bc[:], lnb.partition_broadcast(128), w=['lnb_bc'])
    if pre is not None:
        pre()
    with ExitStack() as es2:
        mw = [_sb(nc, es2, "mw%d" % i, [128, 3 * D], F32) for i in range(2)]
        brow = _sb(nc, es2, "brow", [1, 3 * D], F32)
        bc = _sb(nc, es2, "modbc", [128, 2 * D], F32)
        one11 = _sb(nc, es2, "one11", [1, 1], F32)
        s.op('dve', lambda e: e.memset(one11[:], 1.0), w=['one11'])
        s.dma('sp', brow[:], modb_row[:, :], w=['brow'])
        P = g.ps
        for kc in range(8):
            t = mw[kc % 2]
            nm = 'mw%d' % (kc % 2)
            s.dma('sp', t[:], modw[kc * 128:(kc + 1) * 128, :], w=[nm])
            for j in range(6):
                s.op('pe', lambda e, j=j, kc=kc, t=t: e.matmul(P[:, j * 512:(j + 1) * 512], lhsT=g.scb[:, kc, :],
                                                            rhs=t[:, j * 512:(j + 1) * 512], start=(kc == 0), stop=False),
                     r=[nm, 'scb'], w=['ps%d' % j], inc=(j == 5))
        for j in range(6):
            s.op('pe', lambda e, j=j: e.matmul(P[:, j * 512:(j + 1) * 512], lhsT=g.ones[0:1, :],
                                               rhs=brow[0:1, j * 512:(j + 1) * 512], start=False, stop=True),
                 r=['brow', 'ones'], w=['ps%d' % j])
        for j in range(4):
            s.op('act' if j % 2 else 'dve',
                 (lambda e, j=j: e.activation(out=bc[:, j * 512:(j + 1) * 512], in_=P[:, j * 512:(j + 1) * 512], func=AF.Copy))
                 if j % 2 else
                 (lambda e, j=j: e.tensor_copy(out=bc[:, j * 512:(j + 1) * 512], in_=P[:, j * 512:(j + 1) * 512])),
                 r=['ps%d' % j], w=['modbc%d' % j])
        for j in range(2):
            s.op('dve', lambda e, j=j: e.tensor_copy(out=L.gate_bc[:, j * 512:(j + 1) * 512], in_=P[:, (4 + j) * 512:(5 + j) * 512]),
                 r=['ps%d' % (4 + j)], w=['gate_bc'])
        for j in range(16):
            s.op('pe', lambda e, j=j: e.matmul(P[:, 6 * 512 + j:6 * 512 + j + 1], lhsT=bc[0:1, j * 128:(j + 1) * 128],
                                               rhs=one11[0:1, 0:1], start=True, stop=True),
                 r=['modbc%d' % (j // 4), 'one11'], w=['ps6'])
        s.op('dve', lambda e: e.tensor_copy(out=L.shift[:], in_=P[:, 6 * 512:6 * 512 + 8]), r=['ps6'], w=['shift'])
        s.op('dve', lambda e: e.tensor_scalar(out=L.scale1[:], in0=P[:, 6 * 512 + 8:6 * 512 + 16], scalar1=1.0, scalar2=None,
                                              op0=ALU.add), r=['ps6'], w=['scale1'])
        s.barrier(keep_pool_dma=(pre is not None))


def emit_hT(g, L, xin, xin_nm, hT_ap_fn, hT_nm, pbanks):
    s = g.s
    P = g.ps
    for kc in range(8):
        b = pbanks[kc // 4]
        off = b * 512 + (kc % 4) * 128
        s.op('pe', lambda e, kc=kc, off=off: e.transpose(P[:, off:off + 128], xin[:, kc * 128:(kc + 1) * 128], g.ident[:]),
             r=[xin_nm, 'ident'], w=['ps%d' % b])
    for kc in range(8):
        b = pbanks[kc // 4]
        off = b * 512 + (kc % 4) * 128
        s.op('act', lambda e, kc=kc, off=off: e.activation(out=hT_ap_fn(kc), in_=P[:, off:off + 128], func=AF.Identity,
                                                           scale=L.scale1[:, kc:kc + 1], bias=L.shift[:, kc:kc + 1]),
             r=['ps%d' % b, 'scale1', 'shift'], w=[hT_nm])


def emit_epilogue(g, L, ybanks, xres, xres_nm, zb, xo, xo_nm, dst_rows):
    s = g.s
    P = g.ps
    for j in range(2):
        b = ybanks[j]
        s.op('dve', lambda e, j=j, b=b: e.tensor_tensor(out=zb[:, j * 512:(j + 1) * 512], in0=P[:, b * 512:(b + 1) * 512],
                                                        in1=L.gate_bc[:, j * 512:(j + 1) * 512], op=ALU.mult),
             r=['ps%d' % b, 'gate_bc'], w=['zb'])
    s.op('dve', lambda e: e.scalar_tensor_tensor(out=zb[:], in0=xres[:], scalar=ALPHA, in1=zb[:], op0=ALU.mult, op1=ALU.add),
         r=[xres_nm, 'zb'], w=['zb'])
    st = g.lnst
    for j in range(2):
        s.op('dve', lambda e, j=j: e.bn_stats(out=st[:, j * 6:(j + 1) * 6], in_=zb[:, j * 512:(j + 1) * 512]), r=['zb'], w=['lnst'])
    s.op('dve', lambda e: e.bn_aggr(out=g.lnmv[:, 0:2], in_=st[:, 0:12]), r=['lnst'], w=['lnmv'])
    s.op('dve', lambda e: e.tensor_scalar(out=g.lnmv[:, 2:3], in0=g.lnmv[:, 1:2], scalar1=LN_EPS, scalar2=None, op0=ALU.add),
         r=['lnmv'], w=['lnmv2'])
    s.op('act', lambda e: e.activation(out=g.lnmv[:, 3:4], in_=g.lnmv[:, 2:3], func=AF.Sqrt), r=['lnmv2'], w=['lnmv3'])
    s.op('dve', lambda e: e.reciprocal(out=g.lnmv[:, 4:5], in_=g.lnmv[:, 3:4]), r=['lnmv3'], w=['lnmv4'])
    s.op('dve', lambda e: e.tensor_scalar(out=zb[:], in0=zb[:], scalar1=g.lnmv[:, 0:1], scalar2=g.lnmv[:, 4:5],
                                          op0=ALU.subtract, op1=ALU.mult), r=['zb', 'lnmv', 'lnmv4'], w=['zb'])
    s.op('pool', lambda e: e.tensor_tensor(out=zb[:], in0=zb[:], in1=L.lng_bc[:], op=ALU.mult), r=['zb', 'lng_bc'], w=['zb'])
    s.op('pool', lambda e: e.tensor_tensor(out=xo[:], in0=zb[:], in1=L.lnb_bc[:], op=ALU.add), r=['zb', 'lnb_bc'], w=[xo_nm])
    s.dma('sp', dst_rows, xo[:], r=[xo_nm], w=[])


def emit_ffn(g, li, src, dst):
    nc, s = g.nc, g.s
    TT = 256
    NT = S // TT
    NBT = TT // 128
    with ExitStack() as les:
        L = Ctx()
        L.es = les
        WT = Ctx()

        def pre():
            WT.wup = _sb(nc, les, "wup", [128, 8, 2 * DFF], BF16)
            WT.wdn = _sb(nc, les, "wdn", [128, NFC, D], BF16)
            wupd = g.d_f_w_up[li].rearrange("(kc p) n -> p kc n", p=128)
            for kc in range(8):
                for hf in range(2):
                    s.dma('pool', WT.wup[:, kc, hf * DFF:(hf + 1) * DFF], wupd[:, kc, hf * DFF:(hf + 1) * DFF], w=['wup'])
            wdnd = g.d_f_w_down[li].rearrange("(fc p) n -> p fc n", p=128)
            for fc in range(NFC):
                s.dma('pool', WT.wdn[:, fc, :], wdnd[:, fc, :], w=['wdn'])
        emit_mods(g, L, g.d_f_mod_w[li], g.d_f_mod_b[li], g.d_f_ln_g[li], g.d_f_ln_b[li], pre=pre)
        wup, wdn = WT.wup, WT.wdn
        cw = _sb(nc, les, "cw", [128, 2 * NFC, 4], F32)
        halo = _sb(nc, les, "halo", [128, 2 * NFC, 2], F32)
        hT = [_sb(nc, les, "hT%d" % i, [128, 8, TT], BF16) for i in range(2)]
        gTs = [_sb(nc, les, "gT%d" % i, [128, NFC, TT], BF16) for i in range(2)]
        upre = [_sb(nc, les, "upre%d" % i, [128, TT + 2], F32) for i in range(2)]
        c0 = [_sb(nc, les, "c0%d" % i, [128, TT], F32) for i in range(2)]
        asil = _sb(nc, les, "asil", [128, TT], F32)
        xin = [_sb(nc, les, "xin%d" % i, [128, D], F32) for i in range(2)]
        xrs = [_sb(nc, les, "xrs%d" % i, [128, D], F32) for i in range(2)]
        zb = _sb(nc, les, "zb", [128, D], F32)
        xo = [_sb(nc, les, "xo%d" % i, [128, D], F32) for i in range(2)]
        P = g.ps
        s.dma('sp', cw[:], g.d_f_cw[li], w=['cw'])
        s.op('dve', lambda e: e.memset(halo[:], 0.0), w=['halo%d' % i for i in range(2 * NFC)])
        def phA(t):
            t0 = t * TT
            hs = t % 2
            hnm = 'hT%d' % hs
            for bi in range(NBT):
                xs_ = (t * NBT + bi) % 2
                s.dma('sp', xin[xs_][:], src[t0 + bi * 128:t0 + (bi + 1) * 128, :], w=['xin%d' % xs_])
                emit_hT(g, L, xin[xs_], 'xin%d' % xs_, lambda kc, bi=bi, hs=hs: hT[hs][:, kc, bi * 128:(bi + 1) * 128], hnm, (0, 1))
                yield

        def phU(t):
            hs = t % 2
            hnm = 'hT%d' % hs
            gT = gTs[t % 2]
            gnm = 'gT%d' % (t % 2)
            for j in range(NFC):
                for half in range(2):
                    fc = j + half * NFC
                    b = 2 + ((2 * j + half) % 4)
                    pb = 'ps%d' % b
                    up = upre[half]
                    unm = 'upre%d' % half
                    for kc in range(8):
                        s.op('pe', lambda e, kc=kc, fc=fc, b=b: e.matmul(P[:, b * 512:b * 512 + TT], lhsT=wup[:, kc, fc * 128:(fc + 1) * 128],
                                                                      rhs=hT[hs][:, kc, :], start=(kc == 0), stop=(kc == 7)),
                             r=['wup', hnm], w=[pb], inc=(kc == 7))
                    s.op('pool', lambda e, fc=fc, up=up: e.tensor_copy(out=up[:, 0:2], in_=halo[:, fc, :]), r=['halo%d' % fc], w=[unm])
                    s.op('act', lambda e, b=b, up=up: e.activation(out=up[:, 2:TT + 2], in_=P[:, b * 512:b * 512 + TT], func=AF.Copy),
                         r=[pb], w=[unm])
                    s.op('act', lambda e, b=b, fc=fc, half=half: e.activation(out=c0[half][:], in_=P[:, b * 512:b * 512 + TT], func=AF.Identity,
                                                                             scale=cw[:, fc, 2:3], bias=cw[:, fc, 3:4]),
                         r=[pb, 'cw'], w=['c0%d' % half])
                    s.op('pool', lambda e, fc=fc, up=up: e.tensor_copy(out=halo[:, fc, :], in_=up[:, TT:TT + 2]), r=[unm], w=['halo%d' % fc])
                    s.op('dve', lambda e, fc=fc, up=up, half=half: e.scalar_tensor_tensor(out=c0[half][:], in0=up[:, 1:TT + 1], scalar=cw[:, fc, 1:2],
                                                                                        in1=c0[half][:], op0=ALU.mult, op1=ALU.add),
                         r=[unm, 'cw', 'c0%d' % half], w=['c0%d' % half])
                    s.op('dve', lambda e, fc=fc, up=up, half=half: e.scalar_tensor_tensor(out=c0[half][:], in0=up[:, 0:TT], scalar=cw[:, fc, 0:1],
                                                                                        in1=c0[half][:], op0=ALU.mult, op1=ALU.add),
                         r=[unm, 'cw', 'c0%d' % half], w=['c0%d' % half])
                    if half == 0:
                        s.op('act', lambda e: e.activation(out=asil[:], in_=c0[0][:], func=AF.Silu), r=['c00'], w=['asil'])
                    else:
                        s.op('dve', lambda e, j=j, gT=gT: e.tensor_tensor(out=gT[:, j, :], in0=asil[:], in1=c0[1][:], op=ALU.mult),
                             r=['asil', 'c01'], w=[gnm])
                    yield

        def phD(t):
            t0 = t * TT
            gT = gTs[t % 2]
            gnm = 'gT%d' % (t % 2)
            for bi in range(NBT):
                r0 = t0 + bi * 128
                xs_ = (t * NBT + bi) % 2
                s.dma('sp', xrs[xs_][:], src[r0:r0 + 128, :], w=['xrs%d' % xs_])
                for half in range(2):
                    b = 6 + half
                    for j in range(NFC):
                        s.op('pe', lambda e, j=j, half=half, b=b, bi=bi, gT=gT: e.matmul(P[:, b * 512:(b + 1) * 512], lhsT=gT[:, j, bi * 128:(bi + 1) * 128],
                                                                                         rhs=wdn[:, j, half * 512:(half + 1) * 512],
                                                                                         start=(j == 0), stop=(j == NFC - 1)),
                             r=[gnm, 'wdn'], w=['ps%d' % b], inc=(j == NFC - 1))
                    yield
                emit_epilogue(g, L, (6, 7), xrs[xs_], 'xrs%d' % xs_, zb, xo[xs_], 'xo%d' % xs_, dst[r0:r0 + 128, :])
                yield

        for it in range(NT + 2):
            gens = []
            if 1 <= it <= NT:
                gens.append([phU(it - 1), 1])
            if it < NT:
                gens.append([phA(it), 1])
            if it >= 2:
                gens.append([phD(it - 2), 1])
            _interleave(gens)
        s.barrier()


NEG = -1.0e30
GUARD = 1.0e38
NBISECT = 24
TOPK_EXACT_NK = 1024
DSTOP = int(os.environ.get('DSA_STOP', '99'))
DSUB = int(os.environ.get('DSA_SUB', '99'))
GSTOP = int(os.environ.get('GDN_STOP', '99'))
DSC = int(os.environ.get('DSA_SC', '3'))
DSKIP = os.environ.get('DSA_SKIP', '').split(',')
REP = -3.0e38


def emit_dsa(g, src, dst):
    nc, s = g.nc, g.s
    P = g.ps
    with ExitStack() as les:
        L = Ctx()
        L.es = les
        emit_mods(g, L, g.d_e_mod_w[0], g.d_e_mod_b[0], g.d_e_ln_g[0], g.d_e_ln_b[0])
        A = lambda name, shape, dt: _sb(nc, les, name, shape, dt)
        win = A("win", [128, 8, 1536], BF16)
        wif = A("wif", [128, 8, 4], F32)
        wiw = A("wiw", [128, 8, 4], BF16)
        poolw = A("poolw", [128, 4, 128], BF16)
        pscale = A("pscale", [128, 4], F32)
        kvn_bc = A("kvn_bc", [128, 128], F32)
        ukT = A("ukT", [128, 4, 128], BF16)
        uvpad = A("uvpad", [128, 8, 128], BF16)
        wout = A("wout", [128, 8, D], BF16)
        negI4 = A("negI4", [128, 512], BF16)
        corr = A("corr", [128, 4, 15], F32)
        ckvn_all = A("ckvn_all", [128, NB, 128], BF16)
        ckvnT_all = A("ckvnT_all", [128, S], BF16)
        kiT_all = A("kiT_all", [128, S], BF16)
        xin = [A("xin%d" % i, [128, D], F32) for i in range(3)]
        hT = A("hT", [128, 8, 128], BF16)
        ut = A("ut", [128, 4, 143], F32)
        ta = A("ta", [128, 143], F32)
        tb = A("tb", [128, 143], F32)
        dT = A("dT", [128, 4, 128], BF16)
        qT = A("qT", [128, 512], BF16)
        qiT = [A("qiT%d" % i, [128, 256], BF16) for i in range(2)]
        wis = [A("wis%d" % i, [128, 4], F32) for i in range(2)]
        qlT = [A("qlT%d" % i, [128, 1024], BF16) for i in range(3)]
        W = [A("W%d" % i, [128, S + 8], F32) for i in range(2)]
        bs = A("bs", [128, 8], F32)
        cb = A("cb", [128, S], BF16)
        rbuf = [A("rbuf%d" % i, [128, 512], F32) for i in range(2)]
        rb2 = A("rb2", [128, 512], F32)
        notm = [A("notm%d" % i, [128, S], BF16) for i in range(2)]
        m8 = A("m8", [128, 8], F32)
        pT = [A("pT%d" % i, [128, 512], BF16) for i in range(2)]
        rden = A("rden", [128, 512], F32)
        oTn = A("oTn", [128, 1024], BF16)
        yinT = [A("yinT%d" % i, [128, 1024], BF16) for i in range(3)]
        zb = A("zb", [128, D], F32)
        xo = [A("xo%d" % i, [128, D], F32) for i in range(2)]
        sq = A("sq", [128, 128], F32)
        ckf = A("ckf", [128, 128], F32)
        rs = A("rs", [128, 4], F32)
        Pb3 = P[:, 3 * 512:4 * 512].bitcast(BF16)

        wind = g.d_e_w_in[0].rearrange("(kc p) n -> p kc n", p=128)
        for kc in range(8):
            s.dma('pool', win[:, kc, 0:1472], wind[:, kc, 0:1472], w=['win'])
            s.dma('pool', win[:, kc, 1472:1536], wind[:, kc, 1408:1472], w=['win'])
        s.dma('sp', wif[:], wind[:, :, 1472:1476], w=['wif'])
        s.op('dve', lambda e: e.tensor_copy(out=wiw[:], in_=wif[:]), r=['wif'], w=['wiw'])
        s.dma('pool', poolw[:], g.d_e_pool_w[0].rearrange("g c d -> c g d"), w=['poolw'])
        s.dma('sp', pscale[:], g.d_pscale[:, :], w=['pscale'])
        s.dma('sp', kvn_bc[:], g.d_e_kv_norm[0].partition_broadcast(128), w=['kvn_bc'])
        s.dma('pool', ukT[:], g.d_ukT[:, :, :], w=['ukT'])
        s.dma('pool', uvpad[:], g.d_uvpad[:, :, :], w=['uvpad'])
        woutd = g.d_e_w_out[0].rearrange("(kc p) n -> p kc n", p=128)
        for kc in range(8):
            s.dma('pool', wout[:, kc, :], woutd[:, kc, :], w=['wout'])
        s.dma('pool', negI4[:], g.d_negI4[:, :], w=['negI4'])
        s.dma('sp', corr[:], g.d_corr[:, :, :], w=['corr'])
        s.op('dve', lambda e: e.memset(ut[:], 0.0), w=['ut'])

        def front_a(qb):
            sl = qb % 2
            s3 = qb % 3
            t0 = qb * 128
            nk = t0 + 128
            xn = 'xin%d' % s3
            s.dma('sp', xin[s3][:], src[t0:t0 + 128, :], w=[xn])
            emit_hT(g, L, xin[s3], xn, lambda kc: hT[:, kc, :], 'hT', (0, 1))
            yield
            def grp(out_ap, cols, bank, last=True):
                for kc in range(8):
                    s.op('pe', lambda e, kc=kc: e.matmul(out_ap, lhsT=win[:, kc, cols[0]:cols[1]], rhs=hT[:, kc, :],
                                                        start=(kc == 0), stop=(kc == 7)),
                         r=['win', 'hT'], w=['ps%d' % bank], inc=(kc == 7))
            for gi in range(4):
                grp(P[:, gi * 128:(gi + 1) * 128], (gi * 128, (gi + 1) * 128), 0)
                yield
            for j in range(4):
                grp(P[:, 512 + j * 128:512 + (j + 1) * 128], (512 + j * 128, 512 + (j + 1) * 128), 1)
                yield
            for j in range(2):
                grp(P[:, 1024 + j * 128:1024 + (j + 1) * 128], (1152 + j * 128, 1152 + (j + 1) * 128), 2)
                yield
            grp(P[:, 1024 + 256:1024 + 384], (1408, 1536), 2)
            yield
            for kc in range(8):
                s.op('pe', lambda e, kc=kc: e.matmul(P[:, 1536:1536 + 128], lhsT=hT[:, kc, :], rhs=win[:, kc, 1024:1152],
                                                    start=(kc == 0), stop=(kc == 7)), r=['win', 'hT'], w=['ps3'], inc=(kc == 7))
            for kc in range(8):
                s.op('pe', lambda e, kc=kc: e.matmul(P[:, 1536 + 128:1536 + 132], lhsT=hT[:, kc, :], rhs=wiw[:, kc, :],
                                                    start=(kc == 0), stop=(kc == 7)), r=['wiw', 'hT'], w=['ps3'], inc=(kc == 7))
            yield
            s.op('act', lambda e: e.activation(out=ut[:, :, 15:143], in_=P[:, 0:512].rearrange("p (g t) -> p g t", g=4), func=AF.Copy),
                 r=['ps0'], w=['ut'])
            s.op('act', lambda e: e.activation(out=qT[:], in_=P[:, 512:1024], func=AF.Copy), r=['ps1'], w=['qT'])
            s.op('dve', lambda e: e.tensor_copy(out=qiT[sl][:], in_=P[:, 1024:1024 + 256]), r=['ps2'], w=['qiT%d' % sl])
            s.op('dve', lambda e: e.tensor_copy(out=kiT_all[:, t0:t0 + 128], in_=P[:, 1024 + 256:1024 + 384]), r=['ps2'], w=['kiT%d' % qb])
            s.op('dve', lambda e: e.tensor_copy(out=wis[sl][:], in_=P[:, 1536 + 128:1536 + 132]), r=['ps3'], w=['wis%d' % sl])
            yield
            s.op('act', lambda e: e.activation(out=ckf[:], in_=P[:, 1536:1536 + 128], func=AF.Copy), r=['ps3'], w=['ckf'])
            s.op('dve', lambda e: e.tensor_tensor(out=sq[:], in0=ckf[:], in1=ckf[:], op=ALU.mult), r=['ckf'], w=['sq'])
            s.op('dve', lambda e: e.reduce_sum(out=rs[:, 0:1], in_=sq[:], axis=AX.X), r=['sq'], w=['rs0'])
            s.op('dve', lambda e: e.tensor_scalar(out=rs[:, 1:2], in0=rs[:, 0:1], scalar1=1.0 / 128, scalar2=RMS_EPS, op0=ALU.mult, op1=ALU.add),
                 r=['rs0'], w=['rs1'])
            s.op('act', lambda e: e.activation(out=rs[:, 2:3], in_=rs[:, 1:2], func=AF.Sqrt), r=['rs1'], w=['rs2'])
            s.op('dve', lambda e: e.reciprocal(out=rs[:, 3:4], in_=rs[:, 2:3]), r=['rs2'], w=['rs3'])
            s.op('dve', lambda e: e.scalar_tensor_tensor(out=ckf[:], in0=ckf[:], scalar=rs[:, 3:4], in1=kvn_bc[:],
                                                         op0=ALU.mult, op1=ALU.mult), r=['ckf', 'rs3', 'kvn_bc'], w=['ckf'])
            s.op('act', lambda e: e.activation(out=ckvn_all[:, qb, :], in_=ckf[:], func=AF.Copy), r=['ckf'], w=['ckvn%d' % qb])
            s.op('pe', lambda e: e.transpose(P[:, 1536 + 256:1536 + 384], ckf[:], g.ident[:]), r=['ckf', 'ident'], w=['ps3'])
            s.op('act', lambda e: e.activation(out=ckvnT_all[:, t0:t0 + 128], in_=P[:, 1536 + 256:1536 + 384], func=AF.Copy), r=['ps3'], w=['ckvnT%d' % qb])
            yield
            for gi in range(4):
                win_ = 2 << gi
                U = ut[:, gi, :]
                s.op('dve', lambda e, U=U: e.tensor_tensor(out=ta[:, 1:143], in0=U[:, 1:143], in1=U[:, 0:142], op=ALU.add), r=['ut'], w=['ta'])
                sw = ta
                swn = 'ta'
                if gi >= 1:
                    s.op('dve', lambda e: e.tensor_tensor(out=tb[:, 3:143], in0=ta[:, 3:143], in1=ta[:, 1:141], op=ALU.add), r=['ta'], w=['tb'])
                    sw, swn = tb, 'tb'
                if gi >= 2:
                    s.op('dve', lambda e: e.tensor_tensor(out=ta[:, 7:143], in0=tb[:, 7:143], in1=tb[:, 3:139], op=ALU.add), r=['tb'], w=['ta'])
                    sw, swn = ta, 'ta'
                if gi >= 3:
                    s.op('dve', lambda e: e.tensor_tensor(out=tb[:, 15:143], in0=ta[:, 15:143], in1=ta[:, 7:135], op=ALU.add), r=['ta'], w=['tb'])
                    sw, swn = tb, 'tb'
                if qb == 0:
                    s.op('dve', lambda e, sw=sw, gi=gi, win_=win_: e.tensor_tensor(out=sw[:, 15:15 + win_ - 1], in0=sw[:, 15:15 + win_ - 1],
                                                                                 in1=corr[:, gi, 0:win_ - 1], op=ALU.mult),
                         r=[swn, 'corr'], w=[swn])
                s.op('dve', lambda e, sw=sw, gi=gi, win_=win_, U=U: e.scalar_tensor_tensor(out=dT[:, gi, :], in0=sw[:, 15:143], scalar=1.0 / win_,
                                                                                          in1=U[:, 15:143], op0=ALU.mult, op1=ALU.subtract),
                     r=[swn, 'ut'], w=['dT'])
                yield
            s.op('pool', lambda e: e.tensor_copy(out=ut[:, :, 0:15], in_=ut[:, :, 128:143]), r=['ut'], w=['ut'])
            for gi in range(4):
                s.op('pe', lambda e, gi=gi: e.matmul(P[:, gi * 128:(gi + 1) * 128], lhsT=poolw[:, gi, :], rhs=dT[:, gi, :], start=True, stop=True),
                     r=['poolw', 'dT'], w=['ps0'])
            for gi in range(4):
                s.op('act', lambda e, gi=gi: e.activation(out=yinT[s3][:, gi * 128:(gi + 1) * 128], in_=P[:, gi * 128:(gi + 1) * 128],
                                                          func=AF.Identity, scale=pscale[:, gi:gi + 1]),
                     r=['ps0', 'pscale'], w=['yinT%d' % s3])
            yield
            for h in range(8):
                po = (h % 2) * 64
                bank = 1 + h % 2
                off = bank * 512 + (h // 2) * 128
                s.op('pe', lambda e, h=h, po=po, off=off: e.matmul(P[:, off:off + 128], lhsT=ukT[po:po + 64, h // 2, :],
                                                                  rhs=qT[po:po + 64, (h // 2) * 128:(h // 2 + 1) * 128], start=True, stop=True),
                     r=['ukT', 'qT'], w=['ps%d' % bank])
            for j in range(2):
                s.op('act', lambda e, j=j: e.activation(out=qlT[s3][:, j * 512:(j + 1) * 512], in_=P[:, (1 + j) * 512:(2 + j) * 512], func=AF.Copy),
                     r=['ps%d' % (1 + j)], w=['qlT%d' % s3])
            yield
            Wn = 'W%d' % sl
            cnt = 0
            chunks = []
            k0_ = 0
            while k0_ < nk:
                rem = nk - k0_
                w0 = 512 if rem >= 512 else (256 if rem >= 256 else 128)
                chunks.append((k0_, w0))
                k0_ += w0
            for (k0, w_) in chunks:
                for h in range(4):
                    po = (h % 2) * 64
                    bank = 2 + cnt % 2
                    rb = rbuf[cnt % 2]
                    rbn = 'rbuf%d' % (cnt % 2)
                    cnt += 1
                    s.op('pe', lambda e, h=h, po=po, bank=bank, k0=k0, w_=w_: e.matmul(P[:, bank * 512:bank * 512 + w_],
                                                                                     lhsT=qiT[sl][po:po + 64, (h // 2) * 128:(h // 2 + 1) * 128],
                                                                                     rhs=kiT_all[po:po + 64, k0:k0 + w_], start=True, stop=True),
                         r=['qiT%d' % sl] + ['kiT%d' % kb_ for kb_ in range(k0 // 128, (k0 + w_) // 128)], w=['ps%d' % bank])
                    s.op('act', lambda e, bank=bank, rb=rb, w_=w_: e.activation(out=rb[:, 0:w_], in_=P[:, bank * 512:bank * 512 + w_], func=AF.Relu),
                         r=['ps%d' % bank], w=[rbn])
                    if h == 0:
                        s.op('dve', lambda e, rb=rb, k0=k0, w_=w_: e.tensor_scalar(out=W[sl][:, k0:k0 + w_], in0=rb[:, 0:w_], scalar1=wis[sl][:, 0:1],
                                                                                 scalar2=None, op0=ALU.mult), r=[rbn, 'wis%d' % sl], w=[Wn])
                    else:
                        s.op('dve', lambda e, rb=rb, k0=k0, w_=w_, h=h: e.scalar_tensor_tensor(out=W[sl][:, k0:k0 + w_], in0=rb[:, 0:w_], scalar=wis[sl][:, h:h + 1],
                                                                                            in1=W[sl][:, k0:k0 + w_], op0=ALU.mult, op1=ALU.add),
                             r=[rbn, 'wis%d' % sl, Wn], w=[Wn])
                    yield

        def topk(qb):
            sl = qb % 2
            nk = qb * 128 + 128
            Wn = 'W%d' % sl
            nmn = 'notm%d' % sl
            Wt = W[sl]
            if nk <= 256:
                s.op('dve', lambda e: e.memset(notm[sl][:, 0:nk], 0.0), w=[nmn])
            elif nk <= TOPK_EXACT_NK:
                s.op('dve', lambda e: e.memset(Wt[0:64, nk - 64:nk], NEG), r=[Wn], w=[Wn])
                for it in range(32):
                    s.op('dve', lambda e: e.max(out=m8[:], in_=Wt[:, 0:nk]), r=[Wn], w=['m8'])
                    s.op('dve', lambda e: e.match_replace(out=Wt[:, 0:nk], in_to_replace=m8[:], in_values=Wt[:, 0:nk], imm_value=REP),
                         r=[Wn, 'm8'], w=[Wn])
                    yield
                s.op('dve', lambda e: e.tensor_scalar(out=notm[sl][:, 0:nk], in0=Wt[:, 0:nk], scalar1=0.5 * REP, scalar2=None, op0=ALU.is_gt),
                     r=[Wn], w=[nmn])
            else:
                s.op('dve', lambda e: e.tensor_reduce(out=bs[:, 7:8], in_=Wt[:, 0:nk], axis=AX.X, op=ALU.max, apply_absolute_value=True),
                     r=[Wn], w=['bs7'])
                s.op('dve', lambda e: e.memset(Wt[0:64, nk - 64:nk], NEG), r=[Wn], w=[Wn])
                s.op('dve', lambda e: e.tensor_scalar(out=bs[:, 0:1], in0=bs[:, 7:8], scalar1=-1.001, scalar2=None, op0=ALU.mult), r=['bs7'], w=['bs0'])
                s.op('dve', lambda e: e.tensor_scalar(out=bs[:, 1:2], in0=bs[:, 7:8], scalar1=1.0005, scalar2=None, op0=ALU.mult), r=['bs7'], w=['bs1'])
                s.op('dve', lambda e: e.tensor_scalar(out=bs[:, 2:3], in0=bs[:, 0:1], scalar1=bs[:, 1:2], scalar2=-1.0, op0=ALU.add, op1=ALU.mult),
                     r=['bs0', 'bs1'], w=['bs2'])
                yield
                for it in range(NBISECT):
                    s.op('act', lambda e: e.activation(out=notm[sl][:, 0:nk], in_=Wt[:, 0:nk], func=AF.Sign, bias=bs[:, 2:3], scale=1.0, accum_out=bs[:, 3:4]),
                         r=[Wn, 'bs2'], w=[nmn, 'bs3'])
                    s.op('dve', lambda e: e.tensor_scalar(out=bs[:, 4:5], in0=bs[:, 3:4], scalar1=float(512 - nk), scalar2=None, op0=ALU.is_ge), r=['bs3'], w=['bs4'])
                    s.op('dve', lambda e: e.scalar_tensor_tensor(out=bs[:, 0:1], in0=bs[:, 4:5], scalar=bs[:, 1:2], in1=bs[:, 0:1], op0=ALU.mult, op1=ALU.add),
                         r=['bs4', 'bs1', 'bs0'], w=['bs0'])
                    s.op('dve', lambda e: e.tensor_scalar(out=bs[:, 1:2], in0=bs[:, 1:2], scalar1=0.5, scalar2=None, op0=ALU.mult), r=['bs1', 'bs0'], w=['bs1'])
                    s.op('dve', lambda e: e.tensor_scalar(out=bs[:, 2:3], in0=bs[:, 0:1], scalar1=bs[:, 1:2], scalar2=-1.0, op0=ALU.add, op1=ALU.mult),
                         r=['bs0', 'bs1'], w=['bs2'])
                    yield
                s.op('dve', lambda e: e.scalar_tensor_tensor(out=bs[:, 7:8], in0=bs[:, 1:2], scalar=2.0, in1=bs[:, 0:1], op0=ALU.mult, op1=ALU.add),
                     r=['bs1', 'bs0'], w=['bs7'])
                s.op('dve', lambda e: e.tensor_scalar(out=bs[:, 6:7], in0=bs[:, 7:8], scalar1=-1.0, scalar2=None, op0=ALU.mult), r=['bs7'], w=['bs6'])
                s.op('act', lambda e: e.activation(out=notm[sl][:, 0:nk], in_=Wt[:, 0:nk], func=AF.Sign, bias=bs[:, 6:7], scale=1.0, accum_out=bs[:, 3:4]),
                     r=[Wn, 'bs6'], w=[nmn, 'bs3'])
                s.op('dve', lambda e: e.tensor_scalar(out=bs[:, 5:6], in0=bs[:, 3:4], scalar1=-0.5, scalar2=float(256 - nk // 2), op0=ALU.mult, op1=ALU.add),
                     r=['bs3'], w=['bs5'])
                yield
                s.op('dve', lambda e: e.tensor_scalar(out=notm[sl][:, 0:nk], in0=Wt[:, 0:nk], scalar1=bs[:, 7:8], scalar2=None, op0=ALU.is_gt),
                     r=[Wn, 'bs7'], w=[nmn])
                s.op('dve', lambda e: e.tensor_scalar(out=cb[:, 0:nk], in0=Wt[:, 0:nk], scalar1=bs[:, 0:1], scalar2=None, op0=ALU.is_gt),
                     r=[Wn, 'bs0'], w=['cb'])
                yield
                s.op('dve', lambda e: e.scalar_tensor_tensor(out=cb[:, 0:nk], in0=Wt[:, 0:nk], scalar=bs[:, 7:8], in1=cb[:, 0:nk], op0=ALU.is_le, op1=ALU.mult),
                     r=[Wn, 'bs7', 'cb'], w=['cb'])
                yield
                s.op('dve', lambda e: e.tensor_tensor_scan(out=Wt[:, 0:nk], data0=g.onesb[:, 0:1].to_broadcast([128, nk]), data1=cb[:, 0:nk], initial=0.0, op0=ALU.mult, op1=ALU.add),
                     r=['onesb', 'cb', Wn], w=[Wn])
                yield
                s.op('dve', lambda e: e.scalar_tensor_tensor(out=cb[:, 0:nk], in0=Wt[:, 0:nk], scalar=bs[:, 5:6], in1=cb[:, 0:nk], op0=ALU.is_le, op1=ALU.mult),
                     r=[Wn, 'bs5', 'cb'], w=['cb'])
                yield
                s.op('dve', lambda e: e.tensor_tensor(out=notm[sl][:, 0:nk], in0=notm[sl][:, 0:nk], in1=cb[:, 0:nk], op=ALU.add), r=[nmn, 'cb'], w=[nmn])
                s.op('dve', lambda e: e.tensor_scalar(out=notm[sl][:, 0:nk], in0=notm[sl][:, 0:nk], scalar1=-1.0, scalar2=1.0, op0=ALU.mult, op1=ALU.add),
                     r=[nmn], w=[nmn])
            s.op('dve', lambda e: e.memset(notm[sl][0:64, nk - 64:nk], 1.0), r=[nmn], w=[nmn])
            yield

        def back(qb):
            sl = qb % 2
            s3 = qb % 3
            t0 = qb * 128
            cnt = 0
            for hg in range(2):
                for kb in range(qb + 1):
                    j = cnt % 2
                    cnt += 1
                    bank = 4 + j
                    s.op('pe', lambda e, bank=bank, kb=kb, hg=hg: e.matmul(P[:, bank * 512:(bank + 1) * 512], lhsT=ckvnT_all[:, kb * 128:(kb + 1) * 128],
                                                                          rhs=qlT[s3][:, hg * 512:(hg + 1) * 512], start=True, stop=False),
                         r=['ckvnT%d' % kb, 'qlT%d' % s3], w=['ps%d' % bank], inc=False)
                    s.op('pe', lambda e, bank=bank, kb=kb: e.matmul(P[:, bank * 512:(bank + 1) * 512], lhsT=notm[sl][:, kb * 128:(kb + 1) * 128],
                                                                   rhs=negI4[:], start=False, stop=True),
                         r=['notm%d' % sl, 'negI4'], w=['ps%d' % bank])
                    s.op('act', lambda e, bank=bank, j=j: e.activation(out=pT[j][:], in_=P[:, bank * 512:(bank + 1) * 512], func=AF.Exp, scale=0.125),
                         r=['ps%d' % bank], w=['pT%d' % j])
                    s.op('pe', lambda e, kb=kb, j=j: e.matmul(P[:, 6 * 512:7 * 512], lhsT=ckvn_all[:, kb, :], rhs=pT[j][:], start=(kb == 0), stop=(kb == qb)),
                         r=['ckvn%d' % kb, 'pT%d' % j], w=['ps6'], inc=False)
                    s.op('pe', lambda e, kb=kb, j=j: e.matmul(P[:, 7 * 512:8 * 512], lhsT=g.onesb[:], rhs=pT[j][:], start=(kb == 0), stop=(kb == qb)),
                         r=['onesb', 'pT%d' % j], w=['ps7'])
                    yield
                s.op('dve', lambda e: e.reciprocal(out=rden[:], in_=P[:, 7 * 512:8 * 512]), r=['ps7'], w=['rden'])
                s.op('dve', lambda e, hg=hg: e.tensor_tensor(out=oTn[:, hg * 512:(hg + 1) * 512], in0=P[:, 6 * 512:7 * 512], in1=rden[:], op=ALU.mult),
                     r=['ps6', 'rden'], w=['oTn'])
                yield
            for hp in range(4):
                for h2 in range(2):
                    h = 2 * hp + h2
                    s.op('pe', lambda e, hp=hp, h=h, h2=h2: e.matmul(P[:, 7 * 512 + hp * 128:7 * 512 + (hp + 1) * 128], lhsT=uvpad[:, h, :],
                                                                    rhs=oTn[:, (h // 2 + 4 * (h % 2)) * 128:(h // 2 + 4 * (h % 2) + 1) * 128], start=(h2 == 0), stop=(h2 == 1)),
                         r=['uvpad', 'oTn'], w=['ps7'], inc=(h2 == 1))
            s.op('act', lambda e: e.activation(out=yinT[s3][:, 512:1024], in_=P[:, 7 * 512:8 * 512], func=AF.Copy), r=['ps7'], w=['yinT%d' % s3])
            yield
            for half in range(2):
                b = 4 + half
                for kc in range(8):
                    s.op('pe', lambda e, kc=kc, half=half, b=b: e.matmul(P[:, b * 512:(b + 1) * 512], lhsT=yinT[s3][:, kc * 128:(kc + 1) * 128],
                                                                        rhs=wout[:, kc, half * 512:(half + 1) * 512], start=(kc == 0), stop=(kc == 7)),
                         r=['yinT%d' % s3, 'wout'], w=['ps%d' % b], inc=(kc == 7))
            yield
            emit_epilogue(g, L, (4, 5), xin[s3], 'xin%d' % s3, zb, xo[sl], 'xo%d' % sl, dst[t0:t0 + 128, :])
            yield

        nblk = g.nblk_dsa
        for it in range(nblk + 2):
            gens = []
            if it < nblk:
                gens.append([front_a(it), 1])
            if 1 <= it <= nblk:
                gens.append([topk(it - 1), 1])
            if it >= 2:
                gens.append([back(it - 2), 1])
            _interleave(gens)
        s.barrier()


def emit_gdn(g, src, dst):
    nc, s = g.nc, g.s
    P = g.ps
    with ExitStack() as les:
        L = Ctx()
        L.es = les
        A = lambda name, shape, dt: _sb(nc, les, name, shape, dt)
        WT = Ctx()

        def pre():
            WT.win = A("gwin", [128, 8, 4096], BF16)
            WT.wout = A("gwout", [128, 8, D], BF16)
            wind_ = g.d_o_w_in[0].rearrange("(kc p) n -> p kc n", p=128)
            for kc in range(8):
                for q4 in range(2):
                    s.dma('pool', WT.win[:, kc, q4 * 2048:(q4 + 1) * 2048], wind_[:, kc, q4 * 2048:(q4 + 1) * 2048], w=['gwin'])
            woutd_ = g.d_o_w_out[0].rearrange("(kc p) n -> p kc n", p=128)
            for kc in range(8):
                s.dma('pool', WT.wout[:, kc, :], woutd_[:, kc, :], w=['gwout'])
        emit_mods(g, L, g.d_o_mod_w[0], g.d_o_mod_b[0], g.d_o_ln_g[0], g.d_o_ln_b[0], pre=pre)
        win = WT.win
        wbf = A("gwbf", [128, 8, 16], F32)
        wbb = A("gwbb", [128, 8, 16], BF16)
        wout = WT.wout
        cw = A("gcw", [128, 24, 4], F32)
        msl = A("msl", [128, 128], F32)
        mil = A("mil", [128, 128], F32)
        triu = A("triu", [128, 128], F32)
        mdm = A("mdm", [128, 5, 128], BF16)
        mdmT = A("mdmT", [128, 5, 128], BF16)
        alog = A("alog", [128, 8], F32)
        dtb = A("dtb", [128, 8], F32)
        onb = A("onb", [128, D], F32)
        xin = [A("gxin%d" % i, [128, D], F32) for i in range(1)]
        xrs = A("gxrs", [128, D], F32)
        hT = A("ghT", [128, 8, 128], BF16)
        xw = [A("xw%d" % i, [128, 131], F32) for i in range(2)]
        halo = A("ghalo", [128, 24, 3], F32)
        cbuf = [A("cbuf%d" % i, [128, 128], F32) for i in range(2)]
        act1 = A("gact", [128, 24 * 128], F32)
        vb = [A("gvb%d" % i, [128, 1024], BF16) for i in range(2)]
        sq = A("gsq", [128, 1024], F32)
        rstd = sq
        qkb = [A("qkb%d" % i, [128, 2048], BF16) for i in range(2)]
        gsil = [A("gsil%d" % i, [128, D], BF16) for i in range(3)]
        ba = A("ba", [128, 16], F32)
        tmpa = A("tmpa", [128, 8], F32)
        smb = [A("smb%d" % i, [128, 48], F32) for i in range(2)]
        zsm = A("zsm", [128, 8], F32)
        HP = []
        NTH = 3
        for p in range(NTH):
            h_ = Ctx()
            h_.sm = A("hsm%d" % p, [128, 2], F32)
            for nm in ("E", "EB", "t1", "Nf", "atf"):
                setattr(h_, nm, A("h%s%d" % (nm, p), [128, 128], F32))
            for nm in ("X", "Xt", "Q2", "Q2t", "Q4", "Q4t", "Yv", "Yt", "attT", "qdT", "kd", "Kbg", "Vb", "nwT", "vnew"):
                setattr(h_, nm, A("h%s%d" % (nm, p), [128, 128], BF16))
            h_.NM = A("hNM%d" % p, [128, 5, 128], BF16)
            h_.NMt = A("hNMt%d" % p, [128, 5, 128], BF16)
            HP.append(h_)
        Sf = A("gSf", [128, 8, 128], F32)
        Sb = A("gSb", [128, 8, 128], BF16)
        osb = [A("gosb%d" % i, [128, D], F32) for i in range(2)]
        yinT = A("gyinT", [128, 1024], BF16)
        zb = A("gzb", [128, D], F32)
        xo = [A("gxo%d" % i, [128, D], F32) for i in range(1)]
        wind = g.d_o_w_in[0].rearrange("(kc p) n -> p kc n", p=128)
        s.dma('sp', wbf[:], wind[:, :, 4096:4112], w=['gwbf'])
        s.op('dve', lambda e: e.tensor_copy(out=wbb[:], in_=wbf[:]), r=['gwbf'], w=['gwbb'])
        s.dma('sp', cw[:], g.d_o_cw[:, :, :], w=['gcw'])
        s.dma('sp', msl[:], g.d_msl[:, :], w=['msl'])
        s.dma('sp', mil[:], g.d_mil[:, :], w=['mil'])
        s.dma('sp', triu[:], g.d_triu[:, :], w=['triu'])
        s.dma('pool', mdm[:], g.d_mdm[:, :, :], w=['mdm'])
        s.dma('pool', mdmT[:], g.d_mdmT[:, :, :], w=['mdmT'])
        s.dma('sp', alog[:], g.d_o_a_log[0].partition_broadcast(128), w=['alog'])
        s.dma('sp', dtb[:], g.d_o_dt_bias[0].partition_broadcast(128), w=['dtb'])
        s.dma('sp', onb[:], g.d_onb[0].partition_broadcast(128), w=['onb'])
        s.op('act', lambda e: e.activation(out=alog[:], in_=alog[:], func=AF.Exp), r=['alog'], w=['alog'])
        s.op('dve', lambda e: e.tensor_scalar(out=alog[:], in0=alog[:], scalar1=-1.0, scalar2=None, op0=ALU.mult), r=['alog'], w=['alog'])
        s.op('dve', lambda e: e.memset(halo[:], 0.0), w=['ghalo%d' % i for i in range(24)])
        s.op('dve', lambda e: e.memset(Sf[:], 0.0), w=['gSf0', 'gSf1', 'gSf2'])
        s.op('dve', lambda e: e.memset(Sb[:], 0.0), w=['gSb0', 'gSb1', 'gSb2'])
        DK = float(128 ** -0.5)

        def phaseA(qb):
            t0 = qb * 128
            bp = qb % 2
            x3 = qb % 3
            xn = 'gxin0'
            an = 'gact'
            sn = 'smb%d' % bp
            sm = smb[bp]
            ac = act1
            s.dma('sp', xin[0][:], src[t0:t0 + 128, :], w=[xn])
            emit_hT(g, L, xin[0], xn, lambda kc: hT[:, kc, :], 'ghT', (0, 1))
            yield
            for ch in range(24):
                b = ch % 2
                cb = cbuf[b]
                cn = 'cbuf%d' % b
                for kc in range(8):
                    s.op('pe', lambda e, kc=kc, ch=ch, b=b: e.matmul(P[:, b * 512:b * 512 + 128], lhsT=win[:, kc, ch * 128:(ch + 1) * 128], rhs=hT[:, kc, :],
                                                                    start=(kc == 0), stop=(kc == 7)), r=['gwin', 'ghT'], w=['ps%d' % b], inc=(kc == 7))
                xb = xw[b]
                xbn = 'xw%d' % b
                s.op('pool', lambda e, ch=ch, xb=xb: e.tensor_copy(out=xb[:, 0:3], in_=halo[:, ch, :]), r=['ghalo%d' % ch], w=[xbn])
                s.op('act', lambda e, b=b, xb=xb: e.activation(out=xb[:, 3:131], in_=P[:, b * 512:b * 512 + 128], func=AF.Copy), r=['ps%d' % b], w=[xbn])
                s.op('act', lambda e, ch=ch, b=b, cb=cb: e.activation(out=cb[:], in_=P[:, b * 512:b * 512 + 128], func=AF.Identity, scale=cw[:, ch, 3:4]),
                     r=['ps%d' % b, 'gcw'], w=[cn])
                s.op('pool', lambda e, ch=ch, xb=xb: e.tensor_copy(out=halo[:, ch, :], in_=xb[:, 128:131]), r=[xbn], w=['ghalo%d' % ch])
                for j in range(3):
                    dst_ = cb[:] if j < 2 else ac[:, ch * 128:(ch + 1) * 128]
                    s.op('dve', lambda e, ch=ch, j=j, cb=cb, xb=xb, dst_=dst_: e.scalar_tensor_tensor(out=dst_, in0=xb[:, j:j + 128], scalar=cw[:, ch, j:j + 1], in1=cb[:],
                                                                                                 op0=ALU.mult, op1=ALU.add),
                         r=[xbn, 'gcw', cn], w=([cn] if j < 2 else [an]))
                yield
            s.op('act', lambda e: e.activation(out=ac[:], in_=ac[:], func=AF.Silu), r=[an], w=[an])
            yield
            s.op('act', lambda e: e.activation(out=vb[bp][:], in_=ac[:, 2048:3072], func=AF.Copy), r=[an], w=['gvb%d' % bp])
            yield
            for hf in range(2):
                seg = ac[:, hf * 1024:(hf + 1) * 1024]
                s.op('dve', lambda e, seg=seg: e.tensor_tensor(out=sq[:], in0=seg, in1=seg, op=ALU.mult), r=[an], w=['gsq'])
                yield
                for j in range(2):
                    b = j % 2
                    s.op('pe', lambda e, j=j, b=b: e.matmul(P[:, b * 512:(b + 1) * 512], lhsT=g.ones[:], rhs=sq[:, j * 512:(j + 1) * 512], start=True, stop=True),
                         r=['ones', 'gsq'], w=['ps%d' % b])
                for j in range(2):
                    b = j % 2
                    s.op('dve', lambda e, j=j, b=b: e.tensor_scalar(out=sq[:, j * 512:(j + 1) * 512], in0=P[:, b * 512:(b + 1) * 512], scalar1=RMS_EPS, scalar2=None, op0=ALU.add),
                         r=['ps%d' % b], w=['gsq'])
                yield
                s.op('act', lambda e: e.activation(out=sq[:], in_=sq[:], func=AF.Sqrt), r=['gsq'], w=['gsq'])
                s.op('dve', lambda e: e.reciprocal(out=sq[:], in_=sq[:]), r=['gsq'], w=['gsq'])
                yield
                s.op('dve', lambda e, seg=seg: e.tensor_tensor(out=seg, in0=seg, in1=sq[:], op=ALU.mult), r=[an, 'gsq'], w=[an])
                s.op('act', lambda e, seg=seg, hf=hf: e.activation(out=qkb[bp][:, hf * 1024:(hf + 1) * 1024], in_=seg, func=AF.Copy), r=[an], w=['qkb%d' % bp])
                yield
            for half in range(2):
                b = half
                for kc in range(8):
                    s.op('pe', lambda e, kc=kc, half=half, b=b: e.matmul(P[:, b * 512:(b + 1) * 512], lhsT=hT[:, kc, :],
                                                                        rhs=win[:, kc, 3072 + half * 512:3072 + (half + 1) * 512],
                                                                        start=(kc == 0), stop=(kc == 7)), r=['gwin', 'ghT'], w=['ps%d' % b], inc=(kc == 7))
                s.op('act', lambda e, half=half, b=b: e.activation(out=gsil[x3][:, half * 512:(half + 1) * 512], in_=P[:, b * 512:(b + 1) * 512], func=AF.Silu),
                     r=['ps%d' % b], w=['gsil%d' % x3])
                yield
            for kc in range(8):
                s.op('pe', lambda e, kc=kc: e.matmul(P[:, 0:16], lhsT=hT[:, kc, :], rhs=wbb[:, kc, :], start=(kc == 0), stop=(kc == 7)),
                     r=['gwbb', 'ghT'], w=['ps0'], inc=(kc == 7))
            s.op('dve', lambda e: e.tensor_copy(out=ba[:], in_=P[:, 0:16]), r=['ps0'], w=['ba'])
            yield
            s.op('act', lambda e: e.activation(out=sm[:, 0:8], in_=ba[:, 0:8], func=AF.Exp, scale=-1.0), r=['ba'], w=[sn])
            s.op('dve', lambda e: e.tensor_scalar(out=sm[:, 0:8], in0=sm[:, 0:8], scalar1=1.0, scalar2=None, op0=ALU.add), r=[sn], w=[sn])
            s.op('dve', lambda e: e.reciprocal(out=sm[:, 0:8], in_=sm[:, 0:8]), r=[sn], w=[sn])
            s.op('dve', lambda e: e.tensor_scalar(out=sm[:, 32:40], in0=sm[:, 0:8], scalar1=-1.0, scalar2=None, op0=ALU.mult), r=[sn], w=[sn])
            yield
            s.op('dve', lambda e: e.tensor_tensor(out=tmpa[:], in0=ba[:, 8:16], in1=dtb[:], op=ALU.add), r=['ba', 'dtb'], w=['tmpa'])
            s.op('act', lambda e: e.activation(out=tmpa[:], in_=tmpa[:], func=AF.Exp), r=['tmpa'], w=['tmpa'])
            s.op('act', lambda e: e.activation(out=tmpa[:], in_=tmpa[:], func=AF.Ln, bias=1.0), r=['tmpa'], w=['tmpa'])
            s.op('dve', lambda e: e.tensor_tensor(out=sm[:, 8:16], in0=tmpa[:], in1=alog[:], op=ALU.mult), r=['tmpa', 'alog', sn], w=[sn])
            yield
            s.op('pe', lambda e: e.matmul(P[:, 16:24], lhsT=triu[:], rhs=sm[:, 8:16], start=True, stop=True), r=['triu', sn], w=['ps0'])
            s.op('dve', lambda e: e.tensor_copy(out=sm[:, 16:24], in_=P[:, 16:24]), r=['ps0', sn], w=[sn])
            s.op('act', lambda e: e.activation(out=sm[:, 24:32], in_=sm[:, 16:24], func=AF.Exp), r=[sn], w=[sn])
            s.op('dve', lambda e: e.tensor_tensor(out=sm[:, 40:48], in0=sm[:, 24:32], in1=sm[:, 0:8], op=ALU.mult), r=[sn], w=[sn])
            yield

        def heads(qb, p):
            bp = qb % 2
            sm = smb[bp]
            sn = 'smb%d' % bp
            vn = 'gvb%d' % bp
            H = HP[p]
            X0 = (2 + 2 * p) * 512
            Y0 = X0 + 512
            xn_, yn_ = 'ps%d' % (2 + 2 * p), 'ps%d' % (3 + 2 * p)
            zn_ = xn_
            n = lambda nm: 'h%s%d' % (nm, p)
            for h in range(p, 8, NTH):
                vvb = vb[bp][:, h * 128:(h + 1) * 128]
                qnb = qkb[bp][:, h * 128:(h + 1) * 128]
                knb = qkb[bp][:, (8 + h) * 128:(9 + h) * 128]
                qbn = 'qkb%d' % bp
                s.op('dve', lambda e, h=h: e.tensor_scalar(out=H.t1[:], in0=g.ident[:], scalar1=sm[:, 16 + h:17 + h], scalar2=None, op0=ALU.mult),
                     r=['ident', sn], w=[n('t1')])
                s.op('pe', lambda e: e.matmul(P[:, X0:X0 + 128], lhsT=g.ones[:], rhs=H.t1[:], start=True, stop=True), r=['ones', n('t1')], w=[xn_])
                s.op('dve', lambda e, h=h: e.tensor_scalar(out=H.E[:], in0=P[:, X0:X0 + 128], scalar1=sm[:, 16 + h:17 + h], scalar2=0.0, op0=ALU.subtract, op1=ALU.max),
                     r=[xn_, sn], w=[n('E')])
                s.op('act', lambda e: e.activation(out=H.E[:], in_=H.E[:], func=AF.Exp, scale=-1.0), r=[n('E')], w=[n('E')])
                s.op('act', lambda e: e.activation(out=H.EB[:], in_=P[:, X0:X0 + 128], func=AF.Exp), r=[xn_], w=[n('EB')])
                s.op('act', lambda e: e.activation(out=H.sm[:, 0:1], in_=P[:, X0 + 127:X0 + 128], func=AF.Exp), r=[xn_], w=[n('sm0')])
                s.op('dve', lambda e, h=h: e.tensor_scalar(out=H.sm[:, 1:2], in0=P[:, X0 + 127:X0 + 128], scalar1=sm[:, 16 + h:17 + h], scalar2=None, op0=ALU.subtract),
                     r=[xn_, sn], w=[n('sm1')])
                s.op('act', lambda e: e.activation(out=H.sm[:, 1:2], in_=H.sm[:, 1:2], func=AF.Exp), r=[n('sm1')], w=[n('sm1')])
                yield
                s.op('pe', lambda e, knb=knb: e.matmul(P[:, X0 + 128:X0 + 256], lhsT=knb, rhs=knb, start=True, stop=True), r=[qbn], w=[xn_])
                s.op('pe', lambda e, knb=knb, qnb=qnb: e.matmul(P[:, X0 + 256:X0 + 384], lhsT=qnb, rhs=knb, start=True, stop=True), r=[qbn], w=[xn_])
                s.op('dve', lambda e: e.tensor_tensor(out=H.t1[:], in0=H.E[:], in1=msl[:], op=ALU.mult), r=[n('E'), 'msl'], w=[n('t1')])
                s.op('dve', lambda e, h=h: e.scalar_tensor_tensor(out=H.Nf[:], in0=P[:, X0 + 128:X0 + 256], scalar=sm[:, 32 + h:33 + h], in1=H.t1[:], op0=ALU.mult, op1=ALU.mult),
                     r=[xn_, sn, n('t1')], w=[n('Nf')])
                s.op('dve', lambda e: e.tensor_tensor(out=H.t1[:], in0=H.E[:], in1=mil[:], op=ALU.mult), r=[n('E'), 'mil', n('Nf')], w=[n('t1')])
                s.op('dve', lambda e: e.scalar_tensor_tensor(out=H.atf[:], in0=P[:, X0 + 256:X0 + 384], scalar=DK, in1=H.t1[:], op0=ALU.mult, op1=ALU.mult),
                     r=[xn_, n('t1')], w=[n('atf')])
                yield
                s.op('pe', lambda e: e.transpose(P[:, Y0:Y0 + 128], H.Nf[:], g.ident[:]), r=[n('Nf'), 'ident'], w=[yn_])
                s.op('pe', lambda e: e.transpose(P[:, Y0 + 128:Y0 + 256], H.atf[:], g.ident[:]), r=[n('atf'), 'ident'], w=[yn_])
                s.op('pe', lambda e, knb=knb: e.matmul(P[:, Y0 + 256:Y0 + 384], lhsT=knb, rhs=g.identb[:], start=True, stop=True), r=[qbn, 'identb'], w=[yn_])
                s.op('pe', lambda e, vvb=vvb: e.matmul(P[:, Y0 + 384:Y0 + 512], lhsT=vvb, rhs=g.identb[:], start=True, stop=True), r=[vn, 'identb'], w=[yn_])
                s.op('dve', lambda e: e.tensor_tensor(out=H.NM[:], in0=H.Nf[:].unsqueeze(1).to_broadcast([128, 5, 128]), in1=mdm[:], op=ALU.mult),
                     r=[n('Nf'), 'mdm'], w=[n('NM')])
                s.op('dve', lambda e: e.tensor_tensor(out=H.NMt[:], in0=P[:, Y0:Y0 + 128].unsqueeze(1).to_broadcast([128, 5, 128]), in1=mdmT[:], op=ALU.mult),
                     r=[yn_, 'mdmT'], w=[n('NMt')])
                s.op('act', lambda e: e.activation(out=H.attT[:], in_=P[:, Y0 + 128:Y0 + 256], func=AF.Copy), r=[yn_], w=[n('attT')])
                s.op('dve', lambda e: e.tensor_scalar(out=H.kd[:], in0=P[:, Y0 + 256:Y0 + 384], scalar1=H.sm[:, 1:2], scalar2=None, op0=ALU.mult), r=[yn_, n('sm1')], w=[n('kd')])
                s.op('dve', lambda e, h=h: e.tensor_scalar(out=H.Kbg[:], in0=P[:, Y0 + 256:Y0 + 384], scalar1=sm[:, 40 + h:41 + h], scalar2=None, op0=ALU.mult),
                     r=[yn_, sn], w=[n('Kbg')])
                s.op('dve', lambda e, h=h: e.tensor_scalar(out=H.Vb[:], in0=P[:, Y0 + 384:Y0 + 512], scalar1=sm[:, h:h + 1], scalar2=None, op0=ALU.mult), r=[yn_, sn], w=[n('Vb')])
                s.op('dve', lambda e, qnb=qnb: e.scalar_tensor_tensor(out=H.qdT[:], in0=qnb, scalar=DK, in1=H.EB[:], op0=ALU.mult, op1=ALU.mult),
                     r=[qbn, n('EB')], w=[n('qdT')])
                yield
                zc = [0]

                def mm(lhsT, rhs, rn):
                    c0 = X0 + (zc[0] % 3) * 128
                    zc[0] += 1
                    s.op('pe', lambda e: e.matmul(P[:, c0:c0 + 128], lhsT=lhsT, rhs=rhs, start=True, stop=True), r=rn, w=[zn_])
                    return P[:, c0:c0 + 128]

                def cp(dst, dn, src_ps):
                    s.op('act', lambda e: e.activation(out=dst, in_=src_ps, func=AF.Copy), r=[zn_], w=[dn])

                def acc(dst, dn, src_ps):
                    s.op('dve', lambda e: e.tensor_tensor(out=dst, in0=src_ps, in1=dst, op=ALU.add), r=[zn_, dn], w=[dn])

                M0, M0t = H.NM[:, 0, :], H.NMt[:, 0, :]
                s.op('dve', lambda e: e.tensor_tensor(out=H.X[:], in0=M0, in1=g.identb[:], op=ALU.add), r=[n('NM'), 'identb'], w=[n('X')])
                s.op('dve', lambda e: e.tensor_tensor(out=H.Xt[:], in0=M0t, in1=g.identb[:], op=ALU.add), r=[n('NMt'), 'identb'], w=[n('Xt')])
                cp(H.Q2[:], n('Q2'), mm(M0t, M0, [n('NM'), n('NMt')]))
                cp(H.Q2t[:], n('Q2t'), mm(M0, M0t, [n('NM'), n('NMt')]))
                yield
                acc(H.X[:], n('X'), mm(H.Q2t[:], H.X[:], [n('Q2t'), n('X')]))
                acc(H.Xt[:], n('Xt'), mm(H.Q2[:], H.Xt[:], [n('Q2'), n('Xt')]))
                cp(H.Q4[:], n('Q4'), mm(H.Q2t[:], H.Q2[:], [n('Q2'), n('Q2t')]))
                cp(H.Q4t[:], n('Q4t'), mm(H.Q2[:], H.Q2t[:], [n('Q2'), n('Q2t')]))
                yield
                acc(H.X[:], n('X'), mm(H.Q4t[:], H.X[:], [n('Q4t'), n('X')]))
                acc(H.Xt[:], n('Xt'), mm(H.Q4[:], H.Xt[:], [n('Q4'), n('Xt')]))
                yield
                for lv in range(1, 5):
                    Nb, Nbt = H.NM[:, lv, :], H.NMt[:, lv, :]
                    if lv < 4:
                        cp(H.Yv[:], n('Yv'), mm(Nbt, H.X[:], [n('NMt'), n('X')]))
                    cp(H.Yt[:], n('Yt'), mm(Nb, H.Xt[:], [n('NM'), n('Xt')]))
                    yield
                    pa = mm(H.Xt[:], H.Yv[:], [n('Xt'), n('Yv')]) if lv < 4 else None
                    pb = mm(H.X[:], H.Yt[:], [n('X'), n('Yt')])
                    if lv < 4:
                        acc(H.X[:], n('X'), pa)
                    acc(H.Xt[:], n('Xt'), pb)
                    yield
                pw = mm(H.Kbg[:], H.Xt[:], [n('Kbg'), n('Xt')])
                s.op('act', lambda e: e.activation(out=H.nwT[:], in_=pw, func=AF.Copy, scale=-1.0), r=[zn_], w=[n('nwT')])
                yield
                s.op('pe', lambda e: e.matmul(P[:, Y0:Y0 + 128], lhsT=H.Xt[:], rhs=H.Vb[:], start=True, stop=False), r=[n('Xt'), n('Vb')], w=[yn_], inc=False)
                s.op('pe', lambda e, h=h: e.matmul(P[:, Y0:Y0 + 128], lhsT=H.nwT[:], rhs=Sb[:, h, :], start=False, stop=True), r=[n('nwT'), 'gSb%d' % p], w=[yn_])
                s.op('act', lambda e: e.activation(out=H.vnew[:], in_=P[:, Y0:Y0 + 128], func=AF.Copy), r=[yn_], w=[n('vnew')])
                yield
                s.op('pe', lambda e, h=h: e.matmul(P[:, X0 + 384:X0 + 512], lhsT=H.qdT[:], rhs=Sb[:, h, :], start=True, stop=False),
                     r=[n('qdT'), 'gSb%d' % p], w=[xn_], inc=False)
                s.op('pe', lambda e: e.matmul(P[:, X0 + 384:X0 + 512], lhsT=H.attT[:], rhs=H.vnew[:], start=False, stop=True),
                     r=[n('attT'), n('vnew')], w=[xn_])
                s.op('act', lambda e, h=h: e.activation(out=osb[bp][:, h * 128:(h + 1) * 128], in_=P[:, X0 + 384:X0 + 512], func=AF.Copy),
                     r=[xn_], w=['gosb%d_%d' % (bp, p)])
                s.op('pe', lambda e: e.matmul(P[:, Y0 + 128:Y0 + 256], lhsT=H.kd[:], rhs=H.vnew[:], start=True, stop=True), r=[n('kd'), n('vnew')], w=[yn_])
                s.op('dve', lambda e, h=h: e.scalar_tensor_tensor(out=Sf[:, h, :], in0=Sf[:, h, :], scalar=H.sm[:, 0:1], in1=P[:, Y0 + 128:Y0 + 256], op0=ALU.mult, op1=ALU.add),
                     r=['gSf%d' % p, n('sm0'), yn_], w=['gSf%d' % p])
                s.op('act', lambda e, h=h: e.activation(out=Sb[:, h, :], in_=Sf[:, h, :], func=AF.Copy), r=['gSf%d' % p], w=['gSb%d' % p])
                yield

        def phaseZ(qb):
            t0 = qb * 128
            bp = qb % 2
            x3 = qb % 3
            ob = osb[bp]
            on = ['gosb%d_%d' % (bp, p_) for p_ in range(NTH)]
            s.op('dve', lambda e: e.tensor_tensor(out=zb[:], in0=ob[:], in1=ob[:], op=ALU.mult), r=on, w=['zb'])
            for h in range(8):
                s.op('dve', lambda e, h=h: e.reduce_sum(out=zsm[:, h:h + 1], in_=zb[:, h * 128:(h + 1) * 128], axis=AX.X), r=['zb'], w=['zsm'])
            yield
            s.op('dve', lambda e: e.tensor_scalar(out=zsm[:], in0=zsm[:], scalar1=1.0 / 128, scalar2=RMS_EPS, op0=ALU.mult, op1=ALU.add), r=['zsm'], w=['zsm'])
            s.op('act', lambda e: e.activation(out=zsm[:], in_=zsm[:], func=AF.Sqrt), r=['zsm'], w=['zsm'])
            s.op('dve', lambda e: e.reciprocal(out=zsm[:], in_=zsm[:]), r=['zsm'], w=['zsm'])
            yield
            for h in range(8):
                s.op('dve', lambda e, h=h: e.tensor_scalar(out=ob[:, h * 128:(h + 1) * 128], in0=ob[:, h * 128:(h + 1) * 128], scalar1=zsm[:, h:h + 1],
                                                           scalar2=None, op0=ALU.mult), r=on + ['zsm'], w=on)
            yield
            s.op('dve', lambda e: e.tensor_tensor(out=ob[:], in0=ob[:], in1=onb[:], op=ALU.mult), r=on + ['onb'], w=on)
            s.op('dve', lambda e: e.tensor_tensor(out=ob[:], in0=ob[:], in1=gsil[x3][:], op=ALU.mult), r=on + ['gsil%d' % x3], w=on)
            yield
            for kc in range(8):
                b = kc // 4
                off = b * 512 + (kc % 4) * 128
                s.op('pe', lambda e, kc=kc, off=off: e.transpose(P[:, off:off + 128], ob[:, kc * 128:(kc + 1) * 128], g.ident[:]), r=on + ['ident'], w=['ps%d' % b])
            for j in range(2):
                s.op('act', lambda e, j=j: e.activation(out=yinT[:, j * 512:(j + 1) * 512], in_=P[:, j * 512:(j + 1) * 512], func=AF.Copy),
                     r=['ps%d' % j], w=['gyinT'])
            yield
            for half in range(2):
                b = half
                for kc in range(8):
                    s.op('pe', lambda e, kc=kc, half=half, b=b: e.matmul(P[:, b * 512:(b + 1) * 512], lhsT=yinT[:, kc * 128:(kc + 1) * 128],
                                                                        rhs=wout[:, kc, half * 512:(half + 1) * 512], start=(kc == 0), stop=(kc == 7)),
                         r=['gyinT', 'gwout'], w=['ps%d' % b], inc=(kc == 7))
            s.dma('sp', xrs[:], src[t0:t0 + 128, :], w=['gxrs'])
            emit_epilogue(g, L, (0, 1), xrs, 'gxrs', zb, xo[0], 'gxo0', dst[t0:t0 + 128, :])
            yield

        nblk = g.nblk_gdn
        for it in range(nblk + 2):
            gens = []
            if 1 <= it <= nblk:
                for p_ in range(NTH):
                    gens.append([heads(it - 1, p_), 1])
            if it < nblk:
                gens.append([phaseA(it), 1])
            if it >= 2:
                gens.append([phaseZ(it - 2), 1])
            _interleave(gens)
        s.barrier()


W_SPECS = [
    ("e_mod_w", [1, D, 3 * D]), ("e_mod_b", [1, 1, 3 * D]), ("e_ln_g", [1, D]), ("e_ln_b", [1, D]),
    ("o_mod_w", [1, D, 3 * D]), ("o_mod_b", [1, 1, 3 * D]), ("o_ln_g", [1, D]), ("o_ln_b", [1, D]),
    ("f_mod_w", [2, D, 3 * D]), ("f_mod_b", [2, 1, 3 * D]), ("f_ln_g", [2, D]), ("f_ln_b", [2, D]),
    ("f_w_up", [2, D, 2 * DFF]), ("f_w_down", [2, DFF, D]), ("f_cw", [2, 128, 2 * NFC, 4]),
    ("ident", [128, 128]),
    ("e_w_in", [1, D, 1476]), ("e_pool_w", [1, 4, 128, 128]), ("pscale", [128, 4]), ("e_kv_norm", [1, 128]),
    ("o_w_in", [1, D, 4112]), ("o_w_out", [1, D, D]), ("o_cw", [128, 24, 4]), ("msl", [128, 128]), ("mil", [128, 128]), ("triu", [128, 128]), ("mdm", [128, 5, 128]), ("mdmT", [128, 5, 128]),
    ("o_a_log", [1, 8]), ("o_dt_bias", [1, 8]), ("onb", [1, D]),
    ("ukT", [128, 4, 128]), ("uvpad", [128, 8, 128]), ("e_w_out", [1, D, D]), ("negI4", [128, 512]), ("corr", [128, 4, 15]),
]


def build(stages):
    nc = bass.Bass("TRN2", target_bir_lowering=False)
    g = Ctx()
    g.nc = nc
    g.d_x = nc.dram_tensor("x", [S, D], F32, kind="ExternalInput").ap()
    g.d_ccol = nc.dram_tensor("ccol", [128, 8], F32, kind="ExternalInput").ap()
    for nm, shp in W_SPECS:
        setattr(g, "d_" + nm, nc.dram_tensor(nm, shp, F32, kind="ExternalInput").ap())
    g.d_out = nc.dram_tensor("out", [S, D], F32, kind="ExternalOutput").ap()
    scr = [nc.dram_tensor("xscr%d" % i, [S, D], F32, kind="Internal").ap() for i in range(3)]
    with ExitStack() as es:
        g.es = es
        g.s = Sch(nc, es)
        g.ps = es.enter_context(nc.psum_tensor("ps", [128, 8 * 512], F32))
        g.lnst = _sb(nc, es, "lnst", [128, 12], F32)
        g.lnmv = _sb(nc, es, "lnmv", [128, 8], F32)
        emit_consts(g)
        bufs = [g.d_x] + scr
        n = len(stages)
        for i, st in enumerate(stages):
            src = g.d_x if i == 0 else scr[(i - 1) % 3]
            dst = g.d_out if i == n - 1 else scr[i % 3]
            if st[0] == 'ffn':
                emit_ffn(g, st[1], src, dst)
            elif st[0] == 'gdn':
                g.nblk_gdn = st[1] if len(st) > 1 else NB
                emit_gdn(g, src, dst)
            elif st[0] == 'dsa':
                g.nblk_dsa = st[1] if len(st) > 1 else NB
                emit_dsa(g, src, dst)
            else:
                raise ValueError(st)
        g.s.finish()
    return nc


def prep_weights(inp):
    f = lambda a: np.ascontiguousarray(np.asarray(a, dtype=np.float32))
    w = {}
    for k in ("e_mod_w", "e_ln_g", "e_ln_b", "o_mod_w", "o_ln_g", "o_ln_b", "f_mod_w", "f_ln_g", "f_ln_b", "f_w_up", "f_w_down"):
        w[k] = f(inp[k])
    for k in ("e_mod_b", "o_mod_b", "f_mod_b"):
        a = f(inp[k])
        w[k] = np.ascontiguousarray(a.reshape(a.shape[0], 1, 3 * D))
    cwt = f(inp["f_conv_w"])
    cb = f(inp["f_conv_b"])
    a = np.concatenate([cwt, cb[:, None, :]], axis=1)
    a = a.reshape(2, 4, 2 * NFC, 128).transpose(0, 3, 2, 1)
    w["f_cw"] = np.ascontiguousarray(a)
    w["ident"] = np.eye(128, dtype=np.float32)
    for k in ("e_w_in", "e_pool_w", "e_kv_norm", "e_w_out"):
        w[k] = f(inp[k])
    w["pscale"] = np.ascontiguousarray(f(inp["e_pool_scale"])[0].reshape(4, 128).T)
    uk = f(inp["e_w_uk"])[0]
    w["ukT"] = np.ascontiguousarray(uk.reshape(4, 2, 128, 64).transpose(1, 3, 0, 2).reshape(128, 4, 128))
    uv = f(inp["e_w_uv"])[0]
    uvp = np.zeros((128, 8, 128), np.float32)
    for h in range(8):
        uvp[:, h, (h % 2) * 64:(h % 2) * 64 + 64] = uv[h]
    w["uvpad"] = uvp
    w["negI4"] = np.ascontiguousarray(np.tile(-30000.0 * np.eye(128, dtype=np.float32), (1, 4)))
    corr = np.ones((128, 4, 15), np.float32)
    for gi in range(4):
        win_ = 2 << gi
        for t in range(win_ - 1):
            corr[:, gi, t] = win_ / (t + 1.0)
    w["corr"] = corr
    for k in ("o_w_in", "o_w_out", "o_a_log", "o_dt_bias"):
        w[k] = f(inp[k])
    ocw = f(inp["o_conv_w"])[0]
    w["o_cw"] = np.ascontiguousarray(ocw.reshape(4, 24, 128).transpose(2, 1, 0))
    ar = np.arange(128)
    w["msl"] = (ar[:, None] > ar[None, :]).astype(np.float32)
    w["mil"] = (ar[:, None] >= ar[None, :]).astype(np.float32)
    w["triu"] = (ar[:, None] <= ar[None, :]).astype(np.float32)
    w["onb"] = np.ascontiguousarray(np.tile(f(inp["o_out_norm"])[0], 8)[None, :])
    mdm = np.zeros((128, 5, 128), np.float32)
    mdm[:, 0, :] = (ar[:, None] // 8 == ar[None, :] // 8)
    for li, bsz in enumerate((8, 16, 32, 64)):
        bl = ar // bsz
        mdm[:, 1 + li, :] = (bl[:, None] % 2 == 1) & (bl[None, :] == bl[:, None] - 1)
    w["mdm"] = mdm
    w["mdmT"] = np.ascontiguousarray(mdm.transpose(2, 1, 0))
    return w


STAGES = [('dsa',), ('ffn', 0), ('gdn',), ('ffn', 1)]


def kernel(**inp):
    x = np.asarray(inp["x"], dtype=np.float32)
    c = np.asarray(inp["c"], dtype=np.float32)
    w = prep_weights(inp)
    nc = build(STAGES)
    in_maps = []
    for b in range(8):
        m = dict(w)
        m["x"] = np.ascontiguousarray(x[b])
        m["ccol"] = np.ascontiguousarray(c[b].reshape(8, 128).T)
        in_maps.append(m)
    res = run_bass_kernel_spmd(nc, in_maps, core_ids=list(range(8)))
    return np.stack([np.asarray(r["out"], dtype=np.float32) for r in res.results], axis=0)
```

```python
import os
import numpy as np
from contextlib import ExitStack
import concourse.bass as bass
import concourse.mybir as mybir
from concourse.bass_utils import run_bass_kernel_spmd

F32 = mybir.dt.float32
BF16 = mybir.dt.bfloat16
AF = mybir.ActivationFunctionType
ALU = mybir.AluOpType
AX = mybir.AxisListType

D = 1024
S = 4096
NB = S // 128
DFF = 2688
NFC = DFF // 128
ALPHA = float(4 ** 0.25)
LN_EPS = 1e-5
RMS_EPS = 1e-6
NDS = 12


class Sch:
    def __init__(self, nc, es):
        self.nc = nc
        self.E = {'pe': nc.tensor, 'act': nc.scalar, 'dve': nc.vector,
                  'pool': nc.gpsimd, 'sp': nc.sync}
        self.sem = {}
        for e in self.E:
            self.sem[e] = es.enter_context(nc.semaphore('s_' + e))
        self.cnt = {e: 0 for e in self.E}
        self.waited = {e: {} for e in self.E}
        self.lastw = {}
        self.readers = {}
        self.dq = ('sp', 'pool', 'act')
        self.dcnt = {}
        self.drr = {q: 0 for q in self.dq}
        for q in self.dq:
            for i in range(NDS):
                k = (q, i)
                self.sem[k] = es.enter_context(nc.semaphore('d_%s%d' % (q, i)))
                self.dcnt[k] = 0
        self.nwaits = 0

    def _wait(self, e, tok):
        key, val = tok
        if key == e and e == 'pe':
            return
        if self.waited[e].get(key, 0) >= val:
            return
        self.E[e].wait_ge(self.sem[key], val)
        self.waited[e][key] = val
        self.nwaits += 1

    def _collect(self, r, w, e=None):
        deps = {}

        def add(t):
            if t is None:
                return
            if deps.get(t[0], 0) < t[1]:
                deps[t[0]] = t[1]
        for x in r:
            for k, v in self.lastw.get(x, {}).items():
                add((k, v))
            if x.startswith('ps') and e is not None:
                for k, v in self.readers.get(x, {}).items():
                    if k != e:
                        add((k, v))
        for x in w:
            for k, v in self.lastw.get(x, {}).items():
                add((k, v))
            for k, v in self.readers.get(x, {}).items():
                add((k, v))
        return list(deps.items())

    def _record(self, tok, r, w):
        for x in w:
            self.lastw.setdefault(x, {})[tok[0]] = tok[1]
            self.readers[x] = {}
        for x in r:
            d = self.readers.setdefault(x, {})
            if d.get(tok[0], 0) < tok[1]:
                d[tok[0]] = tok[1]

    def op(self, e, fn, r=(), w=(), inc=True):
        for t in self._collect(r, w, e):
            self._wait(e, t)
        ins = fn(self.E[e])
        if inc:
            self.cnt[e] += 1
            ins.then_inc(self.sem[e], 1)
            tok = (e, self.cnt[e])
        else:
            tok = (e, self.cnt[e] + 1)
        self._record(tok, r, w)
        return ins

    def dma(self, q, out, in_, r=(), w=()):
        i = self.drr[q]
        self.drr[q] = (i + 1) % NDS
        k = (q, i)
        if self.dcnt[k] > 0:
            self._wait(q, (k, self.dcnt[k]))
        for t in self._collect(r, w):
            self._wait(q, t)
        ins = self.E[q].dma_start(out=out, in_=in_)
        self.dcnt[k] += 16
        ins.then_inc(self.sem[k], 16)
        tok = (k, self.dcnt[k])
        self._record(tok, r, w)
        return ins

    def barrier(self, keep_pool_dma=False):
        def is_pool(k):
            return isinstance(k, tuple) and k[0] == 'pool'
        toks = [(e, self.cnt[e]) for e in self.E if self.cnt[e] > 0]
        toks += [(k, v) for k, v in self.dcnt.items() if v > 0 and not (keep_pool_dma and is_pool(k))]
        for e in self.E:
            for t in toks:
                self._wait(e, t)
        keep = {}
        if keep_pool_dma:
            for res, d in self.lastw.items():
                d2 = {k: v for k, v in d.items() if is_pool(k)}
                if d2:
                    keep[res] = d2
        self.lastw = keep
        self.readers = {}

    def finish(self):
        for k, v in self.dcnt.items():
            if v > 0:
                self._wait('sp', (k, v))


class Ctx:
    pass


def _interleave(gens):
    live = list(gens)
    while live:
        nxt = []
        for item in live:
            gen, n = item
            done = False
            for _ in range(n):
                try:
                    next(gen)
                except StopIteration:
                    done = True
                    break
            if not done:
                nxt.append(item)
        live = nxt


_UID = [0]


def _sb(nc, es, name, shape, dt):
    _UID[0] += 1
    return es.enter_context(nc.sbuf_tensor("sb%d_%s" % (_UID[0], name), list(shape), dt))


def emit_consts(g):
    nc, s, es = g.nc, g.s, g.es
    g.ident = _sb(nc, es, "ident", [128, 128], F32)
    g.identb = _sb(nc, es, "identb", [128, 128], BF16)
    g.ones = _sb(nc, es, "ones", [128, 128], F32)
    g.onesb = _sb(nc, es, "onesb", [128, 128], BF16)
    s.dma('sp', g.ident[:], g.d_ident[:, :], w=['ident'])
    s.dma('pool', g.identb[:], g.d_ident[:, :], w=['identb'])
    s.op('dve', lambda e: e.memset(g.ones[:], 1.0), w=['ones'])
    s.op('dve', lambda e: e.memset(g.onesb[:], 1.0), w=['onesb'])
    g.mhalf = _sb(nc, es, "mhalf", [128, 8], F32)
    s.op('pool', lambda e: e.memset(g.mhalf[:], -0.5), w=['mhalf'])
    g.ccol = _sb(nc, es, "ccol", [128, 8], F32)
    g.sc = _sb(nc, es, "sc", [128, 8], F32)
    g.scb = _sb(nc, es, "scb", [128, 8, 128], F32)
    s.dma('sp', g.ccol[:], g.d_ccol[:, :], w=['ccol'])
    s.op('act', lambda e: e.activation(out=g.sc[:], in_=g.ccol[:], func=AF.Silu), r=['ccol'], w=['sc'])
    for kc in range(8):
        s.op('dve', lambda e, kc=kc: e.tensor_scalar(out=g.scb[:, kc, :], in0=g.ones[:], scalar1=g.sc[:, kc:kc + 1],
                                                     scalar2=None, op0=ALU.mult), r=['ones', 'sc'], w=['scb'])


def emit_mods(g, L, modw, modb_row, lng, lnb, pre=None):
    nc, s, es = g.nc, g.s, g.es
    L.shift = _sb(nc, L.es, "shift", [128, 8], F32)
    L.scale1 = _sb(nc, L.es, "scale1", [128, 8], F32)
    L.gate_bc = _sb(nc, L.es, "gate_bc", [128, D], F32)
    L.lng_bc = _sb(nc, L.es, "lng_bc", [128, D], F32)
    L.lnb_bc = _sb(nc, L.es, "lnb_bc", [128, D], F32)
    s.dma('sp', L.lng_bc[:], lng.partition_broadcast(128), w=['lng_bc'])
    s.dma('sp', L.lnb_bc[:], lnb.partition_broadcast(128), w=['lnb_bc'])
    if pre is not None:
        pre()
    with ExitStack() as es2:
        mw = [_sb(nc, es2, "mw%d" % i, [128, 3 * D], F32) for i in range(2)]
        brow = _sb(nc, es2, "brow", [1, 3 * D], F32)
        bc = _sb(nc, es2, "modbc", [128, 2 * D], F32)
        one11 = _sb(nc, es2, "one11", [1, 1], F32)
        s.op('dve', lambda e: e.memset(one11[:], 1.0), w=['one11'])
        s.dma('sp', brow[:], modb_row[:, :], w=['brow'])
        P = g.ps
        for kc in range(8):
            t = mw[kc % 2]
            nm = 'mw%d' % (kc % 2)
            s.dma('sp', t[:], modw[kc * 128:(kc + 1) * 128, :], w=[nm])
            for j in range(6):
                s.op('pe', lambda e, j=j, kc=kc, t=t: e.matmul(P[:, j * 512:(j + 1) * 512], lhsT=g.scb[:, kc, :],
                                                            rhs=t[:, j * 512:(j + 1) * 512], start=(kc == 0), stop=False),
                     r=[nm, 'scb'], w=['ps%d' % j], inc=(j == 5))
        for j in range(6):
            s.op('pe', lambda e, j=j: e.matmul(P[:, j * 512:(j + 1) * 512], lhsT=g.ones[0:1, :],
                                               rhs=brow[0:1, j * 512:(j + 1) * 512], start=False, stop=True),
                 r=['brow', 'ones'], w=['ps%d' % j])
        for j in range(4):
            s.op('act' if j % 2 else 'dve',
                 (lambda e, j=j: e.activation(out=bc[:, j * 512:(j + 1) * 512], in_=P[:, j * 512:(j + 1) * 512], func=AF.Copy))
                 if j % 2 else
                 (lambda e, j=j: e.tensor_copy(out=bc[:, j * 512:(j + 1) * 512], in_=P[:, j * 512:(j + 1) * 512])),
                 r=['ps%d' % j], w=['modbc%d' % j])
        for j in range(2):
            s.op('dve', lambda e, j=j: e.tensor_copy(out=L.gate_bc[:, j * 512:(j + 1) * 512], in_=P[:, (4 + j) * 512:(5 + j) * 512]),
                 r=['ps%d' % (4 + j)], w=['gate_bc'])
        for j in range(16):
            s.op('pe', lambda e, j=j: e.matmul(P[:, 6 * 512 + j:6 * 512 + j + 1], lhsT=bc[0:1, j * 128:(j + 1) * 128],
                                               rhs=one11[0:1, 0:1], start=True, stop=True),
                 r=['modbc%d' % (j // 4), 'one11'], w=['ps6'])
        s.op('dve', lambda e: e.tensor_copy(out=L.shift[:], in_=P[:, 6 * 512:6 * 512 + 8]), r=['ps6'], w=['shift'])
        s.op('dve', lambda e: e.tensor_scalar(out=L.scale1[:], in0=P[:, 6 * 512 + 8:6 * 512 + 16], scalar1=1.0, scalar2=None,
                                              op0=ALU.add), r=['ps6'], w=['scale1'])
        s.barrier(keep_pool_dma=(pre is not None))


def emit_hT(g, L, xin, xin_nm, hT_ap_fn, hT_nm, pbanks):
    s = g.s
    P = g.ps
    for kc in range(8):
        b = pbanks[kc // 4]
        off = b * 512 + (kc % 4) * 128
        s.op('pe', lambda e, kc=kc, off=off: e.transpose(P[:, off:off + 128], xin[:, kc * 128:(kc + 1) * 128], g.ident[:]),
             r=[xin_nm, 'ident'], w=['ps%d' % b])
    for kc in range(8):
        b = pbanks[kc // 4]
        off = b * 512 + (kc % 4) * 128
        s.op('act', lambda e, kc=kc, off=off: e.activation(out=hT_ap_fn(kc), in_=P[:, off:off + 128], func=AF.Identity,
                                                           scale=L.scale1[:, kc:kc + 1], bias=L.shift[:, kc:kc + 1]),
             r=['ps%d' % b, 'scale1', 'shift'], w=[hT_nm])


def emit_epilogue(g, L, ybanks, xres, xres_nm, zb, xo, xo_nm, dst_rows):
    s = g.s
    P = g.ps
    for j in range(2):
        b = ybanks[j]
        s.op('dve', lambda e, j=j, b=b: e.tensor_tensor(out=zb[:, j * 512:(j + 1) * 512], in0=P[:, b * 512:(b + 1) * 512],
                                                        in1=L.gate_bc[:, j * 512:(j + 1) * 512], op=ALU.mult),
             r=['ps%d' % b, 'gate_bc'], w=['zb'])
    s.op('dve', lambda e: e.scalar_tensor_tensor(out=zb[:], in0=xres[:], scalar=ALPHA, in1=zb[:], op0=ALU.mult, op1=ALU.add),
         r=[xres_nm, 'zb'], w=['zb'])
    st = g.lnst
    for j in range(2):
        s.op('dve', lambda e, j=j: e.bn_stats(out=st[:, j * 6:(j + 1) * 6], in_=zb[:, j * 512:(j + 1) * 512]), r=['zb'], w=['lnst'])
    s.op('dve', lambda e: e.bn_aggr(out=g.lnmv[:, 0:2], in_=st[:, 0:12]), r=['lnst'], w=['lnmv'])
    s.op('dve', lambda e: e.tensor_scalar(out=g.lnmv[:, 2:3], in0=g.lnmv[:, 1:2], scalar1=LN_EPS, scalar2=None, op0=ALU.add),
         r=['lnmv'], w=['lnmv2'])
    s.op('pool', lambda e: e.tensor_tensor(out=g.lnmv[:, 4:5], in0=g.lnmv[:, 2:3], in1=g.mhalf[:, 0:1], op=ALU.pow), r=['lnmv2', 'mhalf'], w=['lnmv4'])
    s.op('dve', lambda e: e.tensor_scalar(out=zb[:], in0=zb[:], scalar1=g.lnmv[:, 0:1], scalar2=g.lnmv[:, 4:5],
                                          op0=ALU.subtract, op1=ALU.mult), r=['zb', 'lnmv', 'lnmv4'], w=['zb'])
    s.op('pool', lambda e: e.tensor_tensor(out=zb[:], in0=zb[:], in1=L.lng_bc[:], op=ALU.mult), r=['zb', 'lng_bc'], w=['zb'])
    s.op('pool', lambda e: e.tensor_tensor(out=xo[:], in0=zb[:], in1=L.lnb_bc[:], op=ALU.add), r=['zb', 'lnb_bc'], w=[xo_nm])
    s.dma('sp', dst_rows, xo[:], r=[xo_nm], w=[])


def emit_ffn(g, li, src, dst):
    nc, s = g.nc, g.s
    TT = 256
    NT = S // TT
    NBT = TT // 128
    with ExitStack() as les:
        L = Ctx()
        L.es = les
        WT = Ctx()

        def pre():
            WT.wup = _sb(nc, les, "wup", [128, 8, 2 * DFF], BF16)
            WT.wdn = _sb(nc, les, "wdn", [128, NFC, D], BF16)
            wupd = g.d_f_w_up[li].rearrange("(kc p) n -> p kc n", p=128)
            for kc in range(8):
                for hf in range(2):
                    s.dma('pool', WT.wup[:, kc, hf * DFF:(hf + 1) * DFF], wupd[:, kc, hf * DFF:(hf + 1) * DFF], w=['wup'])
            wdnd = g.d_f_w_down[li].rearrange("(fc p) n -> p fc n", p=128)
            for fc in range(NFC):
                s.dma('pool', WT.wdn[:, fc, :], wdnd[:, fc, :], w=['wdn'])
        emit_mods(g, L, g.d_f_mod_w[li], g.d_f_mod_b[li], g.d_f_ln_g[li], g.d_f_ln_b[li], pre=pre)
        wup, wdn = WT.wup, WT.wdn
        cw = _sb(nc, les, "cw", [128, 2 * NFC, 4], F32)
        halo = _sb(nc, les, "halo", [128, 2 * NFC, 2], F32)
        hT = [_sb(nc, les, "hT%d" % i, [128, 8, TT], BF16) for i in range(2)]
        gTs = [_sb(nc, les, "gT%d" % i, [128, NFC, TT], BF16) for i in range(2)]
        upre = [_sb(nc, les, "upre%d" % i, [128, TT + 2], F32) for i in range(2)]
        c0 = [_sb(nc, les, "c0%d" % i, [128, TT], F32) for i in range(2)]
        asil = _sb(nc, les, "asil", [128, TT], F32)
        xin = [_sb(nc, les, "xin%d" % i, [128, D], F32) for i in range(2)]
        xrs = [_sb(nc, les, "xrs%d" % i, [128, D], F32) for i in range(2)]
        zb = _sb(nc, les, "zb", [128, D], F32)
        xo = [_sb(nc, les, "xo%d" % i, [128, D], F32) for i in range(2)]
        P = g.ps
        s.dma('sp', cw[:], g.d_f_cw[li], w=['cw'])
        s.op('dve', lambda e: e.memset(halo[:], 0.0), w=['halo%d' % i for i in range(2 * NFC)])
        def phA(t):
            t0 = t * TT
            hs = t % 2
            hnm = 'hT%d' % hs
            for bi in range(NBT):
                xs_ = (t * NBT + bi) % 2
                s.dma('sp', xin[xs_][:], src[t0 + bi * 128:t0 + (bi + 1) * 128, :], w=['xin%d' % xs_])
                emit_hT(g, L, xin[xs_], 'xin%d' % xs_, lambda kc, bi=bi, hs=hs: hT[hs][:, kc, bi * 128:(bi + 1) * 128], hnm, (0, 1))
                yield

        def phU(t):
            hs = t % 2
            hnm = 'hT%d' % hs
            gT = gTs[t % 2]
            gnm = 'gT%d' % (t % 2)
            for j in range(NFC):
                for half in range(2):
                    fc = j + half * NFC
                    b = 2 + ((2 * j + half) % 4)
                    pb = 'ps%d' % b
                    up = upre[half]
                    unm = 'upre%d' % half
                    for kc in range(8):
                        s.op('pe', lambda e, kc=kc, fc=fc, b=b: e.matmul(P[:, b * 512:b * 512 + TT], lhsT=wup[:, kc, fc * 128:(fc + 1) * 128],
                                                                      rhs=hT[hs][:, kc, :], start=(kc == 0), stop=(kc == 7)),
                             r=['wup', hnm], w=[pb], inc=(kc == 7))
                    s.op('pool', lambda e, fc=fc, up=up: e.tensor_copy(out=up[:, 0:2], in_=halo[:, fc, :]), r=['halo%d' % fc], w=[unm])
                    s.op('act', lambda e, b=b, up=up: e.activation(out=up[:, 2:TT + 2], in_=P[:, b * 512:b * 512 + TT], func=AF.Copy),
                         r=[pb], w=[unm])
                    s.op('act', lambda e, b=b, fc=fc, half=half: e.activation(out=c0[half][:], in_=P[:, b * 512:b * 512 + TT], func=AF.Identity,
                                                                             scale=cw[:, fc, 2:3], bias=cw[:, fc, 3:4]),
                         r=[pb, 'cw'], w=['c0%d' % half])
                    s.op('pool', lambda e, fc=fc, up=up: e.tensor_copy(out=halo[:, fc, :], in_=up[:, TT:TT + 2]), r=[unm], w=['halo%d' % fc])
                    s.op('dve', lambda e, fc=fc, up=up, half=half: e.scalar_tensor_tensor(out=c0[half][:], in0=up[:, 1:TT + 1], scalar=cw[:, fc, 1:2],
                                                                                        in1=c0[half][:], op0=ALU.mult, op1=ALU.add),
                         r=[unm, 'cw', 'c0%d' % half], w=['c0%d' % half])
                    s.op('dve', lambda e, fc=fc, up=up, half=half: e.scalar_tensor_tensor(out=c0[half][:], in0=up[:, 0:TT], scalar=cw[:, fc, 0:1],
                                                                                        in1=c0[half][:], op0=ALU.mult, op1=ALU.add),
                         r=[unm, 'cw', 'c0%d' % half], w=['c0%d' % half])
                    if half == 0:
                        s.op('act', lambda e: e.activation(out=asil[:], in_=c0[0][:], func=AF.Silu), r=['c00'], w=['asil'])
                    else:
                        s.op('dve', lambda e, j=j, gT=gT: e.tensor_tensor(out=gT[:, j, :], in0=asil[:], in1=c0[1][:], op=ALU.mult),
                             r=['asil', 'c01'], w=[gnm])
                    yield

        def phD(t):
            t0 = t * TT
            gT = gTs[t % 2]
            gnm = 'gT%d' % (t % 2)
            for bi in range(NBT):
                r0 = t0 + bi * 128
                xs_ = (t * NBT + bi) % 2
                s.dma('sp', xrs[xs_][:], src[r0:r0 + 128, :], w=['xrs%d' % xs_])
                for half in range(2):
                    b = 6 + half
                    for j in range(NFC):
                        s.op('pe', lambda e, j=j, half=half, b=b, bi=bi, gT=gT: e.matmul(P[:, b * 512:(b + 1) * 512], lhsT=gT[:, j, bi * 128:(bi + 1) * 128],
                                                                                         rhs=wdn[:, j, half * 512:(half + 1) * 512],
                                                                                         start=(j == 0), stop=(j == NFC - 1)),
                             r=[gnm, 'wdn'], w=['ps%d' % b], inc=(j == NFC - 1))
                    yield
                emit_epilogue(g, L, (6, 7), xrs[xs_], 'xrs%d' % xs_, zb, xo[xs_], 'xo%d' % xs_, dst[r0:r0 + 128, :])
                yield

        for it in range(NT + 2):
            gens = []
            if 1 <= it <= NT:
                gens.append([phU(it - 1), 1])
            if it < NT:
                gens.append([phA(it), 1])
            if it >= 2:
                gens.append([phD(it - 2), 1])
            _interleave(gens)
        s.barrier()


NEG = -1.0e30
GUARD = 1.0e38
NBISECT = 24
TOPK_EXACT_NK = 1024
DSTOP = int(os.environ.get('DSA_STOP', '99'))
DSUB = int(os.environ.get('DSA_SUB', '99'))
GSTOP = int(os.environ.get('GDN_STOP', '99'))
DSC = int(os.environ.get('DSA_SC', '3'))
DSKIP = os.environ.get('DSA_SKIP', '').split(',')
REP = -3.0e38


def emit_dsa(g, src, dst):
    nc, s = g.nc, g.s
    P = g.ps
    with ExitStack() as les:
        L = Ctx()
        L.es = les
        emit_mods(g, L, g.d_e_mod_w[0], g.d_e_mod_b[0], g.d_e_ln_g[0], g.d_e_ln_b[0])
        A = lambda name, shape, dt: _sb(nc, les, name, shape, dt)
        win = A("win", [128, 8, 1536], BF16)
        wif = A("wif", [128, 8, 4], F32)
        wiw = A("wiw", [128, 8, 4], BF16)
        poolw = A("poolw", [128, 4, 128], BF16)
        pscale = A("pscale", [128, 4], F32)
        kvn_bc = A("kvn_bc", [128, 128], F32)
        ukT = A("ukT", [128, 4, 128], BF16)
        uvpad = A("uvpad", [128, 8, 128], BF16)
        wout = A("wout", [128, 8, D], BF16)
        negI4 = A("negI4", [128, 512], BF16)
        corr = A("corr", [128, 4, 15], F32)
        ckvn_all = A("ckvn_all", [128, NB, 128], BF16)
        ckvnT_all = A("ckvnT_all", [128, S], BF16)
        kiT_all = A("kiT_all", [128, S], BF16)
        xin = [A("xin%d" % i, [128, D], F32) for i in range(3)]
        hT = A("hT", [128, 8, 128], BF16)
        ut = A("ut", [128, 4, 143], F32)
        ta = A("ta", [128, 143], F32)
        tb = A("tb", [128, 143], F32)
        dT = A("dT", [128, 4, 128], BF16)
        qT = A("qT", [128, 512], BF16)
        qiT = [A("qiT%d" % i, [128, 256], BF16) for i in range(2)]
        wis = [A("wis%d" % i, [128, 4], F32) for i in range(2)]
        qlT = [A("qlT%d" % i, [128, 1024], BF16) for i in range(3)]
        W = [A("W%d" % i, [128, S + 8], F32) for i in range(2)]
        bs = A("bs", [128, 8], F32)
        cb = A("cb", [128, S], BF16)
        rbuf = [A("rbuf%d" % i, [128, 512], F32) for i in range(2)]
        rb2 = A("rb2", [128, 512], F32)
        notm = [A("notm%d" % i, [128, S], BF16) for i in range(2)]
        m8 = A("m8", [128, 8], F32)
        pT = [A("pT%d" % i, [128, 512], BF16) for i in range(2)]
        rden = A("rden", [128, 512], F32)
        oTn = A("oTn", [128, 1024], BF16)
        yinT = [A("yinT%d" % i, [128, 1024], BF16) for i in range(3)]
        zb = A("zb", [128, D], F32)
        xo = [A("xo%d" % i, [128, D], F32) for i in range(2)]
        sq = A("sq", [128, 128], F32)
        ckf = A("ckf", [128, 128], F32)
        rs = A("rs", [128, 4], F32)
        Pb3 = P[:, 3 * 512:4 * 512].bitcast(BF16)

        wind = g.d_e_w_in[0].rearrange("(kc p) n -> p kc n", p=128)
        for kc in range(8):
            s.dma('pool', win[:, kc, 0:1472], wind[:, kc, 0:1472], w=['win'])
            s.dma('pool', win[:, kc, 1472:1536], wind[:, kc, 1408:1472], w=['win'])
        s.dma('sp', wif[:], wind[:, :, 1472:1476], w=['wif'])
        s.op('dve', lambda e: e.tensor_copy(out=wiw[:], in_=wif[:]), r=['wif'], w=['wiw'])
        s.dma('pool', poolw[:], g.d_e_pool_w[0].rearrange("g c d -> c g d"), w=['poolw'])
        s.dma('sp', pscale[:], g.d_pscale[:, :], w=['pscale'])
        s.dma('sp', kvn_bc[:], g.d_e_kv_norm[0].partition_broadcast(128), w=['kvn_bc'])
        s.dma('pool', ukT[:], g.d_ukT[:, :, :], w=['ukT'])
        s.dma('pool', uvpad[:], g.d_uvpad[:, :, :], w=['uvpad'])
        woutd = g.d_e_w_out[0].rearrange("(kc p) n -> p kc n", p=128)
        for kc in range(8):
            s.dma('pool', wout[:, kc, :], woutd[:, kc, :], w=['wout'])
        s.dma('pool', negI4[:], g.d_negI4[:, :], w=['negI4'])
        s.dma('sp', corr[:], g.d_corr[:, :, :], w=['corr'])
        s.op('dve', lambda e: e.memset(ut[:], 0.0), w=['ut'])

        def front_a(qb):
            sl = qb % 2
            s3 = qb % 3
            t0 = qb * 128
            nk = t0 + 128
            xn = 'xin%d' % s3
            s.dma('sp', xin[s3][:], src[t0:t0 + 128, :], w=[xn])
            emit_hT(g, L, xin[s3], xn, lambda kc: hT[:, kc, :], 'hT', (0, 1))
            yield
            def grp(out_ap, cols, bank, last=True):
                for kc in range(8):
                    s.op('pe', lambda e, kc=kc: e.matmul(out_ap, lhsT=win[:, kc, cols[0]:cols[1]], rhs=hT[:, kc, :],
                                                        start=(kc == 0), stop=(kc == 7)),
                         r=['win', 'hT'], w=['ps%d' % bank], inc=(kc == 7))
            for gi in range(4):
                grp(P[:, gi * 128:(gi + 1) * 128], (gi * 128, (gi + 1) * 128), 0)
                yield
            for j in range(4):
                grp(P[:, 512 + j * 128:512 + (j + 1) * 128], (512 + j * 128, 512 + (j + 1) * 128), 1)
                yield
            for j in range(2):
                grp(P[:, 1024 + j * 128:1024 + (j + 1) * 128], (1152 + j * 128, 1152 + (j + 1) * 128), 2)
                yield
            grp(P[:, 1024 + 256:1024 + 384], (1408, 1536), 2)
            yield
            for kc in range(8):
                s.op('pe', lambda e, kc=kc: e.matmul(P[:, 1536:1536 + 128], lhsT=hT[:, kc, :], rhs=win[:, kc, 1024:1152],
                                                    start=(kc == 0), stop=(kc == 7)), r=['win', 'hT'], w=['ps3'], inc=(kc == 7))
            for kc in range(8):
                s.op('pe', lambda e, kc=kc: e.matmul(P[:, 1536 + 128:1536 + 132], lhsT=hT[:, kc, :], rhs=wiw[:, kc, :],
                                                    start=(kc == 0), stop=(kc == 7)), r=['wiw', 'hT'], w=['ps3'], inc=(kc == 7))
            yield
            s.op('act', lambda e: e.activation(out=ut[:, :, 15:143], in_=P[:, 0:512].rearrange("p (g t) -> p g t", g=4), func=AF.Copy),
                 r=['ps0'], w=['ut'])
            s.op('act', lambda e: e.activation(out=qT[:], in_=P[:, 512:1024], func=AF.Copy), r=['ps1'], w=['qT'])
            s.op('dve', lambda e: e.tensor_copy(out=qiT[sl][:], in_=P[:, 1024:1024 + 256]), r=['ps2'], w=['qiT%d' % sl])
            s.op('dve', lambda e: e.tensor_copy(out=kiT_all[:, t0:t0 + 128], in_=P[:, 1024 + 256:1024 + 384]), r=['ps2'], w=['kiT%d' % qb])
            s.op('dve', lambda e: e.tensor_copy(out=wis[sl][:], in_=P[:, 1536 + 128:1536 + 132]), r=['ps3'], w=['wis%d' % sl])
            yield
            s.op('act', lambda e: e.activation(out=ckf[:], in_=P[:, 1536:1536 + 128], func=AF.Copy), r=['ps3'], w=['ckf'])
            s.op('dve', lambda e: e.tensor_tensor(out=sq[:], in0=ckf[:], in1=ckf[:], op=ALU.mult), r=['ckf'], w=['sq'])
            s.op('dve', lambda e: e.reduce_sum(out=rs[:, 0:1], in_=sq[:], axis=AX.X), r=['sq'], w=['rs0'])
            s.op('dve', lambda e: e.tensor_scalar(out=rs[:, 1:2], in0=rs[:, 0:1], scalar1=1.0 / 128, scalar2=RMS_EPS, op0=ALU.mult, op1=ALU.add),
                 r=['rs0'], w=['rs1'])
            s.op('pool', lambda e: e.tensor_tensor(out=rs[:, 3:4], in0=rs[:, 1:2], in1=g.mhalf[:, 0:1], op=ALU.pow), r=['rs1', 'mhalf'], w=['rs3'])
            s.op('dve', lambda e: e.scalar_tensor_tensor(out=ckf[:], in0=ckf[:], scalar=rs[:, 3:4], in1=kvn_bc[:],
                                                         op0=ALU.mult, op1=ALU.mult), r=['ckf', 'rs3', 'kvn_bc'], w=['ckf'])
            s.op('act', lambda e: e.activation(out=ckvn_all[:, qb, :], in_=ckf[:], func=AF.Copy), r=['ckf'], w=['ckvn%d' % qb])
            s.op('pe', lambda e: e.transpose(P[:, 1536 + 256:1536 + 384], ckf[:], g.ident[:]), r=['ckf', 'ident'], w=['ps3'])
            s.op('act', lambda e: e.activation(out=ckvnT_all[:, t0:t0 + 128], in_=P[:, 1536 + 256:1536 + 384], func=AF.Copy), r=['ps3'], w=['ckvnT%d' % qb])
            yield
            for gi in range(4):
                win_ = 2 << gi
                U = ut[:, gi, :]
                s.op('dve', lambda e, U=U: e.tensor_tensor(out=ta[:, 1:143], in0=U[:, 1:143], in1=U[:, 0:142], op=ALU.add), r=['ut'], w=['ta'])
                sw = ta
                swn = 'ta'
                if gi >= 1:
                    s.op('dve', lambda e: e.tensor_tensor(out=tb[:, 3:143], in0=ta[:, 3:143], in1=ta[:, 1:141], op=ALU.add), r=['ta'], w=['tb'])
                    sw, swn = tb, 'tb'
                if gi >= 2:
                    s.op('dve', lambda e: e.tensor_tensor(out=ta[:, 7:143], in0=tb[:, 7:143], in1=tb[:, 3:139], op=ALU.add), r=['tb'], w=['ta'])
                    sw, swn = ta, 'ta'
                if gi >= 3:
                    s.op('dve', lambda e: e.tensor_tensor(out=tb[:, 15:143], in0=ta[:, 15:143], in1=ta[:, 7:135], op=ALU.add), r=['ta'], w=['tb'])
                    sw, swn = tb, 'tb'
                if qb == 0:
                    s.op('dve', lambda e, sw=sw, gi=gi, win_=win_: e.tensor_tensor(out=sw[:, 15:15 + win_ - 1], in0=sw[:, 15:15 + win_ - 1],
                                                                                 in1=corr[:, gi, 0:win_ - 1], op=ALU.mult),
                         r=[swn, 'corr'], w=[swn])
                s.op('dve', lambda e, sw=sw, gi=gi, win_=win_, U=U: e.scalar_tensor_tensor(out=dT[:, gi, :], in0=sw[:, 15:143], scalar=1.0 / win_,
                                                                                          in1=U[:, 15:143], op0=ALU.mult, op1=ALU.subtract),
                     r=[swn, 'ut'], w=['dT'])
                yield
            s.op('pool', lambda e: e.tensor_copy(out=ut[:, :, 0:15], in_=ut[:, :, 128:143]), r=['ut'], w=['ut'])
            for gi in range(4):
                s.op('pe', lambda e, gi=gi: e.matmul(P[:, gi * 128:(gi + 1) * 128], lhsT=poolw[:, gi, :], rhs=dT[:, gi, :], start=True, stop=True),
                     r=['poolw', 'dT'], w=['ps0'])
            for gi in range(4):
                s.op('act', lambda e, gi=gi: e.activation(out=yinT[s3][:, gi * 128:(gi + 1) * 128], in_=P[:, gi * 128:(gi + 1) * 128],
                                                          func=AF.Identity, scale=pscale[:, gi:gi + 1]),
                     r=['ps0', 'pscale'], w=['yinT%d' % s3])
            yield
            for h in range(8):
                po = (h % 2) * 64
                bank = 1 + h % 2
                off = bank * 512 + (h // 2) * 128
                s.op('pe', lambda e, h=h, po=po, off=off: e.matmul(P[:, off:off + 128], lhsT=ukT[po:po + 64, h // 2, :],
                                                                  rhs=qT[po:po + 64, (h // 2) * 128:(h // 2 + 1) * 128], start=True, stop=True),
                     r=['ukT', 'qT'], w=['ps%d' % bank])
            for j in range(2):
                s.op('act', lambda e, j=j: e.activation(out=qlT[s3][:, j * 512:(j + 1) * 512], in_=P[:, (1 + j) * 512:(2 + j) * 512], func=AF.Copy),
                     r=['ps%d' % (1 + j)], w=['qlT%d' % s3])
            yield
            Wn = 'W%d' % sl
            cnt = 0
            chunks = []
            k0_ = 0
            while k0_ < nk:
                rem = nk - k0_
                w0 = 512 if rem >= 512 else (256 if rem >= 256 else 128)
                chunks.append((k0_, w0))
                k0_ += w0
            for (k0, w_) in chunks:
                for h in range(4):
                    po = (h % 2) * 64
                    bank = 2 + cnt % 2
                    rb = rbuf[cnt % 2]
                    rbn = 'rbuf%d' % (cnt % 2)
                    cnt += 1
                    s.op('pe', lambda e, h=h, po=po, bank=bank, k0=k0, w_=w_: e.matmul(P[:, bank * 512:bank * 512 + w_],
                                                                                     lhsT=qiT[sl][po:po + 64, (h // 2) * 128:(h // 2 + 1) * 128],
                                                                                     rhs=kiT_all[po:po + 64, k0:k0 + w_], start=True, stop=True),
                         r=['qiT%d' % sl] + ['kiT%d' % kb_ for kb_ in range(k0 // 128, (k0 + w_) // 128)], w=['ps%d' % bank])
                    s.op('act', lambda e, bank=bank, rb=rb, w_=w_: e.activation(out=rb[:, 0:w_], in_=P[:, bank * 512:bank * 512 + w_], func=AF.Relu),
                         r=['ps%d' % bank], w=[rbn])
                    if h == 0:
                        s.op('dve', lambda e, rb=rb, k0=k0, w_=w_: e.tensor_scalar(out=W[sl][:, k0:k0 + w_], in0=rb[:, 0:w_], scalar1=wis[sl][:, 0:1],
                                                                                 scalar2=None, op0=ALU.mult), r=[rbn, 'wis%d' % sl], w=[Wn])
                    else:
                        s.op('dve', lambda e, rb=rb, k0=k0, w_=w_, h=h: e.scalar_tensor_tensor(out=W[sl][:, k0:k0 + w_], in0=rb[:, 0:w_], scalar=wis[sl][:, h:h + 1],
                                                                                            in1=W[sl][:, k0:k0 + w_], op0=ALU.mult, op1=ALU.add),
                             r=[rbn, 'wis%d' % sl, Wn], w=[Wn])
                    yield

        def topk(qb):
            sl = qb % 2
            nk = qb * 128 + 128
            Wn = 'W%d' % sl
            nmn = 'notm%d' % sl
            Wt = W[sl]
            if nk <= 256:
                s.op('dve', lambda e: e.memset(notm[sl][:, 0:nk], 0.0), w=[nmn])
            elif nk <= TOPK_EXACT_NK:
                s.op('dve', lambda e: e.memset(Wt[0:64, nk - 64:nk], NEG), r=[Wn], w=[Wn])
                for it in range(32):
                    s.op('dve', lambda e: e.max(out=m8[:], in_=Wt[:, 0:nk]), r=[Wn], w=['m8'])
                    s.op('dve', lambda e: e.match_replace(out=Wt[:, 0:nk], in_to_replace=m8[:], in_values=Wt[:, 0:nk], imm_value=REP),
                         r=[Wn, 'm8'], w=[Wn])
                    yield
                s.op('dve', lambda e: e.tensor_scalar(out=notm[sl][:, 0:nk], in0=Wt[:, 0:nk], scalar1=0.5 * REP, scalar2=None, op0=ALU.is_gt),
                     r=[Wn], w=[nmn])
            else:
                s.op('dve', lambda e: e.tensor_reduce(out=bs[:, 7:8], in_=Wt[:, 0:nk], axis=AX.X, op=ALU.max, apply_absolute_value=True),
                     r=[Wn], w=['bs7'])
                s.op('dve', lambda e: e.memset(Wt[0:64, nk - 64:nk], NEG), r=[Wn], w=[Wn])
                s.op('dve', lambda e: e.tensor_scalar(out=bs[:, 0:1], in0=bs[:, 7:8], scalar1=-1.001, scalar2=None, op0=ALU.mult), r=['bs7'], w=['bs0'])
                s.op('dve', lambda e: e.tensor_scalar(out=bs[:, 1:2], in0=bs[:, 7:8], scalar1=1.0005, scalar2=None, op0=ALU.mult), r=['bs7'], w=['bs1'])
                s.op('dve', lambda e: e.tensor_scalar(out=bs[:, 2:3], in0=bs[:, 0:1], scalar1=bs[:, 1:2], scalar2=-1.0, op0=ALU.add, op1=ALU.mult),
                     r=['bs0', 'bs1'], w=['bs2'])
                yield
                for it in range(NBISECT):
                    s.op('act', lambda e: e.activation(out=notm[sl][:, 0:nk], in_=Wt[:, 0:nk], func=AF.Sign, bias=bs[:, 2:3], scale=1.0, accum_out=bs[:, 3:4]),
                         r=[Wn, 'bs2'], w=[nmn, 'bs3'])
                    s.op('dve', lambda e: e.tensor_scalar(out=bs[:, 4:5], in0=bs[:, 3:4], scalar1=float(512 - nk), scalar2=None, op0=ALU.is_ge), r=['bs3'], w=['bs4'])
                    s.op('dve', lambda e: e.scalar_tensor_tensor(out=bs[:, 0:1], in0=bs[:, 4:5], scalar=bs[:, 1:2], in1=bs[:, 0:1], op0=ALU.mult, op1=ALU.add),
                         r=['bs4', 'bs1', 'bs0'], w=['bs0'])
                    s.op('dve', lambda e: e.tensor_scalar(out=bs[:, 1:2], in0=bs[:, 1:2], scalar1=0.5, scalar2=None, op0=ALU.mult), r=['bs1', 'bs0'], w=['bs1'])
                    s.op('dve', lambda e: e.tensor_scalar(out=bs[:, 2:3], in0=bs[:, 0:1], scalar1=bs[:, 1:2], scalar2=-1.0, op0=ALU.add, op1=ALU.mult),
                         r=['bs0', 'bs1'], w=['bs2'])
                    yield
                s.op('dve', lambda e: e.scalar_tensor_tensor(out=bs[:, 7:8], in0=bs[:, 1:2], scalar=2.0, in1=bs[:, 0:1], op0=ALU.mult, op1=ALU.add),
                     r=['bs1', 'bs0'], w=['bs7'])
                s.op('dve', lambda e: e.tensor_scalar(out=bs[:, 6:7], in0=bs[:, 7:8], scalar1=-1.0, scalar2=None, op0=ALU.mult), r=['bs7'], w=['bs6'])
                s.op('act', lambda e: e.activation(out=notm[sl][:, 0:nk], in_=Wt[:, 0:nk], func=AF.Sign, bias=bs[:, 6:7], scale=1.0, accum_out=bs[:, 3:4]),
                     r=[Wn, 'bs6'], w=[nmn, 'bs3'])
                s.op('dve', lambda e: e.tensor_scalar(out=bs[:, 5:6], in0=bs[:, 3:4], scalar1=-0.5, scalar2=float(256 - nk // 2), op0=ALU.mult, op1=ALU.add),
                     r=['bs3'], w=['bs5'])
                yield
                s.op('dve', lambda e: e.tensor_scalar(out=notm[sl][:, 0:nk], in0=Wt[:, 0:nk], scalar1=bs[:, 7:8], scalar2=None, op0=ALU.is_gt),
                     r=[Wn, 'bs7'], w=[nmn])
                s.op('dve', lambda e: e.tensor_scalar(out=cb[:, 0:nk], in0=Wt[:, 0:nk], scalar1=bs[:, 0:1], scalar2=None, op0=ALU.is_gt),
                     r=[Wn, 'bs0'], w=['cb'])
                yield
                s.op('dve', lambda e: e.scalar_tensor_tensor(out=cb[:, 0:nk], in0=Wt[:, 0:nk], scalar=bs[:, 7:8], in1=cb[:, 0:nk], op0=ALU.is_le, op1=ALU.mult),
                     r=[Wn, 'bs7', 'cb'], w=['cb'])
                yield
                s.op('dve', lambda e: e.tensor_tensor_scan(out=Wt[:, 0:nk], data0=g.onesb[:, 0:1].to_broadcast([128, nk]), data1=cb[:, 0:nk], initial=0.0, op0=ALU.mult, op1=ALU.add),
                     r=['onesb', 'cb', Wn], w=[Wn])
                yield
                s.op('dve', lambda e: e.scalar_tensor_tensor(out=cb[:, 0:nk], in0=Wt[:, 0:nk], scalar=bs[:, 5:6], in1=cb[:, 0:nk], op0=ALU.is_le, op1=ALU.mult),
                     r=[Wn, 'bs5', 'cb'], w=['cb'])
                yield
                s.op('dve', lambda e: e.tensor_tensor(out=notm[sl][:, 0:nk], in0=notm[sl][:, 0:nk], in1=cb[:, 0:nk], op=ALU.add), r=[nmn, 'cb'], w=[nmn])
                s.op('dve', lambda e: e.tensor_scalar(out=notm[sl][:, 0:nk], in0=notm[sl][:, 0:nk], scalar1=-1.0, scalar2=1.0, op0=ALU.mult, op1=ALU.add),
                     r=[nmn], w=[nmn])
            s.op('dve', lambda e: e.memset(notm[sl][0:64, nk - 64:nk], 1.0), r=[nmn], w=[nmn])
            yield

        def back(qb):
            sl = qb % 2
            s3 = qb % 3
            t0 = qb * 128
            cnt = 0
            for hg in range(2):
                for kb in range(qb + 1):
                    j = cnt % 2
                    cnt += 1
                    bank = 4 + j
                    s.op('pe', lambda e, bank=bank, kb=kb, hg=hg: e.matmul(P[:, bank * 512:(bank + 1) * 512], lhsT=ckvnT_all[:, kb * 128:(kb + 1) * 128],
                                                                          rhs=qlT[s3][:, hg * 512:(hg + 1) * 512], start=True, stop=False),
                         r=['ckvnT%d' % kb, 'qlT%d' % s3], w=['ps%d' % bank], inc=False)
                    s.op('pe', lambda e, bank=bank, kb=kb: e.matmul(P[:, bank * 512:(bank + 1) * 512], lhsT=notm[sl][:, kb * 128:(kb + 1) * 128],
                                                                   rhs=negI4[:], start=False, stop=True),
                         r=['notm%d' % sl, 'negI4'], w=['ps%d' % bank])
                    s.op('act', lambda e, bank=bank, j=j: e.activation(out=pT[j][:], in_=P[:, bank * 512:(bank + 1) * 512], func=AF.Exp, scale=0.125),
                         r=['ps%d' % bank], w=['pT%d' % j])
                    s.op('pe', lambda e, kb=kb, j=j: e.matmul(P[:, 6 * 512:7 * 512], lhsT=ckvn_all[:, kb, :], rhs=pT[j][:], start=(kb == 0), stop=(kb == qb)),
                         r=['ckvn%d' % kb, 'pT%d' % j], w=['ps6'], inc=False)
                    s.op('pe', lambda e, kb=kb, j=j: e.matmul(P[:, 7 * 512:8 * 512], lhsT=g.onesb[:], rhs=pT[j][:], start=(kb == 0), stop=(kb == qb)),
                         r=['onesb', 'pT%d' % j], w=['ps7'])
                    yield
                s.op('dve', lambda e: e.reciprocal(out=rden[:], in_=P[:, 7 * 512:8 * 512]), r=['ps7'], w=['rden'])
                s.op('dve', lambda e, hg=hg: e.tensor_tensor(out=oTn[:, hg * 512:(hg + 1) * 512], in0=P[:, 6 * 512:7 * 512], in1=rden[:], op=ALU.mult),
                     r=['ps6', 'rden'], w=['oTn'])
                yield
            for hp in range(4):
                for h2 in range(2):
                    h = 2 * hp + h2
                    s.op('pe', lambda e, hp=hp, h=h, h2=h2: e.matmul(P[:, 7 * 512 + hp * 128:7 * 512 + (hp + 1) * 128], lhsT=uvpad[:, h, :],
                                                                    rhs=oTn[:, (h // 2 + 4 * (h % 2)) * 128:(h // 2 + 4 * (h % 2) + 1) * 128], start=(h2 == 0), stop=(h2 == 1)),
                         r=['uvpad', 'oTn'], w=['ps7'], inc=(h2 == 1))
            s.op('act', lambda e: e.activation(out=yinT[s3][:, 512:1024], in_=P[:, 7 * 512:8 * 512], func=AF.Copy), r=['ps7'], w=['yinT%d' % s3])
            yield
            for half in range(2):
                b = 4 + half
                for kc in range(8):
                    s.op('pe', lambda e, kc=kc, half=half, b=b: e.matmul(P[:, b * 512:(b + 1) * 512], lhsT=yinT[s3][:, kc * 128:(kc + 1) * 128],
                                                                        rhs=wout[:, kc, half * 512:(half + 1) * 512], start=(kc == 0), stop=(kc == 7)),
                         r=['yinT%d' % s3, 'wout'], w=['ps%d' % b], inc=(kc == 7))
            yield
            emit_epilogue(g, L, (4, 5), xin[s3], 'xin%d' % s3, zb, xo[sl], 'xo%d' % sl, dst[t0:t0 + 128, :])
            yield

        nblk = g.nblk_dsa
        for it in range(nblk + 2):
            gens = []
            if it < nblk:
                gens.append([front_a(it), 1])
            if 1 <= it <= nblk:
                gens.append([topk(it - 1), 1])
            if it >= 2:
                gens.append([back(it - 2), 1])
            _interleave(gens)
        s.barrier()


def emit_gdn(g, src, dst):
    nc, s = g.nc, g.s
    P = g.ps
    with ExitStack() as les:
        L = Ctx()
        L.es = les
        A = lambda name, shape, dt: _sb(nc, les, name, shape, dt)
        WT = Ctx()

        def pre():
            WT.win = A("gwin", [128, 8, 4096], BF16)
            WT.wout = A("gwout", [128, 8, D], BF16)
            wind_ = g.d_o_w_in[0].rearrange("(kc p) n -> p kc n", p=128)
            for kc in range(8):
                for q4 in range(2):
                    s.dma('pool', WT.win[:, kc, q4 * 2048:(q4 + 1) * 2048], wind_[:, kc, q4 * 2048:(q4 + 1) * 2048], w=['gwin'])
            woutd_ = g.d_o_w_out[0].rearrange("(kc p) n -> p kc n", p=128)
            for kc in range(8):
                s.dma('pool', WT.wout[:, kc, :], woutd_[:, kc, :], w=['gwout'])
        emit_mods(g, L, g.d_o_mod_w[0], g.d_o_mod_b[0], g.d_o_ln_g[0], g.d_o_ln_b[0], pre=pre)
        win = WT.win
        wbf = A("gwbf", [128, 8, 16], F32)
        wbb = A("gwbb", [128, 8, 16], BF16)
        wout = WT.wout
        cw = A("gcw", [128, 24, 4], F32)
        msl = A("msl", [128, 128], F32)
        mil = A("mil", [128, 128], F32)
        triu = A("triu", [128, 128], F32)
        mdm = A("mdm", [128, 5, 128], BF16)
        mdmT = A("mdmT", [128, 5, 128], BF16)
        alog = A("alog", [128, 8], F32)
        dtb = A("dtb", [128, 8], F32)
        onb = A("onb", [128, D], F32)
        xin = [A("gxin%d" % i, [128, D], F32) for i in range(1)]
        xrs = A("gxrs", [128, D], F32)
        hT = A("ghT", [128, 8, 128], BF16)
        xw = [A("xw%d" % i, [128, 131], F32) for i in range(2)]
        halo = A("ghalo", [128, 24, 3], F32)
        cbuf = [A("cbuf%d" % i, [128, 128], F32) for i in range(2)]
        act1 = A("gact", [128, 24 * 128], F32)
        vb = [A("gvb%d" % i, [128, 1024], BF16) for i in range(2)]
        sq = A("gsq", [128, 1024], F32)
        rstd = sq
        qkb = [A("qkb%d" % i, [128, 2048], BF16) for i in range(2)]
        gsil = [A("gsil%d" % i, [128, D], BF16) for i in range(3)]
        ba = A("ba", [128, 16], F32)
        tmpa = A("tmpa", [128, 8], F32)
        smb = [A("smb%d" % i, [128, 48], F32) for i in range(2)]
        zsm = A("zsm", [128, 8], F32)
        HP = []
        NTH = 3
        for p in range(NTH):
            h_ = Ctx()
            h_.sm = A("hsm%d" % p, [128, 2], F32)
            for nm in ("E", "EB", "t1", "Nf", "atf"):
                setattr(h_, nm, A("h%s%d" % (nm, p), [128, 128], F32))
            for nm in ("X", "Xt", "Q2", "Q2t", "Q4", "Q4t", "Yv", "Yt", "attT", "qdT", "kd", "Kbg", "Vb", "nwT", "vnew"):
                setattr(h_, nm, A("h%s%d" % (nm, p), [128, 128], BF16))
            h_.NM = A("hNM%d" % p, [128, 5, 128], BF16)
            h_.NMt = A("hNMt%d" % p, [128, 5, 128], BF16)
            HP.append(h_)
        Sf = A("gSf", [128, 8, 128], F32)
        Sb = A("gSb", [128, 8, 128], BF16)
        osb = [A("gosb%d" % i, [128, D], F32) for i in range(2)]
        yinT = A("gyinT", [128, 1024], BF16)
        zb = A("gzb", [128, D], F32)
        xo = [A("gxo%d" % i, [128, D], F32) for i in range(1)]
        wind = g.d_o_w_in[0].rearrange("(kc p) n -> p kc n", p=128)
        s.dma('sp', wbf[:], wind[:, :, 4096:4112], w=['gwbf'])
        s.op('dve', lambda e: e.tensor_copy(out=wbb[:], in_=wbf[:]), r=['gwbf'], w=['gwbb'])
        s.dma('sp', cw[:], g.d_o_cw[:, :, :], w=['gcw'])
        s.dma('sp', msl[:], g.d_msl[:, :], w=['msl'])
        s.dma('sp', mil[:], g.d_mil[:, :], w=['mil'])
        s.dma('sp', triu[:], g.d_triu[:, :], w=['triu'])
        s.dma('pool', mdm[:], g.d_mdm[:, :, :], w=['mdm'])
        s.dma('pool', mdmT[:], g.d_mdmT[:, :, :], w=['mdmT'])
        s.dma('sp', alog[:], g.d_o_a_log[0].partition_broadcast(128), w=['alog'])
        s.dma('sp', dtb[:], g.d_o_dt_bias[0].partition_broadcast(128), w=['dtb'])
        s.dma('sp', onb[:], g.d_onb[0].partition_broadcast(128), w=['onb'])
        s.op('act', lambda e: e.activation(out=alog[:], in_=alog[:], func=AF.Exp), r=['alog'], w=['alog'])
        s.op('dve', lambda e: e.tensor_scalar(out=alog[:], in0=alog[:], scalar1=-1.0, scalar2=None, op0=ALU.mult), r=['alog'], w=['alog'])
        s.op('dve', lambda e: e.memset(halo[:], 0.0), w=['ghalo%d' % i for i in range(24)])
        s.op('dve', lambda e: e.memset(Sf[:], 0.0), w=['gSf0', 'gSf1', 'gSf2'])
        s.op('dve', lambda e: e.memset(Sb[:], 0.0), w=['gSb0', 'gSb1', 'gSb2'])
        DK = float(128 ** -0.5)

        def phaseA(qb):
            t0 = qb * 128
            bp = qb % 2
            x3 = qb % 3
            xn = 'gxin0'
            an = 'gact'
            sn = 'smb%d' % bp
            sm = smb[bp]
            ac = act1
            s.dma('sp', xin[0][:], src[t0:t0 + 128, :], w=[xn])
            emit_hT(g, L, xin[0], xn, lambda kc: hT[:, kc, :], 'ghT', (0, 1))
            yield
            for ch in range(24):
                b = ch % 2
                cb = cbuf[b]
                cn = 'cbuf%d' % b
                for kc in range(8):
                    s.op('pe', lambda e, kc=kc, ch=ch, b=b: e.matmul(P[:, b * 512:b * 512 + 128], lhsT=win[:, kc, ch * 128:(ch + 1) * 128], rhs=hT[:, kc, :],
                                                                    start=(kc == 0), stop=(kc == 7)), r=['gwin', 'ghT'], w=['ps%d' % b], inc=(kc == 7))
                xb = xw[b]
                xbn = 'xw%d' % b
                s.op('pool', lambda e, ch=ch, xb=xb: e.tensor_copy(out=xb[:, 0:3], in_=halo[:, ch, :]), r=['ghalo%d' % ch], w=[xbn])
                s.op('act', lambda e, b=b, xb=xb: e.activation(out=xb[:, 3:131], in_=P[:, b * 512:b * 512 + 128], func=AF.Copy), r=['ps%d' % b], w=[xbn])
                s.op('act', lambda e, ch=ch, b=b, cb=cb: e.activation(out=cb[:], in_=P[:, b * 512:b * 512 + 128], func=AF.Identity, scale=cw[:, ch, 3:4]),
                     r=['ps%d' % b, 'gcw'], w=[cn])
                s.op('pool', lambda e, ch=ch, xb=xb: e.tensor_copy(out=halo[:, ch, :], in_=xb[:, 128:131]), r=[xbn], w=['ghalo%d' % ch])
                for j in range(3):
                    dst_ = cb[:] if j < 2 else ac[:, ch * 128:(ch + 1) * 128]
                    s.op('dve', lambda e, ch=ch, j=j, cb=cb, xb=xb, dst_=dst_: e.scalar_tensor_tensor(out=dst_, in0=xb[:, j:j + 128], scalar=cw[:, ch, j:j + 1], in1=cb[:],
                                                                                                 op0=ALU.mult, op1=ALU.add),
                         r=[xbn, 'gcw', cn], w=([cn] if j < 2 else [an]))
                yield
            s.op('act', lambda e: e.activation(out=ac[:], in_=ac[:], func=AF.Silu), r=[an], w=[an])
            yield
            s.op('act', lambda e: e.activation(out=vb[bp][:], in_=ac[:, 2048:3072], func=AF.Copy), r=[an], w=['gvb%d' % bp])
            yield
            for hf in range(2):
                seg = ac[:, hf * 1024:(hf + 1) * 1024]
                s.op('dve', lambda e, seg=seg: e.tensor_tensor(out=sq[:], in0=seg, in1=seg, op=ALU.mult), r=[an], w=['gsq'])
                yield
                for j in range(2):
                    b = j % 2
                    s.op('pe', lambda e, j=j, b=b: e.matmul(P[:, b * 512:(b + 1) * 512], lhsT=g.ones[:], rhs=sq[:, j * 512:(j + 1) * 512], start=True, stop=True),
                         r=['ones', 'gsq'], w=['ps%d' % b])
                for j in range(2):
                    b = j % 2
                    s.op('dve', lambda e, j=j, b=b: e.tensor_scalar(out=sq[:, j * 512:(j + 1) * 512], in0=P[:, b * 512:(b + 1) * 512], scalar1=RMS_EPS, scalar2=None, op0=ALU.add),
                         r=['ps%d' % b], w=['gsq'])
                yield
                s.op('act', lambda e: e.activation(out=sq[:], in_=sq[:], func=AF.Sqrt), r=['gsq'], w=['gsq'])
                s.op('dve', lambda e: e.reciprocal(out=sq[:], in_=sq[:]), r=['gsq'], w=['gsq'])
                yield
                s.op('dve', lambda e, seg=seg: e.tensor_tensor(out=seg, in0=seg, in1=sq[:], op=ALU.mult), r=[an, 'gsq'], w=[an])
                s.op('act', lambda e, seg=seg, hf=hf: e.activation(out=qkb[bp][:, hf * 1024:(hf + 1) * 1024], in_=seg, func=AF.Copy), r=[an], w=['qkb%d' % bp])
                yield
            for half in range(2):
                b = half
                for kc in range(8):
                    s.op('pe', lambda e, kc=kc, half=half, b=b: e.matmul(P[:, b * 512:(b + 1) * 512], lhsT=hT[:, kc, :],
                                                                        rhs=win[:, kc, 3072 + half * 512:3072 + (half + 1) * 512],
                                                                        start=(kc == 0), stop=(kc == 7)), r=['gwin', 'ghT'], w=['ps%d' % b], inc=(kc == 7))
                s.op('act', lambda e, half=half, b=b: e.activation(out=gsil[x3][:, half * 512:(half + 1) * 512], in_=P[:, b * 512:(b + 1) * 512], func=AF.Silu),
                     r=['ps%d' % b], w=['gsil%d' % x3])
                yield
            for kc in range(8):
                s.op('pe', lambda e, kc=kc: e.matmul(P[:, 0:16], lhsT=hT[:, kc, :], rhs=wbb[:, kc, :], start=(kc == 0), stop=(kc == 7)),
                     r=['gwbb', 'ghT'], w=['ps0'], inc=(kc == 7))
            s.op('dve', lambda e: e.tensor_copy(out=ba[:], in_=P[:, 0:16]), r=['ps0'], w=['ba'])
            yield
            s.op('act', lambda e: e.activation(out=sm[:, 0:8], in_=ba[:, 0:8], func=AF.Exp, scale=-1.0), r=['ba'], w=[sn])
            s.op('dve', lambda e: e.tensor_scalar(out=sm[:, 0:8], in0=sm[:, 0:8], scalar1=1.0, scalar2=None, op0=ALU.add), r=[sn], w=[sn])
            s.op('dve', lambda e: e.reciprocal(out=sm[:, 0:8], in_=sm[:, 0:8]), r=[sn], w=[sn])
            s.op('dve', lambda e: e.tensor_scalar(out=sm[:, 32:40], in0=sm[:, 0:8], scalar1=-1.0, scalar2=None, op0=ALU.mult), r=[sn], w=[sn])
            yield
            s.op('dve', lambda e: e.tensor_tensor(out=tmpa[:], in0=ba[:, 8:16], in1=dtb[:], op=ALU.add), r=['ba', 'dtb'], w=['tmpa'])
            s.op('act', lambda e: e.activation(out=tmpa[:], in_=tmpa[:], func=AF.Exp), r=['tmpa'], w=['tmpa'])
            s.op('act', lambda e: e.activation(out=tmpa[:], in_=tmpa[:], func=AF.Ln, bias=1.0), r=['tmpa'], w=['tmpa'])
            s.op('dve', lambda e: e.tensor_tensor(out=sm[:, 8:16], in0=tmpa[:], in1=alog[:], op=ALU.mult), r=['tmpa', 'alog', sn], w=[sn])
            yield
            s.op('pe', lambda e: e.matmul(P[:, 16:24], lhsT=triu[:], rhs=sm[:, 8:16], start=True, stop=True), r=['triu', sn], w=['ps0'])
            s.op('dve', lambda e: e.tensor_copy(out=sm[:, 16:24], in_=P[:, 16:24]), r=['ps0', sn], w=[sn])
            s.op('act', lambda e: e.activation(out=sm[:, 24:32], in_=sm[:, 16:24], func=AF.Exp), r=[sn], w=[sn])
            s.op('dve', lambda e: e.tensor_tensor(out=sm[:, 40:48], in0=sm[:, 24:32], in1=sm[:, 0:8], op=ALU.mult), r=[sn], w=[sn])
            yield

        def heads(qb, p):
            bp = qb % 2
            sm = smb[bp]
            sn = 'smb%d' % bp
            vn = 'gvb%d' % bp
            H = HP[p]
            X0 = (2 + 2 * p) * 512
            Y0 = X0 + 512
            xn_, yn_ = 'ps%d' % (2 + 2 * p), 'ps%d' % (3 + 2 * p)
            zn_ = xn_
            n = lambda nm: 'h%s%d' % (nm, p)
            for h in range(p, 8, NTH):
                vvb = vb[bp][:, h * 128:(h + 1) * 128]
                qnb = qkb[bp][:, h * 128:(h + 1) * 128]
                knb = qkb[bp][:, (8 + h) * 128:(9 + h) * 128]
                qbn = 'qkb%d' % bp
                s.op('dve', lambda e, h=h: e.tensor_scalar(out=H.t1[:], in0=g.ident[:], scalar1=sm[:, 16 + h:17 + h], scalar2=None, op0=ALU.mult),
                     r=['ident', sn], w=[n('t1')])
                s.op('pe', lambda e: e.matmul(P[:, X0:X0 + 128], lhsT=g.ones[:], rhs=H.t1[:], start=True, stop=True), r=['ones', n('t1')], w=[xn_])
                s.op('dve', lambda e, h=h: e.tensor_scalar(out=H.E[:], in0=P[:, X0:X0 + 128], scalar1=sm[:, 16 + h:17 + h], scalar2=0.0, op0=ALU.subtract, op1=ALU.max),
                     r=[xn_, sn], w=[n('E')])
                s.op('act', lambda e: e.activation(out=H.E[:], in_=H.E[:], func=AF.Exp, scale=-1.0), r=[n('E')], w=[n('E')])
                s.op('act', lambda e: e.activation(out=H.EB[:], in_=P[:, X0:X0 + 128], func=AF.Exp), r=[xn_], w=[n('EB')])
                s.op('act', lambda e: e.activation(out=H.sm[:, 0:1], in_=P[:, X0 + 127:X0 + 128], func=AF.Exp), r=[xn_], w=[n('sm0')])
                s.op('dve', lambda e, h=h: e.tensor_scalar(out=H.sm[:, 1:2], in0=P[:, X0 + 127:X0 + 128], scalar1=sm[:, 16 + h:17 + h], scalar2=None, op0=ALU.subtract),
                     r=[xn_, sn], w=[n('sm1')])
                s.op('act', lambda e: e.activation(out=H.sm[:, 1:2], in_=H.sm[:, 1:2], func=AF.Exp), r=[n('sm1')], w=[n('sm1')])
                yield
                s.op('pe', lambda e, knb=knb: e.matmul(P[:, X0 + 128:X0 + 256], lhsT=knb, rhs=knb, start=True, stop=True), r=[qbn], w=[xn_])
                s.op('pe', lambda e, knb=knb, qnb=qnb: e.matmul(P[:, X0 + 256:X0 + 384], lhsT=qnb, rhs=knb, start=True, stop=True), r=[qbn], w=[xn_])
                s.op('dve', lambda e: e.tensor_tensor(out=H.t1[:], in0=H.E[:], in1=msl[:], op=ALU.mult), r=[n('E'), 'msl'], w=[n('t1')])
                s.op('dve', lambda e, h=h: e.scalar_tensor_tensor(out=H.Nf[:], in0=P[:, X0 + 128:X0 + 256], scalar=sm[:, 32 + h:33 + h], in1=H.t1[:], op0=ALU.mult, op1=ALU.mult),
                     r=[xn_, sn, n('t1')], w=[n('Nf')])
                s.op('dve', lambda e: e.tensor_tensor(out=H.t1[:], in0=H.E[:], in1=mil[:], op=ALU.mult), r=[n('E'), 'mil', n('Nf')], w=[n('t1')])
                s.op('dve', lambda e: e.scalar_tensor_tensor(out=H.atf[:], in0=P[:, X0 + 256:X0 + 384], scalar=DK, in1=H.t1[:], op0=ALU.mult, op1=ALU.mult),
                     r=[xn_, n('t1')], w=[n('atf')])
                yield
                s.op('pe', lambda e: e.transpose(P[:, Y0:Y0 + 128], H.Nf[:], g.ident[:]), r=[n('Nf'), 'ident'], w=[yn_])
                s.op('pe', lambda e: e.transpose(P[:, Y0 + 128:Y0 + 256], H.atf[:], g.ident[:]), r=[n('atf'), 'ident'], w=[yn_])
                s.op('pe', lambda e, knb=knb: e.matmul(P[:, Y0 + 256:Y0 + 384], lhsT=knb, rhs=g.identb[:], start=True, stop=True), r=[qbn, 'identb'], w=[yn_])
                s.op('pe', lambda e, vvb=vvb: e.matmul(P[:, Y0 + 384:Y0 + 512], lhsT=vvb, rhs=g.identb[:], start=True, stop=True), r=[vn, 'identb'], w=[yn_])
                s.op('dve', lambda e: e.tensor_tensor(out=H.NM[:], in0=H.Nf[:].unsqueeze(1).to_broadcast([128, 5, 128]), in1=mdm[:], op=ALU.mult),
                     r=[n('Nf'), 'mdm'], w=[n('NM')])
                s.op('dve', lambda e: e.tensor_tensor(out=H.NMt[:], in0=P[:, Y0:Y0 + 128].unsqueeze(1).to_broadcast([128, 5, 128]), in1=mdmT[:], op=ALU.mult),
                     r=[yn_, 'mdmT'], w=[n('NMt')])
                s.op('act', lambda e: e.activation(out=H.attT[:], in_=P[:, Y0 + 128:Y0 + 256], func=AF.Copy), r=[yn_], w=[n('attT')])
                s.op('dve', lambda e: e.tensor_scalar(out=H.kd[:], in0=P[:, Y0 + 256:Y0 + 384], scalar1=H.sm[:, 1:2], scalar2=None, op0=ALU.mult), r=[yn_, n('sm1')], w=[n('kd')])
                s.op('dve', lambda e, h=h: e.tensor_scalar(out=H.Kbg[:], in0=P[:, Y0 + 256:Y0 + 384], scalar1=sm[:, 40 + h:41 + h], scalar2=None, op0=ALU.mult),
                     r=[yn_, sn], w=[n('Kbg')])
                s.op('dve', lambda e, h=h: e.tensor_scalar(out=H.Vb[:], in0=P[:, Y0 + 384:Y0 + 512], scalar1=sm[:, h:h + 1], scalar2=None, op0=ALU.mult), r=[yn_, sn], w=[n('Vb')])
                s.op('dve', lambda e, qnb=qnb: e.scalar_tensor_tensor(out=H.qdT[:], in0=qnb, scalar=DK, in1=H.EB[:], op0=ALU.mult, op1=ALU.mult),
                     r=[qbn, n('EB')], w=[n('qdT')])
                yield
                zc = [0]

                def mm(lhsT, rhs, rn):
                    c0 = X0 + (zc[0] % 3) * 128
                    zc[0] += 1
                    s.op('pe', lambda e: e.matmul(P[:, c0:c0 + 128], lhsT=lhsT, rhs=rhs, start=True, stop=True), r=rn, w=[zn_])
                    return P[:, c0:c0 + 128]

                def cp(dst, dn, src_ps):
                    s.op('act', lambda e: e.activation(out=dst, in_=src_ps, func=AF.Copy), r=[zn_], w=[dn])

                def acc(dst, dn, src_ps):
                    s.op('dve', lambda e: e.tensor_tensor(out=dst, in0=src_ps, in1=dst, op=ALU.add), r=[zn_, dn], w=[dn])

                M0, M0t = H.NM[:, 0, :], H.NMt[:, 0, :]
                s.op('dve', lambda e: e.tensor_tensor(out=H.X[:], in0=M0, in1=g.identb[:], op=ALU.add), r=[n('NM'), 'identb'], w=[n('X')])
                s.op('dve', lambda e: e.tensor_tensor(out=H.Xt[:], in0=M0t, in1=g.identb[:], op=ALU.add), r=[n('NMt'), 'identb'], w=[n('Xt')])
                cp(H.Q2[:], n('Q2'), mm(M0t, M0, [n('NM'), n('NMt')]))
                cp(H.Q2t[:], n('Q2t'), mm(M0, M0t, [n('NM'), n('NMt')]))
                yield
                acc(H.X[:], n('X'), mm(H.Q2t[:], H.X[:], [n('Q2t'), n('X')]))
                acc(H.Xt[:], n('Xt'), mm(H.Q2[:], H.Xt[:], [n('Q2'), n('Xt')]))
                cp(H.Q4[:], n('Q4'), mm(H.Q2t[:], H.Q2[:], [n('Q2'), n('Q2t')]))
                cp(H.Q4t[:], n('Q4t'), mm(H.Q2[:], H.Q2t[:], [n('Q2'), n('Q2t')]))
                yield
                acc(H.X[:], n('X'), mm(H.Q4t[:], H.X[:], [n('Q4t'), n('X')]))
                acc(H.Xt[:], n('Xt'), mm(H.Q4[:], H.Xt[:], [n('Q4'), n('Xt')]))
                yield
                for lv in range(1, 5):
                    Nb, Nbt = H.NM[:, lv, :], H.NMt[:, lv, :]
                    if lv < 4:
                        cp(H.Yv[:], n('Yv'), mm(Nbt, H.X[:], [n('NMt'), n('X')]))
                    cp(H.Yt[:], n('Yt'), mm(Nb, H.Xt[:], [n('NM'), n('Xt')]))
                    yield
                    pa = mm(H.Xt[:], H.Yv[:], [n('Xt'), n('Yv')]) if lv < 4 else None
                    pb = mm(H.X[:], H.Yt[:], [n('X'), n('Yt')])
                    if lv < 4:
                        acc(H.X[:], n('X'), pa)
                    acc(H.Xt[:], n('Xt'), pb)
                    yield
                pw = mm(H.Kbg[:], H.Xt[:], [n('Kbg'), n('Xt')])
                s.op('act', lambda e: e.activation(out=H.nwT[:], in_=pw, func=AF.Copy, scale=-1.0), r=[zn_], w=[n('nwT')])
                yield
                s.op('pe', lambda e: e.matmul(P[:, Y0:Y0 + 128], lhsT=H.Xt[:], rhs=H.Vb[:], start=True, stop=False), r=[n('Xt'), n('Vb')], w=[yn_], inc=False)
                s.op('pe', lambda e, h=h: e.matmul(P[:, Y0:Y0 + 128], lhsT=H.nwT[:], rhs=Sb[:, h, :], start=False, stop=True), r=[n('nwT'), 'gSb%d' % p], w=[yn_])
                s.op('act', lambda e: e.activation(out=H.vnew[:], in_=P[:, Y0:Y0 + 128], func=AF.Copy), r=[yn_], w=[n('vnew')])
                yield
                s.op('pe', lambda e, h=h: e.matmul(P[:, X0 + 384:X0 + 512], lhsT=H.qdT[:], rhs=Sb[:, h, :], start=True, stop=False),
                     r=[n('qdT'), 'gSb%d' % p], w=[xn_], inc=False)
                s.op('pe', lambda e: e.matmul(P[:, X0 + 384:X0 + 512], lhsT=H.attT[:], rhs=H.vnew[:], start=False, stop=True),
                     r=[n('attT'), n('vnew')], w=[xn_])
                s.op('act', lambda e, h=h: e.activation(out=osb[bp][:, h * 128:(h + 1) * 128], in_=P[:, X0 + 384:X0 + 512], func=AF.Copy),
                     r=[xn_], w=['gosb%d_%d' % (bp, p)])
                s.op('pe', lambda e: e.matmul(P[:, Y0 + 128:Y0 + 256], lhsT=H.kd[:], rhs=H.vnew[:], start=True, stop=True), r=[n('kd'), n('vnew')], w=[yn_])
                s.op('dve', lambda e, h=h: e.scalar_tensor_tensor(out=Sf[:, h, :], in0=Sf[:, h, :], scalar=H.sm[:, 0:1], in1=P[:, Y0 + 128:Y0 + 256], op0=ALU.mult, op1=ALU.add),
                     r=['gSf%d' % p, n('sm0'), yn_], w=['gSf%d' % p])
                s.op('act', lambda e, h=h: e.activation(out=Sb[:, h, :], in_=Sf[:, h, :], func=AF.Copy), r=['gSf%d' % p], w=['gSb%d' % p])
                yield

        def phaseZ(qb):
            t0 = qb * 128
            bp = qb % 2
            x3 = qb % 3
            ob = osb[bp]
            on = ['gosb%d_%d' % (bp, p_) for p_ in range(NTH)]
            s.op('dve', lambda e: e.tensor_tensor(out=zb[:], in0=ob[:], in1=ob[:], op=ALU.mult), r=on, w=['zb'])
            for h in range(8):
                s.op('dve', lambda e, h=h: e.reduce_sum(out=zsm[:, h:h + 1], in_=zb[:, h * 128:(h + 1) * 128], axis=AX.X), r=['zb'], w=['zsm'])
            yield
            s.op('dve', lambda e: e.tensor_scalar(out=zsm[:], in0=zsm[:], scalar1=1.0 / 128, scalar2=RMS_EPS, op0=ALU.mult, op1=ALU.add), r=['zsm'], w=['zsm'])
            s.op('pool', lambda e: e.tensor_tensor(out=zsm[:], in0=zsm[:], in1=g.mhalf[:], op=ALU.pow), r=['zsm', 'mhalf'], w=['zsm'])
            yield
            for h in range(8):
                s.op('dve', lambda e, h=h: e.tensor_scalar(out=ob[:, h * 128:(h + 1) * 128], in0=ob[:, h * 128:(h + 1) * 128], scalar1=zsm[:, h:h + 1],
                                                           scalar2=None, op0=ALU.mult), r=on + ['zsm'], w=on)
            yield
            s.op('dve', lambda e: e.tensor_tensor(out=ob[:], in0=ob[:], in1=onb[:], op=ALU.mult), r=on + ['onb'], w=on)
            s.op('dve', lambda e: e.tensor_tensor(out=ob[:], in0=ob[:], in1=gsil[x3][:], op=ALU.mult), r=on + ['gsil%d' % x3], w=on)
            yield
            for kc in range(8):
                b = kc // 4
                off = b * 512 + (kc % 4) * 128
                s.op('pe', lambda e, kc=kc, off=off: e.transpose(P[:, off:off + 128], ob[:, kc * 128:(kc + 1) * 128], g.ident[:]), r=on + ['ident'], w=['ps%d' % b])
            for j in range(2):
                s.op('act', lambda e, j=j: e.activation(out=yinT[:, j * 512:(j + 1) * 512], in_=P[:, j * 512:(j + 1) * 512], func=AF.Copy),
                     r=['ps%d' % j], w=['gyinT'])
            yield
            for half in range(2):
                b = half
                for kc in range(8):
                    s.op('pe', lambda e, kc=kc, half=half, b=b: e.matmul(P[:, b * 512:(b + 1) * 512], lhsT=yinT[:, kc * 128:(kc + 1) * 128],
                                                                        rhs=wout[:, kc, half * 512:(half + 1) * 512], start=(kc == 0), stop=(kc == 7)),
                         r=['gyinT', 'gwout'], w=['ps%d' % b], inc=(kc == 7))
            s.dma('sp', xrs[:], src[t0:t0 + 128, :], w=['gxrs'])
            emit_epilogue(g, L, (0, 1), xrs, 'gxrs', zb, xo[0], 'gxo0', dst[t0:t0 + 128, :])
            yield

        nblk = g.nblk_gdn
        for it in range(nblk + 2):
            gens = []
            if 1 <= it <= nblk:
                for p_ in range(NTH):
                    gens.append([heads(it - 1, p_), 1])
            if it < nblk:
                gens.append([phaseA(it), 1])
            if it >= 2:
                gens.append([phaseZ(it - 2), 1])
            _interleave(gens)
        s.barrier()


W_SPECS = [
    ("e_mod_w", [1, D, 3 * D]), ("e_mod_b", [1, 1, 3 * D]), ("e_ln_g", [1, D]), ("e_ln_b", [1, D]),
    ("o_mod_w", [1, D, 3 * D]), ("o_mod_b", [1, 1, 3 * D]), ("o_ln_g", [1, D]), ("o_ln_b", [1, D]),
    ("f_mod_w", [2, D, 3 * D]), ("f_mod_b", [2, 1, 3 * D]), ("f_ln_g", [2, D]), ("f_ln_b", [2, D]),
    ("f_w_up", [2, D, 2 * DFF]), ("f_w_down", [2, DFF, D]), ("f_cw", [2, 128, 2 * NFC, 4]),
    ("ident", [128, 128]),
    ("e_w_in", [1, D, 1476]), ("e_pool_w", [1, 4, 128, 128]), ("pscale", [128, 4]), ("e_kv_norm", [1, 128]),
    ("o_w_in", [1, D, 4112]), ("o_w_out", [1, D, D]), ("o_cw", [128, 24, 4]), ("msl", [128, 128]), ("mil", [128, 128]), ("triu", [128, 128]), ("mdm", [128, 5, 128]), ("mdmT", [128, 5, 128]),
    ("o_a_log", [1, 8]), ("o_dt_bias", [1, 8]), ("onb", [1, D]),
    ("ukT", [128, 4, 128]), ("uvpad", [128, 8, 128]), ("e_w_out", [1, D, D]), ("negI4", [128, 512]), ("corr", [128, 4, 15]),
]


def build(stages):
    nc = bass.Bass("TRN2", target_bir_lowering=False)
    g = Ctx()
    g.nc = nc
    g.d_x = nc.dram_tensor("x", [S, D], F32, kind="ExternalInput").ap()
    g.d_ccol = nc.dram_tensor("ccol", [128, 8], F32, kind="ExternalInput").ap()
    for nm, shp in W_SPECS:
        setattr(g, "d_" + nm, nc.dram_tensor(nm, shp, F32, kind="ExternalInput").ap())
    g.d_out = nc.dram_tensor("out", [S, D], F32, kind="ExternalOutput").ap()
    scr = [nc.dram_tensor("xscr%d" % i, [S, D], F32, kind="Internal").ap() for i in range(3)]
    with ExitStack() as es:
        g.es = es
        g.s = Sch(nc, es)
        g.ps = es.enter_context(nc.psum_tensor("ps", [128, 8 * 512], F32))
        g.lnst = _sb(nc, es, "lnst", [128, 12], F32)
        g.lnmv = _sb(nc, es, "lnmv", [128, 8], F32)
        emit_consts(g)
        bufs = [g.d_x] + scr
        n = len(stages)
        for i, st in enumerate(stages):
            src = g.d_x if i == 0 else scr[(i - 1) % 3]
            dst = g.d_out if i == n - 1 else scr[i % 3]
            if st[0] == 'ffn':
                emit_ffn(g, st[1], src, dst)
            elif st[0] == 'gdn':
                g.nblk_gdn = st[1] if len(st) > 1 else NB
                emit_gdn(g, src, dst)
            elif st[0] == 'dsa':
                g.nblk_dsa = st[1] if len(st) > 1 else NB
                emit_dsa(g, src, dst)
            else:
                raise ValueError(st)
        g.s.finish()
    return nc


def prep_weights(inp):
    f = lambda a: np.ascontiguousarray(np.asarray(a, dtype=np.float32))
    w = {}
    for k in ("e_mod_w", "e_ln_g", "e_ln_b", "o_mod_w", "o_ln_g", "o_ln_b", "f_mod_w", "f_ln_g", "f_ln_b", "f_w_up", "f_w_down"):
        w[k] = f(inp[k])
    for k in ("e_mod_b", "o_mod_b", "f_mod_b"):
        a = f(inp[k])
        w[k] = np.ascontiguousarray(a.reshape(a.shape[0], 1, 3 * D))
    cwt = f(inp["f_conv_w"])
    cb = f(inp["f_conv_b"])
    a = np.concatenate([cwt, cb[:, None, :]], axis=1)
    a = a.reshape(2, 4, 2 * NFC, 128).transpose(0, 3, 2, 1)
    w["f_cw"] = np.ascontiguousarray(a)
    w["ident"] = np.eye(128, dtype=np.float32)
    for k in ("e_w_in", "e_pool_w", "e_kv_norm", "e_w_out"):
        w[k] = f(inp[k])
    w["pscale"] = np.ascontiguousarray(f(inp["e_pool_scale"])[0].reshape(4, 128).T)
    uk = f(inp["e_w_uk"])[0]
    w["ukT"] = np.ascontiguousarray(uk.reshape(4, 2, 128, 64).transpose(1, 3, 0, 2).reshape(128, 4, 128))
    uv = f(inp["e_w_uv"])[0]
    uvp = np.zeros((128, 8, 128), np.float32)
    for h in range(8):
        uvp[:, h, (h % 2) * 64:(h % 2) * 64 + 64] = uv[h]
    w["uvpad"] = uvp
    w["negI4"] = np.ascontiguousarray(np.tile(-30000.0 * np.eye(128, dtype=np.float32), (1, 4)))
    corr = np.ones((128, 4, 15), np.float32)
    for gi in range(4):
        win_ = 2 << gi
        for t in range(win_ - 1):
            corr[:, gi, t] = win_ / (t + 1.0)
    w["corr"] = corr
    for k in ("o_w_in", "o_w_out", "o_a_log", "o_dt_bias"):
        w[k] = f(inp[k])
    ocw = f(inp["o_conv_w"])[0]
    w["o_cw"] = np.ascontiguousarray(ocw.reshape(4, 24, 128).transpose(2, 1, 0))
    ar = np.arange(128)
    w["msl"] = (ar[:, None] > ar[None, :]).astype(np.float32)
    w["mil"] = (ar[:, None] >= ar[None, :]).astype(np.float32)
    w["triu"] = (ar[:, None] <= ar[None, :]).astype(np.float32)
    w["onb"] = np.ascontiguousarray(np.tile(f(inp["o_out_norm"])[0], 8)[None, :])
    mdm = np.zeros((128, 5, 128), np.float32)
    mdm[:, 0, :] = (ar[:, None] // 8 == ar[None, :] // 8)
    for li, bsz in enumerate((8, 16, 32, 64)):
        bl = ar // bsz
        mdm[:, 1 + li, :] = (bl[:, None] % 2 == 1) & (bl[None, :] == bl[:, None] - 1)
    w["mdm"] = mdm
    w["mdmT"] = np.ascontiguousarray(mdm.transpose(2, 1, 0))
    return w


STAGES = [('dsa',), ('ffn', 0), ('gdn',), ('ffn', 1)]


def kernel(**inp):
    x = np.asarray(inp["x"], dtype=np.float32)
    c = np.asarray(inp["c"], dtype=np.float32)
    w = prep_weights(inp)
    nc = build(STAGES)
    in_maps = []
    for b in range(8):
        m = dict(w)
        m["x"] = np.ascontiguousarray(x[b])
        m["ccol"] = np.ascontiguousarray(c[b].reshape(8, 128).T)
        in_maps.append(m)
    res = run_bass_kernel_spmd(nc, in_maps, core_ids=list(range(8)))
    return np.stack([np.asarray(r["out"], dtype=np.float32) for r in res.results], axis=0)
```

```python
import os
import numpy as np
from contextlib import ExitStack
import concourse.bass as bass
import concourse.mybir as mybir
from concourse.bass_utils import run_bass_kernel_spmd

F32 = mybir.dt.float32
BF16 = mybir.dt.bfloat16
AF = mybir.ActivationFunctionType
ALU = mybir.AluOpType
AX = mybir.AxisListType

D = 1024
S = 4096
NB = S // 128
DFF = 2688
NFC = DFF // 128
ALPHA = float(4 ** 0.25)
LN_EPS = 1e-5
RMS_EPS = 1e-6
NDS = 12


class Sch:
    def __init__(self, nc, es):
        self.nc = nc
        self.E = {'pe': nc.tensor, 'act': nc.scalar, 'dve': nc.vector,
                  'pool': nc.gpsimd, 'sp': nc.sync}
        self.sem = {}
        for e in self.E:
            self.sem[e] = es.enter_context(nc.semaphore('s_' + e))
        self.cnt = {e: 0 for e in self.E}
        self.waited = {e: {} for e in self.E}
        self.lastw = {}
        self.readers = {}
        self.dq = ('sp', 'pool', 'act')
        self.dcnt = {}
        self.drr = {q: 0 for q in self.dq}
        for q in self.dq:
            for i in range(NDS):
                k = (q, i)
                self.sem[k] = es.enter_context(nc.semaphore('d_%s%d' % (q, i)))
                self.dcnt[k] = 0
        self.nwaits = 0

    def _wait(self, e, tok):
        key, val = tok
        if key == e and e == 'pe':
            return
        if self.waited[e].get(key, 0) >= val:
            return
        self.E[e].wait_ge(self.sem[key], val)
        self.waited[e][key] = val
        self.nwaits += 1

    def _collect(self, r, w, e=None):
        deps = {}

        def add(t):
            if t is None:
                return
            if deps.get(t[0], 0) < t[1]:
                deps[t[0]] = t[1]
        for x in r:
            for k, v in self.lastw.get(x, {}).items():
                add((k, v))
            if x.startswith('ps') and e is not None:
                for k, v in self.readers.get(x, {}).items():
                    if k != e:
                        add((k, v))
        for x in w:
            for k, v in self.lastw.get(x, {}).items():
                add((k, v))
            for k, v in self.readers.get(x, {}).items():
                add((k, v))
        return list(deps.items())

    def _record(self, tok, r, w):
        for x in w:
            self.lastw.setdefault(x, {})[tok[0]] = tok[1]
            self.readers[x] = {}
        for x in r:
            d = self.readers.setdefault(x, {})
            if d.get(tok[0], 0) < tok[1]:
                d[tok[0]] = tok[1]

    def op(self, e, fn, r=(), w=(), inc=True):
        for t in self._collect(r, w, e):
            self._wait(e, t)
        ins = fn(self.E[e])
        if inc:
            self.cnt[e] += 1
            ins.then_inc(self.sem[e], 1)
            tok = (e, self.cnt[e])
        else:
            tok = (e, self.cnt[e] + 1)
        self._record(tok, r, w)
        return ins

    def dma(self, q, out, in_, r=(), w=()):
        i = self.drr[q]
        self.drr[q] = (i + 1) % NDS
        k = (q, i)
        if self.dcnt[k] > 0:
            self._wait(q, (k, self.dcnt[k]))
        for t in self._collect(r, w):
            self._wait(q, t)
        ins = self.E[q].dma_start(out=out, in_=in_)
        self.dcnt[k] += 16
        ins.then_inc(self.sem[k], 16)
        tok = (k, self.dcnt[k])
        self._record(tok, r, w)
        return ins

    def barrier(self, keep_pool_dma=False):
        def is_pool(k):
            return isinstance(k, tuple) and k[0] == 'pool'
        toks = [(e, self.cnt[e]) for e in self.E if self.cnt[e] > 0]
        toks += [(k, v) for k, v in self.dcnt.items() if v > 0 and not (keep_pool_dma and is_pool(k))]
        for e in self.E:
            for t in toks:
                self._wait(e, t)
        keep = {}
        if keep_pool_dma:
            for res, d in self.lastw.items():
                d2 = {k: v for k, v in d.items() if is_pool(k)}
                if d2:
                    keep[res] = d2
        self.lastw = keep
        self.readers = {}

    def finish(self):
        for k, v in self.dcnt.items():
            if v > 0:
                self._wait('sp', (k, v))


class Ctx:
    pass


def _interleave(gens):
    live = list(gens)
    while live:
        nxt = []
        for item in live:
            gen, n = item
            done = False
            for _ in range(n):
                try:
                    next(gen)
                except StopIteration:
                    done = True
                    break
            if not done:
                nxt.append(item)
        live = nxt


_UID = [0]


def _sb(nc, es, name, shape, dt):
    _UID[0] += 1
    return es.enter_context(nc.sbuf_tensor("sb%d_%s" % (_UID[0], name), list(shape), dt))


def emit_consts(g):
    nc, s, es = g.nc, g.s, g.es
    g.ident = _sb(nc, es, "ident", [128, 128], F32)
    g.identb = _sb(nc, es, "identb", [128, 128], BF16)
    g.ones = _sb(nc, es, "ones", [128, 128], F32)
    g.onesb = _sb(nc, es, "onesb", [128, 128], BF16)
    s.dma('sp', g.ident[:], g.d_ident[:, :], w=['ident'])
    s.dma('pool', g.identb[:], g.d_ident[:, :], w=['identb'])
    s.op('dve', lambda e: e.memset(g.ones[:], 1.0), w=['ones'])
    s.op('dve', lambda e: e.memset(g.onesb[:], 1.0), w=['onesb'])
    g.mhalf = _sb(nc, es, "mhalf", [128, 8], F32)
    s.op('pool', lambda e: e.memset(g.mhalf[:], -0.5), w=['mhalf'])
    g.ccol = _sb(nc, es, "ccol", [128, 8], F32)
    g.sc = _sb(nc, es, "sc", [128, 8], F32)
    g.scb = _sb(nc, es, "scb", [128, 8, 128], F32)
    s.dma('sp', g.ccol[:], g.d_ccol[:, :], w=['ccol'])
    s.op('act', lambda e: e.activation(out=g.sc[:], in_=g.ccol[:], func=AF.Silu), r=['ccol'], w=['sc'])
    for kc in range(8):
        s.op('dve', lambda e, kc=kc: e.tensor_scalar(out=g.scb[:, kc, :], in0=g.ones[:], scalar1=g.sc[:, kc:kc + 1],
                                                     scalar2=None, op0=ALU.mult), r=['ones', 'sc'], w=['scb'])


def emit_mods(g, L, modw, modb_row, lng, lnb, pre=None):
    nc, s, es = g.nc, g.s, g.es
    L.shift = _sb(nc, L.es, "shift", [128, 8], F32)
    L.scale1 = _sb(nc, L.es, "scale1", [128, 8], F32)
    L.gate_bc = _sb(nc, L.es, "gate_bc", [128, D], F32)
    L.lng_bc = _sb(nc, L.es, "lng_bc", [128, D], F32)
    L.lnb_bc = _sb(nc, L.es, "lnb_bc", [128, D], F32)
    s.dma('sp', L.lng_bc[:], lng.partition_broadcast(128), w=['lng_bc'])
    s.dma('sp', L.lnb_bc[:], lnb.partition_broadcast(128), w=['lnb_bc'])
    if pre is not None:
        pre()
    with ExitStack() as es2:
        mw = [_sb(nc, es2, "mw%d" % i, [128, 3 * D], F32) for i in range(2)]
        brow = _sb(nc, es2, "brow", [1, 3 * D], F32)
        bc = _sb(nc, es2, "modbc", [128, 2 * D], F32)
        one11 = _sb(nc, es2, "one11", [1, 1], F32)
        s.op('dve', lambda e: e.memset(one11[:], 1.0), w=['one11'])
        s.dma('sp', brow[:], modb_row[:, :], w=['brow'])
        P = g.ps
        for kc in range(8):
            t = mw[kc % 2]
            nm = 'mw%d' % (kc % 2)
            s.dma('sp', t[:], modw[kc * 128:(kc + 1) * 128, :], w=[nm])
            for j in range(6):
                s.op('pe', lambda e, j=j, kc=kc, t=t: e.matmul(P[:, j * 512:(j + 1) * 512], lhsT=g.scb[:, kc, :],
                                                            rhs=t[:, j * 512:(j + 1) * 512], start=(kc == 0), stop=False),
                     r=[nm, 'scb'], w=['ps%d' % j], inc=(j == 5))
        for j in range(6):
            s.op('pe', lambda e, j=j: e.matmul(P[:, j * 512:(j + 1) * 512], lhsT=g.ones[0:1, :],
                                               rhs=brow[0:1, j * 512:(j + 1) * 512], start=False, stop=True),
                 r=['brow', 'ones'], w=['ps%d' % j])
        for j in range(4):
            s.op('act' if j % 2 else 'dve',
                 (lambda e, j=j: e.activation(out=bc[:, j * 512:(j + 1) * 512], in_=P[:, j * 512:(j + 1) * 512], func=AF.Copy))
                 if j % 2 else
                 (lambda e, j=j: e.tensor_copy(out=bc[:, j * 512:(j + 1) * 512], in_=P[:, j * 512:(j + 1) * 512])),
                 r=['ps%d' % j], w=['modbc%d' % j])
        for j in range(2):
            s.op('dve', lambda e, j=j: e.tensor_copy(out=L.gate_bc[:, j * 512:(j + 1) * 512], in_=P[:, (4 + j) * 512:(5 + j) * 512]),
                 r=['ps%d' % (4 + j)], w=['gate_bc'])
        for j in range(16):
            s.op('pe', lambda e, j=j: e.matmul(P[:, 6 * 512 + j:6 * 512 + j + 1], lhsT=bc[0:1, j * 128:(j + 1) * 128],
                                               rhs=one11[0:1, 0:1], start=True, stop=True),
                 r=['modbc%d' % (j // 4), 'one11'], w=['ps6'])
        s.op('dve', lambda e: e.tensor_copy(out=L.shift[:], in_=P[:, 6 * 512:6 * 512 + 8]), r=['ps6'], w=['shift'])
        s.op('dve', lambda e: e.tensor_scalar(out=L.scale1[:], in0=P[:, 6 * 512 + 8:6 * 512 + 16], scalar1=1.0, scalar2=None,
                                              op0=ALU.add), r=['ps6'], w=['scale1'])
        s.barrier(keep_pool_dma=(pre is not None))


def emit_hT(g, L, xin, xin_nm, hT_ap_fn, hT_nm, pbanks):
    s = g.s
    P = g.ps
    for kc in range(8):
        b = pbanks[kc // 4]
        off = b * 512 + (kc % 4) * 128
        s.op('pe', lambda e, kc=kc, off=off: e.transpose(P[:, off:off + 128], xin[:, kc * 128:(kc + 1) * 128], g.ident[:]),
             r=[xin_nm, 'ident'], w=['ps%d' % b])
    for kc in range(8):
        b = pbanks[kc // 4]
        off = b * 512 + (kc % 4) * 128
        s.op('act', lambda e, kc=kc, off=off: e.activation(out=hT_ap_fn(kc), in_=P[:, off:off + 128], func=AF.Identity,
                                                           scale=L.scale1[:, kc:kc + 1], bias=L.shift[:, kc:kc + 1]),
             r=['ps%d' % b, 'scale1', 'shift'], w=[hT_nm])


def emit_epilogue(g, L, ybanks, xres, xres_nm, zb, xo, xo_nm, dst_rows):
    s = g.s
    P = g.ps
    for j in range(2):
        b = ybanks[j]
        s.op('dve', lambda e, j=j, b=b: e.tensor_tensor(out=zb[:, j * 512:(j + 1) * 512], in0=P[:, b * 512:(b + 1) * 512],
                                                        in1=L.gate_bc[:, j * 512:(j + 1) * 512], op=ALU.mult),
             r=['ps%d' % b, 'gate_bc'], w=['zb'])
    s.op('dve', lambda e: e.scalar_tensor_tensor(out=zb[:], in0=xres[:], scalar=ALPHA, in1=zb[:], op0=ALU.mult, op1=ALU.add),
         r=[xres_nm, 'zb'], w=['zb'])
    st = g.lnst
    for j in range(2):
        s.op('dve', lambda e, j=j: e.bn_stats(out=st[:, j * 6:(j + 1) * 6], in_=zb[:, j * 512:(j + 1) * 512]), r=['zb'], w=['lnst'])
    s.op('dve', lambda e: e.bn_aggr(out=g.lnmv[:, 0:2], in_=st[:, 0:12]), r=['lnst'], w=['lnmv'])
    s.op('dve', lambda e: e.tensor_scalar(out=g.lnmv[:, 2:3], in0=g.lnmv[:, 1:2], scalar1=LN_EPS, scalar2=None, op0=ALU.add),
         r=['lnmv'], w=['lnmv2'])
    s.op('pool', lambda e: e.tensor_tensor(out=g.lnmv[:, 4:5], in0=g.lnmv[:, 2:3], in1=g.mhalf[:, 0:1], op=ALU.pow), r=['lnmv2', 'mhalf'], w=['lnmv4'])
    s.op('dve', lambda e: e.tensor_scalar(out=zb[:], in0=zb[:], scalar1=g.lnmv[:, 0:1], scalar2=g.lnmv[:, 4:5],
                                          op0=ALU.subtract, op1=ALU.mult), r=['zb', 'lnmv', 'lnmv4'], w=['zb'])
    s.op('pool', lambda e: e.tensor_tensor(out=zb[:], in0=zb[:], in1=L.lng_bc[:], op=ALU.mult), r=['zb', 'lng_bc'], w=['zb'])
    s.op('pool', lambda e: e.tensor_tensor(out=xo[:], in0=zb[:], in1=L.lnb_bc[:], op=ALU.add), r=['zb', 'lnb_bc'], w=[xo_nm])
    s.dma('sp', dst_rows, xo[:], r=[xo_nm], w=[])


def emit_ffn(g, li, src, dst):
    nc, s = g.nc, g.s
    TT = 256
    NT = S // TT
    NBT = TT // 128
    with ExitStack() as les:
        L = Ctx()
        L.es = les
        WT = Ctx()

        def pre():
            WT.wup = _sb(nc, les, "wup", [128, 8, 2 * DFF], BF16)
            WT.wdn = _sb(nc, les, "wdn", [128, NFC, D], BF16)
            wupd = g.d_f_w_up[li].rearrange("(kc p) n -> p kc n", p=128)
            for kc in range(8):
                s.dma('pool', WT.wup[:, kc, :], wupd[:, kc, :], w=['wup'])
            wdnd = g.d_f_w_down[li].rearrange("(fc p) n -> p fc n", p=128)
            for f3 in range(3):
                s.dma('pool', WT.wdn[:, f3 * 7:(f3 + 1) * 7, :], wdnd[:, f3 * 7:(f3 + 1) * 7, :], w=['wdn'])
        emit_mods(g, L, g.d_f_mod_w[li], g.d_f_mod_b[li], g.d_f_ln_g[li], g.d_f_ln_b[li], pre=pre)
        wup, wdn = WT.wup, WT.wdn
        cw = _sb(nc, les, "cw", [128, 2 * NFC, 4], F32)
        halo = _sb(nc, les, "halo", [128, 2 * NFC, 2], F32)
        hT = [_sb(nc, les, "hT%d" % i, [128, 8, TT], BF16) for i in range(2)]
        gTs = [_sb(nc, les, "gT%d" % i, [128, NFC, TT], BF16) for i in range(2)]
        upre = [_sb(nc, les, "upre%d" % i, [128, TT + 2], F32) for i in range(2)]
        c0 = [_sb(nc, les, "c0%d" % i, [128, TT], F32) for i in range(2)]
        asil = _sb(nc, les, "asil", [128, TT], F32)
        xin = [_sb(nc, les, "xin%d" % i, [128, D], F32) for i in range(2)]
        xrs = [_sb(nc, les, "xrs%d" % i, [128, D], F32) for i in range(2)]
        zb = _sb(nc, les, "zb", [128, D], F32)
        xo = [_sb(nc, les, "xo%d" % i, [128, D], F32) for i in range(2)]
        P = g.ps
        s.dma('sp', cw[:], g.d_f_cw[li], w=['cw'])
        s.op('dve', lambda e: e.memset(halo[:], 0.0), w=['halo%d' % i for i in range(2 * NFC)])
        def phA(t):
            t0 = t * TT
            hs = t % 2
            hnm = 'hT%d' % hs
            for bi in range(NBT):
                xs_ = (t * NBT + bi) % 2
                s.dma('sp', xin[xs_][:], src[t0 + bi * 128:t0 + (bi + 1) * 128, :], w=['xin%d' % xs_])
                emit_hT(g, L, xin[xs_], 'xin%d' % xs_, lambda kc, bi=bi, hs=hs: hT[hs][:, kc, bi * 128:(bi + 1) * 128], hnm, (0, 1))
                yield

        def phU(t):
            hs = t % 2
            hnm = 'hT%d' % hs
            gT = gTs[t % 2]
            gnm = 'gT%d' % (t % 2)
            for j in range(NFC):
                for half in range(2):
                    fc = j + half * NFC
                    b = 2 + ((2 * j + half) % 4)
                    pb = 'ps%d' % b
                    up = upre[half]
                    unm = 'upre%d' % half
                    for kc in range(8):
                        s.op('pe', lambda e, kc=kc, fc=fc, b=b: e.matmul(P[:, b * 512:b * 512 + TT], lhsT=wup[:, kc, fc * 128:(fc + 1) * 128],
                                                                      rhs=hT[hs][:, kc, :], start=(kc == 0), stop=(kc == 7)),
                             r=['wup', hnm], w=[pb], inc=(kc == 7))
                    s.op('pool', lambda e, fc=fc, up=up: e.tensor_copy(out=up[:, 0:2], in_=halo[:, fc, :]), r=['halo%d' % fc], w=[unm])
                    s.op('act', lambda e, b=b, up=up: e.activation(out=up[:, 2:TT + 2], in_=P[:, b * 512:b * 512 + TT], func=AF.Copy),
                         r=[pb], w=[unm])
                    s.op('act', lambda e, b=b, fc=fc, half=half: e.activation(out=c0[half][:], in_=P[:, b * 512:b * 512 + TT], func=AF.Identity,
                                                                             scale=cw[:, fc, 2:3], bias=cw[:, fc, 3:4]),
                         r=[pb, 'cw'], w=['c0%d' % half])
                    s.op('pool', lambda e, fc=fc, up=up: e.tensor_copy(out=halo[:, fc, :], in_=up[:, TT:TT + 2]), r=[unm], w=['halo%d' % fc])
                    s.op('dve', lambda e, fc=fc, up=up, half=half: e.scalar_tensor_tensor(out=c0[half][:], in0=up[:, 1:TT + 1], scalar=cw[:, fc, 1:2],
                                                                                        in1=c0[half][:], op0=ALU.mult, op1=ALU.add),
                         r=[unm, 'cw', 'c0%d' % half], w=['c0%d' % half])
                    s.op('dve', lambda e, fc=fc, up=up, half=half: e.scalar_tensor_tensor(out=c0[half][:], in0=up[:, 0:TT], scalar=cw[:, fc, 0:1],
                                                                                        in1=c0[half][:], op0=ALU.mult, op1=ALU.add),
                         r=[unm, 'cw', 'c0%d' % half], w=['c0%d' % half])
                    if half == 0:
                        s.op('act', lambda e: e.activation(out=asil[:], in_=c0[0][:], func=AF.Silu), r=['c00'], w=['asil'])
                    else:
                        s.op('dve', lambda e, j=j, gT=gT: e.tensor_tensor(out=gT[:, j, :], in0=asil[:], in1=c0[1][:], op=ALU.mult),
                             r=['asil', 'c01'], w=[gnm])
                    yield

        def phD(t):
            t0 = t * TT
            gT = gTs[t % 2]
            gnm = 'gT%d' % (t % 2)
            for bi in range(NBT):
                r0 = t0 + bi * 128
                xs_ = (t * NBT + bi) % 2
                s.dma('sp', xrs[xs_][:], src[r0:r0 + 128, :], w=['xrs%d' % xs_])
                for half in range(2):
                    b = 6 + half
                    for j in range(NFC):
                        s.op('pe', lambda e, j=j, half=half, b=b, bi=bi, gT=gT: e.matmul(P[:, b * 512:(b + 1) * 512], lhsT=gT[:, j, bi * 128:(bi + 1) * 128],
                                                                                         rhs=wdn[:, j, half * 512:(half + 1) * 512],
                                                                                         start=(j == 0), stop=(j == NFC - 1)),
                             r=[gnm, 'wdn'], w=['ps%d' % b], inc=(j == NFC - 1))
                    yield
                emit_epilogue(g, L, (6, 7), xrs[xs_], 'xrs%d' % xs_, zb, xo[xs_], 'xo%d' % xs_, dst[r0:r0 + 128, :])
                yield

        for it in range(NT + 2):
            gens = []
            if 1 <= it <= NT:
                gens.append([phU(it - 1), 1])
            if it < NT:
                gens.append([phA(it), 1])
            if it >= 2:
                gens.append([phD(it - 2), 1])
            _interleave(gens)
        s.barrier()


NEG = -1.0e30
GUARD = 1.0e38
NBISECT = 24
TOPK_EXACT_NK = 1024
DSTOP = int(os.environ.get('DSA_STOP', '99'))
DSUB = int(os.environ.get('DSA_SUB', '99'))
GSTOP = int(os.environ.get('GDN_STOP', '99'))
DSC = int(os.environ.get('DSA_SC', '3'))
DSKIP = os.environ.get('DSA_SKIP', '').split(',')
REP = -3.0e38


def emit_dsa(g, src, dst):
    nc, s = g.nc, g.s
    P = g.ps
    with ExitStack() as les:
        L = Ctx()
        L.es = les
        emit_mods(g, L, g.d_e_mod_w[0], g.d_e_mod_b[0], g.d_e_ln_g[0], g.d_e_ln_b[0])
        A = lambda name, shape, dt: _sb(nc, les, name, shape, dt)
        win = A("win", [128, 8, 1536], BF16)
        wif = A("wif", [128, 8, 4], F32)
        wiw = A("wiw", [128, 8, 4], BF16)
        poolw = A("poolw", [128, 4, 128], BF16)
        pscale = A("pscale", [128, 4], F32)
        kvn_bc = A("kvn_bc", [128, 128], F32)
        ukT = A("ukT", [128, 4, 128], BF16)
        uvpad = A("uvpad", [128, 8, 128], BF16)
        wout = A("wout", [128, 8, D], BF16)
        negI4 = A("negI4", [128, 512], BF16)
        corr = A("corr", [128, 4, 15], F32)
        ckvn_all = A("ckvn_all", [128, NB, 128], BF16)
        ckvnT_all = A("ckvnT_all", [128, S], BF16)
        kiT_all = A("kiT_all", [128, S], BF16)
        xin = [A("xin%d" % i, [128, D], F32) for i in range(3)]
        hT = A("hT", [128, 8, 128], BF16)
        ut = A("ut", [128, 4, 143], F32)
        ta = A("ta", [128, 143], F32)
        tb = A("tb", [128, 143], F32)
        dT = A("dT", [128, 4, 128], BF16)
        qT = A("qT", [128, 512], BF16)
        qiT = [A("qiT%d" % i, [128, 256], BF16) for i in range(2)]
        wis = [A("wis%d" % i, [128, 4], F32) for i in range(2)]
        qlT = [A("qlT%d" % i, [128, 1024], BF16) for i in range(3)]
        W = [A("W%d" % i, [128, S + 8], F32) for i in range(2)]
        bs = A("bs", [128, 8], F32)
        cb = A("cb", [128, S], BF16)
        rbuf = [A("rbuf%d" % i, [128, 512], F32) for i in range(2)]
        rb2 = A("rb2", [128, 512], F32)
        notm = [A("notm%d" % i, [128, S], BF16) for i in range(2)]
        m8 = A("m8", [128, 8], F32)
        pT = [A("pT%d" % i, [128, 512], BF16) for i in range(2)]
        rden = A("rden", [128, 512], F32)
        oTn = A("oTn", [128, 1024], BF16)
        yinT = [A("yinT%d" % i, [128, 1024], BF16) for i in range(3)]
        zb = A("zb", [128, D], F32)
        xo = [A("xo%d" % i, [128, D], F32) for i in range(2)]
        sq = A("sq", [128, 128], F32)
        ckf = A("ckf", [128, 128], F32)
        rs = A("rs", [128, 4], F32)
        Pb3 = P[:, 3 * 512:4 * 512].bitcast(BF16)

        wind = g.d_e_w_in[0].rearrange("(kc p) n -> p kc n", p=128)
        for kc in range(8):
            s.dma('pool', win[:, kc, 0:1472], wind[:, kc, 0:1472], w=['win'])
            s.dma('pool', win[:, kc, 1472:1536], wind[:, kc, 1408:1472], w=['win'])
        s.dma('sp', wif[:], wind[:, :, 1472:1476], w=['wif'])
        s.op('dve', lambda e: e.tensor_copy(out=wiw[:], in_=wif[:]), r=['wif'], w=['wiw'])
        s.dma('pool', poolw[:], g.d_e_pool_w[0].rearrange("g c d -> c g d"), w=['poolw'])
        s.dma('sp', pscale[:], g.d_pscale[:, :], w=['pscale'])
        s.dma('sp', kvn_bc[:], g.d_e_kv_norm[0].partition_broadcast(128), w=['kvn_bc'])
        s.dma('pool', ukT[:], g.d_ukT[:, :, :], w=['ukT'])
        s.dma('pool', uvpad[:], g.d_uvpad[:, :, :], w=['uvpad'])
        woutd = g.d_e_w_out[0].rearrange("(kc p) n -> p kc n", p=128)
        for k2 in range(2):
            s.dma('pool', wout[:, k2 * 4:(k2 + 1) * 4, :], woutd[:, k2 * 4:(k2 + 1) * 4, :], w=['wout'])
        s.dma('pool', negI4[:], g.d_negI4[:, :], w=['negI4'])
        s.dma('sp', corr[:], g.d_corr[:, :, :], w=['corr'])
        s.op('dve', lambda e: e.memset(ut[:], 0.0), w=['ut'])

        def front_a(qb):
            sl = qb % 2
            s3 = qb % 3
            t0 = qb * 128
            nk = t0 + 128
            xn = 'xin%d' % s3
            s.dma('sp', xin[s3][:], src[t0:t0 + 128, :], w=[xn])
            emit_hT(g, L, xin[s3], xn, lambda kc: hT[:, kc, :], 'hT', (0, 1))
            yield
            def grp(out_ap, cols, bank, last=True):
                for kc in range(8):
                    s.op('pe', lambda e, kc=kc: e.matmul(out_ap, lhsT=win[:, kc, cols[0]:cols[1]], rhs=hT[:, kc, :],
                                                        start=(kc == 0), stop=(kc == 7)),
                         r=['win', 'hT'], w=['ps%d' % bank], inc=(kc == 7))
            for gi in range(4):
                grp(P[:, gi * 128:(gi + 1) * 128], (gi * 128, (gi + 1) * 128), 0)
                yield
            for j in range(4):
                grp(P[:, 512 + j * 128:512 + (j + 1) * 128], (512 + j * 128, 512 + (j + 1) * 128), 1)
                yield
            for j in range(2):
                grp(P[:, 1024 + j * 128:1024 + (j + 1) * 128], (1152 + j * 128, 1152 + (j + 1) * 128), 2)
                yield
            grp(P[:, 1024 + 256:1024 + 384], (1408, 1536), 2)
            yield
            for kc in range(8):
                s.op('pe', lambda e, kc=kc: e.matmul(P[:, 1536:1536 + 128], lhsT=hT[:, kc, :], rhs=win[:, kc, 1024:1152],
                                                    start=(kc == 0), stop=(kc == 7)), r=['win', 'hT'], w=['ps3'], inc=(kc == 7))
            for kc in range(8):
                s.op('pe', lambda e, kc=kc: e.matmul(P[:, 1536 + 128:1536 + 132], lhsT=hT[:, kc, :], rhs=wiw[:, kc, :],
                                                    start=(kc == 0), stop=(kc == 7)), r=['wiw', 'hT'], w=['ps3'], inc=(kc == 7))
            yield
            s.op('act', lambda e: e.activation(out=ut[:, :, 15:143], in_=P[:, 0:512].rearrange("p (g t) -> p g t", g=4), func=AF.Copy),
                 r=['ps0'], w=['ut'])
            s.op('act', lambda e: e.activation(out=qT[:], in_=P[:, 512:1024], func=AF.Copy), r=['ps1'], w=['qT'])
            s.op('dve', lambda e: e.tensor_copy(out=qiT[sl][:], in_=P[:, 1024:1024 + 256]), r=['ps2'], w=['qiT%d' % sl])
            s.op('dve', lambda e: e.tensor_copy(out=kiT_all[:, t0:t0 + 128], in_=P[:, 1024 + 256:1024 + 384]), r=['ps2'], w=['kiT%d' % qb])
            s.op('dve', lambda e: e.tensor_copy(out=wis[sl][:], in_=P[:, 1536 + 128:1536 + 132]), r=['ps3'], w=['wis%d' % sl])
            yield
            s.op('act', lambda e: e.activation(out=ckf[:], in_=P[:, 1536:1536 + 128], func=AF.Copy), r=['ps3'], w=['ckf'])
            s.op('dve', lambda e: e.tensor_tensor(out=sq[:], in0=ckf[:], in1=ckf[:], op=ALU.mult), r=['ckf'], w=['sq'])
            s.op('dve', lambda e: e.reduce_sum(out=rs[:, 0:1], in_=sq[:], axis=AX.X), r=['sq'], w=['rs0'])
            s.op('dve', lambda e: e.tensor_scalar(out=rs[:, 1:2], in0=rs[:, 0:1], scalar1=1.0 / 128, scalar2=RMS_EPS, op0=ALU.mult, op1=ALU.add),
                 r=['rs0'], w=['rs1'])
            s.op('pool', lambda e: e.tensor_tensor(out=rs[:, 3:4], in0=rs[:, 1:2], in1=g.mhalf[:, 0:1], op=ALU.pow), r=['rs1', 'mhalf'], w=['rs3'])
            s.op('dve', lambda e: e.scalar_tensor_tensor(out=ckf[:], in0=ckf[:], scalar=rs[:, 3:4], in1=kvn_bc[:],
                                                         op0=ALU.mult, op1=ALU.mult), r=['ckf', 'rs3', 'kvn_bc'], w=['ckf'])
            s.op('act', lambda e: e.activation(out=ckvn_all[:, qb, :], in_=ckf[:], func=AF.Copy), r=['ckf'], w=['ckvn%d' % qb])
            s.op('pe', lambda e: e.transpose(P[:, 1536 + 256:1536 + 384], ckf[:], g.ident[:]), r=['ckf', 'ident'], w=['ps3'])
            s.op('act', lambda e: e.activation(out=ckvnT_all[:, t0:t0 + 128], in_=P[:, 1536 + 256:1536 + 384], func=AF.Copy), r=['ps3'], w=['ckvnT%d' % qb])
            yield
            for gi in range(4):
                win_ = 2 << gi
                U = ut[:, gi, :]
                s.op('dve', lambda e, U=U: e.tensor_tensor(out=ta[:, 1:143], in0=U[:, 1:143], in1=U[:, 0:142], op=ALU.add), r=['ut'], w=['ta'])
                sw = ta
                swn = 'ta'
                if gi >= 1:
                    s.op('dve', lambda e: e.tensor_tensor(out=tb[:, 3:143], in0=ta[:, 3:143], in1=ta[:, 1:141], op=ALU.add), r=['ta'], w=['tb'])
                    sw, swn = tb, 'tb'
                if gi >= 2:
                    s.op('dve', lambda e: e.tensor_tensor(out=ta[:, 7:143], in0=tb[:, 7:143], in1=tb[:, 3:139], op=ALU.add), r=['tb'], w=['ta'])
                    sw, swn = ta, 'ta'
                if gi >= 3:
                    s.op('dve', lambda e: e.tensor_tensor(out=tb[:, 15:143], in0=ta[:, 15:143], in1=ta[:, 7:135], op=ALU.add), r=['ta'], w=['tb'])
                    sw, swn = tb, 'tb'
                if qb == 0:
                    s.op('dve', lambda e, sw=sw, gi=gi, win_=win_: e.tensor_tensor(out=sw[:, 15:15 + win_ - 1], in0=sw[:, 15:15 + win_ - 1],
                                                                                 in1=corr[:, gi, 0:win_ - 1], op=ALU.mult),
                         r=[swn, 'corr'], w=[swn])
                s.op('dve', lambda e, sw=sw, gi=gi, win_=win_, U=U: e.scalar_tensor_tensor(out=dT[:, gi, :], in0=sw[:, 15:143], scalar=1.0 / win_,
                                                                                          in1=U[:, 15:143], op0=ALU.mult, op1=ALU.subtract),
                     r=[swn, 'ut'], w=['dT'])
                yield
            s.op('pool', lambda e: e.tensor_copy(out=ut[:, :, 0:15], in_=ut[:, :, 128:143]), r=['ut'], w=['ut'])
            for gi in range(4):
                s.op('pe', lambda e, gi=gi: e.matmul(P[:, gi * 128:(gi + 1) * 128], lhsT=poolw[:, gi, :], rhs=dT[:, gi, :], start=True, stop=True),
                     r=['poolw', 'dT'], w=['ps0'])
            for gi in range(4):
                s.op('act', lambda e, gi=gi: e.activation(out=yinT[s3][:, gi * 128:(gi + 1) * 128], in_=P[:, gi * 128:(gi + 1) * 128],
                                                          func=AF.Identity, scale=pscale[:, gi:gi + 1]),
                     r=['ps0', 'pscale'], w=['yinT%d' % s3])
            yield
            for h in range(8):
                po = (h % 2) * 64
                bank = 1 + h % 2
                off = bank * 512 + (h // 2) * 128
                s.op('pe', lambda e, h=h, po=po, off=off: e.matmul(P[:, off:off + 128], lhsT=ukT[po:po + 64, h // 2, :],
                                                                  rhs=qT[po:po + 64, (h // 2) * 128:(h // 2 + 1) * 128], start=True, stop=True),
                     r=['ukT', 'qT'], w=['ps%d' % bank])
            for j in range(2):
                s.op('act', lambda e, j=j: e.activation(out=qlT[s3][:, j * 512:(j + 1) * 512], in_=P[:, (1 + j) * 512:(2 + j) * 512], func=AF.Copy),
                     r=['ps%d' % (1 + j)], w=['qlT%d' % s3])
            yield
            Wn = 'W%d' % sl
            cnt = 0
            chunks = []
            k0_ = 0
            while k0_ < nk:
                rem = nk - k0_
                w0 = 512 if rem >= 512 else (256 if rem >= 256 else 128)
                chunks.append((k0_, w0))
                k0_ += w0
            for (k0, w_) in chunks:
                for h in range(4):
                    po = (h % 2) * 64
                    bank = 2 + cnt % 2
                    rb = rbuf[cnt % 2]
                    rbn = 'rbuf%d' % (cnt % 2)
                    cnt += 1
                    s.op('pe', lambda e, h=h, po=po, bank=bank, k0=k0, w_=w_: e.matmul(P[:, bank * 512:bank * 512 + w_],
                                                                                     lhsT=qiT[sl][po:po + 64, (h // 2) * 128:(h // 2 + 1) * 128],
                                                                                     rhs=kiT_all[po:po + 64, k0:k0 + w_], start=True, stop=True),
                         r=['qiT%d' % sl] + ['kiT%d' % kb_ for kb_ in range(k0 // 128, (k0 + w_) // 128)], w=['ps%d' % bank])
                    s.op('act', lambda e, bank=bank, rb=rb, w_=w_: e.activation(out=rb[:, 0:w_], in_=P[:, bank * 512:bank * 512 + w_], func=AF.Relu),
                         r=['ps%d' % bank], w=[rbn])
                    if h == 0:
                        s.op('dve', lambda e, rb=rb, k0=k0, w_=w_: e.tensor_scalar(out=W[sl][:, k0:k0 + w_], in0=rb[:, 0:w_], scalar1=wis[sl][:, 0:1],
                                                                                 scalar2=None, op0=ALU.mult), r=[rbn, 'wis%d' % sl], w=[Wn])
                    else:
                        s.op('dve', lambda e, rb=rb, k0=k0, w_=w_, h=h: e.scalar_tensor_tensor(out=W[sl][:, k0:k0 + w_], in0=rb[:, 0:w_], scalar=wis[sl][:, h:h + 1],
                                                                                            in1=W[sl][:, k0:k0 + w_], op0=ALU.mult, op1=ALU.add),
                             r=[rbn, 'wis%d' % sl, Wn], w=[Wn])
                    yield

        def topk(qb):
            sl = qb % 2
            nk = qb * 128 + 128
            Wn = 'W%d' % sl
            nmn = 'notm%d' % sl
            Wt = W[sl]
            if nk <= 256:
                s.op('dve', lambda e: e.memset(notm[sl][:, 0:nk], 0.0), w=[nmn])
            elif nk <= TOPK_EXACT_NK:
                s.op('dve', lambda e: e.memset(Wt[0:64, nk - 64:nk], NEG), r=[Wn], w=[Wn])
                for it in range(32):
                    s.op('dve', lambda e: e.max(out=m8[:], in_=Wt[:, 0:nk]), r=[Wn], w=['m8'])
                    s.op('dve', lambda e: e.match_replace(out=Wt[:, 0:nk], in_to_replace=m8[:], in_values=Wt[:, 0:nk], imm_value=REP),
                         r=[Wn, 'm8'], w=[Wn])
                    yield
                s.op('dve', lambda e: e.tensor_scalar(out=notm[sl][:, 0:nk], in0=Wt[:, 0:nk], scalar1=0.5 * REP, scalar2=None, op0=ALU.is_gt),
                     r=[Wn], w=[nmn])
            else:
                s.op('dve', lambda e: e.tensor_reduce(out=bs[:, 7:8], in_=Wt[:, 0:nk], axis=AX.X, op=ALU.max, apply_absolute_value=True),
                     r=[Wn], w=['bs7'])
                s.op('dve', lambda e: e.memset(Wt[0:64, nk - 64:nk], NEG), r=[Wn], w=[Wn])
                s.op('dve', lambda e: e.tensor_scalar(out=bs[:, 0:1], in0=bs[:, 7:8], scalar1=-1.001, scalar2=None, op0=ALU.mult), r=['bs7'], w=['bs0'])
                s.op('dve', lambda e: e.tensor_scalar(out=bs[:, 1:2], in0=bs[:, 7:8], scalar1=1.0005, scalar2=None, op0=ALU.mult), r=['bs7'], w=['bs1'])
                s.op('dve', lambda e: e.tensor_scalar(out=bs[:, 2:3], in0=bs[:, 0:1], scalar1=bs[:, 1:2], scalar2=-1.0, op0=ALU.add, op1=ALU.mult),
                     r=['bs0', 'bs1'], w=['bs2'])
                yield
                for it in range(NBISECT):
                    s.op('act', lambda e: e.activation(out=notm[sl][:, 0:nk], in_=Wt[:, 0:nk], func=AF.Sign, bias=bs[:, 2:3], scale=1.0, accum_out=bs[:, 3:4]),
                         r=[Wn, 'bs2'], w=[nmn, 'bs3'])
                    s.op('dve', lambda e: e.tensor_scalar(out=bs[:, 4:5], in0=bs[:, 3:4], scalar1=float(512 - nk), scalar2=None, op0=ALU.is_ge), r=['bs3'], w=['bs4'])
                    s.op('dve', lambda e: e.scalar_tensor_tensor(out=bs[:, 0:1], in0=bs[:, 4:5], scalar=bs[:, 1:2], in1=bs[:, 0:1], op0=ALU.mult, op1=ALU.add),
                         r=['bs4', 'bs1', 'bs0'], w=['bs0'])
                    s.op('dve', lambda e: e.tensor_scalar(out=bs[:, 1:2], in0=bs[:, 1:2], scalar1=0.5, scalar2=None, op0=ALU.mult), r=['bs1', 'bs0'], w=['bs1'])
                    s.op('dve', lambda e: e.tensor_scalar(out=bs[:, 2:3], in0=bs[:, 0:1], scalar1=bs[:, 1:2], scalar2=-1.0, op0=ALU.add, op1=ALU.mult),
                         r=['bs0', 'bs1'], w=['bs2'])
                    yield
                s.op('dve', lambda e: e.scalar_tensor_tensor(out=bs[:, 7:8], in0=bs[:, 1:2], scalar=2.0, in1=bs[:, 0:1], op0=ALU.mult, op1=ALU.add),
                     r=['bs1', 'bs0'], w=['bs7'])
                s.op('dve', lambda e: e.tensor_scalar(out=bs[:, 6:7], in0=bs[:, 7:8], scalar1=-1.0, scalar2=None, op0=ALU.mult), r=['bs7'], w=['bs6'])
                s.op('act', lambda e: e.activation(out=notm[sl][:, 0:nk], in_=Wt[:, 0:nk], func=AF.Sign, bias=bs[:, 6:7], scale=1.0, accum_out=bs[:, 3:4]),
                     r=[Wn, 'bs6'], w=[nmn, 'bs3'])
                s.op('dve', lambda e: e.tensor_scalar(out=bs[:, 5:6], in0=bs[:, 3:4], scalar1=-0.5, scalar2=float(256 - nk // 2), op0=ALU.mult, op1=ALU.add),
                     r=['bs3'], w=['bs5'])
                yield
                s.op('dve', lambda e: e.tensor_scalar(out=notm[sl][:, 0:nk], in0=Wt[:, 0:nk], scalar1=bs[:, 7:8], scalar2=None, op0=ALU.is_gt),
                     r=[Wn, 'bs7'], w=[nmn])
                s.op('dve', lambda e: e.tensor_scalar(out=cb[:, 0:nk], in0=Wt[:, 0:nk], scalar1=bs[:, 0:1], scalar2=None, op0=ALU.is_gt),
                     r=[Wn, 'bs0'], w=['cb'])
                yield
                s.op('dve', lambda e: e.scalar_tensor_tensor(out=cb[:, 0:nk], in0=Wt[:, 0:nk], scalar=bs[:, 7:8], in1=cb[:, 0:nk], op0=ALU.is_le, op1=ALU.mult),
                     r=[Wn, 'bs7', 'cb'], w=['cb'])
                yield
                s.op('dve', lambda e: e.tensor_tensor_scan(out=Wt[:, 0:nk], data0=g.onesb[:, 0:1].to_broadcast([128, nk]), data1=cb[:, 0:nk], initial=0.0, op0=ALU.mult, op1=ALU.add),
                     r=['onesb', 'cb', Wn], w=[Wn])
                yield
                s.op('dve', lambda e: e.scalar_tensor_tensor(out=cb[:, 0:nk], in0=Wt[:, 0:nk], scalar=bs[:, 5:6], in1=cb[:, 0:nk], op0=ALU.is_le, op1=ALU.mult),
                     r=[Wn, 'bs5', 'cb'], w=['cb'])
                yield
                s.op('dve', lambda e: e.tensor_tensor(out=notm[sl][:, 0:nk], in0=notm[sl][:, 0:nk], in1=cb[:, 0:nk], op=ALU.add), r=[nmn, 'cb'], w=[nmn])
                s.op('dve', lambda e: e.tensor_scalar(out=notm[sl][:, 0:nk], in0=notm[sl][:, 0:nk], scalar1=-1.0, scalar2=1.0, op0=ALU.mult, op1=ALU.add),
                     r=[nmn], w=[nmn])
            s.op('dve', lambda e: e.memset(notm[sl][0:64, nk - 64:nk], 1.0), r=[nmn], w=[nmn])
            yield

        def back(qb):
            sl = qb % 2
            s3 = qb % 3
            t0 = qb * 128
            cnt = 0
            for hg in range(2):
                for kb in range(qb + 1):
                    j = cnt % 2
                    cnt += 1
                    bank = 4 + j
                    s.op('pe', lambda e, bank=bank, kb=kb, hg=hg: e.matmul(P[:, bank * 512:(bank + 1) * 512], lhsT=ckvnT_all[:, kb * 128:(kb + 1) * 128],
                                                                          rhs=qlT[s3][:, hg * 512:(hg + 1) * 512], start=True, stop=False),
                         r=['ckvnT%d' % kb, 'qlT%d' % s3], w=['ps%d' % bank], inc=False)
                    s.op('pe', lambda e, bank=bank, kb=kb: e.matmul(P[:, bank * 512:(bank + 1) * 512], lhsT=notm[sl][:, kb * 128:(kb + 1) * 128],
                                                                   rhs=negI4[:], start=False, stop=True),
                         r=['notm%d' % sl, 'negI4'], w=['ps%d' % bank])
                    s.op('act', lambda e, bank=bank, j=j: e.activation(out=pT[j][:], in_=P[:, bank * 512:(bank + 1) * 512], func=AF.Exp, scale=0.125),
                         r=['ps%d' % bank], w=['pT%d' % j])
                    s.op('pe', lambda e, kb=kb, j=j: e.matmul(P[:, 6 * 512:7 * 512], lhsT=ckvn_all[:, kb, :], rhs=pT[j][:], start=(kb == 0), stop=(kb == qb)),
                         r=['ckvn%d' % kb, 'pT%d' % j], w=['ps6'], inc=False)
                    s.op('pe', lambda e, kb=kb, j=j: e.matmul(P[:, 7 * 512:8 * 512], lhsT=g.onesb[:], rhs=pT[j][:], start=(kb == 0), stop=(kb == qb)),
                         r=['onesb', 'pT%d' % j], w=['ps7'])
                    yield
                s.op('dve', lambda e: e.reciprocal(out=rden[:], in_=P[:, 7 * 512:8 * 512]), r=['ps7'], w=['rden'])
                s.op('dve', lambda e, hg=hg: e.tensor_tensor(out=oTn[:, hg * 512:(hg + 1) * 512], in0=P[:, 6 * 512:7 * 512], in1=rden[:], op=ALU.mult),
                     r=['ps6', 'rden'], w=['oTn'])
                yield
            for hp in range(4):
                for h2 in range(2):
                    h = 2 * hp + h2
                    s.op('pe', lambda e, hp=hp, h=h, h2=h2: e.matmul(P[:, 7 * 512 + hp * 128:7 * 512 + (hp + 1) * 128], lhsT=uvpad[:, h, :],
                                                                    rhs=oTn[:, (h // 2 + 4 * (h % 2)) * 128:(h // 2 + 4 * (h % 2) + 1) * 128], start=(h2 == 0), stop=(h2 == 1)),
                         r=['uvpad', 'oTn'], w=['ps7'], inc=(h2 == 1))
            s.op('act', lambda e: e.activation(out=yinT[s3][:, 512:1024], in_=P[:, 7 * 512:8 * 512], func=AF.Copy), r=['ps7'], w=['yinT%d' % s3])
            yield
            for half in range(2):
                b = 4 + half
                for kc in range(8):
                    s.op('pe', lambda e, kc=kc, half=half, b=b: e.matmul(P[:, b * 512:(b + 1) * 512], lhsT=yinT[s3][:, kc * 128:(kc + 1) * 128],
                                                                        rhs=wout[:, kc, half * 512:(half + 1) * 512], start=(kc == 0), stop=(kc == 7)),
                         r=['yinT%d' % s3, 'wout'], w=['ps%d' % b], inc=(kc == 7))
            yield
            emit_epilogue(g, L, (4, 5), xin[s3], 'xin%d' % s3, zb, xo[sl], 'xo%d' % sl, dst[t0:t0 + 128, :])
            yield

        nblk = g.nblk_dsa
        for it in range(nblk + 2):
            gens = []
            if it < nblk:
                gens.append([front_a(it), 1])
            if 1 <= it <= nblk:
                gens.append([topk(it - 1), 1])
            if it >= 2:
                gens.append([back(it - 2), 1])
            _interleave(gens)
        s.barrier()


def emit_gdn(g, src, dst):
    nc, s = g.nc, g.s
    P = g.ps
    with ExitStack() as les:
        L = Ctx()
        L.es = les
        A = lambda name, shape, dt: _sb(nc, les, name, shape, dt)
        WT = Ctx()

        def pre():
            WT.win = A("gwin", [128, 8, 4096], BF16)
            WT.wout = A("gwout", [128, 8, D], BF16)
            wind_ = g.d_o_w_in[0].rearrange("(kc p) n -> p kc n", p=128)
            for kc in range(8):
                s.dma('pool', WT.win[:, kc, :], wind_[:, kc, 0:4096], w=['gwin'])
            woutd_ = g.d_o_w_out[0].rearrange("(kc p) n -> p kc n", p=128)
            for k2 in range(2):
                s.dma('pool', WT.wout[:, k2 * 4:(k2 + 1) * 4, :], woutd_[:, k2 * 4:(k2 + 1) * 4, :], w=['gwout'])
        emit_mods(g, L, g.d_o_mod_w[0], g.d_o_mod_b[0], g.d_o_ln_g[0], g.d_o_ln_b[0], pre=pre)
        win = WT.win
        wbf = A("gwbf", [128, 8, 16], F32)
        wbb = A("gwbb", [128, 8, 16], BF16)
        wout = WT.wout
        cw = A("gcw", [128, 24, 4], F32)
        msl = A("msl", [128, 128], F32)
        mil = A("mil", [128, 128], F32)
        triu = A("triu", [128, 128], F32)
        mdm = A("mdm", [128, 5, 128], BF16)
        mdmT = A("mdmT", [128, 5, 128], BF16)
        alog = A("alog", [128, 8], F32)
        dtb = A("dtb", [128, 8], F32)
        onb = A("onb", [128, D], F32)
        xin = [A("gxin%d" % i, [128, D], F32) for i in range(1)]
        xrs = A("gxrs", [128, D], F32)
        hT = A("ghT", [128, 8, 128], BF16)
        xw = [A("xw%d" % i, [128, 131], F32) for i in range(2)]
        halo = A("ghalo", [128, 24, 3], F32)
        cbuf = [A("cbuf%d" % i, [128, 128], F32) for i in range(2)]
        act1 = A("gact", [128, 24 * 128], F32)
        vb = [A("gvb%d" % i, [128, 1024], BF16) for i in range(2)]
        sq = A("gsq", [128, 1024], F32)
        rstd = sq
        qkb = [A("qkb%d" % i, [128, 2048], BF16) for i in range(2)]
        gsil = [A("gsil%d" % i, [128, D], BF16) for i in range(3)]
        ba = A("ba", [128, 16], F32)
        tmpa = A("tmpa", [128, 8], F32)
        smb = [A("smb%d" % i, [128, 48], F32) for i in range(2)]
        zsm = A("zsm", [128, 8], F32)
        HP = []
        NTH = 3
        for p in range(NTH):
            h_ = Ctx()
            h_.sm = A("hsm%d" % p, [128, 2], F32)
            for nm in ("E", "EB", "t1", "Nf", "atf"):
                setattr(h_, nm, A("h%s%d" % (nm, p), [128, 128], F32))
            for nm in ("X", "Xt", "Q2", "Q2t", "Q4", "Q4t", "Yv", "Yt", "attT", "qdT", "kd", "Kbg", "Vb", "nwT", "vnew"):
                setattr(h_, nm, A("h%s%d" % (nm, p), [128, 128], BF16))
            h_.NM = A("hNM%d" % p, [128, 5, 128], BF16)
            h_.NMt = A("hNMt%d" % p, [128, 5, 128], BF16)
            HP.append(h_)
        Sf = A("gSf", [128, 8, 128], F32)
        Sb = A("gSb", [128, 8, 128], BF16)
        osb = [A("gosb%d" % i, [128, D], F32) for i in range(2)]
        yinT = A("gyinT", [128, 1024], BF16)
        zb = A("gzb", [128, D], F32)
        xo = [A("gxo%d" % i, [128, D], F32) for i in range(1)]
        wind = g.d_o_w_in[0].rearrange("(kc p) n -> p kc n", p=128)
        s.dma('sp', wbf[:], wind[:, :, 4096:4112], w=['gwbf'])
        s.op('dve', lambda e: e.tensor_copy(out=wbb[:], in_=wbf[:]), r=['gwbf'], w=['gwbb'])
        s.dma('sp', cw[:], g.d_o_cw[:, :, :], w=['gcw'])
        s.dma('sp', msl[:], g.d_msl[:, :], w=['msl'])
        s.dma('sp', mil[:], g.d_mil[:, :], w=['mil'])
        s.dma('sp', triu[:], g.d_triu[:, :], w=['triu'])
        s.dma('pool', mdm[:], g.d_mdm[:, :, :], w=['mdm'])
        s.dma('pool', mdmT[:], g.d_mdmT[:, :, :], w=['mdmT'])
        s.dma('sp', alog[:], g.d_o_a_log[0].partition_broadcast(128), w=['alog'])
        s.dma('sp', dtb[:], g.d_o_dt_bias[0].partition_broadcast(128), w=['dtb'])
        s.dma('sp', onb[:], g.d_onb[0].partition_broadcast(128), w=['onb'])
        s.op('act', lambda e: e.activation(out=alog[:], in_=alog[:], func=AF.Exp), r=['alog'], w=['alog'])
        s.op('dve', lambda e: e.tensor_scalar(out=alog[:], in0=alog[:], scalar1=-1.0, scalar2=None, op0=ALU.mult), r=['alog'], w=['alog'])
        s.op('dve', lambda e: e.memset(halo[:], 0.0), w=['ghalo%d' % i for i in range(24)])
        s.op('dve', lambda e: e.memset(Sf[:], 0.0), w=['gSf0', 'gSf1', 'gSf2'])
        s.op('dve', lambda e: e.memset(Sb[:], 0.0), w=['gSb0', 'gSb1', 'gSb2'])
        DK = float(128 ** -0.5)

        def phaseA(qb):
            t0 = qb * 128
            bp = qb % 2
            x3 = qb % 3
            xn = 'gxin0'
            an = 'gact'
            sn = 'smb%d' % bp
            sm = smb[bp]
            ac = act1
            s.dma('sp', xin[0][:], src[t0:t0 + 128, :], w=[xn])
            emit_hT(g, L, xin[0], xn, lambda kc: hT[:, kc, :], 'ghT', (0, 1))
            yield
            for ch in range(24):
                b = ch % 2
                cb = cbuf[b]
                cn = 'cbuf%d' % b
                for kc in range(8):
                    s.op('pe', lambda e, kc=kc, ch=ch, b=b: e.matmul(P[:, b * 512:b * 512 + 128], lhsT=win[:, kc, ch * 128:(ch + 1) * 128], rhs=hT[:, kc, :],
                                                                    start=(kc == 0), stop=(kc == 7)), r=['gwin', 'ghT'], w=['ps%d' % b], inc=(kc == 7))
                xb = xw[b]
                xbn = 'xw%d' % b
                s.op('pool', lambda e, ch=ch, xb=xb: e.tensor_copy(out=xb[:, 0:3], in_=halo[:, ch, :]), r=['ghalo%d' % ch], w=[xbn])
                s.op('act', lambda e, b=b, xb=xb: e.activation(out=xb[:, 3:131], in_=P[:, b * 512:b * 512 + 128], func=AF.Copy), r=['ps%d' % b], w=[xbn])
                s.op('act', lambda e, ch=ch, b=b, cb=cb: e.activation(out=cb[:], in_=P[:, b * 512:b * 512 + 128], func=AF.Identity, scale=cw[:, ch, 3:4]),
                     r=['ps%d' % b, 'gcw'], w=[cn])
                s.op('pool', lambda e, ch=ch, xb=xb: e.tensor_copy(out=halo[:, ch, :], in_=xb[:, 128:131]), r=[xbn], w=['ghalo%d' % ch])
                for j in range(3):
                    dst_ = cb[:] if j < 2 else ac[:, ch * 128:(ch + 1) * 128]
                    s.op('dve', lambda e, ch=ch, j=j, cb=cb, xb=xb, dst_=dst_: e.scalar_tensor_tensor(out=dst_, in0=xb[:, j:j + 128], scalar=cw[:, ch, j:j + 1], in1=cb[:],
                                                                                                 op0=ALU.mult, op1=ALU.add),
                         r=[xbn, 'gcw', cn], w=([cn] if j < 2 else [an]))
                yield
            s.op('act', lambda e: e.activation(out=ac[:], in_=ac[:], func=AF.Silu), r=[an], w=[an])
            yield
            s.op('act', lambda e: e.activation(out=vb[bp][:], in_=ac[:, 2048:3072], func=AF.Copy), r=[an], w=['gvb%d' % bp])
            yield
            for hf in range(2):
                seg = ac[:, hf * 1024:(hf + 1) * 1024]
                s.op('dve', lambda e, seg=seg: e.tensor_tensor(out=sq[:], in0=seg, in1=seg, op=ALU.mult), r=[an], w=['gsq'])
                yield
                for j in range(2):
                    b = j % 2
                    s.op('pe', lambda e, j=j, b=b: e.matmul(P[:, b * 512:(b + 1) * 512], lhsT=g.ones[:], rhs=sq[:, j * 512:(j + 1) * 512], start=True, stop=True),
                         r=['ones', 'gsq'], w=['ps%d' % b])
                for j in range(2):
                    b = j % 2
                    s.op('dve', lambda e, j=j, b=b: e.tensor_scalar(out=sq[:, j * 512:(j + 1) * 512], in0=P[:, b * 512:(b + 1) * 512], scalar1=RMS_EPS, scalar2=None, op0=ALU.add),
                         r=['ps%d' % b], w=['gsq'])
                yield
                s.op('act', lambda e: e.activation(out=sq[:], in_=sq[:], func=AF.Sqrt), r=['gsq'], w=['gsq'])
                s.op('dve', lambda e: e.reciprocal(out=sq[:], in_=sq[:]), r=['gsq'], w=['gsq'])
                yield
                s.op('dve', lambda e, seg=seg: e.tensor_tensor(out=seg, in0=seg, in1=sq[:], op=ALU.mult), r=[an, 'gsq'], w=[an])
                s.op('act', lambda e, seg=seg, hf=hf: e.activation(out=qkb[bp][:, hf * 1024:(hf + 1) * 1024], in_=seg, func=AF.Copy), r=[an], w=['qkb%d' % bp])
                yield
            for half in range(2):
                b = half
                for kc in range(8):
                    s.op('pe', lambda e, kc=kc, half=half, b=b: e.matmul(P[:, b * 512:(b + 1) * 512], lhsT=hT[:, kc, :],
                                                                        rhs=win[:, kc, 3072 + half * 512:3072 + (half + 1) * 512],
                                                                        start=(kc == 0), stop=(kc == 7)), r=['gwin', 'ghT'], w=['ps%d' % b], inc=(kc == 7))
                s.op('act', lambda e, half=half, b=b: e.activation(out=gsil[x3][:, half * 512:(half + 1) * 512], in_=P[:, b * 512:(b + 1) * 512], func=AF.Silu),
                     r=['ps%d' % b], w=['gsil%d' % x3])
                yield
            for kc in range(8):
                s.op('pe', lambda e, kc=kc: e.matmul(P[:, 0:16], lhsT=hT[:, kc, :], rhs=wbb[:, kc, :], start=(kc == 0), stop=(kc == 7)),
                     r=['gwbb', 'ghT'], w=['ps0'], inc=(kc == 7))
            s.op('dve', lambda e: e.tensor_copy(out=ba[:], in_=P[:, 0:16]), r=['ps0'], w=['ba'])
            yield
            s.op('act', lambda e: e.activation(out=sm[:, 0:8], in_=ba[:, 0:8], func=AF.Exp, scale=-1.0), r=['ba'], w=[sn])
            s.op('dve', lambda e: e.tensor_scalar(out=sm[:, 0:8], in0=sm[:, 0:8], scalar1=1.0, scalar2=None, op0=ALU.add), r=[sn], w=[sn])
            s.op('dve', lambda e: e.reciprocal(out=sm[:, 0:8], in_=sm[:, 0:8]), r=[sn], w=[sn])
            s.op('dve', lambda e: e.tensor_scalar(out=sm[:, 32:40], in0=sm[:, 0:8], scalar1=-1.0, scalar2=None, op0=ALU.mult), r=[sn], w=[sn])
            yield
            s.op('dve', lambda e: e.tensor_tensor(out=tmpa[:], in0=ba[:, 8:16], in1=dtb[:], op=ALU.add), r=['ba', 'dtb'], w=['tmpa'])
            s.op('act', lambda e: e.activation(out=tmpa[:], in_=tmpa[:], func=AF.Exp), r=['tmpa'], w=['tmpa'])
            s.op('act', lambda e: e.activation(out=tmpa[:], in_=tmpa[:], func=AF.Ln, bias=1.0), r=['tmpa'], w=['tmpa'])
            s.op('dve', lambda e: e.tensor_tensor(out=sm[:, 8:16], in0=tmpa[:], in1=alog[:], op=ALU.mult), r=['tmpa', 'alog', sn], w=[sn])
            yield
            s.op('pe', lambda e: e.matmul(P[:, 16:24], lhsT=triu[:], rhs=sm[:, 8:16], start=True, stop=True), r=['triu', sn], w=['ps0'])
            s.op('dve', lambda e: e.tensor_copy(out=sm[:, 16:24], in_=P[:, 16:24]), r=['ps0', sn], w=[sn])
            s.op('act', lambda e: e.activation(out=sm[:, 24:32], in_=sm[:, 16:24], func=AF.Exp), r=[sn], w=[sn])
            s.op('dve', lambda e: e.tensor_tensor(out=sm[:, 40:48], in0=sm[:, 24:32], in1=sm[:, 0:8], op=ALU.mult), r=[sn], w=[sn])
            yield

        def heads(qb, p):
            bp = qb % 2
            sm = smb[bp]
            sn = 'smb%d' % bp
            vn = 'gvb%d' % bp
            H = HP[p]
            X0 = (2 + 2 * p) * 512
            Y0 = X0 + 512
            xn_, yn_ = 'ps%d' % (2 + 2 * p), 'ps%d' % (3 + 2 * p)
            zn_ = xn_
            n = lambda nm: 'h%s%d' % (nm, p)
            for h in range(p, 8, NTH):
                vvb = vb[bp][:, h * 128:(h + 1) * 128]
                qnb = qkb[bp][:, h * 128:(h + 1) * 128]
                knb = qkb[bp][:, (8 + h) * 128:(9 + h) * 128]
                qbn = 'qkb%d' % bp
                s.op('dve', lambda e, h=h: e.tensor_scalar(out=H.t1[:], in0=g.ident[:], scalar1=sm[:, 16 + h:17 + h], scalar2=None, op0=ALU.mult),
                     r=['ident', sn], w=[n('t1')])
                s.op('pe', lambda e: e.matmul(P[:, X0:X0 + 128], lhsT=g.ones[:], rhs=H.t1[:], start=True, stop=True), r=['ones', n('t1')], w=[xn_])
                s.op('dve', lambda e, h=h: e.tensor_scalar(out=H.E[:], in0=P[:, X0:X0 + 128], scalar1=sm[:, 16 + h:17 + h], scalar2=0.0, op0=ALU.subtract, op1=ALU.max),
                     r=[xn_, sn], w=[n('E')])
                s.op('act', lambda e: e.activation(out=H.E[:], in_=H.E[:], func=AF.Exp, scale=-1.0), r=[n('E')], w=[n('E')])
                s.op('act', lambda e: e.activation(out=H.EB[:], in_=P[:, X0:X0 + 128], func=AF.Exp), r=[xn_], w=[n('EB')])
                s.op('act', lambda e: e.activation(out=H.sm[:, 0:1], in_=P[:, X0 + 127:X0 + 128], func=AF.Exp), r=[xn_], w=[n('sm0')])
                s.op('dve', lambda e, h=h: e.tensor_scalar(out=H.sm[:, 1:2], in0=P[:, X0 + 127:X0 + 128], scalar1=sm[:, 16 + h:17 + h], scalar2=None, op0=ALU.subtract),
                     r=[xn_, sn], w=[n('sm1')])
                s.op('act', lambda e: e.activation(out=H.sm[:, 1:2], in_=H.sm[:, 1:2], func=AF.Exp), r=[n('sm1')], w=[n('sm1')])
                yield
                s.op('pe', lambda e, knb=knb: e.matmul(P[:, X0 + 128:X0 + 256], lhsT=knb, rhs=knb, start=True, stop=True), r=[qbn], w=[xn_])
                s.op('pe', lambda e, knb=knb, qnb=qnb: e.matmul(P[:, X0 + 256:X0 + 384], lhsT=qnb, rhs=knb, start=True, stop=True), r=[qbn], w=[xn_])
                s.op('dve', lambda e: e.tensor_tensor(out=H.t1[:], in0=H.E[:], in1=msl[:], op=ALU.mult), r=[n('E'), 'msl'], w=[n('t1')])
                s.op('dve', lambda e, h=h: e.scalar_tensor_tensor(out=H.Nf[:], in0=P[:, X0 + 128:X0 + 256], scalar=sm[:, 32 + h:33 + h], in1=H.t1[:], op0=ALU.mult, op1=ALU.mult),
                     r=[xn_, sn, n('t1')], w=[n('Nf')])
                s.op('dve', lambda e: e.tensor_tensor(out=H.t1[:], in0=H.E[:], in1=mil[:], op=ALU.mult), r=[n('E'), 'mil', n('Nf')], w=[n('t1')])
                s.op('dve', lambda e: e.scalar_tensor_tensor(out=H.atf[:], in0=P[:, X0 + 256:X0 + 384], scalar=DK, in1=H.t1[:], op0=ALU.mult, op1=ALU.mult),
                     r=[xn_, n('t1')], w=[n('atf')])
                yield
                s.op('pe', lambda e: e.transpose(P[:, Y0:Y0 + 128], H.Nf[:], g.ident[:]), r=[n('Nf'), 'ident'], w=[yn_])
                s.op('pe', lambda e: e.transpose(P[:, Y0 + 128:Y0 + 256], H.atf[:], g.ident[:]), r=[n('atf'), 'ident'], w=[yn_])
                s.op('pe', lambda e, knb=knb: e.matmul(P[:, Y0 + 256:Y0 + 384], lhsT=knb, rhs=g.identb[:], start=True, stop=True), r=[qbn, 'identb'], w=[yn_])
                s.op('pe', lambda e, vvb=vvb: e.matmul(P[:, Y0 + 384:Y0 + 512], lhsT=vvb, rhs=g.identb[:], start=True, stop=True), r=[vn, 'identb'], w=[yn_])
                s.op('dve', lambda e: e.tensor_tensor(out=H.NM[:], in0=H.Nf[:].unsqueeze(1).to_broadcast([128, 5, 128]), in1=mdm[:], op=ALU.mult),
                     r=[n('Nf'), 'mdm'], w=[n('NM')])
                s.op('dve', lambda e: e.tensor_tensor(out=H.NMt[:], in0=P[:, Y0:Y0 + 128].unsqueeze(1).to_broadcast([128, 5, 128]), in1=mdmT[:], op=ALU.mult),
                     r=[yn_, 'mdmT'], w=[n('NMt')])
                s.op('act', lambda e: e.activation(out=H.attT[:], in_=P[:, Y0 + 128:Y0 + 256], func=AF.Copy), r=[yn_], w=[n('attT')])
                s.op('dve', lambda e: e.tensor_scalar(out=H.kd[:], in0=P[:, Y0 + 256:Y0 + 384], scalar1=H.sm[:, 1:2], scalar2=None, op0=ALU.mult), r=[yn_, n('sm1')], w=[n('kd')])
                s.op('dve', lambda e, h=h: e.tensor_scalar(out=H.Kbg[:], in0=P[:, Y0 + 256:Y0 + 384], scalar1=sm[:, 40 + h:41 + h], scalar2=None, op0=ALU.mult),
                     r=[yn_, sn], w=[n('Kbg')])
                s.op('dve', lambda e, h=h: e.tensor_scalar(out=H.Vb[:], in0=P[:, Y0 + 384:Y0 + 512], scalar1=sm[:, h:h + 1], scalar2=None, op0=ALU.mult), r=[yn_, sn], w=[n('Vb')])
                s.op('dve', lambda e, qnb=qnb: e.scalar_tensor_tensor(out=H.qdT[:], in0=qnb, scalar=DK, in1=H.EB[:], op0=ALU.mult, op1=ALU.mult),
                     r=[qbn, n('EB')], w=[n('qdT')])
                yield
                zc = [0]

                def mm(lhsT, rhs, rn):
                    c0 = X0 + (zc[0] % 3) * 128
                    zc[0] += 1
                    s.op('pe', lambda e: e.matmul(P[:, c0:c0 + 128], lhsT=lhsT, rhs=rhs, start=True, stop=True), r=rn, w=[zn_])
                    return P[:, c0:c0 + 128]

                def cp(dst, dn, src_ps):
                    s.op('act', lambda e: e.activation(out=dst, in_=src_ps, func=AF.Copy), r=[zn_], w=[dn])

                def acc(dst, dn, src_ps):
                    s.op('dve', lambda e: e.tensor_tensor(out=dst, in0=src_ps, in1=dst, op=ALU.add), r=[zn_, dn], w=[dn])

                M0, M0t = H.NM[:, 0, :], H.NMt[:, 0, :]
                s.op('dve', lambda e: e.tensor_tensor(out=H.X[:], in0=M0, in1=g.identb[:], op=ALU.add), r=[n('NM'), 'identb'], w=[n('X')])
                s.op('dve', lambda e: e.tensor_tensor(out=H.Xt[:], in0=M0t, in1=g.identb[:], op=ALU.add), r=[n('NMt'), 'identb'], w=[n('Xt')])
                cp(H.Q2[:], n('Q2'), mm(M0t, M0, [n('NM'), n('NMt')]))
                cp(H.Q2t[:], n('Q2t'), mm(M0, M0t, [n('NM'), n('NMt')]))
                yield
                acc(H.X[:], n('X'), mm(H.Q2t[:], H.X[:], [n('Q2t'), n('X')]))
                acc(H.Xt[:], n('Xt'), mm(H.Q2[:], H.Xt[:], [n('Q2'), n('Xt')]))
                cp(H.Q4[:], n('Q4'), mm(H.Q2t[:], H.Q2[:], [n('Q2'), n('Q2t')]))
                cp(H.Q4t[:], n('Q4t'), mm(H.Q2[:], H.Q2t[:], [n('Q2'), n('Q2t')]))
                yield
                acc(H.X[:], n('X'), mm(H.Q4t[:], H.X[:], [n('Q4t'), n('X')]))
                acc(H.Xt[:], n('Xt'), mm(H.Q4[:], H.Xt[:], [n('Q4'), n('Xt')]))
                yield
                for lv in range(1, 5):
                    Nb, Nbt = H.NM[:, lv, :], H.NMt[:, lv, :]
                    if lv < 4:
                        cp(H.Yv[:], n('Yv'), mm(Nbt, H.X[:], [n('NMt'), n('X')]))
                    cp(H.Yt[:], n('Yt'), mm(Nb, H.Xt[:], [n('NM'), n('Xt')]))
                    yield
                    pa = mm(H.Xt[:], H.Yv[:], [n('Xt'), n('Yv')]) if lv < 4 else None
                    pb = mm(H.X[:], H.Yt[:], [n('X'), n('Yt')])
                    if lv < 4:
                        acc(H.X[:], n('X'), pa)
                    acc(H.Xt[:], n('Xt'), pb)
                    yield
                pw = mm(H.Kbg[:], H.Xt[:], [n('Kbg'), n('Xt')])
                s.op('act', lambda e: e.activation(out=H.nwT[:], in_=pw, func=AF.Copy, scale=-1.0), r=[zn_], w=[n('nwT')])
                yield
                s.op('pe', lambda e: e.matmul(P[:, Y0:Y0 + 128], lhsT=H.Xt[:], rhs=H.Vb[:], start=True, stop=False), r=[n('Xt'), n('Vb')], w=[yn_], inc=False)
                s.op('pe', lambda e, h=h: e.matmul(P[:, Y0:Y0 + 128], lhsT=H.nwT[:], rhs=Sb[:, h, :], start=False, stop=True), r=[n('nwT'), 'gSb%d' % p], w=[yn_])
                s.op('act', lambda e: e.activation(out=H.vnew[:], in_=P[:, Y0:Y0 + 128], func=AF.Copy), r=[yn_], w=[n('vnew')])
                yield
                s.op('pe', lambda e, h=h: e.matmul(P[:, X0 + 384:X0 + 512], lhsT=H.qdT[:], rhs=Sb[:, h, :], start=True, stop=False),
                     r=[n('qdT'), 'gSb%d' % p], w=[xn_], inc=False)
                s.op('pe', lambda e: e.matmul(P[:, X0 + 384:X0 + 512], lhsT=H.attT[:], rhs=H.vnew[:], start=False, stop=True),
                     r=[n('attT'), n('vnew')], w=[xn_])
                s.op('act', lambda e, h=h: e.activation(out=osb[bp][:, h * 128:(h + 1) * 128], in_=P[:, X0 + 384:X0 + 512], func=AF.Copy),
                     r=[xn_], w=['gosb%d_%d' % (bp, p)])
                s.op('pe', lambda e: e.matmul(P[:, Y0 + 128:Y0 + 256], lhsT=H.kd[:], rhs=H.vnew[:], start=True, stop=True), r=[n('kd'), n('vnew')], w=[yn_])
                s.op('dve', lambda e, h=h: e.scalar_tensor_tensor(out=Sf[:, h, :], in0=Sf[:, h, :], scalar=H.sm[:, 0:1], in1=P[:, Y0 + 128:Y0 + 256], op0=ALU.mult, op1=ALU.add),
                     r=['gSf%d' % p, n('sm0'), yn_], w=['gSf%d' % p])
                s.op('act', lambda e, h=h: e.activation(out=Sb[:, h, :], in_=Sf[:, h, :], func=AF.Copy), r=['gSf%d' % p], w=['gSb%d' % p])
                yield

        def phaseZ(qb):
            t0 = qb * 128
            bp = qb % 2
            x3 = qb % 3
            ob = osb[bp]
            on = ['gosb%d_%d' % (bp, p_) for p_ in range(NTH)]
            s.op('dve', lambda e: e.tensor_tensor(out=zb[:], in0=ob[:], in1=ob[:], op=ALU.mult), r=on, w=['zb'])
            for h in range(8):
                s.op('dve', lambda e, h=h: e.reduce_sum(out=zsm[:, h:h + 1], in_=zb[:, h * 128:(h + 1) * 128], axis=AX.X), r=['zb'], w=['zsm'])
            yield
            s.op('dve', lambda e: e.tensor_scalar(out=zsm[:], in0=zsm[:], scalar1=1.0 / 128, scalar2=RMS_EPS, op0=ALU.mult, op1=ALU.add), r=['zsm'], w=['zsm'])
            s.op('pool', lambda e: e.tensor_tensor(out=zsm[:], in0=zsm[:], in1=g.mhalf[:], op=ALU.pow), r=['zsm', 'mhalf'], w=['zsm'])
            yield
            for h in range(8):
                s.op('dve', lambda e, h=h: e.tensor_scalar(out=ob[:, h * 128:(h + 1) * 128], in0=ob[:, h * 128:(h + 1) * 128], scalar1=zsm[:, h:h + 1],
                                                           scalar2=None, op0=ALU.mult), r=on + ['zsm'], w=on)
            yield
            s.op('dve', lambda e: e.tensor_tensor(out=ob[:], in0=ob[:], in1=onb[:], op=ALU.mult), r=on + ['onb'], w=on)
            s.op('dve', lambda e: e.tensor_tensor(out=ob[:], in0=ob[:], in1=gsil[x3][:], op=ALU.mult), r=on + ['gsil%d' % x3], w=on)
            yield
            for kc in range(8):
                b = kc // 4
                off = b * 512 + (kc % 4) * 128
                s.op('pe', lambda e, kc=kc, off=off: e.transpose(P[:, off:off + 128], ob[:, kc * 128:(kc + 1) * 128], g.ident[:]), r=on + ['ident'], w=['ps%d' % b])
            for j in range(2):
                s.op('act', lambda e, j=j: e.activation(out=yinT[:, j * 512:(j + 1) * 512], in_=P[:, j * 512:(j + 1) * 512], func=AF.Copy),
                     r=['ps%d' % j], w=['gyinT'])
            yield
            for half in range(2):
                b = half
                for kc in range(8):
                    s.op('pe', lambda e, kc=kc, half=half, b=b: e.matmul(P[:, b * 512:(b + 1) * 512], lhsT=yinT[:, kc * 128:(kc + 1) * 128],
                                                                        rhs=wout[:, kc, half * 512:(half + 1) * 512], start=(kc == 0), stop=(kc == 7)),
                         r=['gyinT', 'gwout'], w=['ps%d' % b], inc=(kc == 7))
            s.dma('sp', xrs[:], src[t0:t0 + 128, :], w=['gxrs'])
            emit_epilogue(g, L, (0, 1), xrs, 'gxrs', zb, xo[0], 'gxo0', dst[t0:t0 + 128, :])
            yield

        nblk = g.nblk_gdn
        for it in range(nblk + 2):
            gens = []
            if 1 <= it <= nblk:
                for p_ in range(NTH):
                    gens.append([heads(it - 1, p_), 1])
            if it < nblk:
                gens.append([phaseA(it), 1])
            if it >= 2:
                gens.append([phaseZ(it - 2), 1])
            _interleave(gens)
        s.barrier()


W_SPECS = [
    ("e_mod_w", [1, D, 3 * D]), ("e_mod_b", [1, 1, 3 * D]), ("e_ln_g", [1, D]), ("e_ln_b", [1, D]),
    ("o_mod_w", [1, D, 3 * D]), ("o_mod_b", [1, 1, 3 * D]), ("o_ln_g", [1, D]), ("o_ln_b", [1, D]),
    ("f_mod_w", [2, D, 3 * D]), ("f_mod_b", [2, 1, 3 * D]), ("f_ln_g", [2, D]), ("f_ln_b", [2, D]),
    ("f_w_up", [2, D, 2 * DFF]), ("f_w_down", [2, DFF, D]), ("f_cw", [2, 128, 2 * NFC, 4]),
    ("ident", [128, 128]),
    ("e_w_in", [1, D, 1476]), ("e_pool_w", [1, 4, 128, 128]), ("pscale", [128, 4]), ("e_kv_norm", [1, 128]),
    ("o_w_in", [1, D, 4112]), ("o_w_out", [1, D, D]), ("o_cw", [128, 24, 4]), ("msl", [128, 128]), ("mil", [128, 128]), ("triu", [128, 128]), ("mdm", [128, 5, 128]), ("mdmT", [128, 5, 128]),
    ("o_a_log", [1, 8]), ("o_dt_bias", [1, 8]), ("onb", [1, D]),
    ("ukT", [128, 4, 128]), ("uvpad", [128, 8, 128]), ("e_w_out", [1, D, D]), ("negI4", [128, 512]), ("corr", [128, 4, 15]),
]


def build(stages):
    nc = bass.Bass("TRN2", target_bir_lowering=False)
    g = Ctx()
    g.nc = nc
    g.d_x = nc.dram_tensor("x", [S, D], F32, kind="ExternalInput").ap()
    g.d_ccol = nc.dram_tensor("ccol", [128, 8], F32, kind="ExternalInput").ap()
    for nm, shp in W_SPECS:
        setattr(g, "d_" + nm, nc.dram_tensor(nm, shp, F32, kind="ExternalInput").ap())
    g.d_out = nc.dram_tensor("out", [S, D], F32, kind="ExternalOutput").ap()
    scr = [nc.dram_tensor("xscr%d" % i, [S, D], F32, kind="Internal").ap() for i in range(3)]
    with ExitStack() as es:
        g.es = es
        g.s = Sch(nc, es)
        g.ps = es.enter_context(nc.psum_tensor("ps", [128, 8 * 512], F32))
        g.lnst = _sb(nc, es, "lnst", [128, 12], F32)
        g.lnmv = _sb(nc, es, "lnmv", [128, 8], F32)
        emit_consts(g)
        bufs = [g.d_x] + scr
        n = len(stages)
        for i, st in enumerate(stages):
            src = g.d_x if i == 0 else scr[(i - 1) % 3]
            dst = g.d_out if i == n - 1 else scr[i % 3]
            if st[0] == 'ffn':
                emit_ffn(g, st[1], src, dst)
            elif st[0] == 'gdn':
                g.nblk_gdn = st[1] if len(st) > 1 else NB
                emit_gdn(g, src, dst)
            elif st[0] == 'dsa':
                g.nblk_dsa = st[1] if len(st) > 1 else NB
                emit_dsa(g, src, dst)
            else:
                raise ValueError(st)
        g.s.finish()
    return nc


def prep_weights(inp):
    f = lambda a: np.ascontiguousarray(np.asarray(a, dtype=np.float32))
    w = {}
    for k in ("e_mod_w", "e_ln_g", "e_ln_b", "o_mod_w", "o_ln_g", "o_ln_b", "f_mod_w", "f_ln_g", "f_ln_b", "f_w_up", "f_w_down"):
        w[k] = f(inp[k])
    for k in ("e_mod_b", "o_mod_b", "f_mod_b"):
        a = f(inp[k])
        w[k] = np.ascontiguousarray(a.reshape(a.shape[0], 1, 3 * D))
    cwt = f(inp["f_conv_w"])
    cb = f(inp["f_conv_b"])
    a = np.concatenate([cwt, cb[:, None, :]], axis=1)
    a = a.reshape(2, 4, 2 * NFC, 128).transpose(0, 3, 2, 1)
    w["f_cw"] = np.ascontiguousarray(a)
    w["ident"] = np.eye(128, dtype=np.float32)
    for k in ("e_w_in", "e_pool_w", "e_kv_norm", "e_w_out"):
        w[k] = f(inp[k])
    w["pscale"] = np.ascontiguousarray(f(inp["e_pool_scale"])[0].reshape(4, 128).T)
    uk = f(inp["e_w_uk"])[0]
    w["ukT"] = np.ascontiguousarray(uk.reshape(4, 2, 128, 64).transpose(1, 3, 0, 2).reshape(128, 4, 128))
    uv = f(inp["e_w_uv"])[0]
    uvp = np.zeros((128, 8, 128), np.float32)
    for h in range(8):
        uvp[:, h, (h % 2) * 64:(h % 2) * 64 + 64] = uv[h]
    w["uvpad"] = uvp
    w["negI4"] = np.ascontiguousarray(np.tile(-30000.0 * np.eye(128, dtype=np.float32), (1, 4)))
    corr = np.ones((128, 4, 15), np.float32)
    for gi in range(4):
        win_ = 2 << gi
        for t in range(win_ - 1):
            corr[:, gi, t] = win_ / (t + 1.0)
    w["corr"] = corr
    for k in ("o_w_in", "o_w_out", "o_a_log", "o_dt_bias"):
        w[k] = f(inp[k])
    ocw = f(inp["o_conv_w"])[0]
    w["o_cw"] = np.ascontiguousarray(ocw.reshape(4, 24, 128).transpose(2, 1, 0))
    ar = np.arange(128)
    w["msl"] = (ar[:, None] > ar[None, :]).astype(np.float32)
    w["mil"] = (ar[:, None] >= ar[None, :]).astype(np.float32)
    w["triu"] = (ar[:, None] <= ar[None, :]).astype(np.float32)
    w["onb"] = np.ascontiguousarray(np.tile(f(inp["o_out_norm"])[0], 8)[None, :])
    mdm = np.zeros((128, 5, 128), np.float32)
    mdm[:, 0, :] = (ar[:, None] // 8 == ar[None, :] // 8)
    for li, bsz in enumerate((8, 16, 32, 64)):
        bl = ar // bsz
        mdm[:, 1 + li, :] = (bl[:, None] % 2 == 1) & (bl[None, :] == bl[:, None] - 1)
    w["mdm"] = mdm
    w["mdmT"] = np.ascontiguousarray(mdm.transpose(2, 1, 0))
    return w


STAGES = [('dsa',), ('ffn', 0), ('gdn',), ('ffn', 1)]


def kernel(**inp):
    x = np.asarray(inp["x"], dtype=np.float32)
    c = np.asarray(inp["c"], dtype=np.float32)
    w = prep_weights(inp)
    nc = build(STAGES)
    in_maps = []
    for b in range(8):
        m = dict(w)
        m["x"] = np.ascontiguousarray(x[b])
        m["ccol"] = np.ascontiguousarray(c[b].reshape(8, 128).T)
        in_maps.append(m)
    res = run_bass_kernel_spmd(nc, in_maps, core_ids=list(range(8)))
    return np.stack([np.asarray(r["out"], dtype=np.float32) for r in res.results], axis=0)
```

```python
import os
import numpy as np
from contextlib import ExitStack
import concourse.bass as bass
import concourse.mybir as mybir
from concourse.bass_utils import run_bass_kernel_spmd

F32 = mybir.dt.float32
BF16 = mybir.dt.bfloat16
AF = mybir.ActivationFunctionType
ALU = mybir.AluOpType
AX = mybir.AxisListType

D = 1024
S = 4096
NB = S // 128
DFF = 2688
NFC = DFF // 128
ALPHA = float(4 ** 0.25)
LN_EPS = 1e-5
RMS_EPS = 1e-6
NDS = 12


class Sch:
    def __init__(self, nc, es):
        self.nc = nc
        self.E = {'pe': nc.tensor, 'act': nc.scalar, 'dve': nc.vector,
                  'pool': nc.gpsimd, 'sp': nc.sync}
        self.sem = {}
        for e in self.E:
            self.sem[e] = es.enter_context(nc.semaphore('s_' + e))
        self.cnt = {e: 0 for e in self.E}
        self.waited = {e: {} for e in self.E}
        self.lastw = {}
        self.readers = {}
        self.dq = ('sp', 'pool', 'act')
        self.dcnt = {}
        self.drr = {q: 0 for q in self.dq}
        for q in self.dq:
            for i in range(NDS):
                k = (q, i)
                self.sem[k] = es.enter_context(nc.semaphore('d_%s%d' % (q, i)))
                self.dcnt[k] = 0
        self.nwaits = 0

    def _wait(self, e, tok):
        key, val = tok
        if key == e and e == 'pe':
            return
        if self.waited[e].get(key, 0) >= val:
            return
        self.E[e].wait_ge(self.sem[key], val)
        self.waited[e][key] = val
        self.nwaits += 1

    def _collect(self, r, w, e=None):
        deps = {}

        def add(t):
            if t is None:
                return
            if deps.get(t[0], 0) < t[1]:
                deps[t[0]] = t[1]
        for x in r:
            for k, v in self.lastw.get(x, {}).items():
                add((k, v))
            if x.startswith('ps') and e is not None:
                for k, v in self.readers.get(x, {}).items():
                    if k != e:
                        add((k, v))
        for x in w:
            for k, v in self.lastw.get(x, {}).items():
                add((k, v))
            for k, v in self.readers.get(x, {}).items():
                add((k, v))
        return list(deps.items())

    def _record(self, tok, r, w):
        for x in w:
            self.lastw.setdefault(x, {})[tok[0]] = tok[1]
            self.readers[x] = {}
        for x in r:
            d = self.readers.setdefault(x, {})
            if d.get(tok[0], 0) < tok[1]:
                d[tok[0]] = tok[1]

    def op(self, e, fn, r=(), w=(), inc=True):
        for t in self._collect(r, w, e):
            self._wait(e, t)
        ins = fn(self.E[e])
        if inc:
            self.cnt[e] += 1
            ins.then_inc(self.sem[e], 1)
            tok = (e, self.cnt[e])
        else:
            tok = (e, self.cnt[e] + 1)
        self._record(tok, r, w)
        return ins

    def dma(self, q, out, in_, r=(), w=()):
        i = self.drr[q]
        self.drr[q] = (i + 1) % NDS
        k = (q, i)
        if self.dcnt[k] > 0:
            self._wait(q, (k, self.dcnt[k]))
        for t in self._collect(r, w):
            self._wait(q, t)
        ins = self.E[q].dma_start(out=out, in_=in_)
        self.dcnt[k] += 16
        ins.then_inc(self.sem[k], 16)
        tok = (k, self.dcnt[k])
        self._record(tok, r, w)
        return ins

    def barrier(self, keep_pool_dma=False):
        def is_pool(k):
            return isinstance(k, tuple) and k[0] == 'pool'
        toks = [(e, self.cnt[e]) for e in self.E if self.cnt[e] > 0]
        toks += [(k, v) for k, v in self.dcnt.items() if v > 0 and not (keep_pool_dma and is_pool(k))]
        for e in self.E:
            for t in toks:
                self._wait(e, t)
        keep = {}
        if keep_pool_dma:
            for res, d in self.lastw.items():
                d2 = {k: v for k, v in d.items() if is_pool(k)}
                if d2:
                    keep[res] = d2
        self.lastw = keep
        self.readers = {}

    def finish(self):
        for k, v in self.dcnt.items():
            if v > 0:
                self._wait('sp', (k, v))


class Ctx:
    pass


def _interleave(gens):
    live = list(gens)
    while live:
        nxt = []
        for item in live:
            gen, n = item
            done = False
            for _ in range(n):
                try:
                    next(gen)
                except StopIteration:
                    done = True
                    break
            if not done:
                nxt.append(item)
        live = nxt


_UID = [0]


def _sb(nc, es, name, shape, dt):
    _UID[0] += 1
    return es.enter_context(nc.sbuf_tensor("sb%d_%s" % (_UID[0], name), list(shape), dt))


def emit_consts(g):
    nc, s, es = g.nc, g.s, g.es
    g.ident = _sb(nc, es, "ident", [128, 128], F32)
    g.identb = _sb(nc, es, "identb", [128, 128], BF16)
    g.ones = _sb(nc, es, "ones", [128, 128], F32)
    g.onesb = _sb(nc, es, "onesb", [128, 128], BF16)
    s.dma('sp', g.ident[:], g.d_ident[:, :], w=['ident'])
    s.dma('pool', g.identb[:], g.d_ident[:, :], w=['identb'])
    s.op('dve', lambda e: e.memset(g.ones[:], 1.0), w=['ones'])
    s.op('dve', lambda e: e.memset(g.onesb[:], 1.0), w=['onesb'])
    g.mhalf = _sb(nc, es, "mhalf", [128, 8], F32)
    s.op('pool', lambda e: e.memset(g.mhalf[:], -0.5), w=['mhalf'])
    g.ccol = _sb(nc, es, "ccol", [128, 8], F32)
    g.sc = _sb(nc, es, "sc", [128, 8], F32)
    g.scb = _sb(nc, es, "scb", [128, 8, 128], F32)
    s.dma('sp', g.ccol[:], g.d_ccol[:, :], w=['ccol'])
    s.op('act', lambda e: e.activation(out=g.sc[:], in_=g.ccol[:], func=AF.Silu), r=['ccol'], w=['sc'])
    for kc in range(8):
        s.op('dve', lambda e, kc=kc: e.tensor_scalar(out=g.scb[:, kc, :], in0=g.ones[:], scalar1=g.sc[:, kc:kc + 1],
                                                     scalar2=None, op0=ALU.mult), r=['ones', 'sc'], w=['scb'])


def emit_mods(g, L, modw, modb_row, lng, lnb, pre=None):
    nc, s, es = g.nc, g.s, g.es
    L.shift = _sb(nc, L.es, "shift", [128, 8], F32)
    L.scale1 = _sb(nc, L.es, "scale1", [128, 8], F32)
    L.gate_bc = _sb(nc, L.es, "gate_bc", [128, D], F32)
    L.lng_bc = _sb(nc, L.es, "lng_bc", [128, D], F32)
    L.lnb_bc = _sb(nc, L.es, "lnb_bc", [128, D], F32)
    s.dma('sp', L.lng_bc[:], lng.partition_broadcast(128), w=['lng_bc'])
    s.dma('sp', L.lnb_bc[:], lnb.partition_broadcast(128), w=['lnb_bc'])
    if pre is not None:
        pre()
    with ExitStack() as es2:
        mw = [_sb(nc, es2, "mw%d" % i, [128, 3 * D], F32) for i in range(2)]
        brow = _sb(nc, es2, "brow", [1, 3 * D], F32)
        bc = _sb(nc, es2, "modbc", [128, 2 * D], F32)
        one11 = _sb(nc, es2, "one11", [1, 1], F32)
        s.op('dve', lambda e: e.memset(one11[:], 1.0), w=['one11'])
        s.dma('sp', brow[:], modb_row[:, :], w=['brow'])
        P = g.ps
        for kc in range(8):
            t = mw[kc % 2]
            nm = 'mw%d' % (kc % 2)
            s.dma('sp', t[:], modw[kc * 128:(kc + 1) * 128, :], w=[nm])
            for j in range(6):
                s.op('pe', lambda e, j=j, kc=kc, t=t: e.matmul(P[:, j * 512:(j + 1) * 512], lhsT=g.scb[:, kc, :],
                                                            rhs=t[:, j * 512:(j + 1) * 512], start=(kc == 0), stop=False),
                     r=[nm, 'scb'], w=['ps%d' % j], inc=(j == 5))
        for j in range(6):
            s.op('pe', lambda e, j=j: e.matmul(P[:, j * 512:(j + 1) * 512], lhsT=g.ones[0:1, :],
                                               rhs=brow[0:1, j * 512:(j + 1) * 512], start=False, stop=True),
                 r=['brow', 'ones'], w=['ps%d' % j])
        for j in range(4):
            s.op('act' if j % 2 else 'dve',
                 (lambda e, j=j: e.activation(out=bc[:, j * 512:(j + 1) * 512], in_=P[:, j * 512:(j + 1) * 512], func=AF.Copy))
                 if j % 2 else
                 (lambda e, j=j: e.tensor_copy(out=bc[:, j * 512:(j + 1) * 512], in_=P[:, j * 512:(j + 1) * 512])),
                 r=['ps%d' % j], w=['modbc%d' % j])
        for j in range(2):
            s.op('dve', lambda e, j=j: e.tensor_copy(out=L.gate_bc[:, j * 512:(j + 1) * 512], in_=P[:, (4 + j) * 512:(5 + j) * 512]),
                 r=['ps%d' % (4 + j)], w=['gate_bc'])
        for j in range(16):
            s.op('pe', lambda e, j=j: e.matmul(P[:, 6 * 512 + j:6 * 512 + j + 1], lhsT=bc[0:1, j * 128:(j + 1) * 128],
                                               rhs=one11[0:1, 0:1], start=True, stop=True),
                 r=['modbc%d' % (j // 4), 'one11'], w=['ps6'])
        s.op('dve', lambda e: e.tensor_copy(out=L.shift[:], in_=P[:, 6 * 512:6 * 512 + 8]), r=['ps6'], w=['shift'])
        s.op('dve', lambda e: e.tensor_scalar(out=L.scale1[:], in0=P[:, 6 * 512 + 8:6 * 512 + 16], scalar1=1.0, scalar2=None,
                                              op0=ALU.add), r=['ps6'], w=['scale1'])
        s.barrier(keep_pool_dma=(pre is not None))


def emit_hT(g, L, xin, xin_nm, hT_ap_fn, hT_nm, pbanks):
    s = g.s
    P = g.ps
    for kc in range(8):
        b = pbanks[kc // 4]
        off = b * 512 + (kc % 4) * 128
        s.op('pe', lambda e, kc=kc, off=off: e.transpose(P[:, off:off + 128], xin[:, kc * 128:(kc + 1) * 128], g.ident[:]),
             r=[xin_nm, 'ident'], w=['ps%d' % b])
    for kc in range(8):
        b = pbanks[kc // 4]
        off = b * 512 + (kc % 4) * 128
        s.op('act', lambda e, kc=kc, off=off: e.activation(out=hT_ap_fn(kc), in_=P[:, off:off + 128], func=AF.Identity,
                                                           scale=L.scale1[:, kc:kc + 1], bias=L.shift[:, kc:kc + 1]),
             r=['ps%d' % b, 'scale1', 'shift'], w=[hT_nm])


def emit_epilogue(g, L, ybanks, xres, xres_nm, zb, xo, xo_nm, dst_rows):
    s = g.s
    P = g.ps
    for j in range(2):
        b = ybanks[j]
        s.op('dve', lambda e, j=j, b=b: e.tensor_tensor(out=zb[:, j * 512:(j + 1) * 512], in0=P[:, b * 512:(b + 1) * 512],
                                                        in1=L.gate_bc[:, j * 512:(j + 1) * 512], op=ALU.mult),
             r=['ps%d' % b, 'gate_bc'], w=['zb'])
    s.op('dve', lambda e: e.scalar_tensor_tensor(out=zb[:], in0=xres[:], scalar=ALPHA, in1=zb[:], op0=ALU.mult, op1=ALU.add),
         r=[xres_nm, 'zb'], w=['zb'])
    st = g.lnst
    for j in range(2):
        s.op('dve', lambda e, j=j: e.bn_stats(out=st[:, j * 6:(j + 1) * 6], in_=zb[:, j * 512:(j + 1) * 512]), r=['zb'], w=['lnst'])
    s.op('dve', lambda e: e.bn_aggr(out=g.lnmv[:, 0:2], in_=st[:, 0:12]), r=['lnst'], w=['lnmv'])
    s.op('dve', lambda e: e.tensor_scalar(out=g.lnmv[:, 2:3], in0=g.lnmv[:, 1:2], scalar1=LN_EPS, scalar2=None, op0=ALU.add),
         r=['lnmv'], w=['lnmv2'])
    s.op('pool', lambda e: e.tensor_tensor(out=g.lnmv[:, 4:5], in0=g.lnmv[:, 2:3], in1=g.mhalf[:, 0:1], op=ALU.pow), r=['lnmv2', 'mhalf'], w=['lnmv4'])
    s.op('dve', lambda e: e.tensor_scalar(out=zb[:], in0=zb[:], scalar1=g.lnmv[:, 0:1], scalar2=g.lnmv[:, 4:5],
                                          op0=ALU.subtract, op1=ALU.mult), r=['zb', 'lnmv', 'lnmv4'], w=['zb'])
    s.op('pool', lambda e: e.tensor_tensor(out=zb[:], in0=zb[:], in1=L.lng_bc[:], op=ALU.mult), r=['zb', 'lng_bc'], w=['zb'])
    s.op('pool', lambda e: e.tensor_tensor(out=xo[:], in0=zb[:], in1=L.lnb_bc[:], op=ALU.add), r=['zb', 'lnb_bc'], w=[xo_nm])
    s.dma('sp', dst_rows, xo[:], r=[xo_nm], w=[])


def emit_ffn(g, li, src, dst):
    nc, s = g.nc, g.s
    TT = 256
    NT = S // TT
    NBT = TT // 128
    with ExitStack() as les:
        L = Ctx()
        L.es = les
        WT = Ctx()

        def pre():
            WT.wup = _sb(nc, les, "wup", [128, 8, 2 * DFF], BF16)
            WT.wdn = _sb(nc, les, "wdn", [128, NFC, D], BF16)
            wupd = g.d_f_w_up[li].rearrange("(kc p) n -> p kc n", p=128)
            for kc in range(8):
                s.dma('pool', WT.wup[:, kc, :], wupd[:, kc, :], w=['wup'])
            wdnd = g.d_f_w_down[li].rearrange("(fc p) n -> p fc n", p=128)
            for f3 in range(3):
                s.dma('pool', WT.wdn[:, f3 * 7:(f3 + 1) * 7, :], wdnd[:, f3 * 7:(f3 + 1) * 7, :], w=['wdn'])
        emit_mods(g, L, g.d_f_mod_w[li], g.d_f_mod_b[li], g.d_f_ln_g[li], g.d_f_ln_b[li], pre=pre)
        wup, wdn = WT.wup, WT.wdn
        cw = _sb(nc, les, "cw", [128, 2 * NFC, 4], F32)
        halo = _sb(nc, les, "halo", [128, 2 * NFC, 2], F32)
        hT = [_sb(nc, les, "hT%d" % i, [128, 8, TT], BF16) for i in range(2)]
        gTs = [_sb(nc, les, "gT%d" % i, [128, NFC, TT], BF16) for i in range(2)]
        upre = [_sb(nc, les, "upre%d" % i, [128, TT + 2], F32) for i in range(2)]
        c0 = [_sb(nc, les, "c0%d" % i, [128, TT], F32) for i in range(2)]
        asil = _sb(nc, les, "asil", [128, TT], F32)
        xin = [_sb(nc, les, "xin%d" % i, [128, D], F32) for i in range(2)]
        xrs = [_sb(nc, les, "xrs%d" % i, [128, D], F32) for i in range(2)]
        zb = _sb(nc, les, "zb", [128, D], F32)
        xo = [_sb(nc, les, "xo%d" % i, [128, D], F32) for i in range(2)]
        P = g.ps
        s.dma('sp', cw[:], g.d_f_cw[li], w=['cw'])
        s.op('dve', lambda e: e.memset(halo[:], 0.0), w=['halo%d' % i for i in range(2 * NFC)])
        def phA(t):
            t0 = t * TT
            hs = t % 2
            hnm = 'hT%d' % hs
            for bi in range(NBT):
                xs_ = (t * NBT + bi) % 2
                s.dma('sp', xin[xs_][:], src[t0 + bi * 128:t0 + (bi + 1) * 128, :], w=['xin%d' % xs_])
                emit_hT(g, L, xin[xs_], 'xin%d' % xs_, lambda kc, bi=bi, hs=hs: hT[hs][:, kc, bi * 128:(bi + 1) * 128], hnm, (0, 1))
                yield

        def phU(t):
            hs = t % 2
            hnm = 'hT%d' % hs
            gT = gTs[t % 2]
            gnm = 'gT%d' % (t % 2)
            for j in range(NFC):
                for half in range(2):
                    fc = j + half * NFC
                    b = 2 + ((2 * j + half) % 4)
                    pb = 'ps%d' % b
                    up = upre[half]
                    unm = 'upre%d' % half
                    for kc in range(8):
                        s.op('pe', lambda e, kc=kc, fc=fc, b=b: e.matmul(P[:, b * 512:b * 512 + TT], lhsT=wup[:, kc, fc * 128:(fc + 1) * 128],
                                                                      rhs=hT[hs][:, kc, :], start=(kc == 0), stop=(kc == 7)),
                             r=['wup', hnm], w=[pb], inc=(kc == 7))
                    s.op('pool', lambda e, fc=fc, up=up: e.tensor_copy(out=up[:, 0:2], in_=halo[:, fc, :]), r=['halo%d' % fc], w=[unm])
                    s.op('act', lambda e, b=b, up=up: e.activation(out=up[:, 2:TT + 2], in_=P[:, b * 512:b * 512 + TT], func=AF.Copy),
                         r=[pb], w=[unm])
                    s.op('act', lambda e, b=b, fc=fc, half=half: e.activation(out=c0[half][:], in_=P[:, b * 512:b * 512 + TT], func=AF.Identity,
                                                                             scale=cw[:, fc, 2:3], bias=cw[:, fc, 3:4]),
                         r=[pb, 'cw'], w=['c0%d' % half])
                    s.op('pool', lambda e, fc=fc, up=up: e.tensor_copy(out=halo[:, fc, :], in_=up[:, TT:TT + 2]), r=[unm], w=['halo%d' % fc])
                    s.op('dve', lambda e, fc=fc, up=up, half=half: e.scalar_tensor_tensor(out=c0[half][:], in0=up[:, 1:TT + 1], scalar=cw[:, fc, 1:2],
                                                                                        in1=c0[half][:], op0=ALU.mult, op1=ALU.add),
                         r=[unm, 'cw', 'c0%d' % half], w=['c0%d' % half])
                    s.op('dve', lambda e, fc=fc, up=up, half=half: e.scalar_tensor_tensor(out=c0[half][:], in0=up[:, 0:TT], scalar=cw[:, fc, 0:1],
                                                                                        in1=c0[half][:], op0=ALU.mult, op1=ALU.add),
                         r=[unm, 'cw', 'c0%d' % half], w=['c0%d' % half])
                    if half == 0:
                        s.op('act', lambda e: e.activation(out=asil[:], in_=c0[0][:], func=AF.Silu), r=['c00'], w=['asil'])
                    else:
                        s.op('dve', lambda e, j=j, gT=gT: e.tensor_tensor(out=gT[:, j, :], in0=asil[:], in1=c0[1][:], op=ALU.mult),
                             r=['asil', 'c01'], w=[gnm])
                    yield

        def phD(t):
            t0 = t * TT
            gT = gTs[t % 2]
            gnm = 'gT%d' % (t % 2)
            for bi in range(NBT):
                r0 = t0 + bi * 128
                xs_ = (t * NBT + bi) % 2
                s.dma('sp', xrs[xs_][:], src[r0:r0 + 128, :], w=['xrs%d' % xs_])
                for half in range(2):
                    b = 6 + half
                    for j in range(NFC):
                        s.op('pe', lambda e, j=j, half=half, b=b, bi=bi, gT=gT: e.matmul(P[:, b * 512:(b + 1) * 512], lhsT=gT[:, j, bi * 128:(bi + 1) * 128],
                                                                                         rhs=wdn[:, j, half * 512:(half + 1) * 512],
                                                                                         start=(j == 0), stop=(j == NFC - 1)),
                             r=[gnm, 'wdn'], w=['ps%d' % b], inc=(j == NFC - 1))
                    yield
                emit_epilogue(g, L, (6, 7), xrs[xs_], 'xrs%d' % xs_, zb, xo[xs_], 'xo%d' % xs_, dst[r0:r0 + 128, :])
                yield

        for it in range(NT + 2):
            gens = []
            if 1 <= it <= NT:
                gens.append([phU(it - 1), 1])
            if it < NT:
                gens.append([phA(it), 1])
            if it >= 2:
                gens.append([phD(it - 2), 1])
            _interleave(gens)
        s.barrier()


NEG = -1.0e30
GUARD = 1.0e38
NBISECT = 24
TOPK_EXACT_NK = 1024
DSTOP = int(os.environ.get('DSA_STOP', '99'))
DSUB = int(os.environ.get('DSA_SUB', '99'))
GSTOP = int(os.environ.get('GDN_STOP', '99'))
DSC = int(os.environ.get('DSA_SC', '3'))
DSKIP = os.environ.get('DSA_SKIP', '').split(',')
REP = -3.0e38


def emit_dsa(g, src, dst):
    nc, s = g.nc, g.s
    P = g.ps
    with ExitStack() as les:
        L = Ctx()
        L.es = les
        emit_mods(g, L, g.d_e_mod_w[0], g.d_e_mod_b[0], g.d_e_ln_g[0], g.d_e_ln_b[0])
        A = lambda name, shape, dt: _sb(nc, les, name, shape, dt)
        win = A("win", [128, 8, 1536], BF16)
        wif = A("wif", [128, 8, 4], F32)
        wiw = A("wiw", [128, 8, 4], BF16)
        poolw = A("poolw", [128, 4, 128], BF16)
        pscale = A("pscale", [128, 4], F32)
        kvn_bc = A("kvn_bc", [128, 128], F32)
        ukT = A("ukT", [128, 4, 128], BF16)
        uvpad = A("uvpad", [128, 8, 128], BF16)
        wout = A("wout", [128, 8, D], BF16)
        negI4 = A("negI4", [128, 512], BF16)
        corr = A("corr", [128, 4, 15], F32)
        ckvn_all = A("ckvn_all", [128, NB, 128], BF16)
        ckvnT_all = A("ckvnT_all", [128, S], BF16)
        kiT_all = A("kiT_all", [128, S], BF16)
        xin = [A("xin%d" % i, [128, D], F32) for i in range(3)]
        hT = A("hT", [128, 8, 128], BF16)
        ut = A("ut", [128, 4, 143], F32)
        ta = A("ta", [128, 143], F32)
        tb = A("tb", [128, 143], F32)
        dT = A("dT", [128, 4, 128], BF16)
        qT = A("qT", [128, 512], BF16)
        qiT = [A("qiT%d" % i, [128, 256], BF16) for i in range(2)]
        wis = [A("wis%d" % i, [128, 4], F32) for i in range(2)]
        qlT = [A("qlT%d" % i, [128, 1024], BF16) for i in range(3)]
        W = [A("W%d" % i, [128, S + 8], F32) for i in range(2)]
        bs = A("bs", [128, 8], F32)
        cb = A("cb", [128, S], BF16)
        rbuf = [A("rbuf%d" % i, [128, 512], F32) for i in range(2)]
        rb2 = A("rb2", [128, 512], F32)
        notm = [A("notm%d" % i, [128, S], BF16) for i in range(2)]
        m8 = A("m8", [128, 8], F32)
        pT = [A("pT%d" % i, [128, 512], BF16) for i in range(2)]
        rden = A("rden", [128, 512], F32)
        oTn = A("oTn", [128, 1024], BF16)
        yinT = [A("yinT%d" % i, [128, 1024], BF16) for i in range(3)]
        zb = A("zb", [128, D], F32)
        xo = [A("xo%d" % i, [128, D], F32) for i in range(2)]
        sq = A("sq", [128, 128], F32)
        ckf = A("ckf", [128, 128], F32)
        rs = A("rs", [128, 4], F32)
        Pb3 = P[:, 3 * 512:4 * 512].bitcast(BF16)

        wind = g.d_e_w_in[0].rearrange("(kc p) n -> p kc n", p=128)
        for k2 in range(2):
            s.dma('pool', win[:, 4 * k2:4 * k2 + 4, 0:1472], wind[:, 4 * k2:4 * k2 + 4, 0:1472], w=['win'])
        s.dma('pool', win[:, :, 1472:1536], wind[:, :, 1408:1472], w=['win'])
        s.dma('sp', wif[:], wind[:, :, 1472:1476], w=['wif'])
        s.op('dve', lambda e: e.tensor_copy(out=wiw[:], in_=wif[:]), r=['wif'], w=['wiw'])
        s.dma('pool', poolw[:], g.d_e_pool_w[0].rearrange("g c d -> c g d"), w=['poolw'])
        s.dma('sp', pscale[:], g.d_pscale[:, :], w=['pscale'])
        s.dma('sp', kvn_bc[:], g.d_e_kv_norm[0].partition_broadcast(128), w=['kvn_bc'])
        s.dma('pool', ukT[:], g.d_ukT[:, :, :], w=['ukT'])
        s.dma('pool', uvpad[:], g.d_uvpad[:, :, :], w=['uvpad'])
        woutd = g.d_e_w_out[0].rearrange("(kc p) n -> p kc n", p=128)
        for k2 in range(2):
            s.dma('pool', wout[:, k2 * 4:(k2 + 1) * 4, :], woutd[:, k2 * 4:(k2 + 1) * 4, :], w=['wout'])
        s.dma('pool', negI4[:], g.d_negI4[:, :], w=['negI4'])
        s.dma('sp', corr[:], g.d_corr[:, :, :], w=['corr'])
        s.op('dve', lambda e: e.memset(ut[:], 0.0), w=['ut'])

        def front_a(qb):
            sl = qb % 2
            s3 = qb % 3
            t0 = qb * 128
            nk = t0 + 128
            xn = 'xin%d' % s3
            s.dma('sp', xin[s3][:], src[t0:t0 + 128, :], w=[xn])
            emit_hT(g, L, xin[s3], xn, lambda kc: hT[:, kc, :], 'hT', (0, 1))
            yield
            def grp(out_ap, cols, bank, last=True):
                for kc in range(8):
                    s.op('pe', lambda e, kc=kc: e.matmul(out_ap, lhsT=win[:, kc, cols[0]:cols[1]], rhs=hT[:, kc, :],
                                                        start=(kc == 0), stop=(kc == 7)),
                         r=['win', 'hT'], w=['ps%d' % bank], inc=(kc == 7))
            for gi in range(4):
                grp(P[:, gi * 128:(gi + 1) * 128], (gi * 128, (gi + 1) * 128), 0)
                yield
            for j in range(4):
                grp(P[:, 512 + j * 128:512 + (j + 1) * 128], (512 + j * 128, 512 + (j + 1) * 128), 1)
                yield
            for j in range(2):
                grp(P[:, 1024 + j * 128:1024 + (j + 1) * 128], (1152 + j * 128, 1152 + (j + 1) * 128), 2)
                yield
            grp(P[:, 1024 + 256:1024 + 384], (1408, 1536), 2)
            yield
            for kc in range(8):
                s.op('pe', lambda e, kc=kc: e.matmul(P[:, 1536:1536 + 128], lhsT=hT[:, kc, :], rhs=win[:, kc, 1024:1152],
                                                    start=(kc == 0), stop=(kc == 7)), r=['win', 'hT'], w=['ps3'], inc=(kc == 7))
            for kc in range(8):
                s.op('pe', lambda e, kc=kc: e.matmul(P[:, 1536 + 128:1536 + 132], lhsT=hT[:, kc, :], rhs=wiw[:, kc, :],
                                                    start=(kc == 0), stop=(kc == 7)), r=['wiw', 'hT'], w=['ps3'], inc=(kc == 7))
            yield
            s.op('act', lambda e: e.activation(out=ut[:, :, 15:143], in_=P[:, 0:512].rearrange("p (g t) -> p g t", g=4), func=AF.Copy),
                 r=['ps0'], w=['ut'])
            s.op('act', lambda e: e.activation(out=qT[:], in_=P[:, 512:1024], func=AF.Copy), r=['ps1'], w=['qT'])
            s.op('dve', lambda e: e.tensor_copy(out=qiT[sl][:], in_=P[:, 1024:1024 + 256]), r=['ps2'], w=['qiT%d' % sl])
            s.op('dve', lambda e: e.tensor_copy(out=kiT_all[:, t0:t0 + 128], in_=P[:, 1024 + 256:1024 + 384]), r=['ps2'], w=['kiT%d' % qb])
            s.op('dve', lambda e: e.tensor_copy(out=wis[sl][:], in_=P[:, 1536 + 128:1536 + 132]), r=['ps3'], w=['wis%d' % sl])
            yield
            s.op('act', lambda e: e.activation(out=ckf[:], in_=P[:, 1536:1536 + 128], func=AF.Copy), r=['ps3'], w=['ckf'])
            s.op('dve', lambda e: e.tensor_tensor(out=sq[:], in0=ckf[:], in1=ckf[:], op=ALU.mult), r=['ckf'], w=['sq'])
            s.op('dve', lambda e: e.reduce_sum(out=rs[:, 0:1], in_=sq[:], axis=AX.X), r=['sq'], w=['rs0'])
            s.op('dve', lambda e: e.tensor_scalar(out=rs[:, 1:2], in0=rs[:, 0:1], scalar1=1.0 / 128, scalar2=RMS_EPS, op0=ALU.mult, op1=ALU.add),
                 r=['rs0'], w=['rs1'])
            s.op('pool', lambda e: e.tensor_tensor(out=rs[:, 3:4], in0=rs[:, 1:2], in1=g.mhalf[:, 0:1], op=ALU.pow), r=['rs1', 'mhalf'], w=['rs3'])
            s.op('dve', lambda e: e.scalar_tensor_tensor(out=ckf[:], in0=ckf[:], scalar=rs[:, 3:4], in1=kvn_bc[:],
                                                         op0=ALU.mult, op1=ALU.mult), r=['ckf', 'rs3', 'kvn_bc'], w=['ckf'])
            s.op('act', lambda e: e.activation(out=ckvn_all[:, qb, :], in_=ckf[:], func=AF.Copy), r=['ckf'], w=['ckvn%d' % qb])
            s.op('pe', lambda e: e.transpose(P[:, 1536 + 256:1536 + 384], ckf[:], g.ident[:]), r=['ckf', 'ident'], w=['ps3'])
            s.op('act', lambda e: e.activation(out=ckvnT_all[:, t0:t0 + 128], in_=P[:, 1536 + 256:1536 + 384], func=AF.Copy), r=['ps3'], w=['ckvnT%d' % qb])
            yield
            for gi in range(4):
                win_ = 2 << gi
                U = ut[:, gi, :]
                s.op('dve', lambda e, U=U: e.tensor_tensor(out=ta[:, 1:143], in0=U[:, 1:143], in1=U[:, 0:142], op=ALU.add), r=['ut'], w=['ta'])
                sw = ta
                swn = 'ta'
                if gi >= 1:
                    s.op('dve', lambda e: e.tensor_tensor(out=tb[:, 3:143], in0=ta[:, 3:143], in1=ta[:, 1:141], op=ALU.add), r=['ta'], w=['tb'])
                    sw, swn = tb, 'tb'
                if gi >= 2:
                    s.op('dve', lambda e: e.tensor_tensor(out=ta[:, 7:143], in0=tb[:, 7:143], in1=tb[:, 3:139], op=ALU.add), r=['tb'], w=['ta'])
                    sw, swn = ta, 'ta'
                if gi >= 3:
                    s.op('dve', lambda e: e.tensor_tensor(out=tb[:, 15:143], in0=ta[:, 15:143], in1=ta[:, 7:135], op=ALU.add), r=['ta'], w=['tb'])
                    sw, swn = tb, 'tb'
                if qb == 0:
                    s.op('dve', lambda e, sw=sw, gi=gi, win_=win_: e.tensor_tensor(out=sw[:, 15:15 + win_ - 1], in0=sw[:, 15:15 + win_ - 1],
                                                                                 in1=corr[:, gi, 0:win_ - 1], op=ALU.mult),
                         r=[swn, 'corr'], w=[swn])
                s.op('dve', lambda e, sw=sw, gi=gi, win_=win_, U=U: e.scalar_tensor_tensor(out=dT[:, gi, :], in0=sw[:, 15:143], scalar=1.0 / win_,
                                                                                          in1=U[:, 15:143], op0=ALU.mult, op1=ALU.subtract),
                     r=[swn, 'ut'], w=['dT'])
                yield
            s.op('pool', lambda e: e.tensor_copy(out=ut[:, :, 0:15], in_=ut[:, :, 128:143]), r=['ut'], w=['ut'])
            for gi in range(4):
                s.op('pe', lambda e, gi=gi: e.matmul(P[:, gi * 128:(gi + 1) * 128], lhsT=poolw[:, gi, :], rhs=dT[:, gi, :], start=True, stop=True),
                     r=['poolw', 'dT'], w=['ps0'])
            for gi in range(4):
                s.op('act', lambda e, gi=gi: e.activation(out=yinT[s3][:, gi * 128:(gi + 1) * 128], in_=P[:, gi * 128:(gi + 1) * 128],
                                                          func=AF.Identity, scale=pscale[:, gi:gi + 1]),
                     r=['ps0', 'pscale'], w=['yinT%d' % s3])
            yield
            for h in range(8):
                po = (h % 2) * 64
                bank = 1 + h % 2
                off = bank * 512 + (h // 2) * 128
                s.op('pe', lambda e, h=h, po=po, off=off: e.matmul(P[:, off:off + 128], lhsT=ukT[po:po + 64, h // 2, :],
                                                                  rhs=qT[po:po + 64, (h // 2) * 128:(h // 2 + 1) * 128], start=True, stop=True),
                     r=['ukT', 'qT'], w=['ps%d' % bank])
            for j in range(2):
                s.op('act', lambda e, j=j: e.activation(out=qlT[s3][:, j * 512:(j + 1) * 512], in_=P[:, (1 + j) * 512:(2 + j) * 512], func=AF.Copy),
                     r=['ps%d' % (1 + j)], w=['qlT%d' % s3])
            yield
            Wn = 'W%d' % sl
            cnt = 0
            chunks = []
            k0_ = 0
            while k0_ < nk:
                rem = nk - k0_
                w0 = 512 if rem >= 512 else (256 if rem >= 256 else 128)
                chunks.append((k0_, w0))
                k0_ += w0
            for (k0, w_) in chunks:
                for h in range(4):
                    po = (h % 2) * 64
                    bank = 2 + cnt % 2
                    rb = rbuf[cnt % 2]
                    rbn = 'rbuf%d' % (cnt % 2)
                    cnt += 1
                    s.op('pe', lambda e, h=h, po=po, bank=bank, k0=k0, w_=w_: e.matmul(P[:, bank * 512:bank * 512 + w_],
                                                                                     lhsT=qiT[sl][po:po + 64, (h // 2) * 128:(h // 2 + 1) * 128],
                                                                                     rhs=kiT_all[po:po + 64, k0:k0 + w_], start=True, stop=True),
                         r=['qiT%d' % sl] + ['kiT%d' % kb_ for kb_ in range(k0 // 128, (k0 + w_) // 128)], w=['ps%d' % bank])
                    s.op('act', lambda e, bank=bank, rb=rb, w_=w_: e.activation(out=rb[:, 0:w_], in_=P[:, bank * 512:bank * 512 + w_], func=AF.Relu),
                         r=['ps%d' % bank], w=[rbn])
                    if h == 0:
                        s.op('dve', lambda e, rb=rb, k0=k0, w_=w_: e.tensor_scalar(out=W[sl][:, k0:k0 + w_], in0=rb[:, 0:w_], scalar1=wis[sl][:, 0:1],
                                                                                 scalar2=None, op0=ALU.mult), r=[rbn, 'wis%d' % sl], w=[Wn])
                    else:
                        s.op('dve', lambda e, rb=rb, k0=k0, w_=w_, h=h: e.scalar_tensor_tensor(out=W[sl][:, k0:k0 + w_], in0=rb[:, 0:w_], scalar=wis[sl][:, h:h + 1],
                                                                                            in1=W[sl][:, k0:k0 + w_], op0=ALU.mult, op1=ALU.add),
                             r=[rbn, 'wis%d' % sl, Wn], w=[Wn])
                    yield

        def topk(qb):
            sl = qb % 2
            nk = qb * 128 + 128
            Wn = 'W%d' % sl
            nmn = 'notm%d' % sl
            Wt = W[sl]
            if nk <= 256:
                s.op('dve', lambda e: e.memset(notm[sl][:, 0:nk], 0.0), w=[nmn])
            elif nk <= TOPK_EXACT_NK:
                s.op('dve', lambda e: e.memset(Wt[0:64, nk - 64:nk], NEG), r=[Wn], w=[Wn])
                for it in range(32):
                    s.op('dve', lambda e: e.max(out=m8[:], in_=Wt[:, 0:nk]), r=[Wn], w=['m8'])
                    s.op('dve', lambda e: e.match_replace(out=Wt[:, 0:nk], in_to_replace=m8[:], in_values=Wt[:, 0:nk], imm_value=REP),
                         r=[Wn, 'm8'], w=[Wn])
                    yield
                s.op('dve', lambda e: e.tensor_scalar(out=notm[sl][:, 0:nk], in0=Wt[:, 0:nk], scalar1=0.5 * REP, scalar2=None, op0=ALU.is_gt),
                     r=[Wn], w=[nmn])
            else:
                s.op('dve', lambda e: e.tensor_reduce(out=bs[:, 7:8], in_=Wt[:, 0:nk], axis=AX.X, op=ALU.max, apply_absolute_value=True),
                     r=[Wn], w=['bs7'])
                s.op('dve', lambda e: e.memset(Wt[0:64, nk - 64:nk], NEG), r=[Wn], w=[Wn])
                s.op('dve', lambda e: e.tensor_scalar(out=bs[:, 0:1], in0=bs[:, 7:8], scalar1=-1.001, scalar2=None, op0=ALU.mult), r=['bs7'], w=['bs0'])
                s.op('dve', lambda e: e.tensor_scalar(out=bs[:, 1:2], in0=bs[:, 7:8], scalar1=1.0005, scalar2=None, op0=ALU.mult), r=['bs7'], w=['bs1'])
                s.op('dve', lambda e: e.tensor_scalar(out=bs[:, 2:3], in0=bs[:, 0:1], scalar1=bs[:, 1:2], scalar2=-1.0, op0=ALU.add, op1=ALU.mult),
                     r=['bs0', 'bs1'], w=['bs2'])
                yield
                for it in range(NBISECT):
                    s.op('act', lambda e: e.activation(out=notm[sl][:, 0:nk], in_=Wt[:, 0:nk], func=AF.Sign, bias=bs[:, 2:3], scale=1.0, accum_out=bs[:, 3:4]),
                         r=[Wn, 'bs2'], w=[nmn, 'bs3'])
                    s.op('dve', lambda e: e.tensor_scalar(out=bs[:, 4:5], in0=bs[:, 3:4], scalar1=float(512 - nk), scalar2=None, op0=ALU.is_ge), r=['bs3'], w=['bs4'])
                    s.op('dve', lambda e: e.scalar_tensor_tensor(out=bs[:, 0:1], in0=bs[:, 4:5], scalar=bs[:, 1:2], in1=bs[:, 0:1], op0=ALU.mult, op1=ALU.add),
                         r=['bs4', 'bs1', 'bs0'], w=['bs0'])
                    s.op('dve', lambda e: e.tensor_scalar(out=bs[:, 1:2], in0=bs[:, 1:2], scalar1=0.5, scalar2=None, op0=ALU.mult), r=['bs1', 'bs0'], w=['bs1'])
                    s.op('dve', lambda e: e.tensor_scalar(out=bs[:, 2:3], in0=bs[:, 0:1], scalar1=bs[:, 1:2], scalar2=-1.0, op0=ALU.add, op1=ALU.mult),
                         r=['bs0', 'bs1'], w=['bs2'])
                    yield
                s.op('dve', lambda e: e.scalar_tensor_tensor(out=bs[:, 7:8], in0=bs[:, 1:2], scalar=2.0, in1=bs[:, 0:1], op0=ALU.mult, op1=ALU.add),
                     r=['bs1', 'bs0'], w=['bs7'])
                s.op('dve', lambda e: e.tensor_scalar(out=bs[:, 6:7], in0=bs[:, 7:8], scalar1=-1.0, scalar2=None, op0=ALU.mult), r=['bs7'], w=['bs6'])
                s.op('act', lambda e: e.activation(out=notm[sl][:, 0:nk], in_=Wt[:, 0:nk], func=AF.Sign, bias=bs[:, 6:7], scale=1.0, accum_out=bs[:, 3:4]),
                     r=[Wn, 'bs6'], w=[nmn, 'bs3'])
                s.op('dve', lambda e: e.tensor_scalar(out=bs[:, 5:6], in0=bs[:, 3:4], scalar1=-0.5, scalar2=float(256 - nk // 2), op0=ALU.mult, op1=ALU.add),
                     r=['bs3'], w=['bs5'])
                yield
                s.op('dve', lambda e: e.tensor_scalar(out=notm[sl][:, 0:nk], in0=Wt[:, 0:nk], scalar1=bs[:, 7:8], scalar2=None, op0=ALU.is_gt),
                     r=[Wn, 'bs7'], w=[nmn])
                s.op('dve', lambda e: e.tensor_scalar(out=cb[:, 0:nk], in0=Wt[:, 0:nk], scalar1=bs[:, 0:1], scalar2=None, op0=ALU.is_gt),
                     r=[Wn, 'bs0'], w=['cb'])
                yield
                s.op('dve', lambda e: e.scalar_tensor_tensor(out=cb[:, 0:nk], in0=Wt[:, 0:nk], scalar=bs[:, 7:8], in1=cb[:, 0:nk], op0=ALU.is_le, op1=ALU.mult),
                     r=[Wn, 'bs7', 'cb'], w=['cb'])
                yield
                s.op('dve', lambda e: e.tensor_tensor_scan(out=Wt[:, 0:nk], data0=g.onesb[:, 0:1].to_broadcast([128, nk]), data1=cb[:, 0:nk], initial=0.0, op0=ALU.mult, op1=ALU.add),
                     r=['onesb', 'cb', Wn], w=[Wn])
                yield
                s.op('dve', lambda e: e.scalar_tensor_tensor(out=cb[:, 0:nk], in0=Wt[:, 0:nk], scalar=bs[:, 5:6], in1=cb[:, 0:nk], op0=ALU.is_le, op1=ALU.mult),
                     r=[Wn, 'bs5', 'cb'], w=['cb'])
                yield
                s.op('dve', lambda e: e.tensor_tensor(out=notm[sl][:, 0:nk], in0=notm[sl][:, 0:nk], in1=cb[:, 0:nk], op=ALU.add), r=[nmn, 'cb'], w=[nmn])
                s.op('dve', lambda e: e.tensor_scalar(out=notm[sl][:, 0:nk], in0=notm[sl][:, 0:nk], scalar1=-1.0, scalar2=1.0, op0=ALU.mult, op1=ALU.add),
                     r=[nmn], w=[nmn])
            s.op('dve', lambda e: e.memset(notm[sl][0:64, nk - 64:nk], 1.0), r=[nmn], w=[nmn])
            yield

        def back(qb):
            sl = qb % 2
            s3 = qb % 3
            t0 = qb * 128
            cnt = 0
            for hg in range(2):
                for kb in range(qb + 1):
                    j = cnt % 2
                    cnt += 1
                    bank = 4 + j
                    s.op('pe', lambda e, bank=bank, kb=kb, hg=hg: e.matmul(P[:, bank * 512:(bank + 1) * 512], lhsT=ckvnT_all[:, kb * 128:(kb + 1) * 128],
                                                                          rhs=qlT[s3][:, hg * 512:(hg + 1) * 512], start=True, stop=False),
                         r=['ckvnT%d' % kb, 'qlT%d' % s3], w=['ps%d' % bank], inc=False)
                    s.op('pe', lambda e, bank=bank, kb=kb: e.matmul(P[:, bank * 512:(bank + 1) * 512], lhsT=notm[sl][:, kb * 128:(kb + 1) * 128],
                                                                   rhs=negI4[:], start=False, stop=True),
                         r=['notm%d' % sl, 'negI4'], w=['ps%d' % bank])
                    s.op('act', lambda e, bank=bank, j=j: e.activation(out=pT[j][:], in_=P[:, bank * 512:(bank + 1) * 512], func=AF.Exp, scale=0.125),
                         r=['ps%d' % bank], w=['pT%d' % j])
                    s.op('pe', lambda e, kb=kb, j=j: e.matmul(P[:, 6 * 512:7 * 512], lhsT=ckvn_all[:, kb, :], rhs=pT[j][:], start=(kb == 0), stop=(kb == qb)),
                         r=['ckvn%d' % kb, 'pT%d' % j], w=['ps6'], inc=False)
                    s.op('pe', lambda e, kb=kb, j=j: e.matmul(P[:, 7 * 512:8 * 512], lhsT=g.onesb[:], rhs=pT[j][:], start=(kb == 0), stop=(kb == qb)),
                         r=['onesb', 'pT%d' % j], w=['ps7'])
                    yield
                s.op('dve', lambda e: e.reciprocal(out=rden[:], in_=P[:, 7 * 512:8 * 512]), r=['ps7'], w=['rden'])
                s.op('dve', lambda e, hg=hg: e.tensor_tensor(out=oTn[:, hg * 512:(hg + 1) * 512], in0=P[:, 6 * 512:7 * 512], in1=rden[:], op=ALU.mult),
                     r=['ps6', 'rden'], w=['oTn'])
                yield
            for hp in range(4):
                for h2 in range(2):
                    h = 2 * hp + h2
                    s.op('pe', lambda e, hp=hp, h=h, h2=h2: e.matmul(P[:, 7 * 512 + hp * 128:7 * 512 + (hp + 1) * 128], lhsT=uvpad[:, h, :],
                                                                    rhs=oTn[:, (h // 2 + 4 * (h % 2)) * 128:(h // 2 + 4 * (h % 2) + 1) * 128], start=(h2 == 0), stop=(h2 == 1)),
                         r=['uvpad', 'oTn'], w=['ps7'], inc=(h2 == 1))
            s.op('act', lambda e: e.activation(out=yinT[s3][:, 512:1024], in_=P[:, 7 * 512:8 * 512], func=AF.Copy), r=['ps7'], w=['yinT%d' % s3])
            yield
            for half in range(2):
                b = 4 + half
                for kc in range(8):
                    s.op('pe', lambda e, kc=kc, half=half, b=b: e.matmul(P[:, b * 512:(b + 1) * 512], lhsT=yinT[s3][:, kc * 128:(kc + 1) * 128],
                                                                        rhs=wout[:, kc, half * 512:(half + 1) * 512], start=(kc == 0), stop=(kc == 7)),
                         r=['yinT%d' % s3, 'wout'], w=['ps%d' % b], inc=(kc == 7))
            yield
            emit_epilogue(g, L, (4, 5), xin[s3], 'xin%d' % s3, zb, xo[sl], 'xo%d' % sl, dst[t0:t0 + 128, :])
            yield

        nblk = g.nblk_dsa
        for it in range(nblk + 2):
            gens = []
            if it < nblk:
                gens.append([front_a(it), 1])
            if 1 <= it <= nblk:
                gens.append([topk(it - 1), 1])
            if it >= 2:
                gens.append([back(it - 2), 1])
            _interleave(gens)
        s.barrier()


def emit_gdn(g, src, dst):
    nc, s = g.nc, g.s
    P = g.ps
    with ExitStack() as les:
        L = Ctx()
        L.es = les
        A = lambda name, shape, dt: _sb(nc, les, name, shape, dt)
        WT = Ctx()

        def pre():
            WT.win = A("gwin", [128, 8, 4096], BF16)
            WT.wout = A("gwout", [128, 8, D], BF16)
            wind_ = g.d_o_w_in[0].rearrange("(kc p) n -> p kc n", p=128)
            for kc in range(8):
                s.dma('pool', WT.win[:, kc, :], wind_[:, kc, 0:4096], w=['gwin'])
            woutd_ = g.d_o_w_out[0].rearrange("(kc p) n -> p kc n", p=128)
            for k2 in range(2):
                s.dma('pool', WT.wout[:, k2 * 4:(k2 + 1) * 4, :], woutd_[:, k2 * 4:(k2 + 1) * 4, :], w=['gwout'])
        emit_mods(g, L, g.d_o_mod_w[0], g.d_o_mod_b[0], g.d_o_ln_g[0], g.d_o_ln_b[0], pre=pre)
        win = WT.win
        wbf = A("gwbf", [128, 8, 16], F32)
        wbb = A("gwbb", [128, 8, 16], BF16)
        wout = WT.wout
        cw = A("gcw", [128, 24, 4], F32)
        msl = A("msl", [128, 128], F32)
        mil = A("mil", [128, 128], F32)
        triu = A("triu", [128, 128], F32)
        mdm = A("mdm", [128, 5, 128], BF16)
        mdmT = A("mdmT", [128, 5, 128], BF16)
        alog = A("alog", [128, 8], F32)
        dtb = A("dtb", [128, 8], F32)
        onb = A("onb", [128, D], F32)
        xin = [A("gxin%d" % i, [128, D], F32) for i in range(1)]
        xrs = A("gxrs", [128, D], F32)
        hT = A("ghT", [128, 8, 128], BF16)
        xw = [A("xw%d" % i, [128, 131], F32) for i in range(2)]
        halo = A("ghalo", [128, 24, 3], F32)
        cbuf = [A("cbuf%d" % i, [128, 128], F32) for i in range(2)]
        act1 = A("gact", [128, 24 * 128], F32)
        vb = [A("gvb%d" % i, [128, 1024], BF16) for i in range(2)]
        sq = A("gsq", [128, 1024], F32)
        rstd = sq
        qkb = [A("qkb%d" % i, [128, 2048], BF16) for i in range(2)]
        gsil = [A("gsil%d" % i, [128, D], BF16) for i in range(3)]
        ba = A("ba", [128, 16], F32)
        tmpa = A("tmpa", [128, 8], F32)
        smb = [A("smb%d" % i, [128, 48], F32) for i in range(2)]
        zsm = A("zsm", [128, 8], F32)
        HP = []
        NTH = 3
        for p in range(NTH):
            h_ = Ctx()
            h_.sm = A("hsm%d" % p, [128, 2], F32)
            for nm in ("E", "EB", "t1", "Nf", "atf"):
                setattr(h_, nm, A("h%s%d" % (nm, p), [128, 128], F32))
            for nm in ("X", "Xt", "Q2", "Q2t", "Q4", "Q4t", "Yv", "Yt", "attT", "qdT", "kd", "Kbg", "Vb", "nwT", "vnew"):
                setattr(h_, nm, A("h%s%d" % (nm, p), [128, 128], BF16))
            h_.NM = A("hNM%d" % p, [128, 5, 128], BF16)
            h_.NMt = A("hNMt%d" % p, [128, 5, 128], BF16)
            HP.append(h_)
        Sf = A("gSf", [128, 8, 128], F32)
        Sb = A("gSb", [128, 8, 128], BF16)
        osb = [A("gosb%d" % i, [128, D], F32) for i in range(2)]
        yinT = A("gyinT", [128, 1024], BF16)
        zb = A("gzb", [128, D], F32)
        xo = [A("gxo%d" % i, [128, D], F32) for i in range(1)]
        wind = g.d_o_w_in[0].rearrange("(kc p) n -> p kc n", p=128)
        s.dma('sp', wbf[:], wind[:, :, 4096:4112], w=['gwbf'])
        s.op('dve', lambda e: e.tensor_copy(out=wbb[:], in_=wbf[:]), r=['gwbf'], w=['gwbb'])
        s.dma('sp', cw[:], g.d_o_cw[:, :, :], w=['gcw'])
        s.dma('sp', msl[:], g.d_msl[:, :], w=['msl'])
        s.dma('sp', mil[:], g.d_mil[:, :], w=['mil'])
        s.dma('sp', triu[:], g.d_triu[:, :], w=['triu'])
        s.dma('pool', mdm[:], g.d_mdm[:, :, :], w=['mdm'])
        s.dma('pool', mdmT[:], g.d_mdmT[:, :, :], w=['mdmT'])
        s.dma('sp', alog[:], g.d_o_a_log[0].partition_broadcast(128), w=['alog'])
        s.dma('sp', dtb[:], g.d_o_dt_bias[0].partition_broadcast(128), w=['dtb'])
        s.dma('sp', onb[:], g.d_onb[0].partition_broadcast(128), w=['onb'])
        s.op('act', lambda e: e.activation(out=alog[:], in_=alog[:], func=AF.Exp), r=['alog'], w=['alog'])
        s.op('dve', lambda e: e.tensor_scalar(out=alog[:], in0=alog[:], scalar1=-1.0, scalar2=None, op0=ALU.mult), r=['alog'], w=['alog'])
        s.op('dve', lambda e: e.memset(halo[:], 0.0), w=['ghalo%d' % i for i in range(24)])
        s.op('dve', lambda e: e.memset(Sf[:], 0.0), w=['gSf0', 'gSf1', 'gSf2'])
        s.op('dve', lambda e: e.memset(Sb[:], 0.0), w=['gSb0', 'gSb1', 'gSb2'])
        DK = float(128 ** -0.5)

        def phaseA(qb):
            t0 = qb * 128
            bp = qb % 2
            x3 = qb % 3
            xn = 'gxin0'
            an = 'gact'
            sn = 'smb%d' % bp
            sm = smb[bp]
            ac = act1
            s.dma('sp', xin[0][:], src[t0:t0 + 128, :], w=[xn])
            emit_hT(g, L, xin[0], xn, lambda kc: hT[:, kc, :], 'ghT', (0, 1))
            yield
            for ch in range(24):
                b = ch % 2
                cb = cbuf[b]
                cn = 'cbuf%d' % b
                for kc in range(8):
                    s.op('pe', lambda e, kc=kc, ch=ch, b=b: e.matmul(P[:, b * 512:b * 512 + 128], lhsT=win[:, kc, ch * 128:(ch + 1) * 128], rhs=hT[:, kc, :],
                                                                    start=(kc == 0), stop=(kc == 7)), r=['gwin', 'ghT'], w=['ps%d' % b], inc=(kc == 7))
                xb = xw[b]
                xbn = 'xw%d' % b
                s.op('pool', lambda e, ch=ch, xb=xb: e.tensor_copy(out=xb[:, 0:3], in_=halo[:, ch, :]), r=['ghalo%d' % ch], w=[xbn])
                s.op('act', lambda e, b=b, xb=xb: e.activation(out=xb[:, 3:131], in_=P[:, b * 512:b * 512 + 128], func=AF.Copy), r=['ps%d' % b], w=[xbn])
                s.op('act', lambda e, ch=ch, b=b, cb=cb: e.activation(out=cb[:], in_=P[:, b * 512:b * 512 + 128], func=AF.Identity, scale=cw[:, ch, 3:4]),
                     r=['ps%d' % b, 'gcw'], w=[cn])
                s.op('pool', lambda e, ch=ch, xb=xb: e.tensor_copy(out=halo[:, ch, :], in_=xb[:, 128:131]), r=[xbn], w=['ghalo%d' % ch])
                for j in range(3):
                    dst_ = cb[:] if j < 2 else ac[:, ch * 128:(ch + 1) * 128]
                    s.op('dve', lambda e, ch=ch, j=j, cb=cb, xb=xb, dst_=dst_: e.scalar_tensor_tensor(out=dst_, in0=xb[:, j:j + 128], scalar=cw[:, ch, j:j + 1], in1=cb[:],
                                                                                                 op0=ALU.mult, op1=ALU.add),
                         r=[xbn, 'gcw', cn], w=([cn] if j < 2 else [an]))
                yield
            s.op('act', lambda e: e.activation(out=ac[:], in_=ac[:], func=AF.Silu), r=[an], w=[an])
            yield
            s.op('act', lambda e: e.activation(out=vb[bp][:], in_=ac[:, 2048:3072], func=AF.Copy), r=[an], w=['gvb%d' % bp])
            yield
            for hf in range(2):
                seg = ac[:, hf * 1024:(hf + 1) * 1024]
                s.op('dve', lambda e, seg=seg: e.tensor_tensor(out=sq[:], in0=seg, in1=seg, op=ALU.mult), r=[an], w=['gsq'])
                yield
                for j in range(2):
                    b = j % 2
                    s.op('pe', lambda e, j=j, b=b: e.matmul(P[:, b * 512:(b + 1) * 512], lhsT=g.ones[:], rhs=sq[:, j * 512:(j + 1) * 512], start=True, stop=True),
                         r=['ones', 'gsq'], w=['ps%d' % b])
                for j in range(2):
                    b = j % 2
                    s.op('dve', lambda e, j=j, b=b: e.tensor_scalar(out=sq[:, j * 512:(j + 1) * 512], in0=P[:, b * 512:(b + 1) * 512], scalar1=RMS_EPS, scalar2=None, op0=ALU.add),
                         r=['ps%d' % b], w=['gsq'])
                yield
                s.op('act', lambda e: e.activation(out=sq[:], in_=sq[:], func=AF.Sqrt), r=['gsq'], w=['gsq'])
                s.op('dve', lambda e: e.reciprocal(out=sq[:], in_=sq[:]), r=['gsq'], w=['gsq'])
                yield
                s.op('dve', lambda e, seg=seg: e.tensor_tensor(out=seg, in0=seg, in1=sq[:], op=ALU.mult), r=[an, 'gsq'], w=[an])
                s.op('act', lambda e, seg=seg, hf=hf: e.activation(out=qkb[bp][:, hf * 1024:(hf + 1) * 1024], in_=seg, func=AF.Copy), r=[an], w=['qkb%d' % bp])
                yield
            for half in range(2):
                b = half
                for kc in range(8):
                    s.op('pe', lambda e, kc=kc, half=half, b=b: e.matmul(P[:, b * 512:(b + 1) * 512], lhsT=hT[:, kc, :],
                                                                        rhs=win[:, kc, 3072 + half * 512:3072 + (half + 1) * 512],
                                                                        start=(kc == 0), stop=(kc == 7)), r=['gwin', 'ghT'], w=['ps%d' % b], inc=(kc == 7))
                s.op('act', lambda e, half=half, b=b: e.activation(out=gsil[x3][:, half * 512:(half + 1) * 512], in_=P[:, b * 512:(b + 1) * 512], func=AF.Silu),
                     r=['ps%d' % b], w=['gsil%d' % x3])
                yield
            for kc in range(8):
                s.op('pe', lambda e, kc=kc: e.matmul(P[:, 0:16], lhsT=hT[:, kc, :], rhs=wbb[:, kc, :], start=(kc == 0), stop=(kc == 7)),
                     r=['gwbb', 'ghT'], w=['ps0'], inc=(kc == 7))
            s.op('dve', lambda e: e.tensor_copy(out=ba[:], in_=P[:, 0:16]), r=['ps0'], w=['ba'])
            yield
            s.op('act', lambda e: e.activation(out=sm[:, 0:8], in_=ba[:, 0:8], func=AF.Exp, scale=-1.0), r=['ba'], w=[sn])
            s.op('dve', lambda e: e.tensor_scalar(out=sm[:, 0:8], in0=sm[:, 0:8], scalar1=1.0, scalar2=None, op0=ALU.add), r=[sn], w=[sn])
            s.op('dve', lambda e: e.reciprocal(out=sm[:, 0:8], in_=sm[:, 0:8]), r=[sn], w=[sn])
            s.op('dve', lambda e: e.tensor_scalar(out=sm[:, 32:40], in0=sm[:, 0:8], scalar1=-1.0, scalar2=None, op0=ALU.mult), r=[sn], w=[sn])
            yield
            s.op('dve', lambda e: e.tensor_tensor(out=tmpa[:], in0=ba[:, 8:16], in1=dtb[:], op=ALU.add), r=['ba', 'dtb'], w=['tmpa'])
            s.op('act', lambda e: e.activation(out=tmpa[:], in_=tmpa[:], func=AF.Exp), r=['tmpa'], w=['tmpa'])
            s.op('act', lambda e: e.activation(out=tmpa[:], in_=tmpa[:], func=AF.Ln, bias=1.0), r=['tmpa'], w=['tmpa'])
            s.op('dve', lambda e: e.tensor_tensor(out=sm[:, 8:16], in0=tmpa[:], in1=alog[:], op=ALU.mult), r=['tmpa', 'alog', sn], w=[sn])
            yield
            s.op('pe', lambda e: e.matmul(P[:, 16:24], lhsT=triu[:], rhs=sm[:, 8:16], start=True, stop=True), r=['triu', sn], w=['ps0'])
            s.op('dve', lambda e: e.tensor_copy(out=sm[:, 16:24], in_=P[:, 16:24]), r=['ps0', sn], w=[sn])
            s.op('act', lambda e: e.activation(out=sm[:, 24:32], in_=sm[:, 16:24], func=AF.Exp), r=[sn], w=[sn])
            s.op('dve', lambda e: e.tensor_tensor(out=sm[:, 40:48], in0=sm[:, 24:32], in1=sm[:, 0:8], op=ALU.mult), r=[sn], w=[sn])
            yield

        def heads(qb, p):
            bp = qb % 2
            sm = smb[bp]
            sn = 'smb%d' % bp
            vn = 'gvb%d' % bp
            H = HP[p]
            X0 = (2 + 2 * p) * 512
            Y0 = X0 + 512
            xn_, yn_ = 'ps%d' % (2 + 2 * p), 'ps%d' % (3 + 2 * p)
            zn_ = xn_
            n = lambda nm: 'h%s%d' % (nm, p)
            for h in range(p, 8, NTH):
                vvb = vb[bp][:, h * 128:(h + 1) * 128]
                qnb = qkb[bp][:, h * 128:(h + 1) * 128]
                knb = qkb[bp][:, (8 + h) * 128:(9 + h) * 128]
                qbn = 'qkb%d' % bp
                s.op('dve', lambda e, h=h: e.tensor_scalar(out=H.t1[:], in0=g.ident[:], scalar1=sm[:, 16 + h:17 + h], scalar2=None, op0=ALU.mult),
                     r=['ident', sn], w=[n('t1')])
                s.op('pe', lambda e: e.matmul(P[:, X0:X0 + 128], lhsT=g.ones[:], rhs=H.t1[:], start=True, stop=True), r=['ones', n('t1')], w=[xn_])
                s.op('dve', lambda e, h=h: e.tensor_scalar(out=H.E[:], in0=P[:, X0:X0 + 128], scalar1=sm[:, 16 + h:17 + h], scalar2=0.0, op0=ALU.subtract, op1=ALU.max),
                     r=[xn_, sn], w=[n('E')])
                s.op('act', lambda e: e.activation(out=H.E[:], in_=H.E[:], func=AF.Exp, scale=-1.0), r=[n('E')], w=[n('E')])
                s.op('act', lambda e: e.activation(out=H.EB[:], in_=P[:, X0:X0 + 128], func=AF.Exp), r=[xn_], w=[n('EB')])
                s.op('act', lambda e: e.activation(out=H.sm[:, 0:1], in_=P[:, X0 + 127:X0 + 128], func=AF.Exp), r=[xn_], w=[n('sm0')])
                s.op('dve', lambda e, h=h: e.tensor_scalar(out=H.sm[:, 1:2], in0=P[:, X0 + 127:X0 + 128], scalar1=sm[:, 16 + h:17 + h], scalar2=None, op0=ALU.subtract),
                     r=[xn_, sn], w=[n('sm1')])
                s.op('act', lambda e: e.activation(out=H.sm[:, 1:2], in_=H.sm[:, 1:2], func=AF.Exp), r=[n('sm1')], w=[n('sm1')])
                yield
                s.op('pe', lambda e, knb=knb: e.matmul(P[:, X0 + 128:X0 + 256], lhsT=knb, rhs=knb, start=True, stop=True), r=[qbn], w=[xn_])
                s.op('pe', lambda e, knb=knb, qnb=qnb: e.matmul(P[:, X0 + 256:X0 + 384], lhsT=qnb, rhs=knb, start=True, stop=True), r=[qbn], w=[xn_])
                s.op('dve', lambda e: e.tensor_tensor(out=H.t1[:], in0=H.E[:], in1=msl[:], op=ALU.mult), r=[n('E'), 'msl'], w=[n('t1')])
                s.op('dve', lambda e, h=h: e.scalar_tensor_tensor(out=H.Nf[:], in0=P[:, X0 + 128:X0 + 256], scalar=sm[:, 32 + h:33 + h], in1=H.t1[:], op0=ALU.mult, op1=ALU.mult),
                     r=[xn_, sn, n('t1')], w=[n('Nf')])
                s.op('dve', lambda e: e.tensor_tensor(out=H.t1[:], in0=H.E[:], in1=mil[:], op=ALU.mult), r=[n('E'), 'mil', n('Nf')], w=[n('t1')])
                s.op('dve', lambda e: e.scalar_tensor_tensor(out=H.atf[:], in0=P[:, X0 + 256:X0 + 384], scalar=DK, in1=H.t1[:], op0=ALU.mult, op1=ALU.mult),
                     r=[xn_, n('t1')], w=[n('atf')])
                yield
                s.op('pe', lambda e: e.transpose(P[:, Y0:Y0 + 128], H.Nf[:], g.ident[:]), r=[n('Nf'), 'ident'], w=[yn_])
                s.op('pe', lambda e: e.transpose(P[:, Y0 + 128:Y0 + 256], H.atf[:], g.ident[:]), r=[n('atf'), 'ident'], w=[yn_])
                s.op('pe', lambda e, knb=knb: e.matmul(P[:, Y0 + 256:Y0 + 384], lhsT=knb, rhs=g.identb[:], start=True, stop=True), r=[qbn, 'identb'], w=[yn_])
                s.op('pe', lambda e, vvb=vvb: e.matmul(P[:, Y0 + 384:Y0 + 512], lhsT=vvb, rhs=g.identb[:], start=True, stop=True), r=[vn, 'identb'], w=[yn_])
                s.op('dve', lambda e: e.tensor_tensor(out=H.NM[:], in0=H.Nf[:].unsqueeze(1).to_broadcast([128, 5, 128]), in1=mdm[:], op=ALU.mult),
                     r=[n('Nf'), 'mdm'], w=[n('NM')])
                s.op('dve', lambda e: e.tensor_tensor(out=H.NMt[:], in0=P[:, Y0:Y0 + 128].unsqueeze(1).to_broadcast([128, 5, 128]), in1=mdmT[:], op=ALU.mult),
                     r=[yn_, 'mdmT'], w=[n('NMt')])
                s.op('act', lambda e: e.activation(out=H.attT[:], in_=P[:, Y0 + 128:Y0 + 256], func=AF.Copy), r=[yn_], w=[n('attT')])
                s.op('dve', lambda e: e.tensor_scalar(out=H.kd[:], in0=P[:, Y0 + 256:Y0 + 384], scalar1=H.sm[:, 1:2], scalar2=None, op0=ALU.mult), r=[yn_, n('sm1')], w=[n('kd')])
                s.op('dve', lambda e, h=h: e.tensor_scalar(out=H.Kbg[:], in0=P[:, Y0 + 256:Y0 + 384], scalar1=sm[:, 40 + h:41 + h], scalar2=None, op0=ALU.mult),
                     r=[yn_, sn], w=[n('Kbg')])
                s.op('dve', lambda e, h=h: e.tensor_scalar(out=H.Vb[:], in0=P[:, Y0 + 384:Y0 + 512], scalar1=sm[:, h:h + 1], scalar2=None, op0=ALU.mult), r=[yn_, sn], w=[n('Vb')])
                s.op('dve', lambda e, qnb=qnb: e.scalar_tensor_tensor(out=H.qdT[:], in0=qnb, scalar=DK, in1=H.EB[:], op0=ALU.mult, op1=ALU.mult),
                     r=[qbn, n('EB')], w=[n('qdT')])
                yield
                zc = [0]

                def mm(lhsT, rhs, rn):
                    c0 = X0 + (zc[0] % 3) * 128
                    zc[0] += 1
                    s.op('pe', lambda e: e.matmul(P[:, c0:c0 + 128], lhsT=lhsT, rhs=rhs, start=True, stop=True), r=rn, w=[zn_])
                    return P[:, c0:c0 + 128]

                def cp(dst, dn, src_ps):
                    s.op('act', lambda e: e.activation(out=dst, in_=src_ps, func=AF.Copy), r=[zn_], w=[dn])

                def acc(dst, dn, src_ps):
                    s.op('dve', lambda e: e.tensor_tensor(out=dst, in0=src_ps, in1=dst, op=ALU.add), r=[zn_, dn], w=[dn])

                M0, M0t = H.NM[:, 0, :], H.NMt[:, 0, :]
                s.op('dve', lambda e: e.tensor_tensor(out=H.X[:], in0=M0, in1=g.identb[:], op=ALU.add), r=[n('NM'), 'identb'], w=[n('X')])
                s.op('dve', lambda e: e.tensor_tensor(out=H.Xt[:], in0=M0t, in1=g.identb[:], op=ALU.add), r=[n('NMt'), 'identb'], w=[n('Xt')])
                cp(H.Q2[:], n('Q2'), mm(M0t, M0, [n('NM'), n('NMt')]))
                cp(H.Q2t[:], n('Q2t'), mm(M0, M0t, [n('NM'), n('NMt')]))
                yield
                acc(H.X[:], n('X'), mm(H.Q2t[:], H.X[:], [n('Q2t'), n('X')]))
                acc(H.Xt[:], n('Xt'), mm(H.Q2[:], H.Xt[:], [n('Q2'), n('Xt')]))
                cp(H.Q4[:], n('Q4'), mm(H.Q2t[:], H.Q2[:], [n('Q2'), n('Q2t')]))
                cp(H.Q4t[:], n('Q4t'), mm(H.Q2[:], H.Q2t[:], [n('Q2'), n('Q2t')]))
                yield
                acc(H.X[:], n('X'), mm(H.Q4t[:], H.X[:], [n('Q4t'), n('X')]))
                acc(H.Xt[:], n('Xt'), mm(H.Q4[:], H.Xt[:], [n('Q4'), n('Xt')]))
                yield
                for lv in range(1, 5):
                    Nb, Nbt = H.NM[:, lv, :], H.NMt[:, lv, :]
                    if lv < 4:
                        cp(H.Yv[:], n('Yv'), mm(Nbt, H.X[:], [n('NMt'), n('X')]))
                    cp(H.Yt[:], n('Yt'), mm(Nb, H.Xt[:], [n('NM'), n('Xt')]))
                    yield
                    pa = mm(H.Xt[:], H.Yv[:], [n('Xt'), n('Yv')]) if lv < 4 else None
                    pb = mm(H.X[:], H.Yt[:], [n('X'), n('Yt')])
                    if lv < 4:
                        acc(H.X[:], n('X'), pa)
                    acc(H.Xt[:], n('Xt'), pb)
                    yield
                pw = mm(H.Kbg[:], H.Xt[:], [n('Kbg'), n('Xt')])
                s.op('act', lambda e: e.activation(out=H.nwT[:], in_=pw, func=AF.Copy, scale=-1.0), r=[zn_], w=[n('nwT')])
                yield
                s.op('pe', lambda e: e.matmul(P[:, Y0:Y0 + 128], lhsT=H.Xt[:], rhs=H.Vb[:], start=True, stop=False), r=[n('Xt'), n('Vb')], w=[yn_], inc=False)
                s.op('pe', lambda e, h=h: e.matmul(P[:, Y0:Y0 + 128], lhsT=H.nwT[:], rhs=Sb[:, h, :], start=False, stop=True), r=[n('nwT'), 'gSb%d' % p], w=[yn_])
                s.op('act', lambda e: e.activation(out=H.vnew[:], in_=P[:, Y0:Y0 + 128], func=AF.Copy), r=[yn_], w=[n('vnew')])
                yield
                s.op('pe', lambda e, h=h: e.matmul(P[:, X0 + 384:X0 + 512], lhsT=H.qdT[:], rhs=Sb[:, h, :], start=True, stop=False),
                     r=[n('qdT'), 'gSb%d' % p], w=[xn_], inc=False)
                s.op('pe', lambda e: e.matmul(P[:, X0 + 384:X0 + 512], lhsT=H.attT[:], rhs=H.vnew[:], start=False, stop=True),
                     r=[n('attT'), n('vnew')], w=[xn_])
                s.op('act', lambda e, h=h: e.activation(out=osb[bp][:, h * 128:(h + 1) * 128], in_=P[:, X0 + 384:X0 + 512], func=AF.Copy),
                     r=[xn_], w=['gosb%d_%d' % (bp, p)])
                s.op('pe', lambda e: e.matmul(P[:, Y0 + 128:Y0 + 256], lhsT=H.kd[:], rhs=H.vnew[:], start=True, stop=True), r=[n('kd'), n('vnew')], w=[yn_])
                s.op('dve', lambda e, h=h: e.scalar_tensor_tensor(out=Sf[:, h, :], in0=Sf[:, h, :], scalar=H.sm[:, 0:1], in1=P[:, Y0 + 128:Y0 + 256], op0=ALU.mult, op1=ALU.add),
                     r=['gSf%d' % p, n('sm0'), yn_], w=['gSf%d' % p])
                s.op('act', lambda e, h=h: e.activation(out=Sb[:, h, :], in_=Sf[:, h, :], func=AF.Copy), r=['gSf%d' % p], w=['gSb%d' % p])
                yield

        def phaseZ(qb):
            t0 = qb * 128
            bp = qb % 2
            x3 = qb % 3
            ob = osb[bp]
            on = ['gosb%d_%d' % (bp, p_) for p_ in range(NTH)]
            s.op('dve', lambda e: e.tensor_tensor(out=zb[:], in0=ob[:], in1=ob[:], op=ALU.mult), r=on, w=['zb'])
            for h in range(8):
                s.op('dve', lambda e, h=h: e.reduce_sum(out=zsm[:, h:h + 1], in_=zb[:, h * 128:(h + 1) * 128], axis=AX.X), r=['zb'], w=['zsm'])
            yield
            s.op('dve', lambda e: e.tensor_scalar(out=zsm[:], in0=zsm[:], scalar1=1.0 / 128, scalar2=RMS_EPS, op0=ALU.mult, op1=ALU.add), r=['zsm'], w=['zsm'])
            s.op('pool', lambda e: e.tensor_tensor(out=zsm[:], in0=zsm[:], in1=g.mhalf[:], op=ALU.pow), r=['zsm', 'mhalf'], w=['zsm'])
            yield
            for h in range(8):
                s.op('dve', lambda e, h=h: e.tensor_scalar(out=ob[:, h * 128:(h + 1) * 128], in0=ob[:, h * 128:(h + 1) * 128], scalar1=zsm[:, h:h + 1],
                                                           scalar2=None, op0=ALU.mult), r=on + ['zsm'], w=on)
            yield
            s.op('dve', lambda e: e.tensor_tensor(out=ob[:], in0=ob[:], in1=onb[:], op=ALU.mult), r=on + ['onb'], w=on)
            s.op('dve', lambda e: e.tensor_tensor(out=ob[:], in0=ob[:], in1=gsil[x3][:], op=ALU.mult), r=on + ['gsil%d' % x3], w=on)
            yield
            for kc in range(8):
                b = kc // 4
                off = b * 512 + (kc % 4) * 128
                s.op('pe', lambda e, kc=kc, off=off: e.transpose(P[:, off:off + 128], ob[:, kc * 128:(kc + 1) * 128], g.ident[:]), r=on + ['ident'], w=['ps%d' % b])
            for j in range(2):
                s.op('act', lambda e, j=j: e.activation(out=yinT[:, j * 512:(j + 1) * 512], in_=P[:, j * 512:(j + 1) * 512], func=AF.Copy),
                     r=['ps%d' % j], w=['gyinT'])
            yield
            for half in range(2):
                b = half
                for kc in range(8):
                    s.op('pe', lambda e, kc=kc, half=half, b=b: e.matmul(P[:, b * 512:(b + 1) * 512], lhsT=yinT[:, kc * 128:(kc + 1) * 128],
                                                                        rhs=wout[:, kc, half * 512:(half + 1) * 512], start=(kc == 0), stop=(kc == 7)),
                         r=['gyinT', 'gwout'], w=['ps%d' % b], inc=(kc == 7))
            s.dma('sp', xrs[:], src[t0:t0 + 128, :], w=['gxrs'])
            emit_epilogue(g, L, (0, 1), xrs, 'gxrs', zb, xo[0], 'gxo0', dst[t0:t0 + 128, :])
            yield

        nblk = g.nblk_gdn
        for it in range(nblk + 2):
            gens = []
            if 1 <= it <= nblk:
                for p_ in range(NTH):
                    gens.append([heads(it - 1, p_), 1])
            if it < nblk:
                gens.append([phaseA(it), 1])
            if it >= 2:
                gens.append([phaseZ(it - 2), 1])
            _interleave(gens)
        s.barrier()


W_SPECS = [
    ("e_mod_w", [1, D, 3 * D]), ("e_mod_b", [1, 1, 3 * D]), ("e_ln_g", [1, D]), ("e_ln_b", [1, D]),
    ("o_mod_w", [1, D, 3 * D]), ("o_mod_b", [1, 1, 3 * D]), ("o_ln_g", [1, D]), ("o_ln_b", [1, D]),
    ("f_mod_w", [2, D, 3 * D]), ("f_mod_b", [2, 1, 3 * D]), ("f_ln_g", [2, D]), ("f_ln_b", [2, D]),
    ("f_w_up", [2, D, 2 * DFF]), ("f_w_down", [2, DFF, D]), ("f_cw", [2, 128, 2 * NFC, 4]),
    ("ident", [128, 128]),
    ("e_w_in", [1, D, 1476]), ("e_pool_w", [1, 4, 128, 128]), ("pscale", [128, 4]), ("e_kv_norm", [1, 128]),
    ("o_w_in", [1, D, 4112]), ("o_w_out", [1, D, D]), ("o_cw", [128, 24, 4]), ("msl", [128, 128]), ("mil", [128, 128]), ("triu", [128, 128]), ("mdm", [128, 5, 128]), ("mdmT", [128, 5, 128]),
    ("o_a_log", [1, 8]), ("o_dt_bias", [1, 8]), ("onb", [1, D]),
    ("ukT", [128, 4, 128]), ("uvpad", [128, 8, 128]), ("e_w_out", [1, D, D]), ("negI4", [128, 512]), ("corr", [128, 4, 15]),
]


def build(stages):
    nc = bass.Bass("TRN2", target_bir_lowering=False)
    g = Ctx()
    g.nc = nc
    g.d_x = nc.dram_tensor("x", [S, D], F32, kind="ExternalInput").ap()
    g.d_ccol = nc.dram_tensor("ccol", [128, 8], F32, kind="ExternalInput").ap()
    for nm, shp in W_SPECS:
        setattr(g, "d_" + nm, nc.dram_tensor(nm, shp, F32, kind="ExternalInput").ap())
    g.d_out = nc.dram_tensor("out", [S, D], F32, kind="ExternalOutput").ap()
    scr = [nc.dram_tensor("xscr%d" % i, [S, D], F32, kind="Internal").ap() for i in range(3)]
    with ExitStack() as es:
        g.es = es
        g.s = Sch(nc, es)
        g.ps = es.enter_context(nc.psum_tensor("ps", [128, 8 * 512], F32))
        g.lnst = _sb(nc, es, "lnst", [128, 12], F32)
        g.lnmv = _sb(nc, es, "lnmv", [128, 8], F32)
        emit_consts(g)
        bufs = [g.d_x] + scr
        n = len(stages)
        for i, st in enumerate(stages):
            src = g.d_x if i == 0 else scr[(i - 1) % 3]
            dst = g.d_out if i == n - 1 else scr[i % 3]
            if st[0] == 'ffn':
                emit_ffn(g, st[1], src, dst)
            elif st[0] == 'gdn':
                g.nblk_gdn = st[1] if len(st) > 1 else NB
                emit_gdn(g, src, dst)
            elif st[0] == 'dsa':
                g.nblk_dsa = st[1] if len(st) > 1 else NB
                emit_dsa(g, src, dst)
            else:
                raise ValueError(st)
        g.s.finish()
    return nc


def prep_weights(inp):
    f = lambda a: np.ascontiguousarray(np.asarray(a, dtype=np.float32))
    w = {}
    for k in ("e_mod_w", "e_ln_g", "e_ln_b", "o_mod_w", "o_ln_g", "o_ln_b", "f_mod_w", "f_ln_g", "f_ln_b", "f_w_up", "f_w_down"):
        w[k] = f(inp[k])
    for k in ("e_mod_b", "o_mod_b", "f_mod_b"):
        a = f(inp[k])
        w[k] = np.ascontiguousarray(a.reshape(a.shape[0], 1, 3 * D))
    cwt = f(inp["f_conv_w"])
    cb = f(inp["f_conv_b"])
    a = np.concatenate([cwt, cb[:, None, :]], axis=1)
    a = a.reshape(2, 4, 2 * NFC, 128).transpose(0, 3, 2, 1)
    w["f_cw"] = np.ascontiguousarray(a)
    w["ident"] = np.eye(128, dtype=np.float32)
    for k in ("e_w_in", "e_pool_w", "e_kv_norm", "e_w_out"):
        w[k] = f(inp[k])
    w["pscale"] = np.ascontiguousarray(f(inp["e_pool_scale"])[0].reshape(4, 128).T)
    uk = f(inp["e_w_uk"])[0]
    w["ukT"] = np.ascontiguousarray(uk.reshape(4, 2, 128, 64).transpose(1, 3, 0, 2).reshape(128, 4, 128))
    uv = f(inp["e_w_uv"])[0]
    uvp = np.zeros((128, 8, 128), np.float32)
    for h in range(8):
        uvp[:, h, (h % 2) * 64:(h % 2) * 64 + 64] = uv[h]
    w["uvpad"] = uvp
    w["negI4"] = np.ascontiguousarray(np.tile(-30000.0 * np.eye(128, dtype=np.float32), (1, 4)))
    corr = np.ones((128, 4, 15), np.float32)
    for gi in range(4):
        win_ = 2 << gi
        for t in range(win_ - 1):
            corr[:, gi, t] = win_ / (t + 1.0)
    w["corr"] = corr
    for k in ("o_w_in", "o_w_out", "o_a_log", "o_dt_bias"):
        w[k] = f(inp[k])
    ocw = f(inp["o_conv_w"])[0]
    w["o_cw"] = np.ascontiguousarray(ocw.reshape(4, 24, 128).transpose(2, 1, 0))
    ar = np.arange(128)
    w["msl"] = (ar[:, None] > ar[None, :]).astype(np.float32)
    w["mil"] = (ar[:, None] >= ar[None, :]).astype(np.float32)
    w["triu"] = (ar[:, None] <= ar[None, :]).astype(np.float32)
    w["onb"] = np.ascontiguousarray(np.tile(f(inp["o_out_norm"])[0], 8)[None, :])
    mdm = np.zeros((128, 5, 128), np.float32)
    mdm[:, 0, :] = (ar[:, None] // 8 == ar[None, :] // 8)
    for li, bsz in enumerate((8, 16, 32, 64)):
        bl = ar // bsz
        mdm[:, 1 + li, :] = (bl[:, None] % 2 == 1) & (bl[None, :] == bl[:, None] - 1)
    w["mdm"] = mdm
    w["mdmT"] = np.ascontiguousarray(mdm.transpose(2, 1, 0))
    return w


STAGES = [('dsa',), ('ffn', 0), ('gdn',), ('ffn', 1)]


def kernel(**inp):
    x = np.asarray(inp["x"], dtype=np.float32)
    c = np.asarray(inp["c"], dtype=np.float32)
    w = prep_weights(inp)
    nc = build(STAGES)
    in_maps = []
    for b in range(8):
        m = dict(w)
        m["x"] = np.ascontiguousarray(x[b])
        m["ccol"] = np.ascontiguousarray(c[b].reshape(8, 128).T)
        in_maps.append(m)
    res = run_bass_kernel_spmd(nc, in_maps, core_ids=list(range(8)))
    return np.stack([np.asarray(r["out"], dtype=np.float32) for r in res.results], axis=0)
```
